# Optimizing a Trainium2 kernel written in Bass

```python
import jax, jax.numpy as jnp
from jax import lax
import numpy as np

D_MODEL = 1024
BATCH = 2
SEQ = 8192
DEPTH = 2

N_EVEN = (DEPTH + 1) // 2
N_ODD = DEPTH // 2
RMS_EPS = 1e-6
A_WIDTH = D_MODEL
CONV_WIDTH = 3
B_WIDTH = D_MODEL
B_GROUPS = 8
CHUNK = 128
SGU_LN_EPS = 1e-5
C_WIDTH = D_MODEL
C_HEAD_DIM = 64
C_HEADS = C_WIDTH // C_HEAD_DIM
DECAY_LORA = 64
AAA_LORA = 64
GN_EPS = 64e-5
RWKV_STREAM = 3 * C_WIDTH + DECAY_LORA + AAA_LORA
RWKV_SPLITS = (C_WIDTH, 2 * C_WIDTH, 3 * C_WIDTH, 3 * C_WIDTH + DECAY_LORA)
D_WIDTH = D_MODEL // 2
D_GROUPS = 4
D_GROUP_DIM = D_WIDTH // D_GROUPS
EVEN_PROJ = 4 * A_WIDTH + 3 * B_WIDTH
EVEN_SPLITS = (A_WIDTH, 2 * A_WIDTH, 3 * A_WIDTH, 4 * A_WIDTH,
               4 * A_WIDTH + B_WIDTH, 4 * A_WIDTH + 2 * B_WIDTH)
ODD_PROJ = RWKV_STREAM + C_WIDTH + 2 * D_WIDTH
ODD_SPLITS = (RWKV_STREAM, RWKV_STREAM + C_WIDTH, RWKV_STREAM + C_WIDTH + D_WIDTH)

kernel_name = "hybrid_conv_sgu_rwkv7_fnet_encoder"


def rms_norm(x, g):
    xf = x.astype(jnp.float32)
    y = xf * lax.rsqrt(jnp.mean(xf * xf, axis=-1, keepdims=True) + RMS_EPS)
    return (y * g.astype(jnp.float32)).astype(x.dtype)


def layer_norm(x, g, b, eps):
    xf = x.astype(jnp.float32)
    mu = jnp.mean(xf, axis=-1, keepdims=True)
    var = jnp.mean(jnp.square(xf - mu), axis=-1, keepdims=True)
    y = (xf - mu) * lax.rsqrt(var + eps)
    return (y * g + b).astype(x.dtype)


def shift_prev(p):
    pad = [(0, 0)] * (p.ndim - 2) + [(1, 0), (0, 0)]
    return jnp.pad(p, pad)[..., :-1, :]


def shift_next(p):
    pad = [(0, 0)] * (p.ndim - 2) + [(0, 1), (0, 0)]
    return jnp.pad(p, pad)[..., 1:, :]


def short_conv_branch(h, gate_b, gate_c, conv_w):
    xc = gate_c * h
    y = conv_w[0] * shift_prev(xc) + conv_w[1] * xc + conv_w[2] * shift_next(xc)
    return gate_b * y


def chunked_sgu_branch(u, v, ln_g, ln_b, w_s, b_s):
    bsz, s, c = v.shape
    vn = layer_norm(v, ln_g, ln_b, SGU_LN_EPS)
    vc = vn.reshape(bsz, s // CHUNK, CHUNK, B_GROUPS, c // B_GROUPS)
    mixed = jnp.einsum('gij,bnjgd->bnigd', w_s, vc) + b_s.T[:, :, None]
    return u * mixed.reshape(bsz, s, c)


def wkv7_step(state, inp):
    r, w, k, v, a, b = inp
    sa = jnp.einsum('...ij,...j->...i', state, a)
    state = (state * w[..., None, :] + sa[..., :, None] * b[..., None, :]
             + v[..., :, None] * k[..., None, :])
    y = jnp.einsum('...ij,...j->...i', state, r)
    return state, y


def rwkv7_bidir_branch(p, mu, w0, w2, a0, a2, k_k, k_a, r_k, lnx_g, lnx_b):
    dtype = p.dtype
    bsz, s, _ = p.shape
    pf = p.astype(jnp.float32)
    shifted = jnp.stack([shift_prev(pf), shift_next(pf)])
    q = pf[None] + mu[:, None, None, :] * (shifted - pf[None])
    r, k, v, wd, ad = jnp.split(q, RWKV_SPLITS, axis=-1)
    z_w = w0[:, None, None, :] + jnp.einsum('dbsl,dlc->dbsc', jnp.tanh(wd), w2)
    decay = jnp.exp(-jnp.exp(-jax.nn.softplus(-z_w) - 0.5))
    a = jax.nn.sigmoid(a0[:, None, None, :] + jnp.einsum('dbsl,dlc->dbsc', ad, a2))
    heads = lambda t: t.reshape(t.shape[:3] + (C_HEADS, C_HEAD_DIM))
    kk = heads(k * k_k)
    kk = kk * lax.rsqrt(jnp.maximum(jnp.sum(kk * kk, axis=-1, keepdims=True), 1e-12))
    k = k * (1.0 + (a - 1.0) * k_a)
    r, k, v, decay, a = heads(r), heads(k), heads(v), heads(decay), heads(a)

    def to_scan(t):
        t = jnp.stack([t[0], jnp.flip(t[1], axis=1)])
        return jnp.moveaxis(t, 2, 0)

    init = jnp.zeros((2, bsz, C_HEADS, C_HEAD_DIM, C_HEAD_DIM), jnp.float32)
    xs = (to_scan(r), to_scan(decay), to_scan(k), to_scan(v), to_scan(-kk), to_scan(kk * a))
    _, y = lax.scan(wkv7_step, init, xs)
    y = jnp.moveaxis(y, 0, 2)
    y_sum = y[0] + jnp.flip(y[1], axis=1)
    out = layer_norm(y_sum, lnx_g.reshape(C_HEADS, C_HEAD_DIM),
                     lnx_b.reshape(C_HEADS, C_HEAD_DIM), GN_EPS)
    bonus = jnp.sum(r * k * r_k, axis=-1, keepdims=True) * v
    out = out + bonus[0] + bonus[1]
    return out.reshape(bsz, s, C_WIDTH).astype(dtype)


def fourier_branch(f, w_f):
    bsz, s, c = f.shape
    fg = f.reshape(bsz, s, D_GROUPS, D_GROUP_DIM).astype(jnp.float32)
    spec = jnp.fft.fft2(fg, axes=(1, 3), norm="ortho").real.astype(f.dtype)
    y = jnp.einsum('bsgd,gde->bsge', spec, w_f)
    return y.reshape(bsz, s, c)


def even_layer(h, w_in, conv_w, sgu_ln_g, sgu_ln_b, sgu_w, sgu_b, w_out):
    p = jnp.einsum('bsd,de->bse', h, w_in)
    xa, ba, ca, za, ub, vb, zb = jnp.split(p, EVEN_SPLITS, axis=-1)
    ya = short_conv_branch(xa, ba, ca, conv_w) * jax.nn.silu(za)
    yb = chunked_sgu_branch(ub, vb, sgu_ln_g, sgu_ln_b, sgu_w, sgu_b) * jax.nn.silu(zb)
    return jnp.einsum('bse,ed->bsd', jnp.concatenate([ya, yb], axis=-1), w_out)


def odd_layer(h, w_in, mu, w0, w2, a0, a2, k_k, k_a, r_k, lnx_g, lnx_b, fnet_w, w_out):
    p = jnp.einsum('bsd,de->bse', h, w_in)
    pc, zc, fd, zd = jnp.split(p, ODD_SPLITS, axis=-1)
    yc = rwkv7_bidir_branch(pc, mu, w0, w2, a0, a2, k_k, k_a, r_k, lnx_g, lnx_b) * jax.nn.silu(zc)
    yd = fourier_branch(fd, fnet_w) * jax.nn.silu(zd)
    return jnp.einsum('bse,ed->bsd', jnp.concatenate([yc, yd], axis=-1), w_out)


def setup_inputs(seed: int = 0) -> dict:
    key = jax.random.key(seed)
    ks = iter(jax.random.split(key, 32))
    nrm = lambda shape, scale: scale * jax.random.normal(next(ks), shape, jnp.float32)
    NE, NO = N_EVEN, N_ODD
    return {
        "x": nrm((BATCH, SEQ, D_MODEL), 1.0),
        "e_norm_g": 1.0 + nrm((NE, D_MODEL), 0.02),
        "e_w_in": nrm((NE, D_MODEL, EVEN_PROJ), D_MODEL ** -0.5),
        "e_conv_w": nrm((NE, CONV_WIDTH, A_WIDTH), CONV_WIDTH ** -0.5),
        "e_sgu_ln_g": 1.0 + nrm((NE, B_WIDTH), 0.02),
        "e_sgu_ln_b": nrm((NE, B_WIDTH), 0.02),
        "e_sgu_w": nrm((NE, B_GROUPS, CHUNK, CHUNK), CHUNK ** -0.5),
        "e_sgu_b": 1.0 + nrm((NE, B_GROUPS, CHUNK), 0.01),
        "e_w_out": nrm((NE, A_WIDTH + B_WIDTH, D_MODEL), (A_WIDTH + B_WIDTH) ** -0.5),
        "o_norm_g": 1.0 + nrm((NO, D_MODEL), 0.02),
        "o_w_in": nrm((NO, D_MODEL, ODD_PROJ), D_MODEL ** -0.5),
        "o_mu": jax.random.uniform(next(ks), (NO, 2, RWKV_STREAM), jnp.float32),
        "o_w0": nrm((NO, 2, C_WIDTH), 0.5),
        "o_w2": nrm((NO, 2, DECAY_LORA, C_WIDTH), 0.5 * DECAY_LORA ** -0.5),
        "o_a0": nrm((NO, 2, C_WIDTH), 0.1),
        "o_a2": nrm((NO, 2, AAA_LORA, C_WIDTH), 0.5 * AAA_LORA ** -0.5),
        "o_k_k": 0.85 + nrm((NO, C_WIDTH), 0.05),
        "o_k_a": 1.0 + nrm((NO, C_WIDTH), 0.05),
        "o_r_k": nrm((NO, C_HEADS, C_HEAD_DIM), 0.1),
        "o_lnx_g": 1.0 + nrm((NO, C_WIDTH), 0.02),
        "o_lnx_b": nrm((NO, C_WIDTH), 0.02),
        "o_fnet_w": nrm((NO, D_GROUPS, D_GROUP_DIM, D_GROUP_DIM), D_GROUP_DIM ** -0.5),
        "o_w_out": nrm((NO, C_WIDTH + D_WIDTH, D_MODEL), (C_WIDTH + D_WIDTH) ** -0.5),
        "final_norm_g": 1.0 + nrm((D_MODEL,), 0.02),
    }


def reference(x, e_norm_g, e_w_in, e_conv_w, e_sgu_ln_g, e_sgu_ln_b, e_sgu_w, e_sgu_b, e_w_out,
              o_norm_g, o_w_in, o_mu, o_w0, o_w2, o_a0, o_a2, o_k_k, o_k_a, o_r_k,
              o_lnx_g, o_lnx_b, o_fnet_w, o_w_out, final_norm_g):
    h = x
    for layer in range(DEPTH):
        i = layer // 2
        if layer % 2 == 0:
            h = h + even_layer(rms_norm(h, e_norm_g[i]), e_w_in[i], e_conv_w[i],
                               e_sgu_ln_g[i], e_sgu_ln_b[i], e_sgu_w[i], e_sgu_b[i], e_w_out[i])
        else:
            h = h + odd_layer(rms_norm(h, o_norm_g[i]), o_w_in[i], o_mu[i], o_w0[i], o_w2[i],
                              o_a0[i], o_a2[i], o_k_k[i], o_k_a[i], o_r_k[i],
                              o_lnx_g[i], o_lnx_b[i], o_fnet_w[i], o_w_out[i])
    return rms_norm(h, final_norm_g)
```

```python
from contextlib import ExitStack
import numpy as np
import concourse.bass as bass
import concourse.mybir as mybir
from concourse.bass_utils import run_bass_kernel_spmd


F32 = mybir.dt.float32
BF16 = mybir.dt.bfloat16
AF = mybir.ActivationFunctionType
ALU = mybir.AluOpType
AX = mybir.AxisListType

N_DMA_SEMS = 8


class Region:
    __slots__ = ("w", "r", "name")

    def __init__(self, name=""):
        self.w = None
        self.r = {}
        self.name = name


class Sched:
    ENGS = ("pe", "dve", "act", "pool", "sp")

    def __init__(self, nc):
        self.nc = nc
        self.prog = {e: [] for e in self.ENGS}
        self.cnt = {}
        self.seen = {e: {} for e in self.ENGS}
        self.dma_rr = {e: 0 for e in self.ENGS}
        self.dma_last = {}
        self.same_engine_raw = True

    def _collect(self, eng, mykey, reads, writes):
        waits = {}

        def need(tok, kind):
            if tok is None:
                return
            k, v = tok
            if k == mykey:
                if eng == "pe":
                    return
                if not self.same_engine_raw:
                    return
            if waits.get(k, 0) < v:
                waits[k] = v

        for R in reads:
            need(R.w, "raw")
        for R in writes:
            need(R.w, "waw")
            for k, v in R.r.items():
                need((k, v), "war")
        out = []
        seen = self.seen[eng]
        for k, v in waits.items():
            if seen.get(k, 0) < v:
                seen[k] = v
                out.append((k, v))
        return out

    def _commit(self, tok, reads, writes):
        for R in writes:
            R.w = tok
            R.r = {}
        k, v = tok
        for R in reads:
            if R.r.get(k, 0) < v:
                R.r[k] = v

    def op(self, eng, fn, reads=(), writes=()):
        key = eng
        waits = self._collect(eng, key, reads, writes)
        idx = self.cnt.get(key, 0) + 1
        self.cnt[key] = idx
        tok = (key, idx)
        self.prog[eng].append([waits, fn, tok])
        self._commit(tok, reads, writes)
        return tok

    def dma(self, fn, reads=(), writes=(), q="sp"):
        i = self.dma_rr[q]
        self.dma_rr[q] = (i + 1) % N_DMA_SEMS
        key = "dma_%s_%d" % (q, i)
        waits = self._collect(q, key, reads, writes)
        prev = self.cnt.get(key, 0)
        if prev > 0 and self.seen[q].get(key, 0) < prev:
            self.seen[q][key] = prev
            waits.append((key, prev))
        idx = prev + 1
        self.cnt[key] = idx
        tok = (key, idx)
        self.prog[q].append([waits, fn, tok])
        self._commit(tok, reads, writes)
        return tok

    def finalize(self):
        nc = self.nc
        waited = {}
        for e in self.ENGS:
            for waits, fn, tok in self.prog[e]:
                for k, v in waits:
                    waited.setdefault(k, set()).add(v)
        self.final_waits = []
        sem_of = {}
        val_of = {}
        for k, s in waited.items():
            sem_of[k] = nc.alloc_semaphore("s_" + k)
            isdma = k.startswith("dma_")
            step = 16 if isdma else 1
            if isdma:
                val_of[k] = None
            else:
                val_of[k] = {v: (i + 1) for i, v in enumerate(sorted(s))}
        engobj = {"pe": nc.tensor, "dve": nc.vector, "act": nc.scalar,
                  "pool": nc.gpsimd, "sp": nc.sync}

        def value(k, v):
            if val_of[k] is None:
                return 16 * v
            return val_of[k][v]

        def emit(e):
            def body(eng):
                for waits, fn, tok in self.prog[e]:
                    for k, v in waits:
                        eng.wait_ge(sem_of[k], value(k, v))
                    if fn is None:
                        continue
                    if isinstance(fn, tuple):
                        ins = getattr(eng, fn[0])(**fn[1])
                    else:
                        ins = fn(eng)
                    k, v = tok
                    if k in sem_of:
                        if val_of[k] is None:
                            ins.then_inc(sem_of[k], 16)
                        elif v in val_of[k]:
                            ins.then_inc(sem_of[k], 1)
            return body

        with nc.Block() as block:
            for e, dec in (("sp", block.sync), ("pe", block.tensor), ("dve", block.vector),
                           ("act", block.scalar), ("pool", block.gpsimd)):
                if self.prog[e]:
                    dec(emit(e))
        self.n_sems = len(sem_of)
        return self.n_sems

    def barrier_on(self, eng, toks):
        waits = []
        for k, v in toks:
            if self.seen[eng].get(k, 0) < v:
                self.seen[eng][k] = v
                waits.append((k, v))
        if waits:
            self.prog[eng].append([waits, None, ("_none", 0)])


C = 128
BLK = 512
NEG_E = -float(np.exp(-0.5))
GN_EPS = 64e-5


def build_consts_np():
    idx = np.arange(128)
    lt = (idx[:, None] < idx[None, :]).astype(np.float32)
    le = (idx[:, None] <= idx[None, :]).astype(np.float32)
    gt = lt.T.copy()
    ge = le.T.copy()
    m4f = np.stack([lt, gt, gt, le], axis=1)
    m4b = np.stack([gt, lt, lt, ge], axis=1)
    mk = np.stack([le, ge], axis=1)
    ident = np.eye(128, dtype=np.float32)
    bd = np.kron(np.eye(2, dtype=np.float32), np.ones((64, 64), np.float32))
    scanm = np.ones((128, BLK), np.float32)
    scanm[:, ::C] = 0.0
    return {"c_m4": np.stack([m4f, m4b], axis=1).reshape(128, 2 * 4 * 128).copy(),
            "c_mk": mk.reshape(128, 256).copy(), "c_ident": ident, "c_bd": bd, "c_scanm": scanm}


def emit_rwkv(S, nc, A, pr, pk, pv, pwa, prm, w2a2, yout, NB, T):
    sb = lambda name, shape, dt=F32: nc.alloc_sbuf_tensor(name, shape, dt)
    ps = lambda name, shape, dt=F32: nc.alloc_psum_tensor(name, shape, dt)
    R = Region
    nblk = T // BLK

    m4f = sb("m4f", [128, 2, 4, 128]); Rm4 = R()
    mkf = sb("mkf", [128, 2, 128]); Rmk = R()
    identf = sb("identf", [128, 128]); Ridf = R()
    identb = sb("identb", [128, 128], BF16); Ridb = R()
    bdf = sb("bdf", [128, 128]); Rbd = R()
    bdr = sb("bdr", [128, 128]); Rbdr = R()
    bdm = sb("bdm", [128, 128]); Rbdm = R()
    scanm = sb("scanm", [128, BLK]); Rsc = R()
    prmt = sb("prmt", [128, 17]); Rprm = R()
    w2f = sb("w2f", [128, 2, 128]); Rw2f = R()
    w2b = sb("w2b", [128, 2, 128], BF16); Rw2b = R()
    S.dma(("dma_start", dict(out=m4f[:].rearrange("p a b c -> p (a b c)"), in_=A["c_m4"])), writes=[Rm4])
    S.dma(("dma_start", dict(out=mkf[:].rearrange("p a c -> p (a c)"), in_=A["c_mk"])), writes=[Rmk])
    S.dma(("dma_start", dict(out=identf[:], in_=A["c_ident"])), writes=[Ridf])
    S.dma(("dma_start", dict(out=bdf[:], in_=A["c_bd"])), writes=[Rbd])
    S.dma(("dma_start", dict(out=scanm[:], in_=A["c_scanm"])), writes=[Rsc])
    S.dma(("dma_start", dict(out=prmt[:], in_=prm)), writes=[Rprm])
    S.dma(("dma_start", dict(out=w2f[:], in_=w2a2)), writes=[Rw2f])
    S.op("dve", ("tensor_copy", dict(out=identb[:], in_=identf[:])), reads=[Ridf], writes=[Ridb])
    S.op("dve", ("tensor_copy", dict(out=w2b[:], in_=w2f[:])), reads=[Rw2f], writes=[Rw2b])
    PM = lambda c: prmt[:, c:c + 1]
    S.op("dve", ("tensor_scalar", dict(out=bdr[:], in0=bdf[:], scalar1=PM(14), scalar2=None, op0=ALU.mult)), reads=[Rbd, Rprm], writes=[Rbdr])
    S.op("dve", ("tensor_scalar", dict(out=bdm[:], in0=bdf[:], scalar1=1.0 / 64, scalar2=None, op0=ALU.mult)), reads=[Rbd], writes=[Rbdm])

    def T2(name, dt=F32, n=BLK):
        return sb(name, [128, n], dt), R()
    ld = {}
    for nm in ("pr", "pk", "pv", "pwa"):
        ld[nm] = (sb("ld_" + nm, [128, BLK + 2]), R())
    tmp, Rtmp = T2("tmp")
    qr, Rqr = T2("qr"); qk, Rqk = T2("qk"); qv, Rqv = T2("qv"); qwa, Rqwa = T2("qwa")
    twa, Rtwa = T2("twa", BF16)
    sw, Rsw = T2("sw"); asg, Rasg = T2("asg")
    logw, Rlogw = T2("logw"); lin, Rlin = T2("lin"); linm, Rlinm = T2("linm"); lexm, Rlexm = T2("lexm")
    lex, Rlex = T2("lex"); lint, Rlint = T2("lint")
    e1, Re1 = T2("e1"); e1x, Re1x = T2("e1x"); e2, Re2 = T2("e2"); e3, Re3 = T2("e3"); e3x, Re3x = T2("e3x"); e4, Re4 = T2("e4")
    kk, Rkk = T2("kk"); kk2, Rkk2 = T2("kk2"); rin, Rrin = T2("rin"); kkn, Rkkn = T2("kkn")
    kp, Rkp = T2("kp"); bv, Rbv = T2("bv"); rk, Rrk = T2("rk")
    rt, Rrt = T2("rt", BF16); at, Rat = T2("at", BF16); kt, Rkt = T2("kt", BF16); bt, Rbt = T2("bt", BF16)
    r0, Rr0 = T2("r0"); a0b, Ra0b = T2("a0b", BF16); kEb, RkEb = T2("kEb", BF16); bEb, RbEb = T2("bEb", BF16)
    qvb, Rqvb = T2("qvb", BF16)
    ysum = sb("ysum", [128, T]); Rys = [R() for _ in range(T // C)]
    bsum = sb("bsum", [128, T]); Rbs = [R() for _ in range(nblk)]
    TT = sb("TT", [128, 4, 128], BF16); RTT = R()
    SBM = [sb("SBM%d" % h, [128, 4, 128], BF16) for h in range(2)]; RSBM = [R(), R()]
    MKR = [sb("MKR%d" % h, [128, 128]) for h in range(2)]; RMKR = [R(), R()]
    N0 = [sb("N0%d" % h, [128, 192]) for h in range(2)]; RN0 = [R(), R()]
    SX = [sb("SX%d" % h, [128, 192], BF16) for h in range(2)]; RSX = [R(), R()]
    SAB = [[sb("SAB%d%d" % (h, i), [128, 2, 128], BF16) for i in range(2)] for h in range(2)]
    RSAB = [[R(), R()], [R(), R()]]
    Gb = sb("Gb", [128, 128], BF16); RGb = [R(), R()]
    Hb = [sb("Hb%d" % h, [128, 128], BF16) for h in range(2)]; RHb = [R(), R()]
    Pb = sb("Pb", [128, 64], BF16); RPb = [R(), R()]
    Zb = [sb("Zb%d" % h, [128, 64], BF16) for h in range(2)]; RZb = [R(), R()]
    ST = sb("ST", [128, 64], BF16); RST = [R(), R()]
    fin1, Rfin1 = T2("fin1"); fin2, Rfin2 = T2("fin2"); fin3, Rfin3 = T2("fin3")

    PS_M = ps("PS_M", [128, 4, 128]); RPS_M = R()
    PS_B = ps("PS_B", [128, 512]); RPS_B = R()
    PS_X = ps("PS_X", [128, 512]); RPS_X = R()
    PS_AB = ps("PS_AB", [128, 4, 128]); RPS_AB = R()
    PS_T = ps("PS_T", [128, 8, 128], BF16); RPS_T = R()
    PS_S = ps("PS_S", [128, 512]); RPS_S = R()
    PS_P1 = ps("PS_P1", [128, 512]); RPS_P1 = R()
    PS_P2 = ps("PS_P2", [128, 512]); RPS_P2 = R()

    STOP = 99
    def early(tile):
        return [S.dma(("dma_start", dict(out=yout[:, 0, 0:BLK], in_=tile[:])), reads=[Rqr, Rqk, Rqv, Rqwa, Re1, Re4, Rkkn, Rbs[0], Rrt, Rat, Rkt, Rbt, Rr0, RTT, RSBM[0], RMKR[0], RSX[0], RSX[1]])]
    out_toks = []
    for b in range(NB):
        for d in range(2):
            bwd = (d == 1)
            midc, totc = (C // 2 - 1, C - 1) if not bwd else (C // 2, 0)
            S.op("pool", ("memset", dict(ap=ST[:], constant=0.0)), writes=[RST[0], RST[1]])
            blocks = range(nblk) if not bwd else range(nblk - 1, -1, -1)
            for blk in blocks:
                t0 = blk * BLK
                for nm, src in (("pr", pr), ("pk", pk), ("pv", pv), ("pwa", pwa)):
                    tl, Rl = ld[nm]
                    S.dma(("dma_start", dict(out=tl[:], in_=src[:, b, t0:t0 + BLK + 2])), writes=[Rl])
                sh = (slice(0, BLK) if not bwd else slice(2, BLK + 2))
                cur = slice(1, BLK + 1)
                for nm, q, Rq, mc in (("pr", qr, Rqr, 0), ("pk", qk, Rqk, 2), ("pv", qv, Rqv, 4), ("pwa", qwa, Rqwa, 6)):
                    tl, Rl = ld[nm]
                    S.op("dve", ("tensor_tensor", dict(out=tmp[:], in0=tl[:, sh], in1=tl[:, cur], op=ALU.subtract)), reads=[Rl], writes=[Rtmp])
                    S.op("dve", ("scalar_tensor_tensor", dict(out=q[:], in0=tmp[:], scalar=PM(mc + d), in1=tl[:, cur], op0=ALU.mult, op1=ALU.add)), reads=[Rtmp, Rl, Rprm], writes=[Rq])
                if STOP <= 1:
                    return early(qr)
                S.op("act", ("activation", dict(out=twa[0:64, :], in_=qwa[0:64, :], func=AF.Tanh)), reads=[Rqwa], writes=[Rtwa])
                S.op("dve", ("tensor_copy", dict(out=twa[64:128, :], in_=qwa[64:128, :])), reads=[Rqwa], writes=[Rtwa])
                S.op("pe", ("matmul", dict(out=PS_P1[:], lhsT=w2b[0:64, d, :], rhs=twa[0:64, :], start=True, stop=True)), reads=[Rw2b, Rtwa], writes=[RPS_P1])
                S.op("pe", ("matmul", dict(out=PS_P2[:], lhsT=w2b[64:128, d, :], rhs=twa[64:128, :], start=True, stop=True)), reads=[Rw2b, Rtwa], writes=[RPS_P2])
                S.op("act", ("activation", dict(out=sw[:], in_=PS_P1[:], func=AF.Sigmoid, bias=PM(8 + d))), reads=[RPS_P1, Rprm], writes=[Rsw])
                S.op("act", ("activation", dict(out=asg[:], in_=PS_P2[:], func=AF.Sigmoid, bias=PM(10 + d))), reads=[RPS_P2, Rprm], writes=[Rasg])
                S.op("dve", ("tensor_scalar", dict(out=logw[:], in0=sw[:], scalar1=NEG_E, scalar2=None, op0=ALU.mult)), reads=[Rsw], writes=[Rlogw])
                S.op("dve", ("tensor_tensor_scan", dict(out=lin[:], data0=scanm[:], data1=logw[:], initial=0.0, op0=ALU.mult, op1=ALU.add)), reads=[Rsc, Rlogw], writes=[Rlin])
                lin3 = lambda tl: tl[:].rearrange("p (c t) -> p c t", t=C)
                bc = lambda tl, col: lin3(tl)[:, :, col:col + 1].to_broadcast([128, BLK // C, C])
                if bwd:
                    S.op("dve", ("tensor_tensor", dict(out=lin3(tmp), in0=bc(lin, C - 1), in1=lin3(lin), op=ALU.subtract)), reads=[Rlin], writes=[Rtmp])
                    S.op("dve", ("tensor_tensor", dict(out=lin[:], in0=tmp[:], in1=logw[:], op=ALU.add)), reads=[Rtmp, Rlogw], writes=[Rlin])
                S.op("dve", ("tensor_tensor", dict(out=lin3(linm), in0=lin3(lin), in1=bc(lin, midc), op=ALU.subtract)), reads=[Rlin], writes=[Rlinm])
                S.op("dve", ("tensor_tensor", dict(out=lexm[:], in0=linm[:], in1=logw[:], op=ALU.subtract)), reads=[Rlinm, Rlogw], writes=[Rlexm])
                S.op("dve", ("tensor_tensor", dict(out=lex[:], in0=lin[:], in1=logw[:], op=ALU.subtract)), reads=[Rlin, Rlogw], writes=[Rlex])
                S.op("dve", ("tensor_tensor", dict(out=lin3(lint), in0=lin3(lin), in1=bc(lin, totc), op=ALU.subtract)), reads=[Rlin], writes=[Rlint])
                S.op("act", ("activation", dict(out=e1[:], in_=linm[:], func=AF.Exp)), reads=[Rlinm], writes=[Re1])
                S.op("act", ("activation", dict(out=e1x[:], in_=lexm[:], func=AF.Exp)), reads=[Rlexm], writes=[Re1x])
                S.op("act", ("activation", dict(out=e2[:], in_=linm[:], func=AF.Exp, scale=-1.0)), reads=[Rlinm], writes=[Re2])
                S.op("act", ("activation", dict(out=e3[:], in_=lin[:], func=AF.Exp)), reads=[Rlin], writes=[Re3])
                S.op("act", ("activation", dict(out=e3x[:], in_=lex[:], func=AF.Exp)), reads=[Rlex], writes=[Re3x])
                S.op("act", ("activation", dict(out=e4[:], in_=lint[:], func=AF.Exp, scale=-1.0)), reads=[Rlint], writes=[Re4])
                if STOP <= 2:
                    return early(e1)
                S.op("dve", ("tensor_scalar", dict(out=kk[:], in0=qk[:], scalar1=PM(12), scalar2=None, op0=ALU.mult)), reads=[Rqk, Rprm], writes=[Rkk])
                S.op("pool", ("tensor_tensor", dict(out=kk2[:], in0=kk[:], in1=kk[:], op=ALU.mult)), reads=[Rkk], writes=[Rkk2])
                S.op("pe", ("matmul", dict(out=PS_P1[:], lhsT=bdf[:], rhs=kk2[:], start=True, stop=True)), reads=[Rbd, Rkk2], writes=[RPS_P1])
                S.op("dve", ("tensor_scalar", dict(out=rin[:], in0=PS_P1[:], scalar1=1e-12, scalar2=None, op0=ALU.max)), reads=[RPS_P1], writes=[Rrin])
                S.op("act", ("activation", dict(out=rin[:], in_=rin[:], func=AF.Sqrt)), reads=[Rrin], writes=[Rrin])
                S.op("dve", ("reciprocal", dict(out=rin[:], in_=rin[:])), reads=[Rrin], writes=[Rrin])
                S.op("dve", ("tensor_tensor", dict(out=kkn[:], in0=kk[:], in1=rin[:], op=ALU.mult)), reads=[Rkk, Rrin], writes=[Rkkn])
                S.op("dve", ("tensor_scalar", dict(out=tmp[:], in0=asg[:], scalar1=-1.0, scalar2=PM(13), op0=ALU.add, op1=ALU.mult)), reads=[Rasg, Rprm], writes=[Rtmp])
                S.op("dve", ("scalar_tensor_tensor", dict(out=kp[:], in0=tmp[:], scalar=1.0, in1=qk[:], op0=ALU.add, op1=ALU.mult)), reads=[Rtmp, Rqk], writes=[Rkp])
                S.op("pool", ("tensor_tensor", dict(out=bv[:], in0=kkn[:], in1=asg[:], op=ALU.mult)), reads=[Rkkn, Rasg], writes=[Rbv])
                S.op("pool", ("tensor_tensor", dict(out=rk[:], in0=qr[:], in1=kp[:], op=ALU.mult)), reads=[Rqr, Rkp], writes=[Rrk])
                S.op("pe", ("matmul", dict(out=PS_P2[:], lhsT=bdr[:], rhs=rk[:], start=True, stop=True)), reads=[Rbdr, Rrk], writes=[RPS_P2])
                bsl = bsum[:, t0:t0 + BLK]
                if d == 0:
                    S.op("dve", ("tensor_tensor", dict(out=bsl, in0=PS_P2[:], in1=qv[:], op=ALU.mult)), reads=[RPS_P2, Rqv], writes=[Rbs[blk]])
                else:
                    S.op("dve", ("tensor_tensor", dict(out=tmp[:], in0=PS_P2[:], in1=qv[:], op=ALU.mult)), reads=[RPS_P2, Rqv], writes=[Rtmp])
                    S.op("pool", ("tensor_tensor", dict(out=bsl, in0=bsl, in1=tmp[:], op=ALU.add)), reads=[Rtmp, Rbs[blk]], writes=[Rbs[blk]])
                if STOP <= 3:
                    return early(kkn)
                S.op("dve", ("tensor_tensor", dict(out=rt[:], in0=qr[:], in1=e1[:], op=ALU.mult)), reads=[Rqr, Re1], writes=[Rrt])
                S.op("dve", ("scalar_tensor_tensor", dict(out=at[:], in0=kkn[:], scalar=-1.0, in1=e1x[:], op0=ALU.mult, op1=ALU.mult)), reads=[Rkkn, Re1x], writes=[Rat])
                S.op("pool", ("tensor_tensor", dict(out=kt[:], in0=kp[:], in1=e2[:], op=ALU.mult)), reads=[Rkp, Re2], writes=[Rkt])
                S.op("pool", ("tensor_tensor", dict(out=bt[:], in0=bv[:], in1=e2[:], op=ALU.mult)), reads=[Rbv, Re2], writes=[Rbt])
                S.op("pool", ("tensor_tensor", dict(out=r0[:], in0=qr[:], in1=e3[:], op=ALU.mult)), reads=[Rqr, Re3], writes=[Rr0])
                S.op("dve", ("scalar_tensor_tensor", dict(out=a0b[:], in0=kkn[:], scalar=-1.0, in1=e3x[:], op0=ALU.mult, op1=ALU.mult)), reads=[Rkkn, Re3x], writes=[Ra0b])
                S.op("pool", ("tensor_tensor", dict(out=kEb[:], in0=kp[:], in1=e4[:], op=ALU.mult)), reads=[Rkp, Re4], writes=[RkEb])
                S.op("pool", ("tensor_tensor", dict(out=bEb[:], in0=bv[:], in1=e4[:], op=ALU.mult)), reads=[Rbv, Re4], writes=[RbEb])
                S.op("act", ("activation", dict(out=qvb[:], in_=qv[:], func=AF.Copy)), reads=[Rqv], writes=[Rqvb])

                if STOP <= 4:
                    return early(r0)
                chunks = range(BLK // C) if not bwd else range(BLK // C - 1, -1, -1)
                for ci in chunks:
                    cs = slice(ci * C, (ci + 1) * C)
                    gci = (t0 // C) + ci
                    for i, (src, Rs) in enumerate(((qvb, Rqvb), (a0b, Ra0b), (bEb, RbEb), (kEb, RkEb))):
                        S.op("pe", ("transpose", dict(out=PS_T[:, i, :], in_=src[:, cs], identity=identb[:])), reads=[Rs, Ridb], writes=[RPS_T])
                    S.op("act", ("activation", dict(out=TT[:], in_=PS_T[:, 0:4, :], func=AF.Copy)), reads=[RPS_T], writes=[RTT])
                    if STOP <= 5:
                        return early(r0)
                    for h in range(2):
                        hs = slice(h * 64, (h + 1) * 64)
                        mm = lambda out, l, r_, rd, wr, st=True, sp=True, sg=False: S.op("pe", ("matmul", dict(out=out, lhsT=l, rhs=r_, start=st, stop=sp, skip_group_check=sg)), reads=rd, writes=wr)
                        mm(PS_M[:, 0, :], bt[hs, cs], at[hs, cs], [Rbt, Rat], [RPS_M])
                        mm(PS_M[:, 1, :], at[hs, cs], bt[hs, cs], [Rbt, Rat], [RPS_M])
                        mm(PS_M[:, 2, :], at[hs, cs], kt[hs, cs], [Rkt, Rat], [RPS_M])
                        mm(PS_M[:, 3, :], bt[hs, cs], rt[hs, cs], [Rbt, Rrt], [RPS_M])
                        mm(PS_B[:, 0:128], kt[hs, cs], rt[hs, cs], [Rkt, Rrt], [RPS_B])
                        S.op("dve", ("tensor_tensor", dict(out=SBM[h][:], in0=PS_M[:], in1=m4f[:, d, :, :], op=ALU.mult)), reads=[RPS_M, Rm4], writes=[RSBM[h]])
                        S.op("dve", ("tensor_tensor", dict(out=MKR[h][:], in0=PS_B[:, 0:128], in1=mkf[:, d, :], op=ALU.mult)), reads=[RPS_B, Rmk], writes=[RMKR[h]])
                        S.op("act", ("activation", dict(out=N0[h][:, 0:128], in_=SBM[h][:, 3, :], func=AF.Copy)), reads=[RSBM[h]], writes=[RN0[h]])
                        S.op("act", ("activation", dict(out=N0[h][:, 128:192], in_=TT[:, 2, hs], func=AF.Copy)), reads=[RTT], writes=[RN0[h]])
                        S.op("pool", ("tensor_copy", dict(out=SX[h][:], in_=N0[h][:])), reads=[RN0[h]], writes=[RSX[h]])
                        if STOP <= 6 and h == 1:
                            return early(r0)
                        Acur, Bcur, Rcur = SBM[h][:, 1, :], SBM[h][:, 0, :], RSBM[h]
                        for lv in range(7):
                            mm(PS_X[:, 0:192], Acur, SX[h][:], [Rcur, RSX[h]], [RPS_X], st=(lv == 0), sp=True, sg=(lv > 0))
                            if lv < 6:
                                nb = lv % 2
                                mm(PS_AB[:, 0, :], Bcur, Acur, [Rcur], [RPS_AB])
                                mm(PS_AB[:, 1, :], Acur, Bcur, [Rcur], [RPS_AB])
                                S.op("act", ("activation", dict(out=SAB[h][nb][:], in_=PS_AB[:, 0:2, :], func=AF.Copy)), reads=[RPS_AB], writes=[RSAB[h][nb]])
                                Acur, Bcur, Rcur = SAB[h][nb][:, 0, :], SAB[h][nb][:, 1, :], RSAB[h][nb]
                            S.op("dve", ("tensor_tensor", dict(out=SX[h][:], in0=PS_X[:, 0:192], in1=N0[h][:], op=ALU.add)), reads=[RPS_X, RN0[h]], writes=[RSX[h]])
                        if STOP <= 7 and h == 1:
                            return early(r0)
                        a0T = TT[:, 1, hs]
                        mm(PS_B[hs, 128:256], a0T, SX[h][:, 0:128], [RTT, RSX[h]], [RPS_B])
                        mm(PS_B[:, 256:384], SBM[h][:, 2, :], SX[h][:, 0:128], [RSBM[h], RSX[h]], [RPS_B])
                        mm(PS_B[hs, 384:448], a0T, SX[h][:, 128:192], [RTT, RSX[h]], [RPS_B])
                        mm(PS_B[:, 448:512], SBM[h][:, 2, :], SX[h][:, 128:192], [RSBM[h], RSX[h]], [RPS_B])
                        S.op("dve", ("tensor_tensor", dict(out=Gb[hs, :], in0=PS_B[hs, 128:256], in1=r0[hs, cs], op=ALU.add)), reads=[RPS_B, Rr0], writes=[RGb[h]])
                        S.op("dve", ("tensor_tensor", dict(out=Hb[h][:], in0=PS_B[:, 256:384], in1=MKR[h][:], op=ALU.add)), reads=[RPS_B, RMKR[h]], writes=[RHb[h]])
                        tcol = ci * C + totc
                        S.op("dve", ("scalar_tensor_tensor", dict(out=Pb[hs, :], in0=identf[hs, hs], scalar=e3[hs, tcol:tcol + 1], in1=PS_B[hs, 384:448], op0=ALU.mult, op1=ALU.add)), reads=[RPS_B, Ridf, Re3], writes=[RPb[h]])
                        S.op("dve", ("tensor_tensor", dict(out=Zb[h][:], in0=PS_B[:, 448:512], in1=TT[:, 3, hs], op=ALU.add)), reads=[RPS_B, RTT], writes=[RZb[h]])
                        if STOP <= 8 and h == 1:
                            return early(r0)
                        yc0 = 64 + 192 * h
                        sc0 = 192 * h
                        mm(PS_S[0:64, yc0:yc0 + 128], ST[hs, :], Gb[hs, :], [RST[h], RGb[h]], [RPS_S], st=True, sp=False)
                        mm(PS_S[0:64, yc0:yc0 + 128], TT[:, 0, hs], Hb[h][:], [RTT, RHb[h]], [RPS_S], st=False, sp=True)
                        mm(PS_S[0:64, sc0:sc0 + 64], Pb[hs, :], ST[hs, :], [RPb[h], RST[h]], [RPS_S], st=True, sp=False)
                        mm(PS_S[0:64, sc0:sc0 + 64], Zb[h][:], TT[:, 0, hs], [RZb[h], RTT], [RPS_S], st=False, sp=True)
                        ysl = ysum[hs, t0 + ci * C: t0 + (ci + 1) * C]
                        if d == 0:
                            S.op("act", ("activation", dict(out=ysl, in_=PS_S[0:64, yc0:yc0 + 128], func=AF.Copy)), reads=[RPS_S], writes=[Rys[gci]])
                        else:
                            S.op("act", ("activation", dict(out=tmp[hs, 0:128], in_=PS_S[0:64, yc0:yc0 + 128], func=AF.Copy)), reads=[RPS_S], writes=[Rtmp])
                            S.op("dve", ("tensor_tensor", dict(out=ysl, in0=tmp[hs, 0:128], in1=ysl, op=ALU.add)), reads=[Rtmp, Rys[gci]], writes=[Rys[gci]])
                        S.op("act", ("activation", dict(out=ST[hs, :], in_=PS_S[0:64, sc0:sc0 + 64], func=AF.Copy)), reads=[RPS_S], writes=[RST[h]])
        for blk in range(nblk):
            t0 = blk * BLK
            ysl = ysum[:, t0:t0 + BLK]
            Rin = Rys[t0 // C: (t0 + BLK) // C]
            S.op("pe", ("matmul", dict(out=PS_P1[:], lhsT=bdm[:], rhs=ysl, start=True, stop=True)), reads=[Rbdm] + Rin, writes=[RPS_P1])
            S.op("dve", ("tensor_tensor", dict(out=fin1[:], in0=ysl, in1=PS_P1[:], op=ALU.subtract)), reads=[RPS_P1] + Rin, writes=[Rfin1])
            S.op("pool", ("tensor_tensor", dict(out=fin2[:], in0=fin1[:], in1=fin1[:], op=ALU.mult)), reads=[Rfin1], writes=[Rfin2])
            S.op("pe", ("matmul", dict(out=PS_P2[:], lhsT=bdm[:], rhs=fin2[:], start=True, stop=True)), reads=[Rbdm, Rfin2], writes=[RPS_P2])
            S.op("dve", ("tensor_scalar", dict(out=fin3[:], in0=PS_P2[:], scalar1=GN_EPS, scalar2=None, op0=ALU.add)), reads=[RPS_P2], writes=[Rfin3])
            S.op("act", ("activation", dict(out=fin3[:], in_=fin3[:], func=AF.Sqrt)), reads=[Rfin3], writes=[Rfin3])
            S.op("dve", ("reciprocal", dict(out=fin3[:], in_=fin3[:])), reads=[Rfin3], writes=[Rfin3])
            S.op("dve", ("tensor_tensor", dict(out=fin1[:], in0=fin1[:], in1=fin3[:], op=ALU.mult)), reads=[Rfin1, Rfin3], writes=[Rfin1])
            S.op("dve", ("tensor_scalar", dict(out=fin2[:], in0=fin1[:], scalar1=PM(15), scalar2=PM(16), op0=ALU.mult, op1=ALU.add)), reads=[Rfin1, Rprm], writes=[Rfin2])
            S.op("dve", ("tensor_tensor", dict(out=fin2[:], in0=fin2[:], in1=bsum[:, t0:t0 + BLK], op=ALU.add)), reads=[Rfin2, Rbs[blk]], writes=[Rfin2])
            out_toks.append(S.dma(("dma_start", dict(out=yout[:, b, t0:t0 + BLK], in_=fin2[:])), reads=[Rfin2]))
    return out_toks


NT = 2048
NTH = NT + 2


def emit_p1(S, nc, I, O):
    sb = lambda name, shape, dt=F32: nc.alloc_sbuf_tensor(name, shape, dt)
    ps = lambda name, shape, dt=F32: nc.alloc_psum_tensor(name, shape, dt)
    R = Region
    op = S.op
    mm = lambda out, l, r_, rd, wr, st=True, sp=True: op("pe", ("matmul", dict(out=out, lhsT=l, rhs=r_, start=st, stop=sp)), reads=rd, writes=wr)

    identf = sb("identf", [128, 128]); identb = sb("identb", [128, 128], BF16); Rid = R()
    gE = sb("gE", [128, 8, 1]); gO = sb("gO", [128, 8, 1]); Rg = R()
    S.dma(("dma_start", dict(out=identf[:], in_=I["c_ident"])), writes=[Rid])
    op("dve", ("tensor_copy", dict(out=identb[:], in_=identf[:])), reads=[Rid], writes=[Rid])
    S.dma(("dma_start", dict(out=gE[:, :, 0], in_=I["e_norm_g"].rearrange("(k p) -> p k", p=128), allow_slow_non_contiguous=True)), writes=[Rg])
    S.dma(("dma_start", dict(out=gO[:, :, 0], in_=I["o_norm_g"].rearrange("(k p) -> p k", p=128), allow_slow_non_contiguous=True)), writes=[Rg])
    hnT = sb("hnT", [128, 8, NTH], BF16); RhnT = [R() for _ in range(18)]
    yT = nc.dram_tensor("yT_d", [16, 128, NT], BF16).ap(); RyT = [[R() for _ in range(4)] for _ in range(16)]
    U = sb("U", [128, 4096]); RU = R()
    xt = [sb("xt0", [128, 1024])] * 2; Rxt = [R()] * 2
    yo = [sb("yo%d" % i, [128, 512], BF16) for i in range(2)]; Ryo = [R(), R()]
    ytl = [sb("ytl%d" % i, [128, 16, 128], BF16) for i in range(2)]; Rytl = [R(), R()]
    xn = sb("xn", [128, 1024], BF16); Rxn = R()
    sq = sb("sq", [128, 1024]); Rsq = R()
    st = sb("st", [128, 8]); Rst = R()
    stg = [sb("stg%d" % i, [128, 8, 256]) for i in range(2)]; Rstg = [R(), R()]
    wbf = [sb("wbf%d" % i, [128, 8, 512], BF16) for i in range(2)]; Rwbf = [R(), R()]
    wbig = sb("wbig", [128, 16, 1024], BF16); Rwbig = R()
    t1 = sb("t1", [128, 512]); Rt1 = R()
    t2 = sb("t2", [128, 512]); Rt2 = R()
    t3 = sb("t3", [128, 512]); Rt3 = R()
    cw = sb("cw", [128, 8, 3]); Rcw = R()
    PS_a = ps("PS_a", [128, 512]); RPa = R()
    PS_b = ps("PS_b", [128, 512]); RPb = R()
    PS_c = ps("PS_c", [128, 512]); RPc = R()
    PS_d = ps("PS_d", [128, 512]); RPd = R()
    PS_t = ps("PS_t", [128, 8, 128], BF16); RPt = R()
    PS_m = ps("PS_m", [128, 8, 128]); RPm = R()
    for j_ in range(3):
        S.dma(("dma_start", dict(out=cw[:, :, j_], in_=I["e_conv_w"][j_].rearrange("(cb p) -> p cb", p=128), allow_slow_non_contiguous=True)), writes=[Rcw])

    def norm_tile(xtile, Rx, gt, dst_fn, Rdst, nvalid=128):
        op("act", ("activation", dict(out=sq[:], in_=xtile[:], func=AF.Square)), reads=[Rx], writes=[Rsq])
        op("dve", ("reduce_sum", dict(out=st[:, 0:1], in_=sq[:], axis=AX.X)), reads=[Rsq], writes=[Rst])
        op("dve", ("tensor_scalar", dict(out=st[:, 1:2], in0=st[:, 0:1], scalar1=1.0 / 1024, scalar2=1e-6, op0=ALU.mult, op1=ALU.add)), reads=[Rst], writes=[Rst])
        op("act", ("activation", dict(out=st[:, 2:3], in_=st[:, 1:2], func=AF.Sqrt)), reads=[Rst], writes=[Rst])
        op("dve", ("reciprocal", dict(out=st[:, 3:4], in_=st[:, 2:3])), reads=[Rst], writes=[Rst])
        op("dve", ("tensor_scalar", dict(out=xn[:], in0=xtile[:], scalar1=st[:, 3:4], scalar2=None, op0=ALU.mult)), reads=[Rx, Rst], writes=[Rxn])
        for k in range(8):
            op("pe", ("transpose", dict(out=PS_t[:, k, :], in_=xn[:, k * 128:(k + 1) * 128], identity=identb[:])), reads=[Rxn, Rid], writes=[RPt])
        dst_fn(gt)

    xh = I["xh"]
    for i in range(17):
        xb, Rx = xt[i % 2], Rxt[i % 2]
        if i < 16:
            S.dma(("dma_start", dict(out=xb[:], in_=xh[1 + 128 * i: 1 + 128 * (i + 1), :])), writes=[Rx])
            def dst(gt, i=i):
                op("dve", ("tensor_tensor", dict(out=hnT[:, :, 1 + 128 * i: 1 + 128 * (i + 1)], in0=PS_t[:], in1=gt[:].to_broadcast([128, 8, 128]), op=ALU.mult)), reads=[RPt, Rg], writes=[RhnT[i]])
        else:
            op("pool", ("memset", dict(ap=xb[:], constant=0.0)), writes=[Rx])
            S.dma(("dma_start", dict(out=xb[0:1, :], in_=xh[0:1, :])), writes=[Rx])
            S.dma(("dma_start", dict(out=xb[1:2, :], in_=xh[NT + 1:NT + 2, :])), writes=[Rx])
            def dst(gt):
                op("dve", ("tensor_tensor", dict(out=hnT[:, :, 0:1], in0=PS_t[:, :, 0:1], in1=gt[:], op=ALU.mult)), reads=[RPt, Rg], writes=[RhnT[16]])
                op("dve", ("tensor_tensor", dict(out=hnT[:, :, NT + 1:NT + 2], in0=PS_t[:, :, 1:2], in1=gt[:], op=ALU.mult)), reads=[RPt, Rg], writes=[RhnT[17]])
        norm_tile(xb, Rx, gE, dst, None)
    allh = RhnT

    wi = I["e_w_in"].rearrange("(k p) (s c) -> p k s c", p=128, c=1024)

    def load_w(buf, src4, nsp):
        for s_ in range(nsp):
            sb_ = s_ % 2
            S.dma(("dma_start", dict(out=stg[sb_][:, :, 0:128], in_=src4[:, :, s_, :])), writes=[Rstg[sb_]])
            op("pool", ("tensor_copy", dict(out=wbf[buf][:, :, s_ * 128:(s_ + 1) * 128], in_=stg[sb_][:, :, 0:128])), reads=[Rstg[sb_]], writes=[Rwbf[buf]])
        return wbf[buf][:, :, 0:nsp * 128].rearrange("p k (s c) -> p k s c", c=128)

    xc = U[:, 0:NTH]
    chunksA = [(0, 512), (512, 512), (1024, 512), (1536, 512), (2048, 2)]
    for cb in range(8):
        w4 = load_w(cb % 2, wi[:, :, 0:4, cb * 128:(cb + 1) * 128], 4)
        Rw = Rwbf[cb % 2]
        for (c0, n) in chunksA:
            for k in range(8):
                mm(PS_a[:, 0:n], w4[:, k, 0, :], hnT[:, k, c0:c0 + n], [Rw] + allh, [RPa], st=(k == 0), sp=(k == 7))
            for k in range(8):
                mm(PS_b[:, 0:n], w4[:, k, 2, :], hnT[:, k, c0:c0 + n], [Rw] + allh, [RPb], st=(k == 0), sp=(k == 7))
            op("act", ("activation", dict(out=t1[:, 0:n], in_=PS_a[:, 0:n], func=AF.Copy)), reads=[RPa], writes=[Rt1])
            op("dve", ("tensor_tensor", dict(out=xc[:, c0:c0 + n], in0=PS_b[:, 0:n], in1=t1[:, 0:n], op=ALU.mult)), reads=[RPb, Rt1], writes=[RU])
        for j in range(4):
            c0 = 1 + 512 * j
            for k in range(8):
                mm(PS_c[:], w4[:, k, 1, :], hnT[:, k, c0:c0 + 512], [Rw] + allh, [RPc], st=(k == 0), sp=(k == 7))
            for k in range(8):
                mm(PS_d[:], w4[:, k, 3, :], hnT[:, k, c0:c0 + 512], [Rw] + allh, [RPd], st=(k == 0), sp=(k == 7))
            op("dve", ("tensor_scalar", dict(out=t2[:], in0=xc[:, c0 - 1:c0 + 511], scalar1=cw[:, cb, 0:1], scalar2=None, op0=ALU.mult)), reads=[RU, Rcw], writes=[Rt2])
            op("dve", ("scalar_tensor_tensor", dict(out=t2[:], in0=xc[:, c0:c0 + 512], scalar=cw[:, cb, 1:2], in1=t2[:], op0=ALU.mult, op1=ALU.add)), reads=[RU, Rcw, Rt2], writes=[Rt2])
            op("dve", ("scalar_tensor_tensor", dict(out=t2[:], in0=xc[:, c0 + 1:c0 + 513], scalar=cw[:, cb, 2:3], in1=t2[:], op0=ALU.mult, op1=ALU.add)), reads=[RU, Rcw, Rt2], writes=[Rt2])
            op("act", ("activation", dict(out=t3[:], in_=PS_d[:], func=AF.Silu)), reads=[RPd], writes=[Rt3])
            op("dve", ("tensor_tensor", dict(out=t2[:], in0=PS_c[:], in1=t2[:], op=ALU.mult)), reads=[RPc, Rt2], writes=[Rt2])
            op("pool", ("tensor_tensor", dict(out=yo[j % 2][:], in0=t2[:], in1=t3[:], op=ALU.mult)), reads=[Rt2, Rt3], writes=[Ryo[j % 2]])
            S.dma(("dma_start", dict(out=yT[cb, :, 512 * j:512 * (j + 1)], in_=yo[j % 2][:])), reads=[Ryo[j % 2]], writes=[RyT[cb][j]])

    for hf in range(4):
        S.dma(("dma_start", dict(out=stg[hf % 2][:], in_=wi[:, :, 5, hf * 256:(hf + 1) * 256])), writes=[Rstg[hf % 2]])
        op("pool", ("tensor_copy", dict(out=wbig[:, 0:8, hf * 256:(hf + 1) * 256], in_=stg[hf % 2][:])), reads=[Rstg[hf % 2]], writes=[Rwbig])
    wsn = sb("wsn", [128, 8, 128]); wsnb = sb("wsnb", [128, 8, 128], BF16); wsT = sb("wsT", [128, 8, 128], BF16); Rws = R()
    S.dma(("dma_start", dict(out=wsn[:], in_=I["e_sgu_w"].rearrange("g i j -> i g j"))), writes=[Rws])
    op("dve", ("tensor_copy", dict(out=wsnb[:], in_=wsn[:])), reads=[Rws], writes=[Rws])
    for g in range(8):
        op("pe", ("transpose", dict(out=PS_t[:, g, :], in_=wsnb[:, g, :], identity=identb[:])), reads=[Rws, Rid], writes=[RPt])
    op("act", ("activation", dict(out=wsT[:], in_=PS_t[:], func=AF.Copy)), reads=[RPt], writes=[Rws])
    bsB = sb("bsB", [128, 8, 128]); lnG = sb("lnG", [128, 1024]); lnB = sb("lnB", [128, 1024]); Rbc = R()
    S.dma(("dma_start", dict(out=bsB[:].rearrange("p g i -> p (g i)"), in_=I["e_sgu_b"].rearrange("g i -> (g i)").partition_broadcast(128))), writes=[Rbc])
    S.dma(("dma_start", dict(out=lnG[:], in_=I["e_sgu_ln_g"].partition_broadcast(128))), writes=[Rbc])
    S.dma(("dma_start", dict(out=lnB[:], in_=I["e_sgu_ln_b"].partition_broadcast(128))), writes=[Rbc])
    vsb = sb("vsb", [128, 1024]); Rvsb = R()
    vnb = sb("vnb", [128, 1024], BF16); Rvnb = R()
    mixall = U[:, 0:4096].rearrange("p (g t) -> p g t", g=8)
    for tg in range(4):
        for ti in range(4):
            c0 = 1 + 128 * (4 * tg + ti)
            for hf, (P_, RP_) in enumerate(((PS_a, RPa), (PS_b, RPb))):
                for k in range(8):
                    mm(P_[:], hnT[:, k, c0:c0 + 128], wbig[:, k, hf * 512:(hf + 1) * 512], [Rwbig] + allh, [RP_], st=(k == 0), sp=(k == 7))
                op("act", ("activation", dict(out=vsb[:, hf * 512:(hf + 1) * 512], in_=P_[:], func=AF.Copy)), reads=[RP_], writes=[Rvsb])
            op("act", ("activation", dict(out=sq[:], in_=vsb[:], func=AF.Square)), reads=[Rvsb], writes=[Rsq])
            op("dve", ("reduce_sum", dict(out=st[:, 0:1], in_=vsb[:], axis=AX.X)), reads=[Rvsb], writes=[Rst])
            op("dve", ("reduce_sum", dict(out=st[:, 1:2], in_=sq[:], axis=AX.X)), reads=[Rsq], writes=[Rst])
            op("dve", ("tensor_scalar", dict(out=st[:, 2:3], in0=st[:, 0:1], scalar1=1.0 / 1024, scalar2=None, op0=ALU.mult)), reads=[Rst], writes=[Rst])
            op("dve", ("tensor_tensor", dict(out=st[:, 3:4], in0=st[:, 2:3], in1=st[:, 2:3], op=ALU.mult)), reads=[Rst], writes=[Rst])
            op("dve", ("scalar_tensor_tensor", dict(out=st[:, 4:5], in0=st[:, 1:2], scalar=1.0 / 1024, in1=st[:, 3:4], op0=ALU.mult, op1=ALU.subtract)), reads=[Rst], writes=[Rst])
            op("dve", ("tensor_scalar", dict(out=st[:, 4:5], in0=st[:, 4:5], scalar1=1e-5, scalar2=None, op0=ALU.add)), reads=[Rst], writes=[Rst])
            op("act", ("activation", dict(out=st[:, 5:6], in_=st[:, 4:5], func=AF.Sqrt)), reads=[Rst], writes=[Rst])
            op("dve", ("reciprocal", dict(out=st[:, 6:7], in_=st[:, 5:6])), reads=[Rst], writes=[Rst])
            op("dve", ("tensor_scalar", dict(out=vsb[:], in0=vsb[:], scalar1=st[:, 2:3], scalar2=st[:, 6:7], op0=ALU.subtract, op1=ALU.mult)), reads=[Rvsb, Rst], writes=[Rvsb])
            op("dve", ("tensor_tensor", dict(out=vsb[:], in0=vsb[:], in1=lnG[:], op=ALU.mult)), reads=[Rvsb, Rbc], writes=[Rvsb])
            op("pool", ("tensor_tensor", dict(out=vnb[:], in0=vsb[:], in1=lnB[:], op=ALU.add)), reads=[Rvsb, Rbc], writes=[Rvnb])
            for g in range(8):
                mm(PS_m[:, g, :], vnb[:, g * 128:(g + 1) * 128], wsT[:, g, :], [Rvnb, Rws], [RPm])
            op("dve", ("tensor_tensor", dict(out=mixall[:, :, ti * 128:(ti + 1) * 128], in0=PS_m[:], in1=bsB[:], op=ALU.add)), reads=[RPm, Rbc], writes=[RU])
        c0 = 1 + 512 * tg
        for g in range(8):
            buf = g % 2
            S.dma(("dma_start", dict(out=stg[buf][:, :, 0:128], in_=wi[:, :, 4, g * 128:(g + 1) * 128])), writes=[Rstg[buf]])
            S.dma(("dma_start", dict(out=stg[buf][:, :, 128:256], in_=wi[:, :, 6, g * 128:(g + 1) * 128])), writes=[Rstg[buf]])
            op("pool", ("tensor_copy", dict(out=wbf[buf][:, :, 0:256], in_=stg[buf][:, :, 0:256])), reads=[Rstg[buf]], writes=[Rwbf[buf]])
            for k in range(8):
                mm(PS_c[:], wbf[buf][:, k, 0:128], hnT[:, k, c0:c0 + 512], [Rwbf[buf]] + allh, [RPc], st=(k == 0), sp=(k == 7))
            for k in range(8):
                mm(PS_d[:], wbf[buf][:, k, 128:256], hnT[:, k, c0:c0 + 512], [Rwbf[buf]] + allh, [RPd], st=(k == 0), sp=(k == 7))
            op("act", ("activation", dict(out=t3[:], in_=PS_d[:], func=AF.Silu)), reads=[RPd], writes=[Rt3])
            op("dve", ("tensor_tensor", dict(out=t2[:], in0=PS_c[:], in1=mixall[:, g, :], op=ALU.mult)), reads=[RPc, RU], writes=[Rt2])
            op("pool", ("tensor_tensor", dict(out=yo[g % 2][:], in0=t2[:], in1=t3[:], op=ALU.mult)), reads=[Rt2, Rt3], writes=[Ryo[g % 2]])
            S.dma(("dma_start", dict(out=yT[8 + g, :, 512 * tg:512 * (tg + 1)], in_=yo[g % 2][:])), reads=[Ryo[g % 2]], writes=[RyT[8 + g][tg]])

    wo = I["e_w_out"].rearrange("(k p) n -> p k n", p=128)
    for q in range(2):
        for hf in range(4):
            S.dma(("dma_start", dict(out=stg[hf % 2][:], in_=wo[:, 8 * q:8 * q + 8, hf * 256:(hf + 1) * 256])), writes=[Rstg[hf % 2]])
            op("pool", ("tensor_copy", dict(out=wbig[:, 8 * q:8 * q + 8, hf * 256:(hf + 1) * 256], in_=stg[hf % 2][:])), reads=[Rstg[hf % 2]], writes=[Rwbig])
    ally = [r for row in RyT for r in row]
    h1t = sb("h1t", [128, 1024]); Rh1 = R()
    for i in range(16):
        xb, Rx = xt[i % 2], Rxt[i % 2]
        S.dma(("dma_start", dict(out=xb[:], in_=xh[1 + 128 * i: 1 + 128 * (i + 1), :])), writes=[Rx])
        S.dma(("dma_start", dict(out=ytl[i % 2][:], in_=yT[:, :, 128 * i:128 * (i + 1)].rearrange("k p t -> p k t"))), reads=ally, writes=[Rytl[i % 2]])
        for hf, (P_, RP_) in enumerate(((PS_a, RPa), (PS_b, RPb))):
            for k in range(16):
                mm(P_[:], ytl[i % 2][:, k, :], wbig[:, k, hf * 512:(hf + 1) * 512], [Rwbig, Rytl[i % 2]], [RP_], st=(k == 0), sp=(k == 15))
            op("dve", ("tensor_tensor", dict(out=h1t[:, hf * 512:(hf + 1) * 512], in0=P_[:], in1=xb[:, hf * 512:(hf + 1) * 512], op=ALU.add)), reads=[RP_, Rx], writes=[Rh1])
        S.dma(("dma_start", dict(out=O["h1"][128 * i:128 * (i + 1), :], in_=h1t[:])), reads=[Rh1])

        def dst(gt, i=i):
            op("dve", ("tensor_tensor", dict(out=hnT[:, :, 1 + 128 * i: 1 + 128 * (i + 1)], in0=PS_t[:], in1=gt[:].to_broadcast([128, 8, 128]), op=ALU.mult)), reads=[RPt, Rg], writes=[RhnT[i]])
        norm_tile(h1t, Rh1, gO, dst, None)

    wi1 = I["o_w_in"].rearrange("(k p) n -> p k n", p=128)
    blocks = [(c * 128, c * 128, False) for c in range(25)]
    blocks += [(3200 + c * 128, 3200 + c * 128, True) for c in range(8)]
    blocks += [(4736 + c * 128, 4224 + c * 128, True) for c in range(4)]
    ob = [sb("ob%d" % i, [128, 512]) for i in range(2)]; Rob = [R(), R()]
    oi = 0
    toks = []
    for bi, (sc, dr, act) in enumerate(blocks):
        buf = bi % 2
        S.dma(("dma_start", dict(out=stg[buf][:, :, 0:128], in_=wi1[:, :, sc:sc + 128])), writes=[Rstg[buf]])
        op("pool", ("tensor_copy", dict(out=wbf[buf][:, :, 0:128], in_=stg[buf][:, :, 0:128])), reads=[Rstg[buf]], writes=[Rwbf[buf]])
        for j in range(4):
            P_, RP_ = ((PS_a, RPa), (PS_b, RPb), (PS_c, RPc), (PS_d, RPd))[j]
            c0 = 1 + 512 * j
            for k in range(8):
                mm(P_[:], wbf[buf][:, k, 0:128], hnT[:, k, c0:c0 + 512], [Rwbf[buf]] + allh, [RP_], st=(k == 0), sp=(k == 7))
            o_, Ro = ob[oi % 2], Rob[oi % 2]
            oi += 1
            op("act", ("activation", dict(out=o_[:], in_=P_[:], func=(AF.Silu if act else AF.Copy))), reads=[RP_], writes=[Ro])
            toks.append(S.dma(("dma_start", dict(out=O["pT"][dr:dr + 128, 512 * j:512 * (j + 1)], in_=o_[:])), reads=[Ro]))
    for hf in range(2):
        S.dma(("dma_start", dict(out=stg[hf][:], in_=wi1[:, :, 4224 + 256 * hf:4224 + 256 * (hf + 1)])), writes=[Rstg[hf]])
        op("pool", ("tensor_copy", dict(out=wbf[0][:, :, 256 * hf:256 * (hf + 1)], in_=stg[hf][:])), reads=[Rstg[hf]], writes=[Rwbf[0]])
    for i in range(16):
        c0 = 1 + 128 * i
        P_, RP_ = ((PS_a, RPa), (PS_b, RPb))[i % 2]
        for k in range(8):
            mm(P_[:], hnT[:, k, c0:c0 + 128], wbf[0][:, k, :], [Rwbf[0]] + allh, [RP_], st=(k == 0), sp=(k == 7))
        o_, Ro = ob[oi % 2], Rob[oi % 2]
        oi += 1
        op("act", ("activation", dict(out=o_[:], in_=P_[:], func=AF.Copy)), reads=[RP_], writes=[Ro])
        toks.append(S.dma(("dma_start", dict(out=O["fd"][128 * i:128 * (i + 1), :], in_=o_[:])), reads=[Ro]))
    return toks


def full_barrier(S):
    keys = list(S.cnt.items())
    for e in S.ENGS:
        waits = []
        for k, v in keys:
            if k == e:
                continue
            if S.seen[e].get(k, 0) < v:
                S.seen[e][k] = v
                waits.append((k, v))
        if waits:
            S.prog[e].append([waits, None, ("_none", 0)])


def emit_fnet(S, nc, I, ydT):
    R = Region
    op = S.op
    mm = lambda out, l, r_, rd, wr, st=True, sp=True: op("pe", ("matmul", dict(out=out, lhsT=l, rhs=r_, start=st, stop=sp)), reads=rd, writes=wr)
    toks = []
    with ExitStack() as es:
        sb = lambda name, shape, dt=F32: es.enter_context(nc.sbuf_tensor(name, shape, dt))
        ps = lambda name, shape, dt=F32: es.enter_context(nc.psum_tensor(name, shape, dt))
        xs = sb("f_xs", [128, 4096]); Rxs = R()
        xb = sb("f_xb", [128, 64, 128], BF16); Rxb = R()
        Fb = sb("f_F", [128, 256], BF16); RF = R()
        A_sb = sb("f_A", [64, 128, 256], BF16); RA = R()
        PQ = sb("f_PQ", [128, 2, 64, 128], BF16); RPQ = R()
        Tg = [[sb("f_T%d%d" % (i, j), [64, 16, 128], BF16) for j in range(2)] for i in range(2)]; RTg = [R(), R()]
        wf32 = sb("f_w32", [128, 128]); wfb = sb("f_wb", [128, 128], BF16); Rwf = R()
        Ccb = sb("f_Cc", [128, 128], BF16); mScb = sb("f_mSc", [128, 128], BF16); Rcs = R()
        Gb = sb("f_G", [128, 256], BF16); RG = R()
        ob = [sb("f_ob%d" % i, [128, 512]) for i in range(2)]; Rob = [R(), R()]
        PS = [ps("f_ps%d" % i, [128, 512]) for i in range(2)]; RPS = [R(), R()]
        S.dma(("dma_start", dict(out=Fb[:], in_=I["c_F"])), writes=[RF])
        S.dma(("dma_start", dict(out=Ccb[:], in_=I["c_Cc"])), writes=[Rcs])
        S.dma(("dma_start", dict(out=mScb[:], in_=I["c_mSc"])), writes=[Rcs])
        S.dma(("dma_start", dict(out=wf32[:], in_=I["fw"])), writes=[Rwf])
        op("dve", ("tensor_copy", dict(out=wfb[:], in_=wf32[:])), reads=[Rwf], writes=[Rwf])
        xbf = xb[:].rearrange("p l c -> p (l c)")
        for hf in range(2):
            S.dma(("dma_start", dict(out=xs[:], in_=I["fx"][:, hf * 4096:(hf + 1) * 4096])), writes=[Rxs])
            op("pool", ("tensor_copy", dict(out=xbf[:, hf * 4096:(hf + 1) * 4096], in_=xs[:])), reads=[Rxs], writes=[Rxb])
        for c2 in range(64):
            P_, RP_ = PS[c2 % 2], RPS[c2 % 2]
            for j in range(2):
                mm(P_[0:64, j * 256:(j + 1) * 256], xb[:, :, 2 * c2 + j], Fb[:], [Rxb, RF], [RP_])
            op("act" if c2 % 2 == 0 else "dve", ("activation", dict(out=A_sb[0:64, 2 * c2:2 * c2 + 2, :], in_=P_[0:64, :].rearrange("p (j k) -> p j k", j=2), func=AF.Copy)) if c2 % 2 == 0 else
               ("tensor_copy", dict(out=A_sb[0:64, 2 * c2:2 * c2 + 2, :], in_=P_[0:64, :].rearrange("p (j k) -> p j k", j=2))), reads=[RP_], writes=[RA])
        T1d = I["c_T1"].rearrange("p (k h) -> p k h", h=128)
        T2d = I["c_T2"].rearrange("p (k h) -> p k h", h=128)
        ei = 0
        for grp in range(8):
            tb = grp % 2
            S.dma(("dma_start", dict(out=Tg[tb][0][:], in_=T1d[:, grp * 16:(grp + 1) * 16, :])), writes=[RTg[tb]])
            S.dma(("dma_start", dict(out=Tg[tb][1][:], in_=T2d[:, grp * 16:(grp + 1) * 16, :])), writes=[RTg[tb]])
            for q in range(4):
                P_, RP_ = PS[ei % 2], RPS[ei % 2]
                for j in range(4):
                    kk_ = q * 4 + j
                    kl = grp * 16 + kk_
                    mm(P_[:, j * 128:(j + 1) * 128], A_sb[0:64, :, kl], Tg[tb][0][0:64, kk_, :], [RA, RTg[tb]], [RP_], st=True, sp=False)
                    mm(P_[:, j * 128:(j + 1) * 128], A_sb[0:64, :, 128 + kl], Tg[tb][1][0:64, kk_, :], [RA, RTg[tb]], [RP_], st=False, sp=True)
                kl0 = grp * 16 + q * 4
                for qq in range(2):
                    op("act" if qq == 0 else "dve",
                       ("activation", dict(out=PQ[:, qq, :, kl0:kl0 + 4].rearrange("p h l -> p l h"), in_=P_[:].rearrange("p (l q h) -> p l q h", l=4, q=2)[:, :, qq, :], func=AF.Copy)) if qq == 0 else
                       ("tensor_copy", dict(out=PQ[:, qq, :, kl0:kl0 + 4].rearrange("p h l -> p l h"), in_=P_[:].rearrange("p (l q h) -> p l q h", l=4, q=2)[:, :, qq, :])),
                       reads=[RP_], writes=[RPQ])
                ei += 1
        P_, RP_ = PS[0], RPS[0]
        mm(P_[:, 0:128], Ccb[:], wfb[:], [Rcs, Rwf], [RP_])
        mm(P_[:, 128:256], mScb[:], wfb[:], [Rcs, Rwf], [RP_])
        op("act", ("activation", dict(out=Gb[:], in_=P_[:, 0:256], func=AF.Copy)), reads=[RP_], writes=[RG])
        for t4 in range(16):
            P_, RP_ = PS[(t4 + 1) % 2], RPS[(t4 + 1) % 2]
            for j in range(4):
                kh = 4 * t4 + j
                mm(P_[:, j * 128:(j + 1) * 128], Gb[:, 0:128], PQ[:, 0, kh, :], [RG, RPQ], [RP_], st=True, sp=False)
                mm(P_[:, j * 128:(j + 1) * 128], Gb[:, 128:256], PQ[:, 1, kh, :], [RG, RPQ], [RP_], st=False, sp=True)
            o_, Ro = ob[t4 % 2], Rob[t4 % 2]
            op("act", ("activation", dict(out=o_[:], in_=P_[:], func=AF.Copy)), reads=[RP_], writes=[Ro])
            toks.append(S.dma(("dma_start", dict(out=ydT[:, 512 * t4:512 * (t4 + 1)], in_=o_[:])), reads=[Ro]))
    full_barrier(S)
    return toks


def emit_p3(S, nc, I, yout):
    sb = lambda name, shape, dt=F32: nc.alloc_sbuf_tensor(name, shape, dt)
    ps = lambda name, shape, dt=F32: nc.alloc_psum_tensor(name, shape, dt)
    R = Region
    op = S.op
    mm = lambda out, l, r_, rd, wr, st=True, sp=True: op("pe", ("matmul", dict(out=out, lhsT=l, rhs=r_, start=st, stop=sp)), reads=rd, writes=wr)
    stg = [sb("stg%d" % i, [128, 8, 256]) for i in range(2)]; Rstg = [R(), R()]
    wO = sb("wO", [128, 12, 1024], BF16); RwO = R()
    gN = sb("gN", [128, 1024]); RgN = R()
    gt_all = sb("gt_all", [128, 12, 2048], BF16); Rgt = R()
    ya = [sb("ya%d" % i, [128, 512]) for i in range(2)]; Rya = [R(), R()]
    ga = [sb("ga%d" % i, [128, 512]) for i in range(2)]; Rga = [R(), R()]
    h1t = [sb("h1t%d" % i, [128, 1024]) for i in range(2)]; Rh1 = [R(), R()]
    h2 = sb("h2", [128, 1024]); Rh2 = R()
    sq = sb("sq", [128, 1024]); Rsq = R()
    st = sb("st", [128, 8]); Rst = R()
    yo = [sb("yo%d" % i, [128, 1024]) for i in range(2)]; Ryo = [R(), R()]
    PS_a = ps("PS_a", [128, 512]); RPa = R()
    PS_b = ps("PS_b", [128, 512]); RPb = R()
    wo3 = I["o_w_out"].rearrange("(k p) n -> p k n", p=128)
    si = 0
    for (k0, nk) in ((0, 8), (8, 4)):
        for cq in range(4):
            b_ = si % 2; si += 1
            S.dma(("dma_start", dict(out=stg[b_][:, 0:nk, :], in_=wo3[:, k0:k0 + nk, cq * 256:(cq + 1) * 256])), writes=[Rstg[b_]])
            op("pool", ("tensor_copy", dict(out=wO[:, k0:k0 + nk, cq * 256:(cq + 1) * 256], in_=stg[b_][:, 0:nk, :])), reads=[Rstg[b_]], writes=[RwO])
    S.dma(("dma_start", dict(out=gN[:], in_=I["final_norm_g"].partition_broadcast(128))), writes=[RgN])
    ii = 0
    for blk in range(12):
        src = I["ycT"][blk * 128:(blk + 1) * 128] if blk < 8 else I["ydT"][(blk - 8) * 128:(blk - 7) * 128]
        gsrc = I["gT"][blk * 128:(blk + 1) * 128]
        for j in range(4):
            b_ = ii % 2; ii += 1
            S.dma(("dma_start", dict(out=ya[b_][:], in_=src[:, 512 * j:512 * (j + 1)])), writes=[Rya[b_]])
            S.dma(("dma_start", dict(out=ga[b_][:], in_=gsrc[:, 512 * j:512 * (j + 1)])), writes=[Rga[b_]])
            op("dve" if ii % 2 else "pool", ("tensor_tensor", dict(out=gt_all[:, blk, 512 * j:512 * (j + 1)], in0=ya[b_][:], in1=ga[b_][:], op=ALU.mult)), reads=[Rya[b_], Rga[b_]], writes=[Rgt])
    toks = []
    for i in range(16):
        hb, Rh = h1t[i % 2], Rh1[i % 2]
        S.dma(("dma_start", dict(out=hb[:], in_=I["h1"][128 * i:128 * (i + 1), :])), writes=[Rh])
        for hf, (P_, RP_) in enumerate(((PS_a, RPa), (PS_b, RPb))):
            for k in range(12):
                mm(P_[:], gt_all[:, k, 128 * i:128 * (i + 1)], wO[:, k, hf * 512:(hf + 1) * 512], [Rgt, RwO], [RP_], st=(k == 0), sp=(k == 11))
            op("dve", ("tensor_tensor", dict(out=h2[:, hf * 512:(hf + 1) * 512], in0=P_[:], in1=hb[:, hf * 512:(hf + 1) * 512], op=ALU.add)), reads=[RP_, Rh], writes=[Rh2])
        op("act", ("activation", dict(out=sq[:], in_=h2[:], func=AF.Square)), reads=[Rh2], writes=[Rsq])
        op("dve", ("reduce_sum", dict(out=st[:, 0:1], in_=sq[:], axis=AX.X)), reads=[Rsq], writes=[Rst])
        op("dve", ("tensor_scalar", dict(out=st[:, 1:2], in0=st[:, 0:1], scalar1=1.0 / 1024, scalar2=1e-6, op0=ALU.mult, op1=ALU.add)), reads=[Rst], writes=[Rst])
        op("act", ("activation", dict(out=st[:, 2:3], in_=st[:, 1:2], func=AF.Sqrt)), reads=[Rst], writes=[Rst])
        op("dve", ("reciprocal", dict(out=st[:, 3:4], in_=st[:, 2:3])), reads=[Rst], writes=[Rst])
        op("dve", ("tensor_scalar", dict(out=h2[:], in0=h2[:], scalar1=st[:, 3:4], scalar2=None, op0=ALU.mult)), reads=[Rh2, Rst], writes=[Rh2])
        o_, Ro = yo[i % 2], Ryo[i % 2]
        op("pool", ("tensor_tensor", dict(out=o_[:], in0=h2[:], in1=gN[:], op=ALU.mult)), reads=[Rh2, RgN], writes=[Ro])
        toks.append(S.dma(("dma_start", dict(out=yout[128 * i:128 * (i + 1), :], in_=o_[:])), reads=[Ro]))
    return toks


def _mk(nc, name, shape, dt=None, out=False):
    return nc.dram_tensor(name, list(shape), dt or F32, kind=("ExternalOutput" if out else "ExternalInput")).ap()


W1 = ["e_norm_g", "e_w_in", "e_conv_w", "e_sgu_ln_g", "e_sgu_ln_b", "e_sgu_w", "e_sgu_b", "e_w_out", "o_norm_g", "o_w_in"]


def build_l1(shapes):
    nc = bass.Bass("TRN2", target_bir_lowering=False)
    I = {"xh": _mk(nc, "xh", [2050, 1024]), "c_ident": _mk(nc, "c_ident", [128, 128])}
    for n in W1:
        I[n] = _mk(nc, n, shapes[n])
    O = {"h1": _mk(nc, "h1", [2048, 1024], out=True), "pT": _mk(nc, "pT", [4736, 2048], out=True),
         "fd": _mk(nc, "fd", [2048, 512], out=True)}
    S = Sched(nc)
    toks = emit_p1(S, nc, I, O)
    S.barrier_on("sp", toks)
    S.finalize()
    return nc


def build_l2(consts):
    NB, T = 2, 8192
    nc = bass.Bass("TRN2", target_bir_lowering=False)
    pr, pk, pv, pwa = (_mk(nc, n, [128, NB, T + 2]) for n in ("pr", "pk", "pv", "pwa"))
    prm = _mk(nc, "prm", [128, 17]); w2a2 = _mk(nc, "w2a2", [128, 2, 128])
    A = {k: _mk(nc, k, v.shape) for k, v in consts.items()}
    FI = {"fx": _mk(nc, "fx", [128, 8192]), "fw": _mk(nc, "fw", [128, 128]),
          "c_F": _mk(nc, "c_F", [128, 256], BF16), "c_T1": _mk(nc, "c_T1", [64, 16384], BF16),
          "c_T2": _mk(nc, "c_T2", [64, 16384], BF16), "c_Cc": _mk(nc, "c_Cc", [128, 128], BF16),
          "c_mSc": _mk(nc, "c_mSc", [128, 128], BF16)}
    yout = _mk(nc, "yout", [128, NB, T], out=True)
    ydT = _mk(nc, "ydT", [128, T], out=True)
    S = Sched(nc)
    toks = emit_fnet(S, nc, FI, ydT)
    toks += emit_rwkv(S, nc, A, pr, pk, pv, pwa, prm, w2a2, yout, NB, T)
    S.barrier_on("sp", toks)
    S.finalize()
    return nc


def build_l3():
    nc = bass.Bass("TRN2", target_bir_lowering=False)
    I = {"ycT": _mk(nc, "ycT", [1024, 2048]), "ydT": _mk(nc, "ydT", [512, 2048]), "gT": _mk(nc, "gT", [1536, 2048]),
         "h1": _mk(nc, "h1", [2048, 1024]), "o_w_out": _mk(nc, "o_w_out", [1536, 1024]),
         "final_norm_g": _mk(nc, "final_norm_g", [1024])}
    y = _mk(nc, "y", [2048, 1024], out=True)
    S = Sched(nc)
    toks = emit_p3(S, nc, I, y)
    S.barrier_on("sp", toks)
    S.finalize()
    return nc


def fnet_tables():
    import ml_dtypes
    N = 8192
    nh = np.arange(128); kl = np.arange(128)
    ang = 2 * np.pi * np.outer(nh, kl) / 128
    F = np.concatenate([np.cos(ang), np.sin(ang)], axis=1)
    nl = np.arange(64)[:, None, None]; klo = np.arange(128)[None, :, None]; kh = np.arange(64)[None, None, :]
    beta = 2 * np.pi * ((nl * (klo + 128 * kh)) % N) / N
    T1 = np.concatenate([np.cos(beta), np.sin(beta)], axis=2).reshape(64, 16384)
    T2 = np.concatenate([-np.sin(beta), np.cos(beta)], axis=2).reshape(64, 16384)
    c = np.arange(128); phi = 2 * np.pi * np.outer(c, c) / 128
    nrm = 1 / np.sqrt(N * 128)
    bf = lambda a: np.ascontiguousarray(a.astype(np.float32)).astype(ml_dtypes.bfloat16)
    return {"c_F": bf(F), "c_T1": bf(T1), "c_T2": bf(T2), "c_Cc": bf(np.cos(phi) * nrm), "c_mSc": bf(-np.sin(phi) * nrm)}


def kernel(**inputs):
    f32 = lambda a: np.ascontiguousarray(np.asarray(a), dtype=np.float32)
    inp = {k: f32(v) for k, v in inputs.items()}
    x = inp["x"]
    ncores = 8
    cores = list(range(ncores))
    w1 = {n: np.ascontiguousarray(inp[n][0]) for n in W1}
    ident = np.eye(128, dtype=np.float32)
    maps = []
    for c in cores:
        b, s0 = c // 4, (c % 4) * 2048
        xh = np.zeros((2050, 1024), np.float32)
        xh[1:2049] = x[b, s0:s0 + 2048]
        if s0 > 0:
            xh[0] = x[b, s0 - 1]
        if s0 + 2048 < 8192:
            xh[2049] = x[b, s0 + 2048]
        m = {"xh": xh, "c_ident": ident}
        m.update(w1)
        maps.append(m)
    nc1 = build_l1({n: w1[n].shape for n in W1})
    r1 = run_bass_kernel_spmd(nc1, maps, core_ids=cores).results
    PT = np.concatenate([np.asarray(r["pT"]) for r in r1], axis=1)
    FD = np.concatenate([np.asarray(r["fd"]) for r in r1], axis=0)
    consts = build_consts_np()
    ft = fnet_tables()
    mu, w0, w2, a0, a2 = inp["o_mu"][0], inp["o_w0"][0], inp["o_w2"][0], inp["o_a0"][0], inp["o_a2"][0]
    k_k, k_a, r_k = inp["o_k_k"][0], inp["o_k_a"][0], inp["o_r_k"][0].reshape(-1)
    lg, lb = inp["o_lnx_g"][0], inp["o_lnx_b"][0]
    PT3 = PT.reshape(4736, 2, 8192)
    pad = lambda a: np.ascontiguousarray(np.pad(a, ((0, 0), (0, 0), (1, 1))))
    maps = []
    for c in cores:
        ch = slice(c * 128, (c + 1) * 128)
        m = {"pr": pad(PT3[0:1024][ch]), "pk": pad(PT3[1024:2048][ch]), "pv": pad(PT3[2048:3072][ch]),
             "pwa": pad(PT3[3072:3200])}
        prm = np.zeros((128, 17), np.float32)
        for d in range(2):
            prm[:, 0 + d] = mu[d, 0:1024][ch]; prm[:, 2 + d] = mu[d, 1024:2048][ch]; prm[:, 4 + d] = mu[d, 2048:3072][ch]
            prm[:, 6 + d] = mu[d, 3072:3200]; prm[:, 8 + d] = w0[d][ch]; prm[:, 10 + d] = a0[d][ch]
        prm[:, 12] = k_k[ch]; prm[:, 13] = k_a[ch]; prm[:, 14] = r_k[ch]; prm[:, 15] = lg[ch]; prm[:, 16] = lb[ch]
        m["prm"] = prm
        m["w2a2"] = np.ascontiguousarray(np.concatenate([w2[:, :, ch], a2[:, :, ch]], axis=1).transpose(1, 0, 2))
        m.update(consts)
        b, g = c // 4, c % 4
        m["fx"] = np.ascontiguousarray(FD[b * 8192:(b + 1) * 8192, g * 128:(g + 1) * 128]).reshape(128, 8192)
        m["fw"] = np.ascontiguousarray(inp["o_fnet_w"][0, g])
        m.update(ft)
        maps.append(m)
    nc2 = build_l2(consts)
    r2 = run_bass_kernel_spmd(nc2, maps, core_ids=cores).results
    YC = np.concatenate([np.asarray(r["yout"]).reshape(128, 16384) for r in r2], axis=0)
    YD = np.concatenate([np.concatenate([np.asarray(r2[b * 4 + g]["ydT"]) for g in range(4)], axis=0) for b in range(2)], axis=1)
    maps = []
    for c in cores:
        ts = slice(c * 2048, (c + 1) * 2048)
        maps.append({"ycT": np.ascontiguousarray(YC[:, ts]), "ydT": np.ascontiguousarray(YD[:, ts]),
                     "gT": np.ascontiguousarray(PT[3200:4736, ts]), "h1": np.asarray(r1[c]["h1"]),
                     "o_w_out": np.ascontiguousarray(inp["o_w_out"][0]), "final_norm_g": inp["final_norm_g"]})
    nc3 = build_l3()
    r3 = run_bass_kernel_spmd(nc3, maps, core_ids=cores).results
    y = np.concatenate([np.asarray(r["y"]) for r in r3], axis=0).reshape(2, 8192, 1024)
    return y.astype(np.float32)
```

```python
from contextlib import ExitStack
import numpy as np
import concourse.bass as bass
import concourse.mybir as mybir
from concourse.bass_utils import run_bass_kernel_spmd


F32 = mybir.dt.float32
BF16 = mybir.dt.bfloat16
AF = mybir.ActivationFunctionType
ALU = mybir.AluOpType
AX = mybir.AxisListType

N_DMA_SEMS = 8


class Region:
    __slots__ = ("w", "r", "name")

    def __init__(self, name=""):
        self.w = None
        self.r = {}
        self.name = name


class Sched:
    ENGS = ("pe", "dve", "act", "pool", "sp")

    def __init__(self, nc):
        self.nc = nc
        self.prog = {e: [] for e in self.ENGS}
        self.cnt = {}
        self.seen = {e: {} for e in self.ENGS}
        self.dma_rr = {e: 0 for e in self.ENGS}
        self.dma_last = {}
        self.same_engine_raw = True
        self.cut = 0
        self.nrec = 0
        self.log = []

    def _collect(self, eng, mykey, reads, writes):
        waits = {}

        def need(tok, kind):
            if tok is None:
                return
            k, v = tok
            if k == mykey:
                if eng == "pe":
                    return
                if not self.same_engine_raw:
                    return
            if waits.get(k, 0) < v:
                waits[k] = v

        for R in reads:
            need(R.w, "raw")
        for R in writes:
            need(R.w, "waw")
            for k, v in R.r.items():
                need((k, v), "war")
        out = []
        seen = self.seen[eng]
        for k, v in waits.items():
            if seen.get(k, 0) < v:
                seen[k] = v
                out.append((k, v))
        return out

    def _commit(self, tok, reads, writes):
        for R in writes:
            R.w = tok
            R.r = {}
        k, v = tok
        for R in reads:
            if R.r.get(k, 0) < v:
                R.r[k] = v

    def op(self, eng, fn, reads=(), writes=()):
        self.nrec += 1
        if self.cut and self.nrec > self.cut:
            return None
        if self.cut:
            self.log.append((self.nrec, eng, fn[0] if isinstance(fn, tuple) else "fn", str(fn[1].get("out", ""))[:120] if isinstance(fn, tuple) else ""))
        key = eng
        waits = self._collect(eng, key, reads, writes)
        idx = self.cnt.get(key, 0) + 1
        self.cnt[key] = idx
        tok = (key, idx)
        self.prog[eng].append([waits, fn, tok])
        self._commit(tok, reads, writes)
        return tok

    def dma(self, fn, reads=(), writes=(), q="sp"):
        self.nrec += 1
        if self.cut and self.nrec > self.cut:
            return None
        i = self.dma_rr[q]
        self.dma_rr[q] = (i + 1) % N_DMA_SEMS
        key = "dma_%s_%d" % (q, i)
        waits = self._collect(q, key, reads, writes)
        prev = self.cnt.get(key, 0)
        if prev > 0 and self.seen[q].get(key, 0) < prev:
            self.seen[q][key] = prev
            waits.append((key, prev))
        idx = prev + 1
        self.cnt[key] = idx
        tok = (key, idx)
        self.prog[q].append([waits, fn, tok])
        self._commit(tok, reads, writes)
        return tok

    def finalize(self):
        nc = self.nc
        waited = {}
        for e in self.ENGS:
            for waits, fn, tok in self.prog[e]:
                for k, v in waits:
                    waited.setdefault(k, set()).add(v)
        self.final_waits = []
        sem_of = {}
        val_of = {}
        for k, s in waited.items():
            sem_of[k] = nc.alloc_semaphore("s_" + k)
            isdma = k.startswith("dma_")
            step = 16 if isdma else 1
            if isdma:
                val_of[k] = None
            else:
                val_of[k] = {v: (i + 1) for i, v in enumerate(sorted(s))}
        engobj = {"pe": nc.tensor, "dve": nc.vector, "act": nc.scalar,
                  "pool": nc.gpsimd, "sp": nc.sync}

        def value(k, v):
            if val_of[k] is None:
                return 16 * v
            return val_of[k][v]

        def emit(e):
            def body(eng):
                for waits, fn, tok in self.prog[e]:
                    for k, v in waits:
                        eng.wait_ge(sem_of[k], value(k, v))
                    if fn is None:
                        continue
                    if isinstance(fn, tuple):
                        ins = getattr(eng, fn[0])(**fn[1])
                    else:
                        ins = fn(eng)
                    k, v = tok
                    if k in sem_of:
                        if val_of[k] is None:
                            ins.then_inc(sem_of[k], 16)
                        elif v in val_of[k]:
                            ins.then_inc(sem_of[k], 1)
            return body

        with nc.Block() as block:
            for e, dec in (("sp", block.sync), ("pe", block.tensor), ("dve", block.vector),
                           ("act", block.scalar), ("pool", block.gpsimd)):
                if self.prog[e]:
                    dec(emit(e))
        self.n_sems = len(sem_of)
        return self.n_sems

    def barrier_on(self, eng, toks):
        waits = []
        for tk in toks:
            if tk is None:
                continue
            k, v = tk
            if self.seen[eng].get(k, 0) < v:
                self.seen[eng][k] = v
                waits.append((k, v))
        if waits:
            self.prog[eng].append([waits, None, ("_none", 0)])


C = 128
BLK = 512
NEG_E = -float(np.exp(-0.5))
GN_EPS = 64e-5


def build_consts_np():
    idx = np.arange(128)
    lt = (idx[:, None] < idx[None, :]).astype(np.float32)
    le = (idx[:, None] <= idx[None, :]).astype(np.float32)
    gt = lt.T.copy()
    ge = le.T.copy()
    m4f = np.stack([lt, gt, gt, le], axis=1)
    m4b = np.stack([gt, lt, lt, ge], axis=1)
    mk = np.stack([le, ge], axis=1)
    ident = np.eye(128, dtype=np.float32)
    bd = np.kron(np.eye(2, dtype=np.float32), np.ones((64, 64), np.float32))
    scanm = np.ones((128, BLK), np.float32)
    scanm[:, ::C] = 0.0
    return {"c_m4": np.stack([m4f, m4b], axis=1).reshape(128, 2 * 4 * 128).copy(),
            "c_mk": mk.reshape(128, 256).copy(), "c_ident": ident, "c_bd": bd, "c_scanm": scanm}


XST = False


def emit_rwkv(S, nc, A, pr, pk, pv, pwa, prm, w2a2, yout, NB, T):
    sb = lambda name, shape, dt=F32: nc.alloc_sbuf_tensor(name, shape, dt)
    ps = lambda name, shape, dt=F32: nc.alloc_psum_tensor(name, shape, dt)
    R = Region
    nblk = T // BLK

    m4f = sb("m4f", [128, 2, 4, 128]); Rm4 = R()
    mkf = sb("mkf", [128, 2, 128]); Rmk = R()
    identf = sb("identf", [128, 128]); Ridf = R()
    identb = sb("identb", [128, 128], BF16); Ridb = R()
    bdf = sb("bdf", [128, 128]); Rbd = R()
    bdr = sb("bdr", [128, 128]); Rbdr = R()
    bdm = sb("bdm", [128, 128]); Rbdm = R()
    scanm = sb("scanm", [128, BLK]); Rsc = R()
    prmt = sb("prmt", [128, 17]); Rprm = R()
    w2f = sb("w2f", [128, 2, 128]); Rw2f = R()
    w2b = sb("w2b", [128, 2, 128], BF16); Rw2b = R()
    S.dma(("dma_start", dict(out=m4f[:].rearrange("p a b c -> p (a b c)"), in_=A["c_m4"])), writes=[Rm4])
    S.dma(("dma_start", dict(out=mkf[:].rearrange("p a c -> p (a c)"), in_=A["c_mk"])), writes=[Rmk])
    S.dma(("dma_start", dict(out=identf[:], in_=A["c_ident"])), writes=[Ridf])
    S.dma(("dma_start", dict(out=bdf[:], in_=A["c_bd"])), writes=[Rbd])
    S.dma(("dma_start", dict(out=scanm[:], in_=A["c_scanm"])), writes=[Rsc])
    S.dma(("dma_start", dict(out=prmt[:], in_=prm)), writes=[Rprm])
    S.dma(("dma_start", dict(out=w2f[:], in_=w2a2)), writes=[Rw2f])
    S.op("dve", ("tensor_copy", dict(out=identb[:], in_=identf[:])), reads=[Ridf], writes=[Ridb])
    S.op("dve", ("tensor_copy", dict(out=w2b[:], in_=w2f[:])), reads=[Rw2f], writes=[Rw2b])
    PM = lambda c: prmt[:, c:c + 1]
    S.op("dve", ("tensor_scalar", dict(out=bdr[:], in0=bdf[:], scalar1=PM(14), scalar2=None, op0=ALU.mult)), reads=[Rbd, Rprm], writes=[Rbdr])
    S.op("dve", ("tensor_scalar", dict(out=bdm[:], in0=bdf[:], scalar1=1.0 / 64, scalar2=None, op0=ALU.mult)), reads=[Rbd], writes=[Rbdm])

    def T2(name, dt=F32, n=BLK):
        return sb(name, [128, n], dt), R()
    ld = {}
    for nm in ("pr", "pk", "pv", "pwa"):
        ld[nm] = (sb("ld_" + nm, [128, BLK + 2]), R())
    tmp, Rtmp = T2("tmp")
    qr, Rqr = T2("qr"); qk, Rqk = T2("qk"); qv, Rqv = T2("qv"); qwa, Rqwa = T2("qwa")
    twa, Rtwa = T2("twa", BF16)
    sw, Rsw = T2("sw"); asg, Rasg = T2("asg")
    logw, Rlogw = T2("logw"); lin, Rlin = T2("lin"); linm, Rlinm = T2("linm"); lexm, Rlexm = T2("lexm")
    lex, Rlex = T2("lex"); lint, Rlint = T2("lint")
    e1, Re1 = T2("e1"); e1x, Re1x = T2("e1x"); e2, Re2 = T2("e2"); e3, Re3 = T2("e3"); e3x, Re3x = T2("e3x"); e4, Re4 = T2("e4")
    kk, Rkk = T2("kk"); kk2, Rkk2 = T2("kk2"); rin, Rrin = T2("rin"); kkn, Rkkn = T2("kkn")
    kp, Rkp = T2("kp"); bv, Rbv = T2("bv"); rk, Rrk = T2("rk")
    rt, Rrt = T2("rt", BF16); at, Rat = T2("at", BF16); kt, Rkt = T2("kt", BF16); bt, Rbt = T2("bt", BF16)
    r0, Rr0 = T2("r0"); a0b, Ra0b = T2("a0b", BF16); kEb, RkEb = T2("kEb", BF16); bEb, RbEb = T2("bEb", BF16)
    qvb, Rqvb = T2("qvb", BF16)
    ysum = sb("ysum", [128, T]); Rys = [R() for _ in range(T // C)]
    bsum = sb("bsum", [128, T]); Rbs = [R() for _ in range(nblk)]
    TT = sb("TT", [128, 4, 128], BF16); RTT = R()
    SBM = sb("SBM", [128, 2, 4, 128], BF16); RSBM = R()
    MKR = sb("MKR", [128, 2, 128]); RMKR = R()
    SX = sb("SX", [128, 2, 192], BF16); RSX = R()
    SAB = [sb("SAB%d" % i, [128, 2, 2, 128], BF16) for i in range(2)]; RSAB = [R(), R()]
    Gb = sb("Gb", [128, 128], BF16); RGb = R()
    Hb = sb("Hb", [128, 2, 128], BF16); RHb = R()
    Pb = sb("Pb", [128, 64], BF16); RPb = R()
    Zb = sb("Zb", [128, 2, 64], BF16); RZb = R()
    STz = [sb("STz%d" % h, [128, 64], BF16) for h in range(2)]; RST = [R(), R()]
    identP = sb("identP", [128, 64]); mkb = sb("mkb", [128, 2, 2, 128])
    HS = [slice(0, 64), slice(64, 128)]
    fin1, Rfin1 = T2("fin1"); fin2, Rfin2 = T2("fin2"); fin3, Rfin3 = T2("fin3")

    PS_M = ps("PS_M", [128, 2, 4, 128]); RPS_M = R()
    PS_K = ps("PS_K", [128, 512]); RPS_K = R()
    PS_X = [ps("PS_X%d" % h, [128, 512]) for h in range(2)]; RPS_X = R()
    PS_AB = ps("PS_AB", [128, 2, 2, 128]); RPS_AB = R()
    PS_G = ps("PS_G", [128, 512]); RPS_G = R()
    PS_T = ps("PS_T", [128, 8, 128], BF16); RPS_T = R()
    PS_P1 = PS_K; RPS_P1 = RPS_K
    PS_P2 = PS_G; RPS_P2 = RPS_G
    mm = lambda out, l, r_, rd, wr, st=True, sp=True, sg=False: S.op("pe", ("matmul", dict(out=out, lhsT=l, rhs=r_, start=st, stop=sp, skip_group_check=sg)), reads=rd, writes=wr)
    S.op("pool", ("tensor_copy", dict(out=identP[0:64, :], in_=identf[0:64, 0:64])), reads=[Ridf], writes=[Ridf])
    S.op("pool", ("tensor_copy", dict(out=identP[64:128, :], in_=identf[64:128, 64:128])), reads=[Ridf], writes=[Ridf])
    for h in range(2):
        S.op("pool", ("tensor_copy", dict(out=mkb[:, :, h, :], in_=mkf[:])), reads=[Rmk], writes=[Rmk])
    out_toks = []
    for b in range(NB):
        for d in range(2):
            bwd = (d == 1)
            midc, totc = (C // 2 - 1, C - 1) if not bwd else (C // 2, 0)
            S.op("pool", ("memset", dict(ap=STz[0][:], constant=0.0)), writes=[RST[0]])
            S.op("pool", ("memset", dict(ap=STz[1][:], constant=0.0)), writes=[RST[1]])
            blocks = range(nblk) if not bwd else range(nblk - 1, -1, -1)
            for blk in blocks:
                t0 = blk * BLK
                for nm, src in (("pr", pr), ("pk", pk), ("pv", pv), ("pwa", pwa)):
                    tl, Rl = ld[nm]
                    S.dma(("dma_start", dict(out=tl[:], in_=src[:, b, t0:t0 + BLK + 2])), writes=[Rl])
                sh = (slice(0, BLK) if not bwd else slice(2, BLK + 2))
                cur = slice(1, BLK + 1)
                for nm, q, Rq, mc in (("pr", qr, Rqr, 0), ("pk", qk, Rqk, 2), ("pv", qv, Rqv, 4), ("pwa", qwa, Rqwa, 6)):
                    tl, Rl = ld[nm]
                    S.op("dve", ("tensor_tensor", dict(out=tmp[:], in0=tl[:, sh], in1=tl[:, cur], op=ALU.subtract)), reads=[Rl], writes=[Rtmp])
                    S.op("dve", ("scalar_tensor_tensor", dict(out=q[:], in0=tmp[:], scalar=PM(mc + d), in1=tl[:, cur], op0=ALU.mult, op1=ALU.add)), reads=[Rtmp, Rl, Rprm], writes=[Rq])
                S.op("act", ("activation", dict(out=twa[0:64, :], in_=qwa[0:64, :], func=AF.Tanh)), reads=[Rqwa], writes=[Rtwa])
                S.op("dve", ("tensor_copy", dict(out=twa[64:128, :], in_=qwa[64:128, :])), reads=[Rqwa], writes=[Rtwa])
                S.op("pe", ("matmul", dict(out=PS_P1[:], lhsT=w2b[0:64, d, :], rhs=twa[0:64, :], start=True, stop=True)), reads=[Rw2b, Rtwa], writes=[RPS_P1])
                S.op("pe", ("matmul", dict(out=PS_P2[:], lhsT=w2b[64:128, d, :], rhs=twa[64:128, :], start=True, stop=True)), reads=[Rw2b, Rtwa], writes=[RPS_P2])
                S.op("act", ("activation", dict(out=sw[:], in_=PS_P1[:], func=AF.Sigmoid, bias=PM(8 + d))), reads=[RPS_P1, Rprm], writes=[Rsw])
                S.op("act", ("activation", dict(out=asg[:], in_=PS_P2[:], func=AF.Sigmoid, bias=PM(10 + d))), reads=[RPS_P2, Rprm], writes=[Rasg])
                S.op("dve", ("tensor_scalar", dict(out=logw[:], in0=sw[:], scalar1=NEG_E, scalar2=None, op0=ALU.mult)), reads=[Rsw], writes=[Rlogw])
                S.op("dve", ("tensor_tensor_scan", dict(out=lin[:], data0=scanm[:], data1=logw[:], initial=0.0, op0=ALU.mult, op1=ALU.add)), reads=[Rsc, Rlogw], writes=[Rlin])
                lin3 = lambda tl: tl[:].rearrange("p (c t) -> p c t", t=C)
                bc = lambda tl, col: lin3(tl)[:, :, col:col + 1].to_broadcast([128, BLK // C, C])
                if bwd:
                    S.op("dve", ("tensor_tensor", dict(out=lin3(tmp), in0=bc(lin, C - 1), in1=lin3(lin), op=ALU.subtract)), reads=[Rlin], writes=[Rtmp])
                    S.op("dve", ("tensor_tensor", dict(out=lin[:], in0=tmp[:], in1=logw[:], op=ALU.add)), reads=[Rtmp, Rlogw], writes=[Rlin])
                S.op("dve", ("tensor_tensor", dict(out=lin3(linm), in0=lin3(lin), in1=bc(lin, midc), op=ALU.subtract)), reads=[Rlin], writes=[Rlinm])
                S.op("dve", ("tensor_tensor", dict(out=lexm[:], in0=linm[:], in1=logw[:], op=ALU.subtract)), reads=[Rlinm, Rlogw], writes=[Rlexm])
                S.op("dve", ("tensor_tensor", dict(out=lex[:], in0=lin[:], in1=logw[:], op=ALU.subtract)), reads=[Rlin, Rlogw], writes=[Rlex])
                S.op("dve", ("tensor_tensor", dict(out=lin3(lint), in0=lin3(lin), in1=bc(lin, totc), op=ALU.subtract)), reads=[Rlin], writes=[Rlint])
                S.op("act", ("activation", dict(out=e1[:], in_=linm[:], func=AF.Exp)), reads=[Rlinm], writes=[Re1])
                S.op("act", ("activation", dict(out=e1x[:], in_=lexm[:], func=AF.Exp)), reads=[Rlexm], writes=[Re1x])
                S.op("act", ("activation", dict(out=e2[:], in_=linm[:], func=AF.Exp, scale=-1.0)), reads=[Rlinm], writes=[Re2])
                S.op("act", ("activation", dict(out=e3[:], in_=lin[:], func=AF.Exp)), reads=[Rlin], writes=[Re3])
                S.op("act", ("activation", dict(out=e3x[:], in_=lex[:], func=AF.Exp)), reads=[Rlex], writes=[Re3x])
                S.op("act", ("activation", dict(out=e4[:], in_=lint[:], func=AF.Exp, scale=-1.0)), reads=[Rlint], writes=[Re4])
                S.op("dve", ("tensor_scalar", dict(out=kk[:], in0=qk[:], scalar1=PM(12), scalar2=None, op0=ALU.mult)), reads=[Rqk, Rprm], writes=[Rkk])
                S.op("pool", ("tensor_tensor", dict(out=kk2[:], in0=kk[:], in1=kk[:], op=ALU.mult)), reads=[Rkk], writes=[Rkk2])
                S.op("pe", ("matmul", dict(out=PS_P1[:], lhsT=bdf[:], rhs=kk2[:], start=True, stop=True)), reads=[Rbd, Rkk2], writes=[RPS_P1])
                S.op("dve", ("tensor_scalar", dict(out=rin[:], in0=PS_P1[:], scalar1=1e-12, scalar2=None, op0=ALU.max)), reads=[RPS_P1], writes=[Rrin])
                S.op("act", ("activation", dict(out=rin[:], in_=rin[:], func=AF.Sqrt)), reads=[Rrin], writes=[Rrin])
                S.op("dve", ("reciprocal", dict(out=rin[:], in_=rin[:])), reads=[Rrin], writes=[Rrin])
                S.op("dve", ("tensor_tensor", dict(out=kkn[:], in0=kk[:], in1=rin[:], op=ALU.mult)), reads=[Rkk, Rrin], writes=[Rkkn])
                S.op("dve", ("tensor_scalar", dict(out=tmp[:], in0=asg[:], scalar1=-1.0, scalar2=PM(13), op0=ALU.add, op1=ALU.mult)), reads=[Rasg, Rprm], writes=[Rtmp])
                S.op("dve", ("scalar_tensor_tensor", dict(out=kp[:], in0=tmp[:], scalar=1.0, in1=qk[:], op0=ALU.add, op1=ALU.mult)), reads=[Rtmp, Rqk], writes=[Rkp])
                S.op("pool", ("tensor_tensor", dict(out=bv[:], in0=kkn[:], in1=asg[:], op=ALU.mult)), reads=[Rkkn, Rasg], writes=[Rbv])
                S.op("pool", ("tensor_tensor", dict(out=rk[:], in0=qr[:], in1=kp[:], op=ALU.mult)), reads=[Rqr, Rkp], writes=[Rrk])
                S.op("pe", ("matmul", dict(out=PS_P2[:], lhsT=bdr[:], rhs=rk[:], start=True, stop=True)), reads=[Rbdr, Rrk], writes=[RPS_P2])
                bsl = bsum[:, t0:t0 + BLK]
                if d == 0:
                    S.op("dve", ("tensor_tensor", dict(out=bsl, in0=PS_P2[:], in1=qv[:], op=ALU.mult)), reads=[RPS_P2, Rqv], writes=[Rbs[blk]])
                else:
                    S.op("dve", ("tensor_tensor", dict(out=tmp[:], in0=PS_P2[:], in1=qv[:], op=ALU.mult)), reads=[RPS_P2, Rqv], writes=[Rtmp])
                    S.op("pool", ("tensor_tensor", dict(out=bsl, in0=bsl, in1=tmp[:], op=ALU.add)), reads=[Rtmp, Rbs[blk]], writes=[Rbs[blk]])
                S.op("dve", ("tensor_tensor", dict(out=rt[:], in0=qr[:], in1=e1[:], op=ALU.mult)), reads=[Rqr, Re1], writes=[Rrt])
                S.op("dve", ("scalar_tensor_tensor", dict(out=at[:], in0=kkn[:], scalar=-1.0, in1=e1x[:], op0=ALU.mult, op1=ALU.mult)), reads=[Rkkn, Re1x], writes=[Rat])
                S.op("pool", ("tensor_tensor", dict(out=kt[:], in0=kp[:], in1=e2[:], op=ALU.mult)), reads=[Rkp, Re2], writes=[Rkt])
                S.op("pool", ("tensor_tensor", dict(out=bt[:], in0=bv[:], in1=e2[:], op=ALU.mult)), reads=[Rbv, Re2], writes=[Rbt])
                S.op("pool", ("tensor_tensor", dict(out=r0[:], in0=qr[:], in1=e3[:], op=ALU.mult)), reads=[Rqr, Re3], writes=[Rr0])
                S.op("dve", ("scalar_tensor_tensor", dict(out=a0b[:], in0=kkn[:], scalar=-1.0, in1=e3x[:], op0=ALU.mult, op1=ALU.mult)), reads=[Rkkn, Re3x], writes=[Ra0b])
                S.op("pool", ("tensor_tensor", dict(out=kEb[:], in0=kp[:], in1=e4[:], op=ALU.mult)), reads=[Rkp, Re4], writes=[RkEb])
                S.op("pool", ("tensor_tensor", dict(out=bEb[:], in0=bv[:], in1=e4[:], op=ALU.mult)), reads=[Rbv, Re4], writes=[RbEb])
                S.op("act", ("activation", dict(out=qvb[:], in_=qv[:], func=AF.Copy)), reads=[Rqv], writes=[Rqvb])

                chunks = range(BLK // C) if not bwd else range(BLK // C - 1, -1, -1)
                for ci in chunks:
                    cs = slice(ci * C, (ci + 1) * C)
                    gci = (t0 // C) + ci
                    for i, (src, Rs) in enumerate(((qvb, Rqvb), (a0b, Ra0b), (bEb, RbEb), (kEb, RkEb))):
                        S.op("pe", ("transpose", dict(out=PS_T[:, i, :], in_=src[:, cs], identity=identb[:])), reads=[Rs, Ridb], writes=[RPS_T])
                    S.op("act", ("activation", dict(out=TT[:], in_=PS_T[:, 0:4, :], func=AF.Copy)), reads=[RPS_T], writes=[RTT])
                    for h in range(2):
                        hs = HS[h]
                        mm(PS_M[:, h, 0, :], bt[hs, cs], at[hs, cs], [Rbt, Rat], [RPS_M])
                        mm(PS_M[:, h, 1, :], at[hs, cs], bt[hs, cs], [Rbt, Rat], [RPS_M])
                        mm(PS_M[:, h, 2, :], at[hs, cs], kt[hs, cs], [Rkt, Rat], [RPS_M])
                        mm(PS_M[:, h, 3, :], bt[hs, cs], rt[hs, cs], [Rbt, Rrt], [RPS_M])
                        mm((PS_K if h == 0 else PS_G)[:, 0:128], kt[hs, cs], rt[hs, cs], [Rkt, Rrt], [RPS_K if h == 0 else RPS_G])
                    for h in range(2):
                        S.op("dve", ("tensor_tensor", dict(out=SBM[:, h], in0=PS_M[:, h], in1=m4f[:, d, :, :], op=ALU.mult)), reads=[RPS_M, Rm4], writes=[RSBM])
                    S.op("dve", ("tensor_tensor", dict(out=MKR[:, 0, :], in0=PS_K[:, 0:128], in1=mkf[:, d, :], op=ALU.mult)), reads=[RPS_K, Rmk], writes=[RMKR])
                    S.op("dve", ("tensor_tensor", dict(out=MKR[:, 1, :], in0=PS_G[:, 0:128], in1=mkf[:, d, :], op=ALU.mult)), reads=[RPS_G, Rmk], writes=[RMKR])
                    S.op("act", ("activation", dict(out=SX[:, :, 0:128], in_=SBM[:, :, 3, :], func=AF.Copy)), reads=[RSBM], writes=[RSX])
                    S.op("pool", ("tensor_copy", dict(out=SX[:, :, 128:192], in_=TT[:, 2, :].rearrange("p (h j) -> p h j", h=2))), reads=[RTT], writes=[RSX])
                    for h in range(2):
                        mm(PS_X[h][:, 0:192], identb[:], SX[:, h, :], [Ridb, RSX], [RPS_X], st=True, sp=True)
                    A_ = [SBM[:, h, 1, :] for h in range(2)]
                    B_ = [SBM[:, h, 0, :] for h in range(2)]
                    Rcur = RSBM
                    for lv in range(7):
                        for h in range(2):
                            mm(PS_X[h][:, 0:192], A_[h], SX[:, h, :], [Rcur, RSX], [RPS_X], st=False, sp=True, sg=True)
                        if lv < 6:
                            nb = lv % 2
                            for h in range(2):
                                mm(PS_AB[:, h, 0, :], B_[h], A_[h], [Rcur], [RPS_AB])
                                mm(PS_AB[:, h, 1, :], A_[h], B_[h], [Rcur], [RPS_AB])
                            S.op("act", ("activation", dict(out=SAB[nb][:].rearrange("p a b c -> p (a b c)"), in_=PS_AB[:].rearrange("p a b c -> p (a b c)"), func=AF.Copy)), reads=[RPS_AB], writes=[RSAB[nb]])
                            A_ = [SAB[nb][:, h, 0, :] for h in range(2)]
                            B_ = [SAB[nb][:, h, 1, :] for h in range(2)]
                            Rcur = RSAB[nb]
                        S.op("dve", ("tensor_copy", dict(out=SX[:, 0, :], in_=PS_X[0][:, 0:192])), reads=[RPS_X], writes=[RSX])
                        S.op("act", ("activation", dict(out=SX[:, 1, :], in_=PS_X[1][:, 0:192], func=AF.Copy)), reads=[RPS_X], writes=[RSX])
                    for h in range(2):
                        hs = HS[h]
                        a0T = TT[:, 1, hs]
                        mm(PS_G[hs, 0:128], a0T, SX[:, h, 0:128], [RTT, RSX], [RPS_G])
                        mm(PS_G[hs, 128:192], a0T, SX[:, h, 128:192], [RTT, RSX], [RPS_G])
                        mm(PS_G[:, 192 + 128 * h:320 + 128 * h], SBM[:, h, 2, :], SX[:, h, 0:128], [RSBM, RSX], [RPS_G])
                        mm(PS_K[:, 256 + 64 * h:320 + 64 * h], SBM[:, h, 2, :], SX[:, h, 128:192], [RSBM, RSX], [RPS_K])
                    S.op("dve", ("tensor_tensor", dict(out=Gb[:], in0=PS_G[:, 0:128], in1=r0[:, cs], op=ALU.add)), reads=[RPS_G, Rr0], writes=[RGb])
                    S.op("dve", ("tensor_tensor", dict(out=Hb[:], in0=PS_G[:, 192:448].rearrange("p (h t) -> p h t", h=2), in1=MKR[:], op=ALU.add)), reads=[RPS_G, RMKR], writes=[RHb])
                    tcol = ci * C + totc
                    S.op("dve", ("scalar_tensor_tensor", dict(out=Pb[:], in0=identP[:], scalar=e3[:, tcol:tcol + 1], in1=PS_G[:, 128:192], op0=ALU.mult, op1=ALU.add)), reads=[RPS_G, Ridf, Re3], writes=[RPb])
                    S.op("dve", ("tensor_tensor", dict(out=Zb[:], in0=PS_K[:, 256:384].rearrange("p (h j) -> p h j", h=2), in1=TT[:, 3, :].rearrange("p (h j) -> p h j", h=2), op=ALU.add)), reads=[RPS_K, RTT], writes=[RZb])
                    for h in range(2):
                        hs = HS[h]
                        mm(PS_M[hs, 0, 0, :], STz[h][:], Gb[:], [RST[h], RGb], [RPS_M], st=True, sp=False)
                        mm(PS_M[hs, 0, 0, :], TT[:, 0, hs], Hb[:, h, :], [RTT, RHb], [RPS_M], st=False, sp=True)
                        mm(PS_M[hs, 0, 1, 0:64], Pb[:], STz[h][:], [RPb, RST[h]], [RPS_M], st=True, sp=False)
                        mm(PS_M[hs, 0, 1, 0:64], Zb[:, h, :], TT[:, 0, hs], [RZb, RTT], [RPS_M], st=False, sp=True)
                    ysl = ysum[:, t0 + ci * C: t0 + (ci + 1) * C]
                    if d == 0:
                        S.op("act", ("activation", dict(out=ysl, in_=PS_M[:, 0, 0, :], func=AF.Copy)), reads=[RPS_M], writes=[Rys[gci]])
                    else:
                        S.op("act", ("activation", dict(out=tmp[:, 0:128], in_=PS_M[:, 0, 0, :], func=AF.Copy)), reads=[RPS_M], writes=[Rtmp])
                        S.op("dve", ("tensor_tensor", dict(out=ysl, in0=tmp[:, 0:128], in1=ysl, op=ALU.add)), reads=[Rtmp, Rys[gci]], writes=[Rys[gci]])
                    for h in range(2):
                        hs = HS[h]
                        S.op("act", ("activation", dict(out=STz[h][hs, :], in_=PS_M[hs, 0, 1, 0:64], func=AF.Copy)), reads=[RPS_M], writes=[RST[h]])
        for blk in range(nblk):
            t0 = blk * BLK
            ysl = ysum[:, t0:t0 + BLK]
            Rin = Rys[t0 // C: (t0 + BLK) // C]
            S.op("pe", ("matmul", dict(out=PS_P1[:], lhsT=bdm[:], rhs=ysl, start=True, stop=True)), reads=[Rbdm] + Rin, writes=[RPS_P1])
            S.op("dve", ("tensor_tensor", dict(out=fin1[:], in0=ysl, in1=PS_P1[:], op=ALU.subtract)), reads=[RPS_P1] + Rin, writes=[Rfin1])
            S.op("pool", ("tensor_tensor", dict(out=fin2[:], in0=fin1[:], in1=fin1[:], op=ALU.mult)), reads=[Rfin1], writes=[Rfin2])
            S.op("pe", ("matmul", dict(out=PS_P2[:], lhsT=bdm[:], rhs=fin2[:], start=True, stop=True)), reads=[Rbdm, Rfin2], writes=[RPS_P2])
            S.op("dve", ("tensor_scalar", dict(out=fin3[:], in0=PS_P2[:], scalar1=GN_EPS, scalar2=None, op0=ALU.add)), reads=[RPS_P2], writes=[Rfin3])
            S.op("act", ("activation", dict(out=fin3[:], in_=fin3[:], func=AF.Sqrt)), reads=[Rfin3], writes=[Rfin3])
            S.op("dve", ("reciprocal", dict(out=fin3[:], in_=fin3[:])), reads=[Rfin3], writes=[Rfin3])
            S.op("dve", ("tensor_tensor", dict(out=fin1[:], in0=fin1[:], in1=fin3[:], op=ALU.mult)), reads=[Rfin1, Rfin3], writes=[Rfin1])
            S.op("dve", ("tensor_scalar", dict(out=fin2[:], in0=fin1[:], scalar1=PM(15), scalar2=PM(16), op0=ALU.mult, op1=ALU.add)), reads=[Rfin1, Rprm], writes=[Rfin2])
            S.op("dve", ("tensor_tensor", dict(out=fin2[:], in0=fin2[:], in1=bsum[:, t0:t0 + BLK], op=ALU.add)), reads=[Rfin2, Rbs[blk]], writes=[Rfin2])
            out_toks.append(S.dma(("dma_start", dict(out=yout[:, b, t0:t0 + BLK], in_=fin2[:])), reads=[Rfin2]))
    return out_toks


NT = 2048
NTH = NT + 2


def emit_p1(S, nc, I, O):
    sb = lambda name, shape, dt=F32: nc.alloc_sbuf_tensor(name, shape, dt)
    ps = lambda name, shape, dt=F32: nc.alloc_psum_tensor(name, shape, dt)
    R = Region
    op = S.op
    mm = lambda out, l, r_, rd, wr, st=True, sp=True: op("pe", ("matmul", dict(out=out, lhsT=l, rhs=r_, start=st, stop=sp)), reads=rd, writes=wr)

    identf = sb("identf", [128, 128]); identb = sb("identb", [128, 128], BF16); Rid = R()
    gE = sb("gE", [128, 8, 1]); gO = sb("gO", [128, 8, 1]); Rg = R()
    S.dma(("dma_start", dict(out=identf[:], in_=I["c_ident"])), writes=[Rid])
    op("dve", ("tensor_copy", dict(out=identb[:], in_=identf[:])), reads=[Rid], writes=[Rid])
    S.dma(("dma_start", dict(out=gE[:, :, 0], in_=I["e_norm_g"].rearrange("(k p) -> p k", p=128), allow_slow_non_contiguous=True)), writes=[Rg])
    S.dma(("dma_start", dict(out=gO[:, :, 0], in_=I["o_norm_g"].rearrange("(k p) -> p k", p=128), allow_slow_non_contiguous=True)), writes=[Rg])
    hnT = sb("hnT", [128, 8, NTH], BF16); RhnT = [R() for _ in range(18)]
    yT = nc.dram_tensor("yT_d", [16, 128, NT], BF16).ap(); RyT = [[R() for _ in range(4)] for _ in range(16)]
    U = sb("U", [128, 4096]); RU = R()
    xt = [sb("xt%d" % i, [128, 1024]) for i in range(2)]; Rxt = [R(), R()]
    yo = [sb("yo%d" % i, [128, 512], BF16) for i in range(2)]; Ryo = [R(), R()]
    ytl = [sb("ytl%d" % i, [128, 16, 128], BF16) for i in range(2)]; Rytl = [R(), R()]
    xn = sb("xn", [128, 1024], BF16); Rxn = R()
    sq = sb("sq", [128, 1024]); Rsq = R()
    st = sb("st", [128, 8]); Rst = R()
    stg = [sb("stg%d" % i, [128, 8, 256]) for i in range(2)]; Rstg = [R(), R()]
    wbf = [sb("wbf%d" % i, [128, 8, 512], BF16) for i in range(2)]; Rwbf = [R(), R()]
    wbig = sb("wbig", [128, 16, 1024], BF16); Rwbig = R()
    t1 = sb("t1", [128, 512]); Rt1 = R()
    t1b = sb("t1b", [128, 512]); t1s = [t1, t1b]; Rt1s = [Rt1, R()]
    t2b = sb("t2b", [128, 512]); t3b = sb("t3b", [128, 512])
    t2 = sb("t2", [128, 512]); Rt2 = R()
    t3 = sb("t3", [128, 512]); Rt3 = R()
    cw = sb("cw", [128, 8, 3]); Rcw = R()
    PS_a = ps("PS_a", [128, 512]); RPa = R()
    PS_b = ps("PS_b", [128, 512]); RPb = R()
    PS_c = ps("PS_c", [128, 512]); RPc = R()
    PS_d = ps("PS_d", [128, 512]); RPd = R()
    PS_t = ps("PS_t", [128, 8, 128], BF16); RPt = R()
    PS_m = ps("PS_m", [128, 8, 128]); RPm = R()
    for j_ in range(3):
        S.dma(("dma_start", dict(out=cw[:, :, j_], in_=I["e_conv_w"][j_].rearrange("(cb p) -> p cb", p=128), allow_slow_non_contiguous=True)), writes=[Rcw])

    def norm_tile(xtile, Rx, gt, dst_fn, Rdst, nvalid=128):
        op("act", ("activation", dict(out=sq[:], in_=xtile[:], func=AF.Square)), reads=[Rx], writes=[Rsq])
        op("dve", ("reduce_sum", dict(out=st[:, 0:1], in_=sq[:], axis=AX.X)), reads=[Rsq], writes=[Rst])
        op("dve", ("tensor_scalar", dict(out=st[:, 1:2], in0=st[:, 0:1], scalar1=1.0 / 1024, scalar2=1e-6, op0=ALU.mult, op1=ALU.add)), reads=[Rst], writes=[Rst])
        op("act", ("activation", dict(out=st[:, 2:3], in_=st[:, 1:2], func=AF.Sqrt)), reads=[Rst], writes=[Rst])
        op("dve", ("reciprocal", dict(out=st[:, 3:4], in_=st[:, 2:3])), reads=[Rst], writes=[Rst])
        op("dve", ("tensor_scalar", dict(out=xn[:], in0=xtile[:], scalar1=st[:, 3:4], scalar2=None, op0=ALU.mult)), reads=[Rx, Rst], writes=[Rxn])
        for k in range(8):
            op("pe", ("transpose", dict(out=PS_t[:, k, :], in_=xn[:, k * 128:(k + 1) * 128], identity=identb[:])), reads=[Rxn, Rid], writes=[RPt])
        dst_fn(gt)

    xh = I["xh"]
    for i in range(17):
        xb, Rx = xt[i % 2], Rxt[i % 2]
        if i < 16:
            S.dma(("dma_start", dict(out=xb[:], in_=xh[1 + 128 * i: 1 + 128 * (i + 1), :])), writes=[Rx])
            def dst(gt, i=i):
                op("dve", ("tensor_tensor", dict(out=hnT[:, :, 1 + 128 * i: 1 + 128 * (i + 1)], in0=PS_t[:], in1=gt[:].to_broadcast([128, 8, 128]), op=ALU.mult)), reads=[RPt, Rg], writes=[RhnT[i]])
        else:
            op("pool", ("memset", dict(ap=xb[:], constant=0.0)), writes=[Rx])
            S.dma(("dma_start", dict(out=xb[0:1, :], in_=xh[0:1, :])), writes=[Rx])
            S.dma(("dma_start", dict(out=xb[1:2, :], in_=xh[NT + 1:NT + 2, :])), writes=[Rx])
            def dst(gt):
                op("dve", ("tensor_tensor", dict(out=hnT[:, :, 0:1], in0=PS_t[:, :, 0:1], in1=gt[:], op=ALU.mult)), reads=[RPt, Rg], writes=[RhnT[16]])
                op("dve", ("tensor_tensor", dict(out=hnT[:, :, NT + 1:NT + 2], in0=PS_t[:, :, 1:2], in1=gt[:], op=ALU.mult)), reads=[RPt, Rg], writes=[RhnT[17]])
        norm_tile(xb, Rx, gE, dst, None)
    allh = RhnT

    wi = I["e_w_in"].rearrange("(k p) (s c) -> p k s c", p=128, c=1024)

    def load_w(buf, src4, nsp):
        for s_ in range(nsp):
            sb_ = s_ % 2
            S.dma(("dma_start", dict(out=stg[sb_][:, :, 0:128], in_=src4[:, :, s_, :])), writes=[Rstg[sb_]])
            op("pool", ("tensor_copy", dict(out=wbf[buf][:, :, s_ * 128:(s_ + 1) * 128], in_=stg[sb_][:, :, 0:128])), reads=[Rstg[sb_]], writes=[Rwbf[buf]])
        return wbf[buf][:, :, 0:nsp * 128].rearrange("p k (s c) -> p k s c", c=128)

    PSc0, RPc0, PSd0, RPd0 = PS_c, RPc, PS_d, RPd
    t2s = [t2, t2b]; Rt2s = [Rt2, R()]
    t3s = [t3, t3b]; Rt3s = [Rt3, R()]
    xc = U[:, 0:NTH]
    chunksA = [(0, 512), (512, 512), (1024, 512), (1536, 512), (2048, 2)]
    for cb in range(8):
        w4 = load_w(cb % 2, wi[:, :, 0:4, cb * 128:(cb + 1) * 128], 4)
        Rw = Rwbf[cb % 2]
        for ci_, (c0, n) in enumerate(chunksA):
            (PA, RA_), (PB, RB_) = (((PS_a, RPa), (PS_b, RPb)) if ci_ % 2 == 0 else ((PS_c, RPc), (PS_d, RPd)))
            for k in range(8):
                mm(PA[:, 0:n], w4[:, k, 0, :], hnT[:, k, c0:c0 + n], [Rw] + allh, [RA_], st=(k == 0), sp=(k == 7))
            for k in range(8):
                mm(PB[:, 0:n], w4[:, k, 2, :], hnT[:, k, c0:c0 + n], [Rw] + allh, [RB_], st=(k == 0), sp=(k == 7))
            t1_, Rt1_ = t1s[ci_ % 2], Rt1s[ci_ % 2]
            op("act", ("activation", dict(out=t1_[:, 0:n], in_=PA[:, 0:n], func=AF.Copy)), reads=[RA_], writes=[Rt1_])
            op("dve", ("tensor_tensor", dict(out=xc[:, c0:c0 + n], in0=PB[:, 0:n], in1=t1_[:, 0:n], op=ALU.mult)), reads=[RB_, Rt1_], writes=[RU])
        for j in range(4):
            c0 = 1 + 512 * j
            (PS_c, RPc), (PS_d, RPd) = ((PSc0, RPc0), (PSd0, RPd0)) if j % 2 == 1 else ((PS_a, RPa), (PS_b, RPb))
            t2, Rt2 = t2s[j % 2], Rt2s[j % 2]
            t3, Rt3 = t3s[j % 2], Rt3s[j % 2]
            for k in range(8):
                mm(PS_c[:], w4[:, k, 1, :], hnT[:, k, c0:c0 + 512], [Rw] + allh, [RPc], st=(k == 0), sp=(k == 7))
            for k in range(8):
                mm(PS_d[:], w4[:, k, 3, :], hnT[:, k, c0:c0 + 512], [Rw] + allh, [RPd], st=(k == 0), sp=(k == 7))
            op("dve", ("tensor_scalar", dict(out=t2[:], in0=xc[:, c0 - 1:c0 + 511], scalar1=cw[:, cb, 0:1], scalar2=None, op0=ALU.mult)), reads=[RU, Rcw], writes=[Rt2])
            op("dve", ("scalar_tensor_tensor", dict(out=t2[:], in0=xc[:, c0:c0 + 512], scalar=cw[:, cb, 1:2], in1=t2[:], op0=ALU.mult, op1=ALU.add)), reads=[RU, Rcw, Rt2], writes=[Rt2])
            op("dve", ("scalar_tensor_tensor", dict(out=t2[:], in0=xc[:, c0 + 1:c0 + 513], scalar=cw[:, cb, 2:3], in1=t2[:], op0=ALU.mult, op1=ALU.add)), reads=[RU, Rcw, Rt2], writes=[Rt2])
            op("act", ("activation", dict(out=t3[:], in_=PS_d[:], func=AF.Silu)), reads=[RPd], writes=[Rt3])
            op("dve", ("tensor_tensor", dict(out=t2[:], in0=PS_c[:], in1=t2[:], op=ALU.mult)), reads=[RPc, Rt2], writes=[Rt2])
            op("pool", ("tensor_tensor", dict(out=yo[j % 2][:], in0=t2[:], in1=t3[:], op=ALU.mult)), reads=[Rt2, Rt3], writes=[Ryo[j % 2]])
            S.dma(("dma_start", dict(out=yT[cb, :, 512 * j:512 * (j + 1)], in_=yo[j % 2][:])), reads=[Ryo[j % 2]], writes=[RyT[cb][j]])

    PS_c, RPc, PS_d, RPd = PSc0, RPc0, PSd0, RPd0
    t2, Rt2, t3, Rt3 = t2s[0], Rt2s[0], t3s[0], Rt3s[0]
    for hf in range(4):
        S.dma(("dma_start", dict(out=stg[hf % 2][:], in_=wi[:, :, 5, hf * 256:(hf + 1) * 256])), writes=[Rstg[hf % 2]])
        op("pool", ("tensor_copy", dict(out=wbig[:, 0:8, hf * 256:(hf + 1) * 256], in_=stg[hf % 2][:])), reads=[Rstg[hf % 2]], writes=[Rwbig])
    wsn = sb("wsn", [128, 8, 128]); wsnb = sb("wsnb", [128, 8, 128], BF16); wsT = sb("wsT", [128, 8, 128], BF16); Rws = R()
    S.dma(("dma_start", dict(out=wsn[:], in_=I["e_sgu_w"].rearrange("g i j -> i g j"))), writes=[Rws])
    op("dve", ("tensor_copy", dict(out=wsnb[:], in_=wsn[:])), reads=[Rws], writes=[Rws])
    for g in range(8):
        op("pe", ("transpose", dict(out=PS_t[:, g, :], in_=wsnb[:, g, :], identity=identb[:])), reads=[Rws, Rid], writes=[RPt])
    op("act", ("activation", dict(out=wsT[:], in_=PS_t[:], func=AF.Copy)), reads=[RPt], writes=[Rws])
    bsB = sb("bsB", [128, 8, 128]); lnG = sb("lnG", [128, 1024]); lnB = sb("lnB", [128, 1024]); Rbc = R()
    S.dma(("dma_start", dict(out=bsB[:].rearrange("p g i -> p (g i)"), in_=I["e_sgu_b"].rearrange("g i -> (g i)").partition_broadcast(128))), writes=[Rbc])
    S.dma(("dma_start", dict(out=lnG[:], in_=I["e_sgu_ln_g"].partition_broadcast(128))), writes=[Rbc])
    S.dma(("dma_start", dict(out=lnB[:], in_=I["e_sgu_ln_b"].partition_broadcast(128))), writes=[Rbc])
    vsb = sb("vsb", [128, 1024]); Rvsb = R()
    vnb = sb("vnb", [128, 1024], BF16); Rvnb = R()
    mixall = U[:, 0:4096].rearrange("p (g t) -> p g t", g=8)
    for tg in range(4):
        for ti in range(4):
            c0 = 1 + 128 * (4 * tg + ti)
            for hf, (P_, RP_) in enumerate((((PS_a, RPa), (PS_b, RPb)) if ti % 2 == 0 else ((PS_c, RPc), (PS_d, RPd)))):
                for k in range(8):
                    mm(P_[:], hnT[:, k, c0:c0 + 128], wbig[:, k, hf * 512:(hf + 1) * 512], [Rwbig] + allh, [RP_], st=(k == 0), sp=(k == 7))
                op("act", ("activation", dict(out=vsb[:, hf * 512:(hf + 1) * 512], in_=P_[:], func=AF.Copy)), reads=[RP_], writes=[Rvsb])
            op("act", ("activation", dict(out=sq[:], in_=vsb[:], func=AF.Square)), reads=[Rvsb], writes=[Rsq])
            op("dve", ("reduce_sum", dict(out=st[:, 0:1], in_=vsb[:], axis=AX.X)), reads=[Rvsb], writes=[Rst])
            op("dve", ("reduce_sum", dict(out=st[:, 1:2], in_=sq[:], axis=AX.X)), reads=[Rsq], writes=[Rst])
            op("dve", ("tensor_scalar", dict(out=st[:, 2:3], in0=st[:, 0:1], scalar1=1.0 / 1024, scalar2=None, op0=ALU.mult)), reads=[Rst], writes=[Rst])
            op("dve", ("tensor_tensor", dict(out=st[:, 3:4], in0=st[:, 2:3], in1=st[:, 2:3], op=ALU.mult)), reads=[Rst], writes=[Rst])
            op("dve", ("scalar_tensor_tensor", dict(out=st[:, 4:5], in0=st[:, 1:2], scalar=1.0 / 1024, in1=st[:, 3:4], op0=ALU.mult, op1=ALU.subtract)), reads=[Rst], writes=[Rst])
            op("dve", ("tensor_scalar", dict(out=st[:, 4:5], in0=st[:, 4:5], scalar1=1e-5, scalar2=None, op0=ALU.add)), reads=[Rst], writes=[Rst])
            op("act", ("activation", dict(out=st[:, 5:6], in_=st[:, 4:5], func=AF.Sqrt)), reads=[Rst], writes=[Rst])
            op("dve", ("reciprocal", dict(out=st[:, 6:7], in_=st[:, 5:6])), reads=[Rst], writes=[Rst])
            op("dve", ("tensor_scalar", dict(out=vsb[:], in0=vsb[:], scalar1=st[:, 2:3], scalar2=st[:, 6:7], op0=ALU.subtract, op1=ALU.mult)), reads=[Rvsb, Rst], writes=[Rvsb])
            op("dve", ("tensor_tensor", dict(out=vsb[:], in0=vsb[:], in1=lnG[:], op=ALU.mult)), reads=[Rvsb, Rbc], writes=[Rvsb])
            op("pool", ("tensor_tensor", dict(out=vnb[:], in0=vsb[:], in1=lnB[:], op=ALU.add)), reads=[Rvsb, Rbc], writes=[Rvnb])
            for g in range(8):
                mm(PS_m[:, g, :], vnb[:, g * 128:(g + 1) * 128], wsT[:, g, :], [Rvnb, Rws], [RPm])
            op("dve", ("tensor_tensor", dict(out=mixall[:, :, ti * 128:(ti + 1) * 128], in0=PS_m[:], in1=bsB[:], op=ALU.add)), reads=[RPm, Rbc], writes=[RU])
        c0 = 1 + 512 * tg
        for g in range(8):
            buf = g % 2
            S.dma(("dma_start", dict(out=stg[buf][:, :, 0:128], in_=wi[:, :, 4, g * 128:(g + 1) * 128])), writes=[Rstg[buf]])
            S.dma(("dma_start", dict(out=stg[buf][:, :, 128:256], in_=wi[:, :, 6, g * 128:(g + 1) * 128])), writes=[Rstg[buf]])
            op("pool", ("tensor_copy", dict(out=wbf[buf][:, :, 0:256], in_=stg[buf][:, :, 0:256])), reads=[Rstg[buf]], writes=[Rwbf[buf]])
            (PU, RPU), (PZ, RPZ) = ((PS_c, RPc), (PS_d, RPd)) if g % 2 == 0 else ((PS_a, RPa), (PS_b, RPb))
            t2, Rt2 = t2s[g % 2], Rt2s[g % 2]
            t3, Rt3 = t3s[g % 2], Rt3s[g % 2]
            for k in range(8):
                mm(PU[:], wbf[buf][:, k, 0:128], hnT[:, k, c0:c0 + 512], [Rwbf[buf]] + allh, [RPU], st=(k == 0), sp=(k == 7))
            for k in range(8):
                mm(PZ[:], wbf[buf][:, k, 128:256], hnT[:, k, c0:c0 + 512], [Rwbf[buf]] + allh, [RPZ], st=(k == 0), sp=(k == 7))
            op("act", ("activation", dict(out=t3[:], in_=PZ[:], func=AF.Silu)), reads=[RPZ], writes=[Rt3])
            op("dve", ("tensor_tensor", dict(out=t2[:], in0=PU[:], in1=mixall[:, g, :], op=ALU.mult)), reads=[RPU, RU], writes=[Rt2])
            op("pool", ("tensor_tensor", dict(out=yo[g % 2][:], in0=t2[:], in1=t3[:], op=ALU.mult)), reads=[Rt2, Rt3], writes=[Ryo[g % 2]])
            S.dma(("dma_start", dict(out=yT[8 + g, :, 512 * tg:512 * (tg + 1)], in_=yo[g % 2][:])), reads=[Ryo[g % 2]], writes=[RyT[8 + g][tg]])

    wo = I["e_w_out"].rearrange("(k p) n -> p k n", p=128)
    for q in range(2):
        for hf in range(4):
            S.dma(("dma_start", dict(out=stg[hf % 2][:], in_=wo[:, 8 * q:8 * q + 8, hf * 256:(hf + 1) * 256])), writes=[Rstg[hf % 2]])
            op("pool", ("tensor_copy", dict(out=wbig[:, 8 * q:8 * q + 8, hf * 256:(hf + 1) * 256], in_=stg[hf % 2][:])), reads=[Rstg[hf % 2]], writes=[Rwbig])
    ally = [r for row in RyT for r in row]
    h1ts = [sb("h1t%d" % i, [128, 1024]) for i in range(2)]; Rh1s = [R(), R()]
    for i in range(16):
        xb, Rx = xt[i % 2], Rxt[i % 2]
        h1t, Rh1 = h1ts[i % 2], Rh1s[i % 2]
        S.dma(("dma_start", dict(out=xb[:], in_=xh[1 + 128 * i: 1 + 128 * (i + 1), :])), writes=[Rx])
        S.dma(("dma_start", dict(out=ytl[i % 2][:], in_=yT[:, :, 128 * i:128 * (i + 1)].rearrange("k p t -> p k t"))), reads=ally, writes=[Rytl[i % 2]])
        for hf, (P_, RP_) in enumerate((((PS_a, RPa), (PS_b, RPb)) if i % 2 == 0 else ((PS_c, RPc), (PS_d, RPd)))):
            for k in range(16):
                mm(P_[:], ytl[i % 2][:, k, :], wbig[:, k, hf * 512:(hf + 1) * 512], [Rwbig, Rytl[i % 2]], [RP_], st=(k == 0), sp=(k == 15))
            op("dve", ("tensor_tensor", dict(out=h1t[:, hf * 512:(hf + 1) * 512], in0=P_[:], in1=xb[:, hf * 512:(hf + 1) * 512], op=ALU.add)), reads=[RP_, Rx], writes=[Rh1])
        S.dma(("dma_start", dict(out=O["h1"][128 * i:128 * (i + 1), :], in_=h1t[:])), reads=[Rh1])

        def dst(gt, i=i):
            op("dve", ("tensor_tensor", dict(out=hnT[:, :, 1 + 128 * i: 1 + 128 * (i + 1)], in0=PS_t[:], in1=gt[:].to_broadcast([128, 8, 128]), op=ALU.mult)), reads=[RPt, Rg], writes=[RhnT[i]])
        norm_tile(h1t, Rh1, gO, dst, None)

    wi1 = I["o_w_in"].rearrange("(k p) n -> p k n", p=128)
    blocks = [(c * 128, c * 128, False) for c in range(25)]
    blocks += [(3200 + c * 128, 3200 + c * 128, True) for c in range(8)]
    blocks += [(4736 + c * 128, 4224 + c * 128, True) for c in range(4)]
    ob = [sb("ob%d" % i, [128, 512]) for i in range(2)]; Rob = [R(), R()]
    oi = 0
    toks = []
    for bi, (sc, dr, act) in enumerate(blocks):
        buf = bi % 2
        S.dma(("dma_start", dict(out=stg[buf][:, :, 0:128], in_=wi1[:, :, sc:sc + 128])), writes=[Rstg[buf]])
        op("pool", ("tensor_copy", dict(out=wbf[buf][:, :, 0:128], in_=stg[buf][:, :, 0:128])), reads=[Rstg[buf]], writes=[Rwbf[buf]])
        for j in range(4):
            P_, RP_ = ((PS_a, RPa), (PS_b, RPb), (PS_c, RPc), (PS_d, RPd))[j]
            c0 = 1 + 512 * j
            for k in range(8):
                mm(P_[:], wbf[buf][:, k, 0:128], hnT[:, k, c0:c0 + 512], [Rwbf[buf]] + allh, [RP_], st=(k == 0), sp=(k == 7))
            o_, Ro = ob[oi % 2], Rob[oi % 2]
            oi += 1
            op("act", ("activation", dict(out=o_[:], in_=P_[:], func=(AF.Silu if act else AF.Copy))), reads=[RP_], writes=[Ro])
            toks.append(S.dma(("dma_start", dict(out=O["pT"][dr:dr + 128, 512 * j:512 * (j + 1)], in_=o_[:])), reads=[Ro]))
    for hf in range(2):
        S.dma(("dma_start", dict(out=stg[hf][:], in_=wi1[:, :, 4224 + 256 * hf:4224 + 256 * (hf + 1)])), writes=[Rstg[hf]])
        op("pool", ("tensor_copy", dict(out=wbf[0][:, :, 256 * hf:256 * (hf + 1)], in_=stg[hf][:])), reads=[Rstg[hf]], writes=[Rwbf[0]])
    for i in range(16):
        c0 = 1 + 128 * i
        P_, RP_ = ((PS_a, RPa), (PS_b, RPb))[i % 2]
        for k in range(8):
            mm(P_[:], hnT[:, k, c0:c0 + 128], wbf[0][:, k, :], [Rwbf[0]] + allh, [RP_], st=(k == 0), sp=(k == 7))
        o_, Ro = ob[oi % 2], Rob[oi % 2]
        oi += 1
        op("act", ("activation", dict(out=o_[:], in_=P_[:], func=AF.Copy)), reads=[RP_], writes=[Ro])
        toks.append(S.dma(("dma_start", dict(out=O["fd"][128 * i:128 * (i + 1), :], in_=o_[:])), reads=[Ro]))
    return toks


def full_barrier(S):
    keys = list(S.cnt.items())
    for e in S.ENGS:
        waits = []
        for k, v in keys:
            if k == e:
                continue
            if S.seen[e].get(k, 0) < v:
                S.seen[e][k] = v
                waits.append((k, v))
        if waits:
            S.prog[e].append([waits, None, ("_none", 0)])


def emit_fnet(S, nc, I, ydT):
    R = Region
    op = S.op
    mm = lambda out, l, r_, rd, wr, st=True, sp=True: op("pe", ("matmul", dict(out=out, lhsT=l, rhs=r_, start=st, stop=sp)), reads=rd, writes=wr)
    toks = []
    with ExitStack() as es:
        sb = lambda name, shape, dt=F32: es.enter_context(nc.sbuf_tensor(name, shape, dt))
        ps = lambda name, shape, dt=F32: es.enter_context(nc.psum_tensor(name, shape, dt))
        xs = sb("f_xs", [128, 4096]); Rxs = R()
        xb = sb("f_xb", [128, 64, 128], BF16); Rxb = R()
        Fb = sb("f_F", [128, 256], BF16); RF = R()
        A_sb = sb("f_A", [64, 128, 256], BF16); RA = R()
        PQ = sb("f_PQ", [128, 2, 64, 128], BF16); RPQ = R()
        Tg = [[sb("f_T%d%d" % (i, j), [64, 16, 128], BF16) for j in range(2)] for i in range(2)]; RTg = [R(), R()]
        wf32 = sb("f_w32", [128, 128]); wfb = sb("f_wb", [128, 128], BF16); Rwf = R()
        Ccb = sb("f_Cc", [128, 128], BF16); mScb = sb("f_mSc", [128, 128], BF16); Rcs = R()
        Gb = sb("f_G", [128, 256], BF16); RG = R()
        ob = [sb("f_ob%d" % i, [128, 512]) for i in range(2)]; Rob = [R(), R()]
        PS = [ps("f_ps%d" % i, [128, 512]) for i in range(2)]; RPS = [R(), R()]
        S.dma(("dma_start", dict(out=Fb[:], in_=I["c_F"])), writes=[RF])
        S.dma(("dma_start", dict(out=Ccb[:], in_=I["c_Cc"])), writes=[Rcs])
        S.dma(("dma_start", dict(out=mScb[:], in_=I["c_mSc"])), writes=[Rcs])
        S.dma(("dma_start", dict(out=wf32[:], in_=I["fw"])), writes=[Rwf])
        op("dve", ("tensor_copy", dict(out=wfb[:], in_=wf32[:])), reads=[Rwf], writes=[Rwf])
        xbf = xb[:].rearrange("p l c -> p (l c)")
        for hf in range(2):
            S.dma(("dma_start", dict(out=xs[:], in_=I["fx"][:, hf * 4096:(hf + 1) * 4096])), writes=[Rxs])
            op("pool", ("tensor_copy", dict(out=xbf[:, hf * 4096:(hf + 1) * 4096], in_=xs[:])), reads=[Rxs], writes=[Rxb])
        for c2 in range(64):
            P_, RP_ = PS[c2 % 2], RPS[c2 % 2]
            for j in range(2):
                mm(P_[0:64, j * 256:(j + 1) * 256], xb[:, :, 2 * c2 + j], Fb[:], [Rxb, RF], [RP_])
            op("act" if c2 % 2 == 0 else "dve", ("activation", dict(out=A_sb[0:64, 2 * c2:2 * c2 + 2, :], in_=P_[0:64, :].rearrange("p (j k) -> p j k", j=2), func=AF.Copy)) if c2 % 2 == 0 else
               ("tensor_copy", dict(out=A_sb[0:64, 2 * c2:2 * c2 + 2, :], in_=P_[0:64, :].rearrange("p (j k) -> p j k", j=2))), reads=[RP_], writes=[RA])
        T1d = I["c_T1"].rearrange("p (k h) -> p k h", h=128)
        T2d = I["c_T2"].rearrange("p (k h) -> p k h", h=128)
        ei = 0
        for grp in range(8):
            tb = grp % 2
            S.dma(("dma_start", dict(out=Tg[tb][0][:], in_=T1d[:, grp * 16:(grp + 1) * 16, :])), writes=[RTg[tb]])
            S.dma(("dma_start", dict(out=Tg[tb][1][:], in_=T2d[:, grp * 16:(grp + 1) * 16, :])), writes=[RTg[tb]])
            for q in range(4):
                P_, RP_ = PS[ei % 2], RPS[ei % 2]
                for j in range(4):
                    kk_ = q * 4 + j
                    kl = grp * 16 + kk_
                    mm(P_[:, j * 128:(j + 1) * 128], A_sb[0:64, :, kl], Tg[tb][0][0:64, kk_, :], [RA, RTg[tb]], [RP_], st=True, sp=False)
                    mm(P_[:, j * 128:(j + 1) * 128], A_sb[0:64, :, 128 + kl], Tg[tb][1][0:64, kk_, :], [RA, RTg[tb]], [RP_], st=False, sp=True)
                kl0 = grp * 16 + q * 4
                for qq in range(2):
                    op("act" if qq == 0 else "dve",
                       ("activation", dict(out=PQ[:, qq, :, kl0:kl0 + 4].rearrange("p h l -> p l h"), in_=P_[:].rearrange("p (l q h) -> p l q h", l=4, q=2)[:, :, qq, :], func=AF.Copy)) if qq == 0 else
                       ("tensor_copy", dict(out=PQ[:, qq, :, kl0:kl0 + 4].rearrange("p h l -> p l h"), in_=P_[:].rearrange("p (l q h) -> p l q h", l=4, q=2)[:, :, qq, :])),
                       reads=[RP_], writes=[RPQ])
                ei += 1
        P_, RP_ = PS[0], RPS[0]
        mm(P_[:, 0:128], Ccb[:], wfb[:], [Rcs, Rwf], [RP_])
        mm(P_[:, 128:256], mScb[:], wfb[:], [Rcs, Rwf], [RP_])
        op("act", ("activation", dict(out=Gb[:], in_=P_[:, 0:256], func=AF.Copy)), reads=[RP_], writes=[RG])
        for t4 in range(16):
            P_, RP_ = PS[(t4 + 1) % 2], RPS[(t4 + 1) % 2]
            for j in range(4):
                kh = 4 * t4 + j
                mm(P_[:, j * 128:(j + 1) * 128], Gb[:, 0:128], PQ[:, 0, kh, :], [RG, RPQ], [RP_], st=True, sp=False)
                mm(P_[:, j * 128:(j + 1) * 128], Gb[:, 128:256], PQ[:, 1, kh, :], [RG, RPQ], [RP_], st=False, sp=True)
            o_, Ro = ob[t4 % 2], Rob[t4 % 2]
            op("act", ("activation", dict(out=o_[:], in_=P_[:], func=AF.Copy)), reads=[RP_], writes=[Ro])
            toks.append(S.dma(("dma_start", dict(out=ydT[:, 512 * t4:512 * (t4 + 1)], in_=o_[:])), reads=[Ro]))
    full_barrier(S)
    return toks


def emit_p3(S, nc, I, yout):
    sb = lambda name, shape, dt=F32: nc.alloc_sbuf_tensor(name, shape, dt)
    ps = lambda name, shape, dt=F32: nc.alloc_psum_tensor(name, shape, dt)
    R = Region
    op = S.op
    mm = lambda out, l, r_, rd, wr, st=True, sp=True: op("pe", ("matmul", dict(out=out, lhsT=l, rhs=r_, start=st, stop=sp)), reads=rd, writes=wr)
    stg = [sb("stg%d" % i, [128, 8, 256]) for i in range(2)]; Rstg = [R(), R()]
    wO = sb("wO", [128, 12, 1024], BF16); RwO = R()
    gN = sb("gN", [128, 1024]); RgN = R()
    gt_all = sb("gt_all", [128, 12, 2048], BF16); Rgt = R()
    ya = [sb("ya%d" % i, [128, 512]) for i in range(2)]; Rya = [R(), R()]
    ga = [sb("ga%d" % i, [128, 512]) for i in range(2)]; Rga = [R(), R()]
    h1t = [sb("h1t%d" % i, [128, 1024]) for i in range(2)]; Rh1 = [R(), R()]
    h2 = sb("h2", [128, 1024]); Rh2 = R()
    sq = sb("sq", [128, 1024]); Rsq = R()
    st = sb("st", [128, 8]); Rst = R()
    yo = [sb("yo%d" % i, [128, 1024]) for i in range(2)]; Ryo = [R(), R()]
    PS_a = ps("PS_a", [128, 512]); RPa = R()
    PS_b = ps("PS_b", [128, 512]); RPb = R()
    wo3 = I["o_w_out"].rearrange("(k p) n -> p k n", p=128)
    si = 0
    for (k0, nk) in ((0, 8), (8, 4)):
        for cq in range(4):
            b_ = si % 2; si += 1
            S.dma(("dma_start", dict(out=stg[b_][:, 0:nk, :], in_=wo3[:, k0:k0 + nk, cq * 256:(cq + 1) * 256])), writes=[Rstg[b_]])
            op("pool", ("tensor_copy", dict(out=wO[:, k0:k0 + nk, cq * 256:(cq + 1) * 256], in_=stg[b_][:, 0:nk, :])), reads=[Rstg[b_]], writes=[RwO])
    S.dma(("dma_start", dict(out=gN[:], in_=I["final_norm_g"].partition_broadcast(128))), writes=[RgN])
    ii = 0
    for blk in range(12):
        src = I["ycT"][blk * 128:(blk + 1) * 128] if blk < 8 else I["ydT"][(blk - 8) * 128:(blk - 7) * 128]
        gsrc = I["gT"][blk * 128:(blk + 1) * 128]
        for j in range(4):
            b_ = ii % 2; ii += 1
            S.dma(("dma_start", dict(out=ya[b_][:], in_=src[:, 512 * j:512 * (j + 1)])), writes=[Rya[b_]])
            S.dma(("dma_start", dict(out=ga[b_][:], in_=gsrc[:, 512 * j:512 * (j + 1)])), writes=[Rga[b_]])
            op("dve" if ii % 2 else "pool", ("tensor_tensor", dict(out=gt_all[:, blk, 512 * j:512 * (j + 1)], in0=ya[b_][:], in1=ga[b_][:], op=ALU.mult)), reads=[Rya[b_], Rga[b_]], writes=[Rgt])
    toks = []
    for i in range(16):
        hb, Rh = h1t[i % 2], Rh1[i % 2]
        S.dma(("dma_start", dict(out=hb[:], in_=I["h1"][128 * i:128 * (i + 1), :])), writes=[Rh])
        for hf, (P_, RP_) in enumerate(((PS_a, RPa), (PS_b, RPb))):
            for k in range(12):
                mm(P_[:], gt_all[:, k, 128 * i:128 * (i + 1)], wO[:, k, hf * 512:(hf + 1) * 512], [Rgt, RwO], [RP_], st=(k == 0), sp=(k == 11))
            op("dve", ("tensor_tensor", dict(out=h2[:, hf * 512:(hf + 1) * 512], in0=P_[:], in1=hb[:, hf * 512:(hf + 1) * 512], op=ALU.add)), reads=[RP_, Rh], writes=[Rh2])
        op("act", ("activation", dict(out=sq[:], in_=h2[:], func=AF.Square)), reads=[Rh2], writes=[Rsq])
        op("dve", ("reduce_sum", dict(out=st[:, 0:1], in_=sq[:], axis=AX.X)), reads=[Rsq], writes=[Rst])
        op("dve", ("tensor_scalar", dict(out=st[:, 1:2], in0=st[:, 0:1], scalar1=1.0 / 1024, scalar2=1e-6, op0=ALU.mult, op1=ALU.add)), reads=[Rst], writes=[Rst])
        op("act", ("activation", dict(out=st[:, 2:3], in_=st[:, 1:2], func=AF.Sqrt)), reads=[Rst], writes=[Rst])
        op("dve", ("reciprocal", dict(out=st[:, 3:4], in_=st[:, 2:3])), reads=[Rst], writes=[Rst])
        op("dve", ("tensor_scalar", dict(out=h2[:], in0=h2[:], scalar1=st[:, 3:4], scalar2=None, op0=ALU.mult)), reads=[Rh2, Rst], writes=[Rh2])
        o_, Ro = yo[i % 2], Ryo[i % 2]
        op("pool", ("tensor_tensor", dict(out=o_[:], in0=h2[:], in1=gN[:], op=ALU.mult)), reads=[Rh2, RgN], writes=[Ro])
        toks.append(S.dma(("dma_start", dict(out=yout[128 * i:128 * (i + 1), :], in_=o_[:])), reads=[Ro]))
    return toks


def _mk(nc, name, shape, dt=None, out=False):
    return nc.dram_tensor(name, list(shape), dt or F32, kind=("ExternalOutput" if out else "ExternalInput")).ap()


W1 = ["e_norm_g", "e_w_in", "e_conv_w", "e_sgu_ln_g", "e_sgu_ln_b", "e_sgu_w", "e_sgu_b", "e_w_out", "o_norm_g", "o_w_in"]


def build_l1(shapes):
    nc = bass.Bass("TRN2", target_bir_lowering=False)
    I = {"xh": _mk(nc, "xh", [2050, 1024]), "c_ident": _mk(nc, "c_ident", [128, 128])}
    for n in W1:
        I[n] = _mk(nc, n, shapes[n])
    O = {"h1": _mk(nc, "h1", [2048, 1024], out=True), "pT": _mk(nc, "pT", [4736, 2048], out=True),
         "fd": _mk(nc, "fd", [2048, 512], out=True)}
    S = Sched(nc)
    toks = emit_p1(S, nc, I, O)
    S.barrier_on("sp", toks)
    S.finalize()
    return nc


def build_l2(consts):
    NB, T = 2, 8192
    nc = bass.Bass("TRN2", target_bir_lowering=False)
    pr, pk, pv, pwa = (_mk(nc, n, [128, NB, T + 2]) for n in ("pr", "pk", "pv", "pwa"))
    prm = _mk(nc, "prm", [128, 17]); w2a2 = _mk(nc, "w2a2", [128, 2, 128])
    A = {k: _mk(nc, k, v.shape) for k, v in consts.items()}
    FI = {"fx": _mk(nc, "fx", [128, 8192]), "fw": _mk(nc, "fw", [128, 128]),
          "c_F": _mk(nc, "c_F", [128, 256], BF16), "c_T1": _mk(nc, "c_T1", [64, 16384], BF16),
          "c_T2": _mk(nc, "c_T2", [64, 16384], BF16), "c_Cc": _mk(nc, "c_Cc", [128, 128], BF16),
          "c_mSc": _mk(nc, "c_mSc", [128, 128], BF16)}
    yout = _mk(nc, "yout", [128, NB, T], out=True)
    ydT = _mk(nc, "ydT", [128, T], out=True)
    S = Sched(nc)
    toks = emit_fnet(S, nc, FI, ydT)
    toks += emit_rwkv(S, nc, A, pr, pk, pv, pwa, prm, w2a2, yout, NB, T)
    S.barrier_on("sp", toks)
    S.finalize()
    return nc


def build_l3():
    nc = bass.Bass("TRN2", target_bir_lowering=False)
    I = {"ycT": _mk(nc, "ycT", [1024, 2048]), "ydT": _mk(nc, "ydT", [512, 2048]), "gT": _mk(nc, "gT", [1536, 2048]),
         "h1": _mk(nc, "h1", [2048, 1024]), "o_w_out": _mk(nc, "o_w_out", [1536, 1024]),
         "final_norm_g": _mk(nc, "final_norm_g", [1024])}
    y = _mk(nc, "y", [2048, 1024], out=True)
    S = Sched(nc)
    toks = emit_p3(S, nc, I, y)
    S.barrier_on("sp", toks)
    S.finalize()
    return nc


def fnet_tables():
    import ml_dtypes
    N = 8192
    nh = np.arange(128); kl = np.arange(128)
    ang = 2 * np.pi * np.outer(nh, kl) / 128
    F = np.concatenate([np.cos(ang), np.sin(ang)], axis=1)
    nl = np.arange(64)[:, None, None]; klo = np.arange(128)[None, :, None]; kh = np.arange(64)[None, None, :]
    beta = 2 * np.pi * ((nl * (klo + 128 * kh)) % N) / N
    T1 = np.concatenate([np.cos(beta), np.sin(beta)], axis=2).reshape(64, 16384)
    T2 = np.concatenate([-np.sin(beta), np.cos(beta)], axis=2).reshape(64, 16384)
    c = np.arange(128); phi = 2 * np.pi * np.outer(c, c) / 128
    nrm = 1 / np.sqrt(N * 128)
    bf = lambda a: np.ascontiguousarray(a.astype(np.float32)).astype(ml_dtypes.bfloat16)
    return {"c_F": bf(F), "c_T1": bf(T1), "c_T2": bf(T2), "c_Cc": bf(np.cos(phi) * nrm), "c_mSc": bf(-np.sin(phi) * nrm)}


def kernel(**inputs):
    f32 = lambda a: np.ascontiguousarray(np.asarray(a), dtype=np.float32)
    inp = {k: f32(v) for k, v in inputs.items()}
    x = inp["x"]
    ncores = 8
    cores = list(range(ncores))
    w1 = {n: np.ascontiguousarray(inp[n][0]) for n in W1}
    ident = np.eye(128, dtype=np.float32)
    maps = []
    for c in cores:
        b, s0 = c // 4, (c % 4) * 2048
        xh = np.zeros((2050, 1024), np.float32)
        xh[1:2049] = x[b, s0:s0 + 2048]
        if s0 > 0:
            xh[0] = x[b, s0 - 1]
        if s0 + 2048 < 8192:
            xh[2049] = x[b, s0 + 2048]
        m = {"xh": xh, "c_ident": ident}
        m.update(w1)
        maps.append(m)
    nc1 = build_l1({n: w1[n].shape for n in W1})
    r1 = run_bass_kernel_spmd(nc1, maps, core_ids=cores).results
    PT = np.concatenate([np.asarray(r["pT"]) for r in r1], axis=1)
    FD = np.concatenate([np.asarray(r["fd"]) for r in r1], axis=0)
    consts = build_consts_np()
    ft = fnet_tables()
    mu, w0, w2, a0, a2 = inp["o_mu"][0], inp["o_w0"][0], inp["o_w2"][0], inp["o_a0"][0], inp["o_a2"][0]
    k_k, k_a, r_k = inp["o_k_k"][0], inp["o_k_a"][0], inp["o_r_k"][0].reshape(-1)
    lg, lb = inp["o_lnx_g"][0], inp["o_lnx_b"][0]
    PT3 = PT.reshape(4736, 2, 8192)
    pad = lambda a: np.ascontiguousarray(np.pad(a, ((0, 0), (0, 0), (1, 1))))
    maps = []
    for c in cores:
        ch = slice(c * 128, (c + 1) * 128)
        m = {"pr": pad(PT3[0:1024][ch]), "pk": pad(PT3[1024:2048][ch]), "pv": pad(PT3[2048:3072][ch]),
             "pwa": pad(PT3[3072:3200])}
        prm = np.zeros((128, 17), np.float32)
        for d in range(2):
            prm[:, 0 + d] = mu[d, 0:1024][ch]; prm[:, 2 + d] = mu[d, 1024:2048][ch]; prm[:, 4 + d] = mu[d, 2048:3072][ch]
            prm[:, 6 + d] = mu[d, 3072:3200]; prm[:, 8 + d] = w0[d][ch]; prm[:, 10 + d] = a0[d][ch]
        prm[:, 12] = k_k[ch]; prm[:, 13] = k_a[ch]; prm[:, 14] = r_k[ch]; prm[:, 15] = lg[ch]; prm[:, 16] = lb[ch]
        m["prm"] = prm
        m["w2a2"] = np.ascontiguousarray(np.concatenate([w2[:, :, ch], a2[:, :, ch]], axis=1).transpose(1, 0, 2))
        m.update(consts)
        b, g = c // 4, c % 4
        m["fx"] = np.ascontiguousarray(FD[b * 8192:(b + 1) * 8192, g * 128:(g + 1) * 128]).reshape(128, 8192)
        m["fw"] = np.ascontiguousarray(inp["o_fnet_w"][0, g])
        m.update(ft)
        maps.append(m)
    nc2 = build_l2(consts)
    r2 = run_bass_kernel_spmd(nc2, maps, core_ids=cores).results
    YC = np.concatenate([np.asarray(r["yout"]).reshape(128, 16384) for r in r2], axis=0)
    YD = np.concatenate([np.concatenate([np.asarray(r2[b * 4 + g]["ydT"]) for g in range(4)], axis=0) for b in range(2)], axis=1)
    maps = []
    for c in cores:
        ts = slice(c * 2048, (c + 1) * 2048)
        maps.append({"ycT": np.ascontiguousarray(YC[:, ts]), "ydT": np.ascontiguousarray(YD[:, ts]),
                     "gT": np.ascontiguousarray(PT[3200:4736, ts]), "h1": np.asarray(r1[c]["h1"]),
                     "o_w_out": np.ascontiguousarray(inp["o_w_out"][0]), "final_norm_g": inp["final_norm_g"]})
    nc3 = build_l3()
    r3 = run_bass_kernel_spmd(nc3, maps, core_ids=cores).results
    y = np.concatenate([np.asarray(r["y"]) for r in r3], axis=0).reshape(2, 8192, 1024)
    return y.astype(np.float32)
```

```python
from contextlib import ExitStack
import itertools
import numpy as np
import concourse.bass as bass
import concourse.mybir as mybir
from concourse.bass_utils import run_bass_kernel_spmd


F32 = mybir.dt.float32
BF16 = mybir.dt.bfloat16
AF = mybir.ActivationFunctionType
ALU = mybir.AluOpType
AX = mybir.AxisListType

N_DMA_SEMS = 8


class Region:
    __slots__ = ("w", "r", "name")

    def __init__(self, name=""):
        self.w = None
        self.r = {}
        self.name = name


class Sched:
    ENGS = ("pe", "dve", "act", "pool", "sp")

    def __init__(self, nc):
        self.nc = nc
        self.prog = {e: [] for e in self.ENGS}
        self.cnt = {}
        self.seen = {e: {} for e in self.ENGS}
        self.dma_rr = {e: 0 for e in self.ENGS}
        self.dma_last = {}
        self.same_engine_raw = True
        self.cut = 0
        self.nrec = 0
        self.log = []

    def _collect(self, eng, mykey, reads, writes):
        waits = {}

        def need(tok, kind):
            if tok is None:
                return
            k, v = tok
            if k == mykey:
                if eng == "pe":
                    return
                if not self.same_engine_raw:
                    return
            if waits.get(k, 0) < v:
                waits[k] = v

        for R in reads:
            need(R.w, "raw")
        for R in writes:
            need(R.w, "waw")
            for k, v in R.r.items():
                need((k, v), "war")
        out = []
        seen = self.seen[eng]
        for k, v in waits.items():
            if seen.get(k, 0) < v:
                seen[k] = v
                out.append((k, v))
        return out

    def _commit(self, tok, reads, writes):
        for R in writes:
            R.w = tok
            R.r = {}
        k, v = tok
        for R in reads:
            if R.r.get(k, 0) < v:
                R.r[k] = v

    def op(self, eng, fn, reads=(), writes=()):
        self.nrec += 1
        if self.cut and self.nrec > self.cut:
            return None
        if self.cut:
            self.log.append((self.nrec, eng, fn[0] if isinstance(fn, tuple) else "fn", str(fn[1].get("out", ""))[:120] if isinstance(fn, tuple) else ""))
        key = eng
        waits = self._collect(eng, key, reads, writes)
        idx = self.cnt.get(key, 0) + 1
        self.cnt[key] = idx
        tok = (key, idx)
        self.prog[eng].append([waits, fn, tok])
        self._commit(tok, reads, writes)
        return tok

    def dma(self, fn, reads=(), writes=(), q="sp"):
        self.nrec += 1
        if self.cut and self.nrec > self.cut:
            return None
        i = self.dma_rr[q]
        self.dma_rr[q] = (i + 1) % N_DMA_SEMS
        key = "dma_%s_%d" % (q, i)
        waits = self._collect(q, key, reads, writes)
        prev = self.cnt.get(key, 0)
        if prev > 0 and self.seen[q].get(key, 0) < prev:
            self.seen[q][key] = prev
            waits.append((key, prev))
        idx = prev + 1
        self.cnt[key] = idx
        tok = (key, idx)
        self.prog[q].append([waits, fn, tok])
        self._commit(tok, reads, writes)
        return tok

    def finalize(self):
        nc = self.nc
        waited = {}
        for e in self.ENGS:
            for waits, fn, tok in self.prog[e]:
                for k, v in waits:
                    waited.setdefault(k, set()).add(v)
        self.final_waits = []
        sem_of = {}
        val_of = {}
        for k, s in waited.items():
            sem_of[k] = nc.alloc_semaphore("s_" + k)
            isdma = k.startswith("dma_")
            step = 16 if isdma else 1
            if isdma:
                val_of[k] = None
            else:
                val_of[k] = {v: (i + 1) for i, v in enumerate(sorted(s))}
        engobj = {"pe": nc.tensor, "dve": nc.vector, "act": nc.scalar,
                  "pool": nc.gpsimd, "sp": nc.sync}

        def value(k, v):
            if val_of[k] is None:
                return 16 * v
            return val_of[k][v]

        def emit(e):
            def body(eng):
                for waits, fn, tok in self.prog[e]:
                    for k, v in waits:
                        eng.wait_ge(sem_of[k], value(k, v))
                    if fn is None:
                        continue
                    if isinstance(fn, tuple):
                        ins = getattr(eng, fn[0])(**fn[1])
                    else:
                        ins = fn(eng)
                    k, v = tok
                    if k in sem_of:
                        if val_of[k] is None:
                            ins.then_inc(sem_of[k], 16)
                        elif v in val_of[k]:
                            ins.then_inc(sem_of[k], 1)
            return body

        with nc.Block() as block:
            for e, dec in (("sp", block.sync), ("pe", block.tensor), ("dve", block.vector),
                           ("act", block.scalar), ("pool", block.gpsimd)):
                if self.prog[e]:
                    dec(emit(e))
        self.n_sems = len(sem_of)
        return self.n_sems

    def barrier_on(self, eng, toks):
        waits = []
        for tk in toks:
            if tk is None:
                continue
            k, v = tk
            if self.seen[eng].get(k, 0) < v:
                self.seen[eng][k] = v
                waits.append((k, v))
        if waits:
            self.prog[eng].append([waits, None, ("_none", 0)])


C = 128
BLK = 512
NEG_E = -float(np.exp(-0.5))
GN_EPS = 64e-5


def build_consts_np():
    idx = np.arange(128)
    lt = (idx[:, None] < idx[None, :]).astype(np.float32)
    le = (idx[:, None] <= idx[None, :]).astype(np.float32)
    gt = lt.T.copy()
    ge = le.T.copy()
    m4f = np.stack([lt, gt, gt, le], axis=1)
    m4b = np.stack([gt, lt, lt, ge], axis=1)
    mk = np.stack([le, ge], axis=1)
    ident = np.eye(128, dtype=np.float32)
    bd = np.kron(np.eye(2, dtype=np.float32), np.ones((64, 64), np.float32))
    scanm = np.ones((128, BLK), np.float32)
    scanm[:, ::C] = 0.0
    return {"c_m4": np.stack([m4f, m4b], axis=1).reshape(128, 2 * 4 * 128).copy(),
            "c_mk": mk.reshape(128, 256).copy(), "c_ident": ident, "c_bd": bd, "c_scanm": scanm}


XST = False


def emit_rwkv(S, nc, A, pr, pk, pv, pwa, prm, w2a2, yout, NB, T):
    sb = lambda name, shape, dt=F32: nc.alloc_sbuf_tensor(name, shape, dt)
    ps = lambda name, shape, dt=F32: nc.alloc_psum_tensor(name, shape, dt)
    R = Region
    nblk = T // BLK

    m4f = sb("m4f", [128, 2, 4, 128]); Rm4 = R()
    mkf = sb("mkf", [128, 2, 128]); Rmk = R()
    identf = sb("identf", [128, 128]); Ridf = R()
    identb = sb("identb", [128, 128], BF16); Ridb = R()
    bdf = sb("bdf", [128, 128]); Rbd = R()
    bdr = sb("bdr", [128, 128]); Rbdr = R()
    bdm = sb("bdm", [128, 128]); Rbdm = R()
    scanm = sb("scanm", [128, BLK]); Rsc = R()
    prmt = sb("prmt", [128, 17]); Rprm = R()
    w2f = sb("w2f", [128, 2, 128]); Rw2f = R()
    w2b = sb("w2b", [128, 2, 128], BF16); Rw2b = R()
    S.dma(("dma_start", dict(out=m4f[:].rearrange("p a b c -> p (a b c)"), in_=A["c_m4"])), writes=[Rm4])
    S.dma(("dma_start", dict(out=mkf[:].rearrange("p a c -> p (a c)"), in_=A["c_mk"])), writes=[Rmk])
    S.dma(("dma_start", dict(out=identf[:], in_=A["c_ident"])), writes=[Ridf])
    S.dma(("dma_start", dict(out=bdf[:], in_=A["c_bd"])), writes=[Rbd])
    S.dma(("dma_start", dict(out=scanm[:], in_=A["c_scanm"])), writes=[Rsc])
    S.dma(("dma_start", dict(out=prmt[:], in_=prm)), writes=[Rprm])
    S.dma(("dma_start", dict(out=w2f[:], in_=w2a2)), writes=[Rw2f])
    S.op("dve", ("tensor_copy", dict(out=identb[:], in_=identf[:])), reads=[Ridf], writes=[Ridb])
    S.op("dve", ("tensor_copy", dict(out=w2b[:], in_=w2f[:])), reads=[Rw2f], writes=[Rw2b])
    PM = lambda c: prmt[:, c:c + 1]
    S.op("dve", ("tensor_scalar", dict(out=bdr[:], in0=bdf[:], scalar1=PM(14), scalar2=None, op0=ALU.mult)), reads=[Rbd, Rprm], writes=[Rbdr])
    S.op("dve", ("tensor_scalar", dict(out=bdm[:], in0=bdf[:], scalar1=1.0 / 64, scalar2=None, op0=ALU.mult)), reads=[Rbd], writes=[Rbdm])

    def T2(name, dt=F32, n=BLK):
        return sb(name, [128, n], dt), R()
    ld = {}
    for nm in ("pr", "pk", "pv", "pwa"):
        ld[nm] = (sb("ld_" + nm, [128, BLK + 2]), R())
    tmp, Rtmp = T2("tmp")
    qr, Rqr = T2("qr"); qk, Rqk = T2("qk"); qv, Rqv = T2("qv"); qwa, Rqwa = T2("qwa")
    twa, Rtwa = T2("twa", BF16)
    sw, Rsw = T2("sw"); asg, Rasg = T2("asg")
    logw, Rlogw = T2("logw"); lin, Rlin = T2("lin"); linm, Rlinm = T2("linm"); lexm, Rlexm = T2("lexm")
    lex, Rlex = T2("lex"); lint, Rlint = T2("lint")
    e1, Re1 = T2("e1"); e1x, Re1x = T2("e1x"); e2, Re2 = T2("e2"); e3, Re3 = T2("e3"); e3x, Re3x = T2("e3x"); e4, Re4 = T2("e4")
    kk, Rkk = T2("kk"); kk2, Rkk2 = T2("kk2"); rin, Rrin = T2("rin"); kkn, Rkkn = T2("kkn")
    kp, Rkp = T2("kp"); bv, Rbv = T2("bv"); rk, Rrk = T2("rk")
    rt, Rrt = T2("rt", BF16); at, Rat = T2("at", BF16); kt, Rkt = T2("kt", BF16); bt, Rbt = T2("bt", BF16)
    r0, Rr0 = T2("r0"); a0b, Ra0b = T2("a0b", BF16); kEb, RkEb = T2("kEb", BF16); bEb, RbEb = T2("bEb", BF16)
    qvb, Rqvb = T2("qvb", BF16)
    ysum = sb("ysum", [128, T]); Rys = [R() for _ in range(T // C)]
    bsum = sb("bsum", [128, T]); Rbs = [R() for _ in range(nblk)]
    TT = [sb("TT%d" % i, [128, 4, 128], BF16) for i in range(2)]; RTT = [R(), R()]
    SBM = [sb("SBM%d" % i, [128, 2, 4, 128], BF16) for i in range(2)]; RSBM = [R(), R()]
    MKR = [sb("MKR%d" % i, [128, 2, 128]) for i in range(2)]; RMKR = [R(), R()]
    SX = [sb("SX%d" % i, [128, 2, 192], BF16) for i in range(2)]; RSX = [R(), R()]
    SAB = [sb("SAB%d" % i, [128, 2, 2, 128], BF16) for i in range(2)]; RSAB = [R(), R()]
    Gb = sb("Gb", [128, 128], BF16); RGb = R()
    Hb = sb("Hb", [128, 2, 128], BF16); RHb = R()
    Pb = sb("Pb", [128, 64], BF16); RPb = R()
    Zb = sb("Zb", [128, 2, 64], BF16); RZb = R()
    STz = [sb("STz%d" % h, [128, 64], BF16) for h in range(2)]; RST = [R(), R()]
    identP = sb("identP", [128, 64]); mkb = sb("mkb", [128, 2, 2, 128])
    HS = [slice(0, 64), slice(64, 128)]
    fin1, Rfin1 = T2("fin1"); fin2, Rfin2 = T2("fin2"); fin3, Rfin3 = T2("fin3")

    PS_M = ps("PS_M", [128, 2, 4, 128]); RPS_M = R()
    PS_K = ps("PS_K", [128, 512]); RPS_K = R()
    PS_X = [ps("PS_X%d" % h, [128, 512]) for h in range(2)]; RPS_X = R()
    PS_AB = ps("PS_AB", [128, 2, 2, 128]); RPS_AB = R()
    PS_G = ps("PS_G", [128, 512]); RPS_G = R()
    PS_T = ps("PS_T", [128, 8, 128], BF16); RPS_T = R()
    PS_P1 = PS_K; RPS_P1 = RPS_K
    PS_P2 = PS_G; RPS_P2 = RPS_G
    mm = lambda out, l, r_, rd, wr, st=True, sp=True, sg=False: S.op("pe", ("matmul", dict(out=out, lhsT=l, rhs=r_, start=st, stop=sp, skip_group_check=sg)), reads=rd, writes=wr)
    S.op("pool", ("tensor_copy", dict(out=identP[0:64, :], in_=identf[0:64, 0:64])), reads=[Ridf], writes=[Ridf])
    S.op("pool", ("tensor_copy", dict(out=identP[64:128, :], in_=identf[64:128, 64:128])), reads=[Ridf], writes=[Ridf])
    for h in range(2):
        S.op("pool", ("tensor_copy", dict(out=mkb[:, :, h, :], in_=mkf[:])), reads=[Rmk], writes=[Rmk])
    out_toks = []
    pcount = 0
    NFILL = 7
    for b in range(NB):
        for d in range(2):
            bwd = (d == 1)
            midc, totc = (C // 2 - 1, C - 1) if not bwd else (C // 2, 0)
            S.op("pool", ("memset", dict(ap=STz[0][:], constant=0.0)), writes=[RST[0]])
            S.op("pool", ("memset", dict(ap=STz[1][:], constant=0.0)), writes=[RST[1]])
            blocks = range(nblk) if not bwd else range(nblk - 1, -1, -1)
            for blk in blocks:
                t0 = blk * BLK
                for nm, src in (("pr", pr), ("pk", pk), ("pv", pv), ("pwa", pwa)):
                    tl, Rl = ld[nm]
                    S.dma(("dma_start", dict(out=tl[:], in_=src[:, b, t0:t0 + BLK + 2])), writes=[Rl])
                sh = (slice(0, BLK) if not bwd else slice(2, BLK + 2))
                cur = slice(1, BLK + 1)
                for nm, q, Rq, mc in (("pr", qr, Rqr, 0), ("pk", qk, Rqk, 2), ("pv", qv, Rqv, 4), ("pwa", qwa, Rqwa, 6)):
                    tl, Rl = ld[nm]
                    S.op("dve", ("tensor_tensor", dict(out=tmp[:], in0=tl[:, sh], in1=tl[:, cur], op=ALU.subtract)), reads=[Rl], writes=[Rtmp])
                    S.op("dve", ("scalar_tensor_tensor", dict(out=q[:], in0=tmp[:], scalar=PM(mc + d), in1=tl[:, cur], op0=ALU.mult, op1=ALU.add)), reads=[Rtmp, Rl, Rprm], writes=[Rq])
                S.op("act", ("activation", dict(out=twa[0:64, :], in_=qwa[0:64, :], func=AF.Tanh)), reads=[Rqwa], writes=[Rtwa])
                S.op("dve", ("tensor_copy", dict(out=twa[64:128, :], in_=qwa[64:128, :])), reads=[Rqwa], writes=[Rtwa])
                S.op("pe", ("matmul", dict(out=PS_P1[:], lhsT=w2b[0:64, d, :], rhs=twa[0:64, :], start=True, stop=True)), reads=[Rw2b, Rtwa], writes=[RPS_P1])
                S.op("pe", ("matmul", dict(out=PS_P2[:], lhsT=w2b[64:128, d, :], rhs=twa[64:128, :], start=True, stop=True)), reads=[Rw2b, Rtwa], writes=[RPS_P2])
                S.op("act", ("activation", dict(out=sw[:], in_=PS_P1[:], func=AF.Sigmoid, bias=PM(8 + d))), reads=[RPS_P1, Rprm], writes=[Rsw])
                S.op("act", ("activation", dict(out=asg[:], in_=PS_P2[:], func=AF.Sigmoid, bias=PM(10 + d))), reads=[RPS_P2, Rprm], writes=[Rasg])
                S.op("dve", ("tensor_scalar", dict(out=logw[:], in0=sw[:], scalar1=NEG_E, scalar2=None, op0=ALU.mult)), reads=[Rsw], writes=[Rlogw])
                S.op("dve", ("tensor_tensor_scan", dict(out=lin[:], data0=scanm[:], data1=logw[:], initial=0.0, op0=ALU.mult, op1=ALU.add)), reads=[Rsc, Rlogw], writes=[Rlin])
                lin3 = lambda tl: tl[:].rearrange("p (c t) -> p c t", t=C)
                bc = lambda tl, col: lin3(tl)[:, :, col:col + 1].to_broadcast([128, BLK // C, C])
                if bwd:
                    S.op("dve", ("tensor_tensor", dict(out=lin3(tmp), in0=bc(lin, C - 1), in1=lin3(lin), op=ALU.subtract)), reads=[Rlin], writes=[Rtmp])
                    S.op("dve", ("tensor_tensor", dict(out=lin[:], in0=tmp[:], in1=logw[:], op=ALU.add)), reads=[Rtmp, Rlogw], writes=[Rlin])
                S.op("dve", ("tensor_tensor", dict(out=lin3(linm), in0=lin3(lin), in1=bc(lin, midc), op=ALU.subtract)), reads=[Rlin], writes=[Rlinm])
                S.op("dve", ("tensor_tensor", dict(out=lexm[:], in0=linm[:], in1=logw[:], op=ALU.subtract)), reads=[Rlinm, Rlogw], writes=[Rlexm])
                S.op("dve", ("tensor_tensor", dict(out=lex[:], in0=lin[:], in1=logw[:], op=ALU.subtract)), reads=[Rlin, Rlogw], writes=[Rlex])
                S.op("dve", ("tensor_tensor", dict(out=lin3(lint), in0=lin3(lin), in1=bc(lin, totc), op=ALU.subtract)), reads=[Rlin], writes=[Rlint])
                S.op("act", ("activation", dict(out=e1[:], in_=linm[:], func=AF.Exp)), reads=[Rlinm], writes=[Re1])
                S.op("act", ("activation", dict(out=e1x[:], in_=lexm[:], func=AF.Exp)), reads=[Rlexm], writes=[Re1x])
                S.op("act", ("activation", dict(out=e2[:], in_=linm[:], func=AF.Exp, scale=-1.0)), reads=[Rlinm], writes=[Re2])
                S.op("act", ("activation", dict(out=e3[:], in_=lin[:], func=AF.Exp)), reads=[Rlin], writes=[Re3])
                S.op("act", ("activation", dict(out=e3x[:], in_=lex[:], func=AF.Exp)), reads=[Rlex], writes=[Re3x])
                S.op("act", ("activation", dict(out=e4[:], in_=lint[:], func=AF.Exp, scale=-1.0)), reads=[Rlint], writes=[Re4])
                S.op("dve", ("tensor_scalar", dict(out=kk[:], in0=qk[:], scalar1=PM(12), scalar2=None, op0=ALU.mult)), reads=[Rqk, Rprm], writes=[Rkk])
                S.op("pool", ("tensor_tensor", dict(out=kk2[:], in0=kk[:], in1=kk[:], op=ALU.mult)), reads=[Rkk], writes=[Rkk2])
                S.op("pe", ("matmul", dict(out=PS_P1[:], lhsT=bdf[:], rhs=kk2[:], start=True, stop=True)), reads=[Rbd, Rkk2], writes=[RPS_P1])
                S.op("dve", ("tensor_scalar", dict(out=rin[:], in0=PS_P1[:], scalar1=1e-12, scalar2=None, op0=ALU.max)), reads=[RPS_P1], writes=[Rrin])
                S.op("act", ("activation", dict(out=rin[:], in_=rin[:], func=AF.Sqrt)), reads=[Rrin], writes=[Rrin])
                S.op("dve", ("reciprocal", dict(out=rin[:], in_=rin[:])), reads=[Rrin], writes=[Rrin])
                S.op("dve", ("tensor_tensor", dict(out=kkn[:], in0=kk[:], in1=rin[:], op=ALU.mult)), reads=[Rkk, Rrin], writes=[Rkkn])
                S.op("dve", ("tensor_scalar", dict(out=tmp[:], in0=asg[:], scalar1=-1.0, scalar2=PM(13), op0=ALU.add, op1=ALU.mult)), reads=[Rasg, Rprm], writes=[Rtmp])
                S.op("dve", ("scalar_tensor_tensor", dict(out=kp[:], in0=tmp[:], scalar=1.0, in1=qk[:], op0=ALU.add, op1=ALU.mult)), reads=[Rtmp, Rqk], writes=[Rkp])
                S.op("pool", ("tensor_tensor", dict(out=bv[:], in0=kkn[:], in1=asg[:], op=ALU.mult)), reads=[Rkkn, Rasg], writes=[Rbv])
                S.op("pool", ("tensor_tensor", dict(out=rk[:], in0=qr[:], in1=kp[:], op=ALU.mult)), reads=[Rqr, Rkp], writes=[Rrk])
                S.op("pe", ("matmul", dict(out=PS_P2[:], lhsT=bdr[:], rhs=rk[:], start=True, stop=True)), reads=[Rbdr, Rrk], writes=[RPS_P2])
                bsl = bsum[:, t0:t0 + BLK]
                if d == 0:
                    S.op("dve", ("tensor_tensor", dict(out=bsl, in0=PS_P2[:], in1=qv[:], op=ALU.mult)), reads=[RPS_P2, Rqv], writes=[Rbs[blk]])
                else:
                    S.op("dve", ("tensor_tensor", dict(out=tmp[:], in0=PS_P2[:], in1=qv[:], op=ALU.mult)), reads=[RPS_P2, Rqv], writes=[Rtmp])
                    S.op("pool", ("tensor_tensor", dict(out=bsl, in0=bsl, in1=tmp[:], op=ALU.add)), reads=[Rtmp, Rbs[blk]], writes=[Rbs[blk]])
                S.op("dve", ("tensor_tensor", dict(out=rt[:], in0=qr[:], in1=e1[:], op=ALU.mult)), reads=[Rqr, Re1], writes=[Rrt])
                S.op("dve", ("scalar_tensor_tensor", dict(out=at[:], in0=kkn[:], scalar=-1.0, in1=e1x[:], op0=ALU.mult, op1=ALU.mult)), reads=[Rkkn, Re1x], writes=[Rat])
                S.op("pool", ("tensor_tensor", dict(out=kt[:], in0=kp[:], in1=e2[:], op=ALU.mult)), reads=[Rkp, Re2], writes=[Rkt])
                S.op("pool", ("tensor_tensor", dict(out=bt[:], in0=bv[:], in1=e2[:], op=ALU.mult)), reads=[Rbv, Re2], writes=[Rbt])
                S.op("pool", ("tensor_tensor", dict(out=r0[:], in0=qr[:], in1=e3[:], op=ALU.mult)), reads=[Rqr, Re3], writes=[Rr0])
                S.op("dve", ("scalar_tensor_tensor", dict(out=a0b[:], in0=kkn[:], scalar=-1.0, in1=e3x[:], op0=ALU.mult, op1=ALU.mult)), reads=[Rkkn, Re3x], writes=[Ra0b])
                S.op("pool", ("tensor_tensor", dict(out=kEb[:], in0=kp[:], in1=e4[:], op=ALU.mult)), reads=[Rkp, Re4], writes=[RkEb])
                S.op("pool", ("tensor_tensor", dict(out=bEb[:], in0=bv[:], in1=e4[:], op=ALU.mult)), reads=[Rbv, Re4], writes=[RbEb])
                S.op("act", ("activation", dict(out=qvb[:], in_=qv[:], func=AF.Copy)), reads=[Rqv], writes=[Rqvb])

                chunks = list(range(BLK // C)) if not bwd else list(range(BLK // C - 1, -1, -1))
                cks = [(ci, slice(ci * C, (ci + 1) * C), (t0 // C) + ci, (pcount + n_) % 2) for n_, ci in enumerate(chunks)]
                pcount += len(chunks)

                def stage1(ck):
                    ci, cs, gci, p = ck
                    for i, (src, Rs) in enumerate(((qvb, Rqvb), (a0b, Ra0b), (bEb, RbEb), (kEb, RkEb))):
                        S.op("pe", ("transpose", dict(out=PS_T[:, i, :], in_=src[:, cs], identity=identb[:])), reads=[Rs, Ridb], writes=[RPS_T])
                    yield
                    S.op("act", ("activation", dict(out=TT[p][:], in_=PS_T[:, 0:4, :], func=AF.Copy)), reads=[RPS_T], writes=[RTT[p]])
                    yield
                    for h in range(2):
                        hs = HS[h]
                        mm(PS_M[:, h, 0, :], bt[hs, cs], at[hs, cs], [Rbt, Rat], [RPS_M])
                        mm(PS_M[:, h, 1, :], at[hs, cs], bt[hs, cs], [Rbt, Rat], [RPS_M])
                        yield
                        mm(PS_M[:, h, 2, :], at[hs, cs], kt[hs, cs], [Rkt, Rat], [RPS_M])
                        mm(PS_M[:, h, 3, :], bt[hs, cs], rt[hs, cs], [Rbt, Rrt], [RPS_M])
                        yield
                        mm((PS_K if h == 0 else PS_G)[:, 0:128], kt[hs, cs], rt[hs, cs], [Rkt, Rrt], [RPS_K if h == 0 else RPS_G])
                        yield
                    for h in range(2):
                        S.op("dve", ("tensor_tensor", dict(out=SBM[p][:, h], in0=PS_M[:, h], in1=m4f[:, d, :, :], op=ALU.mult)), reads=[RPS_M, Rm4], writes=[RSBM[p]])
                        yield
                    S.op("dve", ("tensor_tensor", dict(out=MKR[p][:, 0, :], in0=PS_K[:, 0:128], in1=mkf[:, d, :], op=ALU.mult)), reads=[RPS_K, Rmk], writes=[RMKR[p]])
                    yield
                    S.op("dve", ("tensor_tensor", dict(out=MKR[p][:, 1, :], in0=PS_G[:, 0:128], in1=mkf[:, d, :], op=ALU.mult)), reads=[RPS_G, Rmk], writes=[RMKR[p]])
                    yield
                    S.op("act", ("activation", dict(out=SX[p][:, :, 0:128], in_=SBM[p][:, :, 3, :], func=AF.Copy)), reads=[RSBM[p]], writes=[RSX[p]])
                    S.op("pool", ("tensor_copy", dict(out=SX[p][:, :, 128:192], in_=TT[p][:, 2, :].rearrange("p (h j) -> p h j", h=2))), reads=[RTT[p]], writes=[RSX[p]])
                    yield

                def stage2(ck):
                    ci, cs, gci, p = ck
                    for h in range(2):
                        mm(PS_X[h][:, 0:192], identb[:], SX[p][:, h, :], [Ridb, RSX[p]], [RPS_X], st=True, sp=True)
                    A_ = [SBM[p][:, h, 1, :] for h in range(2)]
                    B_ = [SBM[p][:, h, 0, :] for h in range(2)]
                    Rcur = RSBM[p]
                    for lv in range(7):
                        for h in range(2):
                            mm(PS_X[h][:, 0:192], A_[h], SX[p][:, h, :], [Rcur, RSX[p]], [RPS_X], st=False, sp=True, sg=True)
                        if lv < 6:
                            nb = lv % 2
                            for h in range(2):
                                mm(PS_AB[:, h, 0, :], B_[h], A_[h], [Rcur], [RPS_AB])
                                mm(PS_AB[:, h, 1, :], A_[h], B_[h], [Rcur], [RPS_AB])
                        S.op("dve", ("tensor_copy", dict(out=SX[p][:, 0, :], in_=PS_X[0][:, 0:192])), reads=[RPS_X], writes=[RSX[p]])
                        S.op("act", ("activation", dict(out=SX[p][:, 1, :], in_=PS_X[1][:, 0:192], func=AF.Copy)), reads=[RPS_X], writes=[RSX[p]])
                        if lv < 6:
                            S.op("act", ("activation", dict(out=SAB[nb][:].rearrange("p a b c -> p (a b c)"), in_=PS_AB[:].rearrange("p a b c -> p (a b c)"), func=AF.Copy)), reads=[RPS_AB], writes=[RSAB[nb]])
                            A_ = [SAB[nb][:, h, 0, :] for h in range(2)]
                            B_ = [SAB[nb][:, h, 1, :] for h in range(2)]
                            Rcur = RSAB[nb]
                        yield

                def stage3(ck):
                    ci, cs, gci, p = ck
                    for h in range(2):
                        hs = HS[h]
                        a0T = TT[p][:, 1, hs]
                        mm(PS_G[hs, 0:128], a0T, SX[p][:, h, 0:128], [RTT[p], RSX[p]], [RPS_G])
                        mm(PS_G[hs, 128:192], a0T, SX[p][:, h, 128:192], [RTT[p], RSX[p]], [RPS_G])
                        yield
                        mm(PS_G[:, 192 + 128 * h:320 + 128 * h], SBM[p][:, h, 2, :], SX[p][:, h, 0:128], [RSBM[p], RSX[p]], [RPS_G])
                        mm(PS_K[:, 256 + 64 * h:320 + 64 * h], SBM[p][:, h, 2, :], SX[p][:, h, 128:192], [RSBM[p], RSX[p]], [RPS_K])
                        yield
                    S.op("dve", ("tensor_tensor", dict(out=Gb[:], in0=PS_G[:, 0:128], in1=r0[:, cs], op=ALU.add)), reads=[RPS_G, Rr0], writes=[RGb])
                    yield
                    S.op("dve", ("tensor_tensor", dict(out=Hb[:], in0=PS_G[:, 192:448].rearrange("p (h t) -> p h t", h=2), in1=MKR[p][:], op=ALU.add)), reads=[RPS_G, RMKR[p]], writes=[RHb])
                    yield
                    tcol = ci * C + totc
                    S.op("dve", ("scalar_tensor_tensor", dict(out=Pb[:], in0=identP[:], scalar=e3[:, tcol:tcol + 1], in1=PS_G[:, 128:192], op0=ALU.mult, op1=ALU.add)), reads=[RPS_G, Ridf, Re3], writes=[RPb])
                    yield
                    S.op("dve", ("tensor_tensor", dict(out=Zb[:], in0=PS_K[:, 256:384].rearrange("p (h j) -> p h j", h=2), in1=TT[p][:, 3, :].rearrange("p (h j) -> p h j", h=2), op=ALU.add)), reads=[RPS_K, RTT[p]], writes=[RZb])
                    yield
                    for h in range(2):
                        hs = HS[h]
                        mm(PS_M[hs, 0, 0, :], STz[h][:], Gb[:], [RST[h], RGb], [RPS_M], st=True, sp=False)
                        mm(PS_M[hs, 0, 0, :], TT[p][:, 0, hs], Hb[:, h, :], [RTT[p], RHb], [RPS_M], st=False, sp=True)
                        yield
                        mm(PS_M[hs, 0, 1, 0:64], Pb[:], STz[h][:], [RPb, RST[h]], [RPS_M], st=True, sp=False)
                        mm(PS_M[hs, 0, 1, 0:64], Zb[:, h, :], TT[p][:, 0, hs], [RZb, RTT[p]], [RPS_M], st=False, sp=True)
                        yield
                    ysl = ysum[:, t0 + ci * C: t0 + (ci + 1) * C]
                    if d == 0:
                        S.op("act", ("activation", dict(out=ysl, in_=PS_M[:, 0, 0, :], func=AF.Copy)), reads=[RPS_M], writes=[Rys[gci]])
                    else:
                        S.op("act", ("activation", dict(out=tmp[:, 0:128], in_=PS_M[:, 0, 0, :], func=AF.Copy)), reads=[RPS_M], writes=[Rtmp])
                        S.op("dve", ("tensor_tensor", dict(out=ysl, in0=tmp[:, 0:128], in1=ysl, op=ALU.add)), reads=[Rtmp, Rys[gci]], writes=[Rys[gci]])
                    yield
                    for h in range(2):
                        hs = HS[h]
                        S.op("act", ("activation", dict(out=STz[h][hs, :], in_=PS_M[hs, 0, 1, 0:64], func=AF.Copy)), reads=[RPS_M], writes=[RST[h]])
                    yield

                for _ in stage1(cks[0]):
                    pass
                for idx, ck in enumerate(cks):
                    fill = itertools.chain(stage3(cks[idx - 1]) if idx > 0 else iter(()), stage1(cks[idx + 1]) if idx + 1 < len(cks) else iter(()))
                    for _ in stage2(ck):
                        for _k in range(NFILL):
                            next(fill, None)
                    for _ in fill:
                        pass
                for _ in stage3(cks[-1]):
                    pass
        for blk in range(nblk):
            t0 = blk * BLK
            ysl = ysum[:, t0:t0 + BLK]
            Rin = Rys[t0 // C: (t0 + BLK) // C]
            S.op("pe", ("matmul", dict(out=PS_P1[:], lhsT=bdm[:], rhs=ysl, start=True, stop=True)), reads=[Rbdm] + Rin, writes=[RPS_P1])
            S.op("dve", ("tensor_tensor", dict(out=fin1[:], in0=ysl, in1=PS_P1[:], op=ALU.subtract)), reads=[RPS_P1] + Rin, writes=[Rfin1])
            S.op("pool", ("tensor_tensor", dict(out=fin2[:], in0=fin1[:], in1=fin1[:], op=ALU.mult)), reads=[Rfin1], writes=[Rfin2])
            S.op("pe", ("matmul", dict(out=PS_P2[:], lhsT=bdm[:], rhs=fin2[:], start=True, stop=True)), reads=[Rbdm, Rfin2], writes=[RPS_P2])
            S.op("dve", ("tensor_scalar", dict(out=fin3[:], in0=PS_P2[:], scalar1=GN_EPS, scalar2=None, op0=ALU.add)), reads=[RPS_P2], writes=[Rfin3])
            S.op("act", ("activation", dict(out=fin3[:], in_=fin3[:], func=AF.Sqrt)), reads=[Rfin3], writes=[Rfin3])
            S.op("dve", ("reciprocal", dict(out=fin3[:], in_=fin3[:])), reads=[Rfin3], writes=[Rfin3])
            S.op("dve", ("tensor_tensor", dict(out=fin1[:], in0=fin1[:], in1=fin3[:], op=ALU.mult)), reads=[Rfin1, Rfin3], writes=[Rfin1])
            S.op("dve", ("tensor_scalar", dict(out=fin2[:], in0=fin1[:], scalar1=PM(15), scalar2=PM(16), op0=ALU.mult, op1=ALU.add)), reads=[Rfin1, Rprm], writes=[Rfin2])
            S.op("dve", ("tensor_tensor", dict(out=fin2[:], in0=fin2[:], in1=bsum[:, t0:t0 + BLK], op=ALU.add)), reads=[Rfin2, Rbs[blk]], writes=[Rfin2])
            out_toks.append(S.dma(("dma_start", dict(out=yout[:, b, t0:t0 + BLK], in_=fin2[:])), reads=[Rfin2]))
    return out_toks


NT = 2048
NTH = NT + 2


def emit_p1(S, nc, I, O):
    sb = lambda name, shape, dt=F32: nc.alloc_sbuf_tensor(name, shape, dt)
    ps = lambda name, shape, dt=F32: nc.alloc_psum_tensor(name, shape, dt)
    R = Region
    op = S.op
    mm = lambda out, l, r_, rd, wr, st=True, sp=True: op("pe", ("matmul", dict(out=out, lhsT=l, rhs=r_, start=st, stop=sp)), reads=rd, writes=wr)

    identf = sb("identf", [128, 128]); identb = sb("identb", [128, 128], BF16); Rid = R()
    gE = sb("gE", [128, 8, 1]); gO = sb("gO", [128, 8, 1]); Rg = R()
    S.dma(("dma_start", dict(out=identf[:], in_=I["c_ident"])), writes=[Rid])
    op("dve", ("tensor_copy", dict(out=identb[:], in_=identf[:])), reads=[Rid], writes=[Rid])
    S.dma(("dma_start", dict(out=gE[:, :, 0], in_=I["e_norm_g"].rearrange("(k p) -> p k", p=128), allow_slow_non_contiguous=True)), writes=[Rg])
    S.dma(("dma_start", dict(out=gO[:, :, 0], in_=I["o_norm_g"].rearrange("(k p) -> p k", p=128), allow_slow_non_contiguous=True)), writes=[Rg])
    hnT = sb("hnT", [128, 8, NTH], BF16); RhnT = [R() for _ in range(18)]
    yT = nc.dram_tensor("yT_d", [16, 128, NT], BF16).ap(); RyT = [[R() for _ in range(4)] for _ in range(16)]
    U = sb("U", [128, 4096]); RU = R()
    xt = [sb("xt%d" % i, [128, 1024]) for i in range(2)]; Rxt = [R(), R()]
    yo = [sb("yo%d" % i, [128, 512], BF16) for i in range(2)]; Ryo = [R(), R()]
    ytl = [sb("ytl%d" % i, [128, 16, 128], BF16) for i in range(2)]; Rytl = [R(), R()]
    xn = sb("xn", [128, 1024], BF16); Rxn = R()
    sq = sb("sq", [128, 1024]); Rsq = R()
    st = sb("st", [128, 8]); Rst = R()
    stg = [sb("stg%d" % i, [128, 8, 256]) for i in range(2)]; Rstg = [R(), R()]
    wbf = [sb("wbf%d" % i, [128, 8, 512], BF16) for i in range(2)]; Rwbf = [R(), R()]
    wbig = sb("wbig", [128, 16, 1024], BF16); Rwbig = R()
    t1 = sb("t1", [128, 512]); Rt1 = R()
    t1b = sb("t1b", [128, 512]); t1s = [t1, t1b]; Rt1s = [Rt1, R()]
    t2b = sb("t2b", [128, 512]); t3b = sb("t3b", [128, 512])
    t2 = sb("t2", [128, 512]); Rt2 = R()
    t3 = sb("t3", [128, 512]); Rt3 = R()
    cw = sb("cw", [128, 8, 3]); Rcw = R()
    PS_a = ps("PS_a", [128, 512]); RPa = R()
    PS_b = ps("PS_b", [128, 512]); RPb = R()
    PS_c = ps("PS_c", [128, 512]); RPc = R()
    PS_d = ps("PS_d", [128, 512]); RPd = R()
    PS_t = ps("PS_t", [128, 8, 128], BF16); RPt = R()
    PS_m = ps("PS_m", [128, 8, 128]); RPm = R()
    for j_ in range(3):
        S.dma(("dma_start", dict(out=cw[:, :, j_], in_=I["e_conv_w"][j_].rearrange("(cb p) -> p cb", p=128), allow_slow_non_contiguous=True)), writes=[Rcw])

    def norm_tile(xtile, Rx, gt, dst_fn, Rdst, nvalid=128):
        op("act", ("activation", dict(out=sq[:], in_=xtile[:], func=AF.Square)), reads=[Rx], writes=[Rsq])
        op("dve", ("reduce_sum", dict(out=st[:, 0:1], in_=sq[:], axis=AX.X)), reads=[Rsq], writes=[Rst])
        op("dve", ("tensor_scalar", dict(out=st[:, 1:2], in0=st[:, 0:1], scalar1=1.0 / 1024, scalar2=1e-6, op0=ALU.mult, op1=ALU.add)), reads=[Rst], writes=[Rst])
        op("act", ("activation", dict(out=st[:, 2:3], in_=st[:, 1:2], func=AF.Sqrt)), reads=[Rst], writes=[Rst])
        op("dve", ("reciprocal", dict(out=st[:, 3:4], in_=st[:, 2:3])), reads=[Rst], writes=[Rst])
        op("dve", ("tensor_scalar", dict(out=xn[:], in0=xtile[:], scalar1=st[:, 3:4], scalar2=None, op0=ALU.mult)), reads=[Rx, Rst], writes=[Rxn])
        for k in range(8):
            op("pe", ("transpose", dict(out=PS_t[:, k, :], in_=xn[:, k * 128:(k + 1) * 128], identity=identb[:])), reads=[Rxn, Rid], writes=[RPt])
        dst_fn(gt)

    xh = I["xh"]
    for i in range(17):
        xb, Rx = xt[i % 2], Rxt[i % 2]
        if i < 16:
            S.dma(("dma_start", dict(out=xb[:], in_=xh[1 + 128 * i: 1 + 128 * (i + 1), :])), writes=[Rx])
            def dst(gt, i=i):
                op("dve", ("tensor_tensor", dict(out=hnT[:, :, 1 + 128 * i: 1 + 128 * (i + 1)], in0=PS_t[:], in1=gt[:].to_broadcast([128, 8, 128]), op=ALU.mult)), reads=[RPt, Rg], writes=[RhnT[i]])
        else:
            op("pool", ("memset", dict(ap=xb[:], constant=0.0)), writes=[Rx])
            S.dma(("dma_start", dict(out=xb[0:1, :], in_=xh[0:1, :])), writes=[Rx])
            S.dma(("dma_start", dict(out=xb[1:2, :], in_=xh[NT + 1:NT + 2, :])), writes=[Rx])
            def dst(gt):
                op("dve", ("tensor_tensor", dict(out=hnT[:, :, 0:1], in0=PS_t[:, :, 0:1], in1=gt[:], op=ALU.mult)), reads=[RPt, Rg], writes=[RhnT[16]])
                op("dve", ("tensor_tensor", dict(out=hnT[:, :, NT + 1:NT + 2], in0=PS_t[:, :, 1:2], in1=gt[:], op=ALU.mult)), reads=[RPt, Rg], writes=[RhnT[17]])
        norm_tile(xb, Rx, gE, dst, None)
    allh = RhnT

    wi = I["e_w_in"].rearrange("(k p) (s c) -> p k s c", p=128, c=1024)

    def load_w(buf, src4, nsp):
        for s_ in range(nsp):
            sb_ = s_ % 2
            S.dma(("dma_start", dict(out=stg[sb_][:, :, 0:128], in_=src4[:, :, s_, :])), writes=[Rstg[sb_]])
            op("pool", ("tensor_copy", dict(out=wbf[buf][:, :, s_ * 128:(s_ + 1) * 128], in_=stg[sb_][:, :, 0:128])), reads=[Rstg[sb_]], writes=[Rwbf[buf]])
        return wbf[buf][:, :, 0:nsp * 128].rearrange("p k (s c) -> p k s c", c=128)

    PSc0, RPc0, PSd0, RPd0 = PS_c, RPc, PS_d, RPd
    t2s = [t2, t2b]; Rt2s = [Rt2, R()]
    t3s = [t3, t3b]; Rt3s = [Rt3, R()]
    xc = U[:, 0:NTH]
    chunksA = [(0, 512), (512, 512), (1024, 512), (1536, 512), (2048, 2)]
    for cb in range(8):
        w4 = load_w(cb % 2, wi[:, :, 0:4, cb * 128:(cb + 1) * 128], 4)
        Rw = Rwbf[cb % 2]
        for ci_, (c0, n) in enumerate(chunksA):
            (PA, RA_), (PB, RB_) = (((PS_a, RPa), (PS_b, RPb)) if ci_ % 2 == 0 else ((PS_c, RPc), (PS_d, RPd)))
            for k in range(8):
                mm(PA[:, 0:n], w4[:, k, 0, :], hnT[:, k, c0:c0 + n], [Rw] + allh, [RA_], st=(k == 0), sp=(k == 7))
            for k in range(8):
                mm(PB[:, 0:n], w4[:, k, 2, :], hnT[:, k, c0:c0 + n], [Rw] + allh, [RB_], st=(k == 0), sp=(k == 7))
            t1_, Rt1_ = t1s[ci_ % 2], Rt1s[ci_ % 2]
            op("act", ("activation", dict(out=t1_[:, 0:n], in_=PA[:, 0:n], func=AF.Copy)), reads=[RA_], writes=[Rt1_])
            op("dve", ("tensor_tensor", dict(out=xc[:, c0:c0 + n], in0=PB[:, 0:n], in1=t1_[:, 0:n], op=ALU.mult)), reads=[RB_, Rt1_], writes=[RU])
        for j in range(4):
            c0 = 1 + 512 * j
            (PS_c, RPc), (PS_d, RPd) = ((PSc0, RPc0), (PSd0, RPd0)) if j % 2 == 1 else ((PS_a, RPa), (PS_b, RPb))
            t2, Rt2 = t2s[j % 2], Rt2s[j % 2]
            t3, Rt3 = t3s[j % 2], Rt3s[j % 2]
            for k in range(8):
                mm(PS_c[:], w4[:, k, 1, :], hnT[:, k, c0:c0 + 512], [Rw] + allh, [RPc], st=(k == 0), sp=(k == 7))
            for k in range(8):
                mm(PS_d[:], w4[:, k, 3, :], hnT[:, k, c0:c0 + 512], [Rw] + allh, [RPd], st=(k == 0), sp=(k == 7))
            op("dve", ("tensor_scalar", dict(out=t2[:], in0=xc[:, c0 - 1:c0 + 511], scalar1=cw[:, cb, 0:1], scalar2=None, op0=ALU.mult)), reads=[RU, Rcw], writes=[Rt2])
            op("dve", ("scalar_tensor_tensor", dict(out=t2[:], in0=xc[:, c0:c0 + 512], scalar=cw[:, cb, 1:2], in1=t2[:], op0=ALU.mult, op1=ALU.add)), reads=[RU, Rcw, Rt2], writes=[Rt2])
            op("dve", ("scalar_tensor_tensor", dict(out=t2[:], in0=xc[:, c0 + 1:c0 + 513], scalar=cw[:, cb, 2:3], in1=t2[:], op0=ALU.mult, op1=ALU.add)), reads=[RU, Rcw, Rt2], writes=[Rt2])
            op("act", ("activation", dict(out=t3[:], in_=PS_d[:], func=AF.Silu)), reads=[RPd], writes=[Rt3])
            op("dve", ("tensor_tensor", dict(out=t2[:], in0=PS_c[:], in1=t2[:], op=ALU.mult)), reads=[RPc, Rt2], writes=[Rt2])
            op("pool", ("tensor_tensor", dict(out=yo[j % 2][:], in0=t2[:], in1=t3[:], op=ALU.mult)), reads=[Rt2, Rt3], writes=[Ryo[j % 2]])
            S.dma(("dma_start", dict(out=yT[cb, :, 512 * j:512 * (j + 1)], in_=yo[j % 2][:])), reads=[Ryo[j % 2]], writes=[RyT[cb][j]])

    PS_c, RPc, PS_d, RPd = PSc0, RPc0, PSd0, RPd0
    t2, Rt2, t3, Rt3 = t2s[0], Rt2s[0], t3s[0], Rt3s[0]
    for hf in range(4):
        S.dma(("dma_start", dict(out=stg[hf % 2][:], in_=wi[:, :, 5, hf * 256:(hf + 1) * 256])), writes=[Rstg[hf % 2]])
        op("pool", ("tensor_copy", dict(out=wbig[:, 0:8, hf * 256:(hf + 1) * 256], in_=stg[hf % 2][:])), reads=[Rstg[hf % 2]], writes=[Rwbig])
    wsn = sb("wsn", [128, 8, 128]); wsnb = sb("wsnb", [128, 8, 128], BF16); wsT = sb("wsT", [128, 8, 128], BF16); Rws = R()
    S.dma(("dma_start", dict(out=wsn[:], in_=I["e_sgu_w"].rearrange("g i j -> i g j"))), writes=[Rws])
    op("dve", ("tensor_copy", dict(out=wsnb[:], in_=wsn[:])), reads=[Rws], writes=[Rws])
    for g in range(8):
        op("pe", ("transpose", dict(out=PS_t[:, g, :], in_=wsnb[:, g, :], identity=identb[:])), reads=[Rws, Rid], writes=[RPt])
    op("act", ("activation", dict(out=wsT[:], in_=PS_t[:], func=AF.Copy)), reads=[RPt], writes=[Rws])
    bsB = sb("bsB", [128, 8, 128]); lnG = sb("lnG", [128, 1024]); lnB = sb("lnB", [128, 1024]); Rbc = R()
    S.dma(("dma_start", dict(out=bsB[:].rearrange("p g i -> p (g i)"), in_=I["e_sgu_b"].rearrange("g i -> (g i)").partition_broadcast(128))), writes=[Rbc])
    S.dma(("dma_start", dict(out=lnG[:], in_=I["e_sgu_ln_g"].partition_broadcast(128))), writes=[Rbc])
    S.dma(("dma_start", dict(out=lnB[:], in_=I["e_sgu_ln_b"].partition_broadcast(128))), writes=[Rbc])
    vsb = sb("vsb", [128, 1024]); Rvsb = R()
    vnb = sb("vnb", [128, 1024], BF16); Rvnb = R()
    mixall = U[:, 0:4096].rearrange("p (g t) -> p g t", g=8)
    for tg in range(4):
        for ti in range(4):
            c0 = 1 + 128 * (4 * tg + ti)
            for hf, (P_, RP_) in enumerate((((PS_a, RPa), (PS_b, RPb)) if ti % 2 == 0 else ((PS_c, RPc), (PS_d, RPd)))):
                for k in range(8):
                    mm(P_[:], hnT[:, k, c0:c0 + 128], wbig[:, k, hf * 512:(hf + 1) * 512], [Rwbig] + allh, [RP_], st=(k == 0), sp=(k == 7))
                op("act", ("activation", dict(out=vsb[:, hf * 512:(hf + 1) * 512], in_=P_[:], func=AF.Copy)), reads=[RP_], writes=[Rvsb])
            op("act", ("activation", dict(out=sq[:], in_=vsb[:], func=AF.Square)), reads=[Rvsb], writes=[Rsq])
            op("dve", ("reduce_sum", dict(out=st[:, 0:1], in_=vsb[:], axis=AX.X)), reads=[Rvsb], writes=[Rst])
            op("dve", ("reduce_sum", dict(out=st[:, 1:2], in_=sq[:], axis=AX.X)), reads=[Rsq], writes=[Rst])
            op("dve", ("tensor_scalar", dict(out=st[:, 2:3], in0=st[:, 0:1], scalar1=1.0 / 1024, scalar2=None, op0=ALU.mult)), reads=[Rst], writes=[Rst])
            op("dve", ("tensor_tensor", dict(out=st[:, 3:4], in0=st[:, 2:3], in1=st[:, 2:3], op=ALU.mult)), reads=[Rst], writes=[Rst])
            op("dve", ("scalar_tensor_tensor", dict(out=st[:, 4:5], in0=st[:, 1:2], scalar=1.0 / 1024, in1=st[:, 3:4], op0=ALU.mult, op1=ALU.subtract)), reads=[Rst], writes=[Rst])
            op("dve", ("tensor_scalar", dict(out=st[:, 4:5], in0=st[:, 4:5], scalar1=1e-5, scalar2=None, op0=ALU.add)), reads=[Rst], writes=[Rst])
            op("act", ("activation", dict(out=st[:, 5:6], in_=st[:, 4:5], func=AF.Sqrt)), reads=[Rst], writes=[Rst])
            op("dve", ("reciprocal", dict(out=st[:, 6:7], in_=st[:, 5:6])), reads=[Rst], writes=[Rst])
            op("dve", ("tensor_scalar", dict(out=vsb[:], in0=vsb[:], scalar1=st[:, 2:3], scalar2=st[:, 6:7], op0=ALU.subtract, op1=ALU.mult)), reads=[Rvsb, Rst], writes=[Rvsb])
            op("dve", ("tensor_tensor", dict(out=vsb[:], in0=vsb[:], in1=lnG[:], op=ALU.mult)), reads=[Rvsb, Rbc], writes=[Rvsb])
            op("pool", ("tensor_tensor", dict(out=vnb[:], in0=vsb[:], in1=lnB[:], op=ALU.add)), reads=[Rvsb, Rbc], writes=[Rvnb])
            for g in range(8):
                mm(PS_m[:, g, :], vnb[:, g * 128:(g + 1) * 128], wsT[:, g, :], [Rvnb, Rws], [RPm])
            op("dve", ("tensor_tensor", dict(out=mixall[:, :, ti * 128:(ti + 1) * 128], in0=PS_m[:], in1=bsB[:], op=ALU.add)), reads=[RPm, Rbc], writes=[RU])
        c0 = 1 + 512 * tg
        for g in range(8):
            buf = g % 2
            S.dma(("dma_start", dict(out=stg[buf][:, :, 0:128], in_=wi[:, :, 4, g * 128:(g + 1) * 128])), writes=[Rstg[buf]])
            S.dma(("dma_start", dict(out=stg[buf][:, :, 128:256], in_=wi[:, :, 6, g * 128:(g + 1) * 128])), writes=[Rstg[buf]])
            op("pool", ("tensor_copy", dict(out=wbf[buf][:, :, 0:256], in_=stg[buf][:, :, 0:256])), reads=[Rstg[buf]], writes=[Rwbf[buf]])
            (PU, RPU), (PZ, RPZ) = ((PS_c, RPc), (PS_d, RPd)) if g % 2 == 0 else ((PS_a, RPa), (PS_b, RPb))
            t2, Rt2 = t2s[g % 2], Rt2s[g % 2]
            t3, Rt3 = t3s[g % 2], Rt3s[g % 2]
            for k in range(8):
                mm(PU[:], wbf[buf][:, k, 0:128], hnT[:, k, c0:c0 + 512], [Rwbf[buf]] + allh, [RPU], st=(k == 0), sp=(k == 7))
            for k in range(8):
                mm(PZ[:], wbf[buf][:, k, 128:256], hnT[:, k, c0:c0 + 512], [Rwbf[buf]] + allh, [RPZ], st=(k == 0), sp=(k == 7))
            op("act", ("activation", dict(out=t3[:], in_=PZ[:], func=AF.Silu)), reads=[RPZ], writes=[Rt3])
            op("dve", ("tensor_tensor", dict(out=t2[:], in0=PU[:], in1=mixall[:, g, :], op=ALU.mult)), reads=[RPU, RU], writes=[Rt2])
            op("pool", ("tensor_tensor", dict(out=yo[g % 2][:], in0=t2[:], in1=t3[:], op=ALU.mult)), reads=[Rt2, Rt3], writes=[Ryo[g % 2]])
            S.dma(("dma_start", dict(out=yT[8 + g, :, 512 * tg:512 * (tg + 1)], in_=yo[g % 2][:])), reads=[Ryo[g % 2]], writes=[RyT[8 + g][tg]])

    wo = I["e_w_out"].rearrange("(k p) n -> p k n", p=128)
    for q in range(2):
        for hf in range(4):
            S.dma(("dma_start", dict(out=stg[hf % 2][:], in_=wo[:, 8 * q:8 * q + 8, hf * 256:(hf + 1) * 256])), writes=[Rstg[hf % 2]])
            op("pool", ("tensor_copy", dict(out=wbig[:, 8 * q:8 * q + 8, hf * 256:(hf + 1) * 256], in_=stg[hf % 2][:])), reads=[Rstg[hf % 2]], writes=[Rwbig])
    ally = [r for row in RyT for r in row]
    h1ts = [sb("h1t%d" % i, [128, 1024]) for i in range(2)]; Rh1s = [R(), R()]
    for i in range(16):
        xb, Rx = xt[i % 2], Rxt[i % 2]
        h1t, Rh1 = h1ts[i % 2], Rh1s[i % 2]
        S.dma(("dma_start", dict(out=xb[:], in_=xh[1 + 128 * i: 1 + 128 * (i + 1), :])), writes=[Rx])
        S.dma(("dma_start", dict(out=ytl[i % 2][:], in_=yT[:, :, 128 * i:128 * (i + 1)].rearrange("k p t -> p k t"))), reads=ally, writes=[Rytl[i % 2]])
        for hf, (P_, RP_) in enumerate((((PS_a, RPa), (PS_b, RPb)) if i % 2 == 0 else ((PS_c, RPc), (PS_d, RPd)))):
            for k in range(16):
                mm(P_[:], ytl[i % 2][:, k, :], wbig[:, k, hf * 512:(hf + 1) * 512], [Rwbig, Rytl[i % 2]], [RP_], st=(k == 0), sp=(k == 15))
            op("dve", ("tensor_tensor", dict(out=h1t[:, hf * 512:(hf + 1) * 512], in0=P_[:], in1=xb[:, hf * 512:(hf + 1) * 512], op=ALU.add)), reads=[RP_, Rx], writes=[Rh1])
        S.dma(("dma_start", dict(out=O["h1"][128 * i:128 * (i + 1), :], in_=h1t[:])), reads=[Rh1])

        def dst(gt, i=i):
            op("dve", ("tensor_tensor", dict(out=hnT[:, :, 1 + 128 * i: 1 + 128 * (i + 1)], in0=PS_t[:], in1=gt[:].to_broadcast([128, 8, 128]), op=ALU.mult)), reads=[RPt, Rg], writes=[RhnT[i]])
        norm_tile(h1t, Rh1, gO, dst, None)

    wi1 = I["o_w_in"].rearrange("(k p) n -> p k n", p=128)
    blocks = [(c * 128, c * 128, False) for c in range(25)]
    blocks += [(3200 + c * 128, 3200 + c * 128, True) for c in range(8)]
    blocks += [(4736 + c * 128, 4224 + c * 128, True) for c in range(4)]
    ob = [sb("ob%d" % i, [128, 512]) for i in range(2)]; Rob = [R(), R()]
    oi = 0
    toks = []
    for bi, (sc, dr, act) in enumerate(blocks):
        buf = bi % 2
        S.dma(("dma_start", dict(out=stg[buf][:, :, 0:128], in_=wi1[:, :, sc:sc + 128])), writes=[Rstg[buf]])
        op("pool", ("tensor_copy", dict(out=wbf[buf][:, :, 0:128], in_=stg[buf][:, :, 0:128])), reads=[Rstg[buf]], writes=[Rwbf[buf]])
        for j in range(4):
            P_, RP_ = ((PS_a, RPa), (PS_b, RPb), (PS_c, RPc), (PS_d, RPd))[j]
            c0 = 1 + 512 * j
            for k in range(8):
                mm(P_[:], wbf[buf][:, k, 0:128], hnT[:, k, c0:c0 + 512], [Rwbf[buf]] + allh, [RP_], st=(k == 0), sp=(k == 7))
            o_, Ro = ob[oi % 2], Rob[oi % 2]
            oi += 1
            op("act", ("activation", dict(out=o_[:], in_=P_[:], func=(AF.Silu if act else AF.Copy))), reads=[RP_], writes=[Ro])
            toks.append(S.dma(("dma_start", dict(out=O["pT"][dr:dr + 128, 512 * j:512 * (j + 1)], in_=o_[:])), reads=[Ro]))
    for hf in range(2):
        S.dma(("dma_start", dict(out=stg[hf][:], in_=wi1[:, :, 4224 + 256 * hf:4224 + 256 * (hf + 1)])), writes=[Rstg[hf]])
        op("pool", ("tensor_copy", dict(out=wbf[0][:, :, 256 * hf:256 * (hf + 1)], in_=stg[hf][:])), reads=[Rstg[hf]], writes=[Rwbf[0]])
    for i in range(16):
        c0 = 1 + 128 * i
        P_, RP_ = ((PS_a, RPa), (PS_b, RPb))[i % 2]
        for k in range(8):
            mm(P_[:], hnT[:, k, c0:c0 + 128], wbf[0][:, k, :], [Rwbf[0]] + allh, [RP_], st=(k == 0), sp=(k == 7))
        o_, Ro = ob[oi % 2], Rob[oi % 2]
        oi += 1
        op("act", ("activation", dict(out=o_[:], in_=P_[:], func=AF.Copy)), reads=[RP_], writes=[Ro])
        toks.append(S.dma(("dma_start", dict(out=O["fd"][128 * i:128 * (i + 1), :], in_=o_[:])), reads=[Ro]))
    return toks


def full_barrier(S):
    keys = list(S.cnt.items())
    for e in S.ENGS:
        waits = []
        for k, v in keys:
            if k == e:
                continue
            if S.seen[e].get(k, 0) < v:
                S.seen[e][k] = v
                waits.append((k, v))
        if waits:
            S.prog[e].append([waits, None, ("_none", 0)])


def emit_fnet(S, nc, I, ydT):
    R = Region
    op = S.op
    mm = lambda out, l, r_, rd, wr, st=True, sp=True: op("pe", ("matmul", dict(out=out, lhsT=l, rhs=r_, start=st, stop=sp)), reads=rd, writes=wr)
    toks = []
    with ExitStack() as es:
        sb = lambda name, shape, dt=F32: es.enter_context(nc.sbuf_tensor(name, shape, dt))
        ps = lambda name, shape, dt=F32: es.enter_context(nc.psum_tensor(name, shape, dt))
        xs = sb("f_xs", [128, 4096]); Rxs = R()
        xb = sb("f_xb", [128, 64, 128], BF16); Rxb = R()
        Fb = sb("f_F", [128, 256], BF16); RF = R()
        A_sb = sb("f_A", [64, 128, 256], BF16); RA = R()
        PQ = sb("f_PQ", [128, 2, 64, 128], BF16); RPQ = R()
        Tg = [[sb("f_T%d%d" % (i, j), [64, 16, 128], BF16) for j in range(2)] for i in range(2)]; RTg = [R(), R()]
        wf32 = sb("f_w32", [128, 128]); wfb = sb("f_wb", [128, 128], BF16); Rwf = R()
        Ccb = sb("f_Cc", [128, 128], BF16); mScb = sb("f_mSc", [128, 128], BF16); Rcs = R()
        Gb = sb("f_G", [128, 256], BF16); RG = R()
        ob = [sb("f_ob%d" % i, [128, 512]) for i in range(2)]; Rob = [R(), R()]
        PS = [ps("f_ps%d" % i, [128, 512]) for i in range(2)]; RPS = [R(), R()]
        S.dma(("dma_start", dict(out=Fb[:], in_=I["c_F"])), writes=[RF])
        S.dma(("dma_start", dict(out=Ccb[:], in_=I["c_Cc"])), writes=[Rcs])
        S.dma(("dma_start", dict(out=mScb[:], in_=I["c_mSc"])), writes=[Rcs])
        S.dma(("dma_start", dict(out=wf32[:], in_=I["fw"])), writes=[Rwf])
        op("dve", ("tensor_copy", dict(out=wfb[:], in_=wf32[:])), reads=[Rwf], writes=[Rwf])
        xbf = xb[:].rearrange("p l c -> p (l c)")
        for hf in range(2):
            S.dma(("dma_start", dict(out=xs[:], in_=I["fx"][:, hf * 4096:(hf + 1) * 4096])), writes=[Rxs])
            op("pool", ("tensor_copy", dict(out=xbf[:, hf * 4096:(hf + 1) * 4096], in_=xs[:])), reads=[Rxs], writes=[Rxb])
        for c2 in range(64):
            P_, RP_ = PS[c2 % 2], RPS[c2 % 2]
            for j in range(2):
                mm(P_[0:64, j * 256:(j + 1) * 256], xb[:, :, 2 * c2 + j], Fb[:], [Rxb, RF], [RP_])
            op("act" if c2 % 2 == 0 else "dve", ("activation", dict(out=A_sb[0:64, 2 * c2:2 * c2 + 2, :], in_=P_[0:64, :].rearrange("p (j k) -> p j k", j=2), func=AF.Copy)) if c2 % 2 == 0 else
               ("tensor_copy", dict(out=A_sb[0:64, 2 * c2:2 * c2 + 2, :], in_=P_[0:64, :].rearrange("p (j k) -> p j k", j=2))), reads=[RP_], writes=[RA])
        T1d = I["c_T1"].rearrange("p (k h) -> p k h", h=128)
        T2d = I["c_T2"].rearrange("p (k h) -> p k h", h=128)
        ei = 0
        for grp in range(8):
            tb = grp % 2
            S.dma(("dma_start", dict(out=Tg[tb][0][:], in_=T1d[:, grp * 16:(grp + 1) * 16, :])), writes=[RTg[tb]])
            S.dma(("dma_start", dict(out=Tg[tb][1][:], in_=T2d[:, grp * 16:(grp + 1) * 16, :])), writes=[RTg[tb]])
            for q in range(4):
                P_, RP_ = PS[ei % 2], RPS[ei % 2]
                for j in range(4):
                    kk_ = q * 4 + j
                    kl = grp * 16 + kk_
                    mm(P_[:, j * 128:(j + 1) * 128], A_sb[0:64, :, kl], Tg[tb][0][0:64, kk_, :], [RA, RTg[tb]], [RP_], st=True, sp=False)
                    mm(P_[:, j * 128:(j + 1) * 128], A_sb[0:64, :, 128 + kl], Tg[tb][1][0:64, kk_, :], [RA, RTg[tb]], [RP_], st=False, sp=True)
                kl0 = grp * 16 + q * 4
                for qq in range(2):
                    op("act" if qq == 0 else "dve",
                       ("activation", dict(out=PQ[:, qq, :, kl0:kl0 + 4].rearrange("p h l -> p l h"), in_=P_[:].rearrange("p (l q h) -> p l q h", l=4, q=2)[:, :, qq, :], func=AF.Copy)) if qq == 0 else
                       ("tensor_copy", dict(out=PQ[:, qq, :, kl0:kl0 + 4].rearrange("p h l -> p l h"), in_=P_[:].rearrange("p (l q h) -> p l q h", l=4, q=2)[:, :, qq, :])),
                       reads=[RP_], writes=[RPQ])
                ei += 1
        P_, RP_ = PS[0], RPS[0]
        mm(P_[:, 0:128], Ccb[:], wfb[:], [Rcs, Rwf], [RP_])
        mm(P_[:, 128:256], mScb[:], wfb[:], [Rcs, Rwf], [RP_])
        op("act", ("activation", dict(out=Gb[:], in_=P_[:, 0:256], func=AF.Copy)), reads=[RP_], writes=[RG])
        for t4 in range(16):
            P_, RP_ = PS[(t4 + 1) % 2], RPS[(t4 + 1) % 2]
            for j in range(4):
                kh = 4 * t4 + j
                mm(P_[:, j * 128:(j + 1) * 128], Gb[:, 0:128], PQ[:, 0, kh, :], [RG, RPQ], [RP_], st=True, sp=False)
                mm(P_[:, j * 128:(j + 1) * 128], Gb[:, 128:256], PQ[:, 1, kh, :], [RG, RPQ], [RP_], st=False, sp=True)
            o_, Ro = ob[t4 % 2], Rob[t4 % 2]
            op("act", ("activation", dict(out=o_[:], in_=P_[:], func=AF.Copy)), reads=[RP_], writes=[Ro])
            toks.append(S.dma(("dma_start", dict(out=ydT[:, 512 * t4:512 * (t4 + 1)], in_=o_[:])), reads=[Ro]))
    full_barrier(S)
    return toks


def emit_p3(S, nc, I, yout):
    sb = lambda name, shape, dt=F32: nc.alloc_sbuf_tensor(name, shape, dt)
    ps = lambda name, shape, dt=F32: nc.alloc_psum_tensor(name, shape, dt)
    R = Region
    op = S.op
    mm = lambda out, l, r_, rd, wr, st=True, sp=True: op("pe", ("matmul", dict(out=out, lhsT=l, rhs=r_, start=st, stop=sp)), reads=rd, writes=wr)
    stg = [sb("stg%d" % i, [128, 8, 256]) for i in range(2)]; Rstg = [R(), R()]
    wO = sb("wO", [128, 12, 1024], BF16); RwO = R()
    gN = sb("gN", [128, 1024]); RgN = R()
    gt_all = sb("gt_all", [128, 12, 2048], BF16); Rgt = R()
    ya = [sb("ya%d" % i, [128, 512]) for i in range(2)]; Rya = [R(), R()]
    ga = [sb("ga%d" % i, [128, 512]) for i in range(2)]; Rga = [R(), R()]
    h1t = [sb("h1t%d" % i, [128, 1024]) for i in range(2)]; Rh1 = [R(), R()]
    h2 = sb("h2", [128, 1024]); Rh2 = R()
    sq = sb("sq", [128, 1024]); Rsq = R()
    st = sb("st", [128, 8]); Rst = R()
    yo = [sb("yo%d" % i, [128, 1024]) for i in range(2)]; Ryo = [R(), R()]
    PS_a = ps("PS_a", [128, 512]); RPa = R()
    PS_b = ps("PS_b", [128, 512]); RPb = R()
    wo3 = I["o_w_out"].rearrange("(k p) n -> p k n", p=128)
    si = 0
    for (k0, nk) in ((0, 8), (8, 4)):
        for cq in range(4):
            b_ = si % 2; si += 1
            S.dma(("dma_start", dict(out=stg[b_][:, 0:nk, :], in_=wo3[:, k0:k0 + nk, cq * 256:(cq + 1) * 256])), writes=[Rstg[b_]])
            op("pool", ("tensor_copy", dict(out=wO[:, k0:k0 + nk, cq * 256:(cq + 1) * 256], in_=stg[b_][:, 0:nk, :])), reads=[Rstg[b_]], writes=[RwO])
    S.dma(("dma_start", dict(out=gN[:], in_=I["final_norm_g"].partition_broadcast(128))), writes=[RgN])
    ii = 0
    for blk in range(12):
        src = I["ycT"][blk * 128:(blk + 1) * 128] if blk < 8 else I["ydT"][(blk - 8) * 128:(blk - 7) * 128]
        gsrc = I["gT"][blk * 128:(blk + 1) * 128]
        for j in range(4):
            b_ = ii % 2; ii += 1
            S.dma(("dma_start", dict(out=ya[b_][:], in_=src[:, 512 * j:512 * (j + 1)])), writes=[Rya[b_]])
            S.dma(("dma_start", dict(out=ga[b_][:], in_=gsrc[:, 512 * j:512 * (j + 1)])), writes=[Rga[b_]])
            op("dve" if ii % 2 else "pool", ("tensor_tensor", dict(out=gt_all[:, blk, 512 * j:512 * (j + 1)], in0=ya[b_][:], in1=ga[b_][:], op=ALU.mult)), reads=[Rya[b_], Rga[b_]], writes=[Rgt])
    toks = []
    for i in range(16):
        hb, Rh = h1t[i % 2], Rh1[i % 2]
        S.dma(("dma_start", dict(out=hb[:], in_=I["h1"][128 * i:128 * (i + 1), :])), writes=[Rh])
        for hf, (P_, RP_) in enumerate(((PS_a, RPa), (PS_b, RPb))):
            for k in range(12):
                mm(P_[:], gt_all[:, k, 128 * i:128 * (i + 1)], wO[:, k, hf * 512:(hf + 1) * 512], [Rgt, RwO], [RP_], st=(k == 0), sp=(k == 11))
            op("dve", ("tensor_tensor", dict(out=h2[:, hf * 512:(hf + 1) * 512], in0=P_[:], in1=hb[:, hf * 512:(hf + 1) * 512], op=ALU.add)), reads=[RP_, Rh], writes=[Rh2])
        op("act", ("activation", dict(out=sq[:], in_=h2[:], func=AF.Square)), reads=[Rh2], writes=[Rsq])
        op("dve", ("reduce_sum", dict(out=st[:, 0:1], in_=sq[:], axis=AX.X)), reads=[Rsq], writes=[Rst])
        op("dve", ("tensor_scalar", dict(out=st[:, 1:2], in0=st[:, 0:1], scalar1=1.0 / 1024, scalar2=1e-6, op0=ALU.mult, op1=ALU.add)), reads=[Rst], writes=[Rst])
        op("act", ("activation", dict(out=st[:, 2:3], in_=st[:, 1:2], func=AF.Sqrt)), reads=[Rst], writes=[Rst])
        op("dve", ("reciprocal", dict(out=st[:, 3:4], in_=st[:, 2:3])), reads=[Rst], writes=[Rst])
        op("dve", ("tensor_scalar", dict(out=h2[:], in0=h2[:], scalar1=st[:, 3:4], scalar2=None, op0=ALU.mult)), reads=[Rh2, Rst], writes=[Rh2])
        o_, Ro = yo[i % 2], Ryo[i % 2]
        op("pool", ("tensor_tensor", dict(out=o_[:], in0=h2[:], in1=gN[:], op=ALU.mult)), reads=[Rh2, RgN], writes=[Ro])
        toks.append(S.dma(("dma_start", dict(out=yout[128 * i:128 * (i + 1), :], in_=o_[:])), reads=[Ro]))
    return toks


def _mk(nc, name, shape, dt=None, out=False):
    return nc.dram_tensor(name, list(shape), dt or F32, kind=("ExternalOutput" if out else "ExternalInput")).ap()


W1 = ["e_norm_g", "e_w_in", "e_conv_w", "e_sgu_ln_g", "e_sgu_ln_b", "e_sgu_w", "e_sgu_b", "e_w_out", "o_norm_g", "o_w_in"]


def build_l1(shapes):
    nc = bass.Bass("TRN2", target_bir_lowering=False)
    I = {"xh": _mk(nc, "xh", [2050, 1024]), "c_ident": _mk(nc, "c_ident", [128, 128])}
    for n in W1:
        I[n] = _mk(nc, n, shapes[n])
    O = {"h1": _mk(nc, "h1", [2048, 1024], out=True), "pT": _mk(nc, "pT", [4736, 2048], out=True),
         "fd": _mk(nc, "fd", [2048, 512], out=True)}
    S = Sched(nc)
    toks = emit_p1(S, nc, I, O)
    S.barrier_on("sp", toks)
    S.finalize()
    return nc


def build_l2(consts):
    NB, T = 2, 8192
    nc = bass.Bass("TRN2", target_bir_lowering=False)
    pr, pk, pv, pwa = (_mk(nc, n, [128, NB, T + 2]) for n in ("pr", "pk", "pv", "pwa"))
    prm = _mk(nc, "prm", [128, 17]); w2a2 = _mk(nc, "w2a2", [128, 2, 128])
    A = {k: _mk(nc, k, v.shape) for k, v in consts.items()}
    FI = {"fx": _mk(nc, "fx", [128, 8192]), "fw": _mk(nc, "fw", [128, 128]),
          "c_F": _mk(nc, "c_F", [128, 256], BF16), "c_T1": _mk(nc, "c_T1", [64, 16384], BF16),
          "c_T2": _mk(nc, "c_T2", [64, 16384], BF16), "c_Cc": _mk(nc, "c_Cc", [128, 128], BF16),
          "c_mSc": _mk(nc, "c_mSc", [128, 128], BF16)}
    yout = _mk(nc, "yout", [128, NB, T], out=True)
    ydT = _mk(nc, "ydT", [128, T], out=True)
    S = Sched(nc)
    toks = emit_fnet(S, nc, FI, ydT)
    toks += emit_rwkv(S, nc, A, pr, pk, pv, pwa, prm, w2a2, yout, NB, T)
    S.barrier_on("sp", toks)
    S.finalize()
    return nc


def build_l3():
    nc = bass.Bass("TRN2", target_bir_lowering=False)
    I = {"ycT": _mk(nc, "ycT", [1024, 2048]), "ydT": _mk(nc, "ydT", [512, 2048]), "gT": _mk(nc, "gT", [1536, 2048]),
         "h1": _mk(nc, "h1", [2048, 1024]), "o_w_out": _mk(nc, "o_w_out", [1536, 1024]),
         "final_norm_g": _mk(nc, "final_norm_g", [1024])}
    y = _mk(nc, "y", [2048, 1024], out=True)
    S = Sched(nc)
    toks = emit_p3(S, nc, I, y)
    S.barrier_on("sp", toks)
    S.finalize()
    return nc


def fnet_tables():
    import ml_dtypes
    N = 8192
    nh = np.arange(128); kl = np.arange(128)
    ang = 2 * np.pi * np.outer(nh, kl) / 128
    F = np.concatenate([np.cos(ang), np.sin(ang)], axis=1)
    nl = np.arange(64)[:, None, None]; klo = np.arange(128)[None, :, None]; kh = np.arange(64)[None, None, :]
    beta = 2 * np.pi * ((nl * (klo + 128 * kh)) % N) / N
    T1 = np.concatenate([np.cos(beta), np.sin(beta)], axis=2).reshape(64, 16384)
    T2 = np.concatenate([-np.sin(beta), np.cos(beta)], axis=2).reshape(64, 16384)
    c = np.arange(128); phi = 2 * np.pi * np.outer(c, c) / 128
    nrm = 1 / np.sqrt(N * 128)
    bf = lambda a: np.ascontiguousarray(a.astype(np.float32)).astype(ml_dtypes.bfloat16)
    return {"c_F": bf(F), "c_T1": bf(T1), "c_T2": bf(T2), "c_Cc": bf(np.cos(phi) * nrm), "c_mSc": bf(-np.sin(phi) * nrm)}


def kernel(**inputs):
    f32 = lambda a: np.ascontiguousarray(np.asarray(a), dtype=np.float32)
    inp = {k: f32(v) for k, v in inputs.items()}
    x = inp["x"]
    ncores = 8
    cores = list(range(ncores))
    w1 = {n: np.ascontiguousarray(inp[n][0]) for n in W1}
    ident = np.eye(128, dtype=np.float32)
    maps = []
    for c in cores:
        b, s0 = c // 4, (c % 4) * 2048
        xh = np.zeros((2050, 1024), np.float32)
        xh[1:2049] = x[b, s0:s0 + 2048]
        if s0 > 0:
            xh[0] = x[b, s0 - 1]
        if s0 + 2048 < 8192:
            xh[2049] = x[b, s0 + 2048]
        m = {"xh": xh, "c_ident": ident}
        m.update(w1)
        maps.append(m)
    nc1 = build_l1({n: w1[n].shape for n in W1})
    r1 = run_bass_kernel_spmd(nc1, maps, core_ids=cores).results
    PT = np.concatenate([np.asarray(r["pT"]) for r in r1], axis=1)
    FD = np.concatenate([np.asarray(r["fd"]) for r in r1], axis=0)
    consts = build_consts_np()
    ft = fnet_tables()
    mu, w0, w2, a0, a2 = inp["o_mu"][0], inp["o_w0"][0], inp["o_w2"][0], inp["o_a0"][0], inp["o_a2"][0]
    k_k, k_a, r_k = inp["o_k_k"][0], inp["o_k_a"][0], inp["o_r_k"][0].reshape(-1)
    lg, lb = inp["o_lnx_g"][0], inp["o_lnx_b"][0]
    PT3 = PT.reshape(4736, 2, 8192)
    pad = lambda a: np.ascontiguousarray(np.pad(a, ((0, 0), (0, 0), (1, 1))))
    maps = []
    for c in cores:
        ch = slice(c * 128, (c + 1) * 128)
        m = {"pr": pad(PT3[0:1024][ch]), "pk": pad(PT3[1024:2048][ch]), "pv": pad(PT3[2048:3072][ch]),
             "pwa": pad(PT3[3072:3200])}
        prm = np.zeros((128, 17), np.float32)
        for d in range(2):
            prm[:, 0 + d] = mu[d, 0:1024][ch]; prm[:, 2 + d] = mu[d, 1024:2048][ch]; prm[:, 4 + d] = mu[d, 2048:3072][ch]
            prm[:, 6 + d] = mu[d, 3072:3200]; prm[:, 8 + d] = w0[d][ch]; prm[:, 10 + d] = a0[d][ch]
        prm[:, 12] = k_k[ch]; prm[:, 13] = k_a[ch]; prm[:, 14] = r_k[ch]; prm[:, 15] = lg[ch]; prm[:, 16] = lb[ch]
        m["prm"] = prm
        m["w2a2"] = np.ascontiguousarray(np.concatenate([w2[:, :, ch], a2[:, :, ch]], axis=1).transpose(1, 0, 2))
        m.update(consts)
        b, g = c // 4, c % 4
        m["fx"] = np.ascontiguousarray(FD[b * 8192:(b + 1) * 8192, g * 128:(g + 1) * 128]).reshape(128, 8192)
        m["fw"] = np.ascontiguousarray(inp["o_fnet_w"][0, g])
        m.update(ft)
        maps.append(m)
    nc2 = build_l2(consts)
    r2 = run_bass_kernel_spmd(nc2, maps, core_ids=cores).results
    YC = np.concatenate([np.asarray(r["yout"]).reshape(128, 16384) for r in r2], axis=0)
    YD = np.concatenate([np.concatenate([np.asarray(r2[b * 4 + g]["ydT"]) for g in range(4)], axis=0) for b in range(2)], axis=1)
    maps = []
    for c in cores:
        ts = slice(c * 2048, (c + 1) * 2048)
        maps.append({"ycT": np.ascontiguousarray(YC[:, ts]), "ydT": np.ascontiguousarray(YD[:, ts]),
                     "gT": np.ascontiguousarray(PT[3200:4736, ts]), "h1": np.asarray(r1[c]["h1"]),
                     "o_w_out": np.ascontiguousarray(inp["o_w_out"][0]), "final_norm_g": inp["final_norm_g"]})
    nc3 = build_l3()
    r3 = run_bass_kernel_spmd(nc3, maps, core_ids=cores).results
    y = np.concatenate([np.asarray(r["y"]) for r in r3], axis=0).reshape(2, 8192, 1024)
    return y.astype(np.float32)
```

```python
from contextlib import ExitStack
import itertools
import numpy as np
import concourse.bass as bass
import concourse.mybir as mybir
from concourse.bass_utils import run_bass_kernel_spmd


F32 = mybir.dt.float32
BF16 = mybir.dt.bfloat16
AF = mybir.ActivationFunctionType
ALU = mybir.AluOpType
AX = mybir.AxisListType

N_DMA_SEMS = 8


class Region:
    __slots__ = ("w", "r", "name")

    def __init__(self, name=""):
        self.w = None
        self.r = {}
        self.name = name


class Sched:
    ENGS = ("pe", "dve", "act", "pool", "sp")

    def __init__(self, nc):
        self.nc = nc
        self.prog = {e: [] for e in self.ENGS}
        self.cnt = {}
        self.seen = {e: {} for e in self.ENGS}
        self.dma_rr = {e: 0 for e in self.ENGS}
        self.dma_last = {}
        self.same_engine_raw = True
        self.cut = 0
        self.nrec = 0
        self.log = []

    def _collect(self, eng, mykey, reads, writes):
        waits = {}

        def need(tok, kind):
            if tok is None:
                return
            k, v = tok
            if k == mykey:
                if eng == "pe":
                    return
                if not self.same_engine_raw:
                    return
            if waits.get(k, 0) < v:
                waits[k] = v

        for R in reads:
            need(R.w, "raw")
        for R in writes:
            need(R.w, "waw")
            for k, v in R.r.items():
                need((k, v), "war")
        out = []
        seen = self.seen[eng]
        for k, v in waits.items():
            if seen.get(k, 0) < v:
                seen[k] = v
                out.append((k, v))
        return out

    def _commit(self, tok, reads, writes):
        for R in writes:
            R.w = tok
            R.r = {}
        k, v = tok
        for R in reads:
            if R.r.get(k, 0) < v:
                R.r[k] = v

    def op(self, eng, fn, reads=(), writes=()):
        self.nrec += 1
        if self.cut and self.nrec > self.cut:
            return None
        if self.cut:
            self.log.append((self.nrec, eng, fn[0] if isinstance(fn, tuple) else "fn", str(fn[1].get("out", ""))[:120] if isinstance(fn, tuple) else ""))
        key = eng
        waits = self._collect(eng, key, reads, writes)
        idx = self.cnt.get(key, 0) + 1
        self.cnt[key] = idx
        tok = (key, idx)
        self.prog[eng].append([waits, fn, tok])
        self._commit(tok, reads, writes)
        return tok

    def dma(self, fn, reads=(), writes=(), q="sp"):
        self.nrec += 1
        if self.cut and self.nrec > self.cut:
            return None
        i = self.dma_rr[q]
        self.dma_rr[q] = (i + 1) % N_DMA_SEMS
        key = "dma_%s_%d" % (q, i)
        waits = self._collect(q, key, reads, writes)
        prev = self.cnt.get(key, 0)
        if prev > 0 and self.seen[q].get(key, 0) < prev:
            self.seen[q][key] = prev
            waits.append((key, prev))
        idx = prev + 1
        self.cnt[key] = idx
        tok = (key, idx)
        self.prog[q].append([waits, fn, tok])
        self._commit(tok, reads, writes)
        return tok

    def finalize(self):
        nc = self.nc
        waited = {}
        for e in self.ENGS:
            for waits, fn, tok in self.prog[e]:
                for k, v in waits:
                    waited.setdefault(k, set()).add(v)
        self.final_waits = []
        sem_of = {}
        val_of = {}
        for k, s in waited.items():
            sem_of[k] = nc.alloc_semaphore("s_" + k)
            isdma = k.startswith("dma_")
            step = 16 if isdma else 1
            if isdma:
                val_of[k] = None
            else:
                val_of[k] = {v: (i + 1) for i, v in enumerate(sorted(s))}
        engobj = {"pe": nc.tensor, "dve": nc.vector, "act": nc.scalar,
                  "pool": nc.gpsimd, "sp": nc.sync}

        def value(k, v):
            if val_of[k] is None:
                return 16 * v
            return val_of[k][v]

        def emit(e):
            def body(eng):
                for waits, fn, tok in self.prog[e]:
                    for k, v in waits:
                        eng.wait_ge(sem_of[k], value(k, v))
                    if fn is None:
                        continue
                    if isinstance(fn, tuple):
                        ins = getattr(eng, fn[0])(**fn[1])
                    else:
                        ins = fn(eng)
                    k, v = tok
                    if k in sem_of:
                        if val_of[k] is None:
                            ins.then_inc(sem_of[k], 16)
                        elif v in val_of[k]:
                            ins.then_inc(sem_of[k], 1)
            return body

        with nc.Block() as block:
            for e, dec in (("sp", block.sync), ("pe", block.tensor), ("dve", block.vector),
                           ("act", block.scalar), ("pool", block.gpsimd)):
                if self.prog[e]:
                    dec(emit(e))
        self.n_sems = len(sem_of)
        return self.n_sems

    def barrier_on(self, eng, toks):
        waits = []
        for tk in toks:
            if tk is None:
                continue
            k, v = tk
            if self.seen[eng].get(k, 0) < v:
                self.seen[eng][k] = v
                waits.append((k, v))
        if waits:
            self.prog[eng].append([waits, None, ("_none", 0)])


C = 128
BLK = 512
NEG_E = -float(np.exp(-0.5))
GN_EPS = 64e-5


def build_consts_np():
    idx = np.arange(128)
    lt = (idx[:, None] < idx[None, :]).astype(np.float32)
    le = (idx[:, None] <= idx[None, :]).astype(np.float32)
    gt = lt.T.copy()
    ge = le.T.copy()
    m4f = np.stack([lt, gt, gt, le], axis=1)
    m4b = np.stack([gt, lt, lt, ge], axis=1)
    mk = np.stack([le, ge], axis=1)
    ident = np.eye(128, dtype=np.float32)
    bd = np.kron(np.eye(2, dtype=np.float32), np.ones((64, 64), np.float32))
    scanm = np.ones((128, BLK), np.float32)
    scanm[:, ::C] = 0.0
    return {"c_m4": np.stack([m4f, m4b], axis=1).reshape(128, 2 * 4 * 128).copy(),
            "c_mk": mk.reshape(128, 256).copy(), "c_ident": ident, "c_bd": bd, "c_scanm": scanm}


XST = False


def emit_rwkv(S, nc, A, pr, pk, pv, pwa, prm, w2a2, yout, NB, T):
    sb = lambda name, shape, dt=F32: nc.alloc_sbuf_tensor(name, shape, dt)
    ps = lambda name, shape, dt=F32: nc.alloc_psum_tensor(name, shape, dt)
    R = Region
    nblk = T // BLK

    m4f = sb("m4f", [128, 2, 4, 128]); Rm4 = R()
    mkf = sb("mkf", [128, 2, 128]); Rmk = R()
    identf = sb("identf", [128, 128]); Ridf = R()
    identb = sb("identb", [128, 128], BF16); Ridb = R()
    bdf = sb("bdf", [128, 128]); Rbd = R()
    bdr = sb("bdr", [128, 128]); Rbdr = R()
    bdm = sb("bdm", [128, 128]); Rbdm = R()
    scanm = sb("scanm", [128, BLK]); Rsc = R()
    prmt = sb("prmt", [128, 17]); Rprm = R()
    w2f = sb("w2f", [128, 2, 128]); Rw2f = R()
    w2b = sb("w2b", [128, 2, 128], BF16); Rw2b = R()
    S.dma(("dma_start", dict(out=m4f[:].rearrange("p a b c -> p (a b c)"), in_=A["c_m4"])), writes=[Rm4])
    S.dma(("dma_start", dict(out=mkf[:].rearrange("p a c -> p (a c)"), in_=A["c_mk"])), writes=[Rmk])
    S.dma(("dma_start", dict(out=identf[:], in_=A["c_ident"])), writes=[Ridf])
    S.dma(("dma_start", dict(out=bdf[:], in_=A["c_bd"])), writes=[Rbd])
    S.dma(("dma_start", dict(out=scanm[:], in_=A["c_scanm"])), writes=[Rsc])
    S.dma(("dma_start", dict(out=prmt[:], in_=prm)), writes=[Rprm])
    S.dma(("dma_start", dict(out=w2f[:], in_=w2a2)), writes=[Rw2f])
    S.op("dve", ("tensor_copy", dict(out=identb[:], in_=identf[:])), reads=[Ridf], writes=[Ridb])
    S.op("dve", ("tensor_copy", dict(out=w2b[:], in_=w2f[:])), reads=[Rw2f], writes=[Rw2b])
    PM = lambda c: prmt[:, c:c + 1]
    S.op("dve", ("tensor_scalar", dict(out=bdr[:], in0=bdf[:], scalar1=PM(14), scalar2=None, op0=ALU.mult)), reads=[Rbd, Rprm], writes=[Rbdr])
    S.op("dve", ("tensor_scalar", dict(out=bdm[:], in0=bdf[:], scalar1=1.0 / 64, scalar2=None, op0=ALU.mult)), reads=[Rbd], writes=[Rbdm])

    def T2(name, dt=F32, n=BLK):
        return sb(name, [128, n], dt), R()
    ld = {}
    for nm in ("pr", "pk", "pv", "pwa"):
        ld[nm] = (sb("ld_" + nm, [128, BLK + 2]), R())
    tmp, Rtmp = T2("tmp")
    qr, Rqr = T2("qr"); qk, Rqk = T2("qk"); qv, Rqv = T2("qv"); qwa, Rqwa = T2("qwa")
    twa, Rtwa = T2("twa", BF16)
    sw, Rsw = T2("sw"); asg, Rasg = T2("asg")
    logw, Rlogw = T2("logw"); lin, Rlin = T2("lin"); linm, Rlinm = T2("linm"); lexm, Rlexm = T2("lexm")
    lex, Rlex = T2("lex"); lint, Rlint = T2("lint")
    e1, Re1 = T2("e1"); e1x, Re1x = T2("e1x"); e2, Re2 = T2("e2"); e3S = [sb("e3%d" % i, [128, BLK]) for i in range(2)]; Re3S = [R(), R()]; e3x, Re3x = T2("e3x"); e4, Re4 = T2("e4")
    kk, Rkk = T2("kk"); kk2, Rkk2 = T2("kk2"); rin, Rrin = T2("rin"); kkn, Rkkn = T2("kkn")
    kp, Rkp = T2("kp"); bv, Rbv = T2("bv"); rk, Rrk = T2("rk")
    rtS = [sb("rt%d" % i, [128, BLK], BF16) for i in range(2)]; RrtS = [R(), R()]; atS = [sb("at%d" % i, [128, BLK], BF16) for i in range(2)]; RatS = [R(), R()]; ktS = [sb("kt%d" % i, [128, BLK], BF16) for i in range(2)]; RktS = [R(), R()]; btS = [sb("bt%d" % i, [128, BLK], BF16) for i in range(2)]; RbtS = [R(), R()]
    r0S = [sb("r0%d" % i, [128, BLK]) for i in range(2)]; Rr0S = [R(), R()]; a0bS = [sb("a0b%d" % i, [128, BLK], BF16) for i in range(2)]; Ra0bS = [R(), R()]; kEbS = [sb("kEb%d" % i, [128, BLK], BF16) for i in range(2)]; RkEbS = [R(), R()]; bEbS = [sb("bEb%d" % i, [128, BLK], BF16) for i in range(2)]; RbEbS = [R(), R()]
    qvbS = [sb("qvb%d" % i, [128, BLK], BF16) for i in range(2)]; RqvbS = [R(), R()]
    ysum = sb("ysum", [128, T]); Rys = [R() for _ in range(T // C)]
    bsum = sb("bsum", [128, T]); Rbs = [R() for _ in range(nblk)]
    TT = [sb("TT%d" % i, [128, 4, 128], BF16) for i in range(2)]; RTT = [R(), R()]
    SBM = [sb("SBM%d" % i, [128, 2, 4, 128], BF16) for i in range(2)]; RSBM = [R(), R()]
    MKR = [sb("MKR%d" % i, [128, 2, 128]) for i in range(2)]; RMKR = [R(), R()]
    SX = [sb("SX%d" % i, [128, 2, 192], BF16) for i in range(2)]; RSX = [R(), R()]
    SAB = [sb("SAB%d" % i, [128, 2, 2, 128], BF16) for i in range(2)]; RSAB = [R(), R()]
    Gb = sb("Gb", [128, 128], BF16); RGb = R()
    Hb = sb("Hb", [128, 2, 128], BF16); RHb = R()
    Pb = sb("Pb", [128, 64], BF16); RPb = R()
    Zb = sb("Zb", [128, 2, 64], BF16); RZb = R()
    STz = [sb("STz%d" % h, [128, 64], BF16) for h in range(2)]; RST = [R(), R()]
    identP = sb("identP", [128, 64]); mkb = sb("mkb", [128, 2, 2, 128])
    HS = [slice(0, 64), slice(64, 128)]
    fin1, Rfin1 = T2("fin1"); fin2, Rfin2 = T2("fin2"); fin3, Rfin3 = T2("fin3")

    PS_M = ps("PS_M", [128, 2, 4, 128]); RPS_M = R()
    PS_K = ps("PS_K", [128, 512]); RPS_K = R()
    PS_X = [ps("PS_X%d" % h, [128, 512]) for h in range(2)]; RPS_X = R()
    PS_AB = ps("PS_AB", [128, 2, 2, 128]); RPS_AB = R()
    PS_G = ps("PS_G", [128, 512]); RPS_G = R()
    PS_T = ps("PS_T", [128, 8, 128], BF16); RPS_T = R()
    PS_P1 = PS_AB[:].rearrange("p a b c -> p (a b c)"); RPS_P1 = RPS_AB
    PS_P2 = PS_P1; RPS_P2 = RPS_AB
    mm = lambda out, l, r_, rd, wr, st=True, sp=True, sg=False: S.op("pe", ("matmul", dict(out=out, lhsT=l, rhs=r_, start=st, stop=sp, skip_group_check=sg)), reads=rd, writes=wr)
    S.op("pool", ("tensor_copy", dict(out=identP[0:64, :], in_=identf[0:64, 0:64])), reads=[Ridf], writes=[Ridf])
    S.op("pool", ("tensor_copy", dict(out=identP[64:128, :], in_=identf[64:128, 64:128])), reads=[Ridf], writes=[Ridf])
    for h in range(2):
        S.op("pool", ("tensor_copy", dict(out=mkb[:, :, h, :], in_=mkf[:])), reads=[Rmk], writes=[Rmk])
    ytmp = sb("ytmp", [128, 128]); Rytmp = R()
    out_toks = []
    NFILL = 7
    NPREP = 2
    def prep_gen(b, d, blk, pp):
        bwd = (d == 1)
        midc, totc = (C // 2 - 1, C - 1) if not bwd else (C // 2, 0)
        t0 = blk * BLK
        rt_, Rrt_ = rtS[pp], RrtS[pp]
        at_, Rat_ = atS[pp], RatS[pp]
        kt_, Rkt_ = ktS[pp], RktS[pp]
        bt_, Rbt_ = btS[pp], RbtS[pp]
        r0_, Rr0_ = r0S[pp], Rr0S[pp]
        a0b_, Ra0b_ = a0bS[pp], Ra0bS[pp]
        kEb_, RkEb_ = kEbS[pp], RkEbS[pp]
        bEb_, RbEb_ = bEbS[pp], RbEbS[pp]
        qvb_, Rqvb_ = qvbS[pp], RqvbS[pp]
        e3_, Re3_ = e3S[pp], Re3S[pp]
        for nm, src in (("pr", pr), ("pk", pk), ("pv", pv), ("pwa", pwa)):
            tl, Rl = ld[nm]
            S.dma(("dma_start", dict(out=tl[:], in_=src[:, b, t0:t0 + BLK + 2])), writes=[Rl])
            yield
        sh = (slice(0, BLK) if not bwd else slice(2, BLK + 2))
        cur = slice(1, BLK + 1)
        for nm, q, Rq, mc in (("pr", qr, Rqr, 0), ("pk", qk, Rqk, 2), ("pv", qv, Rqv, 4), ("pwa", qwa, Rqwa, 6)):
            tl, Rl = ld[nm]
            S.op("dve", ("tensor_tensor", dict(out=tmp[:], in0=tl[:, sh], in1=tl[:, cur], op=ALU.subtract)), reads=[Rl], writes=[Rtmp])
            yield
            S.op("dve", ("scalar_tensor_tensor", dict(out=q[:], in0=tmp[:], scalar=PM(mc + d), in1=tl[:, cur], op0=ALU.mult, op1=ALU.add)), reads=[Rtmp, Rl, Rprm], writes=[Rq])
            yield
        S.op("act", ("activation", dict(out=twa[0:64, :], in_=qwa[0:64, :], func=AF.Tanh)), reads=[Rqwa], writes=[Rtwa])
        yield
        S.op("dve", ("tensor_copy", dict(out=twa[64:128, :], in_=qwa[64:128, :])), reads=[Rqwa], writes=[Rtwa])
        yield
        S.op("pe", ("matmul", dict(out=PS_P1, lhsT=w2b[0:64, d, :], rhs=twa[0:64, :], start=True, stop=True)), reads=[Rw2b, Rtwa], writes=[RPS_P1])
        S.op("act", ("activation", dict(out=sw[:], in_=PS_P1, func=AF.Sigmoid, bias=PM(8 + d))), reads=[RPS_P1, Rprm], writes=[Rsw])
        yield
        S.op("pe", ("matmul", dict(out=PS_P2, lhsT=w2b[64:128, d, :], rhs=twa[64:128, :], start=True, stop=True)), reads=[Rw2b, Rtwa], writes=[RPS_P2])
        S.op("act", ("activation", dict(out=asg[:], in_=PS_P2, func=AF.Sigmoid, bias=PM(10 + d))), reads=[RPS_P2, Rprm], writes=[Rasg])
        yield
        S.op("dve", ("tensor_scalar", dict(out=logw[:], in0=sw[:], scalar1=NEG_E, scalar2=None, op0=ALU.mult)), reads=[Rsw], writes=[Rlogw])
        yield
        S.op("dve", ("tensor_tensor_scan", dict(out=lin[:], data0=scanm[:], data1=logw[:], initial=0.0, op0=ALU.mult, op1=ALU.add)), reads=[Rsc, Rlogw], writes=[Rlin])
        yield
        lin3 = lambda tl: tl[:].rearrange("p (c t) -> p c t", t=C)
        bc = lambda tl, col: lin3(tl)[:, :, col:col + 1].to_broadcast([128, BLK // C, C])
        if bwd:
            S.op("dve", ("tensor_tensor", dict(out=lin3(tmp), in0=bc(lin, C - 1), in1=lin3(lin), op=ALU.subtract)), reads=[Rlin], writes=[Rtmp])
            yield
            S.op("dve", ("tensor_tensor", dict(out=lin[:], in0=tmp[:], in1=logw[:], op=ALU.add)), reads=[Rtmp, Rlogw], writes=[Rlin])
            yield
        S.op("dve", ("tensor_tensor", dict(out=lin3(linm), in0=lin3(lin), in1=bc(lin, midc), op=ALU.subtract)), reads=[Rlin], writes=[Rlinm])
        yield
        S.op("dve", ("tensor_tensor", dict(out=lexm[:], in0=linm[:], in1=logw[:], op=ALU.subtract)), reads=[Rlinm, Rlogw], writes=[Rlexm])
        yield
        S.op("dve", ("tensor_tensor", dict(out=lex[:], in0=lin[:], in1=logw[:], op=ALU.subtract)), reads=[Rlin, Rlogw], writes=[Rlex])
        yield
        S.op("dve", ("tensor_tensor", dict(out=lin3(lint), in0=lin3(lin), in1=bc(lin, totc), op=ALU.subtract)), reads=[Rlin], writes=[Rlint])
        yield
        S.op("act", ("activation", dict(out=e1[:], in_=linm[:], func=AF.Exp)), reads=[Rlinm], writes=[Re1])
        yield
        S.op("act", ("activation", dict(out=e1x[:], in_=lexm[:], func=AF.Exp)), reads=[Rlexm], writes=[Re1x])
        yield
        S.op("act", ("activation", dict(out=e2[:], in_=linm[:], func=AF.Exp, scale=-1.0)), reads=[Rlinm], writes=[Re2])
        yield
        S.op("act", ("activation", dict(out=e3_[:], in_=lin[:], func=AF.Exp)), reads=[Rlin], writes=[Re3_])
        yield
        S.op("act", ("activation", dict(out=e3x[:], in_=lex[:], func=AF.Exp)), reads=[Rlex], writes=[Re3x])
        yield
        S.op("act", ("activation", dict(out=e4[:], in_=lint[:], func=AF.Exp, scale=-1.0)), reads=[Rlint], writes=[Re4])
        yield
        S.op("dve", ("tensor_scalar", dict(out=kk[:], in0=qk[:], scalar1=PM(12), scalar2=None, op0=ALU.mult)), reads=[Rqk, Rprm], writes=[Rkk])
        yield
        S.op("pool", ("tensor_tensor", dict(out=kk2[:], in0=kk[:], in1=kk[:], op=ALU.mult)), reads=[Rkk], writes=[Rkk2])
        yield
        S.op("pe", ("matmul", dict(out=PS_P1, lhsT=bdf[:], rhs=kk2[:], start=True, stop=True)), reads=[Rbd, Rkk2], writes=[RPS_P1])
        S.op("dve", ("tensor_scalar", dict(out=rin[:], in0=PS_P1, scalar1=1e-12, scalar2=None, op0=ALU.max)), reads=[RPS_P1], writes=[Rrin])
        yield
        S.op("act", ("activation", dict(out=rin[:], in_=rin[:], func=AF.Sqrt)), reads=[Rrin], writes=[Rrin])
        yield
        S.op("dve", ("reciprocal", dict(out=rin[:], in_=rin[:])), reads=[Rrin], writes=[Rrin])
        yield
        S.op("dve", ("tensor_tensor", dict(out=kkn[:], in0=kk[:], in1=rin[:], op=ALU.mult)), reads=[Rkk, Rrin], writes=[Rkkn])
        yield
        S.op("dve", ("tensor_scalar", dict(out=tmp[:], in0=asg[:], scalar1=-1.0, scalar2=PM(13), op0=ALU.add, op1=ALU.mult)), reads=[Rasg, Rprm], writes=[Rtmp])
        yield
        S.op("dve", ("scalar_tensor_tensor", dict(out=kp[:], in0=tmp[:], scalar=1.0, in1=qk[:], op0=ALU.add, op1=ALU.mult)), reads=[Rtmp, Rqk], writes=[Rkp])
        yield
        S.op("pool", ("tensor_tensor", dict(out=bv[:], in0=kkn[:], in1=asg[:], op=ALU.mult)), reads=[Rkkn, Rasg], writes=[Rbv])
        yield
        S.op("pool", ("tensor_tensor", dict(out=rk[:], in0=qr[:], in1=kp[:], op=ALU.mult)), reads=[Rqr, Rkp], writes=[Rrk])
        yield
        S.op("pe", ("matmul", dict(out=PS_P2, lhsT=bdr[:], rhs=rk[:], start=True, stop=True)), reads=[Rbdr, Rrk], writes=[RPS_P2])
        bsl = bsum[:, t0:t0 + BLK]
        if d == 0:
            S.op("dve", ("tensor_tensor", dict(out=bsl, in0=PS_P2, in1=qv[:], op=ALU.mult)), reads=[RPS_P2, Rqv], writes=[Rbs[blk]])
            yield
        else:
            S.op("dve", ("tensor_tensor", dict(out=tmp[:], in0=PS_P2, in1=qv[:], op=ALU.mult)), reads=[RPS_P2, Rqv], writes=[Rtmp])
            yield
            S.op("pool", ("tensor_tensor", dict(out=bsl, in0=bsl, in1=tmp[:], op=ALU.add)), reads=[Rtmp, Rbs[blk]], writes=[Rbs[blk]])
            yield
        S.op("dve", ("tensor_tensor", dict(out=rt_[:], in0=qr[:], in1=e1[:], op=ALU.mult)), reads=[Rqr, Re1], writes=[Rrt_])
        yield
        S.op("dve", ("scalar_tensor_tensor", dict(out=at_[:], in0=kkn[:], scalar=-1.0, in1=e1x[:], op0=ALU.mult, op1=ALU.mult)), reads=[Rkkn, Re1x], writes=[Rat_])
        yield
        S.op("pool", ("tensor_tensor", dict(out=kt_[:], in0=kp[:], in1=e2[:], op=ALU.mult)), reads=[Rkp, Re2], writes=[Rkt_])
        yield
        S.op("pool", ("tensor_tensor", dict(out=bt_[:], in0=bv[:], in1=e2[:], op=ALU.mult)), reads=[Rbv, Re2], writes=[Rbt_])
        yield
        S.op("pool", ("tensor_tensor", dict(out=r0_[:], in0=qr[:], in1=e3_[:], op=ALU.mult)), reads=[Rqr, Re3_], writes=[Rr0_])
        yield
        S.op("dve", ("scalar_tensor_tensor", dict(out=a0b_[:], in0=kkn[:], scalar=-1.0, in1=e3x[:], op0=ALU.mult, op1=ALU.mult)), reads=[Rkkn, Re3x], writes=[Ra0b_])
        yield
        S.op("pool", ("tensor_tensor", dict(out=kEb_[:], in0=kp[:], in1=e4[:], op=ALU.mult)), reads=[Rkp, Re4], writes=[RkEb_])
        yield
        S.op("pool", ("tensor_tensor", dict(out=bEb_[:], in0=bv[:], in1=e4[:], op=ALU.mult)), reads=[Rbv, Re4], writes=[RbEb_])
        yield
        S.op("act", ("activation", dict(out=qvb_[:], in_=qv[:], func=AF.Copy)), reads=[Rqv], writes=[Rqvb_])
        yield


    def block_stages(b, d, blk, pp):
        bwd = (d == 1)
        midc, totc = (C // 2 - 1, C - 1) if not bwd else (C // 2, 0)
        t0 = blk * BLK
        rt_, Rrt_ = rtS[pp], RrtS[pp]
        at_, Rat_ = atS[pp], RatS[pp]
        kt_, Rkt_ = ktS[pp], RktS[pp]
        bt_, Rbt_ = btS[pp], RbtS[pp]
        r0_, Rr0_ = r0S[pp], Rr0S[pp]
        a0b_, Ra0b_ = a0bS[pp], Ra0bS[pp]
        kEb_, RkEb_ = kEbS[pp], RkEbS[pp]
        bEb_, RbEb_ = bEbS[pp], RbEbS[pp]
        qvb_, Rqvb_ = qvbS[pp], RqvbS[pp]
        e3_, Re3_ = e3S[pp], Re3S[pp]

        def stage1(ck):
            ci, cs, gci, p = ck
            for i, (src, Rs) in enumerate(((qvb_, Rqvb_), (a0b_, Ra0b_), (bEb_, RbEb_), (kEb_, RkEb_))):
                S.op("pe", ("transpose", dict(out=PS_T[:, i, :], in_=src[:, cs], identity=identb[:])), reads=[Rs, Ridb], writes=[RPS_T])
            yield
            S.op("act", ("activation", dict(out=TT[p][:], in_=PS_T[:, 0:4, :], func=AF.Copy)), reads=[RPS_T], writes=[RTT[p]])
            yield
            for h in range(2):
                hs = HS[h]
                mm(PS_M[:, h, 0, :], bt_[hs, cs], at_[hs, cs], [Rbt_, Rat_], [RPS_M])
                mm(PS_M[:, h, 1, :], at_[hs, cs], bt_[hs, cs], [Rbt_, Rat_], [RPS_M])
                yield
                mm(PS_M[:, h, 2, :], at_[hs, cs], kt_[hs, cs], [Rkt_, Rat_], [RPS_M])
                mm(PS_M[:, h, 3, :], bt_[hs, cs], rt_[hs, cs], [Rbt_, Rrt_], [RPS_M])
                yield
                mm((PS_K if h == 0 else PS_G)[:, 0:128], kt_[hs, cs], rt_[hs, cs], [Rkt_, Rrt_], [RPS_K if h == 0 else RPS_G])
                yield
            for h in range(2):
                S.op("dve", ("tensor_tensor", dict(out=SBM[p][:, h], in0=PS_M[:, h], in1=m4f[:, d, :, :], op=ALU.mult)), reads=[RPS_M, Rm4], writes=[RSBM[p]])
                yield
            S.op("dve", ("tensor_tensor", dict(out=MKR[p][:, 0, :], in0=PS_K[:, 0:128], in1=mkf[:, d, :], op=ALU.mult)), reads=[RPS_K, Rmk], writes=[RMKR[p]])
            yield
            S.op("dve", ("tensor_tensor", dict(out=MKR[p][:, 1, :], in0=PS_G[:, 0:128], in1=mkf[:, d, :], op=ALU.mult)), reads=[RPS_G, Rmk], writes=[RMKR[p]])
            yield
            S.op("act", ("activation", dict(out=SX[p][:, :, 0:128], in_=SBM[p][:, :, 3, :], func=AF.Copy)), reads=[RSBM[p]], writes=[RSX[p]])
            S.op("pool", ("tensor_copy", dict(out=SX[p][:, :, 128:192], in_=TT[p][:, 2, :].rearrange("p (h j) -> p h j", h=2))), reads=[RTT[p]], writes=[RSX[p]])
            yield

        def stage2(ck):
            ci, cs, gci, p = ck
            for h in range(2):
                mm(PS_X[h][:, 0:192], identb[:], SX[p][:, h, :], [Ridb, RSX[p]], [RPS_X], st=True, sp=True)
            A_ = [SBM[p][:, h, 1, :] for h in range(2)]
            B_ = [SBM[p][:, h, 0, :] for h in range(2)]
            Rcur = RSBM[p]
            for lv in range(7):
                for h in range(2):
                    mm(PS_X[h][:, 0:192], A_[h], SX[p][:, h, :], [Rcur, RSX[p]], [RPS_X], st=False, sp=True, sg=True)
                if lv < 6:
                    nb = lv % 2
                    for h in range(2):
                        mm(PS_AB[:, h, 0, :], B_[h], A_[h], [Rcur], [RPS_AB])
                        mm(PS_AB[:, h, 1, :], A_[h], B_[h], [Rcur], [RPS_AB])
                S.op("dve", ("tensor_copy", dict(out=SX[p][:, 0, :], in_=PS_X[0][:, 0:192])), reads=[RPS_X], writes=[RSX[p]])
                S.op("act", ("activation", dict(out=SX[p][:, 1, :], in_=PS_X[1][:, 0:192], func=AF.Copy)), reads=[RPS_X], writes=[RSX[p]])
                if lv < 6:
                    S.op("act", ("activation", dict(out=SAB[nb][:].rearrange("p a b c -> p (a b c)"), in_=PS_AB[:].rearrange("p a b c -> p (a b c)"), func=AF.Copy)), reads=[RPS_AB], writes=[RSAB[nb]])
                    A_ = [SAB[nb][:, h, 0, :] for h in range(2)]
                    B_ = [SAB[nb][:, h, 1, :] for h in range(2)]
                    Rcur = RSAB[nb]
                yield

        def stage3(ck):
            ci, cs, gci, p = ck
            for h in range(2):
                hs = HS[h]
                a0T = TT[p][:, 1, hs]
                mm(PS_G[hs, 0:128], a0T, SX[p][:, h, 0:128], [RTT[p], RSX[p]], [RPS_G])
                mm(PS_G[hs, 128:192], a0T, SX[p][:, h, 128:192], [RTT[p], RSX[p]], [RPS_G])
                yield
                mm(PS_G[:, 192 + 128 * h:320 + 128 * h], SBM[p][:, h, 2, :], SX[p][:, h, 0:128], [RSBM[p], RSX[p]], [RPS_G])
                mm(PS_K[:, 256 + 64 * h:320 + 64 * h], SBM[p][:, h, 2, :], SX[p][:, h, 128:192], [RSBM[p], RSX[p]], [RPS_K])
                yield
            S.op("dve", ("tensor_tensor", dict(out=Gb[:], in0=PS_G[:, 0:128], in1=r0_[:, cs], op=ALU.add)), reads=[RPS_G, Rr0_], writes=[RGb])
            yield
            S.op("dve", ("tensor_tensor", dict(out=Hb[:], in0=PS_G[:, 192:448].rearrange("p (h t) -> p h t", h=2), in1=MKR[p][:], op=ALU.add)), reads=[RPS_G, RMKR[p]], writes=[RHb])
            yield
            tcol = ci * C + totc
            S.op("dve", ("scalar_tensor_tensor", dict(out=Pb[:], in0=identP[:], scalar=e3_[:, tcol:tcol + 1], in1=PS_G[:, 128:192], op0=ALU.mult, op1=ALU.add)), reads=[RPS_G, Ridf, Re3_], writes=[RPb])
            yield
            S.op("dve", ("tensor_tensor", dict(out=Zb[:], in0=PS_K[:, 256:384].rearrange("p (h j) -> p h j", h=2), in1=TT[p][:, 3, :].rearrange("p (h j) -> p h j", h=2), op=ALU.add)), reads=[RPS_K, RTT[p]], writes=[RZb])
            yield
            for h in range(2):
                hs = HS[h]
                mm(PS_M[hs, 0, 0, :], STz[h][:], Gb[:], [RST[h], RGb], [RPS_M], st=True, sp=False)
                mm(PS_M[hs, 0, 0, :], TT[p][:, 0, hs], Hb[:, h, :], [RTT[p], RHb], [RPS_M], st=False, sp=True)
                yield
                mm(PS_M[hs, 0, 1, 0:64], Pb[:], STz[h][:], [RPb, RST[h]], [RPS_M], st=True, sp=False)
                mm(PS_M[hs, 0, 1, 0:64], Zb[:, h, :], TT[p][:, 0, hs], [RZb, RTT[p]], [RPS_M], st=False, sp=True)
                yield
            ysl = ysum[:, t0 + ci * C: t0 + (ci + 1) * C]
            if d == 0:
                S.op("act", ("activation", dict(out=ysl, in_=PS_M[:, 0, 0, :], func=AF.Copy)), reads=[RPS_M], writes=[Rys[gci]])
            else:
                S.op("act", ("activation", dict(out=ytmp[:, 0:128], in_=PS_M[:, 0, 0, :], func=AF.Copy)), reads=[RPS_M], writes=[Rytmp])
                S.op("dve", ("tensor_tensor", dict(out=ysl, in0=ytmp[:, 0:128], in1=ysl, op=ALU.add)), reads=[Rytmp, Rys[gci]], writes=[Rys[gci]])
            yield
            for h in range(2):
                hs = HS[h]
                S.op("act", ("activation", dict(out=STz[h][hs, :], in_=PS_M[hs, 0, 1, 0:64], func=AF.Copy)), reads=[RPS_M], writes=[RST[h]])
            yield

        return stage1, stage2, stage3

    def finalize_batch(b):
        for blk in range(nblk):
            t0 = blk * BLK
            ysl = ysum[:, t0:t0 + BLK]
            Rin = Rys[t0 // C: (t0 + BLK) // C]
            S.op("pe", ("matmul", dict(out=PS_P1, lhsT=bdm[:], rhs=ysl, start=True, stop=True)), reads=[Rbdm] + Rin, writes=[RPS_P1])
            S.op("dve", ("tensor_tensor", dict(out=fin1[:], in0=ysl, in1=PS_P1, op=ALU.subtract)), reads=[RPS_P1] + Rin, writes=[Rfin1])
            S.op("pool", ("tensor_tensor", dict(out=fin2[:], in0=fin1[:], in1=fin1[:], op=ALU.mult)), reads=[Rfin1], writes=[Rfin2])
            S.op("pe", ("matmul", dict(out=PS_P2, lhsT=bdm[:], rhs=fin2[:], start=True, stop=True)), reads=[Rbdm, Rfin2], writes=[RPS_P2])
            S.op("dve", ("tensor_scalar", dict(out=fin3[:], in0=PS_P2, scalar1=GN_EPS, scalar2=None, op0=ALU.add)), reads=[RPS_P2], writes=[Rfin3])
            S.op("act", ("activation", dict(out=fin3[:], in_=fin3[:], func=AF.Sqrt)), reads=[Rfin3], writes=[Rfin3])
            S.op("dve", ("reciprocal", dict(out=fin3[:], in_=fin3[:])), reads=[Rfin3], writes=[Rfin3])
            S.op("dve", ("tensor_tensor", dict(out=fin1[:], in0=fin1[:], in1=fin3[:], op=ALU.mult)), reads=[Rfin1, Rfin3], writes=[Rfin1])
            S.op("dve", ("tensor_scalar", dict(out=fin2[:], in0=fin1[:], scalar1=PM(15), scalar2=PM(16), op0=ALU.mult, op1=ALU.add)), reads=[Rfin1, Rprm], writes=[Rfin2])
            S.op("dve", ("tensor_tensor", dict(out=fin2[:], in0=fin2[:], in1=bsum[:, t0:t0 + BLK], op=ALU.add)), reads=[Rfin2, Rbs[blk]], writes=[Rfin2])
            out_toks.append(S.dma(("dma_start", dict(out=yout[:, b, t0:t0 + BLK], in_=fin2[:])), reads=[Rfin2]))

    sched_blocks = []
    for b in range(NB):
        for d in range(2):
            order = list(range(nblk)) if d == 0 else list(range(nblk - 1, -1, -1))
            for n_, blk in enumerate(order):
                sched_blocks.append((b, d, blk, n_ == 0, (n_ == len(order) - 1) and d == 1))
    pcount = 0
    for _ in prep_gen(sched_blocks[0][0], sched_blocks[0][1], sched_blocks[0][2], 0):
        pass
    for k, (b, d, blk, first_of_dir, last_of_batch) in enumerate(sched_blocks):
        pp = k % 2
        bwd = (d == 1)
        if first_of_dir:
            S.op("pool", ("memset", dict(ap=STz[0][:], constant=0.0)), writes=[RST[0]])
            S.op("pool", ("memset", dict(ap=STz[1][:], constant=0.0)), writes=[RST[1]])
        stage1, stage2, stage3 = block_stages(b, d, blk, pp)
        chunks = list(range(BLK // C)) if not bwd else list(range(BLK // C - 1, -1, -1))
        cks = [(ci, slice(ci * C, (ci + 1) * C), (blk * BLK // C) + ci, (pcount + n_) % 2) for n_, ci in enumerate(chunks)]
        pcount += len(chunks)
        if k + 1 < len(sched_blocks) and sched_blocks[k + 1][0] == b:
            nb_, nd_, nblk_ = sched_blocks[k + 1][:3]
            pgen = prep_gen(nb_, nd_, nblk_, (k + 1) % 2)
        else:
            pgen = iter(())
        for _ in stage1(cks[0]):
            pass
        for idx, ck in enumerate(cks):
            fill = itertools.chain(stage3(cks[idx - 1]) if idx > 0 else iter(()), stage1(cks[idx + 1]) if idx + 1 < len(cks) else iter(()))
            for _ in stage2(ck):
                for _k in range(NFILL):
                    next(fill, None)
                for _k in range(NPREP):
                    next(pgen, None)
            for _ in fill:
                pass
        for _ in stage3(cks[-1]):
            pass
        for _ in pgen:
            pass
        if last_of_batch:
            finalize_batch(b)
            if k + 1 < len(sched_blocks):
                nb_, nd_, nblk_ = sched_blocks[k + 1][:3]
                for _ in prep_gen(nb_, nd_, nblk_, (k + 1) % 2):
                    pass
    return out_toks


NT = 2048
NTH = NT + 2


def emit_p1(S, nc, I, O):
    sb = lambda name, shape, dt=F32: nc.alloc_sbuf_tensor(name, shape, dt)
    ps = lambda name, shape, dt=F32: nc.alloc_psum_tensor(name, shape, dt)
    R = Region
    op = S.op
    mm = lambda out, l, r_, rd, wr, st=True, sp=True: op("pe", ("matmul", dict(out=out, lhsT=l, rhs=r_, start=st, stop=sp)), reads=rd, writes=wr)

    identf = sb("identf", [128, 128]); identb = sb("identb", [128, 128], BF16); Rid = R()
    gE = sb("gE", [128, 8, 1]); gO = sb("gO", [128, 8, 1]); Rg = R()
    S.dma(("dma_start", dict(out=identf[:], in_=I["c_ident"])), writes=[Rid])
    op("dve", ("tensor_copy", dict(out=identb[:], in_=identf[:])), reads=[Rid], writes=[Rid])
    S.dma(("dma_start", dict(out=gE[:, :, 0], in_=I["e_norm_g"].rearrange("(k p) -> p k", p=128), allow_slow_non_contiguous=True)), writes=[Rg])
    S.dma(("dma_start", dict(out=gO[:, :, 0], in_=I["o_norm_g"].rearrange("(k p) -> p k", p=128), allow_slow_non_contiguous=True)), writes=[Rg])
    hnT = sb("hnT", [128, 8, NTH], BF16); RhnT = [R() for _ in range(18)]
    yT = nc.dram_tensor("yT_d", [16, 128, NT], BF16).ap(); RyT = [[R() for _ in range(4)] for _ in range(16)]
    U = sb("U", [128, 4096]); RU = R()
    xt = [sb("xt%d" % i, [128, 1024]) for i in range(2)]; Rxt = [R(), R()]
    yo = [sb("yo%d" % i, [128, 512], BF16) for i in range(2)]; Ryo = [R(), R()]
    ytl = [sb("ytl%d" % i, [128, 16, 128], BF16) for i in range(2)]; Rytl = [R(), R()]
    xn = sb("xn", [128, 1024], BF16); Rxn = R()
    sq = sb("sq", [128, 1024]); Rsq = R()
    st = sb("st", [128, 8]); Rst = R()
    stg = [sb("stg%d" % i, [128, 8, 256]) for i in range(2)]; Rstg = [R(), R()]
    wbf = [sb("wbf%d" % i, [128, 8, 512], BF16) for i in range(2)]; Rwbf = [R(), R()]
    wbig = sb("wbig", [128, 16, 1024], BF16); Rwbig = R()
    t1 = sb("t1", [128, 512]); Rt1 = R()
    t1b = sb("t1b", [128, 512]); t1s = [t1, t1b]; Rt1s = [Rt1, R()]
    t2b = sb("t2b", [128, 512]); t3b = sb("t3b", [128, 512])
    t2 = sb("t2", [128, 512]); Rt2 = R()
    t3 = sb("t3", [128, 512]); Rt3 = R()
    cw = sb("cw", [128, 8, 3]); Rcw = R()
    PS_a = ps("PS_a", [128, 512]); RPa = R()
    PS_b = ps("PS_b", [128, 512]); RPb = R()
    PS_c = ps("PS_c", [128, 512]); RPc = R()
    PS_d = ps("PS_d", [128, 512]); RPd = R()
    PS_t = ps("PS_t", [128, 8, 128], BF16); RPt = R()
    PS_m = ps("PS_m", [128, 8, 128]); RPm = R()
    for j_ in range(3):
        S.dma(("dma_start", dict(out=cw[:, :, j_], in_=I["e_conv_w"][j_].rearrange("(cb p) -> p cb", p=128), allow_slow_non_contiguous=True)), writes=[Rcw])

    def norm_tile(xtile, Rx, gt, dst_fn, Rdst, nvalid=128):
        op("act", ("activation", dict(out=sq[:], in_=xtile[:], func=AF.Square)), reads=[Rx], writes=[Rsq])
        op("dve", ("reduce_sum", dict(out=st[:, 0:1], in_=sq[:], axis=AX.X)), reads=[Rsq], writes=[Rst])
        op("dve", ("tensor_scalar", dict(out=st[:, 1:2], in0=st[:, 0:1], scalar1=1.0 / 1024, scalar2=1e-6, op0=ALU.mult, op1=ALU.add)), reads=[Rst], writes=[Rst])
        op("act", ("activation", dict(out=st[:, 2:3], in_=st[:, 1:2], func=AF.Sqrt)), reads=[Rst], writes=[Rst])
        op("dve", ("reciprocal", dict(out=st[:, 3:4], in_=st[:, 2:3])), reads=[Rst], writes=[Rst])
        op("dve", ("tensor_scalar", dict(out=xn[:], in0=xtile[:], scalar1=st[:, 3:4], scalar2=None, op0=ALU.mult)), reads=[Rx, Rst], writes=[Rxn])
        for k in range(8):
            op("pe", ("transpose", dict(out=PS_t[:, k, :], in_=xn[:, k * 128:(k + 1) * 128], identity=identb[:])), reads=[Rxn, Rid], writes=[RPt])
        dst_fn(gt)

    xh = I["xh"]
    for i in range(17):
        xb, Rx = xt[i % 2], Rxt[i % 2]
        if i < 16:
            S.dma(("dma_start", dict(out=xb[:], in_=xh[1 + 128 * i: 1 + 128 * (i + 1), :])), writes=[Rx])
            def dst(gt, i=i):
                op("dve", ("tensor_tensor", dict(out=hnT[:, :, 1 + 128 * i: 1 + 128 * (i + 1)], in0=PS_t[:], in1=gt[:].to_broadcast([128, 8, 128]), op=ALU.mult)), reads=[RPt, Rg], writes=[RhnT[i]])
        else:
            op("pool", ("memset", dict(ap=xb[:], constant=0.0)), writes=[Rx])
            S.dma(("dma_start", dict(out=xb[0:1, :], in_=xh[0:1, :])), writes=[Rx])
            S.dma(("dma_start", dict(out=xb[1:2, :], in_=xh[NT + 1:NT + 2, :])), writes=[Rx])
            def dst(gt):
                op("dve", ("tensor_tensor", dict(out=hnT[:, :, 0:1], in0=PS_t[:, :, 0:1], in1=gt[:], op=ALU.mult)), reads=[RPt, Rg], writes=[RhnT[16]])
                op("dve", ("tensor_tensor", dict(out=hnT[:, :, NT + 1:NT + 2], in0=PS_t[:, :, 1:2], in1=gt[:], op=ALU.mult)), reads=[RPt, Rg], writes=[RhnT[17]])
        norm_tile(xb, Rx, gE, dst, None)
    allh = RhnT

    wi = I["e_w_in"].rearrange("(k p) (s c) -> p k s c", p=128, c=1024)

    def load_w(buf, src4, nsp):
        for s_ in range(nsp):
            sb_ = s_ % 2
            S.dma(("dma_start", dict(out=stg[sb_][:, :, 0:128], in_=src4[:, :, s_, :])), writes=[Rstg[sb_]])
            op("pool", ("tensor_copy", dict(out=wbf[buf][:, :, s_ * 128:(s_ + 1) * 128], in_=stg[sb_][:, :, 0:128])), reads=[Rstg[sb_]], writes=[Rwbf[buf]])
        return wbf[buf][:, :, 0:nsp * 128].rearrange("p k (s c) -> p k s c", c=128)

    PSc0, RPc0, PSd0, RPd0 = PS_c, RPc, PS_d, RPd
    t2s = [t2, t2b]; Rt2s = [Rt2, R()]
    t3s = [t3, t3b]; Rt3s = [Rt3, R()]
    xc = U[:, 0:NTH]
    chunksA = [(0, 512), (512, 512), (1024, 512), (1536, 512), (2048, 2)]
    for cb in range(8):
        w4 = load_w(cb % 2, wi[:, :, 0:4, cb * 128:(cb + 1) * 128], 4)
        Rw = Rwbf[cb % 2]
        for ci_, (c0, n) in enumerate(chunksA):
            (PA, RA_), (PB, RB_) = (((PS_a, RPa), (PS_b, RPb)) if ci_ % 2 == 0 else ((PS_c, RPc), (PS_d, RPd)))
            for k in range(8):
                mm(PA[:, 0:n], w4[:, k, 0, :], hnT[:, k, c0:c0 + n], [Rw] + allh, [RA_], st=(k == 0), sp=(k == 7))
            for k in range(8):
                mm(PB[:, 0:n], w4[:, k, 2, :], hnT[:, k, c0:c0 + n], [Rw] + allh, [RB_], st=(k == 0), sp=(k == 7))
            t1_, Rt1_ = t1s[ci_ % 2], Rt1s[ci_ % 2]
            op("act", ("activation", dict(out=t1_[:, 0:n], in_=PA[:, 0:n], func=AF.Copy)), reads=[RA_], writes=[Rt1_])
            op("dve", ("tensor_tensor", dict(out=xc[:, c0:c0 + n], in0=PB[:, 0:n], in1=t1_[:, 0:n], op=ALU.mult)), reads=[RB_, Rt1_], writes=[RU])
        for j in range(4):
            c0 = 1 + 512 * j
            (PS_c, RPc), (PS_d, RPd) = ((PSc0, RPc0), (PSd0, RPd0)) if j % 2 == 1 else ((PS_a, RPa), (PS_b, RPb))
            t2, Rt2 = t2s[j % 2], Rt2s[j % 2]
            t3, Rt3 = t3s[j % 2], Rt3s[j % 2]
            for k in range(8):
                mm(PS_c[:], w4[:, k, 1, :], hnT[:, k, c0:c0 + 512], [Rw] + allh, [RPc], st=(k == 0), sp=(k == 7))
            for k in range(8):
                mm(PS_d[:], w4[:, k, 3, :], hnT[:, k, c0:c0 + 512], [Rw] + allh, [RPd], st=(k == 0), sp=(k == 7))
            op("dve", ("tensor_scalar", dict(out=t2[:], in0=xc[:, c0 - 1:c0 + 511], scalar1=cw[:, cb, 0:1], scalar2=None, op0=ALU.mult)), reads=[RU, Rcw], writes=[Rt2])
            op("dve", ("scalar_tensor_tensor", dict(out=t2[:], in0=xc[:, c0:c0 + 512], scalar=cw[:, cb, 1:2], in1=t2[:], op0=ALU.mult, op1=ALU.add)), reads=[RU, Rcw, Rt2], writes=[Rt2])
            op("dve", ("scalar_tensor_tensor", dict(out=t2[:], in0=xc[:, c0 + 1:c0 + 513], scalar=cw[:, cb, 2:3], in1=t2[:], op0=ALU.mult, op1=ALU.add)), reads=[RU, Rcw, Rt2], writes=[Rt2])
            op("act", ("activation", dict(out=t3[:], in_=PS_d[:], func=AF.Silu)), reads=[RPd], writes=[Rt3])
            op("dve", ("tensor_tensor", dict(out=t2[:], in0=PS_c[:], in1=t2[:], op=ALU.mult)), reads=[RPc, Rt2], writes=[Rt2])
            op("pool", ("tensor_tensor", dict(out=yo[j % 2][:], in0=t2[:], in1=t3[:], op=ALU.mult)), reads=[Rt2, Rt3], writes=[Ryo[j % 2]])
            S.dma(("dma_start", dict(out=yT[cb, :, 512 * j:512 * (j + 1)], in_=yo[j % 2][:])), reads=[Ryo[j % 2]], writes=[RyT[cb][j]], q="pool")

    PS_c, RPc, PS_d, RPd = PSc0, RPc0, PSd0, RPd0
    t2, Rt2, t3, Rt3 = t2s[0], Rt2s[0], t3s[0], Rt3s[0]
    for hf in range(4):
        S.dma(("dma_start", dict(out=stg[hf % 2][:], in_=wi[:, :, 5, hf * 256:(hf + 1) * 256])), writes=[Rstg[hf % 2]])
        op("pool", ("tensor_copy", dict(out=wbig[:, 0:8, hf * 256:(hf + 1) * 256], in_=stg[hf % 2][:])), reads=[Rstg[hf % 2]], writes=[Rwbig])
    wsn = sb("wsn", [128, 8, 128]); wsnb = sb("wsnb", [128, 8, 128], BF16); wsT = sb("wsT", [128, 8, 128], BF16); Rws = R()
    S.dma(("dma_start", dict(out=wsn[:], in_=I["e_sgu_w"].rearrange("g i j -> i g j"))), writes=[Rws])
    op("dve", ("tensor_copy", dict(out=wsnb[:], in_=wsn[:])), reads=[Rws], writes=[Rws])
    for g in range(8):
        op("pe", ("transpose", dict(out=PS_t[:, g, :], in_=wsnb[:, g, :], identity=identb[:])), reads=[Rws, Rid], writes=[RPt])
    op("act", ("activation", dict(out=wsT[:], in_=PS_t[:], func=AF.Copy)), reads=[RPt], writes=[Rws])
    bsB = sb("bsB", [128, 8, 128]); lnG = sb("lnG", [128, 1024]); lnB = sb("lnB", [128, 1024]); Rbc = R()
    S.dma(("dma_start", dict(out=bsB[:].rearrange("p g i -> p (g i)"), in_=I["e_sgu_b"].rearrange("g i -> (g i)").partition_broadcast(128))), writes=[Rbc])
    S.dma(("dma_start", dict(out=lnG[:], in_=I["e_sgu_ln_g"].partition_broadcast(128))), writes=[Rbc])
    S.dma(("dma_start", dict(out=lnB[:], in_=I["e_sgu_ln_b"].partition_broadcast(128))), writes=[Rbc])
    vsb = sb("vsb", [128, 1024]); Rvsb = R()
    vnb = sb("vnb", [128, 1024], BF16); Rvnb = R()
    mixall = U[:, 0:4096].rearrange("p (g t) -> p g t", g=8)
    for tg in range(4):
        for ti in range(4):
            c0 = 1 + 128 * (4 * tg + ti)
            for hf, (P_, RP_) in enumerate((((PS_a, RPa), (PS_b, RPb)) if ti % 2 == 0 else ((PS_c, RPc), (PS_d, RPd)))):
                for k in range(8):
                    mm(P_[:], hnT[:, k, c0:c0 + 128], wbig[:, k, hf * 512:(hf + 1) * 512], [Rwbig] + allh, [RP_], st=(k == 0), sp=(k == 7))
                op("act", ("activation", dict(out=vsb[:, hf * 512:(hf + 1) * 512], in_=P_[:], func=AF.Copy)), reads=[RP_], writes=[Rvsb])
            op("act", ("activation", dict(out=sq[:], in_=vsb[:], func=AF.Square)), reads=[Rvsb], writes=[Rsq])
            op("dve", ("reduce_sum", dict(out=st[:, 0:1], in_=vsb[:], axis=AX.X)), reads=[Rvsb], writes=[Rst])
            op("dve", ("reduce_sum", dict(out=st[:, 1:2], in_=sq[:], axis=AX.X)), reads=[Rsq], writes=[Rst])
            op("dve", ("tensor_scalar", dict(out=st[:, 2:3], in0=st[:, 0:1], scalar1=1.0 / 1024, scalar2=None, op0=ALU.mult)), reads=[Rst], writes=[Rst])
            op("dve", ("tensor_tensor", dict(out=st[:, 3:4], in0=st[:, 2:3], in1=st[:, 2:3], op=ALU.mult)), reads=[Rst], writes=[Rst])
            op("dve", ("scalar_tensor_tensor", dict(out=st[:, 4:5], in0=st[:, 1:2], scalar=1.0 / 1024, in1=st[:, 3:4], op0=ALU.mult, op1=ALU.subtract)), reads=[Rst], writes=[Rst])
            op("dve", ("tensor_scalar", dict(out=st[:, 4:5], in0=st[:, 4:5], scalar1=1e-5, scalar2=None, op0=ALU.add)), reads=[Rst], writes=[Rst])
            op("act", ("activation", dict(out=st[:, 5:6], in_=st[:, 4:5], func=AF.Sqrt)), reads=[Rst], writes=[Rst])
            op("dve", ("reciprocal", dict(out=st[:, 6:7], in_=st[:, 5:6])), reads=[Rst], writes=[Rst])
            op("dve", ("tensor_scalar", dict(out=vsb[:], in0=vsb[:], scalar1=st[:, 2:3], scalar2=st[:, 6:7], op0=ALU.subtract, op1=ALU.mult)), reads=[Rvsb, Rst], writes=[Rvsb])
            op("dve", ("tensor_tensor", dict(out=vsb[:], in0=vsb[:], in1=lnG[:], op=ALU.mult)), reads=[Rvsb, Rbc], writes=[Rvsb])
            op("pool", ("tensor_tensor", dict(out=vnb[:], in0=vsb[:], in1=lnB[:], op=ALU.add)), reads=[Rvsb, Rbc], writes=[Rvnb])
            for g in range(8):
                mm(PS_m[:, g, :], vnb[:, g * 128:(g + 1) * 128], wsT[:, g, :], [Rvnb, Rws], [RPm])
            op("dve", ("tensor_tensor", dict(out=mixall[:, :, ti * 128:(ti + 1) * 128], in0=PS_m[:], in1=bsB[:], op=ALU.add)), reads=[RPm, Rbc], writes=[RU])
        c0 = 1 + 512 * tg
        for g in range(8):
            buf = g % 2
            S.dma(("dma_start", dict(out=stg[buf][:, :, 0:128], in_=wi[:, :, 4, g * 128:(g + 1) * 128])), writes=[Rstg[buf]])
            S.dma(("dma_start", dict(out=stg[buf][:, :, 128:256], in_=wi[:, :, 6, g * 128:(g + 1) * 128])), writes=[Rstg[buf]])
            op("pool", ("tensor_copy", dict(out=wbf[buf][:, :, 0:256], in_=stg[buf][:, :, 0:256])), reads=[Rstg[buf]], writes=[Rwbf[buf]])
            (PU, RPU), (PZ, RPZ) = ((PS_c, RPc), (PS_d, RPd)) if g % 2 == 0 else ((PS_a, RPa), (PS_b, RPb))
            t2, Rt2 = t2s[g % 2], Rt2s[g % 2]
            t3, Rt3 = t3s[g % 2], Rt3s[g % 2]
            for k in range(8):
                mm(PU[:], wbf[buf][:, k, 0:128], hnT[:, k, c0:c0 + 512], [Rwbf[buf]] + allh, [RPU], st=(k == 0), sp=(k == 7))
            for k in range(8):
                mm(PZ[:], wbf[buf][:, k, 128:256], hnT[:, k, c0:c0 + 512], [Rwbf[buf]] + allh, [RPZ], st=(k == 0), sp=(k == 7))
            op("act", ("activation", dict(out=t3[:], in_=PZ[:], func=AF.Silu)), reads=[RPZ], writes=[Rt3])
            op("dve", ("tensor_tensor", dict(out=t2[:], in0=PU[:], in1=mixall[:, g, :], op=ALU.mult)), reads=[RPU, RU], writes=[Rt2])
            op("pool", ("tensor_tensor", dict(out=yo[g % 2][:], in0=t2[:], in1=t3[:], op=ALU.mult)), reads=[Rt2, Rt3], writes=[Ryo[g % 2]])
            S.dma(("dma_start", dict(out=yT[8 + g, :, 512 * tg:512 * (tg + 1)], in_=yo[g % 2][:])), reads=[Ryo[g % 2]], writes=[RyT[8 + g][tg]], q="pool")

    wo = I["e_w_out"].rearrange("(k p) n -> p k n", p=128)
    for q in range(2):
        for hf in range(4):
            S.dma(("dma_start", dict(out=stg[hf % 2][:], in_=wo[:, 8 * q:8 * q + 8, hf * 256:(hf + 1) * 256])), writes=[Rstg[hf % 2]])
            op("pool", ("tensor_copy", dict(out=wbig[:, 8 * q:8 * q + 8, hf * 256:(hf + 1) * 256], in_=stg[hf % 2][:])), reads=[Rstg[hf % 2]], writes=[Rwbig])
    ally = [r for row in RyT for r in row]
    h1ts = [sb("h1t%d" % i, [128, 1024]) for i in range(2)]; Rh1s = [R(), R()]
    for i in range(16):
        xb, Rx = xt[i % 2], Rxt[i % 2]
        h1t, Rh1 = h1ts[i % 2], Rh1s[i % 2]
        S.dma(("dma_start", dict(out=xb[:], in_=xh[1 + 128 * i: 1 + 128 * (i + 1), :])), writes=[Rx])
        S.dma(("dma_start", dict(out=ytl[i % 2][:], in_=yT[:, :, 128 * i:128 * (i + 1)].rearrange("k p t -> p k t"))), reads=ally, writes=[Rytl[i % 2]])
        for hf, (P_, RP_) in enumerate((((PS_a, RPa), (PS_b, RPb)) if i % 2 == 0 else ((PS_c, RPc), (PS_d, RPd)))):
            for k in range(16):
                mm(P_[:], ytl[i % 2][:, k, :], wbig[:, k, hf * 512:(hf + 1) * 512], [Rwbig, Rytl[i % 2]], [RP_], st=(k == 0), sp=(k == 15))
            op("dve", ("tensor_tensor", dict(out=h1t[:, hf * 512:(hf + 1) * 512], in0=P_[:], in1=xb[:, hf * 512:(hf + 1) * 512], op=ALU.add)), reads=[RP_, Rx], writes=[Rh1])
        S.dma(("dma_start", dict(out=O["h1"][128 * i:128 * (i + 1), :], in_=h1t[:])), reads=[Rh1], q="act")

        def dst(gt, i=i):
            op("dve", ("tensor_tensor", dict(out=hnT[:, :, 1 + 128 * i: 1 + 128 * (i + 1)], in0=PS_t[:], in1=gt[:].to_broadcast([128, 8, 128]), op=ALU.mult)), reads=[RPt, Rg], writes=[RhnT[i]])
        norm_tile(h1t, Rh1, gO, dst, None)

    wi1 = I["o_w_in"].rearrange("(k p) n -> p k n", p=128)
    blocks = [(c * 128, c * 128, False) for c in range(25)]
    blocks += [(3200 + c * 128, 3200 + c * 128, True) for c in range(8)]
    blocks += [(4736 + c * 128, 4224 + c * 128, True) for c in range(4)]
    ob = [sb("ob%d" % i, [128, 512]) for i in range(2)]; Rob = [R(), R()]
    oi = 0
    toks = []
    for bi, (sc, dr, act) in enumerate(blocks):
        buf = bi % 2
        S.dma(("dma_start", dict(out=stg[buf][:, :, 0:128], in_=wi1[:, :, sc:sc + 128])), writes=[Rstg[buf]])
        op("pool", ("tensor_copy", dict(out=wbf[buf][:, :, 0:128], in_=stg[buf][:, :, 0:128])), reads=[Rstg[buf]], writes=[Rwbf[buf]])
        for j in range(4):
            P_, RP_ = ((PS_a, RPa), (PS_b, RPb), (PS_c, RPc), (PS_d, RPd))[j]
            c0 = 1 + 512 * j
            for k in range(8):
                mm(P_[:], wbf[buf][:, k, 0:128], hnT[:, k, c0:c0 + 512], [Rwbf[buf]] + allh, [RP_], st=(k == 0), sp=(k == 7))
            o_, Ro = ob[oi % 2], Rob[oi % 2]
            oi += 1
            op("act", ("activation", dict(out=o_[:], in_=P_[:], func=(AF.Silu if act else AF.Copy))), reads=[RP_], writes=[Ro])
            toks.append(S.dma(("dma_start", dict(out=O["pT"][dr:dr + 128, 512 * j:512 * (j + 1)], in_=o_[:])), reads=[Ro], q="act"))
    for hf in range(2):
        S.dma(("dma_start", dict(out=stg[hf][:], in_=wi1[:, :, 4224 + 256 * hf:4224 + 256 * (hf + 1)])), writes=[Rstg[hf]])
        op("pool", ("tensor_copy", dict(out=wbf[0][:, :, 256 * hf:256 * (hf + 1)], in_=stg[hf][:])), reads=[Rstg[hf]], writes=[Rwbf[0]])
    for i in range(16):
        c0 = 1 + 128 * i
        P_, RP_ = ((PS_a, RPa), (PS_b, RPb))[i % 2]
        for k in range(8):
            mm(P_[:], hnT[:, k, c0:c0 + 128], wbf[0][:, k, :], [Rwbf[0]] + allh, [RP_], st=(k == 0), sp=(k == 7))
        o_, Ro = ob[oi % 2], Rob[oi % 2]
        oi += 1
        op("act", ("activation", dict(out=o_[:], in_=P_[:], func=AF.Copy)), reads=[RP_], writes=[Ro])
        toks.append(S.dma(("dma_start", dict(out=O["fd"][128 * i:128 * (i + 1), :], in_=o_[:])), reads=[Ro], q="act"))
    return toks


def full_barrier(S):
    keys = list(S.cnt.items())
    for e in S.ENGS:
        waits = []
        for k, v in keys:
            if k == e:
                continue
            if S.seen[e].get(k, 0) < v:
                S.seen[e][k] = v
                waits.append((k, v))
        if waits:
            S.prog[e].append([waits, None, ("_none", 0)])


def emit_fnet(S, nc, I, ydT):
    R = Region
    op = S.op
    mm = lambda out, l, r_, rd, wr, st=True, sp=True: op("pe", ("matmul", dict(out=out, lhsT=l, rhs=r_, start=st, stop=sp)), reads=rd, writes=wr)
    toks = []
    with ExitStack() as es:
        sb = lambda name, shape, dt=F32: es.enter_context(nc.sbuf_tensor(name, shape, dt))
        ps = lambda name, shape, dt=F32: es.enter_context(nc.psum_tensor(name, shape, dt))
        xs = sb("f_xs", [128, 4096]); Rxs = R()
        xb = sb("f_xb", [128, 64, 128], BF16); Rxb = R()
        Fb = sb("f_F", [128, 256], BF16); RF = R()
        A_sb = sb("f_A", [64, 128, 256], BF16); RA = R()
        PQ = sb("f_PQ", [128, 2, 64, 128], BF16); RPQ = R()
        Tg = [[sb("f_T%d%d" % (i, j), [64, 16, 128], BF16) for j in range(2)] for i in range(2)]; RTg = [R(), R()]
        wf32 = sb("f_w32", [128, 128]); wfb = sb("f_wb", [128, 128], BF16); Rwf = R()
        Ccb = sb("f_Cc", [128, 128], BF16); mScb = sb("f_mSc", [128, 128], BF16); Rcs = R()
        Gb = sb("f_G", [128, 256], BF16); RG = R()
        ob = [sb("f_ob%d" % i, [128, 512]) for i in range(2)]; Rob = [R(), R()]
        PS = [ps("f_ps%d" % i, [128, 512]) for i in range(2)]; RPS = [R(), R()]
        S.dma(("dma_start", dict(out=Fb[:], in_=I["c_F"])), writes=[RF])
        S.dma(("dma_start", dict(out=Ccb[:], in_=I["c_Cc"])), writes=[Rcs])
        S.dma(("dma_start", dict(out=mScb[:], in_=I["c_mSc"])), writes=[Rcs])
        S.dma(("dma_start", dict(out=wf32[:], in_=I["fw"])), writes=[Rwf])
        op("dve", ("tensor_copy", dict(out=wfb[:], in_=wf32[:])), reads=[Rwf], writes=[Rwf])
        xbf = xb[:].rearrange("p l c -> p (l c)")
        for hf in range(2):
            S.dma(("dma_start", dict(out=xs[:], in_=I["fx"][:, hf * 4096:(hf + 1) * 4096])), writes=[Rxs])
            op("pool", ("tensor_copy", dict(out=xbf[:, hf * 4096:(hf + 1) * 4096], in_=xs[:])), reads=[Rxs], writes=[Rxb])
        for c2 in range(64):
            P_, RP_ = PS[c2 % 2], RPS[c2 % 2]
            for j in range(2):
                mm(P_[0:64, j * 256:(j + 1) * 256], xb[:, :, 2 * c2 + j], Fb[:], [Rxb, RF], [RP_])
            op("act" if c2 % 2 == 0 else "dve", ("activation", dict(out=A_sb[0:64, 2 * c2:2 * c2 + 2, :], in_=P_[0:64, :].rearrange("p (j k) -> p j k", j=2), func=AF.Copy)) if c2 % 2 == 0 else
               ("tensor_copy", dict(out=A_sb[0:64, 2 * c2:2 * c2 + 2, :], in_=P_[0:64, :].rearrange("p (j k) -> p j k", j=2))), reads=[RP_], writes=[RA])
        T1d = I["c_T1"].rearrange("p (k h) -> p k h", h=128)
        T2d = I["c_T2"].rearrange("p (k h) -> p k h", h=128)
        ei = 0
        for grp in range(8):
            tb = grp % 2
            S.dma(("dma_start", dict(out=Tg[tb][0][:], in_=T1d[:, grp * 16:(grp + 1) * 16, :])), writes=[RTg[tb]])
            S.dma(("dma_start", dict(out=Tg[tb][1][:], in_=T2d[:, grp * 16:(grp + 1) * 16, :])), writes=[RTg[tb]])
            for q in range(4):
                P_, RP_ = PS[ei % 2], RPS[ei % 2]
                for j in range(4):
                    kk_ = q * 4 + j
                    kl = grp * 16 + kk_
                    mm(P_[:, j * 128:(j + 1) * 128], A_sb[0:64, :, kl], Tg[tb][0][0:64, kk_, :], [RA, RTg[tb]], [RP_], st=True, sp=False)
                    mm(P_[:, j * 128:(j + 1) * 128], A_sb[0:64, :, 128 + kl], Tg[tb][1][0:64, kk_, :], [RA, RTg[tb]], [RP_], st=False, sp=True)
                kl0 = grp * 16 + q * 4
                for qq in range(2):
                    op("act" if qq == 0 else "dve",
                       ("activation", dict(out=PQ[:, qq, :, kl0:kl0 + 4].rearrange("p h l -> p l h"), in_=P_[:].rearrange("p (l q h) -> p l q h", l=4, q=2)[:, :, qq, :], func=AF.Copy)) if qq == 0 else
                       ("tensor_copy", dict(out=PQ[:, qq, :, kl0:kl0 + 4].rearrange("p h l -> p l h"), in_=P_[:].rearrange("p (l q h) -> p l q h", l=4, q=2)[:, :, qq, :])),
                       reads=[RP_], writes=[RPQ])
                ei += 1
        P_, RP_ = PS[0], RPS[0]
        mm(P_[:, 0:128], Ccb[:], wfb[:], [Rcs, Rwf], [RP_])
        mm(P_[:, 128:256], mScb[:], wfb[:], [Rcs, Rwf], [RP_])
        op("act", ("activation", dict(out=Gb[:], in_=P_[:, 0:256], func=AF.Copy)), reads=[RP_], writes=[RG])
        for t4 in range(16):
            P_, RP_ = PS[(t4 + 1) % 2], RPS[(t4 + 1) % 2]
            for j in range(4):
                kh = 4 * t4 + j
                mm(P_[:, j * 128:(j + 1) * 128], Gb[:, 0:128], PQ[:, 0, kh, :], [RG, RPQ], [RP_], st=True, sp=False)
                mm(P_[:, j * 128:(j + 1) * 128], Gb[:, 128:256], PQ[:, 1, kh, :], [RG, RPQ], [RP_], st=False, sp=True)
            o_, Ro = ob[t4 % 2], Rob[t4 % 2]
            op("act", ("activation", dict(out=o_[:], in_=P_[:], func=AF.Copy)), reads=[RP_], writes=[Ro])
            toks.append(S.dma(("dma_start", dict(out=ydT[:, 512 * t4:512 * (t4 + 1)], in_=o_[:])), reads=[Ro]))
    full_barrier(S)
    return toks


def emit_p3(S, nc, I, yout):
    sb = lambda name, shape, dt=F32: nc.alloc_sbuf_tensor(name, shape, dt)
    ps = lambda name, shape, dt=F32: nc.alloc_psum_tensor(name, shape, dt)
    R = Region
    op = S.op
    mm = lambda out, l, r_, rd, wr, st=True, sp=True: op("pe", ("matmul", dict(out=out, lhsT=l, rhs=r_, start=st, stop=sp)), reads=rd, writes=wr)
    stg = [sb("stg%d" % i, [128, 8, 256]) for i in range(2)]; Rstg = [R(), R()]
    wO = sb("wO", [128, 12, 1024], BF16); RwO = R()
    gN = sb("gN", [128, 1024]); RgN = R()
    gt_all = sb("gt_all", [128, 12, 2048], BF16); Rgt = R()
    ya = [sb("ya%d" % i, [128, 512]) for i in range(2)]; Rya = [R(), R()]
    ga = [sb("ga%d" % i, [128, 512]) for i in range(2)]; Rga = [R(), R()]
    h1t = [sb("h1t%d" % i, [128, 1024]) for i in range(2)]; Rh1 = [R(), R()]
    h2 = sb("h2", [128, 1024]); Rh2 = R()
    sq = sb("sq", [128, 1024]); Rsq = R()
    st = sb("st", [128, 8]); Rst = R()
    yo = [sb("yo%d" % i, [128, 1024]) for i in range(2)]; Ryo = [R(), R()]
    PS_a = ps("PS_a", [128, 512]); RPa = R()
    PS_b = ps("PS_b", [128, 512]); RPb = R()
    wo3 = I["o_w_out"].rearrange("(k p) n -> p k n", p=128)
    si = 0
    for (k0, nk) in ((0, 8), (8, 4)):
        for cq in range(4):
            b_ = si % 2; si += 1
            S.dma(("dma_start", dict(out=stg[b_][:, 0:nk, :], in_=wo3[:, k0:k0 + nk, cq * 256:(cq + 1) * 256])), writes=[Rstg[b_]])
            op("pool", ("tensor_copy", dict(out=wO[:, k0:k0 + nk, cq * 256:(cq + 1) * 256], in_=stg[b_][:, 0:nk, :])), reads=[Rstg[b_]], writes=[RwO])
    S.dma(("dma_start", dict(out=gN[:], in_=I["final_norm_g"].partition_broadcast(128))), writes=[RgN])
    ii = 0
    for blk in range(12):
        src = I["ycT"][blk * 128:(blk + 1) * 128] if blk < 8 else I["ydT"][(blk - 8) * 128:(blk - 7) * 128]
        gsrc = I["gT"][blk * 128:(blk + 1) * 128]
        for j in range(4):
            b_ = ii % 2; ii += 1
            S.dma(("dma_start", dict(out=ya[b_][:], in_=src[:, 512 * j:512 * (j + 1)])), writes=[Rya[b_]])
            S.dma(("dma_start", dict(out=ga[b_][:], in_=gsrc[:, 512 * j:512 * (j + 1)])), writes=[Rga[b_]])
            op("dve" if ii % 2 else "pool", ("tensor_tensor", dict(out=gt_all[:, blk, 512 * j:512 * (j + 1)], in0=ya[b_][:], in1=ga[b_][:], op=ALU.mult)), reads=[Rya[b_], Rga[b_]], writes=[Rgt])
    toks = []
    for i in range(16):
        hb, Rh = h1t[i % 2], Rh1[i % 2]
        S.dma(("dma_start", dict(out=hb[:], in_=I["h1"][128 * i:128 * (i + 1), :])), writes=[Rh])
        for hf, (P_, RP_) in enumerate(((PS_a, RPa), (PS_b, RPb))):
            for k in range(12):
                mm(P_[:], gt_all[:, k, 128 * i:128 * (i + 1)], wO[:, k, hf * 512:(hf + 1) * 512], [Rgt, RwO], [RP_], st=(k == 0), sp=(k == 11))
            op("dve", ("tensor_tensor", dict(out=h2[:, hf * 512:(hf + 1) * 512], in0=P_[:], in1=hb[:, hf * 512:(hf + 1) * 512], op=ALU.add)), reads=[RP_, Rh], writes=[Rh2])
        op("act", ("activation", dict(out=sq[:], in_=h2[:], func=AF.Square)), reads=[Rh2], writes=[Rsq])
        op("dve", ("reduce_sum", dict(out=st[:, 0:1], in_=sq[:], axis=AX.X)), reads=[Rsq], writes=[Rst])
        op("dve", ("tensor_scalar", dict(out=st[:, 1:2], in0=st[:, 0:1], scalar1=1.0 / 1024, scalar2=1e-6, op0=ALU.mult, op1=ALU.add)), reads=[Rst], writes=[Rst])
        op("act", ("activation", dict(out=st[:, 2:3], in_=st[:, 1:2], func=AF.Sqrt)), reads=[Rst], writes=[Rst])
        op("dve", ("reciprocal", dict(out=st[:, 3:4], in_=st[:, 2:3])), reads=[Rst], writes=[Rst])
        op("dve", ("tensor_scalar", dict(out=h2[:], in0=h2[:], scalar1=st[:, 3:4], scalar2=None, op0=ALU.mult)), reads=[Rh2, Rst], writes=[Rh2])
        o_, Ro = yo[i % 2], Ryo[i % 2]
        op("pool", ("tensor_tensor", dict(out=o_[:], in0=h2[:], in1=gN[:], op=ALU.mult)), reads=[Rh2, RgN], writes=[Ro])
        toks.append(S.dma(("dma_start", dict(out=yout[128 * i:128 * (i + 1), :], in_=o_[:])), reads=[Ro], q="pool"))
    return toks


def _mk(nc, name, shape, dt=None, out=False):
    return nc.dram_tensor(name, list(shape), dt or F32, kind=("ExternalOutput" if out else "ExternalInput")).ap()


W1 = ["e_norm_g", "e_w_in", "e_conv_w", "e_sgu_ln_g", "e_sgu_ln_b", "e_sgu_w", "e_sgu_b", "e_w_out", "o_norm_g", "o_w_in"]


def build_l1(shapes):
    nc = bass.Bass("TRN2", target_bir_lowering=False)
    I = {"xh": _mk(nc, "xh", [2050, 1024]), "c_ident": _mk(nc, "c_ident", [128, 128])}
    for n in W1:
        I[n] = _mk(nc, n, shapes[n])
    O = {"h1": _mk(nc, "h1", [2048, 1024], out=True), "pT": _mk(nc, "pT", [4736, 2048], out=True),
         "fd": _mk(nc, "fd", [2048, 512], out=True)}
    S = Sched(nc)
    toks = emit_p1(S, nc, I, O)
    S.barrier_on("sp", toks)
    S.finalize()
    return nc


def build_l2(consts):
    NB, T = 2, 8192
    nc = bass.Bass("TRN2", target_bir_lowering=False)
    pr, pk, pv, pwa = (_mk(nc, n, [128, NB, T + 2]) for n in ("pr", "pk", "pv", "pwa"))
    prm = _mk(nc, "prm", [128, 17]); w2a2 = _mk(nc, "w2a2", [128, 2, 128])
    A = {k: _mk(nc, k, v.shape) for k, v in consts.items()}
    FI = {"fx": _mk(nc, "fx", [128, 8192]), "fw": _mk(nc, "fw", [128, 128]),
          "c_F": _mk(nc, "c_F", [128, 256], BF16), "c_T1": _mk(nc, "c_T1", [64, 16384], BF16),
          "c_T2": _mk(nc, "c_T2", [64, 16384], BF16), "c_Cc": _mk(nc, "c_Cc", [128, 128], BF16),
          "c_mSc": _mk(nc, "c_mSc", [128, 128], BF16)}
    yout = _mk(nc, "yout", [128, NB, T], out=True)
    ydT = _mk(nc, "ydT", [128, T], out=True)
    S = Sched(nc)
    toks = emit_fnet(S, nc, FI, ydT)
    toks += emit_rwkv(S, nc, A, pr, pk, pv, pwa, prm, w2a2, yout, NB, T)
    S.barrier_on("sp", toks)
    S.finalize()
    return nc


def build_l3():
    nc = bass.Bass("TRN2", target_bir_lowering=False)
    I = {"ycT": _mk(nc, "ycT", [1024, 2048]), "ydT": _mk(nc, "ydT", [512, 2048]), "gT": _mk(nc, "gT", [1536, 2048]),
         "h1": _mk(nc, "h1", [2048, 1024]), "o_w_out": _mk(nc, "o_w_out", [1536, 1024]),
         "final_norm_g": _mk(nc, "final_norm_g", [1024])}
    y = _mk(nc, "y", [2048, 1024], out=True)
    S = Sched(nc)
    toks = emit_p3(S, nc, I, y)
    S.barrier_on("sp", toks)
    S.finalize()
    return nc


def fnet_tables():
    import ml_dtypes
    N = 8192
    nh = np.arange(128); kl = np.arange(128)
    ang = 2 * np.pi * np.outer(nh, kl) / 128
    F = np.concatenate([np.cos(ang), np.sin(ang)], axis=1)
    nl = np.arange(64)[:, None, None]; klo = np.arange(128)[None, :, None]; kh = np.arange(64)[None, None, :]
    beta = 2 * np.pi * ((nl * (klo + 128 * kh)) % N) / N
    T1 = np.concatenate([np.cos(beta), np.sin(beta)], axis=2).reshape(64, 16384)
    T2 = np.concatenate([-np.sin(beta), np.cos(beta)], axis=2).reshape(64, 16384)
    c = np.arange(128); phi = 2 * np.pi * np.outer(c, c) / 128
    nrm = 1 / np.sqrt(N * 128)
    bf = lambda a: np.ascontiguousarray(a.astype(np.float32)).astype(ml_dtypes.bfloat16)
    return {"c_F": bf(F), "c_T1": bf(T1), "c_T2": bf(T2), "c_Cc": bf(np.cos(phi) * nrm), "c_mSc": bf(-np.sin(phi) * nrm)}


def kernel(**inputs):
    f32 = lambda a: np.ascontiguousarray(np.asarray(a), dtype=np.float32)
    inp = {k: f32(v) for k, v in inputs.items()}
    x = inp["x"]
    ncores = 8
    cores = list(range(ncores))
    w1 = {n: np.ascontiguousarray(inp[n][0]) for n in W1}
    ident = np.eye(128, dtype=np.float32)
    maps = []
    for c in cores:
        b, s0 = c // 4, (c % 4) * 2048
        xh = np.zeros((2050, 1024), np.float32)
        xh[1:2049] = x[b, s0:s0 + 2048]
        if s0 > 0:
            xh[0] = x[b, s0 - 1]
        if s0 + 2048 < 8192:
            xh[2049] = x[b, s0 + 2048]
        m = {"xh": xh, "c_ident": ident}
        m.update(w1)
        maps.append(m)
    nc1 = build_l1({n: w1[n].shape for n in W1})
    r1 = run_bass_kernel_spmd(nc1, maps, core_ids=cores).results
    PT = np.concatenate([np.asarray(r["pT"]) for r in r1], axis=1)
    FD = np.concatenate([np.asarray(r["fd"]) for r in r1], axis=0)
    consts = build_consts_np()
    ft = fnet_tables()
    mu, w0, w2, a0, a2 = inp["o_mu"][0], inp["o_w0"][0], inp["o_w2"][0], inp["o_a0"][0], inp["o_a2"][0]
    k_k, k_a, r_k = inp["o_k_k"][0], inp["o_k_a"][0], inp["o_r_k"][0].reshape(-1)
    lg, lb = inp["o_lnx_g"][0], inp["o_lnx_b"][0]
    PT3 = PT.reshape(4736, 2, 8192)
    pad = lambda a: np.ascontiguousarray(np.pad(a, ((0, 0), (0, 0), (1, 1))))
    maps = []
    for c in cores:
        ch = slice(c * 128, (c + 1) * 128)
        m = {"pr": pad(PT3[0:1024][ch]), "pk": pad(PT3[1024:2048][ch]), "pv": pad(PT3[2048:3072][ch]),
             "pwa": pad(PT3[3072:3200])}
        prm = np.zeros((128, 17), np.float32)
        for d in range(2):
            prm[:, 0 + d] = mu[d, 0:1024][ch]; prm[:, 2 + d] = mu[d, 1024:2048][ch]; prm[:, 4 + d] = mu[d, 2048:3072][ch]
            prm[:, 6 + d] = mu[d, 3072:3200]; prm[:, 8 + d] = w0[d][ch]; prm[:, 10 + d] = a0[d][ch]
        prm[:, 12] = k_k[ch]; prm[:, 13] = k_a[ch]; prm[:, 14] = r_k[ch]; prm[:, 15] = lg[ch]; prm[:, 16] = lb[ch]
        m["prm"] = prm
        m["w2a2"] = np.ascontiguousarray(np.concatenate([w2[:, :, ch], a2[:, :, ch]], axis=1).transpose(1, 0, 2))
        m.update(consts)
        b, g = c // 4, c % 4
        m["fx"] = np.ascontiguousarray(FD[b * 8192:(b + 1) * 8192, g * 128:(g + 1) * 128]).reshape(128, 8192)
        m["fw"] = np.ascontiguousarray(inp["o_fnet_w"][0, g])
        m.update(ft)
        maps.append(m)
    nc2 = build_l2(consts)
    r2 = run_bass_kernel_spmd(nc2, maps, core_ids=cores).results
    YC = np.concatenate([np.asarray(r["yout"]).reshape(128, 16384) for r in r2], axis=0)
    YD = np.concatenate([np.concatenate([np.asarray(r2[b * 4 + g]["ydT"]) for g in range(4)], axis=0) for b in range(2)], axis=1)
    maps = []
    for c in cores:
        ts = slice(c * 2048, (c + 1) * 2048)
        maps.append({"ycT": np.ascontiguousarray(YC[:, ts]), "ydT": np.ascontiguousarray(YD[:, ts]),
                     "gT": np.ascontiguousarray(PT[3200:4736, ts]), "h1": np.asarray(r1[c]["h1"]),
                     "o_w_out": np.ascontiguousarray(inp["o_w_out"][0]), "final_norm_g": inp["final_norm_g"]})
    nc3 = build_l3()
    r3 = run_bass_kernel_spmd(nc3, maps, core_ids=cores).results
    y = np.concatenate([np.asarray(r["y"]) for r in r3], axis=0).reshape(2, 8192, 1024)
    return y.astype(np.float32)
```

```python
from contextlib import ExitStack
import itertools
import numpy as np
import concourse.bass as bass
import concourse.mybir as mybir
from concourse.bass_utils import run_bass_kernel_spmd


F32 = mybir.dt.float32
BF16 = mybir.dt.bfloat16
AF = mybir.ActivationFunctionType
ALU = mybir.AluOpType
AX = mybir.AxisListType

N_DMA_SEMS = 8


class Region:
    __slots__ = ("w", "r", "name")

    def __init__(self, name=""):
        self.w = None
        self.r = {}
        self.name = name


class Sched:
    ENGS = ("pe", "dve", "act", "pool", "sp")

    def __init__(self, nc):
        self.nc = nc
        self.prog = {e: [] for e in self.ENGS}
        self.cnt = {}
        self.seen = {e: {} for e in self.ENGS}
        self.dma_rr = {e: 0 for e in self.ENGS}
        self.dma_last = {}
        self.same_engine_raw = True
        self.cut = 0
        self.nrec = 0
        self.log = []

    def _collect(self, eng, mykey, reads, writes):
        waits = {}

        def need(tok, kind):
            if tok is None:
                return
            k, v = tok
            if k == mykey:
                if eng == "pe":
                    return
                if not self.same_engine_raw:
                    return
            if waits.get(k, 0) < v:
                waits[k] = v

        for R in reads:
            need(R.w, "raw")
        for R in writes:
            need(R.w, "waw")
            for k, v in R.r.items():
                need((k, v), "war")
        out = []
        seen = self.seen[eng]
        for k, v in waits.items():
            if seen.get(k, 0) < v:
                seen[k] = v
                out.append((k, v))
        return out

    def _commit(self, tok, reads, writes):
        for R in writes:
            R.w = tok
            R.r = {}
        k, v = tok
        for R in reads:
            if R.r.get(k, 0) < v:
                R.r[k] = v

    def op(self, eng, fn, reads=(), writes=()):
        self.nrec += 1
        if self.cut and self.nrec > self.cut:
            return None
        if self.cut:
            self.log.append((self.nrec, eng, fn[0] if isinstance(fn, tuple) else "fn", str(fn[1].get("out", ""))[:120] if isinstance(fn, tuple) else ""))
        key = eng
        waits = self._collect(eng, key, reads, writes)
        idx = self.cnt.get(key, 0) + 1
        self.cnt[key] = idx
        tok = (key, idx)
        self.prog[eng].append([waits, fn, tok])
        self._commit(tok, reads, writes)
        return tok

    def dma(self, fn, reads=(), writes=(), q="sp"):
        self.nrec += 1
        if self.cut and self.nrec > self.cut:
            return None
        i = self.dma_rr[q]
        self.dma_rr[q] = (i + 1) % N_DMA_SEMS
        key = "dma_%s_%d" % (q, i)
        waits = self._collect(q, key, reads, writes)
        prev = self.cnt.get(key, 0)
        if prev > 0 and self.seen[q].get(key, 0) < prev:
            self.seen[q][key] = prev
            waits.append((key, prev))
        idx = prev + 1
        self.cnt[key] = idx
        tok = (key, idx)
        self.prog[q].append([waits, fn, tok])
        self._commit(tok, reads, writes)
        return tok

    def finalize(self):
        nc = self.nc
        waited = {}
        for e in self.ENGS:
            for waits, fn, tok in self.prog[e]:
                for k, v in waits:
                    waited.setdefault(k, set()).add(v)
        self.final_waits = []
        sem_of = {}
        val_of = {}
        for k, s in waited.items():
            sem_of[k] = nc.alloc_semaphore("s_" + k)
            isdma = k.startswith("dma_")
            step = 16 if isdma else 1
            if isdma:
                val_of[k] = None
            else:
                val_of[k] = {v: (i + 1) for i, v in enumerate(sorted(s))}
        engobj = {"pe": nc.tensor, "dve": nc.vector, "act": nc.scalar,
                  "pool": nc.gpsimd, "sp": nc.sync}

        def value(k, v):
            if val_of[k] is None:
                return 16 * v
            return val_of[k][v]

        def emit(e):
            def body(eng):
                for waits, fn, tok in self.prog[e]:
                    for k, v in waits:
                        eng.wait_ge(sem_of[k], value(k, v))
                    if fn is None:
                        continue
                    if isinstance(fn, tuple):
                        ins = getattr(eng, fn[0])(**fn[1])
                    else:
                        ins = fn(eng)
                    k, v = tok
                    if k in sem_of:
                        if val_of[k] is None:
                            ins.then_inc(sem_of[k], 16)
                        elif v in val_of[k]:
                            ins.then_inc(sem_of[k], 1)
            return body

        with nc.Block() as block:
            for e, dec in (("sp", block.sync), ("pe", block.tensor), ("dve", block.vector),
                           ("act", block.scalar), ("pool", block.gpsimd)):
                if self.prog[e]:
                    dec(emit(e))
        self.n_sems = len(sem_of)
        return self.n_sems

    def barrier_on(self, eng, toks):
        waits = []
        for tk in toks:
            if tk is None:
                continue
            k, v = tk
            if self.seen[eng].get(k, 0) < v:
                self.seen[eng][k] = v
                waits.append((k, v))
        if waits:
            self.prog[eng].append([waits, None, ("_none", 0)])


C = 128
BLK = 512
NEG_E = -float(np.exp(-0.5))
GN_EPS = 64e-5


def build_consts_np():
    idx = np.arange(128)
    lt = (idx[:, None] < idx[None, :]).astype(np.float32)
    le = (idx[:, None] <= idx[None, :]).astype(np.float32)
    gt = lt.T.copy()
    ge = le.T.copy()
    m4f = np.stack([lt, gt, gt, le], axis=1)
    m4b = np.stack([gt, lt, lt, ge], axis=1)
    mk = np.stack([le, ge], axis=1)
    ident = np.eye(128, dtype=np.float32)
    bd = np.kron(np.eye(2, dtype=np.float32), np.ones((64, 64), np.float32))
    scanm = np.ones((128, BLK), np.float32)
    scanm[:, ::C] = 0.0
    return {"c_m4": np.stack([m4f, m4b], axis=1).reshape(128, 2 * 4 * 128).copy(),
            "c_mk": mk.reshape(128, 256).copy(), "c_ident": ident, "c_bd": bd, "c_scanm": scanm}


XST = False


def emit_rwkv(S, nc, A, pr, pk, pv, pwa, prm, w2a2, yout, NB, T):
    sb = lambda name, shape, dt=F32: nc.alloc_sbuf_tensor(name, shape, dt)
    ps = lambda name, shape, dt=F32: nc.alloc_psum_tensor(name, shape, dt)
    R = Region
    nblk = T // BLK

    m4f = sb("m4f", [128, 2, 4, 128]); Rm4 = R()
    mkf = sb("mkf", [128, 2, 128]); Rmk = R()
    identf = sb("identf", [128, 128]); Ridf = R()
    identb = sb("identb", [128, 128], BF16); Ridb = R()
    bdf = sb("bdf", [128, 128]); Rbd = R()
    bdr = sb("bdr", [128, 128]); Rbdr = R()
    bdm = sb("bdm", [128, 128]); Rbdm = R()
    scanm = sb("scanm", [128, BLK]); Rsc = R()
    prmt = sb("prmt", [128, 17]); Rprm = R()
    w2f = sb("w2f", [128, 2, 128]); Rw2f = R()
    w2b = sb("w2b", [128, 2, 128], BF16); Rw2b = R()
    S.dma(("dma_start", dict(out=m4f[:].rearrange("p a b c -> p (a b c)"), in_=A["c_m4"])), writes=[Rm4])
    S.dma(("dma_start", dict(out=mkf[:].rearrange("p a c -> p (a c)"), in_=A["c_mk"])), writes=[Rmk])
    S.dma(("dma_start", dict(out=identf[:], in_=A["c_ident"])), writes=[Ridf])
    S.dma(("dma_start", dict(out=bdf[:], in_=A["c_bd"])), writes=[Rbd])
    S.dma(("dma_start", dict(out=scanm[:], in_=A["c_scanm"])), writes=[Rsc])
    S.dma(("dma_start", dict(out=prmt[:], in_=prm)), writes=[Rprm])
    S.dma(("dma_start", dict(out=w2f[:], in_=w2a2)), writes=[Rw2f])
    S.op("dve", ("tensor_copy", dict(out=identb[:], in_=identf[:])), reads=[Ridf], writes=[Ridb])
    S.op("dve", ("tensor_copy", dict(out=w2b[:], in_=w2f[:])), reads=[Rw2f], writes=[Rw2b])
    PM = lambda c: prmt[:, c:c + 1]
    S.op("dve", ("tensor_scalar", dict(out=bdr[:], in0=bdf[:], scalar1=PM(14), scalar2=None, op0=ALU.mult)), reads=[Rbd, Rprm], writes=[Rbdr])
    S.op("dve", ("tensor_scalar", dict(out=bdm[:], in0=bdf[:], scalar1=1.0 / 64, scalar2=None, op0=ALU.mult)), reads=[Rbd], writes=[Rbdm])

    def T2(name, dt=F32, n=BLK):
        return sb(name, [128, n], dt), R()
    ld = {}
    for nm in ("pr", "pk", "pv", "pwa"):
        ld[nm] = (sb("ld_" + nm, [128, BLK + 2]), R())
    tmp, Rtmp = T2("tmp")
    qr, Rqr = T2("qr"); qk, Rqk = T2("qk"); qv, Rqv = T2("qv"); qwa, Rqwa = T2("qwa")
    twa, Rtwa = T2("twa", BF16)
    sw, Rsw = T2("sw"); asg, Rasg = T2("asg")
    logw, Rlogw = T2("logw"); lin, Rlin = T2("lin"); linm, Rlinm = T2("linm"); lexm, Rlexm = T2("lexm")
    lex, Rlex = T2("lex"); lint, Rlint = T2("lint")
    e1, Re1 = T2("e1"); e1x, Re1x = T2("e1x"); e2, Re2 = T2("e2"); e3S = [sb("e3%d" % i, [128, BLK]) for i in range(2)]; Re3S = [R(), R()]; e3x, Re3x = T2("e3x"); e4, Re4 = T2("e4")
    kk, Rkk = T2("kk"); kk2, Rkk2 = T2("kk2"); rin, Rrin = T2("rin"); kkn, Rkkn = T2("kkn")
    kp, Rkp = T2("kp"); bv, Rbv = T2("bv"); rk, Rrk = T2("rk")
    rtS = [sb("rt%d" % i, [128, BLK], BF16) for i in range(2)]; RrtS = [R(), R()]; atS = [sb("at%d" % i, [128, BLK], BF16) for i in range(2)]; RatS = [R(), R()]; ktS = [sb("kt%d" % i, [128, BLK], BF16) for i in range(2)]; RktS = [R(), R()]; btS = [sb("bt%d" % i, [128, BLK], BF16) for i in range(2)]; RbtS = [R(), R()]
    r0S = [sb("r0%d" % i, [128, BLK]) for i in range(2)]; Rr0S = [R(), R()]; a0bS = [sb("a0b%d" % i, [128, BLK], BF16) for i in range(2)]; Ra0bS = [R(), R()]; kEbS = [sb("kEb%d" % i, [128, BLK], BF16) for i in range(2)]; RkEbS = [R(), R()]; bEbS = [sb("bEb%d" % i, [128, BLK], BF16) for i in range(2)]; RbEbS = [R(), R()]
    qvbS = [sb("qvb%d" % i, [128, BLK], BF16) for i in range(2)]; RqvbS = [R(), R()]
    ysum = sb("ysum", [128, T]); Rys = [R() for _ in range(T // C)]
    bsum = sb("bsum", [128, T]); Rbs = [R() for _ in range(nblk)]
    TT = [sb("TT%d" % i, [128, 4, 128], BF16) for i in range(2)]; RTT = [R(), R()]
    SBM = [sb("SBM%d" % i, [128, 2, 4, 128], BF16) for i in range(2)]; RSBM = [R(), R()]
    MKR = [sb("MKR%d" % i, [128, 2, 128]) for i in range(2)]; RMKR = [R(), R()]
    SX = [sb("SX%d" % i, [128, 2, 192], BF16) for i in range(2)]; RSX = [R(), R()]
    SAB = [sb("SAB%d" % i, [128, 2, 2, 128], BF16) for i in range(2)]; RSAB = [R(), R()]
    Gb = sb("Gb", [128, 128], BF16); RGb = R()
    Hb = sb("Hb", [128, 2, 128], BF16); RHb = R()
    Pb = sb("Pb", [128, 64], BF16); RPb = R()
    Zb = sb("Zb", [128, 2, 64], BF16); RZb = R()
    STz = [sb("STz%d" % h, [128, 64], BF16) for h in range(2)]; RST = [R(), R()]
    identP = sb("identP", [128, 64]); mkb = sb("mkb", [128, 2, 2, 128])
    HS = [slice(0, 64), slice(64, 128)]
    fin1, Rfin1 = T2("fin1"); fin2, Rfin2 = T2("fin2"); fin3, Rfin3 = T2("fin3")

    PS_M = ps("PS_M", [128, 2, 4, 128]); RPS_M = R()
    PS_K = ps("PS_K", [128, 512]); RPS_K = R()
    PS_X = [ps("PS_X%d" % h, [128, 512]) for h in range(2)]; RPS_X = R()
    PS_AB = ps("PS_AB", [128, 2, 2, 128]); RPS_AB = R()
    PS_G = ps("PS_G", [128, 512]); RPS_G = R()
    PS_T = ps("PS_T", [128, 8, 128], BF16); RPS_T = R()
    PS_P1 = PS_AB[:].rearrange("p a b c -> p (a b c)"); RPS_P1 = RPS_AB
    PS_P2 = PS_P1; RPS_P2 = RPS_AB
    mm = lambda out, l, r_, rd, wr, st=True, sp=True, sg=False: S.op("pe", ("matmul", dict(out=out, lhsT=l, rhs=r_, start=st, stop=sp, skip_group_check=sg)), reads=rd, writes=wr)
    S.op("pool", ("tensor_copy", dict(out=identP[0:64, :], in_=identf[0:64, 0:64])), reads=[Ridf], writes=[Ridf])
    S.op("pool", ("tensor_copy", dict(out=identP[64:128, :], in_=identf[64:128, 64:128])), reads=[Ridf], writes=[Ridf])
    for h in range(2):
        S.op("pool", ("tensor_copy", dict(out=mkb[:, :, h, :], in_=mkf[:])), reads=[Rmk], writes=[Rmk])
    ytmp = sb("ytmp", [128, 128]); Rytmp = R()
    out_toks = []
    NFILL = 7
    NPREP = 2
    def prep_gen(b, d, blk, pp):
        bwd = (d == 1)
        midc, totc = (C // 2 - 1, C - 1) if not bwd else (C // 2, 0)
        t0 = blk * BLK
        rt_, Rrt_ = rtS[pp], RrtS[pp]
        at_, Rat_ = atS[pp], RatS[pp]
        kt_, Rkt_ = ktS[pp], RktS[pp]
        bt_, Rbt_ = btS[pp], RbtS[pp]
        r0_, Rr0_ = r0S[pp], Rr0S[pp]
        a0b_, Ra0b_ = a0bS[pp], Ra0bS[pp]
        kEb_, RkEb_ = kEbS[pp], RkEbS[pp]
        bEb_, RbEb_ = bEbS[pp], RbEbS[pp]
        qvb_, Rqvb_ = qvbS[pp], RqvbS[pp]
        e3_, Re3_ = e3S[pp], Re3S[pp]
        for nm, src in (("pr", pr), ("pk", pk), ("pv", pv), ("pwa", pwa)):
            tl, Rl = ld[nm]
            S.dma(("dma_start", dict(out=tl[:], in_=src[:, b, t0:t0 + BLK + 2])), writes=[Rl])
            yield
        sh = (slice(0, BLK) if not bwd else slice(2, BLK + 2))
        cur = slice(1, BLK + 1)
        for nm, q, Rq, mc in (("pr", qr, Rqr, 0), ("pk", qk, Rqk, 2), ("pv", qv, Rqv, 4), ("pwa", qwa, Rqwa, 6)):
            tl, Rl = ld[nm]
            S.op("dve", ("tensor_tensor", dict(out=tmp[:], in0=tl[:, sh], in1=tl[:, cur], op=ALU.subtract)), reads=[Rl], writes=[Rtmp])
            yield
            S.op("dve", ("scalar_tensor_tensor", dict(out=q[:], in0=tmp[:], scalar=PM(mc + d), in1=tl[:, cur], op0=ALU.mult, op1=ALU.add)), reads=[Rtmp, Rl, Rprm], writes=[Rq])
            yield
        S.op("act", ("activation", dict(out=twa[0:64, :], in_=qwa[0:64, :], func=AF.Tanh)), reads=[Rqwa], writes=[Rtwa])
        yield
        S.op("dve", ("tensor_copy", dict(out=twa[64:128, :], in_=qwa[64:128, :])), reads=[Rqwa], writes=[Rtwa])
        yield
        S.op("pe", ("matmul", dict(out=PS_P1, lhsT=w2b[0:64, d, :], rhs=twa[0:64, :], start=True, stop=True)), reads=[Rw2b, Rtwa], writes=[RPS_P1])
        S.op("act", ("activation", dict(out=sw[:], in_=PS_P1, func=AF.Sigmoid, bias=PM(8 + d))), reads=[RPS_P1, Rprm], writes=[Rsw])
        yield
        S.op("pe", ("matmul", dict(out=PS_P2, lhsT=w2b[64:128, d, :], rhs=twa[64:128, :], start=True, stop=True)), reads=[Rw2b, Rtwa], writes=[RPS_P2])
        S.op("act", ("activation", dict(out=asg[:], in_=PS_P2, func=AF.Sigmoid, bias=PM(10 + d))), reads=[RPS_P2, Rprm], writes=[Rasg])
        yield
        S.op("dve", ("tensor_scalar", dict(out=logw[:], in0=sw[:], scalar1=NEG_E, scalar2=None, op0=ALU.mult)), reads=[Rsw], writes=[Rlogw])
        yield
        S.op("dve", ("tensor_tensor_scan", dict(out=lin[:], data0=scanm[:], data1=logw[:], initial=0.0, op0=ALU.mult, op1=ALU.add)), reads=[Rsc, Rlogw], writes=[Rlin])
        yield
        lin3 = lambda tl: tl[:].rearrange("p (c t) -> p c t", t=C)
        bc = lambda tl, col: lin3(tl)[:, :, col:col + 1].to_broadcast([128, BLK // C, C])
        if bwd:
            S.op("dve", ("tensor_tensor", dict(out=lin3(tmp), in0=bc(lin, C - 1), in1=lin3(lin), op=ALU.subtract)), reads=[Rlin], writes=[Rtmp])
            yield
            S.op("dve", ("tensor_tensor", dict(out=lin[:], in0=tmp[:], in1=logw[:], op=ALU.add)), reads=[Rtmp, Rlogw], writes=[Rlin])
            yield
        S.op("dve", ("tensor_tensor", dict(out=lin3(linm), in0=lin3(lin), in1=bc(lin, midc), op=ALU.subtract)), reads=[Rlin], writes=[Rlinm])
        yield
        S.op("dve", ("tensor_tensor", dict(out=lexm[:], in0=linm[:], in1=logw[:], op=ALU.subtract)), reads=[Rlinm, Rlogw], writes=[Rlexm])
        yield
        S.op("dve", ("tensor_tensor", dict(out=lex[:], in0=lin[:], in1=logw[:], op=ALU.subtract)), reads=[Rlin, Rlogw], writes=[Rlex])
        yield
        S.op("dve", ("tensor_tensor", dict(out=lin3(lint), in0=lin3(lin), in1=bc(lin, totc), op=ALU.subtract)), reads=[Rlin], writes=[Rlint])
        yield
        S.op("act", ("activation", dict(out=e1[:], in_=linm[:], func=AF.Exp)), reads=[Rlinm], writes=[Re1])
        yield
        S.op("act", ("activation", dict(out=e1x[:], in_=lexm[:], func=AF.Exp)), reads=[Rlexm], writes=[Re1x])
        yield
        S.op("act", ("activation", dict(out=e2[:], in_=linm[:], func=AF.Exp, scale=-1.0)), reads=[Rlinm], writes=[Re2])
        yield
        S.op("act", ("activation", dict(out=e3_[:], in_=lin[:], func=AF.Exp)), reads=[Rlin], writes=[Re3_])
        yield
        S.op("act", ("activation", dict(out=e3x[:], in_=lex[:], func=AF.Exp)), reads=[Rlex], writes=[Re3x])
        yield
        S.op("act", ("activation", dict(out=e4[:], in_=lint[:], func=AF.Exp, scale=-1.0)), reads=[Rlint], writes=[Re4])
        yield
        S.op("dve", ("tensor_scalar", dict(out=kk[:], in0=qk[:], scalar1=PM(12), scalar2=None, op0=ALU.mult)), reads=[Rqk, Rprm], writes=[Rkk])
        yield
        S.op("pool", ("tensor_tensor", dict(out=kk2[:], in0=kk[:], in1=kk[:], op=ALU.mult)), reads=[Rkk], writes=[Rkk2])
        yield
        S.op("pe", ("matmul", dict(out=PS_P1, lhsT=bdf[:], rhs=kk2[:], start=True, stop=True)), reads=[Rbd, Rkk2], writes=[RPS_P1])
        S.op("dve", ("tensor_scalar", dict(out=rin[:], in0=PS_P1, scalar1=1e-12, scalar2=None, op0=ALU.max)), reads=[RPS_P1], writes=[Rrin])
        yield
        S.op("act", ("activation", dict(out=rin[:], in_=rin[:], func=AF.Sqrt)), reads=[Rrin], writes=[Rrin])
        yield
        S.op("dve", ("reciprocal", dict(out=rin[:], in_=rin[:])), reads=[Rrin], writes=[Rrin])
        yield
        S.op("dve", ("tensor_tensor", dict(out=kkn[:], in0=kk[:], in1=rin[:], op=ALU.mult)), reads=[Rkk, Rrin], writes=[Rkkn])
        yield
        S.op("dve", ("tensor_scalar", dict(out=tmp[:], in0=asg[:], scalar1=-1.0, scalar2=PM(13), op0=ALU.add, op1=ALU.mult)), reads=[Rasg, Rprm], writes=[Rtmp])
        yield
        S.op("dve", ("scalar_tensor_tensor", dict(out=kp[:], in0=tmp[:], scalar=1.0, in1=qk[:], op0=ALU.add, op1=ALU.mult)), reads=[Rtmp, Rqk], writes=[Rkp])
        yield
        S.op("pool", ("tensor_tensor", dict(out=bv[:], in0=kkn[:], in1=asg[:], op=ALU.mult)), reads=[Rkkn, Rasg], writes=[Rbv])
        yield
        S.op("pool", ("tensor_tensor", dict(out=rk[:], in0=qr[:], in1=kp[:], op=ALU.mult)), reads=[Rqr, Rkp], writes=[Rrk])
        yield
        S.op("pe", ("matmul", dict(out=PS_P2, lhsT=bdr[:], rhs=rk[:], start=True, stop=True)), reads=[Rbdr, Rrk], writes=[RPS_P2])
        bsl = bsum[:, t0:t0 + BLK]
        if d == 0:
            S.op("dve", ("tensor_tensor", dict(out=bsl, in0=PS_P2, in1=qv[:], op=ALU.mult)), reads=[RPS_P2, Rqv], writes=[Rbs[blk]])
            yield
        else:
            S.op("dve", ("tensor_tensor", dict(out=tmp[:], in0=PS_P2, in1=qv[:], op=ALU.mult)), reads=[RPS_P2, Rqv], writes=[Rtmp])
            yield
            S.op("pool", ("tensor_tensor", dict(out=bsl, in0=bsl, in1=tmp[:], op=ALU.add)), reads=[Rtmp, Rbs[blk]], writes=[Rbs[blk]])
            yield
        S.op("dve", ("tensor_tensor", dict(out=rt_[:], in0=qr[:], in1=e1[:], op=ALU.mult)), reads=[Rqr, Re1], writes=[Rrt_])
        yield
        S.op("dve", ("scalar_tensor_tensor", dict(out=at_[:], in0=kkn[:], scalar=-1.0, in1=e1x[:], op0=ALU.mult, op1=ALU.mult)), reads=[Rkkn, Re1x], writes=[Rat_])
        yield
        S.op("pool", ("tensor_tensor", dict(out=kt_[:], in0=kp[:], in1=e2[:], op=ALU.mult)), reads=[Rkp, Re2], writes=[Rkt_])
        yield
        S.op("pool", ("tensor_tensor", dict(out=bt_[:], in0=bv[:], in1=e2[:], op=ALU.mult)), reads=[Rbv, Re2], writes=[Rbt_])
        yield
        S.op("pool", ("tensor_tensor", dict(out=r0_[:], in0=qr[:], in1=e3_[:], op=ALU.mult)), reads=[Rqr, Re3_], writes=[Rr0_])
        yield
        S.op("dve", ("scalar_tensor_tensor", dict(out=a0b_[:], in0=kkn[:], scalar=-1.0, in1=e3x[:], op0=ALU.mult, op1=ALU.mult)), reads=[Rkkn, Re3x], writes=[Ra0b_])
        yield
        S.op("pool", ("tensor_tensor", dict(out=kEb_[:], in0=kp[:], in1=e4[:], op=ALU.mult)), reads=[Rkp, Re4], writes=[RkEb_])
        yield
        S.op("pool", ("tensor_tensor", dict(out=bEb_[:], in0=bv[:], in1=e4[:], op=ALU.mult)), reads=[Rbv, Re4], writes=[RbEb_])
        yield
        S.op("act", ("activation", dict(out=qvb_[:], in_=qv[:], func=AF.Copy)), reads=[Rqv], writes=[Rqvb_])
        yield


    def block_stages(b, d, blk, pp):
        bwd = (d == 1)
        midc, totc = (C // 2 - 1, C - 1) if not bwd else (C // 2, 0)
        t0 = blk * BLK
        rt_, Rrt_ = rtS[pp], RrtS[pp]
        at_, Rat_ = atS[pp], RatS[pp]
        kt_, Rkt_ = ktS[pp], RktS[pp]
        bt_, Rbt_ = btS[pp], RbtS[pp]
        r0_, Rr0_ = r0S[pp], Rr0S[pp]
        a0b_, Ra0b_ = a0bS[pp], Ra0bS[pp]
        kEb_, RkEb_ = kEbS[pp], RkEbS[pp]
        bEb_, RbEb_ = bEbS[pp], RbEbS[pp]
        qvb_, Rqvb_ = qvbS[pp], RqvbS[pp]
        e3_, Re3_ = e3S[pp], Re3S[pp]

        def stage1(ck):
            ci, cs, gci, p = ck
            for i, (src, Rs) in enumerate(((qvb_, Rqvb_), (a0b_, Ra0b_), (bEb_, RbEb_), (kEb_, RkEb_))):
                S.op("pe", ("transpose", dict(out=PS_T[:, i, :], in_=src[:, cs], identity=identb[:])), reads=[Rs, Ridb], writes=[RPS_T])
            yield
            S.op("act", ("activation", dict(out=TT[p][:], in_=PS_T[:, 0:4, :], func=AF.Copy)), reads=[RPS_T], writes=[RTT[p]])
            yield
            for h in range(2):
                hs = HS[h]
                mm(PS_M[:, h, 0, :], bt_[hs, cs], at_[hs, cs], [Rbt_, Rat_], [RPS_M])
                mm(PS_M[:, h, 1, :], at_[hs, cs], bt_[hs, cs], [Rbt_, Rat_], [RPS_M])
                yield
                mm(PS_M[:, h, 2, :], at_[hs, cs], kt_[hs, cs], [Rkt_, Rat_], [RPS_M])
                mm(PS_M[:, h, 3, :], bt_[hs, cs], rt_[hs, cs], [Rbt_, Rrt_], [RPS_M])
                yield
                mm((PS_K if h == 0 else PS_G)[:, 0:128], kt_[hs, cs], rt_[hs, cs], [Rkt_, Rrt_], [RPS_K if h == 0 else RPS_G])
                yield
            for h in range(2):
                S.op("dve", ("tensor_tensor", dict(out=SBM[p][:, h], in0=PS_M[:, h], in1=m4f[:, d, :, :], op=ALU.mult)), reads=[RPS_M, Rm4], writes=[RSBM[p]])
                yield
            S.op("dve", ("tensor_tensor", dict(out=MKR[p][:, 0, :], in0=PS_K[:, 0:128], in1=mkf[:, d, :], op=ALU.mult)), reads=[RPS_K, Rmk], writes=[RMKR[p]])
            yield
            S.op("dve", ("tensor_tensor", dict(out=MKR[p][:, 1, :], in0=PS_G[:, 0:128], in1=mkf[:, d, :], op=ALU.mult)), reads=[RPS_G, Rmk], writes=[RMKR[p]])
            yield
            S.op("act", ("activation", dict(out=SX[p][:, :, 0:128], in_=SBM[p][:, :, 3, :], func=AF.Copy)), reads=[RSBM[p]], writes=[RSX[p]])
            S.op("pool", ("tensor_copy", dict(out=SX[p][:, :, 128:192], in_=TT[p][:, 2, :].rearrange("p (h j) -> p h j", h=2))), reads=[RTT[p]], writes=[RSX[p]])
            yield

        def stage2(ck):
            ci, cs, gci, p = ck
            for h in range(2):
                mm(PS_X[h][:, 0:192], identb[:], SX[p][:, h, :], [Ridb, RSX[p]], [RPS_X], st=True, sp=True)
            A_ = [SBM[p][:, h, 1, :] for h in range(2)]
            B_ = [SBM[p][:, h, 0, :] for h in range(2)]
            Rcur = RSBM[p]
            for lv in range(7):
                for h in range(2):
                    mm(PS_X[h][:, 0:192], A_[h], SX[p][:, h, :], [Rcur, RSX[p]], [RPS_X], st=False, sp=True, sg=True)
                if lv < 6:
                    nb = lv % 2
                    for h in range(2):
                        mm(PS_AB[:, h, 0, :], B_[h], A_[h], [Rcur], [RPS_AB])
                        mm(PS_AB[:, h, 1, :], A_[h], B_[h], [Rcur], [RPS_AB])
                S.op("dve", ("tensor_copy", dict(out=SX[p][:, 0, :], in_=PS_X[0][:, 0:192])), reads=[RPS_X], writes=[RSX[p]])
                S.op("act", ("activation", dict(out=SX[p][:, 1, :], in_=PS_X[1][:, 0:192], func=AF.Copy)), reads=[RPS_X], writes=[RSX[p]])
                if lv < 6:
                    S.op("act", ("activation", dict(out=SAB[nb][:].rearrange("p a b c -> p (a b c)"), in_=PS_AB[:].rearrange("p a b c -> p (a b c)"), func=AF.Copy)), reads=[RPS_AB], writes=[RSAB[nb]])
                    A_ = [SAB[nb][:, h, 0, :] for h in range(2)]
                    B_ = [SAB[nb][:, h, 1, :] for h in range(2)]
                    Rcur = RSAB[nb]
                yield

        def stage3(ck):
            ci, cs, gci, p = ck
            for h in range(2):
                hs = HS[h]
                a0T = TT[p][:, 1, hs]
                mm(PS_G[hs, 0:128], a0T, SX[p][:, h, 0:128], [RTT[p], RSX[p]], [RPS_G])
                mm(PS_G[hs, 128:192], a0T, SX[p][:, h, 128:192], [RTT[p], RSX[p]], [RPS_G])
                yield
                mm(PS_G[:, 192 + 128 * h:320 + 128 * h], SBM[p][:, h, 2, :], SX[p][:, h, 0:128], [RSBM[p], RSX[p]], [RPS_G])
                mm(PS_K[:, 256 + 64 * h:320 + 64 * h], SBM[p][:, h, 2, :], SX[p][:, h, 128:192], [RSBM[p], RSX[p]], [RPS_K])
                yield
            S.op("dve", ("tensor_tensor", dict(out=Gb[:], in0=PS_G[:, 0:128], in1=r0_[:, cs], op=ALU.add)), reads=[RPS_G, Rr0_], writes=[RGb])
            yield
            S.op("dve", ("tensor_tensor", dict(out=Hb[:], in0=PS_G[:, 192:448].rearrange("p (h t) -> p h t", h=2), in1=MKR[p][:], op=ALU.add)), reads=[RPS_G, RMKR[p]], writes=[RHb])
            yield
            tcol = ci * C + totc
            S.op("dve", ("scalar_tensor_tensor", dict(out=Pb[:], in0=identP[:], scalar=e3_[:, tcol:tcol + 1], in1=PS_G[:, 128:192], op0=ALU.mult, op1=ALU.add)), reads=[RPS_G, Ridf, Re3_], writes=[RPb])
            yield
            S.op("dve", ("tensor_tensor", dict(out=Zb[:], in0=PS_K[:, 256:384].rearrange("p (h j) -> p h j", h=2), in1=TT[p][:, 3, :].rearrange("p (h j) -> p h j", h=2), op=ALU.add)), reads=[RPS_K, RTT[p]], writes=[RZb])
            yield
            for h in range(2):
                hs = HS[h]
                mm(PS_M[hs, 0, 0, :], STz[h][:], Gb[:], [RST[h], RGb], [RPS_M], st=True, sp=False)
                mm(PS_M[hs, 0, 0, :], TT[p][:, 0, hs], Hb[:, h, :], [RTT[p], RHb], [RPS_M], st=False, sp=True)
                yield
                mm(PS_M[hs, 0, 1, 0:64], Pb[:], STz[h][:], [RPb, RST[h]], [RPS_M], st=True, sp=False)
                mm(PS_M[hs, 0, 1, 0:64], Zb[:, h, :], TT[p][:, 0, hs], [RZb, RTT[p]], [RPS_M], st=False, sp=True)
                yield
            ysl = ysum[:, t0 + ci * C: t0 + (ci + 1) * C]
            if d == 0:
                S.op("act", ("activation", dict(out=ysl, in_=PS_M[:, 0, 0, :], func=AF.Copy)), reads=[RPS_M], writes=[Rys[gci]])
            else:
                S.op("act", ("activation", dict(out=ytmp[:, 0:128], in_=PS_M[:, 0, 0, :], func=AF.Copy)), reads=[RPS_M], writes=[Rytmp])
                S.op("dve", ("tensor_tensor", dict(out=ysl, in0=ytmp[:, 0:128], in1=ysl, op=ALU.add)), reads=[Rytmp, Rys[gci]], writes=[Rys[gci]])
            yield
            for h in range(2):
                hs = HS[h]
                S.op("act", ("activation", dict(out=STz[h][hs, :], in_=PS_M[hs, 0, 1, 0:64], func=AF.Copy)), reads=[RPS_M], writes=[RST[h]])
            yield

        return stage1, stage2, stage3

    def finalize_batch(b):
        for blk in range(nblk):
            t0 = blk * BLK
            ysl = ysum[:, t0:t0 + BLK]
            Rin = Rys[t0 // C: (t0 + BLK) // C]
            S.op("pe", ("matmul", dict(out=PS_P1, lhsT=bdm[:], rhs=ysl, start=True, stop=True)), reads=[Rbdm] + Rin, writes=[RPS_P1])
            S.op("dve", ("tensor_tensor", dict(out=fin1[:], in0=ysl, in1=PS_P1, op=ALU.subtract)), reads=[RPS_P1] + Rin, writes=[Rfin1])
            S.op("pool", ("tensor_tensor", dict(out=fin2[:], in0=fin1[:], in1=fin1[:], op=ALU.mult)), reads=[Rfin1], writes=[Rfin2])
            S.op("pe", ("matmul", dict(out=PS_P2, lhsT=bdm[:], rhs=fin2[:], start=True, stop=True)), reads=[Rbdm, Rfin2], writes=[RPS_P2])
            S.op("dve", ("tensor_scalar", dict(out=fin3[:], in0=PS_P2, scalar1=GN_EPS, scalar2=None, op0=ALU.add)), reads=[RPS_P2], writes=[Rfin3])
            S.op("act", ("activation", dict(out=fin3[:], in_=fin3[:], func=AF.Sqrt)), reads=[Rfin3], writes=[Rfin3])
            S.op("dve", ("reciprocal", dict(out=fin3[:], in_=fin3[:])), reads=[Rfin3], writes=[Rfin3])
            S.op("dve", ("tensor_tensor", dict(out=fin1[:], in0=fin1[:], in1=fin3[:], op=ALU.mult)), reads=[Rfin1, Rfin3], writes=[Rfin1])
            S.op("dve", ("tensor_scalar", dict(out=fin2[:], in0=fin1[:], scalar1=PM(15), scalar2=PM(16), op0=ALU.mult, op1=ALU.add)), reads=[Rfin1, Rprm], writes=[Rfin2])
            S.op("dve", ("tensor_tensor", dict(out=fin2[:], in0=fin2[:], in1=bsum[:, t0:t0 + BLK], op=ALU.add)), reads=[Rfin2, Rbs[blk]], writes=[Rfin2])
            out_toks.append(S.dma(("dma_start", dict(out=yout[:, b, t0:t0 + BLK], in_=fin2[:])), reads=[Rfin2]))

    sched_blocks = []
    for b in range(NB):
        for d in range(2):
            order = list(range(nblk)) if d == 0 else list(range(nblk - 1, -1, -1))
            for n_, blk in enumerate(order):
                sched_blocks.append((b, d, blk, n_ == 0, (n_ == len(order) - 1) and d == 1))
    pcount = 0
    for _ in prep_gen(sched_blocks[0][0], sched_blocks[0][1], sched_blocks[0][2], 0):
        pass
    for k, (b, d, blk, first_of_dir, last_of_batch) in enumerate(sched_blocks):
        pp = k % 2
        bwd = (d == 1)
        if first_of_dir:
            S.op("pool", ("memset", dict(ap=STz[0][:], constant=0.0)), writes=[RST[0]])
            S.op("pool", ("memset", dict(ap=STz[1][:], constant=0.0)), writes=[RST[1]])
        stage1, stage2, stage3 = block_stages(b, d, blk, pp)
        chunks = list(range(BLK // C)) if not bwd else list(range(BLK // C - 1, -1, -1))
        cks = [(ci, slice(ci * C, (ci + 1) * C), (blk * BLK // C) + ci, (pcount + n_) % 2) for n_, ci in enumerate(chunks)]
        pcount += len(chunks)
        if k + 1 < len(sched_blocks) and sched_blocks[k + 1][0] == b:
            nb_, nd_, nblk_ = sched_blocks[k + 1][:3]
            pgen = prep_gen(nb_, nd_, nblk_, (k + 1) % 2)
        else:
            pgen = iter(())
        for _ in stage1(cks[0]):
            pass
        for idx, ck in enumerate(cks):
            fill = itertools.chain(stage3(cks[idx - 1]) if idx > 0 else iter(()), stage1(cks[idx + 1]) if idx + 1 < len(cks) else iter(()))
            for _ in stage2(ck):
                for _k in range(NFILL):
                    next(fill, None)
                for _k in range(NPREP):
                    next(pgen, None)
            for _ in fill:
                pass
        for _ in stage3(cks[-1]):
            pass
        for _ in pgen:
            pass
        if last_of_batch:
            finalize_batch(b)
            if k + 1 < len(sched_blocks):
                nb_, nd_, nblk_ = sched_blocks[k + 1][:3]
                for _ in prep_gen(nb_, nd_, nblk_, (k + 1) % 2):
                    pass
    return out_toks


NT = 2048
NTH = NT + 2


def emit_p1(S, nc, I, O):
    sb = lambda name, shape, dt=F32: nc.alloc_sbuf_tensor(name, shape, dt)
    ps = lambda name, shape, dt=F32: nc.alloc_psum_tensor(name, shape, dt)
    R = Region
    op = S.op
    mm = lambda out, l, r_, rd, wr, st=True, sp=True: op("pe", ("matmul", dict(out=out, lhsT=l, rhs=r_, start=st, stop=sp)), reads=rd, writes=wr)

    identf = sb("identf", [128, 128]); identb = sb("identb", [128, 128], BF16); Rid = R()
    gE = sb("gE", [128, 8, 1]); gO = sb("gO", [128, 8, 1]); Rg = R()
    S.dma(("dma_start", dict(out=identf[:], in_=I["c_ident"])), writes=[Rid])
    op("dve", ("tensor_copy", dict(out=identb[:], in_=identf[:])), reads=[Rid], writes=[Rid])
    S.dma(("dma_start", dict(out=gE[:, :, 0], in_=I["e_norm_g"].rearrange("(k p) -> p k", p=128), allow_slow_non_contiguous=True)), writes=[Rg])
    S.dma(("dma_start", dict(out=gO[:, :, 0], in_=I["o_norm_g"].rearrange("(k p) -> p k", p=128), allow_slow_non_contiguous=True)), writes=[Rg])
    hnT = sb("hnT", [128, 8, NTH], BF16); RhnT = [R() for _ in range(18)]
    yT = nc.dram_tensor("yT_d", [16, 128, NT], BF16).ap(); RyT = [[R() for _ in range(4)] for _ in range(16)]
    U = sb("U", [128, 4096]); RU = R()
    xt = [sb("xt%d" % i, [128, 1024]) for i in range(2)]; Rxt = [R(), R()]
    yo = [sb("yo%d" % i, [128, 512], BF16) for i in range(2)]; Ryo = [R(), R()]
    ytl = [sb("ytl%d" % i, [128, 16, 128], BF16) for i in range(2)]; Rytl = [R(), R()]
    xn = sb("xn", [128, 1024], BF16); Rxn = R()
    sq = sb("sq", [128, 1024]); Rsq = R()
    st = sb("st", [128, 8]); Rst = R()
    stg = [sb("stg%d" % i, [128, 8, 256]) for i in range(2)]; Rstg = [R(), R()]
    wbf = [sb("wbf%d" % i, [128, 8, 512], BF16) for i in range(2)]; Rwbf = [R(), R()]
    wbig = sb("wbig", [128, 16, 1024], BF16); Rwbig = R()
    t1 = sb("t1", [128, 512]); Rt1 = R()
    t1b = sb("t1b", [128, 512]); t1s = [t1, t1b]; Rt1s = [Rt1, R()]
    t2b = sb("t2b", [128, 512]); t3b = sb("t3b", [128, 512])
    t2 = sb("t2", [128, 512]); Rt2 = R()
    t3 = sb("t3", [128, 512]); Rt3 = R()
    cw = sb("cw", [128, 8, 3]); Rcw = R()
    PS_a = ps("PS_a", [128, 512]); RPa = R()
    PS_b = ps("PS_b", [128, 512]); RPb = R()
    PS_c = ps("PS_c", [128, 512]); RPc = R()
    PS_d = ps("PS_d", [128, 512]); RPd = R()
    PS_t = ps("PS_t", [128, 8, 128], BF16); RPt = R()
    PS_m = ps("PS_m", [128, 8, 128]); RPm = R()
    for j_ in range(3):
        S.dma(("dma_start", dict(out=cw[:, :, j_], in_=I["e_conv_w"][j_].rearrange("(cb p) -> p cb", p=128), allow_slow_non_contiguous=True)), writes=[Rcw])

    def norm_tile(xtile, Rx, gt, dst_fn, Rdst, nvalid=128):
        op("act", ("activation", dict(out=sq[:], in_=xtile[:], func=AF.Square)), reads=[Rx], writes=[Rsq])
        op("dve", ("reduce_sum", dict(out=st[:, 0:1], in_=sq[:], axis=AX.X)), reads=[Rsq], writes=[Rst])
        op("dve", ("tensor_scalar", dict(out=st[:, 1:2], in0=st[:, 0:1], scalar1=1.0 / 1024, scalar2=1e-6, op0=ALU.mult, op1=ALU.add)), reads=[Rst], writes=[Rst])
        op("act", ("activation", dict(out=st[:, 2:3], in_=st[:, 1:2], func=AF.Sqrt)), reads=[Rst], writes=[Rst])
        op("dve", ("reciprocal", dict(out=st[:, 3:4], in_=st[:, 2:3])), reads=[Rst], writes=[Rst])
        op("dve", ("tensor_scalar", dict(out=xn[:], in0=xtile[:], scalar1=st[:, 3:4], scalar2=None, op0=ALU.mult)), reads=[Rx, Rst], writes=[Rxn])
        for k in range(8):
            op("pe", ("transpose", dict(out=PS_t[:, k, :], in_=xn[:, k * 128:(k + 1) * 128], identity=identb[:])), reads=[Rxn, Rid], writes=[RPt])
        dst_fn(gt)

    xh = I["xh"]
    for i in range(17):
        xb, Rx = xt[i % 2], Rxt[i % 2]
        if i < 16:
            S.dma(("dma_start", dict(out=xb[:], in_=xh[1 + 128 * i: 1 + 128 * (i + 1), :])), writes=[Rx])
            def dst(gt, i=i):
                op("dve", ("tensor_tensor", dict(out=hnT[:, :, 1 + 128 * i: 1 + 128 * (i + 1)], in0=PS_t[:], in1=gt[:].to_broadcast([128, 8, 128]), op=ALU.mult)), reads=[RPt, Rg], writes=[RhnT[i]])
        else:
            op("pool", ("memset", dict(ap=xb[:], constant=0.0)), writes=[Rx])
            S.dma(("dma_start", dict(out=xb[0:1, :], in_=xh[0:1, :])), writes=[Rx])
            S.dma(("dma_start", dict(out=xb[1:2, :], in_=xh[NT + 1:NT + 2, :])), writes=[Rx])
            def dst(gt):
                op("dve", ("tensor_tensor", dict(out=hnT[:, :, 0:1], in0=PS_t[:, :, 0:1], in1=gt[:], op=ALU.mult)), reads=[RPt, Rg], writes=[RhnT[16]])
                op("dve", ("tensor_tensor", dict(out=hnT[:, :, NT + 1:NT + 2], in0=PS_t[:, :, 1:2], in1=gt[:], op=ALU.mult)), reads=[RPt, Rg], writes=[RhnT[17]])
        norm_tile(xb, Rx, gE, dst, None)
    allh = RhnT

    wi = I["e_w_in"].rearrange("(k p) (s c) -> p k s c", p=128, c=1024)

    def load_w(buf, src4, nsp):
        for s_ in range(nsp):
            sb_ = s_ % 2
            S.dma(("dma_start", dict(out=stg[sb_][:, :, 0:128], in_=src4[:, :, s_, :])), writes=[Rstg[sb_]])
            op("pool", ("tensor_copy", dict(out=wbf[buf][:, :, s_ * 128:(s_ + 1) * 128], in_=stg[sb_][:, :, 0:128])), reads=[Rstg[sb_]], writes=[Rwbf[buf]])
        return wbf[buf][:, :, 0:nsp * 128].rearrange("p k (s c) -> p k s c", c=128)

    PSc0, RPc0, PSd0, RPd0 = PS_c, RPc, PS_d, RPd
    t2s = [t2, t2b]; Rt2s = [Rt2, R()]
    t3s = [t3, t3b]; Rt3s = [Rt3, R()]
    xc = U[:, 0:NTH]
    chunksA = [(0, 512), (512, 512), (1024, 512), (1536, 512), (2048, 2)]
    for cb in range(8):
        w4 = load_w(cb % 2, wi[:, :, 0:4, cb * 128:(cb + 1) * 128], 4)
        Rw = Rwbf[cb % 2]
        for ci_, (c0, n) in enumerate(chunksA):
            (PA, RA_), (PB, RB_) = (((PS_a, RPa), (PS_b, RPb)) if ci_ % 2 == 0 else ((PS_c, RPc), (PS_d, RPd)))
            for k in range(8):
                mm(PA[:, 0:n], w4[:, k, 0, :], hnT[:, k, c0:c0 + n], [Rw] + allh, [RA_], st=(k == 0), sp=(k == 7))
            for k in range(8):
                mm(PB[:, 0:n], w4[:, k, 2, :], hnT[:, k, c0:c0 + n], [Rw] + allh, [RB_], st=(k == 0), sp=(k == 7))
            t1_, Rt1_ = t1s[ci_ % 2], Rt1s[ci_ % 2]
            op("act", ("activation", dict(out=t1_[:, 0:n], in_=PA[:, 0:n], func=AF.Copy)), reads=[RA_], writes=[Rt1_])
            op("dve", ("tensor_tensor", dict(out=xc[:, c0:c0 + n], in0=PB[:, 0:n], in1=t1_[:, 0:n], op=ALU.mult)), reads=[RB_, Rt1_], writes=[RU])
        for j in range(4):
            c0 = 1 + 512 * j
            (PS_c, RPc), (PS_d, RPd) = ((PSc0, RPc0), (PSd0, RPd0)) if j % 2 == 1 else ((PS_a, RPa), (PS_b, RPb))
            t2, Rt2 = t2s[j % 2], Rt2s[j % 2]
            t3, Rt3 = t3s[j % 2], Rt3s[j % 2]
            for k in range(8):
                mm(PS_c[:], w4[:, k, 1, :], hnT[:, k, c0:c0 + 512], [Rw] + allh, [RPc], st=(k == 0), sp=(k == 7))
            for k in range(8):
                mm(PS_d[:], w4[:, k, 3, :], hnT[:, k, c0:c0 + 512], [Rw] + allh, [RPd], st=(k == 0), sp=(k == 7))
            op("dve", ("tensor_scalar", dict(out=t2[:], in0=xc[:, c0 - 1:c0 + 511], scalar1=cw[:, cb, 0:1], scalar2=None, op0=ALU.mult)), reads=[RU, Rcw], writes=[Rt2])
            op("dve", ("scalar_tensor_tensor", dict(out=t2[:], in0=xc[:, c0:c0 + 512], scalar=cw[:, cb, 1:2], in1=t2[:], op0=ALU.mult, op1=ALU.add)), reads=[RU, Rcw, Rt2], writes=[Rt2])
            op("dve", ("scalar_tensor_tensor", dict(out=t2[:], in0=xc[:, c0 + 1:c0 + 513], scalar=cw[:, cb, 2:3], in1=t2[:], op0=ALU.mult, op1=ALU.add)), reads=[RU, Rcw, Rt2], writes=[Rt2])
            op("act", ("activation", dict(out=t3[:], in_=PS_d[:], func=AF.Silu)), reads=[RPd], writes=[Rt3])
            op("dve", ("tensor_tensor", dict(out=t2[:], in0=PS_c[:], in1=t2[:], op=ALU.mult)), reads=[RPc, Rt2], writes=[Rt2])
            op("pool", ("tensor_tensor", dict(out=yo[j % 2][:], in0=t2[:], in1=t3[:], op=ALU.mult)), reads=[Rt2, Rt3], writes=[Ryo[j % 2]])
            S.dma(("dma_start", dict(out=yT[cb, :, 512 * j:512 * (j + 1)], in_=yo[j % 2][:])), reads=[Ryo[j % 2]], writes=[RyT[cb][j]], q="pool")

    PS_c, RPc, PS_d, RPd = PSc0, RPc0, PSd0, RPd0
    t2, Rt2, t3, Rt3 = t2s[0], Rt2s[0], t3s[0], Rt3s[0]
    for hf in range(4):
        S.dma(("dma_start", dict(out=stg[hf % 2][:], in_=wi[:, :, 5, hf * 256:(hf + 1) * 256])), writes=[Rstg[hf % 2]])
        op("pool", ("tensor_copy", dict(out=wbig[:, 0:8, hf * 256:(hf + 1) * 256], in_=stg[hf % 2][:])), reads=[Rstg[hf % 2]], writes=[Rwbig])
    for hf in range(4):
        S.dma(("dma_start", dict(out=stg[hf % 2][:], in_=wi[:, :, 4, hf * 256:(hf + 1) * 256])), writes=[Rstg[hf % 2]])
        op("pool", ("tensor_copy", dict(out=wbig[:, 8:16, hf * 256:(hf + 1) * 256], in_=stg[hf % 2][:])), reads=[Rstg[hf % 2]], writes=[Rwbig])
    for hf in range(4):
        S.dma(("dma_start", dict(out=stg[hf % 2][:], in_=wi[:, :, 6, hf * 256:(hf + 1) * 256])), writes=[Rstg[hf % 2]])
        op("pool", ("tensor_copy", dict(out=wbf[hf // 2][:, :, (hf % 2) * 256:(hf % 2 + 1) * 256], in_=stg[hf % 2][:])), reads=[Rstg[hf % 2]], writes=[Rwbf[hf // 2]])
    wsn = sb("wsn", [128, 8, 128]); wsnb = sb("wsnb", [128, 8, 128], BF16); wsT = sb("wsT", [128, 8, 128], BF16); Rws = R()
    S.dma(("dma_start", dict(out=wsn[:], in_=I["e_sgu_w"].rearrange("g i j -> i g j"))), writes=[Rws])
    op("dve", ("tensor_copy", dict(out=wsnb[:], in_=wsn[:])), reads=[Rws], writes=[Rws])
    for g in range(8):
        op("pe", ("transpose", dict(out=PS_t[:, g, :], in_=wsnb[:, g, :], identity=identb[:])), reads=[Rws, Rid], writes=[RPt])
    op("act", ("activation", dict(out=wsT[:], in_=PS_t[:], func=AF.Copy)), reads=[RPt], writes=[Rws])
    bsB = sb("bsB", [128, 8, 128]); lnG = sb("lnG", [128, 1024]); lnB = sb("lnB", [128, 1024]); Rbc = R()
    S.dma(("dma_start", dict(out=bsB[:].rearrange("p g i -> p (g i)"), in_=I["e_sgu_b"].rearrange("g i -> (g i)").partition_broadcast(128))), writes=[Rbc])
    S.dma(("dma_start", dict(out=lnG[:], in_=I["e_sgu_ln_g"].partition_broadcast(128))), writes=[Rbc])
    S.dma(("dma_start", dict(out=lnB[:], in_=I["e_sgu_ln_b"].partition_broadcast(128))), writes=[Rbc])
    vsb = sb("vsb", [128, 1024]); Rvsb = R()
    vnb = sb("vnb", [128, 1024], BF16); Rvnb = R()
    mixall = U[:, 0:4096].rearrange("p (g t) -> p g t", g=8)
    for tg in range(4):
        for ti in range(4):
            c0 = 1 + 128 * (4 * tg + ti)
            for hf, (P_, RP_) in enumerate((((PS_a, RPa), (PS_b, RPb)) if ti % 2 == 0 else ((PS_c, RPc), (PS_d, RPd)))):
                for k in range(8):
                    mm(P_[:], hnT[:, k, c0:c0 + 128], wbig[:, k, hf * 512:(hf + 1) * 512], [Rwbig] + allh, [RP_], st=(k == 0), sp=(k == 7))
                op("act", ("activation", dict(out=vsb[:, hf * 512:(hf + 1) * 512], in_=P_[:], func=AF.Copy)), reads=[RP_], writes=[Rvsb])
            op("act", ("activation", dict(out=sq[:], in_=vsb[:], func=AF.Square)), reads=[Rvsb], writes=[Rsq])
            op("dve", ("reduce_sum", dict(out=st[:, 0:1], in_=vsb[:], axis=AX.X)), reads=[Rvsb], writes=[Rst])
            op("dve", ("reduce_sum", dict(out=st[:, 1:2], in_=sq[:], axis=AX.X)), reads=[Rsq], writes=[Rst])
            op("dve", ("tensor_scalar", dict(out=st[:, 2:3], in0=st[:, 0:1], scalar1=1.0 / 1024, scalar2=None, op0=ALU.mult)), reads=[Rst], writes=[Rst])
            op("dve", ("tensor_tensor", dict(out=st[:, 3:4], in0=st[:, 2:3], in1=st[:, 2:3], op=ALU.mult)), reads=[Rst], writes=[Rst])
            op("dve", ("scalar_tensor_tensor", dict(out=st[:, 4:5], in0=st[:, 1:2], scalar=1.0 / 1024, in1=st[:, 3:4], op0=ALU.mult, op1=ALU.subtract)), reads=[Rst], writes=[Rst])
            op("dve", ("tensor_scalar", dict(out=st[:, 4:5], in0=st[:, 4:5], scalar1=1e-5, scalar2=None, op0=ALU.add)), reads=[Rst], writes=[Rst])
            op("act", ("activation", dict(out=st[:, 5:6], in_=st[:, 4:5], func=AF.Sqrt)), reads=[Rst], writes=[Rst])
            op("dve", ("reciprocal", dict(out=st[:, 6:7], in_=st[:, 5:6])), reads=[Rst], writes=[Rst])
            op("dve", ("tensor_scalar", dict(out=vsb[:], in0=vsb[:], scalar1=st[:, 2:3], scalar2=st[:, 6:7], op0=ALU.subtract, op1=ALU.mult)), reads=[Rvsb, Rst], writes=[Rvsb])
            op("dve", ("tensor_tensor", dict(out=vsb[:], in0=vsb[:], in1=lnG[:], op=ALU.mult)), reads=[Rvsb, Rbc], writes=[Rvsb])
            op("pool", ("tensor_tensor", dict(out=vnb[:], in0=vsb[:], in1=lnB[:], op=ALU.add)), reads=[Rvsb, Rbc], writes=[Rvnb])
            for g in range(8):
                mm(PS_m[:, g, :], vnb[:, g * 128:(g + 1) * 128], wsT[:, g, :], [Rvnb, Rws], [RPm])
            op("dve", ("tensor_tensor", dict(out=mixall[:, :, ti * 128:(ti + 1) * 128], in0=PS_m[:], in1=bsB[:], op=ALU.add)), reads=[RPm, Rbc], writes=[RU])
        c0 = 1 + 512 * tg
        for g in range(8):
            buf = g % 2

            (PU, RPU), (PZ, RPZ) = ((PS_c, RPc), (PS_d, RPd)) if g % 2 == 0 else ((PS_a, RPa), (PS_b, RPb))
            t2, Rt2 = t2s[g % 2], Rt2s[g % 2]
            t3, Rt3 = t3s[g % 2], Rt3s[g % 2]
            for k in range(8):
                mm(PU[:], wbig[:, 8 + k, g * 128:(g + 1) * 128], hnT[:, k, c0:c0 + 512], [Rwbig] + allh, [RPU], st=(k == 0), sp=(k == 7))
            for k in range(8):
                mm(PZ[:], wbf[g // 4][:, k, (g % 4) * 128:(g % 4 + 1) * 128], hnT[:, k, c0:c0 + 512], [Rwbf[g // 4]] + allh, [RPZ], st=(k == 0), sp=(k == 7))
            op("act", ("activation", dict(out=t3[:], in_=PZ[:], func=AF.Silu)), reads=[RPZ], writes=[Rt3])
            op("dve", ("tensor_tensor", dict(out=t2[:], in0=PU[:], in1=mixall[:, g, :], op=ALU.mult)), reads=[RPU, RU], writes=[Rt2])
            op("pool", ("tensor_tensor", dict(out=yo[g % 2][:], in0=t2[:], in1=t3[:], op=ALU.mult)), reads=[Rt2, Rt3], writes=[Ryo[g % 2]])
            S.dma(("dma_start", dict(out=yT[8 + g, :, 512 * tg:512 * (tg + 1)], in_=yo[g % 2][:])), reads=[Ryo[g % 2]], writes=[RyT[8 + g][tg]], q="pool")

    wo = I["e_w_out"].rearrange("(k p) n -> p k n", p=128)
    for q in range(2):
        for hf in range(4):
            S.dma(("dma_start", dict(out=stg[hf % 2][:], in_=wo[:, 8 * q:8 * q + 8, hf * 256:(hf + 1) * 256])), writes=[Rstg[hf % 2]])
            op("pool", ("tensor_copy", dict(out=wbig[:, 8 * q:8 * q + 8, hf * 256:(hf + 1) * 256], in_=stg[hf % 2][:])), reads=[Rstg[hf % 2]], writes=[Rwbig])
    ally = [r for row in RyT for r in row]
    h1ts = [sb("h1t%d" % i, [128, 1024]) for i in range(2)]; Rh1s = [R(), R()]
    for i in range(16):
        xb, Rx = xt[i % 2], Rxt[i % 2]
        h1t, Rh1 = h1ts[i % 2], Rh1s[i % 2]
        S.dma(("dma_start", dict(out=xb[:], in_=xh[1 + 128 * i: 1 + 128 * (i + 1), :])), writes=[Rx])
        S.dma(("dma_start", dict(out=ytl[i % 2][:], in_=yT[:, :, 128 * i:128 * (i + 1)].rearrange("k p t -> p k t"))), reads=ally, writes=[Rytl[i % 2]])
        for hf, (P_, RP_) in enumerate((((PS_a, RPa), (PS_b, RPb)) if i % 2 == 0 else ((PS_c, RPc), (PS_d, RPd)))):
            for k in range(16):
                mm(P_[:], ytl[i % 2][:, k, :], wbig[:, k, hf * 512:(hf + 1) * 512], [Rwbig, Rytl[i % 2]], [RP_], st=(k == 0), sp=(k == 15))
            op("dve", ("tensor_tensor", dict(out=h1t[:, hf * 512:(hf + 1) * 512], in0=P_[:], in1=xb[:, hf * 512:(hf + 1) * 512], op=ALU.add)), reads=[RP_, Rx], writes=[Rh1])
        S.dma(("dma_start", dict(out=O["h1"][128 * i:128 * (i + 1), :], in_=h1t[:])), reads=[Rh1], q="act")

        def dst(gt, i=i):
            op("dve", ("tensor_tensor", dict(out=hnT[:, :, 1 + 128 * i: 1 + 128 * (i + 1)], in0=PS_t[:], in1=gt[:].to_broadcast([128, 8, 128]), op=ALU.mult)), reads=[RPt, Rg], writes=[RhnT[i]])
        norm_tile(h1t, Rh1, gO, dst, None)

    wi1 = I["o_w_in"].rearrange("(k p) n -> p k n", p=128)
    blocks = [(c * 128, c * 128, False) for c in range(25)]
    blocks += [(3200 + c * 128, 3200 + c * 128, True) for c in range(8)]
    blocks += [(4736 + c * 128, 4224 + c * 128, True) for c in range(4)]
    ob = [sb("ob%d" % i, [128, 512]) for i in range(2)]; Rob = [R(), R()]
    oi = 0
    toks = []
    for bi, (sc, dr, act) in enumerate(blocks):
        buf = bi % 2
        S.dma(("dma_start", dict(out=stg[buf][:, :, 0:128], in_=wi1[:, :, sc:sc + 128])), writes=[Rstg[buf]])
        op("pool", ("tensor_copy", dict(out=wbf[buf][:, :, 0:128], in_=stg[buf][:, :, 0:128])), reads=[Rstg[buf]], writes=[Rwbf[buf]])
        for j in range(4):
            P_, RP_ = ((PS_a, RPa), (PS_b, RPb), (PS_c, RPc), (PS_d, RPd))[j]
            c0 = 1 + 512 * j
            for k in range(8):
                mm(P_[:], wbf[buf][:, k, 0:128], hnT[:, k, c0:c0 + 512], [Rwbf[buf]] + allh, [RP_], st=(k == 0), sp=(k == 7))
            o_, Ro = ob[oi % 2], Rob[oi % 2]
            oi += 1
            op("act", ("activation", dict(out=o_[:], in_=P_[:], func=(AF.Silu if act else AF.Copy))), reads=[RP_], writes=[Ro])
            toks.append(S.dma(("dma_start", dict(out=O["pT"][dr:dr + 128, 512 * j:512 * (j + 1)], in_=o_[:])), reads=[Ro], q="act"))
    for hf in range(2):
        S.dma(("dma_start", dict(out=stg[hf][:], in_=wi1[:, :, 4224 + 256 * hf:4224 + 256 * (hf + 1)])), writes=[Rstg[hf]])
        op("pool", ("tensor_copy", dict(out=wbf[0][:, :, 256 * hf:256 * (hf + 1)], in_=stg[hf][:])), reads=[Rstg[hf]], writes=[Rwbf[0]])
    for i in range(16):
        c0 = 1 + 128 * i
        P_, RP_ = ((PS_a, RPa), (PS_b, RPb))[i % 2]
        for k in range(8):
            mm(P_[:], hnT[:, k, c0:c0 + 128], wbf[0][:, k, :], [Rwbf[0]] + allh, [RP_], st=(k == 0), sp=(k == 7))
        o_, Ro = ob[oi % 2], Rob[oi % 2]
        oi += 1
        op("act", ("activation", dict(out=o_[:], in_=P_[:], func=AF.Copy)), reads=[RP_], writes=[Ro])
        toks.append(S.dma(("dma_start", dict(out=O["fd"][128 * i:128 * (i + 1), :], in_=o_[:])), reads=[Ro], q="act"))
    return toks


def full_barrier(S):
    keys = list(S.cnt.items())
    for e in S.ENGS:
        waits = []
        for k, v in keys:
            if k == e:
                continue
            if S.seen[e].get(k, 0) < v:
                S.seen[e][k] = v
                waits.append((k, v))
        if waits:
            S.prog[e].append([waits, None, ("_none", 0)])


def emit_fnet(S, nc, I, ydT):
    R = Region
    op = S.op
    mm = lambda out, l, r_, rd, wr, st=True, sp=True: op("pe", ("matmul", dict(out=out, lhsT=l, rhs=r_, start=st, stop=sp)), reads=rd, writes=wr)
    toks = []
    with ExitStack() as es:
        sb = lambda name, shape, dt=F32: es.enter_context(nc.sbuf_tensor(name, shape, dt))
        ps = lambda name, shape, dt=F32: es.enter_context(nc.psum_tensor(name, shape, dt))
        xs = sb("f_xs", [128, 4096]); Rxs = R()
        xb = sb("f_xb", [128, 64, 128], BF16); Rxb = R()
        Fb = sb("f_F", [128, 256], BF16); RF = R()
        A_sb = sb("f_A", [64, 128, 256], BF16); RA = R()
        PQ = sb("f_PQ", [128, 2, 64, 128], BF16); RPQ = R()
        Tg = [[sb("f_T%d%d" % (i, j), [64, 16, 128], BF16) for j in range(2)] for i in range(2)]; RTg = [R(), R()]
        wf32 = sb("f_w32", [128, 128]); wfb = sb("f_wb", [128, 128], BF16); Rwf = R()
        Ccb = sb("f_Cc", [128, 128], BF16); mScb = sb("f_mSc", [128, 128], BF16); Rcs = R()
        Gb = sb("f_G", [128, 256], BF16); RG = R()
        ob = [sb("f_ob%d" % i, [128, 512]) for i in range(2)]; Rob = [R(), R()]
        PS = [ps("f_ps%d" % i, [128, 512]) for i in range(2)]; RPS = [R(), R()]
        S.dma(("dma_start", dict(out=Fb[:], in_=I["c_F"])), writes=[RF])
        S.dma(("dma_start", dict(out=Ccb[:], in_=I["c_Cc"])), writes=[Rcs])
        S.dma(("dma_start", dict(out=mScb[:], in_=I["c_mSc"])), writes=[Rcs])
        S.dma(("dma_start", dict(out=wf32[:], in_=I["fw"])), writes=[Rwf])
        op("dve", ("tensor_copy", dict(out=wfb[:], in_=wf32[:])), reads=[Rwf], writes=[Rwf])
        xbf = xb[:].rearrange("p l c -> p (l c)")
        for hf in range(2):
            S.dma(("dma_start", dict(out=xs[:], in_=I["fx"][:, hf * 4096:(hf + 1) * 4096])), writes=[Rxs])
            op("pool", ("tensor_copy", dict(out=xbf[:, hf * 4096:(hf + 1) * 4096], in_=xs[:])), reads=[Rxs], writes=[Rxb])
        for c2 in range(64):
            P_, RP_ = PS[c2 % 2], RPS[c2 % 2]
            for j in range(2):
                mm(P_[0:64, j * 256:(j + 1) * 256], xb[:, :, 2 * c2 + j], Fb[:], [Rxb, RF], [RP_])
            op("act" if c2 % 2 == 0 else "dve", ("activation", dict(out=A_sb[0:64, 2 * c2:2 * c2 + 2, :], in_=P_[0:64, :].rearrange("p (j k) -> p j k", j=2), func=AF.Copy)) if c2 % 2 == 0 else
               ("tensor_copy", dict(out=A_sb[0:64, 2 * c2:2 * c2 + 2, :], in_=P_[0:64, :].rearrange("p (j k) -> p j k", j=2))), reads=[RP_], writes=[RA])
        T1d = I["c_T1"].rearrange("p (k h) -> p k h", h=128)
        T2d = I["c_T2"].rearrange("p (k h) -> p k h", h=128)
        ei = 0
        for grp in range(8):
            tb = grp % 2
            S.dma(("dma_start", dict(out=Tg[tb][0][:], in_=T1d[:, grp * 16:(grp + 1) * 16, :])), writes=[RTg[tb]])
            S.dma(("dma_start", dict(out=Tg[tb][1][:], in_=T2d[:, grp * 16:(grp + 1) * 16, :])), writes=[RTg[tb]])
            for q in range(4):
                P_, RP_ = PS[ei % 2], RPS[ei % 2]
                for j in range(4):
                    kk_ = q * 4 + j
                    kl = grp * 16 + kk_
                    mm(P_[:, j * 128:(j + 1) * 128], A_sb[0:64, :, kl], Tg[tb][0][0:64, kk_, :], [RA, RTg[tb]], [RP_], st=True, sp=False)
                    mm(P_[:, j * 128:(j + 1) * 128], A_sb[0:64, :, 128 + kl], Tg[tb][1][0:64, kk_, :], [RA, RTg[tb]], [RP_], st=False, sp=True)
                kl0 = grp * 16 + q * 4
                for qq in range(2):
                    op("act" if qq == 0 else "dve",
                       ("activation", dict(out=PQ[:, qq, :, kl0:kl0 + 4].rearrange("p h l -> p l h"), in_=P_[:].rearrange("p (l q h) -> p l q h", l=4, q=2)[:, :, qq, :], func=AF.Copy)) if qq == 0 else
                       ("tensor_copy", dict(out=PQ[:, qq, :, kl0:kl0 + 4].rearrange("p h l -> p l h"), in_=P_[:].rearrange("p (l q h) -> p l q h", l=4, q=2)[:, :, qq, :])),
                       reads=[RP_], writes=[RPQ])
                ei += 1
        P_, RP_ = PS[0], RPS[0]
        mm(P_[:, 0:128], Ccb[:], wfb[:], [Rcs, Rwf], [RP_])
        mm(P_[:, 128:256], mScb[:], wfb[:], [Rcs, Rwf], [RP_])
        op("act", ("activation", dict(out=Gb[:], in_=P_[:, 0:256], func=AF.Copy)), reads=[RP_], writes=[RG])
        for t4 in range(16):
            P_, RP_ = PS[(t4 + 1) % 2], RPS[(t4 + 1) % 2]
            for j in range(4):
                kh = 4 * t4 + j
                mm(P_[:, j * 128:(j + 1) * 128], Gb[:, 0:128], PQ[:, 0, kh, :], [RG, RPQ], [RP_], st=True, sp=False)
                mm(P_[:, j * 128:(j + 1) * 128], Gb[:, 128:256], PQ[:, 1, kh, :], [RG, RPQ], [RP_], st=False, sp=True)
            o_, Ro = ob[t4 % 2], Rob[t4 % 2]
            op("act", ("activation", dict(out=o_[:], in_=P_[:], func=AF.Copy)), reads=[RP_], writes=[Ro])
            toks.append(S.dma(("dma_start", dict(out=ydT[:, 512 * t4:512 * (t4 + 1)], in_=o_[:])), reads=[Ro]))
    full_barrier(S)
    return toks


def emit_p3(S, nc, I, yout):
    sb = lambda name, shape, dt=F32: nc.alloc_sbuf_tensor(name, shape, dt)
    ps = lambda name, shape, dt=F32: nc.alloc_psum_tensor(name, shape, dt)
    R = Region
    op = S.op
    mm = lambda out, l, r_, rd, wr, st=True, sp=True: op("pe", ("matmul", dict(out=out, lhsT=l, rhs=r_, start=st, stop=sp)), reads=rd, writes=wr)
    stg = [sb("stg%d" % i, [128, 8, 256]) for i in range(2)]; Rstg = [R(), R()]
    wO = sb("wO", [128, 12, 1024], BF16); RwO = R()
    gN = sb("gN", [128, 1024]); RgN = R()
    gt_all = sb("gt_all", [128, 12, 2048], BF16); Rgt = R()
    ya = [sb("ya%d" % i, [128, 512]) for i in range(2)]; Rya = [R(), R()]
    ga = [sb("ga%d" % i, [128, 512]) for i in range(2)]; Rga = [R(), R()]
    h1t = [sb("h1t%d" % i, [128, 1024]) for i in range(2)]; Rh1 = [R(), R()]
    h2 = sb("h2", [128, 1024]); Rh2 = R()
    sq = sb("sq", [128, 1024]); Rsq = R()
    st = sb("st", [128, 8]); Rst = R()
    yo = [sb("yo%d" % i, [128, 1024]) for i in range(2)]; Ryo = [R(), R()]
    PS_a = ps("PS_a", [128, 512]); RPa = R()
    PS_b = ps("PS_b", [128, 512]); RPb = R()
    wo3 = I["o_w_out"].rearrange("(k p) n -> p k n", p=128)
    si = 0
    for (k0, nk) in ((0, 8), (8, 4)):
        for cq in range(4):
            b_ = si % 2; si += 1
            S.dma(("dma_start", dict(out=stg[b_][:, 0:nk, :], in_=wo3[:, k0:k0 + nk, cq * 256:(cq + 1) * 256])), writes=[Rstg[b_]])
            op("pool", ("tensor_copy", dict(out=wO[:, k0:k0 + nk, cq * 256:(cq + 1) * 256], in_=stg[b_][:, 0:nk, :])), reads=[Rstg[b_]], writes=[RwO])
    S.dma(("dma_start", dict(out=gN[:], in_=I["final_norm_g"].partition_broadcast(128))), writes=[RgN])
    ii = 0
    for blk in range(12):
        src = I["ycT"][blk * 128:(blk + 1) * 128] if blk < 8 else I["ydT"][(blk - 8) * 128:(blk - 7) * 128]
        gsrc = I["gT"][blk * 128:(blk + 1) * 128]
        for j in range(4):
            b_ = ii % 2; ii += 1
            S.dma(("dma_start", dict(out=ya[b_][:], in_=src[:, 512 * j:512 * (j + 1)])), writes=[Rya[b_]])
            S.dma(("dma_start", dict(out=ga[b_][:], in_=gsrc[:, 512 * j:512 * (j + 1)])), writes=[Rga[b_]])
            op("dve" if ii % 2 else "pool", ("tensor_tensor", dict(out=gt_all[:, blk, 512 * j:512 * (j + 1)], in0=ya[b_][:], in1=ga[b_][:], op=ALU.mult)), reads=[Rya[b_], Rga[b_]], writes=[Rgt])
    toks = []
    for i in range(16):
        hb, Rh = h1t[i % 2], Rh1[i % 2]
        S.dma(("dma_start", dict(out=hb[:], in_=I["h1"][128 * i:128 * (i + 1), :])), writes=[Rh])
        for hf, (P_, RP_) in enumerate(((PS_a, RPa), (PS_b, RPb))):
            for k in range(12):
                mm(P_[:], gt_all[:, k, 128 * i:128 * (i + 1)], wO[:, k, hf * 512:(hf + 1) * 512], [Rgt, RwO], [RP_], st=(k == 0), sp=(k == 11))
            op("dve", ("tensor_tensor", dict(out=h2[:, hf * 512:(hf + 1) * 512], in0=P_[:], in1=hb[:, hf * 512:(hf + 1) * 512], op=ALU.add)), reads=[RP_, Rh], writes=[Rh2])
        op("act", ("activation", dict(out=sq[:], in_=h2[:], func=AF.Square)), reads=[Rh2], writes=[Rsq])
        op("dve", ("reduce_sum", dict(out=st[:, 0:1], in_=sq[:], axis=AX.X)), reads=[Rsq], writes=[Rst])
        op("dve", ("tensor_scalar", dict(out=st[:, 1:2], in0=st[:, 0:1], scalar1=1.0 / 1024, scalar2=1e-6, op0=ALU.mult, op1=ALU.add)), reads=[Rst], writes=[Rst])
        op("act", ("activation", dict(out=st[:, 2:3], in_=st[:, 1:2], func=AF.Sqrt)), reads=[Rst], writes=[Rst])
        op("dve", ("reciprocal", dict(out=st[:, 3:4], in_=st[:, 2:3])), reads=[Rst], writes=[Rst])
        op("dve", ("tensor_scalar", dict(out=h2[:], in0=h2[:], scalar1=st[:, 3:4], scalar2=None, op0=ALU.mult)), reads=[Rh2, Rst], writes=[Rh2])
        o_, Ro = yo[i % 2], Ryo[i % 2]
        op("pool", ("tensor_tensor", dict(out=o_[:], in0=h2[:], in1=gN[:], op=ALU.mult)), reads=[Rh2, RgN], writes=[Ro])
        toks.append(S.dma(("dma_start", dict(out=yout[128 * i:128 * (i + 1), :], in_=o_[:])), reads=[Ro], q="pool"))
    return toks


def _mk(nc, name, shape, dt=None, out=False):
    return nc.dram_tensor(name, list(shape), dt or F32, kind=("ExternalOutput" if out else "ExternalInput")).ap()


W1 = ["e_norm_g", "e_w_in", "e_conv_w", "e_sgu_ln_g", "e_sgu_ln_b", "e_sgu_w", "e_sgu_b", "e_w_out", "o_norm_g", "o_w_in"]


def build_l1(shapes):
    nc = bass.Bass("TRN2", target_bir_lowering=False)
    I = {"xh": _mk(nc, "xh", [2050, 1024]), "c_ident": _mk(nc, "c_ident", [128, 128])}
    for n in W1:
        I[n] = _mk(nc, n, shapes[n])
    O = {"h1": _mk(nc, "h1", [2048, 1024], out=True), "pT": _mk(nc, "pT", [4736, 2048], out=True),
         "fd": _mk(nc, "fd", [2048, 512], out=True)}
    S = Sched(nc)
    toks = emit_p1(S, nc, I, O)
    S.barrier_on("sp", toks)
    S.finalize()
    return nc


def build_l2(consts):
    NB, T = 2, 8192
    nc = bass.Bass("TRN2", target_bir_lowering=False)
    pr, pk, pv, pwa = (_mk(nc, n, [128, NB, T + 2]) for n in ("pr", "pk", "pv", "pwa"))
    prm = _mk(nc, "prm", [128, 17]); w2a2 = _mk(nc, "w2a2", [128, 2, 128])
    A = {k: _mk(nc, k, v.shape) for k, v in consts.items()}
    FI = {"fx": _mk(nc, "fx", [128, 8192]), "fw": _mk(nc, "fw", [128, 128]),
          "c_F": _mk(nc, "c_F", [128, 256], BF16), "c_T1": _mk(nc, "c_T1", [64, 16384], BF16),
          "c_T2": _mk(nc, "c_T2", [64, 16384], BF16), "c_Cc": _mk(nc, "c_Cc", [128, 128], BF16),
          "c_mSc": _mk(nc, "c_mSc", [128, 128], BF16)}
    yout = _mk(nc, "yout", [128, NB, T], out=True)
    ydT = _mk(nc, "ydT", [128, T], out=True)
    S = Sched(nc)
    toks = emit_fnet(S, nc, FI, ydT)
    toks += emit_rwkv(S, nc, A, pr, pk, pv, pwa, prm, w2a2, yout, NB, T)
    S.barrier_on("sp", toks)
    S.finalize()
    return nc


def build_l3():
    nc = bass.Bass("TRN2", target_bir_lowering=False)
    I = {"ycT": _mk(nc, "ycT", [1024, 2048]), "ydT": _mk(nc, "ydT", [512, 2048]), "gT": _mk(nc, "gT", [1536, 2048]),
         "h1": _mk(nc, "h1", [2048, 1024]), "o_w_out": _mk(nc, "o_w_out", [1536, 1024]),
         "final_norm_g": _mk(nc, "final_norm_g", [1024])}
    y = _mk(nc, "y", [2048, 1024], out=True)
    S = Sched(nc)
    toks = emit_p3(S, nc, I, y)
    S.barrier_on("sp", toks)
    S.finalize()
    return nc


def fnet_tables():
    import ml_dtypes
    N = 8192
    nh = np.arange(128); kl = np.arange(128)
    ang = 2 * np.pi * np.outer(nh, kl) / 128
    F = np.concatenate([np.cos(ang), np.sin(ang)], axis=1)
    nl = np.arange(64)[:, None, None]; klo = np.arange(128)[None, :, None]; kh = np.arange(64)[None, None, :]
    beta = 2 * np.pi * ((nl * (klo + 128 * kh)) % N) / N
    T1 = np.concatenate([np.cos(beta), np.sin(beta)], axis=2).reshape(64, 16384)
    T2 = np.concatenate([-np.sin(beta), np.cos(beta)], axis=2).reshape(64, 16384)
    c = np.arange(128); phi = 2 * np.pi * np.outer(c, c) / 128
    nrm = 1 / np.sqrt(N * 128)
    bf = lambda a: np.ascontiguousarray(a.astype(np.float32)).astype(ml_dtypes.bfloat16)
    return {"c_F": bf(F), "c_T1": bf(T1), "c_T2": bf(T2), "c_Cc": bf(np.cos(phi) * nrm), "c_mSc": bf(-np.sin(phi) * nrm)}


def kernel(**inputs):
    f32 = lambda a: np.ascontiguousarray(np.asarray(a), dtype=np.float32)
    inp = {k: f32(v) for k, v in inputs.items()}
    x = inp["x"]
    ncores = 8
    cores = list(range(ncores))
    w1 = {n: np.ascontiguousarray(inp[n][0]) for n in W1}
    ident = np.eye(128, dtype=np.float32)
    maps = []
    for c in cores:
        b, s0 = c // 4, (c % 4) * 2048
        xh = np.zeros((2050, 1024), np.float32)
        xh[1:2049] = x[b, s0:s0 + 2048]
        if s0 > 0:
            xh[0] = x[b, s0 - 1]
        if s0 + 2048 < 8192:
            xh[2049] = x[b, s0 + 2048]
        m = {"xh": xh, "c_ident": ident}
        m.update(w1)
        maps.append(m)
    nc1 = build_l1({n: w1[n].shape for n in W1})
    r1 = run_bass_kernel_spmd(nc1, maps, core_ids=cores).results
    PT = np.concatenate([np.asarray(r["pT"]) for r in r1], axis=1)
    FD = np.concatenate([np.asarray(r["fd"]) for r in r1], axis=0)
    consts = build_consts_np()
    ft = fnet_tables()
    mu, w0, w2, a0, a2 = inp["o_mu"][0], inp["o_w0"][0], inp["o_w2"][0], inp["o_a0"][0], inp["o_a2"][0]
    k_k, k_a, r_k = inp["o_k_k"][0], inp["o_k_a"][0], inp["o_r_k"][0].reshape(-1)
    lg, lb = inp["o_lnx_g"][0], inp["o_lnx_b"][0]
    PT3 = PT.reshape(4736, 2, 8192)
    pad = lambda a: np.ascontiguousarray(np.pad(a, ((0, 0), (0, 0), (1, 1))))
    maps = []
    for c in cores:
        ch = slice(c * 128, (c + 1) * 128)
        m = {"pr": pad(PT3[0:1024][ch]), "pk": pad(PT3[1024:2048][ch]), "pv": pad(PT3[2048:3072][ch]),
             "pwa": pad(PT3[3072:3200])}
        prm = np.zeros((128, 17), np.float32)
        for d in range(2):
            prm[:, 0 + d] = mu[d, 0:1024][ch]; prm[:, 2 + d] = mu[d, 1024:2048][ch]; prm[:, 4 + d] = mu[d, 2048:3072][ch]
            prm[:, 6 + d] = mu[d, 3072:3200]; prm[:, 8 + d] = w0[d][ch]; prm[:, 10 + d] = a0[d][ch]
        prm[:, 12] = k_k[ch]; prm[:, 13] = k_a[ch]; prm[:, 14] = r_k[ch]; prm[:, 15] = lg[ch]; prm[:, 16] = lb[ch]
        m["prm"] = prm
        m["w2a2"] = np.ascontiguousarray(np.concatenate([w2[:, :, ch], a2[:, :, ch]], axis=1).transpose(1, 0, 2))
        m.update(consts)
        b, g = c // 4, c % 4
        m["fx"] = np.ascontiguousarray(FD[b * 8192:(b + 1) * 8192, g * 128:(g + 1) * 128]).reshape(128, 8192)
        m["fw"] = np.ascontiguousarray(inp["o_fnet_w"][0, g])
        m.update(ft)
        maps.append(m)
    nc2 = build_l2(consts)
    r2 = run_bass_kernel_spmd(nc2, maps, core_ids=cores).results
    YC = np.concatenate([np.asarray(r["yout"]).reshape(128, 16384) for r in r2], axis=0)
    YD = np.concatenate([np.concatenate([np.asarray(r2[b * 4 + g]["ydT"]) for g in range(4)], axis=0) for b in range(2)], axis=1)
    maps = []
    for c in cores:
        ts = slice(c * 2048, (c + 1) * 2048)
        maps.append({"ycT": np.ascontiguousarray(YC[:, ts]), "ydT": np.ascontiguousarray(YD[:, ts]),
                     "gT": np.ascontiguousarray(PT[3200:4736, ts]), "h1": np.asarray(r1[c]["h1"]),
                     "o_w_out": np.ascontiguousarray(inp["o_w_out"][0]), "final_norm_g": inp["final_norm_g"]})
    nc3 = build_l3()
    r3 = run_bass_kernel_spmd(nc3, maps, core_ids=cores).results
    y = np.concatenate([np.asarray(r["y"]) for r in r3], axis=0).reshape(2, 8192, 1024)
    return y.astype(np.float32)
```

```python
from contextlib import ExitStack
import itertools
import numpy as np
import concourse.bass as bass
import concourse.mybir as mybir
from concourse.bass_utils import run_bass_kernel_spmd


F32 = mybir.dt.float32
BF16 = mybir.dt.bfloat16
AF = mybir.ActivationFunctionType
ALU = mybir.AluOpType
AX = mybir.AxisListType

N_DMA_SEMS = 8


class Region:
    __slots__ = ("w", "r", "name")

    def __init__(self, name=""):
        self.w = None
        self.r = {}
        self.name = name


class Sched:
    ENGS = ("pe", "dve", "act", "pool", "sp")

    def __init__(self, nc):
        self.nc = nc
        self.prog = {e: [] for e in self.ENGS}
        self.cnt = {}
        self.seen = {e: {} for e in self.ENGS}
        self.dma_rr = {e: 0 for e in self.ENGS}
        self.dma_last = {}
        self.same_engine_raw = True
        self.cut = 0
        self.nrec = 0
        self.log = []

    def _collect(self, eng, mykey, reads, writes):
        waits = {}

        def need(tok, kind):
            if tok is None:
                return
            k, v = tok
            if k == mykey:
                if eng == "pe":
                    return
                if not self.same_engine_raw:
                    return
            if waits.get(k, 0) < v:
                waits[k] = v

        for R in reads:
            need(R.w, "raw")
        for R in writes:
            need(R.w, "waw")
            for k, v in R.r.items():
                need((k, v), "war")
        out = []
        seen = self.seen[eng]
        for k, v in waits.items():
            if seen.get(k, 0) < v:
                seen[k] = v
                out.append((k, v))
        return out

    def _commit(self, tok, reads, writes):
        for R in writes:
            R.w = tok
            R.r = {}
        k, v = tok
        for R in reads:
            if R.r.get(k, 0) < v:
                R.r[k] = v

    def op(self, eng, fn, reads=(), writes=()):
        self.nrec += 1
        if self.cut and self.nrec > self.cut:
            return None
        if self.cut:
            self.log.append((self.nrec, eng, fn[0] if isinstance(fn, tuple) else "fn", str(fn[1].get("out", ""))[:120] if isinstance(fn, tuple) else ""))
        key = eng
        waits = self._collect(eng, key, reads, writes)
        idx = self.cnt.get(key, 0) + 1
        self.cnt[key] = idx
        tok = (key, idx)
        self.prog[eng].append([waits, fn, tok])
        self._commit(tok, reads, writes)
        return tok

    def dma(self, fn, reads=(), writes=(), q="sp"):
        self.nrec += 1
        if self.cut and self.nrec > self.cut:
            return None
        i = self.dma_rr[q]
        self.dma_rr[q] = (i + 1) % N_DMA_SEMS
        key = "dma_%s_%d" % (q, i)
        waits = self._collect(q, key, reads, writes)
        prev = self.cnt.get(key, 0)
        if prev > 0 and self.seen[q].get(key, 0) < prev:
            self.seen[q][key] = prev
            waits.append((key, prev))
        idx = prev + 1
        self.cnt[key] = idx
        tok = (key, idx)
        self.prog[q].append([waits, fn, tok])
        self._commit(tok, reads, writes)
        return tok

    def finalize(self):
        nc = self.nc
        waited = {}
        for e in self.ENGS:
            for waits, fn, tok in self.prog[e]:
                for k, v in waits:
                    waited.setdefault(k, set()).add(v)
        self.final_waits = []
        sem_of = {}
        val_of = {}
        for k, s in waited.items():
            sem_of[k] = nc.alloc_semaphore("s_" + k)
            isdma = k.startswith("dma_")
            step = 16 if isdma else 1
            if isdma:
                val_of[k] = None
            else:
                val_of[k] = {v: (i + 1) for i, v in enumerate(sorted(s))}
        engobj = {"pe": nc.tensor, "dve": nc.vector, "act": nc.scalar,
                  "pool": nc.gpsimd, "sp": nc.sync}

        def value(k, v):
            if val_of[k] is None:
                return 16 * v
            return val_of[k][v]

        def emit(e):
            def body(eng):
                for waits, fn, tok in self.prog[e]:
                    for k, v in waits:
                        eng.wait_ge(sem_of[k], value(k, v))
                    if fn is None:
                        continue
                    if isinstance(fn, tuple):
                        ins = getattr(eng, fn[0])(**fn[1])
                    else:
                        ins = fn(eng)
                    k, v = tok
                    if k in sem_of:
                        if val_of[k] is None:
                            ins.then_inc(sem_of[k], 16)
                        elif v in val_of[k]:
                            ins.then_inc(sem_of[k], 1)
            return body

        with nc.Block() as block:
            for e, dec in (("sp", block.sync), ("pe", block.tensor), ("dve", block.vector),
                           ("act", block.scalar), ("pool", block.gpsimd)):
                if self.prog[e]:
                    dec(emit(e))
        self.n_sems = len(sem_of)
        return self.n_sems

    def barrier_on(self, eng, toks):
        waits = []
        for tk in toks:
            if tk is None:
                continue
            k, v = tk
            if self.seen[eng].get(k, 0) < v:
                self.seen[eng][k] = v
                waits.append((k, v))
        if waits:
            self.prog[eng].append([waits, None, ("_none", 0)])


C = 128
BLK = 512
NEG_E = -float(np.exp(-0.5))
GN_EPS = 64e-5


def build_consts_np():
    idx = np.arange(128)
    lt = (idx[:, None] < idx[None, :]).astype(np.float32)
    le = (idx[:, None] <= idx[None, :]).astype(np.float32)
    gt = lt.T.copy()
    ge = le.T.copy()
    m4f = np.stack([lt, gt, gt, le], axis=1)
    m4b = np.stack([gt, lt, lt, ge], axis=1)
    mk = np.stack([le, ge], axis=1)
    ident = np.eye(128, dtype=np.float32)
    bd = np.kron(np.eye(2, dtype=np.float32), np.ones((64, 64), np.float32))
    scanm = np.ones((128, BLK), np.float32)
    scanm[:, ::C] = 0.0
    return {"c_m4": np.stack([m4f, m4b], axis=1).reshape(128, 2 * 4 * 128).copy(),
            "c_mk": mk.reshape(128, 256).copy(), "c_ident": ident, "c_bd": bd, "c_scanm": scanm}


XST = False


def emit_rwkv(S, nc, A, pr, pk, pv, pwa, prm, w2a2, yout, NB, T):
    sb = lambda name, shape, dt=F32: nc.alloc_sbuf_tensor(name, shape, dt)
    ps = lambda name, shape, dt=F32: nc.alloc_psum_tensor(name, shape, dt)
    R = Region
    nblk = T // BLK

    m4f = sb("m4f", [128, 2, 4, 128]); Rm4 = R()
    mkf = sb("mkf", [128, 2, 128]); Rmk = R()
    identf = sb("identf", [128, 128]); Ridf = R()
    identb = sb("identb", [128, 128], BF16); Ridb = R()
    bdf = sb("bdf", [128, 128]); Rbd = R()
    bdr = sb("bdr", [128, 128]); Rbdr = R()
    bdm = sb("bdm", [128, 128]); Rbdm = R()
    scanm = sb("scanm", [128, BLK]); Rsc = R()
    prmt = sb("prmt", [128, 17]); Rprm = R()
    w2f = sb("w2f", [128, 2, 128]); Rw2f = R()
    w2b = sb("w2b", [128, 2, 128], BF16); Rw2b = R()
    S.dma(("dma_start", dict(out=m4f[:].rearrange("p a b c -> p (a b c)"), in_=A["c_m4"])), writes=[Rm4])
    S.dma(("dma_start", dict(out=mkf[:].rearrange("p a c -> p (a c)"), in_=A["c_mk"])), writes=[Rmk])
    S.dma(("dma_start", dict(out=identf[:], in_=A["c_ident"])), writes=[Ridf])
    S.dma(("dma_start", dict(out=bdf[:], in_=A["c_bd"])), writes=[Rbd])
    S.dma(("dma_start", dict(out=scanm[:], in_=A["c_scanm"])), writes=[Rsc])
    S.dma(("dma_start", dict(out=prmt[:], in_=prm)), writes=[Rprm])
    S.dma(("dma_start", dict(out=w2f[:], in_=w2a2)), writes=[Rw2f])
    S.op("dve", ("tensor_copy", dict(out=identb[:], in_=identf[:])), reads=[Ridf], writes=[Ridb])
    S.op("dve", ("tensor_copy", dict(out=w2b[:], in_=w2f[:])), reads=[Rw2f], writes=[Rw2b])
    PM = lambda c: prmt[:, c:c + 1]
    S.op("dve", ("tensor_scalar", dict(out=bdr[:], in0=bdf[:], scalar1=PM(14), scalar2=None, op0=ALU.mult)), reads=[Rbd, Rprm], writes=[Rbdr])
    S.op("dve", ("tensor_scalar", dict(out=bdm[:], in0=bdf[:], scalar1=1.0 / 64, scalar2=None, op0=ALU.mult)), reads=[Rbd], writes=[Rbdm])

    def T2(name, dt=F32, n=BLK):
        return sb(name, [128, n], dt), R()
    ld = {}
    for nm in ("pr", "pk", "pv", "pwa"):
        ld[nm] = (sb("ld_" + nm, [128, BLK + 2]), R())
    tmp, Rtmp = T2("tmp")
    qr, Rqr = T2("qr"); qk, Rqk = T2("qk"); qv, Rqv = T2("qv"); qwa, Rqwa = T2("qwa")
    twa, Rtwa = T2("twa", BF16)
    sw, Rsw = T2("sw"); asg, Rasg = T2("asg")
    logw, Rlogw = T2("logw"); lin, Rlin = T2("lin"); linm, Rlinm = T2("linm"); lexm, Rlexm = T2("lexm")
    lex, Rlex = T2("lex"); lint, Rlint = T2("lint")
    e1, Re1 = T2("e1"); e1x, Re1x = T2("e1x"); e2, Re2 = T2("e2"); e3S = [sb("e3%d" % i, [128, BLK]) for i in range(2)]; Re3S = [R(), R()]; e3x, Re3x = T2("e3x"); e4, Re4 = T2("e4")
    kk, Rkk = T2("kk"); kk2, Rkk2 = T2("kk2"); rin, Rrin = T2("rin"); kkn, Rkkn = T2("kkn")
    kp, Rkp = T2("kp"); bv, Rbv = T2("bv"); rk, Rrk = T2("rk")
    rtS = [sb("rt%d" % i, [128, BLK], BF16) for i in range(2)]; RrtS = [R(), R()]; atS = [sb("at%d" % i, [128, BLK], BF16) for i in range(2)]; RatS = [R(), R()]; ktS = [sb("kt%d" % i, [128, BLK], BF16) for i in range(2)]; RktS = [R(), R()]; btS = [sb("bt%d" % i, [128, BLK], BF16) for i in range(2)]; RbtS = [R(), R()]
    r0S = [sb("r0%d" % i, [128, BLK]) for i in range(2)]; Rr0S = [R(), R()]; a0bS = [sb("a0b%d" % i, [128, BLK], BF16) for i in range(2)]; Ra0bS = [R(), R()]; kEbS = [sb("kEb%d" % i, [128, BLK], BF16) for i in range(2)]; RkEbS = [R(), R()]; bEbS = [sb("bEb%d" % i, [128, BLK], BF16) for i in range(2)]; RbEbS = [R(), R()]
    qvbS = [sb("qvb%d" % i, [128, BLK], BF16) for i in range(2)]; RqvbS = [R(), R()]
    ysum = sb("ysum", [128, T]); Rys = [R() for _ in range(T // C)]
    bsum = sb("bsum", [128, T]); Rbs = [R() for _ in range(nblk)]
    TT = [sb("TT%d" % i, [128, 4, 128], BF16) for i in range(2)]; RTT = [R(), R()]
    SBM = [sb("SBM%d" % i, [128, 2, 4, 128], BF16) for i in range(2)]; RSBM = [R(), R()]
    MKR = [sb("MKR%d" % i, [128, 2, 128]) for i in range(2)]; RMKR = [R(), R()]
    SX = [sb("SX%d" % i, [128, 2, 192], BF16) for i in range(2)]; RSX = [R(), R()]
    SAB = [sb("SAB%d" % i, [128, 2, 2, 128], BF16) for i in range(2)]; RSAB = [R(), R()]
    Gb = sb("Gb", [128, 128], BF16); RGb = R()
    Hb = sb("Hb", [128, 2, 128], BF16); RHb = R()
    Pb = sb("Pb", [128, 64], BF16); RPb = R()
    Zb = sb("Zb", [128, 2, 64], BF16); RZb = R()
    STz = [sb("STz%d" % h, [128, 64], BF16) for h in range(2)]; RST = [R(), R()]
    identP = sb("identP", [128, 64]); mkb = sb("mkb", [128, 2, 2, 128])
    HS = [slice(0, 64), slice(64, 128)]
    fin1, Rfin1 = T2("fin1"); fin2, Rfin2 = T2("fin2"); fin3, Rfin3 = T2("fin3")

    PS_M = ps("PS_M", [128, 2, 4, 128]); RPS_M = R()
    PS_K = ps("PS_K", [128, 512]); RPS_K = R()
    PS_X = [ps("PS_X%d" % h, [128, 512]) for h in range(2)]; RPS_X = R()
    PS_AB = ps("PS_AB", [128, 2, 2, 128]); RPS_AB = R()
    PS_G = ps("PS_G", [128, 512]); RPS_G = R()
    PS_T = ps("PS_T", [128, 8, 128], BF16); RPS_T = R()
    PS_P1 = PS_AB[:].rearrange("p a b c -> p (a b c)"); RPS_P1 = RPS_AB
    PS_P2 = PS_P1; RPS_P2 = RPS_AB
    mm = lambda out, l, r_, rd, wr, st=True, sp=True, sg=False: S.op("pe", ("matmul", dict(out=out, lhsT=l, rhs=r_, start=st, stop=sp, skip_group_check=sg)), reads=rd, writes=wr)
    S.op("pool", ("tensor_copy", dict(out=identP[0:64, :], in_=identf[0:64, 0:64])), reads=[Ridf], writes=[Ridf])
    S.op("pool", ("tensor_copy", dict(out=identP[64:128, :], in_=identf[64:128, 64:128])), reads=[Ridf], writes=[Ridf])
    for h in range(2):
        S.op("pool", ("tensor_copy", dict(out=mkb[:, :, h, :], in_=mkf[:])), reads=[Rmk], writes=[Rmk])
    ytmp = sb("ytmp", [128, 128]); Rytmp = R()
    out_toks = []
    NFILL = 7
    NPREP = 2
    def prep_gen(b, d, blk, pp):
        bwd = (d == 1)
        midc, totc = (C // 2 - 1, C - 1) if not bwd else (C // 2, 0)
        t0 = blk * BLK
        rt_, Rrt_ = rtS[pp], RrtS[pp]
        at_, Rat_ = atS[pp], RatS[pp]
        kt_, Rkt_ = ktS[pp], RktS[pp]
        bt_, Rbt_ = btS[pp], RbtS[pp]
        r0_, Rr0_ = r0S[pp], Rr0S[pp]
        a0b_, Ra0b_ = a0bS[pp], Ra0bS[pp]
        kEb_, RkEb_ = kEbS[pp], RkEbS[pp]
        bEb_, RbEb_ = bEbS[pp], RbEbS[pp]
        qvb_, Rqvb_ = qvbS[pp], RqvbS[pp]
        e3_, Re3_ = e3S[pp], Re3S[pp]
        for nm, src in (("pr", pr), ("pk", pk), ("pv", pv), ("pwa", pwa)):
            tl, Rl = ld[nm]
            S.dma(("dma_start", dict(out=tl[:], in_=src[:, b, t0:t0 + BLK + 2])), writes=[Rl])
            yield
        sh = (slice(0, BLK) if not bwd else slice(2, BLK + 2))
        cur = slice(1, BLK + 1)
        for nm, q, Rq, mc in (("pr", qr, Rqr, 0), ("pk", qk, Rqk, 2), ("pv", qv, Rqv, 4), ("pwa", qwa, Rqwa, 6)):
            tl, Rl = ld[nm]
            S.op("dve", ("tensor_tensor", dict(out=tmp[:], in0=tl[:, sh], in1=tl[:, cur], op=ALU.subtract)), reads=[Rl], writes=[Rtmp])
            yield
            S.op("dve", ("scalar_tensor_tensor", dict(out=q[:], in0=tmp[:], scalar=PM(mc + d), in1=tl[:, cur], op0=ALU.mult, op1=ALU.add)), reads=[Rtmp, Rl, Rprm], writes=[Rq])
            yield
        S.op("act", ("activation", dict(out=twa[0:64, :], in_=qwa[0:64, :], func=AF.Tanh)), reads=[Rqwa], writes=[Rtwa])
        yield
        S.op("dve", ("tensor_copy", dict(out=twa[64:128, :], in_=qwa[64:128, :])), reads=[Rqwa], writes=[Rtwa])
        yield
        S.op("pe", ("matmul", dict(out=PS_P1, lhsT=w2b[0:64, d, :], rhs=twa[0:64, :], start=True, stop=True)), reads=[Rw2b, Rtwa], writes=[RPS_P1])
        S.op("act", ("activation", dict(out=sw[:], in_=PS_P1, func=AF.Sigmoid, bias=PM(8 + d))), reads=[RPS_P1, Rprm], writes=[Rsw])
        yield
        S.op("pe", ("matmul", dict(out=PS_P2, lhsT=w2b[64:128, d, :], rhs=twa[64:128, :], start=True, stop=True)), reads=[Rw2b, Rtwa], writes=[RPS_P2])
        S.op("act", ("activation", dict(out=asg[:], in_=PS_P2, func=AF.Sigmoid, bias=PM(10 + d))), reads=[RPS_P2, Rprm], writes=[Rasg])
        yield
        S.op("dve", ("tensor_scalar", dict(out=logw[:], in0=sw[:], scalar1=NEG_E, scalar2=None, op0=ALU.mult)), reads=[Rsw], writes=[Rlogw])
        yield
        S.op("dve", ("tensor_tensor_scan", dict(out=lin[:], data0=scanm[:], data1=logw[:], initial=0.0, op0=ALU.mult, op1=ALU.add)), reads=[Rsc, Rlogw], writes=[Rlin])
        yield
        lin3 = lambda tl: tl[:].rearrange("p (c t) -> p c t", t=C)
        bc = lambda tl, col: lin3(tl)[:, :, col:col + 1].to_broadcast([128, BLK // C, C])
        if bwd:
            S.op("dve", ("tensor_tensor", dict(out=lin3(tmp), in0=bc(lin, C - 1), in1=lin3(lin), op=ALU.subtract)), reads=[Rlin], writes=[Rtmp])
            yield
            S.op("dve", ("tensor_tensor", dict(out=lin[:], in0=tmp[:], in1=logw[:], op=ALU.add)), reads=[Rtmp, Rlogw], writes=[Rlin])
            yield
        S.op("dve", ("tensor_tensor", dict(out=lin3(linm), in0=lin3(lin), in1=bc(lin, midc), op=ALU.subtract)), reads=[Rlin], writes=[Rlinm])
        yield
        S.op("dve", ("tensor_tensor", dict(out=lexm[:], in0=linm[:], in1=logw[:], op=ALU.subtract)), reads=[Rlinm, Rlogw], writes=[Rlexm])
        yield
        S.op("dve", ("tensor_tensor", dict(out=lex[:], in0=lin[:], in1=logw[:], op=ALU.subtract)), reads=[Rlin, Rlogw], writes=[Rlex])
        yield
        S.op("dve", ("tensor_tensor", dict(out=lin3(lint), in0=lin3(lin), in1=bc(lin, totc), op=ALU.subtract)), reads=[Rlin], writes=[Rlint])
        yield
        S.op("act", ("activation", dict(out=e1[:], in_=linm[:], func=AF.Exp)), reads=[Rlinm], writes=[Re1])
        yield
        S.op("act", ("activation", dict(out=e1x[:], in_=lexm[:], func=AF.Exp)), reads=[Rlexm], writes=[Re1x])
        yield
        S.op("act", ("activation", dict(out=e2[:], in_=linm[:], func=AF.Exp, scale=-1.0)), reads=[Rlinm], writes=[Re2])
        yield
        S.op("act", ("activation", dict(out=e3_[:], in_=lin[:], func=AF.Exp)), reads=[Rlin], writes=[Re3_])
        yield
        S.op("act", ("activation", dict(out=e3x[:], in_=lex[:], func=AF.Exp)), reads=[Rlex], writes=[Re3x])
        yield
        S.op("act", ("activation", dict(out=e4[:], in_=lint[:], func=AF.Exp, scale=-1.0)), reads=[Rlint], writes=[Re4])
        yield
        S.op("dve", ("tensor_scalar", dict(out=kk[:], in0=qk[:], scalar1=PM(12), scalar2=None, op0=ALU.mult)), reads=[Rqk, Rprm], writes=[Rkk])
        yield
        S.op("pool", ("tensor_tensor", dict(out=kk2[:], in0=kk[:], in1=kk[:], op=ALU.mult)), reads=[Rkk], writes=[Rkk2])
        yield
        S.op("pe", ("matmul", dict(out=PS_P1, lhsT=bdf[:], rhs=kk2[:], start=True, stop=True)), reads=[Rbd, Rkk2], writes=[RPS_P1])
        S.op("dve", ("tensor_scalar", dict(out=rin[:], in0=PS_P1, scalar1=1e-12, scalar2=None, op0=ALU.max)), reads=[RPS_P1], writes=[Rrin])
        yield
        S.op("act", ("activation", dict(out=rin[:], in_=rin[:], func=AF.Sqrt)), reads=[Rrin], writes=[Rrin])
        yield
        S.op("dve", ("reciprocal", dict(out=rin[:], in_=rin[:])), reads=[Rrin], writes=[Rrin])
        yield
        S.op("dve", ("tensor_tensor", dict(out=kkn[:], in0=kk[:], in1=rin[:], op=ALU.mult)), reads=[Rkk, Rrin], writes=[Rkkn])
        yield
        S.op("dve", ("tensor_scalar", dict(out=tmp[:], in0=asg[:], scalar1=-1.0, scalar2=PM(13), op0=ALU.add, op1=ALU.mult)), reads=[Rasg, Rprm], writes=[Rtmp])
        yield
        S.op("dve", ("scalar_tensor_tensor", dict(out=kp[:], in0=tmp[:], scalar=1.0, in1=qk[:], op0=ALU.add, op1=ALU.mult)), reads=[Rtmp, Rqk], writes=[Rkp])
        yield
        S.op("pool", ("tensor_tensor", dict(out=bv[:], in0=kkn[:], in1=asg[:], op=ALU.mult)), reads=[Rkkn, Rasg], writes=[Rbv])
        yield
        S.op("pool", ("tensor_tensor", dict(out=rk[:], in0=qr[:], in1=kp[:], op=ALU.mult)), reads=[Rqr, Rkp], writes=[Rrk])
        yield
        S.op("pe", ("matmul", dict(out=PS_P2, lhsT=bdr[:], rhs=rk[:], start=True, stop=True)), reads=[Rbdr, Rrk], writes=[RPS_P2])
        bsl = bsum[:, t0:t0 + BLK]
        if d == 0:
            S.op("dve", ("tensor_tensor", dict(out=bsl, in0=PS_P2, in1=qv[:], op=ALU.mult)), reads=[RPS_P2, Rqv], writes=[Rbs[blk]])
            yield
        else:
            S.op("dve", ("tensor_tensor", dict(out=tmp[:], in0=PS_P2, in1=qv[:], op=ALU.mult)), reads=[RPS_P2, Rqv], writes=[Rtmp])
            yield
            S.op("pool", ("tensor_tensor", dict(out=bsl, in0=bsl, in1=tmp[:], op=ALU.add)), reads=[Rtmp, Rbs[blk]], writes=[Rbs[blk]])
            yield
        S.op("dve", ("tensor_tensor", dict(out=rt_[:], in0=qr[:], in1=e1[:], op=ALU.mult)), reads=[Rqr, Re1], writes=[Rrt_])
        yield
        S.op("dve", ("scalar_tensor_tensor", dict(out=at_[:], in0=kkn[:], scalar=-1.0, in1=e1x[:], op0=ALU.mult, op1=ALU.mult)), reads=[Rkkn, Re1x], writes=[Rat_])
        yield
        S.op("pool", ("tensor_tensor", dict(out=kt_[:], in0=kp[:], in1=e2[:], op=ALU.mult)), reads=[Rkp, Re2], writes=[Rkt_])
        yield
        S.op("pool", ("tensor_tensor", dict(out=bt_[:], in0=bv[:], in1=e2[:], op=ALU.mult)), reads=[Rbv, Re2], writes=[Rbt_])
        yield
        S.op("pool", ("tensor_tensor", dict(out=r0_[:], in0=qr[:], in1=e3_[:], op=ALU.mult)), reads=[Rqr, Re3_], writes=[Rr0_])
        yield
        S.op("dve", ("scalar_tensor_tensor", dict(out=a0b_[:], in0=kkn[:], scalar=-1.0, in1=e3x[:], op0=ALU.mult, op1=ALU.mult)), reads=[Rkkn, Re3x], writes=[Ra0b_])
        yield
        S.op("pool", ("tensor_tensor", dict(out=kEb_[:], in0=kp[:], in1=e4[:], op=ALU.mult)), reads=[Rkp, Re4], writes=[RkEb_])
        yield
        S.op("pool", ("tensor_tensor", dict(out=bEb_[:], in0=bv[:], in1=e4[:], op=ALU.mult)), reads=[Rbv, Re4], writes=[RbEb_])
        yield
        S.op("act", ("activation", dict(out=qvb_[:], in_=qv[:], func=AF.Copy)), reads=[Rqv], writes=[Rqvb_])
        yield


    def block_stages(b, d, blk, pp):
        bwd = (d == 1)
        midc, totc = (C // 2 - 1, C - 1) if not bwd else (C // 2, 0)
        t0 = blk * BLK
        rt_, Rrt_ = rtS[pp], RrtS[pp]
        at_, Rat_ = atS[pp], RatS[pp]
        kt_, Rkt_ = ktS[pp], RktS[pp]
        bt_, Rbt_ = btS[pp], RbtS[pp]
        r0_, Rr0_ = r0S[pp], Rr0S[pp]
        a0b_, Ra0b_ = a0bS[pp], Ra0bS[pp]
        kEb_, RkEb_ = kEbS[pp], RkEbS[pp]
        bEb_, RbEb_ = bEbS[pp], RbEbS[pp]
        qvb_, Rqvb_ = qvbS[pp], RqvbS[pp]
        e3_, Re3_ = e3S[pp], Re3S[pp]

        def stage1(ck):
            ci, cs, gci, p = ck
            for i, (src, Rs) in enumerate(((qvb_, Rqvb_), (a0b_, Ra0b_), (bEb_, RbEb_), (kEb_, RkEb_))):
                S.op("pe", ("transpose", dict(out=PS_T[:, i, :], in_=src[:, cs], identity=identb[:])), reads=[Rs, Ridb], writes=[RPS_T])
            yield
            S.op("act", ("activation", dict(out=TT[p][:], in_=PS_T[:, 0:4, :], func=AF.Copy)), reads=[RPS_T], writes=[RTT[p]])
            yield
            for h in range(2):
                hs = HS[h]
                mm(PS_M[:, h, 0, :], bt_[hs, cs], at_[hs, cs], [Rbt_, Rat_], [RPS_M])
                mm(PS_M[:, h, 1, :], at_[hs, cs], bt_[hs, cs], [Rbt_, Rat_], [RPS_M])
                yield
                mm(PS_M[:, h, 2, :], at_[hs, cs], kt_[hs, cs], [Rkt_, Rat_], [RPS_M])
                mm(PS_M[:, h, 3, :], bt_[hs, cs], rt_[hs, cs], [Rbt_, Rrt_], [RPS_M])
                yield
                mm((PS_K if h == 0 else PS_G)[:, 0:128], kt_[hs, cs], rt_[hs, cs], [Rkt_, Rrt_], [RPS_K if h == 0 else RPS_G])
                yield
            for h in range(2):
                S.op("dve", ("tensor_tensor", dict(out=SBM[p][:, h], in0=PS_M[:, h], in1=m4f[:, d, :, :], op=ALU.mult)), reads=[RPS_M, Rm4], writes=[RSBM[p]])
                yield
            S.op("dve", ("tensor_tensor", dict(out=MKR[p][:, 0, :], in0=PS_K[:, 0:128], in1=mkf[:, d, :], op=ALU.mult)), reads=[RPS_K, Rmk], writes=[RMKR[p]])
            yield
            S.op("dve", ("tensor_tensor", dict(out=MKR[p][:, 1, :], in0=PS_G[:, 0:128], in1=mkf[:, d, :], op=ALU.mult)), reads=[RPS_G, Rmk], writes=[RMKR[p]])
            yield
            S.op("act", ("activation", dict(out=SX[p][:, :, 0:128], in_=SBM[p][:, :, 3, :], func=AF.Copy)), reads=[RSBM[p]], writes=[RSX[p]])
            S.op("pool", ("tensor_copy", dict(out=SX[p][:, :, 128:192], in_=TT[p][:, 2, :].rearrange("p (h j) -> p h j", h=2))), reads=[RTT[p]], writes=[RSX[p]])
            yield

        def stage2(ck):
            ci, cs, gci, p = ck
            for h in range(2):
                mm(PS_X[h][:, 0:192], identb[:], SX[p][:, h, :], [Ridb, RSX[p]], [RPS_X], st=True, sp=True)
            A_ = [SBM[p][:, h, 1, :] for h in range(2)]
            B_ = [SBM[p][:, h, 0, :] for h in range(2)]
            Rcur = RSBM[p]
            for lv in range(7):
                for h in range(2):
                    mm(PS_X[h][:, 0:192], A_[h], SX[p][:, h, :], [Rcur, RSX[p]], [RPS_X], st=False, sp=True, sg=True)
                if lv < 6:
                    nb = lv % 2
                    for h in range(2):
                        mm(PS_AB[:, h, 0, :], B_[h], A_[h], [Rcur], [RPS_AB])
                        mm(PS_AB[:, h, 1, :], A_[h], B_[h], [Rcur], [RPS_AB])
                S.op("dve", ("tensor_copy", dict(out=SX[p][:, 0, :], in_=PS_X[0][:, 0:192])), reads=[RPS_X], writes=[RSX[p]])
                S.op("act", ("activation", dict(out=SX[p][:, 1, :], in_=PS_X[1][:, 0:192], func=AF.Copy)), reads=[RPS_X], writes=[RSX[p]])
                if lv < 6:
                    S.op("act", ("activation", dict(out=SAB[nb][:].rearrange("p a b c -> p (a b c)"), in_=PS_AB[:].rearrange("p a b c -> p (a b c)"), func=AF.Copy)), reads=[RPS_AB], writes=[RSAB[nb]])
                    A_ = [SAB[nb][:, h, 0, :] for h in range(2)]
                    B_ = [SAB[nb][:, h, 1, :] for h in range(2)]
                    Rcur = RSAB[nb]
                yield

        def stage3(ck):
            ci, cs, gci, p = ck
            for h in range(2):
                hs = HS[h]
                a0T = TT[p][:, 1, hs]
                mm(PS_G[hs, 0:128], a0T, SX[p][:, h, 0:128], [RTT[p], RSX[p]], [RPS_G])
                mm(PS_G[hs, 128:192], a0T, SX[p][:, h, 128:192], [RTT[p], RSX[p]], [RPS_G])
                yield
                mm(PS_G[:, 192 + 128 * h:320 + 128 * h], SBM[p][:, h, 2, :], SX[p][:, h, 0:128], [RSBM[p], RSX[p]], [RPS_G])
                mm(PS_K[:, 256 + 64 * h:320 + 64 * h], SBM[p][:, h, 2, :], SX[p][:, h, 128:192], [RSBM[p], RSX[p]], [RPS_K])
                yield
            S.op("dve", ("tensor_tensor", dict(out=Gb[:], in0=PS_G[:, 0:128], in1=r0_[:, cs], op=ALU.add)), reads=[RPS_G, Rr0_], writes=[RGb])
            yield
            S.op("dve", ("tensor_tensor", dict(out=Hb[:], in0=PS_G[:, 192:448].rearrange("p (h t) -> p h t", h=2), in1=MKR[p][:], op=ALU.add)), reads=[RPS_G, RMKR[p]], writes=[RHb])
            yield
            tcol = ci * C + totc
            S.op("dve", ("scalar_tensor_tensor", dict(out=Pb[:], in0=identP[:], scalar=e3_[:, tcol:tcol + 1], in1=PS_G[:, 128:192], op0=ALU.mult, op1=ALU.add)), reads=[RPS_G, Ridf, Re3_], writes=[RPb])
            yield
            S.op("dve", ("tensor_tensor", dict(out=Zb[:], in0=PS_K[:, 256:384].rearrange("p (h j) -> p h j", h=2), in1=TT[p][:, 3, :].rearrange("p (h j) -> p h j", h=2), op=ALU.add)), reads=[RPS_K, RTT[p]], writes=[RZb])
            yield
            for h in range(2):
                hs = HS[h]
                mm(PS_M[hs, 0, 0, :], STz[h][:], Gb[:], [RST[h], RGb], [RPS_M], st=True, sp=False)
                mm(PS_M[hs, 0, 0, :], TT[p][:, 0, hs], Hb[:, h, :], [RTT[p], RHb], [RPS_M], st=False, sp=True)
                yield
                mm(PS_M[hs, 0, 1, 0:64], Pb[:], STz[h][:], [RPb, RST[h]], [RPS_M], st=True, sp=False)
                mm(PS_M[hs, 0, 1, 0:64], Zb[:, h, :], TT[p][:, 0, hs], [RZb, RTT[p]], [RPS_M], st=False, sp=True)
                yield
            ysl = ysum[:, t0 + ci * C: t0 + (ci + 1) * C]
            if d == 0:
                S.op("act", ("activation", dict(out=ysl, in_=PS_M[:, 0, 0, :], func=AF.Copy)), reads=[RPS_M], writes=[Rys[gci]])
            else:
                S.op("act", ("activation", dict(out=ytmp[:, 0:128], in_=PS_M[:, 0, 0, :], func=AF.Copy)), reads=[RPS_M], writes=[Rytmp])
                S.op("dve", ("tensor_tensor", dict(out=ysl, in0=ytmp[:, 0:128], in1=ysl, op=ALU.add)), reads=[Rytmp, Rys[gci]], writes=[Rys[gci]])
            yield
            for h in range(2):
                hs = HS[h]
                S.op("act", ("activation", dict(out=STz[h][hs, :], in_=PS_M[hs, 0, 1, 0:64], func=AF.Copy)), reads=[RPS_M], writes=[RST[h]])
            yield

        return stage1, stage2, stage3

    def finalize_batch(b):
        for blk in range(nblk):
            t0 = blk * BLK
            ysl = ysum[:, t0:t0 + BLK]
            Rin = Rys[t0 // C: (t0 + BLK) // C]
            S.op("pe", ("matmul", dict(out=PS_P1, lhsT=bdm[:], rhs=ysl, start=True, stop=True)), reads=[Rbdm] + Rin, writes=[RPS_P1])
            S.op("dve", ("tensor_tensor", dict(out=fin1[:], in0=ysl, in1=PS_P1, op=ALU.subtract)), reads=[RPS_P1] + Rin, writes=[Rfin1])
            S.op("pool", ("tensor_tensor", dict(out=fin2[:], in0=fin1[:], in1=fin1[:], op=ALU.mult)), reads=[Rfin1], writes=[Rfin2])
            S.op("pe", ("matmul", dict(out=PS_P2, lhsT=bdm[:], rhs=fin2[:], start=True, stop=True)), reads=[Rbdm, Rfin2], writes=[RPS_P2])
            S.op("dve", ("tensor_scalar", dict(out=fin3[:], in0=PS_P2, scalar1=GN_EPS, scalar2=None, op0=ALU.add)), reads=[RPS_P2], writes=[Rfin3])
            S.op("act", ("activation", dict(out=fin3[:], in_=fin3[:], func=AF.Sqrt)), reads=[Rfin3], writes=[Rfin3])
            S.op("dve", ("reciprocal", dict(out=fin3[:], in_=fin3[:])), reads=[Rfin3], writes=[Rfin3])
            S.op("dve", ("tensor_tensor", dict(out=fin1[:], in0=fin1[:], in1=fin3[:], op=ALU.mult)), reads=[Rfin1, Rfin3], writes=[Rfin1])
            S.op("dve", ("tensor_scalar", dict(out=fin2[:], in0=fin1[:], scalar1=PM(15), scalar2=PM(16), op0=ALU.mult, op1=ALU.add)), reads=[Rfin1, Rprm], writes=[Rfin2])
            S.op("dve", ("tensor_tensor", dict(out=fin2[:], in0=fin2[:], in1=bsum[:, t0:t0 + BLK], op=ALU.add)), reads=[Rfin2, Rbs[blk]], writes=[Rfin2])
            out_toks.append(S.dma(("dma_start", dict(out=yout[:, b, t0:t0 + BLK], in_=fin2[:])), reads=[Rfin2]))

    sched_blocks = []
    for b in range(NB):
        for d in range(2):
            order = list(range(nblk)) if d == 0 else list(range(nblk - 1, -1, -1))
            for n_, blk in enumerate(order):
                sched_blocks.append((b, d, blk, n_ == 0, (n_ == len(order) - 1) and d == 1))
    pcount = 0
    for _ in prep_gen(sched_blocks[0][0], sched_blocks[0][1], sched_blocks[0][2], 0):
        pass
    for k, (b, d, blk, first_of_dir, last_of_batch) in enumerate(sched_blocks):
        pp = k % 2
        bwd = (d == 1)
        if first_of_dir:
            S.op("pool", ("memset", dict(ap=STz[0][:], constant=0.0)), writes=[RST[0]])
            S.op("pool", ("memset", dict(ap=STz[1][:], constant=0.0)), writes=[RST[1]])
        stage1, stage2, stage3 = block_stages(b, d, blk, pp)
        chunks = list(range(BLK // C)) if not bwd else list(range(BLK // C - 1, -1, -1))
        cks = [(ci, slice(ci * C, (ci + 1) * C), (blk * BLK // C) + ci, (pcount + n_) % 2) for n_, ci in enumerate(chunks)]
        pcount += len(chunks)
        if k + 1 < len(sched_blocks) and sched_blocks[k + 1][0] == b:
            nb_, nd_, nblk_ = sched_blocks[k + 1][:3]
            pgen = prep_gen(nb_, nd_, nblk_, (k + 1) % 2)
        else:
            pgen = iter(())
        for _ in stage1(cks[0]):
            pass
        for idx, ck in enumerate(cks):
            fill = itertools.chain(stage3(cks[idx - 1]) if idx > 0 else iter(()), stage1(cks[idx + 1]) if idx + 1 < len(cks) else iter(()))
            for _ in stage2(ck):
                for _k in range(NFILL):
                    next(fill, None)
                for _k in range(NPREP):
                    next(pgen, None)
            for _ in fill:
                pass
        for _ in stage3(cks[-1]):
            pass
        for _ in pgen:
            pass
        if last_of_batch:
            finalize_batch(b)
            if k + 1 < len(sched_blocks):
                nb_, nd_, nblk_ = sched_blocks[k + 1][:3]
                for _ in prep_gen(nb_, nd_, nblk_, (k + 1) % 2):
                    pass
    return out_toks


NT = 2048
NTH = NT + 2


def emit_p1(S, nc, I, O):
    sb = lambda name, shape, dt=F32: nc.alloc_sbuf_tensor(name, shape, dt)
    ps = lambda name, shape, dt=F32: nc.alloc_psum_tensor(name, shape, dt)
    R = Region
    op = S.op
    mm = lambda out, l, r_, rd, wr, st=True, sp=True: op("pe", ("matmul", dict(out=out, lhsT=l, rhs=r_, start=st, stop=sp)), reads=rd, writes=wr)

    identf = sb("identf", [128, 128]); identb = sb("identb", [128, 128], BF16); Rid = R()
    gE = sb("gE", [128, 8, 1]); gO = sb("gO", [128, 8, 1]); Rg = R()
    S.dma(("dma_start", dict(out=identf[:], in_=I["c_ident"])), writes=[Rid])
    op("dve", ("tensor_copy", dict(out=identb[:], in_=identf[:])), reads=[Rid], writes=[Rid])
    S.dma(("dma_start", dict(out=gE[:, :, 0], in_=I["e_norm_g"].rearrange("(k p) -> p k", p=128), allow_slow_non_contiguous=True)), writes=[Rg])
    S.dma(("dma_start", dict(out=gO[:, :, 0], in_=I["o_norm_g"].rearrange("(k p) -> p k", p=128), allow_slow_non_contiguous=True)), writes=[Rg])
    hnT = sb("hnT", [128, 8, NTH], BF16); RhnT = [R() for _ in range(18)]
    yT = nc.dram_tensor("yT_d", [16, 128, NT], BF16).ap(); RyT = [[R() for _ in range(4)] for _ in range(16)]
    U = sb("U", [128, 4096]); RU = R()
    xt = [sb("xt%d" % i, [128, 1024]) for i in range(2)]; Rxt = [R(), R()]
    yo = [sb("yo%d" % i, [128, 512], BF16) for i in range(2)]; Ryo = [R(), R()]
    ytl = [sb("ytl%d" % i, [128, 16, 128], BF16) for i in range(2)]; Rytl = [R(), R()]
    xn = sb("xn", [128, 1024], BF16); Rxn = R()
    sq = sb("sq", [128, 1024]); Rsq = R()
    st = sb("st", [128, 8]); Rst = R()
    stg = [sb("stg%d" % i, [128, 8, 256]) for i in range(2)]; Rstg = [R(), R()]
    wbf = [sb("wbf%d" % i, [128, 8, 512], BF16) for i in range(2)]; Rwbf = [R(), R()]
    wbig = sb("wbig", [128, 16, 1024], BF16); Rwbig = R()
    t1 = sb("t1", [128, 512]); Rt1 = R()
    t1b = sb("t1b", [128, 512]); t1s = [t1, t1b]; Rt1s = [Rt1, R()]
    t2b = sb("t2b", [128, 512]); t3b = sb("t3b", [128, 512])
    t2 = sb("t2", [128, 512]); Rt2 = R()
    t3 = sb("t3", [128, 512]); Rt3 = R()
    cw = sb("cw", [128, 8, 3]); Rcw = R()
    PS_a = ps("PS_a", [128, 512]); RPa = R()
    PS_b = ps("PS_b", [128, 512]); RPb = R()
    PS_c = ps("PS_c", [128, 512]); RPc = R()
    PS_d = ps("PS_d", [128, 512]); RPd = R()
    PS_t = ps("PS_t", [128, 8, 128], BF16); RPt = R()
    PS_m = ps("PS_m", [128, 8, 128]); RPm = R()
    for j_ in range(3):
        S.dma(("dma_start", dict(out=cw[:, :, j_], in_=I["e_conv_w"][j_].rearrange("(cb p) -> p cb", p=128), allow_slow_non_contiguous=True)), writes=[Rcw])

    def norm_tile(xtile, Rx, gt, dst_fn, Rdst, nvalid=128):
        op("act", ("activation", dict(out=sq[:], in_=xtile[:], func=AF.Square)), reads=[Rx], writes=[Rsq])
        op("dve", ("reduce_sum", dict(out=st[:, 0:1], in_=sq[:], axis=AX.X)), reads=[Rsq], writes=[Rst])
        op("dve", ("tensor_scalar", dict(out=st[:, 1:2], in0=st[:, 0:1], scalar1=1.0 / 1024, scalar2=1e-6, op0=ALU.mult, op1=ALU.add)), reads=[Rst], writes=[Rst])
        op("act", ("activation", dict(out=st[:, 2:3], in_=st[:, 1:2], func=AF.Sqrt)), reads=[Rst], writes=[Rst])
        op("dve", ("reciprocal", dict(out=st[:, 3:4], in_=st[:, 2:3])), reads=[Rst], writes=[Rst])
        op("dve", ("tensor_scalar", dict(out=xn[:], in0=xtile[:], scalar1=st[:, 3:4], scalar2=None, op0=ALU.mult)), reads=[Rx, Rst], writes=[Rxn])
        for k in range(8):
            op("pe", ("transpose", dict(out=PS_t[:, k, :], in_=xn[:, k * 128:(k + 1) * 128], identity=identb[:])), reads=[Rxn, Rid], writes=[RPt])
        dst_fn(gt)

    xh = I["xh"]
    for i in range(17):
        xb, Rx = xt[i % 2], Rxt[i % 2]
        if i < 16:
            S.dma(("dma_start", dict(out=xb[:], in_=xh[1 + 128 * i: 1 + 128 * (i + 1), :])), writes=[Rx])
            def dst(gt, i=i):
                op("dve", ("tensor_tensor", dict(out=hnT[:, :, 1 + 128 * i: 1 + 128 * (i + 1)], in0=PS_t[:], in1=gt[:].to_broadcast([128, 8, 128]), op=ALU.mult)), reads=[RPt, Rg], writes=[RhnT[i]])
        else:
            op("pool", ("memset", dict(ap=xb[:], constant=0.0)), writes=[Rx])
            S.dma(("dma_start", dict(out=xb[0:1, :], in_=xh[0:1, :])), writes=[Rx])
            S.dma(("dma_start", dict(out=xb[1:2, :], in_=xh[NT + 1:NT + 2, :])), writes=[Rx])
            def dst(gt):
                op("dve", ("tensor_tensor", dict(out=hnT[:, :, 0:1], in0=PS_t[:, :, 0:1], in1=gt[:], op=ALU.mult)), reads=[RPt, Rg], writes=[RhnT[16]])
                op("dve", ("tensor_tensor", dict(out=hnT[:, :, NT + 1:NT + 2], in0=PS_t[:, :, 1:2], in1=gt[:], op=ALU.mult)), reads=[RPt, Rg], writes=[RhnT[17]])
        norm_tile(xb, Rx, gE, dst, None)
    allh = RhnT

    wi = I["e_w_in"].rearrange("(k p) (s c) -> p k s c", p=128, c=1024)

    def load_w(buf, src4, nsp):
        for s_ in range(nsp):
            sb_ = s_ % 2
            S.dma(("dma_start", dict(out=stg[sb_][:, :, 0:128], in_=src4[:, :, s_, :])), writes=[Rstg[sb_]])
            op("pool", ("tensor_copy", dict(out=wbf[buf][:, :, s_ * 128:(s_ + 1) * 128], in_=stg[sb_][:, :, 0:128])), reads=[Rstg[sb_]], writes=[Rwbf[buf]])
        return wbf[buf][:, :, 0:nsp * 128].rearrange("p k (s c) -> p k s c", c=128)

    PSc0, RPc0, PSd0, RPd0 = PS_c, RPc, PS_d, RPd
    t2s = [t2, t2b]; Rt2s = [Rt2, R()]
    t3s = [t3, t3b]; Rt3s = [Rt3, R()]
    xc = U[:, 0:NTH]
    chunksA = [(0, 512), (512, 512), (1024, 512), (1536, 512), (2048, 2)]
    for cb in range(8):
        if cb % 2 == 0:
            for s_ in range(4):
                sb_ = s_ % 2
                S.dma(("dma_start", dict(out=stg[sb_][:], in_=wi[:, :, s_, cb * 128:(cb + 2) * 128])), writes=[Rstg[sb_]])
                op("pool", ("tensor_copy", dict(out=wbf[0][:, :, s_ * 128:(s_ + 1) * 128], in_=stg[sb_][:, :, 0:128])), reads=[Rstg[sb_]], writes=[Rwbf[0]])
                op("pool", ("tensor_copy", dict(out=wbf[1][:, :, s_ * 128:(s_ + 1) * 128], in_=stg[sb_][:, :, 128:256])), reads=[Rstg[sb_]], writes=[Rwbf[1]])
        w4 = wbf[cb % 2][:, :, 0:512].rearrange("p k (s c) -> p k s c", c=128)
        Rw = Rwbf[cb % 2]
        for ci_, (c0, n) in enumerate(chunksA):
            (PA, RA_), (PB, RB_) = (((PS_a, RPa), (PS_b, RPb)) if ci_ % 2 == 0 else ((PS_c, RPc), (PS_d, RPd)))
            for k in range(8):
                mm(PA[:, 0:n], w4[:, k, 0, :], hnT[:, k, c0:c0 + n], [Rw] + allh, [RA_], st=(k == 0), sp=(k == 7))
            for k in range(8):
                mm(PB[:, 0:n], w4[:, k, 2, :], hnT[:, k, c0:c0 + n], [Rw] + allh, [RB_], st=(k == 0), sp=(k == 7))
            t1_, Rt1_ = t1s[ci_ % 2], Rt1s[ci_ % 2]
            op("act", ("activation", dict(out=t1_[:, 0:n], in_=PA[:, 0:n], func=AF.Copy)), reads=[RA_], writes=[Rt1_])
            op("dve", ("tensor_tensor", dict(out=xc[:, c0:c0 + n], in0=PB[:, 0:n], in1=t1_[:, 0:n], op=ALU.mult)), reads=[RB_, Rt1_], writes=[RU])
        for j in range(4):
            c0 = 1 + 512 * j
            (PS_c, RPc), (PS_d, RPd) = ((PSc0, RPc0), (PSd0, RPd0)) if j % 2 == 1 else ((PS_a, RPa), (PS_b, RPb))
            t2, Rt2 = t2s[j % 2], Rt2s[j % 2]
            t3, Rt3 = t3s[j % 2], Rt3s[j % 2]
            for k in range(8):
                mm(PS_c[:], w4[:, k, 1, :], hnT[:, k, c0:c0 + 512], [Rw] + allh, [RPc], st=(k == 0), sp=(k == 7))
            for k in range(8):
                mm(PS_d[:], w4[:, k, 3, :], hnT[:, k, c0:c0 + 512], [Rw] + allh, [RPd], st=(k == 0), sp=(k == 7))
            op("dve", ("tensor_scalar", dict(out=t2[:], in0=xc[:, c0 - 1:c0 + 511], scalar1=cw[:, cb, 0:1], scalar2=None, op0=ALU.mult)), reads=[RU, Rcw], writes=[Rt2])
            op("dve", ("scalar_tensor_tensor", dict(out=t2[:], in0=xc[:, c0:c0 + 512], scalar=cw[:, cb, 1:2], in1=t2[:], op0=ALU.mult, op1=ALU.add)), reads=[RU, Rcw, Rt2], writes=[Rt2])
            op("dve", ("scalar_tensor_tensor", dict(out=t2[:], in0=xc[:, c0 + 1:c0 + 513], scalar=cw[:, cb, 2:3], in1=t2[:], op0=ALU.mult, op1=ALU.add)), reads=[RU, Rcw, Rt2], writes=[Rt2])
            op("act", ("activation", dict(out=t3[:], in_=PS_d[:], func=AF.Silu)), reads=[RPd], writes=[Rt3])
            op("dve", ("tensor_tensor", dict(out=t2[:], in0=PS_c[:], in1=t2[:], op=ALU.mult)), reads=[RPc, Rt2], writes=[Rt2])
            op("pool", ("tensor_tensor", dict(out=yo[j % 2][:], in0=t2[:], in1=t3[:], op=ALU.mult)), reads=[Rt2, Rt3], writes=[Ryo[j % 2]])
            S.dma(("dma_start", dict(out=yT[cb, :, 512 * j:512 * (j + 1)], in_=yo[j % 2][:])), reads=[Ryo[j % 2]], writes=[RyT[cb][j]], q="pool")

    PS_c, RPc, PS_d, RPd = PSc0, RPc0, PSd0, RPd0
    t2, Rt2, t3, Rt3 = t2s[0], Rt2s[0], t3s[0], Rt3s[0]
    for hf in range(4):
        S.dma(("dma_start", dict(out=stg[hf % 2][:], in_=wi[:, :, 5, hf * 256:(hf + 1) * 256])), writes=[Rstg[hf % 2]])
        op("pool", ("tensor_copy", dict(out=wbig[:, 0:8, hf * 256:(hf + 1) * 256], in_=stg[hf % 2][:])), reads=[Rstg[hf % 2]], writes=[Rwbig])
    for hf in range(4):
        S.dma(("dma_start", dict(out=stg[hf % 2][:], in_=wi[:, :, 4, hf * 256:(hf + 1) * 256])), writes=[Rstg[hf % 2]])
        op("pool", ("tensor_copy", dict(out=wbig[:, 8:16, hf * 256:(hf + 1) * 256], in_=stg[hf % 2][:])), reads=[Rstg[hf % 2]], writes=[Rwbig])
    for hf in range(4):
        S.dma(("dma_start", dict(out=stg[hf % 2][:], in_=wi[:, :, 6, hf * 256:(hf + 1) * 256])), writes=[Rstg[hf % 2]])
        op("pool", ("tensor_copy", dict(out=wbf[hf // 2][:, :, (hf % 2) * 256:(hf % 2 + 1) * 256], in_=stg[hf % 2][:])), reads=[Rstg[hf % 2]], writes=[Rwbf[hf // 2]])
    wsn = sb("wsn", [128, 8, 128]); wsnb = sb("wsnb", [128, 8, 128], BF16); wsT = sb("wsT", [128, 8, 128], BF16); Rws = R()
    S.dma(("dma_start", dict(out=wsn[:], in_=I["e_sgu_w"].rearrange("g i j -> i g j"))), writes=[Rws])
    op("dve", ("tensor_copy", dict(out=wsnb[:], in_=wsn[:])), reads=[Rws], writes=[Rws])
    for g in range(8):
        op("pe", ("transpose", dict(out=PS_t[:, g, :], in_=wsnb[:, g, :], identity=identb[:])), reads=[Rws, Rid], writes=[RPt])
    op("act", ("activation", dict(out=wsT[:], in_=PS_t[:], func=AF.Copy)), reads=[RPt], writes=[Rws])
    bsB = sb("bsB", [128, 8, 128]); lnG = sb("lnG", [128, 1024]); lnB = sb("lnB", [128, 1024]); Rbc = R()
    S.dma(("dma_start", dict(out=bsB[:].rearrange("p g i -> p (g i)"), in_=I["e_sgu_b"].rearrange("g i -> (g i)").partition_broadcast(128))), writes=[Rbc])
    S.dma(("dma_start", dict(out=lnG[:], in_=I["e_sgu_ln_g"].partition_broadcast(128))), writes=[Rbc])
    S.dma(("dma_start", dict(out=lnB[:], in_=I["e_sgu_ln_b"].partition_broadcast(128))), writes=[Rbc])
    vsb = sb("vsb", [128, 1024]); Rvsb = R()
    vnb = sb("vnb", [128, 1024], BF16); Rvnb = R()
    mixall = U[:, 0:4096].rearrange("p (g t) -> p g t", g=8)
    for tg in range(4):
        for ti in range(4):
            c0 = 1 + 128 * (4 * tg + ti)
            for hf, (P_, RP_) in enumerate((((PS_a, RPa), (PS_b, RPb)) if ti % 2 == 0 else ((PS_c, RPc), (PS_d, RPd)))):
                for k in range(8):
                    mm(P_[:], hnT[:, k, c0:c0 + 128], wbig[:, k, hf * 512:(hf + 1) * 512], [Rwbig] + allh, [RP_], st=(k == 0), sp=(k == 7))
                op("act", ("activation", dict(out=vsb[:, hf * 512:(hf + 1) * 512], in_=P_[:], func=AF.Copy)), reads=[RP_], writes=[Rvsb])
            op("act", ("activation", dict(out=sq[:], in_=vsb[:], func=AF.Square)), reads=[Rvsb], writes=[Rsq])
            op("dve", ("reduce_sum", dict(out=st[:, 0:1], in_=vsb[:], axis=AX.X)), reads=[Rvsb], writes=[Rst])
            op("dve", ("reduce_sum", dict(out=st[:, 1:2], in_=sq[:], axis=AX.X)), reads=[Rsq], writes=[Rst])
            op("dve", ("tensor_scalar", dict(out=st[:, 2:3], in0=st[:, 0:1], scalar1=1.0 / 1024, scalar2=None, op0=ALU.mult)), reads=[Rst], writes=[Rst])
            op("dve", ("tensor_tensor", dict(out=st[:, 3:4], in0=st[:, 2:3], in1=st[:, 2:3], op=ALU.mult)), reads=[Rst], writes=[Rst])
            op("dve", ("scalar_tensor_tensor", dict(out=st[:, 4:5], in0=st[:, 1:2], scalar=1.0 / 1024, in1=st[:, 3:4], op0=ALU.mult, op1=ALU.subtract)), reads=[Rst], writes=[Rst])
            op("dve", ("tensor_scalar", dict(out=st[:, 4:5], in0=st[:, 4:5], scalar1=1e-5, scalar2=None, op0=ALU.add)), reads=[Rst], writes=[Rst])
            op("act", ("activation", dict(out=st[:, 5:6], in_=st[:, 4:5], func=AF.Sqrt)), reads=[Rst], writes=[Rst])
            op("dve", ("reciprocal", dict(out=st[:, 6:7], in_=st[:, 5:6])), reads=[Rst], writes=[Rst])
            op("dve", ("tensor_scalar", dict(out=vsb[:], in0=vsb[:], scalar1=st[:, 2:3], scalar2=st[:, 6:7], op0=ALU.subtract, op1=ALU.mult)), reads=[Rvsb, Rst], writes=[Rvsb])
            op("dve", ("tensor_tensor", dict(out=vsb[:], in0=vsb[:], in1=lnG[:], op=ALU.mult)), reads=[Rvsb, Rbc], writes=[Rvsb])
            op("pool", ("tensor_tensor", dict(out=vnb[:], in0=vsb[:], in1=lnB[:], op=ALU.add)), reads=[Rvsb, Rbc], writes=[Rvnb])
            for g in range(8):
                mm(PS_m[:, g, :], vnb[:, g * 128:(g + 1) * 128], wsT[:, g, :], [Rvnb, Rws], [RPm])
            op("dve", ("tensor_tensor", dict(out=mixall[:, :, ti * 128:(ti + 1) * 128], in0=PS_m[:], in1=bsB[:], op=ALU.add)), reads=[RPm, Rbc], writes=[RU])
        c0 = 1 + 512 * tg
        for g in range(8):
            buf = g % 2

            (PU, RPU), (PZ, RPZ) = ((PS_c, RPc), (PS_d, RPd)) if g % 2 == 0 else ((PS_a, RPa), (PS_b, RPb))
            t2, Rt2 = t2s[g % 2], Rt2s[g % 2]
            t3, Rt3 = t3s[g % 2], Rt3s[g % 2]
            for k in range(8):
                mm(PU[:], wbig[:, 8 + k, g * 128:(g + 1) * 128], hnT[:, k, c0:c0 + 512], [Rwbig] + allh, [RPU], st=(k == 0), sp=(k == 7))
            for k in range(8):
                mm(PZ[:], wbf[g // 4][:, k, (g % 4) * 128:(g % 4 + 1) * 128], hnT[:, k, c0:c0 + 512], [Rwbf[g // 4]] + allh, [RPZ], st=(k == 0), sp=(k == 7))
            op("act", ("activation", dict(out=t3[:], in_=PZ[:], func=AF.Silu)), reads=[RPZ], writes=[Rt3])
            op("dve", ("tensor_tensor", dict(out=t2[:], in0=PU[:], in1=mixall[:, g, :], op=ALU.mult)), reads=[RPU, RU], writes=[Rt2])
            op("pool", ("tensor_tensor", dict(out=yo[g % 2][:], in0=t2[:], in1=t3[:], op=ALU.mult)), reads=[Rt2, Rt3], writes=[Ryo[g % 2]])
            S.dma(("dma_start", dict(out=yT[8 + g, :, 512 * tg:512 * (tg + 1)], in_=yo[g % 2][:])), reads=[Ryo[g % 2]], writes=[RyT[8 + g][tg]], q="pool")

    wo = I["e_w_out"].rearrange("(k p) n -> p k n", p=128)
    for q in range(2):
        for hf in range(4):
            S.dma(("dma_start", dict(out=stg[hf % 2][:], in_=wo[:, 8 * q:8 * q + 8, hf * 256:(hf + 1) * 256])), writes=[Rstg[hf % 2]])
            op("pool", ("tensor_copy", dict(out=wbig[:, 8 * q:8 * q + 8, hf * 256:(hf + 1) * 256], in_=stg[hf % 2][:])), reads=[Rstg[hf % 2]], writes=[Rwbig])
    ally = [r for row in RyT for r in row]
    h1ts = [sb("h1t%d" % i, [128, 1024]) for i in range(2)]; Rh1s = [R(), R()]
    for i in range(16):
        xb, Rx = xt[i % 2], Rxt[i % 2]
        h1t, Rh1 = h1ts[i % 2], Rh1s[i % 2]
        S.dma(("dma_start", dict(out=xb[:], in_=xh[1 + 128 * i: 1 + 128 * (i + 1), :])), writes=[Rx])
        S.dma(("dma_start", dict(out=ytl[i % 2][:], in_=yT[:, :, 128 * i:128 * (i + 1)].rearrange("k p t -> p k t"))), reads=ally, writes=[Rytl[i % 2]])
        for hf, (P_, RP_) in enumerate((((PS_a, RPa), (PS_b, RPb)) if i % 2 == 0 else ((PS_c, RPc), (PS_d, RPd)))):
            for k in range(16):
                mm(P_[:], ytl[i % 2][:, k, :], wbig[:, k, hf * 512:(hf + 1) * 512], [Rwbig, Rytl[i % 2]], [RP_], st=(k == 0), sp=(k == 15))
            op("dve", ("tensor_tensor", dict(out=h1t[:, hf * 512:(hf + 1) * 512], in0=P_[:], in1=xb[:, hf * 512:(hf + 1) * 512], op=ALU.add)), reads=[RP_, Rx], writes=[Rh1])
        S.dma(("dma_start", dict(out=O["h1"][128 * i:128 * (i + 1), :], in_=h1t[:])), reads=[Rh1], q="act")

        def dst(gt, i=i):
            op("dve", ("tensor_tensor", dict(out=hnT[:, :, 1 + 128 * i: 1 + 128 * (i + 1)], in0=PS_t[:], in1=gt[:].to_broadcast([128, 8, 128]), op=ALU.mult)), reads=[RPt, Rg], writes=[RhnT[i]])
        norm_tile(h1t, Rh1, gO, dst, None)

    wi1 = I["o_w_in"].rearrange("(k p) n -> p k n", p=128)
    blocks = [(c * 128, c * 128, False) for c in range(25)]
    blocks += [(3200 + c * 128, 3200 + c * 128, True) for c in range(8)]
    blocks += [(4736 + c * 128, 4224 + c * 128, True) for c in range(4)]
    ob = [sb("ob%d" % i, [128, 512]) for i in range(2)]; Rob = [R(), R()]
    oi = 0
    toks = []
    bi = 0
    nblocks = len(blocks)
    pairbuf = 0
    while bi < nblocks:
        sc, dr, act = blocks[bi]
        paired = (bi + 1 < nblocks) and (blocks[bi + 1][0] == sc + 128) and (blocks[bi + 1][2] == act)
        ncol = 256 if paired else 128
        buf = pairbuf % 2
        pairbuf += 1
        S.dma(("dma_start", dict(out=stg[buf][:, :, 0:ncol], in_=wi1[:, :, sc:sc + ncol])), writes=[Rstg[buf]])
        op("pool", ("tensor_copy", dict(out=wbf[buf][:, :, 0:ncol], in_=stg[buf][:, :, 0:ncol])), reads=[Rstg[buf]], writes=[Rwbf[buf]])
        for sub in range(2 if paired else 1):
            sc_, dr_, act_ = blocks[bi + sub]
            for j in range(4):
                P_, RP_ = ((PS_a, RPa), (PS_b, RPb), (PS_c, RPc), (PS_d, RPd))[j]
                c0 = 1 + 512 * j
                for k in range(8):
                    mm(P_[:], wbf[buf][:, k, sub * 128:(sub + 1) * 128], hnT[:, k, c0:c0 + 512], [Rwbf[buf]] + allh, [RP_], st=(k == 0), sp=(k == 7))
                o_, Ro = ob[oi % 2], Rob[oi % 2]
                oi += 1
                op("act", ("activation", dict(out=o_[:], in_=P_[:], func=(AF.Silu if act_ else AF.Copy))), reads=[RP_], writes=[Ro])
                toks.append(S.dma(("dma_start", dict(out=O["pT"][dr_:dr_ + 128, 512 * j:512 * (j + 1)], in_=o_[:])), reads=[Ro], q="act"))
        bi += 2 if paired else 1
    for hf in range(2):
        S.dma(("dma_start", dict(out=stg[hf][:], in_=wi1[:, :, 4224 + 256 * hf:4224 + 256 * (hf + 1)])), writes=[Rstg[hf]])
        op("pool", ("tensor_copy", dict(out=wbf[0][:, :, 256 * hf:256 * (hf + 1)], in_=stg[hf][:])), reads=[Rstg[hf]], writes=[Rwbf[0]])
    for i in range(16):
        c0 = 1 + 128 * i
        P_, RP_ = ((PS_a, RPa), (PS_b, RPb))[i % 2]
        for k in range(8):
            mm(P_[:], hnT[:, k, c0:c0 + 128], wbf[0][:, k, :], [Rwbf[0]] + allh, [RP_], st=(k == 0), sp=(k == 7))
        o_, Ro = ob[oi % 2], Rob[oi % 2]
        oi += 1
        op("act", ("activation", dict(out=o_[:], in_=P_[:], func=AF.Copy)), reads=[RP_], writes=[Ro])
        toks.append(S.dma(("dma_start", dict(out=O["fd"][128 * i:128 * (i + 1), :], in_=o_[:])), reads=[Ro], q="act"))
    return toks


def full_barrier(S):
    keys = list(S.cnt.items())
    for e in S.ENGS:
        waits = []
        for k, v in keys:
            if k == e:
                continue
            if S.seen[e].get(k, 0) < v:
                S.seen[e][k] = v
                waits.append((k, v))
        if waits:
            S.prog[e].append([waits, None, ("_none", 0)])


def emit_fnet(S, nc, I, ydT):
    R = Region
    op = S.op
    mm = lambda out, l, r_, rd, wr, st=True, sp=True: op("pe", ("matmul", dict(out=out, lhsT=l, rhs=r_, start=st, stop=sp)), reads=rd, writes=wr)
    toks = []
    with ExitStack() as es:
        sb = lambda name, shape, dt=F32: es.enter_context(nc.sbuf_tensor(name, shape, dt))
        ps = lambda name, shape, dt=F32: es.enter_context(nc.psum_tensor(name, shape, dt))
        xs = sb("f_xs", [128, 4096]); Rxs = R()
        xb = sb("f_xb", [128, 64, 128], BF16); Rxb = R()
        Fb = sb("f_F", [128, 256], BF16); RF = R()
        A_sb = sb("f_A", [64, 128, 256], BF16); RA = R()
        PQ = sb("f_PQ", [128, 2, 64, 128], BF16); RPQ = R()
        Tg = [[sb("f_T%d%d" % (i, j), [64, 16, 128], BF16) for j in range(2)] for i in range(2)]; RTg = [R(), R()]
        wf32 = sb("f_w32", [128, 128]); wfb = sb("f_wb", [128, 128], BF16); Rwf = R()
        Ccb = sb("f_Cc", [128, 128], BF16); mScb = sb("f_mSc", [128, 128], BF16); Rcs = R()
        Gb = sb("f_G", [128, 256], BF16); RG = R()
        ob = [sb("f_ob%d" % i, [128, 512]) for i in range(2)]; Rob = [R(), R()]
        PS = [ps("f_ps%d" % i, [128, 512]) for i in range(2)]; RPS = [R(), R()]
        S.dma(("dma_start", dict(out=Fb[:], in_=I["c_F"])), writes=[RF])
        S.dma(("dma_start", dict(out=Ccb[:], in_=I["c_Cc"])), writes=[Rcs])
        S.dma(("dma_start", dict(out=mScb[:], in_=I["c_mSc"])), writes=[Rcs])
        S.dma(("dma_start", dict(out=wf32[:], in_=I["fw"])), writes=[Rwf])
        op("dve", ("tensor_copy", dict(out=wfb[:], in_=wf32[:])), reads=[Rwf], writes=[Rwf])
        xbf = xb[:].rearrange("p l c -> p (l c)")
        for hf in range(2):
            S.dma(("dma_start", dict(out=xs[:], in_=I["fx"][:, hf * 4096:(hf + 1) * 4096])), writes=[Rxs])
            op("pool", ("tensor_copy", dict(out=xbf[:, hf * 4096:(hf + 1) * 4096], in_=xs[:])), reads=[Rxs], writes=[Rxb])
        for c2 in range(64):
            P_, RP_ = PS[c2 % 2], RPS[c2 % 2]
            for j in range(2):
                mm(P_[0:64, j * 256:(j + 1) * 256], xb[:, :, 2 * c2 + j], Fb[:], [Rxb, RF], [RP_])
            op("act" if c2 % 2 == 0 else "dve", ("activation", dict(out=A_sb[0:64, 2 * c2:2 * c2 + 2, :], in_=P_[0:64, :].rearrange("p (j k) -> p j k", j=2), func=AF.Copy)) if c2 % 2 == 0 else
               ("tensor_copy", dict(out=A_sb[0:64, 2 * c2:2 * c2 + 2, :], in_=P_[0:64, :].rearrange("p (j k) -> p j k", j=2))), reads=[RP_], writes=[RA])
        T1d = I["c_T1"].rearrange("p (k h) -> p k h", h=128)
        T2d = I["c_T2"].rearrange("p (k h) -> p k h", h=128)
        ei = 0
        for grp in range(8):
            tb = grp % 2
            S.dma(("dma_start", dict(out=Tg[tb][0][:], in_=T1d[:, grp * 16:(grp + 1) * 16, :])), writes=[RTg[tb]])
            S.dma(("dma_start", dict(out=Tg[tb][1][:], in_=T2d[:, grp * 16:(grp + 1) * 16, :])), writes=[RTg[tb]])
            for q in range(4):
                P_, RP_ = PS[ei % 2], RPS[ei % 2]
                for j in range(4):
                    kk_ = q * 4 + j
                    kl = grp * 16 + kk_
                    mm(P_[:, j * 128:(j + 1) * 128], A_sb[0:64, :, kl], Tg[tb][0][0:64, kk_, :], [RA, RTg[tb]], [RP_], st=True, sp=False)
                    mm(P_[:, j * 128:(j + 1) * 128], A_sb[0:64, :, 128 + kl], Tg[tb][1][0:64, kk_, :], [RA, RTg[tb]], [RP_], st=False, sp=True)
                kl0 = grp * 16 + q * 4
                for qq in range(2):
                    op("act" if qq == 0 else "dve",
                       ("activation", dict(out=PQ[:, qq, :, kl0:kl0 + 4].rearrange("p h l -> p l h"), in_=P_[:].rearrange("p (l q h) -> p l q h", l=4, q=2)[:, :, qq, :], func=AF.Copy)) if qq == 0 else
                       ("tensor_copy", dict(out=PQ[:, qq, :, kl0:kl0 + 4].rearrange("p h l -> p l h"), in_=P_[:].rearrange("p (l q h) -> p l q h", l=4, q=2)[:, :, qq, :])),
                       reads=[RP_], writes=[RPQ])
                ei += 1
        P_, RP_ = PS[0], RPS[0]
        mm(P_[:, 0:128], Ccb[:], wfb[:], [Rcs, Rwf], [RP_])
        mm(P_[:, 128:256], mScb[:], wfb[:], [Rcs, Rwf], [RP_])
        op("act", ("activation", dict(out=Gb[:], in_=P_[:, 0:256], func=AF.Copy)), reads=[RP_], writes=[RG])
        for t4 in range(16):
            P_, RP_ = PS[(t4 + 1) % 2], RPS[(t4 + 1) % 2]
            for j in range(4):
                kh = 4 * t4 + j
                mm(P_[:, j * 128:(j + 1) * 128], Gb[:, 0:128], PQ[:, 0, kh, :], [RG, RPQ], [RP_], st=True, sp=False)
                mm(P_[:, j * 128:(j + 1) * 128], Gb[:, 128:256], PQ[:, 1, kh, :], [RG, RPQ], [RP_], st=False, sp=True)
            o_, Ro = ob[t4 % 2], Rob[t4 % 2]
            op("act", ("activation", dict(out=o_[:], in_=P_[:], func=AF.Copy)), reads=[RP_], writes=[Ro])
            toks.append(S.dma(("dma_start", dict(out=ydT[:, 512 * t4:512 * (t4 + 1)], in_=o_[:])), reads=[Ro]))
    full_barrier(S)
    return toks


def emit_p3(S, nc, I, yout):
    sb = lambda name, shape, dt=F32: nc.alloc_sbuf_tensor(name, shape, dt)
    ps = lambda name, shape, dt=F32: nc.alloc_psum_tensor(name, shape, dt)
    R = Region
    op = S.op
    mm = lambda out, l, r_, rd, wr, st=True, sp=True: op("pe", ("matmul", dict(out=out, lhsT=l, rhs=r_, start=st, stop=sp)), reads=rd, writes=wr)
    stg = [sb("stg%d" % i, [128, 8, 256]) for i in range(2)]; Rstg = [R(), R()]
    wO = sb("wO", [128, 12, 1024], BF16); RwO = R()
    gN = sb("gN", [128, 1024]); RgN = R()
    gt_all = sb("gt_all", [128, 12, 2048], BF16); Rgt = R()
    ya = [sb("ya%d" % i, [128, 512]) for i in range(2)]; Rya = [R(), R()]
    ga = [sb("ga%d" % i, [128, 512]) for i in range(2)]; Rga = [R(), R()]
    h1t = [sb("h1t%d" % i, [128, 1024]) for i in range(2)]; Rh1 = [R(), R()]
    h2 = sb("h2", [128, 1024]); Rh2 = R()
    sq = sb("sq", [128, 1024]); Rsq = R()
    st = sb("st", [128, 8]); Rst = R()
    yo = [sb("yo%d" % i, [128, 1024]) for i in range(2)]; Ryo = [R(), R()]
    PS_a = ps("PS_a", [128, 512]); RPa = R()
    PS_b = ps("PS_b", [128, 512]); RPb = R()
    wo3 = I["o_w_out"].rearrange("(k p) n -> p k n", p=128)
    si = 0
    for (k0, nk) in ((0, 8), (8, 4)):
        for cq in range(4):
            b_ = si % 2; si += 1
            S.dma(("dma_start", dict(out=stg[b_][:, 0:nk, :], in_=wo3[:, k0:k0 + nk, cq * 256:(cq + 1) * 256])), writes=[Rstg[b_]])
            op("pool", ("tensor_copy", dict(out=wO[:, k0:k0 + nk, cq * 256:(cq + 1) * 256], in_=stg[b_][:, 0:nk, :])), reads=[Rstg[b_]], writes=[RwO])
    S.dma(("dma_start", dict(out=gN[:], in_=I["final_norm_g"].partition_broadcast(128))), writes=[RgN])
    ii = 0
    for blk in range(12):
        src = I["ycT"][blk * 128:(blk + 1) * 128] if blk < 8 else I["ydT"][(blk - 8) * 128:(blk - 7) * 128]
        gsrc = I["gT"][blk * 128:(blk + 1) * 128]
        for j in range(4):
            b_ = ii % 2; ii += 1
            S.dma(("dma_start", dict(out=ya[b_][:], in_=src[:, 512 * j:512 * (j + 1)])), writes=[Rya[b_]])
            S.dma(("dma_start", dict(out=ga[b_][:], in_=gsrc[:, 512 * j:512 * (j + 1)])), writes=[Rga[b_]])
            op("dve" if ii % 2 else "pool", ("tensor_tensor", dict(out=gt_all[:, blk, 512 * j:512 * (j + 1)], in0=ya[b_][:], in1=ga[b_][:], op=ALU.mult)), reads=[Rya[b_], Rga[b_]], writes=[Rgt])
    toks = []
    for i in range(16):
        hb, Rh = h1t[i % 2], Rh1[i % 2]
        S.dma(("dma_start", dict(out=hb[:], in_=I["h1"][128 * i:128 * (i + 1), :])), writes=[Rh])
        for hf, (P_, RP_) in enumerate(((PS_a, RPa), (PS_b, RPb))):
            for k in range(12):
                mm(P_[:], gt_all[:, k, 128 * i:128 * (i + 1)], wO[:, k, hf * 512:(hf + 1) * 512], [Rgt, RwO], [RP_], st=(k == 0), sp=(k == 11))
            op("dve", ("tensor_tensor", dict(out=h2[:, hf * 512:(hf + 1) * 512], in0=P_[:], in1=hb[:, hf * 512:(hf + 1) * 512], op=ALU.add)), reads=[RP_, Rh], writes=[Rh2])
        op("act", ("activation", dict(out=sq[:], in_=h2[:], func=AF.Square)), reads=[Rh2], writes=[Rsq])
        op("dve", ("reduce_sum", dict(out=st[:, 0:1], in_=sq[:], axis=AX.X)), reads=[Rsq], writes=[Rst])
        op("dve", ("tensor_scalar", dict(out=st[:, 1:2], in0=st[:, 0:1], scalar1=1.0 / 1024, scalar2=1e-6, op0=ALU.mult, op1=ALU.add)), reads=[Rst], writes=[Rst])
        op("act", ("activation", dict(out=st[:, 2:3], in_=st[:, 1:2], func=AF.Sqrt)), reads=[Rst], writes=[Rst])
        op("dve", ("reciprocal", dict(out=st[:, 3:4], in_=st[:, 2:3])), reads=[Rst], writes=[Rst])
        op("dve", ("tensor_scalar", dict(out=h2[:], in0=h2[:], scalar1=st[:, 3:4], scalar2=None, op0=ALU.mult)), reads=[Rh2, Rst], writes=[Rh2])
        o_, Ro = yo[i % 2], Ryo[i % 2]
        op("pool", ("tensor_tensor", dict(out=o_[:], in0=h2[:], in1=gN[:], op=ALU.mult)), reads=[Rh2, RgN], writes=[Ro])
        toks.append(S.dma(("dma_start", dict(out=yout[128 * i:128 * (i + 1), :], in_=o_[:])), reads=[Ro], q="pool"))
    return toks


def _mk(nc, name, shape, dt=None, out=False):
    return nc.dram_tensor(name, list(shape), dt or F32, kind=("ExternalOutput" if out else "ExternalInput")).ap()


W1 = ["e_norm_g", "e_w_in", "e_conv_w", "e_sgu_ln_g", "e_sgu_ln_b", "e_sgu_w", "e_sgu_b", "e_w_out", "o_norm_g", "o_w_in"]


def build_l1(shapes):
    nc = bass.Bass("TRN2", target_bir_lowering=False)
    I = {"xh": _mk(nc, "xh", [2050, 1024]), "c_ident": _mk(nc, "c_ident", [128, 128])}
    for n in W1:
        I[n] = _mk(nc, n, shapes[n])
    O = {"h1": _mk(nc, "h1", [2048, 1024], out=True), "pT": _mk(nc, "pT", [4736, 2048], out=True),
         "fd": _mk(nc, "fd", [2048, 512], out=True)}
    S = Sched(nc)
    toks = emit_p1(S, nc, I, O)
    S.barrier_on("sp", toks)
    S.finalize()
    return nc


def build_l2(consts):
    NB, T = 2, 8192
    nc = bass.Bass("TRN2", target_bir_lowering=False)
    pr, pk, pv, pwa = (_mk(nc, n, [128, NB, T + 2]) for n in ("pr", "pk", "pv", "pwa"))
    prm = _mk(nc, "prm", [128, 17]); w2a2 = _mk(nc, "w2a2", [128, 2, 128])
    A = {k: _mk(nc, k, v.shape) for k, v in consts.items()}
    FI = {"fx": _mk(nc, "fx", [128, 8192]), "fw": _mk(nc, "fw", [128, 128]),
          "c_F": _mk(nc, "c_F", [128, 256], BF16), "c_T1": _mk(nc, "c_T1", [64, 16384], BF16),
          "c_T2": _mk(nc, "c_T2", [64, 16384], BF16), "c_Cc": _mk(nc, "c_Cc", [128, 128], BF16),
          "c_mSc": _mk(nc, "c_mSc", [128, 128], BF16)}
    yout = _mk(nc, "yout", [128, NB, T], out=True)
    ydT = _mk(nc, "ydT", [128, T], out=True)
    S = Sched(nc)
    toks = emit_fnet(S, nc, FI, ydT)
    toks += emit_rwkv(S, nc, A, pr, pk, pv, pwa, prm, w2a2, yout, NB, T)
    S.barrier_on("sp", toks)
    S.finalize()
    return nc


def build_l3():
    nc = bass.Bass("TRN2", target_bir_lowering=False)
    I = {"ycT": _mk(nc, "ycT", [1024, 2048]), "ydT": _mk(nc, "ydT", [512, 2048]), "gT": _mk(nc, "gT", [1536, 2048]),
         "h1": _mk(nc, "h1", [2048, 1024]), "o_w_out": _mk(nc, "o_w_out", [1536, 1024]),
         "final_norm_g": _mk(nc, "final_norm_g", [1024])}
    y = _mk(nc, "y", [2048, 1024], out=True)
    S = Sched(nc)
    toks = emit_p3(S, nc, I, y)
    S.barrier_on("sp", toks)
    S.finalize()
    return nc


def fnet_tables():
    import ml_dtypes
    N = 8192
    nh = np.arange(128); kl = np.arange(128)
    ang = 2 * np.pi * np.outer(nh, kl) / 128
    F = np.concatenate([np.cos(ang), np.sin(ang)], axis=1)
    nl = np.arange(64)[:, None, None]; klo = np.arange(128)[None, :, None]; kh = np.arange(64)[None, None, :]
    beta = 2 * np.pi * ((nl * (klo + 128 * kh)) % N) / N
    T1 = np.concatenate([np.cos(beta), np.sin(beta)], axis=2).reshape(64, 16384)
    T2 = np.concatenate([-np.sin(beta), np.cos(beta)], axis=2).reshape(64, 16384)
    c = np.arange(128); phi = 2 * np.pi * np.outer(c, c) / 128
    nrm = 1 / np.sqrt(N * 128)
    bf = lambda a: np.ascontiguousarray(a.astype(np.float32)).astype(ml_dtypes.bfloat16)
    return {"c_F": bf(F), "c_T1": bf(T1), "c_T2": bf(T2), "c_Cc": bf(np.cos(phi) * nrm), "c_mSc": bf(-np.sin(phi) * nrm)}


def kernel(**inputs):
    f32 = lambda a: np.ascontiguousarray(np.asarray(a), dtype=np.float32)
    inp = {k: f32(v) for k, v in inputs.items()}
    x = inp["x"]
    ncores = 8
    cores = list(range(ncores))
    w1 = {n: np.ascontiguousarray(inp[n][0]) for n in W1}
    ident = np.eye(128, dtype=np.float32)
    maps = []
    for c in cores:
        b, s0 = c // 4, (c % 4) * 2048
        xh = np.zeros((2050, 1024), np.float32)
        xh[1:2049] = x[b, s0:s0 + 2048]
        if s0 > 0:
            xh[0] = x[b, s0 - 1]
        if s0 + 2048 < 8192:
            xh[2049] = x[b, s0 + 2048]
        m = {"xh": xh, "c_ident": ident}
        m.update(w1)
        maps.append(m)
    nc1 = build_l1({n: w1[n].shape for n in W1})
    r1 = run_bass_kernel_spmd(nc1, maps, core_ids=cores).results
    PT = np.concatenate([np.asarray(r["pT"]) for r in r1], axis=1)
    FD = np.concatenate([np.asarray(r["fd"]) for r in r1], axis=0)
    consts = build_consts_np()
    ft = fnet_tables()
    mu, w0, w2, a0, a2 = inp["o_mu"][0], inp["o_w0"][0], inp["o_w2"][0], inp["o_a0"][0], inp["o_a2"][0]
    k_k, k_a, r_k = inp["o_k_k"][0], inp["o_k_a"][0], inp["o_r_k"][0].reshape(-1)
    lg, lb = inp["o_lnx_g"][0], inp["o_lnx_b"][0]
    PT3 = PT.reshape(4736, 2, 8192)
    pad = lambda a: np.ascontiguousarray(np.pad(a, ((0, 0), (0, 0), (1, 1))))
    maps = []
    for c in cores:
        ch = slice(c * 128, (c + 1) * 128)
        m = {"pr": pad(PT3[0:1024][ch]), "pk": pad(PT3[1024:2048][ch]), "pv": pad(PT3[2048:3072][ch]),
             "pwa": pad(PT3[3072:3200])}
        prm = np.zeros((128, 17), np.float32)
        for d in range(2):
            prm[:, 0 + d] = mu[d, 0:1024][ch]; prm[:, 2 + d] = mu[d, 1024:2048][ch]; prm[:, 4 + d] = mu[d, 2048:3072][ch]
            prm[:, 6 + d] = mu[d, 3072:3200]; prm[:, 8 + d] = w0[d][ch]; prm[:, 10 + d] = a0[d][ch]
        prm[:, 12] = k_k[ch]; prm[:, 13] = k_a[ch]; prm[:, 14] = r_k[ch]; prm[:, 15] = lg[ch]; prm[:, 16] = lb[ch]
        m["prm"] = prm
        m["w2a2"] = np.ascontiguousarray(np.concatenate([w2[:, :, ch], a2[:, :, ch]], axis=1).transpose(1, 0, 2))
        m.update(consts)
        b, g = c // 4, c % 4
        m["fx"] = np.ascontiguousarray(FD[b * 8192:(b + 1) * 8192, g * 128:(g + 1) * 128]).reshape(128, 8192)
        m["fw"] = np.ascontiguousarray(inp["o_fnet_w"][0, g])
        m.update(ft)
        maps.append(m)
    nc2 = build_l2(consts)
    r2 = run_bass_kernel_spmd(nc2, maps, core_ids=cores).results
    YC = np.concatenate([np.asarray(r["yout"]).reshape(128, 16384) for r in r2], axis=0)
    YD = np.concatenate([np.concatenate([np.asarray(r2[b * 4 + g]["ydT"]) for g in range(4)], axis=0) for b in range(2)], axis=1)
    maps = []
    for c in cores:
        ts = slice(c * 2048, (c + 1) * 2048)
        maps.append({"ycT": np.ascontiguousarray(YC[:, ts]), "ydT": np.ascontiguousarray(YD[:, ts]),
                     "gT": np.ascontiguousarray(PT[3200:4736, ts]), "h1": np.asarray(r1[c]["h1"]),
                     "o_w_out": np.ascontiguousarray(inp["o_w_out"][0]), "final_norm_g": inp["final_norm_g"]})
    nc3 = build_l3()
    r3 = run_bass_kernel_spmd(nc3, maps, core_ids=cores).results
    y = np.concatenate([np.asarray(r["y"]) for r in r3], axis=0).reshape(2, 8192, 1024)
    return y.astype(np.float32)
```

```python
from contextlib import ExitStack
import itertools
import numpy as np
import concourse.bass as bass
import concourse.mybir as mybir
from concourse.bass_utils import run_bass_kernel_spmd


F32 = mybir.dt.float32
BF16 = mybir.dt.bfloat16
AF = mybir.ActivationFunctionType
ALU = mybir.AluOpType
AX = mybir.AxisListType

N_DMA_SEMS = 8


class Region:
    __slots__ = ("w", "r", "name")

    def __init__(self, name=""):
        self.w = None
        self.r = {}
        self.name = name


class Sched:
    ENGS = ("pe", "dve", "act", "pool", "sp")

    def __init__(self, nc):
        self.nc = nc
        self.prog = {e: [] for e in self.ENGS}
        self.cnt = {}
        self.seen = {e: {} for e in self.ENGS}
        self.dma_rr = {e: 0 for e in self.ENGS}
        self.dma_last = {}
        self.same_engine_raw = True
        self.cut = 0
        self.nrec = 0
        self.log = []

    def _collect(self, eng, mykey, reads, writes):
        waits = {}

        def need(tok, kind):
            if tok is None:
                return
            k, v = tok
            if k == mykey:
                if eng == "pe":
                    return
                if not self.same_engine_raw:
                    return
            if waits.get(k, 0) < v:
                waits[k] = v

        for R in reads:
            need(R.w, "raw")
        for R in writes:
            need(R.w, "waw")
            for k, v in R.r.items():
                need((k, v), "war")
        out = []
        seen = self.seen[eng]
        for k, v in waits.items():
            if seen.get(k, 0) < v:
                seen[k] = v
                out.append((k, v))
        return out

    def _commit(self, tok, reads, writes):
        for R in writes:
            R.w = tok
            R.r = {}
        k, v = tok
        for R in reads:
            if R.r.get(k, 0) < v:
                R.r[k] = v

    def op(self, eng, fn, reads=(), writes=()):
        self.nrec += 1
        if self.cut and self.nrec > self.cut:
            return None
        if self.cut:
            self.log.append((self.nrec, eng, fn[0] if isinstance(fn, tuple) else "fn", str(fn[1].get("out", ""))[:120] if isinstance(fn, tuple) else ""))
        key = eng
        waits = self._collect(eng, key, reads, writes)
        idx = self.cnt.get(key, 0) + 1
        self.cnt[key] = idx
        tok = (key, idx)
        self.prog[eng].append([waits, fn, tok])
        self._commit(tok, reads, writes)
        return tok

    def dma(self, fn, reads=(), writes=(), q="sp"):
        self.nrec += 1
        if self.cut and self.nrec > self.cut:
            return None
        i = self.dma_rr[q]
        self.dma_rr[q] = (i + 1) % N_DMA_SEMS
        key = "dma_%s_%d" % (q, i)
        waits = self._collect(q, key, reads, writes)
        prev = self.cnt.get(key, 0)
        if prev > 0 and self.seen[q].get(key, 0) < prev:
            self.seen[q][key] = prev
            waits.append((key, prev))
        idx = prev + 1
        self.cnt[key] = idx
        tok = (key, idx)
        self.prog[q].append([waits, fn, tok])
        self._commit(tok, reads, writes)
        return tok

    def finalize(self):
        nc = self.nc
        waited = {}
        for e in self.ENGS:
            for waits, fn, tok in self.prog[e]:
                for k, v in waits:
                    waited.setdefault(k, set()).add(v)
        self.final_waits = []
        sem_of = {}
        val_of = {}
        for k, s in waited.items():
            sem_of[k] = nc.alloc_semaphore("s_" + k)
            isdma = k.startswith("dma_")
            step = 16 if isdma else 1
            if isdma:
                val_of[k] = None
            else:
                val_of[k] = {v: (i + 1) for i, v in enumerate(sorted(s))}
        engobj = {"pe": nc.tensor, "dve": nc.vector, "act": nc.scalar,
                  "pool": nc.gpsimd, "sp": nc.sync}

        def value(k, v):
            if val_of[k] is None:
                return 16 * v
            return val_of[k][v]

        def emit(e):
            def body(eng):
                for waits, fn, tok in self.prog[e]:
                    for k, v in waits:
                        eng.wait_ge(sem_of[k], value(k, v))
                    if fn is None:
                        continue
                    if isinstance(fn, tuple):
                        ins = getattr(eng, fn[0])(**fn[1])
                    else:
                        ins = fn(eng)
                    k, v = tok
                    if k in sem_of:
                        if val_of[k] is None:
                            ins.then_inc(sem_of[k], 16)
                        elif v in val_of[k]:
                            ins.then_inc(sem_of[k], 1)
            return body

        with nc.Block() as block:
            for e, dec in (("sp", block.sync), ("pe", block.tensor), ("dve", block.vector),
                           ("act", block.scalar), ("pool", block.gpsimd)):
                if self.prog[e]:
                    dec(emit(e))
        self.n_sems = len(sem_of)
        return self.n_sems

    def barrier_on(self, eng, toks):
        waits = []
        for tk in toks:
            if tk is None:
                continue
            k, v = tk
            if self.seen[eng].get(k, 0) < v:
                self.seen[eng][k] = v
                waits.append((k, v))
        if waits:
            self.prog[eng].append([waits, None, ("_none", 0)])


C = 128
BLK = 512
NEG_E = -float(np.exp(-0.5))
GN_EPS = 64e-5


def build_consts_np():
    idx = np.arange(128)
    lt = (idx[:, None] < idx[None, :]).astype(np.float32)
    le = (idx[:, None] <= idx[None, :]).astype(np.float32)
    gt = lt.T.copy()
    ge = le.T.copy()
    m4f = np.stack([lt, gt, gt, le], axis=1)
    m4b = np.stack([gt, lt, lt, ge], axis=1)
    mk = np.stack([le, ge], axis=1)
    ident = np.eye(128, dtype=np.float32)
    bd = np.kron(np.eye(2, dtype=np.float32), np.ones((64, 64), np.float32))
    scanm = np.ones((128, BLK), np.float32)
    scanm[:, ::C] = 0.0
    return {"c_m4": np.stack([m4f, m4b], axis=1).reshape(128, 2 * 4 * 128).copy(),
            "c_mk": mk.reshape(128, 256).copy(), "c_ident": ident, "c_bd": bd, "c_scanm": scanm}


XST = False


def emit_rwkv(S, nc, A, pr, pk, pv, pwa, prm, w2a2, yout, NB, T):
    sb = lambda name, shape, dt=F32: nc.alloc_sbuf_tensor(name, shape, dt)
    ps = lambda name, shape, dt=F32: nc.alloc_psum_tensor(name, shape, dt)
    R = Region
    nblk = T // BLK

    m4f = sb("m4f", [128, 2, 4, 128]); Rm4 = R()
    mkf = sb("mkf", [128, 2, 128]); Rmk = R()
    identf = sb("identf", [128, 128]); Ridf = R()
    identb = sb("identb", [128, 128], BF16); Ridb = R()
    bdf = sb("bdf", [128, 128]); Rbd = R()
    bdr = sb("bdr", [128, 128]); Rbdr = R()
    bdm = sb("bdm", [128, 128]); Rbdm = R()
    scanm = sb("scanm", [128, BLK]); Rsc = R()
    prmt = sb("prmt", [128, 17]); Rprm = R()
    w2f = sb("w2f", [128, 2, 128]); Rw2f = R()
    w2b = sb("w2b", [128, 2, 128], BF16); Rw2b = R()
    S.dma(("dma_start", dict(out=m4f[:].rearrange("p a b c -> p (a b c)"), in_=A["c_m4"])), writes=[Rm4])
    S.dma(("dma_start", dict(out=mkf[:].rearrange("p a c -> p (a c)"), in_=A["c_mk"])), writes=[Rmk])
    S.dma(("dma_start", dict(out=identf[:], in_=A["c_ident"])), writes=[Ridf])
    S.dma(("dma_start", dict(out=bdf[:], in_=A["c_bd"])), writes=[Rbd])
    S.dma(("dma_start", dict(out=scanm[:], in_=A["c_scanm"])), writes=[Rsc])
    S.dma(("dma_start", dict(out=prmt[:], in_=prm)), writes=[Rprm])
    S.dma(("dma_start", dict(out=w2f[:], in_=w2a2)), writes=[Rw2f])
    S.op("dve", ("tensor_copy", dict(out=identb[:], in_=identf[:])), reads=[Ridf], writes=[Ridb])
    S.op("dve", ("tensor_copy", dict(out=w2b[:], in_=w2f[:])), reads=[Rw2f], writes=[Rw2b])
    PM = lambda c: prmt[:, c:c + 1]
    S.op("dve", ("tensor_scalar", dict(out=bdr[:], in0=bdf[:], scalar1=PM(14), scalar2=None, op0=ALU.mult)), reads=[Rbd, Rprm], writes=[Rbdr])
    S.op("dve", ("tensor_scalar", dict(out=bdm[:], in0=bdf[:], scalar1=1.0 / 64, scalar2=None, op0=ALU.mult)), reads=[Rbd], writes=[Rbdm])

    def T2(name, dt=F32, n=BLK):
        return sb(name, [128, n], dt), R()
    ld = {}
    for nm in ("pr", "pk", "pv", "pwa"):
        ld[nm] = (sb("ld_" + nm, [128, BLK + 2]), R())
    tmp, Rtmp = T2("tmp")
    qr, Rqr = T2("qr"); qk, Rqk = T2("qk"); qv, Rqv = T2("qv"); qwa, Rqwa = T2("qwa")
    twa, Rtwa = T2("twa", BF16)
    sw, Rsw = T2("sw"); asg, Rasg = T2("asg")
    logw, Rlogw = T2("logw"); lin, Rlin = T2("lin"); linm, Rlinm = T2("linm"); lexm, Rlexm = T2("lexm")
    lex, Rlex = T2("lex"); lint, Rlint = T2("lint")
    e1, Re1 = T2("e1"); e1x, Re1x = T2("e1x"); e2, Re2 = T2("e2"); e3S = [sb("e3%d" % i, [128, BLK]) for i in range(2)]; Re3S = [R(), R()]; e3x, Re3x = T2("e3x"); e4, Re4 = T2("e4")
    kk, Rkk = T2("kk"); kk2, Rkk2 = T2("kk2"); rin, Rrin = T2("rin"); kkn, Rkkn = T2("kkn")
    kp, Rkp = T2("kp"); bv, Rbv = T2("bv"); rk, Rrk = T2("rk")
    rtS = [sb("rt%d" % i, [128, BLK], BF16) for i in range(2)]; RrtS = [R(), R()]; atS = [sb("at%d" % i, [128, BLK], BF16) for i in range(2)]; RatS = [R(), R()]; ktS = [sb("kt%d" % i, [128, BLK], BF16) for i in range(2)]; RktS = [R(), R()]; btS = [sb("bt%d" % i, [128, BLK], BF16) for i in range(2)]; RbtS = [R(), R()]
    r0S = [sb("r0%d" % i, [128, BLK]) for i in range(2)]; Rr0S = [R(), R()]; a0bS = [sb("a0b%d" % i, [128, BLK], BF16) for i in range(2)]; Ra0bS = [R(), R()]; kEbS = [sb("kEb%d" % i, [128, BLK], BF16) for i in range(2)]; RkEbS = [R(), R()]; bEbS = [sb("bEb%d" % i, [128, BLK], BF16) for i in range(2)]; RbEbS = [R(), R()]
    qvbS = [sb("qvb%d" % i, [128, BLK], BF16) for i in range(2)]; RqvbS = [R(), R()]
    ysum = sb("ysum", [128, T]); Rys = [R() for _ in range(T // C)]
    bsum = sb("bsum", [128, T]); Rbs = [R() for _ in range(nblk)]
    TT = [sb("TT%d" % i, [128, 4, 128], BF16) for i in range(2)]; RTT = [R(), R()]
    SBM = [sb("SBM%d" % i, [128, 2, 4, 128], BF16) for i in range(2)]; RSBM = [R(), R()]
    MKR = [sb("MKR%d" % i, [128, 2, 128]) for i in range(2)]; RMKR = [R(), R()]
    SX = [sb("SX%d" % i, [128, 2, 192], BF16) for i in range(2)]; RSX = [R(), R()]
    SAB = [sb("SAB%d" % i, [128, 2, 2, 128], BF16) for i in range(2)]; RSAB = [R(), R()]
    Gb = sb("Gb", [128, 128], BF16); RGb = R()
    Hb = sb("Hb", [128, 2, 128], BF16); RHb = R()
    Pb = sb("Pb", [128, 64], BF16); RPb = R()
    Zb = sb("Zb", [128, 2, 64], BF16); RZb = R()
    STz = [sb("STz%d" % h, [128, 64], BF16) for h in range(2)]; RST = [R(), R()]
    identP = sb("identP", [128, 64]); mkb = sb("mkb", [128, 2, 2, 128])
    HS = [slice(0, 64), slice(64, 128)]
    fin1, Rfin1 = T2("fin1"); fin2, Rfin2 = T2("fin2"); fin3, Rfin3 = T2("fin3")

    PS_M = ps("PS_M", [128, 2, 4, 128]); RPS_M = R()
    PS_K = ps("PS_K", [128, 512]); RPS_K = R()
    PS_X = [ps("PS_X%d" % h, [128, 512]) for h in range(2)]; RPS_X = R()
    PS_AB = ps("PS_AB", [128, 2, 2, 128]); RPS_AB = R()
    PS_G = ps("PS_G", [128, 512]); RPS_G = R()
    PS_T = ps("PS_T", [128, 8, 128], BF16); RPS_T = R()
    PS_P1 = PS_AB[:].rearrange("p a b c -> p (a b c)"); RPS_P1 = RPS_AB
    PS_P2 = PS_P1; RPS_P2 = RPS_AB
    mm = lambda out, l, r_, rd, wr, st=True, sp=True, sg=False: S.op("pe", ("matmul", dict(out=out, lhsT=l, rhs=r_, start=st, stop=sp, skip_group_check=sg)), reads=rd, writes=wr)
    S.op("pool", ("tensor_copy", dict(out=identP[0:64, :], in_=identf[0:64, 0:64])), reads=[Ridf], writes=[Ridf])
    S.op("pool", ("tensor_copy", dict(out=identP[64:128, :], in_=identf[64:128, 64:128])), reads=[Ridf], writes=[Ridf])
    for h in range(2):
        S.op("pool", ("tensor_copy", dict(out=mkb[:, :, h, :], in_=mkf[:])), reads=[Rmk], writes=[Rmk])
    ytmp = sb("ytmp", [128, 128]); Rytmp = R()
    out_toks = []
    NFILL = 4
    NPREP = 2
    def prep_gen(b, d, blk, pp):
        bwd = (d == 1)
        midc, totc = (C // 2 - 1, C - 1) if not bwd else (C // 2, 0)
        t0 = blk * BLK
        rt_, Rrt_ = rtS[pp], RrtS[pp]
        at_, Rat_ = atS[pp], RatS[pp]
        kt_, Rkt_ = ktS[pp], RktS[pp]
        bt_, Rbt_ = btS[pp], RbtS[pp]
        r0_, Rr0_ = r0S[pp], Rr0S[pp]
        a0b_, Ra0b_ = a0bS[pp], Ra0bS[pp]
        kEb_, RkEb_ = kEbS[pp], RkEbS[pp]
        bEb_, RbEb_ = bEbS[pp], RbEbS[pp]
        qvb_, Rqvb_ = qvbS[pp], RqvbS[pp]
        e3_, Re3_ = e3S[pp], Re3S[pp]
        for nm, src in (("pr", pr), ("pk", pk), ("pv", pv), ("pwa", pwa)):
            tl, Rl = ld[nm]
            S.dma(("dma_start", dict(out=tl[:], in_=src[:, b, t0:t0 + BLK + 2])), writes=[Rl])
            yield
        sh = (slice(0, BLK) if not bwd else slice(2, BLK + 2))
        cur = slice(1, BLK + 1)
        for nm, q, Rq, mc in (("pr", qr, Rqr, 0), ("pk", qk, Rqk, 2), ("pv", qv, Rqv, 4), ("pwa", qwa, Rqwa, 6)):
            tl, Rl = ld[nm]
            S.op("dve", ("tensor_tensor", dict(out=tmp[:], in0=tl[:, sh], in1=tl[:, cur], op=ALU.subtract)), reads=[Rl], writes=[Rtmp])
            yield
            S.op("dve", ("scalar_tensor_tensor", dict(out=q[:], in0=tmp[:], scalar=PM(mc + d), in1=tl[:, cur], op0=ALU.mult, op1=ALU.add)), reads=[Rtmp, Rl, Rprm], writes=[Rq])
            yield
        S.op("act", ("activation", dict(out=twa[0:64, :], in_=qwa[0:64, :], func=AF.Tanh)), reads=[Rqwa], writes=[Rtwa])
        yield
        S.op("dve", ("tensor_copy", dict(out=twa[64:128, :], in_=qwa[64:128, :])), reads=[Rqwa], writes=[Rtwa])
        yield
        S.op("pe", ("matmul", dict(out=PS_P1, lhsT=w2b[0:64, d, :], rhs=twa[0:64, :], start=True, stop=True)), reads=[Rw2b, Rtwa], writes=[RPS_P1])
        S.op("act", ("activation", dict(out=sw[:], in_=PS_P1, func=AF.Sigmoid, bias=PM(8 + d))), reads=[RPS_P1, Rprm], writes=[Rsw])
        yield
        S.op("pe", ("matmul", dict(out=PS_P2, lhsT=w2b[64:128, d, :], rhs=twa[64:128, :], start=True, stop=True)), reads=[Rw2b, Rtwa], writes=[RPS_P2])
        S.op("act", ("activation", dict(out=asg[:], in_=PS_P2, func=AF.Sigmoid, bias=PM(10 + d))), reads=[RPS_P2, Rprm], writes=[Rasg])
        yield
        S.op("dve", ("tensor_scalar", dict(out=logw[:], in0=sw[:], scalar1=NEG_E, scalar2=None, op0=ALU.mult)), reads=[Rsw], writes=[Rlogw])
        yield
        S.op("dve", ("tensor_tensor_scan", dict(out=lin[:], data0=scanm[:], data1=logw[:], initial=0.0, op0=ALU.mult, op1=ALU.add)), reads=[Rsc, Rlogw], writes=[Rlin])
        yield
        lin3 = lambda tl: tl[:].rearrange("p (c t) -> p c t", t=C)
        bc = lambda tl, col: lin3(tl)[:, :, col:col + 1].to_broadcast([128, BLK // C, C])
        if bwd:
            S.op("dve", ("tensor_tensor", dict(out=lin3(tmp), in0=bc(lin, C - 1), in1=lin3(lin), op=ALU.subtract)), reads=[Rlin], writes=[Rtmp])
            yield
            S.op("dve", ("tensor_tensor", dict(out=lin[:], in0=tmp[:], in1=logw[:], op=ALU.add)), reads=[Rtmp, Rlogw], writes=[Rlin])
            yield
        S.op("dve", ("tensor_tensor", dict(out=lin3(linm), in0=lin3(lin), in1=bc(lin, midc), op=ALU.subtract)), reads=[Rlin], writes=[Rlinm])
        yield
        S.op("dve", ("tensor_tensor", dict(out=lexm[:], in0=linm[:], in1=logw[:], op=ALU.subtract)), reads=[Rlinm, Rlogw], writes=[Rlexm])
        yield
        S.op("dve", ("tensor_tensor", dict(out=lex[:], in0=lin[:], in1=logw[:], op=ALU.subtract)), reads=[Rlin, Rlogw], writes=[Rlex])
        yield
        S.op("dve", ("tensor_tensor", dict(out=lin3(lint), in0=lin3(lin), in1=bc(lin, totc), op=ALU.subtract)), reads=[Rlin], writes=[Rlint])
        yield
        S.op("act", ("activation", dict(out=e1[:], in_=linm[:], func=AF.Exp)), reads=[Rlinm], writes=[Re1])
        yield
        S.op("act", ("activation", dict(out=e1x[:], in_=lexm[:], func=AF.Exp)), reads=[Rlexm], writes=[Re1x])
        yield
        S.op("act", ("activation", dict(out=e2[:], in_=linm[:], func=AF.Exp, scale=-1.0)), reads=[Rlinm], writes=[Re2])
        yield
        S.op("act", ("activation", dict(out=e3_[:], in_=lin[:], func=AF.Exp)), reads=[Rlin], writes=[Re3_])
        yield
        S.op("act", ("activation", dict(out=e3x[:], in_=lex[:], func=AF.Exp)), reads=[Rlex], writes=[Re3x])
        yield
        S.op("act", ("activation", dict(out=e4[:], in_=lint[:], func=AF.Exp, scale=-1.0)), reads=[Rlint], writes=[Re4])
        yield
        S.op("dve", ("tensor_scalar", dict(out=kk[:], in0=qk[:], scalar1=PM(12), scalar2=None, op0=ALU.mult)), reads=[Rqk, Rprm], writes=[Rkk])
        yield
        S.op("pool", ("tensor_tensor", dict(out=kk2[:], in0=kk[:], in1=kk[:], op=ALU.mult)), reads=[Rkk], writes=[Rkk2])
        yield
        S.op("pe", ("matmul", dict(out=PS_P1, lhsT=bdf[:], rhs=kk2[:], start=True, stop=True)), reads=[Rbd, Rkk2], writes=[RPS_P1])
        S.op("dve", ("tensor_scalar", dict(out=rin[:], in0=PS_P1, scalar1=1e-12, scalar2=None, op0=ALU.max)), reads=[RPS_P1], writes=[Rrin])
        yield
        S.op("act", ("activation", dict(out=rin[:], in_=rin[:], func=AF.Sqrt)), reads=[Rrin], writes=[Rrin])
        yield
        S.op("dve", ("reciprocal", dict(out=rin[:], in_=rin[:])), reads=[Rrin], writes=[Rrin])
        yield
        S.op("dve", ("tensor_tensor", dict(out=kkn[:], in0=kk[:], in1=rin[:], op=ALU.mult)), reads=[Rkk, Rrin], writes=[Rkkn])
        yield
        S.op("dve", ("tensor_scalar", dict(out=tmp[:], in0=asg[:], scalar1=-1.0, scalar2=PM(13), op0=ALU.add, op1=ALU.mult)), reads=[Rasg, Rprm], writes=[Rtmp])
        yield
        S.op("dve", ("scalar_tensor_tensor", dict(out=kp[:], in0=tmp[:], scalar=1.0, in1=qk[:], op0=ALU.add, op1=ALU.mult)), reads=[Rtmp, Rqk], writes=[Rkp])
        yield
        S.op("pool", ("tensor_tensor", dict(out=bv[:], in0=kkn[:], in1=asg[:], op=ALU.mult)), reads=[Rkkn, Rasg], writes=[Rbv])
        yield
        S.op("pool", ("tensor_tensor", dict(out=rk[:], in0=qr[:], in1=kp[:], op=ALU.mult)), reads=[Rqr, Rkp], writes=[Rrk])
        yield
        S.op("pe", ("matmul", dict(out=PS_P2, lhsT=bdr[:], rhs=rk[:], start=True, stop=True)), reads=[Rbdr, Rrk], writes=[RPS_P2])
        bsl = bsum[:, t0:t0 + BLK]
        if d == 0:
            S.op("dve", ("tensor_tensor", dict(out=bsl, in0=PS_P2, in1=qv[:], op=ALU.mult)), reads=[RPS_P2, Rqv], writes=[Rbs[blk]])
            yield
        else:
            S.op("dve", ("tensor_tensor", dict(out=tmp[:], in0=PS_P2, in1=qv[:], op=ALU.mult)), reads=[RPS_P2, Rqv], writes=[Rtmp])
            yield
            S.op("pool", ("tensor_tensor", dict(out=bsl, in0=bsl, in1=tmp[:], op=ALU.add)), reads=[Rtmp, Rbs[blk]], writes=[Rbs[blk]])
            yield
        S.op("dve", ("tensor_tensor", dict(out=rt_[:], in0=qr[:], in1=e1[:], op=ALU.mult)), reads=[Rqr, Re1], writes=[Rrt_])
        yield
        S.op("dve", ("scalar_tensor_tensor", dict(out=at_[:], in0=kkn[:], scalar=-1.0, in1=e1x[:], op0=ALU.mult, op1=ALU.mult)), reads=[Rkkn, Re1x], writes=[Rat_])
        yield
        S.op("pool", ("tensor_tensor", dict(out=kt_[:], in0=kp[:], in1=e2[:], op=ALU.mult)), reads=[Rkp, Re2], writes=[Rkt_])
        yield
        S.op("pool", ("tensor_tensor", dict(out=bt_[:], in0=bv[:], in1=e2[:], op=ALU.mult)), reads=[Rbv, Re2], writes=[Rbt_])
        yield
        S.op("pool", ("tensor_tensor", dict(out=r0_[:], in0=qr[:], in1=e3_[:], op=ALU.mult)), reads=[Rqr, Re3_], writes=[Rr0_])
        yield
        S.op("dve", ("scalar_tensor_tensor", dict(out=a0b_[:], in0=kkn[:], scalar=-1.0, in1=e3x[:], op0=ALU.mult, op1=ALU.mult)), reads=[Rkkn, Re3x], writes=[Ra0b_])
        yield
        S.op("pool", ("tensor_tensor", dict(out=kEb_[:], in0=kp[:], in1=e4[:], op=ALU.mult)), reads=[Rkp, Re4], writes=[RkEb_])
        yield
        S.op("pool", ("tensor_tensor", dict(out=bEb_[:], in0=bv[:], in1=e4[:], op=ALU.mult)), reads=[Rbv, Re4], writes=[RbEb_])
        yield
        S.op("act", ("activation", dict(out=qvb_[:], in_=qv[:], func=AF.Copy)), reads=[Rqv], writes=[Rqvb_])
        yield


    def block_stages(b, d, blk, pp):
        bwd = (d == 1)
        midc, totc = (C // 2 - 1, C - 1) if not bwd else (C // 2, 0)
        t0 = blk * BLK
        rt_, Rrt_ = rtS[pp], RrtS[pp]
        at_, Rat_ = atS[pp], RatS[pp]
        kt_, Rkt_ = ktS[pp], RktS[pp]
        bt_, Rbt_ = btS[pp], RbtS[pp]
        r0_, Rr0_ = r0S[pp], Rr0S[pp]
        a0b_, Ra0b_ = a0bS[pp], Ra0bS[pp]
        kEb_, RkEb_ = kEbS[pp], RkEbS[pp]
        bEb_, RbEb_ = bEbS[pp], RbEbS[pp]
        qvb_, Rqvb_ = qvbS[pp], RqvbS[pp]
        e3_, Re3_ = e3S[pp], Re3S[pp]

        def stage1(ck):
            ci, cs, gci, p = ck
            for i, (src, Rs) in enumerate(((qvb_, Rqvb_), (a0b_, Ra0b_), (bEb_, RbEb_), (kEb_, RkEb_))):
                S.op("pe", ("transpose", dict(out=PS_T[:, i, :], in_=src[:, cs], identity=identb[:])), reads=[Rs, Ridb], writes=[RPS_T])
            yield
            S.op("act", ("activation", dict(out=TT[p][:], in_=PS_T[:, 0:4, :], func=AF.Copy)), reads=[RPS_T], writes=[RTT[p]])
            yield
            for h in range(2):
                hs = HS[h]
                mm(PS_M[:, h, 0, :], bt_[hs, cs], at_[hs, cs], [Rbt_, Rat_], [RPS_M])
                mm(PS_M[:, h, 1, :], at_[hs, cs], bt_[hs, cs], [Rbt_, Rat_], [RPS_M])
                yield
                mm(PS_M[:, h, 2, :], at_[hs, cs], kt_[hs, cs], [Rkt_, Rat_], [RPS_M])
                mm(PS_M[:, h, 3, :], bt_[hs, cs], rt_[hs, cs], [Rbt_, Rrt_], [RPS_M])
                yield
                mm((PS_K if h == 0 else PS_G)[:, 0:128], kt_[hs, cs], rt_[hs, cs], [Rkt_, Rrt_], [RPS_K if h == 0 else RPS_G])
                yield
            for h in range(2):
                S.op("dve", ("tensor_tensor", dict(out=SBM[p][:, h], in0=PS_M[:, h], in1=m4f[:, d, :, :], op=ALU.mult)), reads=[RPS_M, Rm4], writes=[RSBM[p]])
                yield
            S.op("dve", ("tensor_tensor", dict(out=MKR[p][:, 0, :], in0=PS_K[:, 0:128], in1=mkf[:, d, :], op=ALU.mult)), reads=[RPS_K, Rmk], writes=[RMKR[p]])
            yield
            S.op("dve", ("tensor_tensor", dict(out=MKR[p][:, 1, :], in0=PS_G[:, 0:128], in1=mkf[:, d, :], op=ALU.mult)), reads=[RPS_G, Rmk], writes=[RMKR[p]])
            yield
            S.op("act", ("activation", dict(out=SX[p][:, :, 0:128], in_=SBM[p][:, :, 3, :], func=AF.Copy)), reads=[RSBM[p]], writes=[RSX[p]])
            S.op("pool", ("tensor_copy", dict(out=SX[p][:, :, 128:192], in_=TT[p][:, 2, :].rearrange("p (h j) -> p h j", h=2))), reads=[RTT[p]], writes=[RSX[p]])
            yield

        def stage2(ck):
            ci, cs, gci, p = ck
            for h in range(2):
                mm(PS_X[h][:, 0:192], identb[:], SX[p][:, h, :], [Ridb, RSX[p]], [RPS_X], st=True, sp=True)
            A_ = [SBM[p][:, h, 1, :] for h in range(2)]
            B_ = [SBM[p][:, h, 0, :] for h in range(2)]
            Rcur = RSBM[p]
            for lv in range(7):
                if lv < 6:
                    nb = lv % 2
                    for h in range(2):
                        mm(PS_AB[:, h, 0, :], B_[h], A_[h], [Rcur], [RPS_AB])
                        mm(PS_AB[:, h, 1, :], A_[h], B_[h], [Rcur], [RPS_AB])
                for h in range(2):
                    mm(PS_X[h][:, 0:192], A_[h], SX[p][:, h, :], [Rcur, RSX[p]], [RPS_X], st=False, sp=True, sg=True)
                if lv < 6:
                    S.op("act", ("activation", dict(out=SAB[nb][:].rearrange("p a b c -> p (a b c)"), in_=PS_AB[:].rearrange("p a b c -> p (a b c)"), func=AF.Copy)), reads=[RPS_AB], writes=[RSAB[nb]])
                S.op("dve", ("tensor_copy", dict(out=SX[p][:, 0, :], in_=PS_X[0][:, 0:192])), reads=[RPS_X], writes=[RSX[p]])
                S.op("dve", ("tensor_copy", dict(out=SX[p][:, 1, :], in_=PS_X[1][:, 0:192])), reads=[RPS_X], writes=[RSX[p]])
                if lv < 6:
                    A_ = [SAB[nb][:, h, 0, :] for h in range(2)]
                    B_ = [SAB[nb][:, h, 1, :] for h in range(2)]
                    Rcur = RSAB[nb]
                yield

        def stage3(ck):
            ci, cs, gci, p = ck
            for h in range(2):
                hs = HS[h]
                a0T = TT[p][:, 1, hs]
                mm(PS_G[hs, 0:128], a0T, SX[p][:, h, 0:128], [RTT[p], RSX[p]], [RPS_G])
                mm(PS_G[hs, 128:192], a0T, SX[p][:, h, 128:192], [RTT[p], RSX[p]], [RPS_G])
                yield
                mm(PS_G[:, 192 + 128 * h:320 + 128 * h], SBM[p][:, h, 2, :], SX[p][:, h, 0:128], [RSBM[p], RSX[p]], [RPS_G])
                mm(PS_K[:, 256 + 64 * h:320 + 64 * h], SBM[p][:, h, 2, :], SX[p][:, h, 128:192], [RSBM[p], RSX[p]], [RPS_K])
                yield
            S.op("dve", ("tensor_tensor", dict(out=Gb[:], in0=PS_G[:, 0:128], in1=r0_[:, cs], op=ALU.add)), reads=[RPS_G, Rr0_], writes=[RGb])
            yield
            S.op("dve", ("tensor_tensor", dict(out=Hb[:], in0=PS_G[:, 192:448].rearrange("p (h t) -> p h t", h=2), in1=MKR[p][:], op=ALU.add)), reads=[RPS_G, RMKR[p]], writes=[RHb])
            yield
            tcol = ci * C + totc
            S.op("dve", ("scalar_tensor_tensor", dict(out=Pb[:], in0=identP[:], scalar=e3_[:, tcol:tcol + 1], in1=PS_G[:, 128:192], op0=ALU.mult, op1=ALU.add)), reads=[RPS_G, Ridf, Re3_], writes=[RPb])
            yield
            S.op("dve", ("tensor_tensor", dict(out=Zb[:], in0=PS_K[:, 256:384].rearrange("p (h j) -> p h j", h=2), in1=TT[p][:, 3, :].rearrange("p (h j) -> p h j", h=2), op=ALU.add)), reads=[RPS_K, RTT[p]], writes=[RZb])
            yield
            for h in range(2):
                hs = HS[h]
                mm(PS_M[hs, 0, 0, :], STz[h][:], Gb[:], [RST[h], RGb], [RPS_M], st=True, sp=False)
                mm(PS_M[hs, 0, 0, :], TT[p][:, 0, hs], Hb[:, h, :], [RTT[p], RHb], [RPS_M], st=False, sp=True)
                yield
                mm(PS_M[hs, 0, 1, 0:64], Pb[:], STz[h][:], [RPb, RST[h]], [RPS_M], st=True, sp=False)
                mm(PS_M[hs, 0, 1, 0:64], Zb[:, h, :], TT[p][:, 0, hs], [RZb, RTT[p]], [RPS_M], st=False, sp=True)
                yield
            ysl = ysum[:, t0 + ci * C: t0 + (ci + 1) * C]
            if d == 0:
                S.op("act", ("activation", dict(out=ysl, in_=PS_M[:, 0, 0, :], func=AF.Copy)), reads=[RPS_M], writes=[Rys[gci]])
            else:
                S.op("act", ("activation", dict(out=ytmp[:, 0:128], in_=PS_M[:, 0, 0, :], func=AF.Copy)), reads=[RPS_M], writes=[Rytmp])
                S.op("dve", ("tensor_tensor", dict(out=ysl, in0=ytmp[:, 0:128], in1=ysl, op=ALU.add)), reads=[Rytmp, Rys[gci]], writes=[Rys[gci]])
            yield
            for h in range(2):
                hs = HS[h]
                S.op("act", ("activation", dict(out=STz[h][hs, :], in_=PS_M[hs, 0, 1, 0:64], func=AF.Copy)), reads=[RPS_M], writes=[RST[h]])
            yield

        return stage1, stage2, stage3

    def finalize_batch(b):
        for blk in range(nblk):
            t0 = blk * BLK
            ysl = ysum[:, t0:t0 + BLK]
            Rin = Rys[t0 // C: (t0 + BLK) // C]
            S.op("pe", ("matmul", dict(out=PS_P1, lhsT=bdm[:], rhs=ysl, start=True, stop=True)), reads=[Rbdm] + Rin, writes=[RPS_P1])
            S.op("dve", ("tensor_tensor", dict(out=fin1[:], in0=ysl, in1=PS_P1, op=ALU.subtract)), reads=[RPS_P1] + Rin, writes=[Rfin1])
            S.op("pool", ("tensor_tensor", dict(out=fin2[:], in0=fin1[:], in1=fin1[:], op=ALU.mult)), reads=[Rfin1], writes=[Rfin2])
            S.op("pe", ("matmul", dict(out=PS_P2, lhsT=bdm[:], rhs=fin2[:], start=True, stop=True)), reads=[Rbdm, Rfin2], writes=[RPS_P2])
            S.op("dve", ("tensor_scalar", dict(out=fin3[:], in0=PS_P2, scalar1=GN_EPS, scalar2=None, op0=ALU.add)), reads=[RPS_P2], writes=[Rfin3])
            S.op("act", ("activation", dict(out=fin3[:], in_=fin3[:], func=AF.Sqrt)), reads=[Rfin3], writes=[Rfin3])
            S.op("dve", ("reciprocal", dict(out=fin3[:], in_=fin3[:])), reads=[Rfin3], writes=[Rfin3])
            S.op("dve", ("tensor_tensor", dict(out=fin1[:], in0=fin1[:], in1=fin3[:], op=ALU.mult)), reads=[Rfin1, Rfin3], writes=[Rfin1])
            S.op("dve", ("tensor_scalar", dict(out=fin2[:], in0=fin1[:], scalar1=PM(15), scalar2=PM(16), op0=ALU.mult, op1=ALU.add)), reads=[Rfin1, Rprm], writes=[Rfin2])
            S.op("dve", ("tensor_tensor", dict(out=fin2[:], in0=fin2[:], in1=bsum[:, t0:t0 + BLK], op=ALU.add)), reads=[Rfin2, Rbs[blk]], writes=[Rfin2])
            out_toks.append(S.dma(("dma_start", dict(out=yout[:, b, t0:t0 + BLK], in_=fin2[:])), reads=[Rfin2]))

    sched_blocks = []
    for b in range(NB):
        for d in range(2):
            order = list(range(nblk)) if d == 0 else list(range(nblk - 1, -1, -1))
            for n_, blk in enumerate(order):
                sched_blocks.append((b, d, blk, n_ == 0, (n_ == len(order) - 1) and d == 1))
    pcount = 0
    for _ in prep_gen(sched_blocks[0][0], sched_blocks[0][1], sched_blocks[0][2], 0):
        pass
    for k, (b, d, blk, first_of_dir, last_of_batch) in enumerate(sched_blocks):
        pp = k % 2
        bwd = (d == 1)
        if first_of_dir:
            S.op("pool", ("memset", dict(ap=STz[0][:], constant=0.0)), writes=[RST[0]])
            S.op("pool", ("memset", dict(ap=STz[1][:], constant=0.0)), writes=[RST[1]])
        stage1, stage2, stage3 = block_stages(b, d, blk, pp)
        chunks = list(range(BLK // C)) if not bwd else list(range(BLK // C - 1, -1, -1))
        cks = [(ci, slice(ci * C, (ci + 1) * C), (blk * BLK // C) + ci, (pcount + n_) % 2) for n_, ci in enumerate(chunks)]
        pcount += len(chunks)
        if k + 1 < len(sched_blocks) and sched_blocks[k + 1][0] == b:
            nb_, nd_, nblk_ = sched_blocks[k + 1][:3]
            pgen = prep_gen(nb_, nd_, nblk_, (k + 1) % 2)
        else:
            pgen = iter(())
        for _ in stage1(cks[0]):
            pass
        for idx, ck in enumerate(cks):
            fill = itertools.chain(stage3(cks[idx - 1]) if idx > 0 else iter(()), stage1(cks[idx + 1]) if idx + 1 < len(cks) else iter(()))
            for _ in stage2(ck):
                for _k in range(NFILL):
                    next(fill, None)
                for _k in range(NPREP):
                    next(pgen, None)
            for _ in fill:
                pass
        for _ in stage3(cks[-1]):
            pass
        for _ in pgen:
            pass
        if last_of_batch:
            finalize_batch(b)
            if k + 1 < len(sched_blocks):
                nb_, nd_, nblk_ = sched_blocks[k + 1][:3]
                for _ in prep_gen(nb_, nd_, nblk_, (k + 1) % 2):
                    pass
    return out_toks


NT = 2048
NTH = NT + 2


def emit_p1(S, nc, I, O):
    sb = lambda name, shape, dt=F32: nc.alloc_sbuf_tensor(name, shape, dt)
    ps = lambda name, shape, dt=F32: nc.alloc_psum_tensor(name, shape, dt)
    R = Region
    op = S.op
    mm = lambda out, l, r_, rd, wr, st=True, sp=True: op("pe", ("matmul", dict(out=out, lhsT=l, rhs=r_, start=st, stop=sp)), reads=rd, writes=wr)

    identf = sb("identf", [128, 128]); identb = sb("identb", [128, 128], BF16); Rid = R()
    gE = sb("gE", [128, 8, 1]); gO = sb("gO", [128, 8, 1]); Rg = R()
    S.dma(("dma_start", dict(out=identf[:], in_=I["c_ident"])), writes=[Rid])
    op("dve", ("tensor_copy", dict(out=identb[:], in_=identf[:])), reads=[Rid], writes=[Rid])
    S.dma(("dma_start", dict(out=gE[:, :, 0], in_=I["e_norm_g"].rearrange("(k p) -> p k", p=128), allow_slow_non_contiguous=True)), writes=[Rg])
    S.dma(("dma_start", dict(out=gO[:, :, 0], in_=I["o_norm_g"].rearrange("(k p) -> p k", p=128), allow_slow_non_contiguous=True)), writes=[Rg])
    hnT = sb("hnT", [128, 8, NTH], BF16); RhnT = [R() for _ in range(18)]
    yT = nc.dram_tensor("yT_d", [16, 128, NT], BF16).ap(); RyT = [[R() for _ in range(4)] for _ in range(16)]
    U = sb("U", [128, 4096]); RU = R()
    xt = [sb("xt%d" % i, [128, 1024]) for i in range(2)]; Rxt = [R(), R()]
    yo = [sb("yo%d" % i, [128, 512], BF16) for i in range(2)]; Ryo = [R(), R()]
    ytl = [sb("ytl%d" % i, [128, 16, 128], BF16) for i in range(2)]; Rytl = [R(), R()]
    xn = sb("xn", [128, 1024], BF16); Rxn = R()
    sq = sb("sq", [128, 1024]); Rsq = R()
    st = sb("st", [128, 8]); Rst = R()
    stg = [sb("stg%d" % i, [128, 8, 256]) for i in range(2)]; Rstg = [R(), R()]
    wbf = [sb("wbf%d" % i, [128, 8, 512], BF16) for i in range(2)]; Rwbf = [R(), R()]
    wbig = sb("wbig", [128, 16, 1024], BF16); Rwbig = R()
    t1 = sb("t1", [128, 512]); Rt1 = R()
    t1b = sb("t1b", [128, 512]); t1s = [t1, t1b]; Rt1s = [Rt1, R()]
    t2b = sb("t2b", [128, 512]); t3b = sb("t3b", [128, 512])
    t2 = sb("t2", [128, 512]); Rt2 = R()
    t3 = sb("t3", [128, 512]); Rt3 = R()
    cw = sb("cw", [128, 8, 3]); Rcw = R()
    PS_a = ps("PS_a", [128, 512]); RPa = R()
    PS_b = ps("PS_b", [128, 512]); RPb = R()
    PS_c = ps("PS_c", [128, 512]); RPc = R()
    PS_d = ps("PS_d", [128, 512]); RPd = R()
    PS_t = ps("PS_t", [128, 8, 128], BF16); RPt = R()
    PS_m = ps("PS_m", [128, 8, 128]); RPm = R()
    for j_ in range(3):
        S.dma(("dma_start", dict(out=cw[:, :, j_], in_=I["e_conv_w"][j_].rearrange("(cb p) -> p cb", p=128), allow_slow_non_contiguous=True)), writes=[Rcw])

    def norm_tile(xtile, Rx, gt, dst_fn, Rdst, nvalid=128):
        op("act", ("activation", dict(out=sq[:], in_=xtile[:], func=AF.Square)), reads=[Rx], writes=[Rsq])
        op("dve", ("reduce_sum", dict(out=st[:, 0:1], in_=sq[:], axis=AX.X)), reads=[Rsq], writes=[Rst])
        op("dve", ("tensor_scalar", dict(out=st[:, 1:2], in0=st[:, 0:1], scalar1=1.0 / 1024, scalar2=1e-6, op0=ALU.mult, op1=ALU.add)), reads=[Rst], writes=[Rst])
        op("act", ("activation", dict(out=st[:, 2:3], in_=st[:, 1:2], func=AF.Sqrt)), reads=[Rst], writes=[Rst])
        op("dve", ("reciprocal", dict(out=st[:, 3:4], in_=st[:, 2:3])), reads=[Rst], writes=[Rst])
        op("dve", ("tensor_scalar", dict(out=xn[:], in0=xtile[:], scalar1=st[:, 3:4], scalar2=None, op0=ALU.mult)), reads=[Rx, Rst], writes=[Rxn])
        for k in range(8):
            op("pe", ("transpose", dict(out=PS_t[:, k, :], in_=xn[:, k * 128:(k + 1) * 128], identity=identb[:])), reads=[Rxn, Rid], writes=[RPt])
        dst_fn(gt)

    xh = I["xh"]
    for i in range(17):
        xb, Rx = xt[i % 2], Rxt[i % 2]
        if i < 16:
            S.dma(("dma_start", dict(out=xb[:], in_=xh[1 + 128 * i: 1 + 128 * (i + 1), :])), writes=[Rx])
            def dst(gt, i=i):
                op("dve", ("tensor_tensor", dict(out=hnT[:, :, 1 + 128 * i: 1 + 128 * (i + 1)], in0=PS_t[:], in1=gt[:].to_broadcast([128, 8, 128]), op=ALU.mult)), reads=[RPt, Rg], writes=[RhnT[i]])
        else:
            op("pool", ("memset", dict(ap=xb[:], constant=0.0)), writes=[Rx])
            S.dma(("dma_start", dict(out=xb[0:1, :], in_=xh[0:1, :])), writes=[Rx])
            S.dma(("dma_start", dict(out=xb[1:2, :], in_=xh[NT + 1:NT + 2, :])), writes=[Rx])
            def dst(gt):
                op("dve", ("tensor_tensor", dict(out=hnT[:, :, 0:1], in0=PS_t[:, :, 0:1], in1=gt[:], op=ALU.mult)), reads=[RPt, Rg], writes=[RhnT[16]])
                op("dve", ("tensor_tensor", dict(out=hnT[:, :, NT + 1:NT + 2], in0=PS_t[:, :, 1:2], in1=gt[:], op=ALU.mult)), reads=[RPt, Rg], writes=[RhnT[17]])
        norm_tile(xb, Rx, gE, dst, None)
    allh = RhnT

    wi = I["e_w_in"].rearrange("(k p) (s c) -> p k s c", p=128, c=1024)

    def load_w(buf, src4, nsp):
        for s_ in range(nsp):
            sb_ = s_ % 2
            S.dma(("dma_start", dict(out=stg[sb_][:, :, 0:128], in_=src4[:, :, s_, :])), writes=[Rstg[sb_]])
            op("pool", ("tensor_copy", dict(out=wbf[buf][:, :, s_ * 128:(s_ + 1) * 128], in_=stg[sb_][:, :, 0:128])), reads=[Rstg[sb_]], writes=[Rwbf[buf]])
        return wbf[buf][:, :, 0:nsp * 128].rearrange("p k (s c) -> p k s c", c=128)

    PSc0, RPc0, PSd0, RPd0 = PS_c, RPc, PS_d, RPd
    t2s = [t2, t2b]; Rt2s = [Rt2, R()]
    t3s = [t3, t3b]; Rt3s = [Rt3, R()]
    xc = U[:, 0:NTH]
    chunksA = [(0, 512), (512, 512), (1024, 512), (1536, 512), (2048, 2)]
    for cb in range(8):
        if cb % 2 == 0:
            for s_ in range(4):
                sb_ = s_ % 2
                S.dma(("dma_start", dict(out=stg[sb_][:], in_=wi[:, :, s_, cb * 128:(cb + 2) * 128])), writes=[Rstg[sb_]])
                op("pool", ("tensor_copy", dict(out=wbf[0][:, :, s_ * 128:(s_ + 1) * 128], in_=stg[sb_][:, :, 0:128])), reads=[Rstg[sb_]], writes=[Rwbf[0]])
                op("pool", ("tensor_copy", dict(out=wbf[1][:, :, s_ * 128:(s_ + 1) * 128], in_=stg[sb_][:, :, 128:256])), reads=[Rstg[sb_]], writes=[Rwbf[1]])
        w4 = wbf[cb % 2][:, :, 0:512].rearrange("p k (s c) -> p k s c", c=128)
        Rw = Rwbf[cb % 2]
        for ci_, (c0, n) in enumerate(chunksA):
            (PA, RA_), (PB, RB_) = (((PS_a, RPa), (PS_b, RPb)) if ci_ % 2 == 0 else ((PS_c, RPc), (PS_d, RPd)))
            for k in range(8):
                mm(PA[:, 0:n], w4[:, k, 0, :], hnT[:, k, c0:c0 + n], [Rw] + allh, [RA_], st=(k == 0), sp=(k == 7))
            for k in range(8):
                mm(PB[:, 0:n], w4[:, k, 2, :], hnT[:, k, c0:c0 + n], [Rw] + allh, [RB_], st=(k == 0), sp=(k == 7))
            t1_, Rt1_ = t1s[ci_ % 2], Rt1s[ci_ % 2]
            op("act", ("activation", dict(out=t1_[:, 0:n], in_=PA[:, 0:n], func=AF.Copy)), reads=[RA_], writes=[Rt1_])
            op("dve", ("tensor_tensor", dict(out=xc[:, c0:c0 + n], in0=PB[:, 0:n], in1=t1_[:, 0:n], op=ALU.mult)), reads=[RB_, Rt1_], writes=[RU])
        for j in range(4):
            c0 = 1 + 512 * j
            (PS_c, RPc), (PS_d, RPd) = ((PSc0, RPc0), (PSd0, RPd0)) if j % 2 == 1 else ((PS_a, RPa), (PS_b, RPb))
            t2, Rt2 = t2s[j % 2], Rt2s[j % 2]
            t3, Rt3 = t3s[j % 2], Rt3s[j % 2]
            for k in range(8):
                mm(PS_c[:], w4[:, k, 1, :], hnT[:, k, c0:c0 + 512], [Rw] + allh, [RPc], st=(k == 0), sp=(k == 7))
            for k in range(8):
                mm(PS_d[:], w4[:, k, 3, :], hnT[:, k, c0:c0 + 512], [Rw] + allh, [RPd], st=(k == 0), sp=(k == 7))
            op("dve", ("tensor_scalar", dict(out=t2[:], in0=xc[:, c0 - 1:c0 + 511], scalar1=cw[:, cb, 0:1], scalar2=None, op0=ALU.mult)), reads=[RU, Rcw], writes=[Rt2])
            op("dve", ("scalar_tensor_tensor", dict(out=t2[:], in0=xc[:, c0:c0 + 512], scalar=cw[:, cb, 1:2], in1=t2[:], op0=ALU.mult, op1=ALU.add)), reads=[RU, Rcw, Rt2], writes=[Rt2])
            op("dve", ("scalar_tensor_tensor", dict(out=t2[:], in0=xc[:, c0 + 1:c0 + 513], scalar=cw[:, cb, 2:3], in1=t2[:], op0=ALU.mult, op1=ALU.add)), reads=[RU, Rcw, Rt2], writes=[Rt2])
            op("act", ("activation", dict(out=t3[:], in_=PS_d[:], func=AF.Silu)), reads=[RPd], writes=[Rt3])
            op("dve", ("tensor_tensor", dict(out=t2[:], in0=PS_c[:], in1=t2[:], op=ALU.mult)), reads=[RPc, Rt2], writes=[Rt2])
            op("pool", ("tensor_tensor", dict(out=yo[j % 2][:], in0=t2[:], in1=t3[:], op=ALU.mult)), reads=[Rt2, Rt3], writes=[Ryo[j % 2]])
            S.dma(("dma_start", dict(out=yT[cb, :, 512 * j:512 * (j + 1)], in_=yo[j % 2][:])), reads=[Ryo[j % 2]], writes=[RyT[cb][j]], q="pool")

    PS_c, RPc, PS_d, RPd = PSc0, RPc0, PSd0, RPd0
    t2, Rt2, t3, Rt3 = t2s[0], Rt2s[0], t3s[0], Rt3s[0]
    for hf in range(4):
        S.dma(("dma_start", dict(out=stg[hf % 2][:], in_=wi[:, :, 5, hf * 256:(hf + 1) * 256])), writes=[Rstg[hf % 2]])
        op("pool", ("tensor_copy", dict(out=wbig[:, 0:8, hf * 256:(hf + 1) * 256], in_=stg[hf % 2][:])), reads=[Rstg[hf % 2]], writes=[Rwbig])
    for hf in range(4):
        S.dma(("dma_start", dict(out=stg[hf % 2][:], in_=wi[:, :, 4, hf * 256:(hf + 1) * 256])), writes=[Rstg[hf % 2]])
        op("pool", ("tensor_copy", dict(out=wbig[:, 8:16, hf * 256:(hf + 1) * 256], in_=stg[hf % 2][:])), reads=[Rstg[hf % 2]], writes=[Rwbig])
    for hf in range(4):
        S.dma(("dma_start", dict(out=stg[hf % 2][:], in_=wi[:, :, 6, hf * 256:(hf + 1) * 256])), writes=[Rstg[hf % 2]])
        op("pool", ("tensor_copy", dict(out=wbf[hf // 2][:, :, (hf % 2) * 256:(hf % 2 + 1) * 256], in_=stg[hf % 2][:])), reads=[Rstg[hf % 2]], writes=[Rwbf[hf // 2]])
    wsn = sb("wsn", [128, 8, 128]); wsnb = sb("wsnb", [128, 8, 128], BF16); wsT = sb("wsT", [128, 8, 128], BF16); Rws = R()
    S.dma(("dma_start", dict(out=wsn[:], in_=I["e_sgu_w"].rearrange("g i j -> i g j"))), writes=[Rws])
    op("dve", ("tensor_copy", dict(out=wsnb[:], in_=wsn[:])), reads=[Rws], writes=[Rws])
    for g in range(8):
        op("pe", ("transpose", dict(out=PS_t[:, g, :], in_=wsnb[:, g, :], identity=identb[:])), reads=[Rws, Rid], writes=[RPt])
    op("act", ("activation", dict(out=wsT[:], in_=PS_t[:], func=AF.Copy)), reads=[RPt], writes=[Rws])
    bsB = sb("bsB", [128, 8, 128]); lnG = sb("lnG", [128, 1024]); lnB = sb("lnB", [128, 1024]); Rbc = R()
    S.dma(("dma_start", dict(out=bsB[:].rearrange("p g i -> p (g i)"), in_=I["e_sgu_b"].rearrange("g i -> (g i)").partition_broadcast(128))), writes=[Rbc])
    S.dma(("dma_start", dict(out=lnG[:], in_=I["e_sgu_ln_g"].partition_broadcast(128))), writes=[Rbc])
    S.dma(("dma_start", dict(out=lnB[:], in_=I["e_sgu_ln_b"].partition_broadcast(128))), writes=[Rbc])
    vsb = sb("vsb", [128, 1024]); Rvsb = R()
    vnb = sb("vnb", [128, 1024], BF16); Rvnb = R()
    mixall = U[:, 0:4096].rearrange("p (g t) -> p g t", g=8)
    for tg in range(4):
        for ti in range(4):
            c0 = 1 + 128 * (4 * tg + ti)
            for hf, (P_, RP_) in enumerate((((PS_a, RPa), (PS_b, RPb)) if ti % 2 == 0 else ((PS_c, RPc), (PS_d, RPd)))):
                for k in range(8):
                    mm(P_[:], hnT[:, k, c0:c0 + 128], wbig[:, k, hf * 512:(hf + 1) * 512], [Rwbig] + allh, [RP_], st=(k == 0), sp=(k == 7))
                op("act", ("activation", dict(out=vsb[:, hf * 512:(hf + 1) * 512], in_=P_[:], func=AF.Copy)), reads=[RP_], writes=[Rvsb])
            op("act", ("activation", dict(out=sq[:], in_=vsb[:], func=AF.Square)), reads=[Rvsb], writes=[Rsq])
            op("dve", ("reduce_sum", dict(out=st[:, 0:1], in_=vsb[:], axis=AX.X)), reads=[Rvsb], writes=[Rst])
            op("dve", ("reduce_sum", dict(out=st[:, 1:2], in_=sq[:], axis=AX.X)), reads=[Rsq], writes=[Rst])
            op("dve", ("tensor_scalar", dict(out=st[:, 2:3], in0=st[:, 0:1], scalar1=1.0 / 1024, scalar2=None, op0=ALU.mult)), reads=[Rst], writes=[Rst])
            op("dve", ("tensor_tensor", dict(out=st[:, 3:4], in0=st[:, 2:3], in1=st[:, 2:3], op=ALU.mult)), reads=[Rst], writes=[Rst])
            op("dve", ("scalar_tensor_tensor", dict(out=st[:, 4:5], in0=st[:, 1:2], scalar=1.0 / 1024, in1=st[:, 3:4], op0=ALU.mult, op1=ALU.subtract)), reads=[Rst], writes=[Rst])
            op("dve", ("tensor_scalar", dict(out=st[:, 4:5], in0=st[:, 4:5], scalar1=1e-5, scalar2=None, op0=ALU.add)), reads=[Rst], writes=[Rst])
            op("act", ("activation", dict(out=st[:, 5:6], in_=st[:, 4:5], func=AF.Sqrt)), reads=[Rst], writes=[Rst])
            op("dve", ("reciprocal", dict(out=st[:, 6:7], in_=st[:, 5:6])), reads=[Rst], writes=[Rst])
            op("dve", ("tensor_scalar", dict(out=vsb[:], in0=vsb[:], scalar1=st[:, 2:3], scalar2=st[:, 6:7], op0=ALU.subtract, op1=ALU.mult)), reads=[Rvsb, Rst], writes=[Rvsb])
            op("dve", ("tensor_tensor", dict(out=vsb[:], in0=vsb[:], in1=lnG[:], op=ALU.mult)), reads=[Rvsb, Rbc], writes=[Rvsb])
            op("pool", ("tensor_tensor", dict(out=vnb[:], in0=vsb[:], in1=lnB[:], op=ALU.add)), reads=[Rvsb, Rbc], writes=[Rvnb])
            for g in range(8):
                mm(PS_m[:, g, :], vnb[:, g * 128:(g + 1) * 128], wsT[:, g, :], [Rvnb, Rws], [RPm])
            op("dve", ("tensor_tensor", dict(out=mixall[:, :, ti * 128:(ti + 1) * 128], in0=PS_m[:], in1=bsB[:], op=ALU.add)), reads=[RPm, Rbc], writes=[RU])
        c0 = 1 + 512 * tg
        for g in range(8):
            buf = g % 2

            (PU, RPU), (PZ, RPZ) = ((PS_c, RPc), (PS_d, RPd)) if g % 2 == 0 else ((PS_a, RPa), (PS_b, RPb))
            t2, Rt2 = t2s[g % 2], Rt2s[g % 2]
            t3, Rt3 = t3s[g % 2], Rt3s[g % 2]
            for k in range(8):
                mm(PU[:], wbig[:, 8 + k, g * 128:(g + 1) * 128], hnT[:, k, c0:c0 + 512], [Rwbig] + allh, [RPU], st=(k == 0), sp=(k == 7))
            for k in range(8):
                mm(PZ[:], wbf[g // 4][:, k, (g % 4) * 128:(g % 4 + 1) * 128], hnT[:, k, c0:c0 + 512], [Rwbf[g // 4]] + allh, [RPZ], st=(k == 0), sp=(k == 7))
            op("act", ("activation", dict(out=t3[:], in_=PZ[:], func=AF.Silu)), reads=[RPZ], writes=[Rt3])
            op("dve", ("tensor_tensor", dict(out=t2[:], in0=PU[:], in1=mixall[:, g, :], op=ALU.mult)), reads=[RPU, RU], writes=[Rt2])
            op("pool", ("tensor_tensor", dict(out=yo[g % 2][:], in0=t2[:], in1=t3[:], op=ALU.mult)), reads=[Rt2, Rt3], writes=[Ryo[g % 2]])
            S.dma(("dma_start", dict(out=yT[8 + g, :, 512 * tg:512 * (tg + 1)], in_=yo[g % 2][:])), reads=[Ryo[g % 2]], writes=[RyT[8 + g][tg]], q="pool")

    wo = I["e_w_out"].rearrange("(k p) n -> p k n", p=128)
    for q in range(2):
        for hf in range(4):
            S.dma(("dma_start", dict(out=stg[hf % 2][:], in_=wo[:, 8 * q:8 * q + 8, hf * 256:(hf + 1) * 256])), writes=[Rstg[hf % 2]])
            op("pool", ("tensor_copy", dict(out=wbig[:, 8 * q:8 * q + 8, hf * 256:(hf + 1) * 256], in_=stg[hf % 2][:])), reads=[Rstg[hf % 2]], writes=[Rwbig])
    ally = [r for row in RyT for r in row]
    h1ts = [sb("h1t%d" % i, [128, 1024]) for i in range(2)]; Rh1s = [R(), R()]
    for i in range(16):
        xb, Rx = xt[i % 2], Rxt[i % 2]
        h1t, Rh1 = h1ts[i % 2], Rh1s[i % 2]
        S.dma(("dma_start", dict(out=xb[:], in_=xh[1 + 128 * i: 1 + 128 * (i + 1), :])), writes=[Rx])
        S.dma(("dma_start", dict(out=ytl[i % 2][:], in_=yT[:, :, 128 * i:128 * (i + 1)].rearrange("k p t -> p k t"))), reads=ally, writes=[Rytl[i % 2]])
        for hf, (P_, RP_) in enumerate((((PS_a, RPa), (PS_b, RPb)) if i % 2 == 0 else ((PS_c, RPc), (PS_d, RPd)))):
            for k in range(16):
                mm(P_[:], ytl[i % 2][:, k, :], wbig[:, k, hf * 512:(hf + 1) * 512], [Rwbig, Rytl[i % 2]], [RP_], st=(k == 0), sp=(k == 15))
            op("dve", ("tensor_tensor", dict(out=h1t[:, hf * 512:(hf + 1) * 512], in0=P_[:], in1=xb[:, hf * 512:(hf + 1) * 512], op=ALU.add)), reads=[RP_, Rx], writes=[Rh1])
        S.dma(("dma_start", dict(out=O["h1"][128 * i:128 * (i + 1), :], in_=h1t[:])), reads=[Rh1], q="act")

        def dst(gt, i=i):
            op("dve", ("tensor_tensor", dict(out=hnT[:, :, 1 + 128 * i: 1 + 128 * (i + 1)], in0=PS_t[:], in1=gt[:].to_broadcast([128, 8, 128]), op=ALU.mult)), reads=[RPt, Rg], writes=[RhnT[i]])
        norm_tile(h1t, Rh1, gO, dst, None)

    wi1 = I["o_w_in"].rearrange("(k p) n -> p k n", p=128)
    blocks = [(c * 128, c * 128, False) for c in range(25)]
    blocks += [(3200 + c * 128, 3200 + c * 128, True) for c in range(8)]
    blocks += [(4736 + c * 128, 4224 + c * 128, True) for c in range(4)]
    ob = [sb("ob%d" % i, [128, 512]) for i in range(2)]; Rob = [R(), R()]
    oi = 0
    toks = []
    bi = 0
    nblocks = len(blocks)
    pairbuf = 0
    while bi < nblocks:
        sc, dr, act = blocks[bi]
        paired = (bi + 1 < nblocks) and (blocks[bi + 1][0] == sc + 128) and (blocks[bi + 1][2] == act)
        ncol = 256 if paired else 128
        buf = pairbuf % 2
        pairbuf += 1
        S.dma(("dma_start", dict(out=stg[buf][:, :, 0:ncol], in_=wi1[:, :, sc:sc + ncol])), writes=[Rstg[buf]])
        op("pool", ("tensor_copy", dict(out=wbf[buf][:, :, 0:ncol], in_=stg[buf][:, :, 0:ncol])), reads=[Rstg[buf]], writes=[Rwbf[buf]])
        for sub in range(2 if paired else 1):
            sc_, dr_, act_ = blocks[bi + sub]
            for j in range(4):
                P_, RP_ = ((PS_a, RPa), (PS_b, RPb), (PS_c, RPc), (PS_d, RPd))[j]
                c0 = 1 + 512 * j
                for k in range(8):
                    mm(P_[:], wbf[buf][:, k, sub * 128:(sub + 1) * 128], hnT[:, k, c0:c0 + 512], [Rwbf[buf]] + allh, [RP_], st=(k == 0), sp=(k == 7))
                o_, Ro = ob[oi % 2], Rob[oi % 2]
                oi += 1
                op("act", ("activation", dict(out=o_[:], in_=P_[:], func=(AF.Silu if act_ else AF.Copy))), reads=[RP_], writes=[Ro])
                toks.append(S.dma(("dma_start", dict(out=O["pT"][dr_:dr_ + 128, 512 * j:512 * (j + 1)], in_=o_[:])), reads=[Ro], q="act"))
        bi += 2 if paired else 1
    for hf in range(2):
        S.dma(("dma_start", dict(out=stg[hf][:], in_=wi1[:, :, 4224 + 256 * hf:4224 + 256 * (hf + 1)])), writes=[Rstg[hf]])
        op("pool", ("tensor_copy", dict(out=wbf[0][:, :, 256 * hf:256 * (hf + 1)], in_=stg[hf][:])), reads=[Rstg[hf]], writes=[Rwbf[0]])
    for i in range(16):
        c0 = 1 + 128 * i
        P_, RP_ = ((PS_a, RPa), (PS_b, RPb))[i % 2]
        for k in range(8):
            mm(P_[:], hnT[:, k, c0:c0 + 128], wbf[0][:, k, :], [Rwbf[0]] + allh, [RP_], st=(k == 0), sp=(k == 7))
        o_, Ro = ob[oi % 2], Rob[oi % 2]
        oi += 1
        op("act", ("activation", dict(out=o_[:], in_=P_[:], func=AF.Copy)), reads=[RP_], writes=[Ro])
        toks.append(S.dma(("dma_start", dict(out=O["fd"][128 * i:128 * (i + 1), :], in_=o_[:])), reads=[Ro], q="act"))
    return toks


def full_barrier(S):
    keys = list(S.cnt.items())
    for e in S.ENGS:
        waits = []
        for k, v in keys:
            if k == e:
                continue
            if S.seen[e].get(k, 0) < v:
                S.seen[e][k] = v
                waits.append((k, v))
        if waits:
            S.prog[e].append([waits, None, ("_none", 0)])


def emit_fnet(S, nc, I, ydT):
    R = Region
    op = S.op
    mm = lambda out, l, r_, rd, wr, st=True, sp=True: op("pe", ("matmul", dict(out=out, lhsT=l, rhs=r_, start=st, stop=sp)), reads=rd, writes=wr)
    toks = []
    with ExitStack() as es:
        sb = lambda name, shape, dt=F32: es.enter_context(nc.sbuf_tensor(name, shape, dt))
        ps = lambda name, shape, dt=F32: es.enter_context(nc.psum_tensor(name, shape, dt))
        xs = sb("f_xs", [128, 4096]); Rxs = R()
        xb = sb("f_xb", [128, 64, 128], BF16); Rxb = R()
        Fb = sb("f_F", [128, 256], BF16); RF = R()
        A_sb = sb("f_A", [64, 128, 256], BF16); RA = R()
        PQ = sb("f_PQ", [128, 2, 64, 128], BF16); RPQ = R()
        Tg = [[sb("f_T%d%d" % (i, j), [64, 16, 128], BF16) for j in range(2)] for i in range(2)]; RTg = [R(), R()]
        wf32 = sb("f_w32", [128, 128]); wfb = sb("f_wb", [128, 128], BF16); Rwf = R()
        Ccb = sb("f_Cc", [128, 128], BF16); mScb = sb("f_mSc", [128, 128], BF16); Rcs = R()
        Gb = sb("f_G", [128, 256], BF16); RG = R()
        ob = [sb("f_ob%d" % i, [128, 512]) for i in range(2)]; Rob = [R(), R()]
        PS = [ps("f_ps%d" % i, [128, 512]) for i in range(2)]; RPS = [R(), R()]
        S.dma(("dma_start", dict(out=Fb[:], in_=I["c_F"])), writes=[RF])
        S.dma(("dma_start", dict(out=Ccb[:], in_=I["c_Cc"])), writes=[Rcs])
        S.dma(("dma_start", dict(out=mScb[:], in_=I["c_mSc"])), writes=[Rcs])
        S.dma(("dma_start", dict(out=wf32[:], in_=I["fw"])), writes=[Rwf])
        op("dve", ("tensor_copy", dict(out=wfb[:], in_=wf32[:])), reads=[Rwf], writes=[Rwf])
        xbf = xb[:].rearrange("p l c -> p (l c)")
        for hf in range(2):
            S.dma(("dma_start", dict(out=xs[:], in_=I["fx"][:, hf * 4096:(hf + 1) * 4096])), writes=[Rxs])
            op("pool", ("tensor_copy", dict(out=xbf[:, hf * 4096:(hf + 1) * 4096], in_=xs[:])), reads=[Rxs], writes=[Rxb])
        for c2 in range(64):
            P_, RP_ = PS[c2 % 2], RPS[c2 % 2]
            for j in range(2):
                mm(P_[0:64, j * 256:(j + 1) * 256], xb[:, :, 2 * c2 + j], Fb[:], [Rxb, RF], [RP_])
            op("act" if c2 % 2 == 0 else "dve", ("activation", dict(out=A_sb[0:64, 2 * c2:2 * c2 + 2, :], in_=P_[0:64, :].rearrange("p (j k) -> p j k", j=2), func=AF.Copy)) if c2 % 2 == 0 else
               ("tensor_copy", dict(out=A_sb[0:64, 2 * c2:2 * c2 + 2, :], in_=P_[0:64, :].rearrange("p (j k) -> p j k", j=2))), reads=[RP_], writes=[RA])
        T1d = I["c_T1"].rearrange("p (k h) -> p k h", h=128)
        T2d = I["c_T2"].rearrange("p (k h) -> p k h", h=128)
        ei = 0
        for grp in range(8):
            tb = grp % 2
            S.dma(("dma_start", dict(out=Tg[tb][0][:], in_=T1d[:, grp * 16:(grp + 1) * 16, :])), writes=[RTg[tb]])
            S.dma(("dma_start", dict(out=Tg[tb][1][:], in_=T2d[:, grp * 16:(grp + 1) * 16, :])), writes=[RTg[tb]])
            for q in range(4):
                P_, RP_ = PS[ei % 2], RPS[ei % 2]
                for j in range(4):
                    kk_ = q * 4 + j
                    kl = grp * 16 + kk_
                    mm(P_[:, j * 128:(j + 1) * 128], A_sb[0:64, :, kl], Tg[tb][0][0:64, kk_, :], [RA, RTg[tb]], [RP_], st=True, sp=False)
                    mm(P_[:, j * 128:(j + 1) * 128], A_sb[0:64, :, 128 + kl], Tg[tb][1][0:64, kk_, :], [RA, RTg[tb]], [RP_], st=False, sp=True)
                kl0 = grp * 16 + q * 4
                for qq in range(2):
                    op("act" if qq == 0 else "dve",
                       ("activation", dict(out=PQ[:, qq, :, kl0:kl0 + 4].rearrange("p h l -> p l h"), in_=P_[:].rearrange("p (l q h) -> p l q h", l=4, q=2)[:, :, qq, :], func=AF.Copy)) if qq == 0 else
                       ("tensor_copy", dict(out=PQ[:, qq, :, kl0:kl0 + 4].rearrange("p h l -> p l h"), in_=P_[:].rearrange("p (l q h) -> p l q h", l=4, q=2)[:, :, qq, :])),
                       reads=[RP_], writes=[RPQ])
                ei += 1
        P_, RP_ = PS[0], RPS[0]
        mm(P_[:, 0:128], Ccb[:], wfb[:], [Rcs, Rwf], [RP_])
        mm(P_[:, 128:256], mScb[:], wfb[:], [Rcs, Rwf], [RP_])
        op("act", ("activation", dict(out=Gb[:], in_=P_[:, 0:256], func=AF.Copy)), reads=[RP_], writes=[RG])
        for t4 in range(16):
            P_, RP_ = PS[(t4 + 1) % 2], RPS[(t4 + 1) % 2]
            for j in range(4):
                kh = 4 * t4 + j
                mm(P_[:, j * 128:(j + 1) * 128], Gb[:, 0:128], PQ[:, 0, kh, :], [RG, RPQ], [RP_], st=True, sp=False)
                mm(P_[:, j * 128:(j + 1) * 128], Gb[:, 128:256], PQ[:, 1, kh, :], [RG, RPQ], [RP_], st=False, sp=True)
            o_, Ro = ob[t4 % 2], Rob[t4 % 2]
            op("act", ("activation", dict(out=o_[:], in_=P_[:], func=AF.Copy)), reads=[RP_], writes=[Ro])
            toks.append(S.dma(("dma_start", dict(out=ydT[:, 512 * t4:512 * (t4 + 1)], in_=o_[:])), reads=[Ro]))
    full_barrier(S)
    return toks


def emit_p3(S, nc, I, yout):
    sb = lambda name, shape, dt=F32: nc.alloc_sbuf_tensor(name, shape, dt)
    ps = lambda name, shape, dt=F32: nc.alloc_psum_tensor(name, shape, dt)
    R = Region
    op = S.op
    mm = lambda out, l, r_, rd, wr, st=True, sp=True: op("pe", ("matmul", dict(out=out, lhsT=l, rhs=r_, start=st, stop=sp)), reads=rd, writes=wr)
    stg = [sb("stg%d" % i, [128, 8, 256]) for i in range(2)]; Rstg = [R(), R()]
    wO = sb("wO", [128, 12, 1024], BF16); RwO = R()
    gN = sb("gN", [128, 1024]); RgN = R()
    gt_all = sb("gt_all", [128, 12, 2048], BF16); Rgt = R()
    ya = [sb("ya%d" % i, [128, 512]) for i in range(2)]; Rya = [R(), R()]
    ga = [sb("ga%d" % i, [128, 512]) for i in range(2)]; Rga = [R(), R()]
    h1t = [sb("h1t%d" % i, [128, 1024]) for i in range(2)]; Rh1 = [R(), R()]
    h2 = sb("h2", [128, 1024]); Rh2 = R()
    sq = sb("sq", [128, 1024]); Rsq = R()
    st = sb("st", [128, 8]); Rst = R()
    yo = [sb("yo%d" % i, [128, 1024]) for i in range(2)]; Ryo = [R(), R()]
    PS_a = ps("PS_a", [128, 512]); RPa = R()
    PS_b = ps("PS_b", [128, 512]); RPb = R()
    wo3 = I["o_w_out"].rearrange("(k p) n -> p k n", p=128)
    si = 0
    for (k0, nk) in ((0, 8), (8, 4)):
        for cq in range(4):
            b_ = si % 2; si += 1
            S.dma(("dma_start", dict(out=stg[b_][:, 0:nk, :], in_=wo3[:, k0:k0 + nk, cq * 256:(cq + 1) * 256])), writes=[Rstg[b_]])
            op("pool", ("tensor_copy", dict(out=wO[:, k0:k0 + nk, cq * 256:(cq + 1) * 256], in_=stg[b_][:, 0:nk, :])), reads=[Rstg[b_]], writes=[RwO])
    S.dma(("dma_start", dict(out=gN[:], in_=I["final_norm_g"].partition_broadcast(128))), writes=[RgN])
    ii = 0
    for blk in range(12):
        src = I["ycT"][blk * 128:(blk + 1) * 128] if blk < 8 else I["ydT"][(blk - 8) * 128:(blk - 7) * 128]
        gsrc = I["gT"][blk * 128:(blk + 1) * 128]
        for j in range(4):
            b_ = ii % 2; ii += 1
            S.dma(("dma_start", dict(out=ya[b_][:], in_=src[:, 512 * j:512 * (j + 1)])), writes=[Rya[b_]])
            S.dma(("dma_start", dict(out=ga[b_][:], in_=gsrc[:, 512 * j:512 * (j + 1)])), writes=[Rga[b_]])
            op("dve" if ii % 2 else "pool", ("tensor_tensor", dict(out=gt_all[:, blk, 512 * j:512 * (j + 1)], in0=ya[b_][:], in1=ga[b_][:], op=ALU.mult)), reads=[Rya[b_], Rga[b_]], writes=[Rgt])
    toks = []
    for i in range(16):
        hb, Rh = h1t[i % 2], Rh1[i % 2]
        S.dma(("dma_start", dict(out=hb[:], in_=I["h1"][128 * i:128 * (i + 1), :])), writes=[Rh])
        for hf, (P_, RP_) in enumerate(((PS_a, RPa), (PS_b, RPb))):
            for k in range(12):
                mm(P_[:], gt_all[:, k, 128 * i:128 * (i + 1)], wO[:, k, hf * 512:(hf + 1) * 512], [Rgt, RwO], [RP_], st=(k == 0), sp=(k == 11))
            op("dve", ("tensor_tensor", dict(out=h2[:, hf * 512:(hf + 1) * 512], in0=P_[:], in1=hb[:, hf * 512:(hf + 1) * 512], op=ALU.add)), reads=[RP_, Rh], writes=[Rh2])
        op("act", ("activation", dict(out=sq[:], in_=h2[:], func=AF.Square)), reads=[Rh2], writes=[Rsq])
        op("dve", ("reduce_sum", dict(out=st[:, 0:1], in_=sq[:], axis=AX.X)), reads=[Rsq], writes=[Rst])
        op("dve", ("tensor_scalar", dict(out=st[:, 1:2], in0=st[:, 0:1], scalar1=1.0 / 1024, scalar2=1e-6, op0=ALU.mult, op1=ALU.add)), reads=[Rst], writes=[Rst])
        op("act", ("activation", dict(out=st[:, 2:3], in_=st[:, 1:2], func=AF.Sqrt)), reads=[Rst], writes=[Rst])
        op("dve", ("reciprocal", dict(out=st[:, 3:4], in_=st[:, 2:3])), reads=[Rst], writes=[Rst])
        op("dve", ("tensor_scalar", dict(out=h2[:], in0=h2[:], scalar1=st[:, 3:4], scalar2=None, op0=ALU.mult)), reads=[Rh2, Rst], writes=[Rh2])
        o_, Ro = yo[i % 2], Ryo[i % 2]
        op("pool", ("tensor_tensor", dict(out=o_[:], in0=h2[:], in1=gN[:], op=ALU.mult)), reads=[Rh2, RgN], writes=[Ro])
        toks.append(S.dma(("dma_start", dict(out=yout[128 * i:128 * (i + 1), :], in_=o_[:])), reads=[Ro], q="pool"))
    return toks


def _mk(nc, name, shape, dt=None, out=False):
    return nc.dram_tensor(name, list(shape), dt or F32, kind=("ExternalOutput" if out else "ExternalInput")).ap()


W1 = ["e_norm_g", "e_w_in", "e_conv_w", "e_sgu_ln_g", "e_sgu_ln_b", "e_sgu_w", "e_sgu_b", "e_w_out", "o_norm_g", "o_w_in"]


def build_l1(shapes):
    nc = bass.Bass("TRN2", target_bir_lowering=False)
    I = {"xh": _mk(nc, "xh", [2050, 1024]), "c_ident": _mk(nc, "c_ident", [128, 128])}
    for n in W1:
        I[n] = _mk(nc, n, shapes[n])
    O = {"h1": _mk(nc, "h1", [2048, 1024], out=True), "pT": _mk(nc, "pT", [4736, 2048], out=True),
         "fd": _mk(nc, "fd", [2048, 512], out=True)}
    S = Sched(nc)
    toks = emit_p1(S, nc, I, O)
    S.barrier_on("sp", toks)
    S.finalize()
    return nc


def build_l2(consts):
    NB, T = 2, 8192
    nc = bass.Bass("TRN2", target_bir_lowering=False)
    pr, pk, pv, pwa = (_mk(nc, n, [128, NB, T + 2]) for n in ("pr", "pk", "pv", "pwa"))
    prm = _mk(nc, "prm", [128, 17]); w2a2 = _mk(nc, "w2a2", [128, 2, 128])
    A = {k: _mk(nc, k, v.shape) for k, v in consts.items()}
    FI = {"fx": _mk(nc, "fx", [128, 8192]), "fw": _mk(nc, "fw", [128, 128]),
          "c_F": _mk(nc, "c_F", [128, 256], BF16), "c_T1": _mk(nc, "c_T1", [64, 16384], BF16),
          "c_T2": _mk(nc, "c_T2", [64, 16384], BF16), "c_Cc": _mk(nc, "c_Cc", [128, 128], BF16),
          "c_mSc": _mk(nc, "c_mSc", [128, 128], BF16)}
    yout = _mk(nc, "yout", [128, NB, T], out=True)
    ydT = _mk(nc, "ydT", [128, T], out=True)
    S = Sched(nc)
    toks = emit_fnet(S, nc, FI, ydT)
    toks += emit_rwkv(S, nc, A, pr, pk, pv, pwa, prm, w2a2, yout, NB, T)
    S.barrier_on("sp", toks)
    S.finalize()
    return nc


def build_l3():
    nc = bass.Bass("TRN2", target_bir_lowering=False)
    I = {"ycT": _mk(nc, "ycT", [1024, 2048]), "ydT": _mk(nc, "ydT", [512, 2048]), "gT": _mk(nc, "gT", [1536, 2048]),
         "h1": _mk(nc, "h1", [2048, 1024]), "o_w_out": _mk(nc, "o_w_out", [1536, 1024]),
         "final_norm_g": _mk(nc, "final_norm_g", [1024])}
    y = _mk(nc, "y", [2048, 1024], out=True)
    S = Sched(nc)
    toks = emit_p3(S, nc, I, y)
    S.barrier_on("sp", toks)
    S.finalize()
    return nc


def fnet_tables():
    import ml_dtypes
    N = 8192
    nh = np.arange(128); kl = np.arange(128)
    ang = 2 * np.pi * np.outer(nh, kl) / 128
    F = np.concatenate([np.cos(ang), np.sin(ang)], axis=1)
    nl = np.arange(64)[:, None, None]; klo = np.arange(128)[None, :, None]; kh = np.arange(64)[None, None, :]
    beta = 2 * np.pi * ((nl * (klo + 128 * kh)) % N) / N
    T1 = np.concatenate([np.cos(beta), np.sin(beta)], axis=2).reshape(64, 16384)
    T2 = np.concatenate([-np.sin(beta), np.cos(beta)], axis=2).reshape(64, 16384)
    c = np.arange(128); phi = 2 * np.pi * np.outer(c, c) / 128
    nrm = 1 / np.sqrt(N * 128)
    bf = lambda a: np.ascontiguousarray(a.astype(np.float32)).astype(ml_dtypes.bfloat16)
    return {"c_F": bf(F), "c_T1": bf(T1), "c_T2": bf(T2), "c_Cc": bf(np.cos(phi) * nrm), "c_mSc": bf(-np.sin(phi) * nrm)}


def kernel(**inputs):
    f32 = lambda a: np.ascontiguousarray(np.asarray(a), dtype=np.float32)
    inp = {k: f32(v) for k, v in inputs.items()}
    x = inp["x"]
    ncores = 8
    cores = list(range(ncores))
    w1 = {n: np.ascontiguousarray(inp[n][0]) for n in W1}
    ident = np.eye(128, dtype=np.float32)
    maps = []
    for c in cores:
        b, s0 = c // 4, (c % 4) * 2048
        xh = np.zeros((2050, 1024), np.float32)
        xh[1:2049] = x[b, s0:s0 + 2048]
        if s0 > 0:
            xh[0] = x[b, s0 - 1]
        if s0 + 2048 < 8192:
            xh[2049] = x[b, s0 + 2048]
        m = {"xh": xh, "c_ident": ident}
        m.update(w1)
        maps.append(m)
    nc1 = build_l1({n: w1[n].shape for n in W1})
    r1 = run_bass_kernel_spmd(nc1, maps, core_ids=cores).results
    PT = np.concatenate([np.asarray(r["pT"]) for r in r1], axis=1)
    FD = np.concatenate([np.asarray(r["fd"]) for r in r1], axis=0)
    consts = build_consts_np()
    ft = fnet_tables()
    mu, w0, w2, a0, a2 = inp["o_mu"][0], inp["o_w0"][0], inp["o_w2"][0], inp["o_a0"][0], inp["o_a2"][0]
    k_k, k_a, r_k = inp["o_k_k"][0], inp["o_k_a"][0], inp["o_r_k"][0].reshape(-1)
    lg, lb = inp["o_lnx_g"][0], inp["o_lnx_b"][0]
    PT3 = PT.reshape(4736, 2, 8192)
    pad = lambda a: np.ascontiguousarray(np.pad(a, ((0, 0), (0, 0), (1, 1))))
    maps = []
    for c in cores:
        ch = slice(c * 128, (c + 1) * 128)
        m = {"pr": pad(PT3[0:1024][ch]), "pk": pad(PT3[1024:2048][ch]), "pv": pad(PT3[2048:3072][ch]),
             "pwa": pad(PT3[3072:3200])}
        prm = np.zeros((128, 17), np.float32)
        for d in range(2):
            prm[:, 0 + d] = mu[d, 0:1024][ch]; prm[:, 2 + d] = mu[d, 1024:2048][ch]; prm[:, 4 + d] = mu[d, 2048:3072][ch]
            prm[:, 6 + d] = mu[d, 3072:3200]; prm[:, 8 + d] = w0[d][ch]; prm[:, 10 + d] = a0[d][ch]
        prm[:, 12] = k_k[ch]; prm[:, 13] = k_a[ch]; prm[:, 14] = r_k[ch]; prm[:, 15] = lg[ch]; prm[:, 16] = lb[ch]
        m["prm"] = prm
        m["w2a2"] = np.ascontiguousarray(np.concatenate([w2[:, :, ch], a2[:, :, ch]], axis=1).transpose(1, 0, 2))
        m.update(consts)
        b, g = c // 4, c % 4
        m["fx"] = np.ascontiguousarray(FD[b * 8192:(b + 1) * 8192, g * 128:(g + 1) * 128]).reshape(128, 8192)
        m["fw"] = np.ascontiguousarray(inp["o_fnet_w"][0, g])
        m.update(ft)
        maps.append(m)
    nc2 = build_l2(consts)
    r2 = run_bass_kernel_spmd(nc2, maps, core_ids=cores).results
    YC = np.concatenate([np.asarray(r["yout"]).reshape(128, 16384) for r in r2], axis=0)
    YD = np.concatenate([np.concatenate([np.asarray(r2[b * 4 + g]["ydT"]) for g in range(4)], axis=0) for b in range(2)], axis=1)
    maps = []
    for c in cores:
        ts = slice(c * 2048, (c + 1) * 2048)
        maps.append({"ycT": np.ascontiguousarray(YC[:, ts]), "ydT": np.ascontiguousarray(YD[:, ts]),
                     "gT": np.ascontiguousarray(PT[3200:4736, ts]), "h1": np.asarray(r1[c]["h1"]),
                     "o_w_out": np.ascontiguousarray(inp["o_w_out"][0]), "final_norm_g": inp["final_norm_g"]})
    nc3 = build_l3()
    r3 = run_bass_kernel_spmd(nc3, maps, core_ids=cores).results
    y = np.concatenate([np.asarray(r["y"]) for r in r3], axis=0).reshape(2, 8192, 1024)
    return y.astype(np.float32)
```

```python
from contextlib import ExitStack
import itertools
import numpy as np
import concourse.bass as bass
import concourse.mybir as mybir
from concourse.bass_utils import run_bass_kernel_spmd


F32 = mybir.dt.float32
BF16 = mybir.dt.bfloat16
AF = mybir.ActivationFunctionType
ALU = mybir.AluOpType
AX = mybir.AxisListType

N_DMA_SEMS = 8


class Region:
    __slots__ = ("w", "r", "name")

    def __init__(self, name=""):
        self.w = None
        self.r = {}
        self.name = name


class Sched:
    ENGS = ("pe", "dve", "act", "pool", "sp")

    def __init__(self, nc):
        self.nc = nc
        self.prog = {e: [] for e in self.ENGS}
        self.cnt = {}
        self.seen = {e: {} for e in self.ENGS}
        self.dma_rr = {e: 0 for e in self.ENGS}
        self.dma_last = {}
        self.same_engine_raw = True
        self.cut = 0
        self.raw_only = True
        self.nrec = 0
        self.log = []

    def _collect(self, eng, mykey, reads, writes):
        waits = {}

        def need(tok, kind):
            if tok is None:
                return
            k, v = tok
            if k == mykey:
                if eng == "pe":
                    return
                if not self.same_engine_raw:
                    return
                if self.raw_only and kind != "raw":
                    return
            if waits.get(k, 0) < v:
                waits[k] = v

        for R in reads:
            need(R.w, "raw")
        for R in writes:
            need(R.w, "waw")
            for k, v in R.r.items():
                need((k, v), "war")
        out = []
        seen = self.seen[eng]
        for k, v in waits.items():
            if seen.get(k, 0) < v:
                seen[k] = v
                out.append((k, v))
        return out

    def _commit(self, tok, reads, writes):
        for R in writes:
            R.w = tok
            R.r = {}
        k, v = tok
        for R in reads:
            if R.r.get(k, 0) < v:
                R.r[k] = v

    def op(self, eng, fn, reads=(), writes=()):
        self.nrec += 1
        if self.cut and self.nrec > self.cut:
            return None
        if self.cut:
            self.log.append((self.nrec, eng, fn[0] if isinstance(fn, tuple) else "fn", str(fn[1].get("out", ""))[:120] if isinstance(fn, tuple) else ""))
        key = eng
        waits = self._collect(eng, key, reads, writes)
        idx = self.cnt.get(key, 0) + 1
        self.cnt[key] = idx
        tok = (key, idx)
        self.prog[eng].append([waits, fn, tok])
        self._commit(tok, reads, writes)
        return tok

    def dma(self, fn, reads=(), writes=(), q="sp"):
        self.nrec += 1
        if self.cut and self.nrec > self.cut:
            return None
        i = self.dma_rr[q]
        self.dma_rr[q] = (i + 1) % N_DMA_SEMS
        key = "dma_%s_%d" % (q, i)
        waits = self._collect(q, key, reads, writes)
        prev = self.cnt.get(key, 0)
        if prev > 0 and self.seen[q].get(key, 0) < prev:
            self.seen[q][key] = prev
            waits.append((key, prev))
        idx = prev + 1
        self.cnt[key] = idx
        tok = (key, idx)
        self.prog[q].append([waits, fn, tok])
        self._commit(tok, reads, writes)
        return tok

    def finalize(self):
        nc = self.nc
        waited = {}
        for e in self.ENGS:
            for waits, fn, tok in self.prog[e]:
                for k, v in waits:
                    waited.setdefault(k, set()).add(v)
        self.final_waits = []
        sem_of = {}
        val_of = {}
        for k, s in waited.items():
            sem_of[k] = nc.alloc_semaphore("s_" + k)
            isdma = k.startswith("dma_")
            step = 16 if isdma else 1
            if isdma:
                val_of[k] = None
            else:
                val_of[k] = {v: (i + 1) for i, v in enumerate(sorted(s))}
        engobj = {"pe": nc.tensor, "dve": nc.vector, "act": nc.scalar,
                  "pool": nc.gpsimd, "sp": nc.sync}

        def value(k, v):
            if val_of[k] is None:
                return 16 * v
            return val_of[k][v]

        def emit(e):
            def body(eng):
                for waits, fn, tok in self.prog[e]:
                    for k, v in waits:
                        eng.wait_ge(sem_of[k], value(k, v))
                    if fn is None:
                        continue
                    if isinstance(fn, tuple):
                        ins = getattr(eng, fn[0])(**fn[1])
                    else:
                        ins = fn(eng)
                    k, v = tok
                    if k in sem_of:
                        if val_of[k] is None:
                            ins.then_inc(sem_of[k], 16)
                        elif v in val_of[k]:
                            ins.then_inc(sem_of[k], 1)
            return body

        with nc.Block() as block:
            for e, dec in (("sp", block.sync), ("pe", block.tensor), ("dve", block.vector),
                           ("act", block.scalar), ("pool", block.gpsimd)):
                if self.prog[e]:
                    dec(emit(e))
        self.n_sems = len(sem_of)
        return self.n_sems

    def barrier_on(self, eng, toks):
        waits = []
        for tk in toks:
            if tk is None:
                continue
            k, v = tk
            if self.seen[eng].get(k, 0) < v:
                self.seen[eng][k] = v
                waits.append((k, v))
        if waits:
            self.prog[eng].append([waits, None, ("_none", 0)])


C = 128
BLK = 512
NEG_E = -float(np.exp(-0.5))
GN_EPS = 64e-5


def build_consts_np():
    idx = np.arange(128)
    lt = (idx[:, None] < idx[None, :]).astype(np.float32)
    le = (idx[:, None] <= idx[None, :]).astype(np.float32)
    gt = lt.T.copy()
    ge = le.T.copy()
    m4f = np.stack([lt, gt, gt, le], axis=1)
    m4b = np.stack([gt, lt, lt, ge], axis=1)
    mk = np.stack([le, ge], axis=1)
    ident = np.eye(128, dtype=np.float32)
    bd = np.kron(np.eye(2, dtype=np.float32), np.ones((64, 64), np.float32))
    scanm = np.ones((128, BLK), np.float32)
    scanm[:, ::C] = 0.0
    return {"c_m4": np.stack([m4f, m4b], axis=1).reshape(128, 2 * 4 * 128).copy(),
            "c_mk": mk.reshape(128, 256).copy(), "c_ident": ident, "c_bd": bd, "c_scanm": scanm}


XST = False


def emit_rwkv(S, nc, A, pr, pk, pv, pwa, prm, w2a2, yout, NB, T):
    sb = lambda name, shape, dt=F32: nc.alloc_sbuf_tensor(name, shape, dt)
    ps = lambda name, shape, dt=F32: nc.alloc_psum_tensor(name, shape, dt)
    R = Region
    nblk = T // BLK

    m4f = sb("m4f", [128, 2, 4, 128]); Rm4 = R()
    mkf = sb("mkf", [128, 2, 128]); Rmk = R()
    identf = sb("identf", [128, 128]); Ridf = R()
    identb = sb("identb", [128, 128], BF16); Ridb = R()
    bdf = sb("bdf", [128, 128]); Rbd = R()
    bdr = sb("bdr", [128, 128]); Rbdr = R()
    bdm = sb("bdm", [128, 128]); Rbdm = R()
    scanm = sb("scanm", [128, BLK]); Rsc = R()
    prmt = sb("prmt", [128, 17]); Rprm = R()
    w2f = sb("w2f", [128, 2, 128]); Rw2f = R()
    w2b = sb("w2b", [128, 2, 128], BF16); Rw2b = R()
    S.dma(("dma_start", dict(out=m4f[:].rearrange("p a b c -> p (a b c)"), in_=A["c_m4"])), writes=[Rm4])
    S.dma(("dma_start", dict(out=mkf[:].rearrange("p a c -> p (a c)"), in_=A["c_mk"])), writes=[Rmk])
    S.dma(("dma_start", dict(out=identf[:], in_=A["c_ident"])), writes=[Ridf])
    S.dma(("dma_start", dict(out=bdf[:], in_=A["c_bd"])), writes=[Rbd])
    S.dma(("dma_start", dict(out=scanm[:], in_=A["c_scanm"])), writes=[Rsc])
    S.dma(("dma_start", dict(out=prmt[:], in_=prm)), writes=[Rprm])
    S.dma(("dma_start", dict(out=w2f[:], in_=w2a2)), writes=[Rw2f])
    S.op("dve", ("tensor_copy", dict(out=identb[:], in_=identf[:])), reads=[Ridf], writes=[Ridb])
    S.op("dve", ("tensor_copy", dict(out=w2b[:], in_=w2f[:])), reads=[Rw2f], writes=[Rw2b])
    PM = lambda c: prmt[:, c:c + 1]
    S.op("dve", ("tensor_scalar", dict(out=bdr[:], in0=bdf[:], scalar1=PM(14), scalar2=None, op0=ALU.mult)), reads=[Rbd, Rprm], writes=[Rbdr])
    S.op("dve", ("tensor_scalar", dict(out=bdm[:], in0=bdf[:], scalar1=1.0 / 64, scalar2=None, op0=ALU.mult)), reads=[Rbd], writes=[Rbdm])

    def T2(name, dt=F32, n=BLK):
        return sb(name, [128, n], dt), R()
    ld = {}
    for nm in ("pr", "pk", "pv", "pwa"):
        ld[nm] = (sb("ld_" + nm, [128, BLK + 2]), R())
    tmp, Rtmp = T2("tmp")
    qr, Rqr = T2("qr"); qk, Rqk = T2("qk"); qv, Rqv = T2("qv"); qwa, Rqwa = T2("qwa")
    twa, Rtwa = T2("twa", BF16)
    sw, Rsw = T2("sw"); asg, Rasg = T2("asg")
    logw, Rlogw = T2("logw"); lin, Rlin = T2("lin"); linm, Rlinm = T2("linm"); lexm, Rlexm = T2("lexm")
    lex, Rlex = T2("lex"); lint, Rlint = T2("lint")
    e1, Re1 = T2("e1"); e1x, Re1x = T2("e1x"); e2, Re2 = T2("e2"); e3S = [sb("e3%d" % i, [128, BLK]) for i in range(2)]; Re3S = [R(), R()]; e3x, Re3x = T2("e3x"); e4, Re4 = T2("e4")
    kk, Rkk = T2("kk"); kk2, Rkk2 = T2("kk2"); rin, Rrin = T2("rin"); kkn, Rkkn = T2("kkn")
    kp, Rkp = T2("kp"); bv, Rbv = T2("bv"); rk, Rrk = T2("rk")
    rtS = [sb("rt%d" % i, [128, BLK], BF16) for i in range(2)]; RrtS = [R(), R()]; atS = [sb("at%d" % i, [128, BLK], BF16) for i in range(2)]; RatS = [R(), R()]; ktS = [sb("kt%d" % i, [128, BLK], BF16) for i in range(2)]; RktS = [R(), R()]; btS = [sb("bt%d" % i, [128, BLK], BF16) for i in range(2)]; RbtS = [R(), R()]
    r0S = [sb("r0%d" % i, [128, BLK]) for i in range(2)]; Rr0S = [R(), R()]; a0bS = [sb("a0b%d" % i, [128, BLK], BF16) for i in range(2)]; Ra0bS = [R(), R()]; kEbS = [sb("kEb%d" % i, [128, BLK], BF16) for i in range(2)]; RkEbS = [R(), R()]; bEbS = [sb("bEb%d" % i, [128, BLK], BF16) for i in range(2)]; RbEbS = [R(), R()]
    qvbS = [sb("qvb%d" % i, [128, BLK], BF16) for i in range(2)]; RqvbS = [R(), R()]
    ysum = sb("ysum", [128, T]); Rys = [R() for _ in range(T // C)]
    bsum = sb("bsum", [128, T]); Rbs = [R() for _ in range(nblk)]
    TT = [sb("TT%d" % i, [128, 4, 128], BF16) for i in range(2)]; RTT = [R(), R()]
    SBM = [sb("SBM%d" % i, [128, 2, 4, 128], BF16) for i in range(2)]; RSBM = [R(), R()]
    MKR = [sb("MKR%d" % i, [128, 2, 128]) for i in range(2)]; RMKR = [R(), R()]
    SX = [sb("SX%d" % i, [128, 2, 192], BF16) for i in range(2)]; RSX = [R(), R()]
    SAB = [sb("SAB%d" % i, [128, 2, 2, 128], BF16) for i in range(2)]; RSAB = [R(), R()]
    Gb = sb("Gb", [128, 128], BF16); RGb = R()
    Hb = sb("Hb", [128, 2, 128], BF16); RHb = R()
    Pb = sb("Pb", [128, 64], BF16); RPb = R()
    Zb = sb("Zb", [128, 2, 64], BF16); RZb = R()
    STz = [sb("STz%d" % h, [128, 64], BF16) for h in range(2)]; RST = [R(), R()]
    identP = sb("identP", [128, 64]); mkb = sb("mkb", [128, 2, 2, 128])
    HS = [slice(0, 64), slice(64, 128)]
    fin1, Rfin1 = T2("fin1"); fin2, Rfin2 = T2("fin2"); fin3, Rfin3 = T2("fin3")

    PS_M = ps("PS_M", [128, 2, 4, 128]); RPS_M = R()
    PS_K = ps("PS_K", [128, 512]); RPS_K = R()
    PS_X = [ps("PS_X%d" % h, [128, 512]) for h in range(2)]; RPS_X = R()
    PS_AB = ps("PS_AB", [128, 2, 2, 128]); RPS_AB = R()
    PS_G = ps("PS_G", [128, 512]); RPS_G = R()
    PS_T = ps("PS_T", [128, 8, 128], BF16); RPS_T = R()
    PS_P1 = PS_AB[:].rearrange("p a b c -> p (a b c)"); RPS_P1 = RPS_AB
    PS_P2 = PS_P1; RPS_P2 = RPS_AB
    mm = lambda out, l, r_, rd, wr, st=True, sp=True, sg=False: S.op("pe", ("matmul", dict(out=out, lhsT=l, rhs=r_, start=st, stop=sp, skip_group_check=sg)), reads=rd, writes=wr)
    S.op("pool", ("tensor_copy", dict(out=identP[0:64, :], in_=identf[0:64, 0:64])), reads=[Ridf], writes=[Ridf])
    S.op("pool", ("tensor_copy", dict(out=identP[64:128, :], in_=identf[64:128, 64:128])), reads=[Ridf], writes=[Ridf])
    for h in range(2):
        S.op("pool", ("tensor_copy", dict(out=mkb[:, :, h, :], in_=mkf[:])), reads=[Rmk], writes=[Rmk])
    ytmp = sb("ytmp", [128, 128]); Rytmp = R()
    out_toks = []
    NFILL = 4
    NPREP = 2
    def prep_gen(b, d, blk, pp):
        bwd = (d == 1)
        midc, totc = (C // 2 - 1, C - 1) if not bwd else (C // 2, 0)
        t0 = blk * BLK
        rt_, Rrt_ = rtS[pp], RrtS[pp]
        at_, Rat_ = atS[pp], RatS[pp]
        kt_, Rkt_ = ktS[pp], RktS[pp]
        bt_, Rbt_ = btS[pp], RbtS[pp]
        r0_, Rr0_ = r0S[pp], Rr0S[pp]
        a0b_, Ra0b_ = a0bS[pp], Ra0bS[pp]
        kEb_, RkEb_ = kEbS[pp], RkEbS[pp]
        bEb_, RbEb_ = bEbS[pp], RbEbS[pp]
        qvb_, Rqvb_ = qvbS[pp], RqvbS[pp]
        e3_, Re3_ = e3S[pp], Re3S[pp]
        for nm, src in (("pr", pr), ("pk", pk), ("pv", pv), ("pwa", pwa)):
            tl, Rl = ld[nm]
            S.dma(("dma_start", dict(out=tl[:], in_=src[:, b, t0:t0 + BLK + 2])), writes=[Rl])
            yield
        sh = (slice(0, BLK) if not bwd else slice(2, BLK + 2))
        cur = slice(1, BLK + 1)
        for nm, q, Rq, mc in (("pr", qr, Rqr, 0), ("pk", qk, Rqk, 2), ("pv", qv, Rqv, 4), ("pwa", qwa, Rqwa, 6)):
            tl, Rl = ld[nm]
            S.op("dve", ("tensor_tensor", dict(out=tmp[:], in0=tl[:, sh], in1=tl[:, cur], op=ALU.subtract)), reads=[Rl], writes=[Rtmp])
            yield
            S.op("dve", ("scalar_tensor_tensor", dict(out=q[:], in0=tmp[:], scalar=PM(mc + d), in1=tl[:, cur], op0=ALU.mult, op1=ALU.add)), reads=[Rtmp, Rl, Rprm], writes=[Rq])
            yield
        S.op("act", ("activation", dict(out=twa[0:64, :], in_=qwa[0:64, :], func=AF.Tanh)), reads=[Rqwa], writes=[Rtwa])
        yield
        S.op("dve", ("tensor_copy", dict(out=twa[64:128, :], in_=qwa[64:128, :])), reads=[Rqwa], writes=[Rtwa])
        yield
        S.op("pe", ("matmul", dict(out=PS_P1, lhsT=w2b[0:64, d, :], rhs=twa[0:64, :], start=True, stop=True)), reads=[Rw2b, Rtwa], writes=[RPS_P1])
        S.op("act", ("activation", dict(out=sw[:], in_=PS_P1, func=AF.Sigmoid, bias=PM(8 + d))), reads=[RPS_P1, Rprm], writes=[Rsw])
        yield
        S.op("pe", ("matmul", dict(out=PS_P2, lhsT=w2b[64:128, d, :], rhs=twa[64:128, :], start=True, stop=True)), reads=[Rw2b, Rtwa], writes=[RPS_P2])
        S.op("act", ("activation", dict(out=asg[:], in_=PS_P2, func=AF.Sigmoid, bias=PM(10 + d))), reads=[RPS_P2, Rprm], writes=[Rasg])
        yield
        S.op("dve", ("tensor_scalar", dict(out=logw[:], in0=sw[:], scalar1=NEG_E, scalar2=None, op0=ALU.mult)), reads=[Rsw], writes=[Rlogw])
        yield
        S.op("dve", ("tensor_tensor_scan", dict(out=lin[:], data0=scanm[:], data1=logw[:], initial=0.0, op0=ALU.mult, op1=ALU.add)), reads=[Rsc, Rlogw], writes=[Rlin])
        yield
        lin3 = lambda tl: tl[:].rearrange("p (c t) -> p c t", t=C)
        bc = lambda tl, col: lin3(tl)[:, :, col:col + 1].to_broadcast([128, BLK // C, C])
        if bwd:
            S.op("dve", ("tensor_tensor", dict(out=lin3(tmp), in0=bc(lin, C - 1), in1=lin3(lin), op=ALU.subtract)), reads=[Rlin], writes=[Rtmp])
            yield
            S.op("dve", ("tensor_tensor", dict(out=lin[:], in0=tmp[:], in1=logw[:], op=ALU.add)), reads=[Rtmp, Rlogw], writes=[Rlin])
            yield
        S.op("dve", ("tensor_tensor", dict(out=lin3(linm), in0=lin3(lin), in1=bc(lin, midc), op=ALU.subtract)), reads=[Rlin], writes=[Rlinm])
        yield
        S.op("dve", ("tensor_tensor", dict(out=lexm[:], in0=linm[:], in1=logw[:], op=ALU.subtract)), reads=[Rlinm, Rlogw], writes=[Rlexm])
        yield
        S.op("dve", ("tensor_tensor", dict(out=lex[:], in0=lin[:], in1=logw[:], op=ALU.subtract)), reads=[Rlin, Rlogw], writes=[Rlex])
        yield
        S.op("dve", ("tensor_tensor", dict(out=lin3(lint), in0=lin3(lin), in1=bc(lin, totc), op=ALU.subtract)), reads=[Rlin], writes=[Rlint])
        yield
        S.op("act", ("activation", dict(out=e1[:], in_=linm[:], func=AF.Exp)), reads=[Rlinm], writes=[Re1])
        yield
        S.op("act", ("activation", dict(out=e1x[:], in_=lexm[:], func=AF.Exp)), reads=[Rlexm], writes=[Re1x])
        yield
        S.op("act", ("activation", dict(out=e2[:], in_=linm[:], func=AF.Exp, scale=-1.0)), reads=[Rlinm], writes=[Re2])
        yield
        S.op("act", ("activation", dict(out=e3_[:], in_=lin[:], func=AF.Exp)), reads=[Rlin], writes=[Re3_])
        yield
        S.op("act", ("activation", dict(out=e3x[:], in_=lex[:], func=AF.Exp)), reads=[Rlex], writes=[Re3x])
        yield
        S.op("act", ("activation", dict(out=e4[:], in_=lint[:], func=AF.Exp, scale=-1.0)), reads=[Rlint], writes=[Re4])
        yield
        S.op("dve", ("tensor_scalar", dict(out=kk[:], in0=qk[:], scalar1=PM(12), scalar2=None, op0=ALU.mult)), reads=[Rqk, Rprm], writes=[Rkk])
        yield
        S.op("pool", ("tensor_tensor", dict(out=kk2[:], in0=kk[:], in1=kk[:], op=ALU.mult)), reads=[Rkk], writes=[Rkk2])
        yield
        S.op("pe", ("matmul", dict(out=PS_P1, lhsT=bdf[:], rhs=kk2[:], start=True, stop=True)), reads=[Rbd, Rkk2], writes=[RPS_P1])
        S.op("dve", ("tensor_scalar", dict(out=rin[:], in0=PS_P1, scalar1=1e-12, scalar2=None, op0=ALU.max)), reads=[RPS_P1], writes=[Rrin])
        yield
        S.op("act", ("activation", dict(out=rin[:], in_=rin[:], func=AF.Sqrt)), reads=[Rrin], writes=[Rrin])
        yield
        S.op("dve", ("reciprocal", dict(out=rin[:], in_=rin[:])), reads=[Rrin], writes=[Rrin])
        yield
        S.op("dve", ("tensor_tensor", dict(out=kkn[:], in0=kk[:], in1=rin[:], op=ALU.mult)), reads=[Rkk, Rrin], writes=[Rkkn])
        yield
        S.op("dve", ("tensor_scalar", dict(out=tmp[:], in0=asg[:], scalar1=-1.0, scalar2=PM(13), op0=ALU.add, op1=ALU.mult)), reads=[Rasg, Rprm], writes=[Rtmp])
        yield
        S.op("dve", ("scalar_tensor_tensor", dict(out=kp[:], in0=tmp[:], scalar=1.0, in1=qk[:], op0=ALU.add, op1=ALU.mult)), reads=[Rtmp, Rqk], writes=[Rkp])
        yield
        S.op("pool", ("tensor_tensor", dict(out=bv[:], in0=kkn[:], in1=asg[:], op=ALU.mult)), reads=[Rkkn, Rasg], writes=[Rbv])
        yield
        S.op("pool", ("tensor_tensor", dict(out=rk[:], in0=qr[:], in1=kp[:], op=ALU.mult)), reads=[Rqr, Rkp], writes=[Rrk])
        yield
        S.op("pe", ("matmul", dict(out=PS_P2, lhsT=bdr[:], rhs=rk[:], start=True, stop=True)), reads=[Rbdr, Rrk], writes=[RPS_P2])
        bsl = bsum[:, t0:t0 + BLK]
        if d == 0:
            S.op("dve", ("tensor_tensor", dict(out=bsl, in0=PS_P2, in1=qv[:], op=ALU.mult)), reads=[RPS_P2, Rqv], writes=[Rbs[blk]])
            yield
        else:
            S.op("dve", ("tensor_tensor", dict(out=tmp[:], in0=PS_P2, in1=qv[:], op=ALU.mult)), reads=[RPS_P2, Rqv], writes=[Rtmp])
            yield
            S.op("pool", ("tensor_tensor", dict(out=bsl, in0=bsl, in1=tmp[:], op=ALU.add)), reads=[Rtmp, Rbs[blk]], writes=[Rbs[blk]])
            yield
        S.op("dve", ("tensor_tensor", dict(out=rt_[:], in0=qr[:], in1=e1[:], op=ALU.mult)), reads=[Rqr, Re1], writes=[Rrt_])
        yield
        S.op("dve", ("scalar_tensor_tensor", dict(out=at_[:], in0=kkn[:], scalar=-1.0, in1=e1x[:], op0=ALU.mult, op1=ALU.mult)), reads=[Rkkn, Re1x], writes=[Rat_])
        yield
        S.op("pool", ("tensor_tensor", dict(out=kt_[:], in0=kp[:], in1=e2[:], op=ALU.mult)), reads=[Rkp, Re2], writes=[Rkt_])
        yield
        S.op("pool", ("tensor_tensor", dict(out=bt_[:], in0=bv[:], in1=e2[:], op=ALU.mult)), reads=[Rbv, Re2], writes=[Rbt_])
        yield
        S.op("pool", ("tensor_tensor", dict(out=r0_[:], in0=qr[:], in1=e3_[:], op=ALU.mult)), reads=[Rqr, Re3_], writes=[Rr0_])
        yield
        S.op("dve", ("scalar_tensor_tensor", dict(out=a0b_[:], in0=kkn[:], scalar=-1.0, in1=e3x[:], op0=ALU.mult, op1=ALU.mult)), reads=[Rkkn, Re3x], writes=[Ra0b_])
        yield
        S.op("pool", ("tensor_tensor", dict(out=kEb_[:], in0=kp[:], in1=e4[:], op=ALU.mult)), reads=[Rkp, Re4], writes=[RkEb_])
        yield
        S.op("pool", ("tensor_tensor", dict(out=bEb_[:], in0=bv[:], in1=e4[:], op=ALU.mult)), reads=[Rbv, Re4], writes=[RbEb_])
        yield
        S.op("act", ("activation", dict(out=qvb_[:], in_=qv[:], func=AF.Copy)), reads=[Rqv], writes=[Rqvb_])
        yield


    def block_stages(b, d, blk, pp):
        bwd = (d == 1)
        midc, totc = (C // 2 - 1, C - 1) if not bwd else (C // 2, 0)
        t0 = blk * BLK
        rt_, Rrt_ = rtS[pp], RrtS[pp]
        at_, Rat_ = atS[pp], RatS[pp]
        kt_, Rkt_ = ktS[pp], RktS[pp]
        bt_, Rbt_ = btS[pp], RbtS[pp]
        r0_, Rr0_ = r0S[pp], Rr0S[pp]
        a0b_, Ra0b_ = a0bS[pp], Ra0bS[pp]
        kEb_, RkEb_ = kEbS[pp], RkEbS[pp]
        bEb_, RbEb_ = bEbS[pp], RbEbS[pp]
        qvb_, Rqvb_ = qvbS[pp], RqvbS[pp]
        e3_, Re3_ = e3S[pp], Re3S[pp]

        def stage1(ck):
            ci, cs, gci, p = ck
            for i, (src, Rs) in enumerate(((qvb_, Rqvb_), (a0b_, Ra0b_), (bEb_, RbEb_), (kEb_, RkEb_))):
                S.op("pe", ("transpose", dict(out=PS_T[:, i, :], in_=src[:, cs], identity=identb[:])), reads=[Rs, Ridb], writes=[RPS_T])
            yield
            S.op("act", ("activation", dict(out=TT[p][:], in_=PS_T[:, 0:4, :], func=AF.Copy)), reads=[RPS_T], writes=[RTT[p]])
            yield
            for h in range(2):
                hs = HS[h]
                mm(PS_M[:, h, 0, :], bt_[hs, cs], at_[hs, cs], [Rbt_, Rat_], [RPS_M])
                mm(PS_M[:, h, 1, :], at_[hs, cs], bt_[hs, cs], [Rbt_, Rat_], [RPS_M])
                yield
                mm(PS_M[:, h, 2, :], at_[hs, cs], kt_[hs, cs], [Rkt_, Rat_], [RPS_M])
                mm(PS_M[:, h, 3, :], bt_[hs, cs], rt_[hs, cs], [Rbt_, Rrt_], [RPS_M])
                yield
                mm((PS_K if h == 0 else PS_G)[:, 0:128], kt_[hs, cs], rt_[hs, cs], [Rkt_, Rrt_], [RPS_K if h == 0 else RPS_G])
                yield
            for h in range(2):
                S.op("dve", ("tensor_tensor", dict(out=SBM[p][:, h], in0=PS_M[:, h], in1=m4f[:, d, :, :], op=ALU.mult)), reads=[RPS_M, Rm4], writes=[RSBM[p]])
                yield
            S.op("dve", ("tensor_tensor", dict(out=MKR[p][:, 0, :], in0=PS_K[:, 0:128], in1=mkf[:, d, :], op=ALU.mult)), reads=[RPS_K, Rmk], writes=[RMKR[p]])
            yield
            S.op("dve", ("tensor_tensor", dict(out=MKR[p][:, 1, :], in0=PS_G[:, 0:128], in1=mkf[:, d, :], op=ALU.mult)), reads=[RPS_G, Rmk], writes=[RMKR[p]])
            yield
            S.op("act", ("activation", dict(out=SX[p][:, :, 0:128], in_=SBM[p][:, :, 3, :], func=AF.Copy)), reads=[RSBM[p]], writes=[RSX[p]])
            S.op("pool", ("tensor_copy", dict(out=SX[p][:, :, 128:192], in_=TT[p][:, 2, :].rearrange("p (h j) -> p h j", h=2))), reads=[RTT[p]], writes=[RSX[p]])
            yield

        def stage2(ck):
            ci, cs, gci, p = ck
            for h in range(2):
                mm(PS_X[h][:, 0:192], identb[:], SX[p][:, h, :], [Ridb, RSX[p]], [RPS_X], st=True, sp=True)
            A_ = [SBM[p][:, h, 1, :] for h in range(2)]
            B_ = [SBM[p][:, h, 0, :] for h in range(2)]
            Rcur = RSBM[p]
            for lv in range(7):
                if lv < 6:
                    nb = lv % 2
                    for h in range(2):
                        mm(PS_AB[:, h, 0, :], B_[h], A_[h], [Rcur], [RPS_AB])
                        mm(PS_AB[:, h, 1, :], A_[h], B_[h], [Rcur], [RPS_AB])
                for h in range(2):
                    mm(PS_X[h][:, 0:192], A_[h], SX[p][:, h, :], [Rcur, RSX[p]], [RPS_X], st=False, sp=True, sg=True)
                if lv < 6:
                    S.op("act", ("activation", dict(out=SAB[nb][:].rearrange("p a b c -> p (a b c)"), in_=PS_AB[:].rearrange("p a b c -> p (a b c)"), func=AF.Copy)), reads=[RPS_AB], writes=[RSAB[nb]])
                S.op("dve", ("tensor_copy", dict(out=SX[p][:, 0, :], in_=PS_X[0][:, 0:192])), reads=[RPS_X], writes=[RSX[p]])
                S.op("dve", ("tensor_copy", dict(out=SX[p][:, 1, :], in_=PS_X[1][:, 0:192])), reads=[RPS_X], writes=[RSX[p]])
                if lv < 6:
                    A_ = [SAB[nb][:, h, 0, :] for h in range(2)]
                    B_ = [SAB[nb][:, h, 1, :] for h in range(2)]
                    Rcur = RSAB[nb]
                yield

        def stage3(ck):
            ci, cs, gci, p = ck
            for h in range(2):
                hs = HS[h]
                a0T = TT[p][:, 1, hs]
                mm(PS_G[hs, 0:128], a0T, SX[p][:, h, 0:128], [RTT[p], RSX[p]], [RPS_G])
                mm(PS_G[hs, 128:192], a0T, SX[p][:, h, 128:192], [RTT[p], RSX[p]], [RPS_G])
                yield
                mm(PS_G[:, 192 + 128 * h:320 + 128 * h], SBM[p][:, h, 2, :], SX[p][:, h, 0:128], [RSBM[p], RSX[p]], [RPS_G])
                mm(PS_K[:, 256 + 64 * h:320 + 64 * h], SBM[p][:, h, 2, :], SX[p][:, h, 128:192], [RSBM[p], RSX[p]], [RPS_K])
                yield
            S.op("dve", ("tensor_tensor", dict(out=Gb[:], in0=PS_G[:, 0:128], in1=r0_[:, cs], op=ALU.add)), reads=[RPS_G, Rr0_], writes=[RGb])
            yield
            S.op("dve", ("tensor_tensor", dict(out=Hb[:], in0=PS_G[:, 192:448].rearrange("p (h t) -> p h t", h=2), in1=MKR[p][:], op=ALU.add)), reads=[RPS_G, RMKR[p]], writes=[RHb])
            yield
            tcol = ci * C + totc
            S.op("dve", ("scalar_tensor_tensor", dict(out=Pb[:], in0=identP[:], scalar=e3_[:, tcol:tcol + 1], in1=PS_G[:, 128:192], op0=ALU.mult, op1=ALU.add)), reads=[RPS_G, Ridf, Re3_], writes=[RPb])
            yield
            S.op("dve", ("tensor_tensor", dict(out=Zb[:], in0=PS_K[:, 256:384].rearrange("p (h j) -> p h j", h=2), in1=TT[p][:, 3, :].rearrange("p (h j) -> p h j", h=2), op=ALU.add)), reads=[RPS_K, RTT[p]], writes=[RZb])
            yield
            for h in range(2):
                hs = HS[h]
                mm(PS_M[hs, 0, 0, :], STz[h][:], Gb[:], [RST[h], RGb], [RPS_M], st=True, sp=False)
                mm(PS_M[hs, 0, 0, :], TT[p][:, 0, hs], Hb[:, h, :], [RTT[p], RHb], [RPS_M], st=False, sp=True)
                yield
                mm(PS_M[hs, 0, 1, 0:64], Pb[:], STz[h][:], [RPb, RST[h]], [RPS_M], st=True, sp=False)
                mm(PS_M[hs, 0, 1, 0:64], Zb[:, h, :], TT[p][:, 0, hs], [RZb, RTT[p]], [RPS_M], st=False, sp=True)
                yield
            ysl = ysum[:, t0 + ci * C: t0 + (ci + 1) * C]
            if d == 0:
                S.op("act", ("activation", dict(out=ysl, in_=PS_M[:, 0, 0, :], func=AF.Copy)), reads=[RPS_M], writes=[Rys[gci]])
            else:
                S.op("act", ("activation", dict(out=ytmp[:, 0:128], in_=PS_M[:, 0, 0, :], func=AF.Copy)), reads=[RPS_M], writes=[Rytmp])
                S.op("dve", ("tensor_tensor", dict(out=ysl, in0=ytmp[:, 0:128], in1=ysl, op=ALU.add)), reads=[Rytmp, Rys[gci]], writes=[Rys[gci]])
            yield
            for h in range(2):
                hs = HS[h]
                S.op("act", ("activation", dict(out=STz[h][hs, :], in_=PS_M[hs, 0, 1, 0:64], func=AF.Copy)), reads=[RPS_M], writes=[RST[h]])
            yield

        return stage1, stage2, stage3

    def finalize_batch(b):
        for blk in range(nblk):
            t0 = blk * BLK
            ysl = ysum[:, t0:t0 + BLK]
            Rin = Rys[t0 // C: (t0 + BLK) // C]
            S.op("pe", ("matmul", dict(out=PS_P1, lhsT=bdm[:], rhs=ysl, start=True, stop=True)), reads=[Rbdm] + Rin, writes=[RPS_P1])
            S.op("dve", ("tensor_tensor", dict(out=fin1[:], in0=ysl, in1=PS_P1, op=ALU.subtract)), reads=[RPS_P1] + Rin, writes=[Rfin1])
            S.op("pool", ("tensor_tensor", dict(out=fin2[:], in0=fin1[:], in1=fin1[:], op=ALU.mult)), reads=[Rfin1], writes=[Rfin2])
            S.op("pe", ("matmul", dict(out=PS_P2, lhsT=bdm[:], rhs=fin2[:], start=True, stop=True)), reads=[Rbdm, Rfin2], writes=[RPS_P2])
            S.op("dve", ("tensor_scalar", dict(out=fin3[:], in0=PS_P2, scalar1=GN_EPS, scalar2=None, op0=ALU.add)), reads=[RPS_P2], writes=[Rfin3])
            S.op("act", ("activation", dict(out=fin3[:], in_=fin3[:], func=AF.Sqrt)), reads=[Rfin3], writes=[Rfin3])
            S.op("dve", ("reciprocal", dict(out=fin3[:], in_=fin3[:])), reads=[Rfin3], writes=[Rfin3])
            S.op("dve", ("tensor_tensor", dict(out=fin1[:], in0=fin1[:], in1=fin3[:], op=ALU.mult)), reads=[Rfin1, Rfin3], writes=[Rfin1])
            S.op("dve", ("tensor_scalar", dict(out=fin2[:], in0=fin1[:], scalar1=PM(15), scalar2=PM(16), op0=ALU.mult, op1=ALU.add)), reads=[Rfin1, Rprm], writes=[Rfin2])
            S.op("dve", ("tensor_tensor", dict(out=fin2[:], in0=fin2[:], in1=bsum[:, t0:t0 + BLK], op=ALU.add)), reads=[Rfin2, Rbs[blk]], writes=[Rfin2])
            out_toks.append(S.dma(("dma_start", dict(out=yout[:, b, t0:t0 + BLK], in_=fin2[:])), reads=[Rfin2]))

    sched_blocks = []
    for b in range(NB):
        for d in range(2):
            order = list(range(nblk)) if d == 0 else list(range(nblk - 1, -1, -1))
            for n_, blk in enumerate(order):
                sched_blocks.append((b, d, blk, n_ == 0, (n_ == len(order) - 1) and d == 1))
    pcount = 0
    for _ in prep_gen(sched_blocks[0][0], sched_blocks[0][1], sched_blocks[0][2], 0):
        pass
    for k, (b, d, blk, first_of_dir, last_of_batch) in enumerate(sched_blocks):
        pp = k % 2
        bwd = (d == 1)
        if first_of_dir:
            S.op("pool", ("memset", dict(ap=STz[0][:], constant=0.0)), writes=[RST[0]])
            S.op("pool", ("memset", dict(ap=STz[1][:], constant=0.0)), writes=[RST[1]])
        stage1, stage2, stage3 = block_stages(b, d, blk, pp)
        chunks = list(range(BLK // C)) if not bwd else list(range(BLK // C - 1, -1, -1))
        cks = [(ci, slice(ci * C, (ci + 1) * C), (blk * BLK // C) + ci, (pcount + n_) % 2) for n_, ci in enumerate(chunks)]
        pcount += len(chunks)
        if k + 1 < len(sched_blocks) and sched_blocks[k + 1][0] == b:
            nb_, nd_, nblk_ = sched_blocks[k + 1][:3]
            pgen = prep_gen(nb_, nd_, nblk_, (k + 1) % 2)
        else:
            pgen = iter(())
        for _ in stage1(cks[0]):
            pass
        for idx, ck in enumerate(cks):
            fill = itertools.chain(stage3(cks[idx - 1]) if idx > 0 else iter(()), stage1(cks[idx + 1]) if idx + 1 < len(cks) else iter(()))
            for _ in stage2(ck):
                for _k in range(NFILL):
                    next(fill, None)
                for _k in range(NPREP):
                    next(pgen, None)
            for _ in fill:
                pass
        for _ in stage3(cks[-1]):
            pass
        for _ in pgen:
            pass
        if last_of_batch:
            finalize_batch(b)
            if k + 1 < len(sched_blocks):
                nb_, nd_, nblk_ = sched_blocks[k + 1][:3]
                for _ in prep_gen(nb_, nd_, nblk_, (k + 1) % 2):
                    pass
    return out_toks


NT = 2048
NTH = NT + 2


def emit_p1(S, nc, I, O):
    sb = lambda name, shape, dt=F32: nc.alloc_sbuf_tensor(name, shape, dt)
    ps = lambda name, shape, dt=F32: nc.alloc_psum_tensor(name, shape, dt)
    R = Region
    op = S.op
    mm = lambda out, l, r_, rd, wr, st=True, sp=True: op("pe", ("matmul", dict(out=out, lhsT=l, rhs=r_, start=st, stop=sp)), reads=rd, writes=wr)

    identf = sb("identf", [128, 128]); identb = sb("identb", [128, 128], BF16); Rid = R()
    gE = sb("gE", [128, 8, 1]); gO = sb("gO", [128, 8, 1]); Rg = R()
    S.dma(("dma_start", dict(out=identf[:], in_=I["c_ident"])), writes=[Rid])
    op("dve", ("tensor_copy", dict(out=identb[:], in_=identf[:])), reads=[Rid], writes=[Rid])
    S.dma(("dma_start", dict(out=gE[:, :, 0], in_=I["e_norm_g"].rearrange("(k p) -> p k", p=128), allow_slow_non_contiguous=True)), writes=[Rg])
    S.dma(("dma_start", dict(out=gO[:, :, 0], in_=I["o_norm_g"].rearrange("(k p) -> p k", p=128), allow_slow_non_contiguous=True)), writes=[Rg])
    hnT = sb("hnT", [128, 8, NTH], BF16); RhnT = [R() for _ in range(18)]
    yT = nc.dram_tensor("yT_d", [16, 128, NT], BF16).ap(); RyT = [[R() for _ in range(4)] for _ in range(16)]
    U = sb("U", [128, 4096]); RU = R()
    xt = [sb("xt%d" % i, [128, 1024]) for i in range(2)]; Rxt = [R(), R()]
    yo = [sb("yo%d" % i, [128, 512], BF16) for i in range(2)]; Ryo = [R(), R()]
    ytl = [sb("ytl%d" % i, [128, 16, 128], BF16) for i in range(2)]; Rytl = [R(), R()]
    xn = sb("xn", [128, 1024], BF16); Rxn = R()
    sq = sb("sq", [128, 1024]); Rsq = R()
    st = sb("st", [128, 8]); Rst = R()
    stg = [sb("stg%d" % i, [128, 8, 256]) for i in range(2)]; Rstg = [R(), R()]
    wbf = [sb("wbf%d" % i, [128, 8, 512], BF16) for i in range(2)]; Rwbf = [R(), R()]
    wbig = sb("wbig", [128, 16, 1024], BF16); Rwbig = R()
    t1 = sb("t1", [128, 512]); Rt1 = R()
    t1b = sb("t1b", [128, 512]); t1s = [t1, t1b]; Rt1s = [Rt1, R()]
    t2b = sb("t2b", [128, 512]); t3b = sb("t3b", [128, 512])
    t2 = sb("t2", [128, 512]); Rt2 = R()
    t3 = sb("t3", [128, 512]); Rt3 = R()
    cw = sb("cw", [128, 8, 3]); Rcw = R()
    PS_a = ps("PS_a", [128, 512]); RPa = R()
    PS_b = ps("PS_b", [128, 512]); RPb = R()
    PS_c = ps("PS_c", [128, 512]); RPc = R()
    PS_d = ps("PS_d", [128, 512]); RPd = R()
    PS_t = ps("PS_t", [128, 8, 128], BF16); RPt = R()
    PS_m = ps("PS_m", [128, 8, 128]); RPm = R()
    for j_ in range(3):
        S.dma(("dma_start", dict(out=cw[:, :, j_], in_=I["e_conv_w"][j_].rearrange("(cb p) -> p cb", p=128), allow_slow_non_contiguous=True)), writes=[Rcw])

    def norm_tile(xtile, Rx, gt, dst_fn, Rdst, nvalid=128):
        op("act", ("activation", dict(out=sq[:], in_=xtile[:], func=AF.Square)), reads=[Rx], writes=[Rsq])
        op("dve", ("reduce_sum", dict(out=st[:, 0:1], in_=sq[:], axis=AX.X)), reads=[Rsq], writes=[Rst])
        op("dve", ("tensor_scalar", dict(out=st[:, 1:2], in0=st[:, 0:1], scalar1=1.0 / 1024, scalar2=1e-6, op0=ALU.mult, op1=ALU.add)), reads=[Rst], writes=[Rst])
        op("act", ("activation", dict(out=st[:, 2:3], in_=st[:, 1:2], func=AF.Sqrt)), reads=[Rst], writes=[Rst])
        op("dve", ("reciprocal", dict(out=st[:, 3:4], in_=st[:, 2:3])), reads=[Rst], writes=[Rst])
        op("dve", ("tensor_scalar", dict(out=xn[:], in0=xtile[:], scalar1=st[:, 3:4], scalar2=None, op0=ALU.mult)), reads=[Rx, Rst], writes=[Rxn])
        for k in range(8):
            op("pe", ("transpose", dict(out=PS_t[:, k, :], in_=xn[:, k * 128:(k + 1) * 128], identity=identb[:])), reads=[Rxn, Rid], writes=[RPt])
        dst_fn(gt)

    xh = I["xh"]
    for i in range(17):
        xb, Rx = xt[i % 2], Rxt[i % 2]
        if i < 16:
            S.dma(("dma_start", dict(out=xb[:], in_=xh[1 + 128 * i: 1 + 128 * (i + 1), :])), writes=[Rx])
            def dst(gt, i=i):
                op("dve", ("tensor_tensor", dict(out=hnT[:, :, 1 + 128 * i: 1 + 128 * (i + 1)], in0=PS_t[:], in1=gt[:].to_broadcast([128, 8, 128]), op=ALU.mult)), reads=[RPt, Rg], writes=[RhnT[i]])
        else:
            op("pool", ("memset", dict(ap=xb[:], constant=0.0)), writes=[Rx])
            S.dma(("dma_start", dict(out=xb[0:1, :], in_=xh[0:1, :])), writes=[Rx])
            S.dma(("dma_start", dict(out=xb[1:2, :], in_=xh[NT + 1:NT + 2, :])), writes=[Rx])
            def dst(gt):
                op("dve", ("tensor_tensor", dict(out=hnT[:, :, 0:1], in0=PS_t[:, :, 0:1], in1=gt[:], op=ALU.mult)), reads=[RPt, Rg], writes=[RhnT[16]])
                op("dve", ("tensor_tensor", dict(out=hnT[:, :, NT + 1:NT + 2], in0=PS_t[:, :, 1:2], in1=gt[:], op=ALU.mult)), reads=[RPt, Rg], writes=[RhnT[17]])
        norm_tile(xb, Rx, gE, dst, None)
    allh = RhnT

    wi = I["e_w_in"].rearrange("(k p) (s c) -> p k s c", p=128, c=1024)

    def load_w(buf, src4, nsp):
        for s_ in range(nsp):
            sb_ = s_ % 2
            S.dma(("dma_start", dict(out=stg[sb_][:, :, 0:128], in_=src4[:, :, s_, :])), writes=[Rstg[sb_]])
            op("pool", ("tensor_copy", dict(out=wbf[buf][:, :, s_ * 128:(s_ + 1) * 128], in_=stg[sb_][:, :, 0:128])), reads=[Rstg[sb_]], writes=[Rwbf[buf]])
        return wbf[buf][:, :, 0:nsp * 128].rearrange("p k (s c) -> p k s c", c=128)

    PSc0, RPc0, PSd0, RPd0 = PS_c, RPc, PS_d, RPd
    t2s = [t2, t2b]; Rt2s = [Rt2, R()]
    t3s = [t3, t3b]; Rt3s = [Rt3, R()]
    xc = U[:, 0:NTH]
    chunksA = [(0, 512), (512, 512), (1024, 512), (1536, 512), (2048, 2)]
    for cb in range(8):
        if cb % 2 == 0:
            for s_ in range(4):
                sb_ = s_ % 2
                S.dma(("dma_start", dict(out=stg[sb_][:], in_=wi[:, :, s_, cb * 128:(cb + 2) * 128])), writes=[Rstg[sb_]])
                op("pool", ("tensor_copy", dict(out=wbf[0][:, :, s_ * 128:(s_ + 1) * 128], in_=stg[sb_][:, :, 0:128])), reads=[Rstg[sb_]], writes=[Rwbf[0]])
                op("pool", ("tensor_copy", dict(out=wbf[1][:, :, s_ * 128:(s_ + 1) * 128], in_=stg[sb_][:, :, 128:256])), reads=[Rstg[sb_]], writes=[Rwbf[1]])
        w4 = wbf[cb % 2][:, :, 0:512].rearrange("p k (s c) -> p k s c", c=128)
        Rw = Rwbf[cb % 2]
        for ci_, (c0, n) in enumerate(chunksA):
            (PA, RA_), (PB, RB_) = (((PS_a, RPa), (PS_b, RPb)) if ci_ % 2 == 0 else ((PS_c, RPc), (PS_d, RPd)))
            for k in range(8):
                mm(PA[:, 0:n], w4[:, k, 0, :], hnT[:, k, c0:c0 + n], [Rw] + allh, [RA_], st=(k == 0), sp=(k == 7))
            for k in range(8):
                mm(PB[:, 0:n], w4[:, k, 2, :], hnT[:, k, c0:c0 + n], [Rw] + allh, [RB_], st=(k == 0), sp=(k == 7))
            t1_, Rt1_ = t1s[ci_ % 2], Rt1s[ci_ % 2]
            op("act", ("activation", dict(out=t1_[:, 0:n], in_=PA[:, 0:n], func=AF.Copy)), reads=[RA_], writes=[Rt1_])
            op("dve", ("tensor_tensor", dict(out=xc[:, c0:c0 + n], in0=PB[:, 0:n], in1=t1_[:, 0:n], op=ALU.mult)), reads=[RB_, Rt1_], writes=[RU])
        for j in range(4):
            c0 = 1 + 512 * j
            (PS_c, RPc), (PS_d, RPd) = ((PSc0, RPc0), (PSd0, RPd0)) if j % 2 == 1 else ((PS_a, RPa), (PS_b, RPb))
            t2, Rt2 = t2s[j % 2], Rt2s[j % 2]
            t3, Rt3 = t3s[j % 2], Rt3s[j % 2]
            for k in range(8):
                mm(PS_c[:], w4[:, k, 1, :], hnT[:, k, c0:c0 + 512], [Rw] + allh, [RPc], st=(k == 0), sp=(k == 7))
            for k in range(8):
                mm(PS_d[:], w4[:, k, 3, :], hnT[:, k, c0:c0 + 512], [Rw] + allh, [RPd], st=(k == 0), sp=(k == 7))
            op("dve", ("tensor_scalar", dict(out=t2[:], in0=xc[:, c0 - 1:c0 + 511], scalar1=cw[:, cb, 0:1], scalar2=None, op0=ALU.mult)), reads=[RU, Rcw], writes=[Rt2])
            op("dve", ("scalar_tensor_tensor", dict(out=t2[:], in0=xc[:, c0:c0 + 512], scalar=cw[:, cb, 1:2], in1=t2[:], op0=ALU.mult, op1=ALU.add)), reads=[RU, Rcw, Rt2], writes=[Rt2])
            op("dve", ("scalar_tensor_tensor", dict(out=t2[:], in0=xc[:, c0 + 1:c0 + 513], scalar=cw[:, cb, 2:3], in1=t2[:], op0=ALU.mult, op1=ALU.add)), reads=[RU, Rcw, Rt2], writes=[Rt2])
            op("act", ("activation", dict(out=t3[:], in_=PS_d[:], func=AF.Silu)), reads=[RPd], writes=[Rt3])
            op("dve", ("tensor_tensor", dict(out=t2[:], in0=PS_c[:], in1=t2[:], op=ALU.mult)), reads=[RPc, Rt2], writes=[Rt2])
            op("pool", ("tensor_tensor", dict(out=yo[j % 2][:], in0=t2[:], in1=t3[:], op=ALU.mult)), reads=[Rt2, Rt3], writes=[Ryo[j % 2]])
            S.dma(("dma_start", dict(out=yT[cb, :, 512 * j:512 * (j + 1)], in_=yo[j % 2][:])), reads=[Ryo[j % 2]], writes=[RyT[cb][j]], q="pool")

    PS_c, RPc, PS_d, RPd = PSc0, RPc0, PSd0, RPd0
    t2, Rt2, t3, Rt3 = t2s[0], Rt2s[0], t3s[0], Rt3s[0]
    for hf in range(4):
        S.dma(("dma_start", dict(out=stg[hf % 2][:], in_=wi[:, :, 5, hf * 256:(hf + 1) * 256])), writes=[Rstg[hf % 2]])
        op("pool", ("tensor_copy", dict(out=wbig[:, 0:8, hf * 256:(hf + 1) * 256], in_=stg[hf % 2][:])), reads=[Rstg[hf % 2]], writes=[Rwbig])
    for hf in range(4):
        S.dma(("dma_start", dict(out=stg[hf % 2][:], in_=wi[:, :, 4, hf * 256:(hf + 1) * 256])), writes=[Rstg[hf % 2]])
        op("pool", ("tensor_copy", dict(out=wbig[:, 8:16, hf * 256:(hf + 1) * 256], in_=stg[hf % 2][:])), reads=[Rstg[hf % 2]], writes=[Rwbig])
    for hf in range(4):
        S.dma(("dma_start", dict(out=stg[hf % 2][:], in_=wi[:, :, 6, hf * 256:(hf + 1) * 256])), writes=[Rstg[hf % 2]])
        op("pool", ("tensor_copy", dict(out=wbf[hf // 2][:, :, (hf % 2) * 256:(hf % 2 + 1) * 256], in_=stg[hf % 2][:])), reads=[Rstg[hf % 2]], writes=[Rwbf[hf // 2]])
    wsn = sb("wsn", [128, 8, 128]); wsnb = sb("wsnb", [128, 8, 128], BF16); wsT = sb("wsT", [128, 8, 128], BF16); Rws = R()
    S.dma(("dma_start", dict(out=wsn[:], in_=I["e_sgu_w"].rearrange("g i j -> i g j"))), writes=[Rws])
    op("dve", ("tensor_copy", dict(out=wsnb[:], in_=wsn[:])), reads=[Rws], writes=[Rws])
    for g in range(8):
        op("pe", ("transpose", dict(out=PS_t[:, g, :], in_=wsnb[:, g, :], identity=identb[:])), reads=[Rws, Rid], writes=[RPt])
    op("act", ("activation", dict(out=wsT[:], in_=PS_t[:], func=AF.Copy)), reads=[RPt], writes=[Rws])
    bsB = sb("bsB", [128, 8, 128]); lnG = sb("lnG", [128, 1024]); lnB = sb("lnB", [128, 1024]); Rbc = R()
    S.dma(("dma_start", dict(out=bsB[:].rearrange("p g i -> p (g i)"), in_=I["e_sgu_b"].rearrange("g i -> (g i)").partition_broadcast(128))), writes=[Rbc])
    S.dma(("dma_start", dict(out=lnG[:], in_=I["e_sgu_ln_g"].partition_broadcast(128))), writes=[Rbc])
    S.dma(("dma_start", dict(out=lnB[:], in_=I["e_sgu_ln_b"].partition_broadcast(128))), writes=[Rbc])
    vsb = sb("vsb", [128, 1024]); Rvsb = R()
    vnb = sb("vnb", [128, 1024], BF16); Rvnb = R()
    mixall = U[:, 0:4096].rearrange("p (g t) -> p g t", g=8)
    for tg in range(4):
        for ti in range(4):
            c0 = 1 + 128 * (4 * tg + ti)
            for hf, (P_, RP_) in enumerate((((PS_a, RPa), (PS_b, RPb)) if ti % 2 == 0 else ((PS_c, RPc), (PS_d, RPd)))):
                for k in range(8):
                    mm(P_[:], hnT[:, k, c0:c0 + 128], wbig[:, k, hf * 512:(hf + 1) * 512], [Rwbig] + allh, [RP_], st=(k == 0), sp=(k == 7))
                op("act", ("activation", dict(out=vsb[:, hf * 512:(hf + 1) * 512], in_=P_[:], func=AF.Copy)), reads=[RP_], writes=[Rvsb])
            op("act", ("activation", dict(out=sq[:], in_=vsb[:], func=AF.Square)), reads=[Rvsb], writes=[Rsq])
            op("dve", ("reduce_sum", dict(out=st[:, 0:1], in_=vsb[:], axis=AX.X)), reads=[Rvsb], writes=[Rst])
            op("dve", ("reduce_sum", dict(out=st[:, 1:2], in_=sq[:], axis=AX.X)), reads=[Rsq], writes=[Rst])
            op("dve", ("tensor_scalar", dict(out=st[:, 2:3], in0=st[:, 0:1], scalar1=1.0 / 1024, scalar2=None, op0=ALU.mult)), reads=[Rst], writes=[Rst])
            op("dve", ("tensor_tensor", dict(out=st[:, 3:4], in0=st[:, 2:3], in1=st[:, 2:3], op=ALU.mult)), reads=[Rst], writes=[Rst])
            op("dve", ("scalar_tensor_tensor", dict(out=st[:, 4:5], in0=st[:, 1:2], scalar=1.0 / 1024, in1=st[:, 3:4], op0=ALU.mult, op1=ALU.subtract)), reads=[Rst], writes=[Rst])
            op("dve", ("tensor_scalar", dict(out=st[:, 4:5], in0=st[:, 4:5], scalar1=1e-5, scalar2=None, op0=ALU.add)), reads=[Rst], writes=[Rst])
            op("act", ("activation", dict(out=st[:, 5:6], in_=st[:, 4:5], func=AF.Sqrt)), reads=[Rst], writes=[Rst])
            op("dve", ("reciprocal", dict(out=st[:, 6:7], in_=st[:, 5:6])), reads=[Rst], writes=[Rst])
            op("dve", ("tensor_scalar", dict(out=vsb[:], in0=vsb[:], scalar1=st[:, 2:3], scalar2=st[:, 6:7], op0=ALU.subtract, op1=ALU.mult)), reads=[Rvsb, Rst], writes=[Rvsb])
            op("dve", ("tensor_tensor", dict(out=vsb[:], in0=vsb[:], in1=lnG[:], op=ALU.mult)), reads=[Rvsb, Rbc], writes=[Rvsb])
            op("pool", ("tensor_tensor", dict(out=vnb[:], in0=vsb[:], in1=lnB[:], op=ALU.add)), reads=[Rvsb, Rbc], writes=[Rvnb])
            for g in range(8):
                mm(PS_m[:, g, :], vnb[:, g * 128:(g + 1) * 128], wsT[:, g, :], [Rvnb, Rws], [RPm])
            op("dve", ("tensor_tensor", dict(out=mixall[:, :, ti * 128:(ti + 1) * 128], in0=PS_m[:], in1=bsB[:], op=ALU.add)), reads=[RPm, Rbc], writes=[RU])
        c0 = 1 + 512 * tg
        for g in range(8):
            buf = g % 2

            (PU, RPU), (PZ, RPZ) = ((PS_c, RPc), (PS_d, RPd)) if g % 2 == 0 else ((PS_a, RPa), (PS_b, RPb))
            t2, Rt2 = t2s[g % 2], Rt2s[g % 2]
            t3, Rt3 = t3s[g % 2], Rt3s[g % 2]
            for k in range(8):
                mm(PU[:], wbig[:, 8 + k, g * 128:(g + 1) * 128], hnT[:, k, c0:c0 + 512], [Rwbig] + allh, [RPU], st=(k == 0), sp=(k == 7))
            for k in range(8):
                mm(PZ[:], wbf[g // 4][:, k, (g % 4) * 128:(g % 4 + 1) * 128], hnT[:, k, c0:c0 + 512], [Rwbf[g // 4]] + allh, [RPZ], st=(k == 0), sp=(k == 7))
            op("act", ("activation", dict(out=t3[:], in_=PZ[:], func=AF.Silu)), reads=[RPZ], writes=[Rt3])
            op("dve", ("tensor_tensor", dict(out=t2[:], in0=PU[:], in1=mixall[:, g, :], op=ALU.mult)), reads=[RPU, RU], writes=[Rt2])
            op("pool", ("tensor_tensor", dict(out=yo[g % 2][:], in0=t2[:], in1=t3[:], op=ALU.mult)), reads=[Rt2, Rt3], writes=[Ryo[g % 2]])
            S.dma(("dma_start", dict(out=yT[8 + g, :, 512 * tg:512 * (tg + 1)], in_=yo[g % 2][:])), reads=[Ryo[g % 2]], writes=[RyT[8 + g][tg]], q="pool")

    wo = I["e_w_out"].rearrange("(k p) n -> p k n", p=128)
    for q in range(2):
        for hf in range(4):
            S.dma(("dma_start", dict(out=stg[hf % 2][:], in_=wo[:, 8 * q:8 * q + 8, hf * 256:(hf + 1) * 256])), writes=[Rstg[hf % 2]])
            op("pool", ("tensor_copy", dict(out=wbig[:, 8 * q:8 * q + 8, hf * 256:(hf + 1) * 256], in_=stg[hf % 2][:])), reads=[Rstg[hf % 2]], writes=[Rwbig])
    ally = [r for row in RyT for r in row]
    h1ts = [sb("h1t%d" % i, [128, 1024]) for i in range(2)]; Rh1s = [R(), R()]
    for i in range(16):
        xb, Rx = xt[i % 2], Rxt[i % 2]
        h1t, Rh1 = h1ts[i % 2], Rh1s[i % 2]
        S.dma(("dma_start", dict(out=xb[:], in_=xh[1 + 128 * i: 1 + 128 * (i + 1), :])), writes=[Rx])
        S.dma(("dma_start", dict(out=ytl[i % 2][:], in_=yT[:, :, 128 * i:128 * (i + 1)].rearrange("k p t -> p k t"))), reads=ally, writes=[Rytl[i % 2]])
        for hf, (P_, RP_) in enumerate((((PS_a, RPa), (PS_b, RPb)) if i % 2 == 0 else ((PS_c, RPc), (PS_d, RPd)))):
            for k in range(16):
                mm(P_[:], ytl[i % 2][:, k, :], wbig[:, k, hf * 512:(hf + 1) * 512], [Rwbig, Rytl[i % 2]], [RP_], st=(k == 0), sp=(k == 15))
            op("dve", ("tensor_tensor", dict(out=h1t[:, hf * 512:(hf + 1) * 512], in0=P_[:], in1=xb[:, hf * 512:(hf + 1) * 512], op=ALU.add)), reads=[RP_, Rx], writes=[Rh1])
        S.dma(("dma_start", dict(out=O["h1"][128 * i:128 * (i + 1), :], in_=h1t[:])), reads=[Rh1], q="act")

        def dst(gt, i=i):
            op("dve", ("tensor_tensor", dict(out=hnT[:, :, 1 + 128 * i: 1 + 128 * (i + 1)], in0=PS_t[:], in1=gt[:].to_broadcast([128, 8, 128]), op=ALU.mult)), reads=[RPt, Rg], writes=[RhnT[i]])
        norm_tile(h1t, Rh1, gO, dst, None)

    wi1 = I["o_w_in"].rearrange("(k p) n -> p k n", p=128)
    blocks = [(c * 128, c * 128, False) for c in range(25)]
    blocks += [(3200 + c * 128, 3200 + c * 128, True) for c in range(8)]
    blocks += [(4736 + c * 128, 4224 + c * 128, True) for c in range(4)]
    ob = [sb("ob%d" % i, [128, 512]) for i in range(2)]; Rob = [R(), R()]
    oi = 0
    toks = []
    bi = 0
    nblocks = len(blocks)
    pairbuf = 0
    while bi < nblocks:
        sc, dr, act = blocks[bi]
        paired = (bi + 1 < nblocks) and (blocks[bi + 1][0] == sc + 128) and (blocks[bi + 1][2] == act)
        ncol = 256 if paired else 128
        buf = pairbuf % 2
        pairbuf += 1
        S.dma(("dma_start", dict(out=stg[buf][:, :, 0:ncol], in_=wi1[:, :, sc:sc + ncol])), writes=[Rstg[buf]])
        op("pool", ("tensor_copy", dict(out=wbf[buf][:, :, 0:ncol], in_=stg[buf][:, :, 0:ncol])), reads=[Rstg[buf]], writes=[Rwbf[buf]])
        for sub in range(2 if paired else 1):
            sc_, dr_, act_ = blocks[bi + sub]
            for j in range(4):
                P_, RP_ = ((PS_a, RPa), (PS_b, RPb), (PS_c, RPc), (PS_d, RPd))[j]
                c0 = 1 + 512 * j
                for k in range(8):
                    mm(P_[:], wbf[buf][:, k, sub * 128:(sub + 1) * 128], hnT[:, k, c0:c0 + 512], [Rwbf[buf]] + allh, [RP_], st=(k == 0), sp=(k == 7))
                o_, Ro = ob[oi % 2], Rob[oi % 2]
                oi += 1
                op("act", ("activation", dict(out=o_[:], in_=P_[:], func=(AF.Silu if act_ else AF.Copy))), reads=[RP_], writes=[Ro])
                toks.append(S.dma(("dma_start", dict(out=O["pT"][dr_:dr_ + 128, 512 * j:512 * (j + 1)], in_=o_[:])), reads=[Ro], q="act"))
        bi += 2 if paired else 1
    for hf in range(2):
        S.dma(("dma_start", dict(out=stg[hf][:], in_=wi1[:, :, 4224 + 256 * hf:4224 + 256 * (hf + 1)])), writes=[Rstg[hf]])
        op("pool", ("tensor_copy", dict(out=wbf[0][:, :, 256 * hf:256 * (hf + 1)], in_=stg[hf][:])), reads=[Rstg[hf]], writes=[Rwbf[0]])
    for i in range(16):
        c0 = 1 + 128 * i
        P_, RP_ = ((PS_a, RPa), (PS_b, RPb))[i % 2]
        for k in range(8):
            mm(P_[:], hnT[:, k, c0:c0 + 128], wbf[0][:, k, :], [Rwbf[0]] + allh, [RP_], st=(k == 0), sp=(k == 7))
        o_, Ro = ob[oi % 2], Rob[oi % 2]
        oi += 1
        op("act", ("activation", dict(out=o_[:], in_=P_[:], func=AF.Copy)), reads=[RP_], writes=[Ro])
        toks.append(S.dma(("dma_start", dict(out=O["fd"][128 * i:128 * (i + 1), :], in_=o_[:])), reads=[Ro], q="act"))
    return toks


def full_barrier(S):
    keys = list(S.cnt.items())
    for e in S.ENGS:
        waits = []
        for k, v in keys:
            if k == e:
                continue
            if S.seen[e].get(k, 0) < v:
                S.seen[e][k] = v
                waits.append((k, v))
        if waits:
            S.prog[e].append([waits, None, ("_none", 0)])


def emit_fnet(S, nc, I, ydT):
    R = Region
    op = S.op
    mm = lambda out, l, r_, rd, wr, st=True, sp=True: op("pe", ("matmul", dict(out=out, lhsT=l, rhs=r_, start=st, stop=sp)), reads=rd, writes=wr)
    toks = []
    with ExitStack() as es:
        sb = lambda name, shape, dt=F32: es.enter_context(nc.sbuf_tensor(name, shape, dt))
        ps = lambda name, shape, dt=F32: es.enter_context(nc.psum_tensor(name, shape, dt))
        xs = sb("f_xs", [128, 4096]); Rxs = R()
        xb = sb("f_xb", [128, 64, 128], BF16); Rxb = R()
        Fb = sb("f_F", [128, 256], BF16); RF = R()
        A_sb = sb("f_A", [64, 128, 256], BF16); RA = R()
        PQ = sb("f_PQ", [128, 2, 64, 128], BF16); RPQ = R()
        Tg = [[sb("f_T%d%d" % (i, j), [64, 16, 128], BF16) for j in range(2)] for i in range(2)]; RTg = [R(), R()]
        wf32 = sb("f_w32", [128, 128]); wfb = sb("f_wb", [128, 128], BF16); Rwf = R()
        Ccb = sb("f_Cc", [128, 128], BF16); mScb = sb("f_mSc", [128, 128], BF16); Rcs = R()
        Gb = sb("f_G", [128, 256], BF16); RG = R()
        ob = [sb("f_ob%d" % i, [128, 512]) for i in range(2)]; Rob = [R(), R()]
        PS = [ps("f_ps%d" % i, [128, 512]) for i in range(2)]; RPS = [R(), R()]
        S.dma(("dma_start", dict(out=Fb[:], in_=I["c_F"])), writes=[RF])
        S.dma(("dma_start", dict(out=Ccb[:], in_=I["c_Cc"])), writes=[Rcs])
        S.dma(("dma_start", dict(out=mScb[:], in_=I["c_mSc"])), writes=[Rcs])
        S.dma(("dma_start", dict(out=wf32[:], in_=I["fw"])), writes=[Rwf])
        op("dve", ("tensor_copy", dict(out=wfb[:], in_=wf32[:])), reads=[Rwf], writes=[Rwf])
        xbf = xb[:].rearrange("p l c -> p (l c)")
        for hf in range(2):
            S.dma(("dma_start", dict(out=xs[:], in_=I["fx"][:, hf * 4096:(hf + 1) * 4096])), writes=[Rxs])
            op("pool", ("tensor_copy", dict(out=xbf[:, hf * 4096:(hf + 1) * 4096], in_=xs[:])), reads=[Rxs], writes=[Rxb])
        for c2 in range(64):
            P_, RP_ = PS[c2 % 2], RPS[c2 % 2]
            for j in range(2):
                mm(P_[0:64, j * 256:(j + 1) * 256], xb[:, :, 2 * c2 + j], Fb[:], [Rxb, RF], [RP_])
            op("act" if c2 % 2 == 0 else "dve", ("activation", dict(out=A_sb[0:64, 2 * c2:2 * c2 + 2, :], in_=P_[0:64, :].rearrange("p (j k) -> p j k", j=2), func=AF.Copy)) if c2 % 2 == 0 else
               ("tensor_copy", dict(out=A_sb[0:64, 2 * c2:2 * c2 + 2, :], in_=P_[0:64, :].rearrange("p (j k) -> p j k", j=2))), reads=[RP_], writes=[RA])
        T1d = I["c_T1"].rearrange("p (k h) -> p k h", h=128)
        T2d = I["c_T2"].rearrange("p (k h) -> p k h", h=128)
        ei = 0
        for grp in range(8):
            tb = grp % 2
            S.dma(("dma_start", dict(out=Tg[tb][0][:], in_=T1d[:, grp * 16:(grp + 1) * 16, :])), writes=[RTg[tb]])
            S.dma(("dma_start", dict(out=Tg[tb][1][:], in_=T2d[:, grp * 16:(grp + 1) * 16, :])), writes=[RTg[tb]])
            for q in range(4):
                P_, RP_ = PS[ei % 2], RPS[ei % 2]
                for j in range(4):
                    kk_ = q * 4 + j
                    kl = grp * 16 + kk_
                    mm(P_[:, j * 128:(j + 1) * 128], A_sb[0:64, :, kl], Tg[tb][0][0:64, kk_, :], [RA, RTg[tb]], [RP_], st=True, sp=False)
                    mm(P_[:, j * 128:(j + 1) * 128], A_sb[0:64, :, 128 + kl], Tg[tb][1][0:64, kk_, :], [RA, RTg[tb]], [RP_], st=False, sp=True)
                kl0 = grp * 16 + q * 4
                for qq in range(2):
                    op("act" if qq == 0 else "dve",
                       ("activation", dict(out=PQ[:, qq, :, kl0:kl0 + 4].rearrange("p h l -> p l h"), in_=P_[:].rearrange("p (l q h) -> p l q h", l=4, q=2)[:, :, qq, :], func=AF.Copy)) if qq == 0 else
                       ("tensor_copy", dict(out=PQ[:, qq, :, kl0:kl0 + 4].rearrange("p h l -> p l h"), in_=P_[:].rearrange("p (l q h) -> p l q h", l=4, q=2)[:, :, qq, :])),
                       reads=[RP_], writes=[RPQ])
                ei += 1
        P_, RP_ = PS[0], RPS[0]
        mm(P_[:, 0:128], Ccb[:], wfb[:], [Rcs, Rwf], [RP_])
        mm(P_[:, 128:256], mScb[:], wfb[:], [Rcs, Rwf], [RP_])
        op("act", ("activation", dict(out=Gb[:], in_=P_[:, 0:256], func=AF.Copy)), reads=[RP_], writes=[RG])
        for t4 in range(16):
            P_, RP_ = PS[(t4 + 1) % 2], RPS[(t4 + 1) % 2]
            for j in range(4):
                kh = 4 * t4 + j
                mm(P_[:, j * 128:(j + 1) * 128], Gb[:, 0:128], PQ[:, 0, kh, :], [RG, RPQ], [RP_], st=True, sp=False)
                mm(P_[:, j * 128:(j + 1) * 128], Gb[:, 128:256], PQ[:, 1, kh, :], [RG, RPQ], [RP_], st=False, sp=True)
            o_, Ro = ob[t4 % 2], Rob[t4 % 2]
            op("act", ("activation", dict(out=o_[:], in_=P_[:], func=AF.Copy)), reads=[RP_], writes=[Ro])
            toks.append(S.dma(("dma_start", dict(out=ydT[:, 512 * t4:512 * (t4 + 1)], in_=o_[:])), reads=[Ro]))
    full_barrier(S)
    return toks


def emit_p3(S, nc, I, yout):
    sb = lambda name, shape, dt=F32: nc.alloc_sbuf_tensor(name, shape, dt)
    ps = lambda name, shape, dt=F32: nc.alloc_psum_tensor(name, shape, dt)
    R = Region
    op = S.op
    mm = lambda out, l, r_, rd, wr, st=True, sp=True: op("pe", ("matmul", dict(out=out, lhsT=l, rhs=r_, start=st, stop=sp)), reads=rd, writes=wr)
    stg = [sb("stg%d" % i, [128, 8, 256]) for i in range(2)]; Rstg = [R(), R()]
    wO = sb("wO", [128, 12, 1024], BF16); RwO = R()
    gN = sb("gN", [128, 1024]); RgN = R()
    gt_all = sb("gt_all", [128, 12, 2048], BF16); Rgt = R()
    ya = [sb("ya%d" % i, [128, 512]) for i in range(2)]; Rya = [R(), R()]
    ga = [sb("ga%d" % i, [128, 512]) for i in range(2)]; Rga = [R(), R()]
    h1t = [sb("h1t%d" % i, [128, 1024]) for i in range(2)]; Rh1 = [R(), R()]
    h2 = sb("h2", [128, 1024]); Rh2 = R()
    sq = sb("sq", [128, 1024]); Rsq = R()
    st = sb("st", [128, 8]); Rst = R()
    yo = [sb("yo%d" % i, [128, 1024]) for i in range(2)]; Ryo = [R(), R()]
    PS_a = ps("PS_a", [128, 512]); RPa = R()
    PS_b = ps("PS_b", [128, 512]); RPb = R()
    wo3 = I["o_w_out"].rearrange("(k p) n -> p k n", p=128)
    si = 0
    for (k0, nk) in ((0, 8), (8, 4)):
        for cq in range(4):
            b_ = si % 2; si += 1
            S.dma(("dma_start", dict(out=stg[b_][:, 0:nk, :], in_=wo3[:, k0:k0 + nk, cq * 256:(cq + 1) * 256])), writes=[Rstg[b_]])
            op("pool", ("tensor_copy", dict(out=wO[:, k0:k0 + nk, cq * 256:(cq + 1) * 256], in_=stg[b_][:, 0:nk, :])), reads=[Rstg[b_]], writes=[RwO])
    S.dma(("dma_start", dict(out=gN[:], in_=I["final_norm_g"].partition_broadcast(128))), writes=[RgN])
    ii = 0
    for blk in range(12):
        src = I["ycT"][blk * 128:(blk + 1) * 128] if blk < 8 else I["ydT"][(blk - 8) * 128:(blk - 7) * 128]
        gsrc = I["gT"][blk * 128:(blk + 1) * 128]
        for j in range(4):
            b_ = ii % 2; ii += 1
            S.dma(("dma_start", dict(out=ya[b_][:], in_=src[:, 512 * j:512 * (j + 1)])), writes=[Rya[b_]])
            S.dma(("dma_start", dict(out=ga[b_][:], in_=gsrc[:, 512 * j:512 * (j + 1)])), writes=[Rga[b_]])
            op("dve" if ii % 2 else "pool", ("tensor_tensor", dict(out=gt_all[:, blk, 512 * j:512 * (j + 1)], in0=ya[b_][:], in1=ga[b_][:], op=ALU.mult)), reads=[Rya[b_], Rga[b_]], writes=[Rgt])
    toks = []
    for i in range(16):
        hb, Rh = h1t[i % 2], Rh1[i % 2]
        S.dma(("dma_start", dict(out=hb[:], in_=I["h1"][128 * i:128 * (i + 1), :])), writes=[Rh])
        for hf, (P_, RP_) in enumerate(((PS_a, RPa), (PS_b, RPb))):
            for k in range(12):
                mm(P_[:], gt_all[:, k, 128 * i:128 * (i + 1)], wO[:, k, hf * 512:(hf + 1) * 512], [Rgt, RwO], [RP_], st=(k == 0), sp=(k == 11))
            op("dve", ("tensor_tensor", dict(out=h2[:, hf * 512:(hf + 1) * 512], in0=P_[:], in1=hb[:, hf * 512:(hf + 1) * 512], op=ALU.add)), reads=[RP_, Rh], writes=[Rh2])
        op("act", ("activation", dict(out=sq[:], in_=h2[:], func=AF.Square)), reads=[Rh2], writes=[Rsq])
        op("dve", ("reduce_sum", dict(out=st[:, 0:1], in_=sq[:], axis=AX.X)), reads=[Rsq], writes=[Rst])
        op("dve", ("tensor_scalar", dict(out=st[:, 1:2], in0=st[:, 0:1], scalar1=1.0 / 1024, scalar2=1e-6, op0=ALU.mult, op1=ALU.add)), reads=[Rst], writes=[Rst])
        op("act", ("activation", dict(out=st[:, 2:3], in_=st[:, 1:2], func=AF.Sqrt)), reads=[Rst], writes=[Rst])
        op("dve", ("reciprocal", dict(out=st[:, 3:4], in_=st[:, 2:3])), reads=[Rst], writes=[Rst])
        op("dve", ("tensor_scalar", dict(out=h2[:], in0=h2[:], scalar1=st[:, 3:4], scalar2=None, op0=ALU.mult)), reads=[Rh2, Rst], writes=[Rh2])
        o_, Ro = yo[i % 2], Ryo[i % 2]
        op("pool", ("tensor_tensor", dict(out=o_[:], in0=h2[:], in1=gN[:], op=ALU.mult)), reads=[Rh2, RgN], writes=[Ro])
        toks.append(S.dma(("dma_start", dict(out=yout[128 * i:128 * (i + 1), :], in_=o_[:])), reads=[Ro], q="pool"))
    return toks


def _mk(nc, name, shape, dt=None, out=False):
    return nc.dram_tensor(name, list(shape), dt or F32, kind=("ExternalOutput" if out else "ExternalInput")).ap()


W1 = ["e_norm_g", "e_w_in", "e_conv_w", "e_sgu_ln_g", "e_sgu_ln_b", "e_sgu_w", "e_sgu_b", "e_w_out", "o_norm_g", "o_w_in"]


def build_l1(shapes):
    nc = bass.Bass("TRN2", target_bir_lowering=False)
    I = {"xh": _mk(nc, "xh", [2050, 1024]), "c_ident": _mk(nc, "c_ident", [128, 128])}
    for n in W1:
        I[n] = _mk(nc, n, shapes[n])
    O = {"h1": _mk(nc, "h1", [2048, 1024], out=True), "pT": _mk(nc, "pT", [4736, 2048], out=True),
         "fd": _mk(nc, "fd", [2048, 512], out=True)}
    S = Sched(nc)
    toks = emit_p1(S, nc, I, O)
    S.barrier_on("sp", toks)
    S.finalize()
    return nc


def build_l2(consts):
    NB, T = 2, 8192
    nc = bass.Bass("TRN2", target_bir_lowering=False)
    pr, pk, pv, pwa = (_mk(nc, n, [128, NB, T + 2]) for n in ("pr", "pk", "pv", "pwa"))
    prm = _mk(nc, "prm", [128, 17]); w2a2 = _mk(nc, "w2a2", [128, 2, 128])
    A = {k: _mk(nc, k, v.shape) for k, v in consts.items()}
    FI = {"fx": _mk(nc, "fx", [128, 8192]), "fw": _mk(nc, "fw", [128, 128]),
          "c_F": _mk(nc, "c_F", [128, 256], BF16), "c_T1": _mk(nc, "c_T1", [64, 16384], BF16),
          "c_T2": _mk(nc, "c_T2", [64, 16384], BF16), "c_Cc": _mk(nc, "c_Cc", [128, 128], BF16),
          "c_mSc": _mk(nc, "c_mSc", [128, 128], BF16)}
    yout = _mk(nc, "yout", [128, NB, T], out=True)
    ydT = _mk(nc, "ydT", [128, T], out=True)
    S = Sched(nc)
    toks = emit_fnet(S, nc, FI, ydT)
    toks += emit_rwkv(S, nc, A, pr, pk, pv, pwa, prm, w2a2, yout, NB, T)
    S.barrier_on("sp", toks)
    S.finalize()
    return nc


def build_l3():
    nc = bass.Bass("TRN2", target_bir_lowering=False)
    I = {"ycT": _mk(nc, "ycT", [1024, 2048]), "ydT": _mk(nc, "ydT", [512, 2048]), "gT": _mk(nc, "gT", [1536, 2048]),
         "h1": _mk(nc, "h1", [2048, 1024]), "o_w_out": _mk(nc, "o_w_out", [1536, 1024]),
         "final_norm_g": _mk(nc, "final_norm_g", [1024])}
    y = _mk(nc, "y", [2048, 1024], out=True)
    S = Sched(nc)
    toks = emit_p3(S, nc, I, y)
    S.barrier_on("sp", toks)
    S.finalize()
    return nc


def fnet_tables():
    import ml_dtypes
    N = 8192
    nh = np.arange(128); kl = np.arange(128)
    ang = 2 * np.pi * np.outer(nh, kl) / 128
    F = np.concatenate([np.cos(ang), np.sin(ang)], axis=1)
    nl = np.arange(64)[:, None, None]; klo = np.arange(128)[None, :, None]; kh = np.arange(64)[None, None, :]
    beta = 2 * np.pi * ((nl * (klo + 128 * kh)) % N) / N
    T1 = np.concatenate([np.cos(beta), np.sin(beta)], axis=2).reshape(64, 16384)
    T2 = np.concatenate([-np.sin(beta), np.cos(beta)], axis=2).reshape(64, 16384)
    c = np.arange(128); phi = 2 * np.pi * np.outer(c, c) / 128
    nrm = 1 / np.sqrt(N * 128)
    bf = lambda a: np.ascontiguousarray(a.astype(np.float32)).astype(ml_dtypes.bfloat16)
    return {"c_F": bf(F), "c_T1": bf(T1), "c_T2": bf(T2), "c_Cc": bf(np.cos(phi) * nrm), "c_mSc": bf(-np.sin(phi) * nrm)}


def kernel(**inputs):
    f32 = lambda a: np.ascontiguousarray(np.asarray(a), dtype=np.float32)
    inp = {k: f32(v) for k, v in inputs.items()}
    x = inp["x"]
    ncores = 8
    cores = list(range(ncores))
    w1 = {n: np.ascontiguousarray(inp[n][0]) for n in W1}
    ident = np.eye(128, dtype=np.float32)
    maps = []
    for c in cores:
        b, s0 = c // 4, (c % 4) * 2048
        xh = np.zeros((2050, 1024), np.float32)
        xh[1:2049] = x[b, s0:s0 + 2048]
        if s0 > 0:
            xh[0] = x[b, s0 - 1]
        if s0 + 2048 < 8192:
            xh[2049] = x[b, s0 + 2048]
        m = {"xh": xh, "c_ident": ident}
        m.update(w1)
        maps.append(m)
    nc1 = build_l1({n: w1[n].shape for n in W1})
    r1 = run_bass_kernel_spmd(nc1, maps, core_ids=cores).results
    PT = np.concatenate([np.asarray(r["pT"]) for r in r1], axis=1)
    FD = np.concatenate([np.asarray(r["fd"]) for r in r1], axis=0)
    consts = build_consts_np()
    ft = fnet_tables()
    mu, w0, w2, a0, a2 = inp["o_mu"][0], inp["o_w0"][0], inp["o_w2"][0], inp["o_a0"][0], inp["o_a2"][0]
    k_k, k_a, r_k = inp["o_k_k"][0], inp["o_k_a"][0], inp["o_r_k"][0].reshape(-1)
    lg, lb = inp["o_lnx_g"][0], inp["o_lnx_b"][0]
    PT3 = PT.reshape(4736, 2, 8192)
    pad = lambda a: np.ascontiguousarray(np.pad(a, ((0, 0), (0, 0), (1, 1))))
    maps = []
    for c in cores:
        ch = slice(c * 128, (c + 1) * 128)
        m = {"pr": pad(PT3[0:1024][ch]), "pk": pad(PT3[1024:2048][ch]), "pv": pad(PT3[2048:3072][ch]),
             "pwa": pad(PT3[3072:3200])}
        prm = np.zeros((128, 17), np.float32)
        for d in range(2):
            prm[:, 0 + d] = mu[d, 0:1024][ch]; prm[:, 2 + d] = mu[d, 1024:2048][ch]; prm[:, 4 + d] = mu[d, 2048:3072][ch]
            prm[:, 6 + d] = mu[d, 3072:3200]; prm[:, 8 + d] = w0[d][ch]; prm[:, 10 + d] = a0[d][ch]
        prm[:, 12] = k_k[ch]; prm[:, 13] = k_a[ch]; prm[:, 14] = r_k[ch]; prm[:, 15] = lg[ch]; prm[:, 16] = lb[ch]
        m["prm"] = prm
        m["w2a2"] = np.ascontiguousarray(np.concatenate([w2[:, :, ch], a2[:, :, ch]], axis=1).transpose(1, 0, 2))
        m.update(consts)
        b, g = c // 4, c % 4
        m["fx"] = np.ascontiguousarray(FD[b * 8192:(b + 1) * 8192, g * 128:(g + 1) * 128]).reshape(128, 8192)
        m["fw"] = np.ascontiguousarray(inp["o_fnet_w"][0, g])
        m.update(ft)
        maps.append(m)
    nc2 = build_l2(consts)
    r2 = run_bass_kernel_spmd(nc2, maps, core_ids=cores).results
    YC = np.concatenate([np.asarray(r["yout"]).reshape(128, 16384) for r in r2], axis=0)
    YD = np.concatenate([np.concatenate([np.asarray(r2[b * 4 + g]["ydT"]) for g in range(4)], axis=0) for b in range(2)], axis=1)
    maps = []
    for c in cores:
        ts = slice(c * 2048, (c + 1) * 2048)
        maps.append({"ycT": np.ascontiguousarray(YC[:, ts]), "ydT": np.ascontiguousarray(YD[:, ts]),
                     "gT": np.ascontiguousarray(PT[3200:4736, ts]), "h1": np.asarray(r1[c]["h1"]),
                     "o_w_out": np.ascontiguousarray(inp["o_w_out"][0]), "final_norm_g": inp["final_norm_g"]})
    nc3 = build_l3()
    r3 = run_bass_kernel_spmd(nc3, maps, core_ids=cores).results
    y = np.concatenate([np.asarray(r["y"]) for r in r3], axis=0).reshape(2, 8192, 1024)
    return y.astype(np.float32)
```

```python
from contextlib import ExitStack
import itertools
import numpy as np
import concourse.bass as bass
import concourse.mybir as mybir
from concourse.bass_utils import run_bass_kernel_spmd


F32 = mybir.dt.float32
BF16 = mybir.dt.bfloat16
AF = mybir.ActivationFunctionType
ALU = mybir.AluOpType
AX = mybir.AxisListType

N_DMA_SEMS = 8


class Region:
    __slots__ = ("w", "r", "name")

    def __init__(self, name=""):
        self.w = None
        self.r = {}
        self.name = name


class Sched:
    ENGS = ("pe", "dve", "act", "pool", "sp")

    def __init__(self, nc):
        self.nc = nc
        self.prog = {e: [] for e in self.ENGS}
        self.cnt = {}
        self.seen = {e: {} for e in self.ENGS}
        self.dma_rr = {e: 0 for e in self.ENGS}
        self.dma_last = {}
        self.same_engine_raw = True
        self.cut = 0
        self.raw_only = True
        self.nrec = 0
        self.log = []

    def _collect(self, eng, mykey, reads, writes):
        waits = {}

        def need(tok, kind):
            if tok is None:
                return
            k, v = tok
            if k == mykey:
                if eng == "pe":
                    return
                if not self.same_engine_raw:
                    return
                if self.raw_only and kind != "raw":
                    return
            if waits.get(k, 0) < v:
                waits[k] = v

        for R in reads:
            need(R.w, "raw")
        for R in writes:
            need(R.w, "waw")
            for k, v in R.r.items():
                need((k, v), "war")
        out = []
        seen = self.seen[eng]
        for k, v in waits.items():
            if seen.get(k, 0) < v:
                seen[k] = v
                out.append((k, v))
        return out

    def _commit(self, tok, reads, writes):
        for R in writes:
            R.w = tok
            R.r = {}
        k, v = tok
        for R in reads:
            if R.r.get(k, 0) < v:
                R.r[k] = v

    def op(self, eng, fn, reads=(), writes=()):
        self.nrec += 1
        if self.cut and self.nrec > self.cut:
            return None
        if self.cut:
            self.log.append((self.nrec, eng, fn[0] if isinstance(fn, tuple) else "fn", str(fn[1].get("out", ""))[:120] if isinstance(fn, tuple) else ""))
        key = eng
        waits = self._collect(eng, key, reads, writes)
        idx = self.cnt.get(key, 0) + 1
        self.cnt[key] = idx
        tok = (key, idx)
        self.prog[eng].append([waits, fn, tok])
        self._commit(tok, reads, writes)
        return tok

    def dma(self, fn, reads=(), writes=(), q="sp"):
        self.nrec += 1
        if self.cut and self.nrec > self.cut:
            return None
        i = self.dma_rr[q]
        self.dma_rr[q] = (i + 1) % N_DMA_SEMS
        key = "dma_%s_%d" % (q, i)
        waits = self._collect(q, key, reads, writes)
        prev = self.cnt.get(key, 0)
        if prev > 0 and self.seen[q].get(key, 0) < prev:
            self.seen[q][key] = prev
            waits.append((key, prev))
        idx = prev + 1
        self.cnt[key] = idx
        tok = (key, idx)
        self.prog[q].append([waits, fn, tok])
        self._commit(tok, reads, writes)
        return tok

    def finalize(self):
        nc = self.nc
        waited = {}
        for e in self.ENGS:
            for waits, fn, tok in self.prog[e]:
                for k, v in waits:
                    waited.setdefault(k, set()).add(v)
        self.final_waits = []
        sem_of = {}
        val_of = {}
        for k, s in waited.items():
            sem_of[k] = nc.alloc_semaphore("s_" + k)
            isdma = k.startswith("dma_")
            step = 16 if isdma else 1
            if isdma:
                val_of[k] = None
            else:
                val_of[k] = {v: (i + 1) for i, v in enumerate(sorted(s))}
        engobj = {"pe": nc.tensor, "dve": nc.vector, "act": nc.scalar,
                  "pool": nc.gpsimd, "sp": nc.sync}

        def value(k, v):
            if val_of[k] is None:
                return 16 * v
            return val_of[k][v]

        def emit(e):
            def body(eng):
                for waits, fn, tok in self.prog[e]:
                    for k, v in waits:
                        eng.wait_ge(sem_of[k], value(k, v))
                    if fn is None:
                        continue
                    if isinstance(fn, tuple):
                        ins = getattr(eng, fn[0])(**fn[1])
                    else:
                        ins = fn(eng)
                    k, v = tok
                    if k in sem_of:
                        if val_of[k] is None:
                            ins.then_inc(sem_of[k], 16)
                        elif v in val_of[k]:
                            ins.then_inc(sem_of[k], 1)
            return body

        with nc.Block() as block:
            for e, dec in (("sp", block.sync), ("pe", block.tensor), ("dve", block.vector),
                           ("act", block.scalar), ("pool", block.gpsimd)):
                if self.prog[e]:
                    dec(emit(e))
        self.n_sems = len(sem_of)
        return self.n_sems

    def barrier_on(self, eng, toks):
        waits = []
        for tk in toks:
            if tk is None:
                continue
            k, v = tk
            if self.seen[eng].get(k, 0) < v:
                self.seen[eng][k] = v
                waits.append((k, v))
        if waits:
            self.prog[eng].append([waits, None, ("_none", 0)])


C = 128
BLK = 512
NEG_E = -float(np.exp(-0.5))
GN_EPS = 64e-5


def build_consts_np():
    idx = np.arange(128)
    lt = (idx[:, None] < idx[None, :]).astype(np.float32)
    le = (idx[:, None] <= idx[None, :]).astype(np.float32)
    gt = lt.T.copy()
    ge = le.T.copy()
    m4f = np.stack([lt, gt, gt, le], axis=1)
    m4b = np.stack([gt, lt, lt, ge], axis=1)
    mk = np.stack([le, ge], axis=1)
    ident = np.eye(128, dtype=np.float32)
    bd = np.kron(np.eye(2, dtype=np.float32), np.ones((64, 64), np.float32))
    scanm = np.ones((128, BLK), np.float32)
    scanm[:, ::C] = 0.0
    return {"c_m4": np.stack([m4f, m4b], axis=1).reshape(128, 2 * 4 * 128).copy(),
            "c_mk": mk.reshape(128, 256).copy(), "c_ident": ident, "c_bd": bd, "c_scanm": scanm}


XST = False


def emit_rwkv(S, nc, A, pr, pk, pv, pwa, prm, w2a2, yout, NB, T):
    sb = lambda name, shape, dt=F32: nc.alloc_sbuf_tensor(name, shape, dt)
    ps = lambda name, shape, dt=F32: nc.alloc_psum_tensor(name, shape, dt)
    R = Region
    nblk = T // BLK

    m4f = sb("m4f", [128, 2, 4, 128]); Rm4 = R()
    mkf = sb("mkf", [128, 2, 128]); Rmk = R()
    identf = sb("identf", [128, 128]); Ridf = R()
    identb = sb("identb", [128, 128], BF16); Ridb = R()
    bdf = sb("bdf", [128, 128]); Rbd = R()
    bdr = sb("bdr", [128, 128]); Rbdr = R()
    bdm = sb("bdm", [128, 128]); Rbdm = R()
    scanm = sb("scanm", [128, BLK]); Rsc = R()
    prmt = sb("prmt", [128, 17]); Rprm = R()
    w2f = sb("w2f", [128, 2, 128]); Rw2f = R()
    w2b = sb("w2b", [128, 2, 128], BF16); Rw2b = R()
    S.dma(("dma_start", dict(out=m4f[:].rearrange("p a b c -> p (a b c)"), in_=A["c_m4"])), writes=[Rm4])
    S.dma(("dma_start", dict(out=mkf[:].rearrange("p a c -> p (a c)"), in_=A["c_mk"])), writes=[Rmk])
    S.dma(("dma_start", dict(out=identf[:], in_=A["c_ident"])), writes=[Ridf])
    S.dma(("dma_start", dict(out=bdf[:], in_=A["c_bd"])), writes=[Rbd])
    S.dma(("dma_start", dict(out=scanm[:], in_=A["c_scanm"])), writes=[Rsc])
    S.dma(("dma_start", dict(out=prmt[:], in_=prm)), writes=[Rprm])
    S.dma(("dma_start", dict(out=w2f[:], in_=w2a2)), writes=[Rw2f])
    S.op("dve", ("tensor_copy", dict(out=identb[:], in_=identf[:])), reads=[Ridf], writes=[Ridb])
    S.op("dve", ("tensor_copy", dict(out=w2b[:], in_=w2f[:])), reads=[Rw2f], writes=[Rw2b])
    PM = lambda c: prmt[:, c:c + 1]
    S.op("dve", ("tensor_scalar", dict(out=bdr[:], in0=bdf[:], scalar1=PM(14), scalar2=None, op0=ALU.mult)), reads=[Rbd, Rprm], writes=[Rbdr])
    S.op("dve", ("tensor_scalar", dict(out=bdm[:], in0=bdf[:], scalar1=1.0 / 64, scalar2=None, op0=ALU.mult)), reads=[Rbd], writes=[Rbdm])

    def T2(name, dt=F32, n=BLK):
        return sb(name, [128, n], dt), R()
    ld = {}
    for nm in ("pr", "pk", "pv", "pwa"):
        ld[nm] = (sb("ld_" + nm, [128, BLK + 2]), R())
    tmp, Rtmp = T2("tmp")
    qr, Rqr = T2("qr"); qk, Rqk = T2("qk"); qv, Rqv = T2("qv"); qwa, Rqwa = T2("qwa")
    twa, Rtwa = T2("twa", BF16)
    sw, Rsw = T2("sw"); asg, Rasg = T2("asg")
    logw, Rlogw = T2("logw"); lin, Rlin = T2("lin"); linm, Rlinm = T2("linm"); lexm, Rlexm = T2("lexm")
    lex, Rlex = T2("lex"); lint, Rlint = T2("lint")
    e1, Re1 = T2("e1"); e1x, Re1x = T2("e1x"); e2, Re2 = T2("e2"); e3S = [sb("e3%d" % i, [128, BLK]) for i in range(2)]; Re3S = [R(), R()]; e3x, Re3x = T2("e3x"); e4, Re4 = T2("e4")
    kk, Rkk = T2("kk"); kk2, Rkk2 = T2("kk2"); rin, Rrin = T2("rin"); kkn, Rkkn = T2("kkn")
    kp, Rkp = T2("kp"); bv, Rbv = T2("bv"); rk, Rrk = T2("rk")
    rtS = [sb("rt%d" % i, [128, BLK], BF16) for i in range(2)]; RrtS = [R(), R()]; atS = [sb("at%d" % i, [128, BLK], BF16) for i in range(2)]; RatS = [R(), R()]; ktS = [sb("kt%d" % i, [128, BLK], BF16) for i in range(2)]; RktS = [R(), R()]; btS = [sb("bt%d" % i, [128, BLK], BF16) for i in range(2)]; RbtS = [R(), R()]
    r0S = [sb("r0%d" % i, [128, BLK]) for i in range(2)]; Rr0S = [R(), R()]; a0bS = [sb("a0b%d" % i, [128, BLK], BF16) for i in range(2)]; Ra0bS = [R(), R()]; kEbS = [sb("kEb%d" % i, [128, BLK], BF16) for i in range(2)]; RkEbS = [R(), R()]; bEbS = [sb("bEb%d" % i, [128, BLK], BF16) for i in range(2)]; RbEbS = [R(), R()]
    qvbS = [sb("qvb%d" % i, [128, BLK], BF16) for i in range(2)]; RqvbS = [R(), R()]
    ysum = sb("ysum", [128, T]); Rys = [R() for _ in range(T // C)]
    bsum = sb("bsum", [128, T]); Rbs = [R() for _ in range(nblk)]
    TT = [sb("TT%d" % i, [128, 4, 128], BF16) for i in range(2)]; RTT = [R(), R()]
    SBM = [sb("SBM%d" % i, [128, 2, 4, 128], BF16) for i in range(2)]; RSBM = [R(), R()]
    MKR = [sb("MKR%d" % i, [128, 2, 128]) for i in range(2)]; RMKR = [R(), R()]
    SX = [sb("SX%d" % i, [128, 2, 192], BF16) for i in range(2)]; RSX = [R(), R()]
    SAB = [sb("SAB%d" % i, [128, 2, 2, 128], BF16) for i in range(2)]; RSAB = [R(), R()]
    Gb = sb("Gb", [128, 128], BF16); RGb = R()
    Hb = sb("Hb", [128, 2, 128], BF16); RHb = R()
    Pb = sb("Pb", [128, 64], BF16); RPb = R()
    Zb = sb("Zb", [128, 2, 64], BF16); RZb = R()
    STz = [sb("STz%d" % h, [128, 64], BF16) for h in range(2)]; RST = [R(), R()]
    identP = sb("identP", [128, 64]); mkb = sb("mkb", [128, 2, 2, 128])
    HS = [slice(0, 64), slice(64, 128)]
    fin1, Rfin1 = T2("fin1"); fin2, Rfin2 = T2("fin2"); fin3, Rfin3 = T2("fin3")

    PS_M = ps("PS_M", [128, 2, 4, 128]); RPS_M = R()
    PS_K = ps("PS_K", [128, 512]); RPS_K = R()
    PS_X = [ps("PS_X%d" % h, [128, 512]) for h in range(2)]; RPS_X = R()
    PS_AB = ps("PS_AB", [128, 2, 2, 128]); RPS_AB = R()
    PS_G = ps("PS_G", [128, 512]); RPS_G = R()
    PS_T = ps("PS_T", [128, 8, 128], BF16); RPS_T = R()
    PS_P1 = PS_AB[:].rearrange("p a b c -> p (a b c)"); RPS_P1 = RPS_AB
    PS_P2 = PS_P1; RPS_P2 = RPS_AB
    mm = lambda out, l, r_, rd, wr, st=True, sp=True, sg=False: S.op("pe", ("matmul", dict(out=out, lhsT=l, rhs=r_, start=st, stop=sp, skip_group_check=sg)), reads=rd, writes=wr)
    S.op("pool", ("tensor_copy", dict(out=identP[0:64, :], in_=identf[0:64, 0:64])), reads=[Ridf], writes=[Ridf])
    S.op("pool", ("tensor_copy", dict(out=identP[64:128, :], in_=identf[64:128, 64:128])), reads=[Ridf], writes=[Ridf])
    for h in range(2):
        S.op("pool", ("tensor_copy", dict(out=mkb[:, :, h, :], in_=mkf[:])), reads=[Rmk], writes=[Rmk])
    ytmp = sb("ytmp", [128, 128]); Rytmp = R()
    out_toks = []
    NFILL = 4
    NPREP = 2
    def prep_gen(b, d, blk, pp):
        bwd = (d == 1)
        midc, totc = (C // 2 - 1, C - 1) if not bwd else (C // 2, 0)
        t0 = blk * BLK
        rt_, Rrt_ = rtS[pp], RrtS[pp]
        at_, Rat_ = atS[pp], RatS[pp]
        kt_, Rkt_ = ktS[pp], RktS[pp]
        bt_, Rbt_ = btS[pp], RbtS[pp]
        r0_, Rr0_ = r0S[pp], Rr0S[pp]
        a0b_, Ra0b_ = a0bS[pp], Ra0bS[pp]
        kEb_, RkEb_ = kEbS[pp], RkEbS[pp]
        bEb_, RbEb_ = bEbS[pp], RbEbS[pp]
        qvb_, Rqvb_ = qvbS[pp], RqvbS[pp]
        e3_, Re3_ = e3S[pp], Re3S[pp]
        for nm, src in (("pr", pr), ("pk", pk), ("pv", pv), ("pwa", pwa)):
            tl, Rl = ld[nm]
            S.dma(("dma_start", dict(out=tl[:], in_=src[:, b, t0:t0 + BLK + 2])), writes=[Rl])
            yield
        sh = (slice(0, BLK) if not bwd else slice(2, BLK + 2))
        cur = slice(1, BLK + 1)
        for nm, q, Rq, mc in (("pr", qr, Rqr, 0), ("pk", qk, Rqk, 2), ("pv", qv, Rqv, 4), ("pwa", qwa, Rqwa, 6)):
            tl, Rl = ld[nm]
            S.op("dve", ("tensor_tensor", dict(out=tmp[:], in0=tl[:, sh], in1=tl[:, cur], op=ALU.subtract)), reads=[Rl], writes=[Rtmp])
            yield
            S.op("dve", ("scalar_tensor_tensor", dict(out=q[:], in0=tmp[:], scalar=PM(mc + d), in1=tl[:, cur], op0=ALU.mult, op1=ALU.add)), reads=[Rtmp, Rl, Rprm], writes=[Rq])
            yield
        S.op("act", ("activation", dict(out=twa[0:64, :], in_=qwa[0:64, :], func=AF.Tanh)), reads=[Rqwa], writes=[Rtwa])
        yield
        S.op("dve", ("tensor_copy", dict(out=twa[64:128, :], in_=qwa[64:128, :])), reads=[Rqwa], writes=[Rtwa])
        yield
        S.op("pe", ("matmul", dict(out=PS_P1, lhsT=w2b[0:64, d, :], rhs=twa[0:64, :], start=True, stop=True)), reads=[Rw2b, Rtwa], writes=[RPS_P1])
        S.op("act", ("activation", dict(out=sw[:], in_=PS_P1, func=AF.Sigmoid, bias=PM(8 + d))), reads=[RPS_P1, Rprm], writes=[Rsw])
        yield
        S.op("pe", ("matmul", dict(out=PS_P2, lhsT=w2b[64:128, d, :], rhs=twa[64:128, :], start=True, stop=True)), reads=[Rw2b, Rtwa], writes=[RPS_P2])
        S.op("act", ("activation", dict(out=asg[:], in_=PS_P2, func=AF.Sigmoid, bias=PM(10 + d))), reads=[RPS_P2, Rprm], writes=[Rasg])
        yield
        S.op("dve", ("tensor_scalar", dict(out=logw[:], in0=sw[:], scalar1=NEG_E, scalar2=None, op0=ALU.mult)), reads=[Rsw], writes=[Rlogw])
        yield
        S.op("dve", ("tensor_tensor_scan", dict(out=lin[:], data0=scanm[:], data1=logw[:], initial=0.0, op0=ALU.mult, op1=ALU.add)), reads=[Rsc, Rlogw], writes=[Rlin])
        yield
        lin3 = lambda tl: tl[:].rearrange("p (c t) -> p c t", t=C)
        bc = lambda tl, col: lin3(tl)[:, :, col:col + 1].to_broadcast([128, BLK // C, C])
        if bwd:
            S.op("dve", ("tensor_tensor", dict(out=lin3(tmp), in0=bc(lin, C - 1), in1=lin3(lin), op=ALU.subtract)), reads=[Rlin], writes=[Rtmp])
            yield
            S.op("dve", ("tensor_tensor", dict(out=lin[:], in0=tmp[:], in1=logw[:], op=ALU.add)), reads=[Rtmp, Rlogw], writes=[Rlin])
            yield
        S.op("dve", ("tensor_tensor", dict(out=lin3(linm), in0=lin3(lin), in1=bc(lin, midc), op=ALU.subtract)), reads=[Rlin], writes=[Rlinm])
        yield
        S.op("dve", ("tensor_tensor", dict(out=lexm[:], in0=linm[:], in1=logw[:], op=ALU.subtract)), reads=[Rlinm, Rlogw], writes=[Rlexm])
        yield
        S.op("dve", ("tensor_tensor", dict(out=lex[:], in0=lin[:], in1=logw[:], op=ALU.subtract)), reads=[Rlin, Rlogw], writes=[Rlex])
        yield
        S.op("dve", ("tensor_tensor", dict(out=lin3(lint), in0=lin3(lin), in1=bc(lin, totc), op=ALU.subtract)), reads=[Rlin], writes=[Rlint])
        yield
        S.op("act", ("activation", dict(out=e1[:], in_=linm[:], func=AF.Exp)), reads=[Rlinm], writes=[Re1])
        yield
        S.op("act", ("activation", dict(out=e1x[:], in_=lexm[:], func=AF.Exp)), reads=[Rlexm], writes=[Re1x])
        yield
        S.op("act", ("activation", dict(out=e2[:], in_=linm[:], func=AF.Exp, scale=-1.0)), reads=[Rlinm], writes=[Re2])
        yield
        S.op("act", ("activation", dict(out=e3_[:], in_=lin[:], func=AF.Exp)), reads=[Rlin], writes=[Re3_])
        yield
        S.op("act", ("activation", dict(out=e3x[:], in_=lex[:], func=AF.Exp)), reads=[Rlex], writes=[Re3x])
        yield
        S.op("act", ("activation", dict(out=e4[:], in_=lint[:], func=AF.Exp, scale=-1.0)), reads=[Rlint], writes=[Re4])
        yield
        S.op("dve", ("tensor_scalar", dict(out=kk[:], in0=qk[:], scalar1=PM(12), scalar2=None, op0=ALU.mult)), reads=[Rqk, Rprm], writes=[Rkk])
        yield
        S.op("pool", ("tensor_tensor", dict(out=kk2[:], in0=kk[:], in1=kk[:], op=ALU.mult)), reads=[Rkk], writes=[Rkk2])
        yield
        S.op("pe", ("matmul", dict(out=PS_P1, lhsT=bdf[:], rhs=kk2[:], start=True, stop=True)), reads=[Rbd, Rkk2], writes=[RPS_P1])
        S.op("dve", ("tensor_scalar", dict(out=rin[:], in0=PS_P1, scalar1=1e-12, scalar2=None, op0=ALU.max)), reads=[RPS_P1], writes=[Rrin])
        yield
        S.op("act", ("activation", dict(out=rin[:], in_=rin[:], func=AF.Sqrt)), reads=[Rrin], writes=[Rrin])
        yield
        S.op("dve", ("reciprocal", dict(out=rin[:], in_=rin[:])), reads=[Rrin], writes=[Rrin])
        yield
        S.op("dve", ("tensor_tensor", dict(out=kkn[:], in0=kk[:], in1=rin[:], op=ALU.mult)), reads=[Rkk, Rrin], writes=[Rkkn])
        yield
        S.op("dve", ("tensor_scalar", dict(out=tmp[:], in0=asg[:], scalar1=-1.0, scalar2=PM(13), op0=ALU.add, op1=ALU.mult)), reads=[Rasg, Rprm], writes=[Rtmp])
        yield
        S.op("dve", ("scalar_tensor_tensor", dict(out=kp[:], in0=tmp[:], scalar=1.0, in1=qk[:], op0=ALU.add, op1=ALU.mult)), reads=[Rtmp, Rqk], writes=[Rkp])
        yield
        S.op("pool", ("tensor_tensor", dict(out=bv[:], in0=kkn[:], in1=asg[:], op=ALU.mult)), reads=[Rkkn, Rasg], writes=[Rbv])
        yield
        S.op("pool", ("tensor_tensor", dict(out=rk[:], in0=qr[:], in1=kp[:], op=ALU.mult)), reads=[Rqr, Rkp], writes=[Rrk])
        yield
        S.op("pe", ("matmul", dict(out=PS_P2, lhsT=bdr[:], rhs=rk[:], start=True, stop=True)), reads=[Rbdr, Rrk], writes=[RPS_P2])
        bsl = bsum[:, t0:t0 + BLK]
        if d == 0:
            S.op("dve", ("tensor_tensor", dict(out=bsl, in0=PS_P2, in1=qv[:], op=ALU.mult)), reads=[RPS_P2, Rqv], writes=[Rbs[blk]])
            yield
        else:
            S.op("dve", ("tensor_tensor", dict(out=tmp[:], in0=PS_P2, in1=qv[:], op=ALU.mult)), reads=[RPS_P2, Rqv], writes=[Rtmp])
            yield
            S.op("pool", ("tensor_tensor", dict(out=bsl, in0=bsl, in1=tmp[:], op=ALU.add)), reads=[Rtmp, Rbs[blk]], writes=[Rbs[blk]])
            yield
        S.op("dve", ("tensor_tensor", dict(out=rt_[:], in0=qr[:], in1=e1[:], op=ALU.mult)), reads=[Rqr, Re1], writes=[Rrt_])
        yield
        S.op("dve", ("scalar_tensor_tensor", dict(out=at_[:], in0=kkn[:], scalar=-1.0, in1=e1x[:], op0=ALU.mult, op1=ALU.mult)), reads=[Rkkn, Re1x], writes=[Rat_])
        yield
        S.op("pool", ("tensor_tensor", dict(out=kt_[:], in0=kp[:], in1=e2[:], op=ALU.mult)), reads=[Rkp, Re2], writes=[Rkt_])
        yield
        S.op("pool", ("tensor_tensor", dict(out=bt_[:], in0=bv[:], in1=e2[:], op=ALU.mult)), reads=[Rbv, Re2], writes=[Rbt_])
        yield
        S.op("pool", ("tensor_tensor", dict(out=r0_[:], in0=qr[:], in1=e3_[:], op=ALU.mult)), reads=[Rqr, Re3_], writes=[Rr0_])
        yield
        S.op("dve", ("scalar_tensor_tensor", dict(out=a0b_[:], in0=kkn[:], scalar=-1.0, in1=e3x[:], op0=ALU.mult, op1=ALU.mult)), reads=[Rkkn, Re3x], writes=[Ra0b_])
        yield
        S.op("pool", ("tensor_tensor", dict(out=kEb_[:], in0=kp[:], in1=e4[:], op=ALU.mult)), reads=[Rkp, Re4], writes=[RkEb_])
        yield
        S.op("pool", ("tensor_tensor", dict(out=bEb_[:], in0=bv[:], in1=e4[:], op=ALU.mult)), reads=[Rbv, Re4], writes=[RbEb_])
        yield
        S.op("act", ("activation", dict(out=qvb_[:], in_=qv[:], func=AF.Copy)), reads=[Rqv], writes=[Rqvb_])
        yield


    def block_stages(b, d, blk, pp):
        bwd = (d == 1)
        midc, totc = (C // 2 - 1, C - 1) if not bwd else (C // 2, 0)
        t0 = blk * BLK
        rt_, Rrt_ = rtS[pp], RrtS[pp]
        at_, Rat_ = atS[pp], RatS[pp]
        kt_, Rkt_ = ktS[pp], RktS[pp]
        bt_, Rbt_ = btS[pp], RbtS[pp]
        r0_, Rr0_ = r0S[pp], Rr0S[pp]
        a0b_, Ra0b_ = a0bS[pp], Ra0bS[pp]
        kEb_, RkEb_ = kEbS[pp], RkEbS[pp]
        bEb_, RbEb_ = bEbS[pp], RbEbS[pp]
        qvb_, Rqvb_ = qvbS[pp], RqvbS[pp]
        e3_, Re3_ = e3S[pp], Re3S[pp]

        def stage1(ck):
            ci, cs, gci, p = ck
            for i, (src, Rs) in enumerate(((qvb_, Rqvb_), (a0b_, Ra0b_), (bEb_, RbEb_), (kEb_, RkEb_))):
                S.op("pe", ("transpose", dict(out=PS_T[:, i, :], in_=src[:, cs], identity=identb[:])), reads=[Rs, Ridb], writes=[RPS_T])
            yield
            S.op("act", ("activation", dict(out=TT[p][:], in_=PS_T[:, 0:4, :], func=AF.Copy)), reads=[RPS_T], writes=[RTT[p]])
            yield
            for h in range(2):
                hs = HS[h]
                mm(PS_M[:, h, 0, :], bt_[hs, cs], at_[hs, cs], [Rbt_, Rat_], [RPS_M])
                mm(PS_M[:, h, 1, :], at_[hs, cs], bt_[hs, cs], [Rbt_, Rat_], [RPS_M])
                yield
                mm(PS_M[:, h, 2, :], at_[hs, cs], kt_[hs, cs], [Rkt_, Rat_], [RPS_M])
                mm(PS_M[:, h, 3, :], bt_[hs, cs], rt_[hs, cs], [Rbt_, Rrt_], [RPS_M])
                yield
                mm((PS_K if h == 0 else PS_G)[:, 0:128], kt_[hs, cs], rt_[hs, cs], [Rkt_, Rrt_], [RPS_K if h == 0 else RPS_G])
                yield
            for h in range(2):
                S.op("dve", ("tensor_tensor", dict(out=SBM[p][:, h], in0=PS_M[:, h], in1=m4f[:, d, :, :], op=ALU.mult)), reads=[RPS_M, Rm4], writes=[RSBM[p]])
                yield
            S.op("dve", ("tensor_tensor", dict(out=MKR[p][:, 0, :], in0=PS_K[:, 0:128], in1=mkf[:, d, :], op=ALU.mult)), reads=[RPS_K, Rmk], writes=[RMKR[p]])
            yield
            S.op("dve", ("tensor_tensor", dict(out=MKR[p][:, 1, :], in0=PS_G[:, 0:128], in1=mkf[:, d, :], op=ALU.mult)), reads=[RPS_G, Rmk], writes=[RMKR[p]])
            yield
            S.op("act", ("activation", dict(out=SX[p][:, :, 0:128], in_=SBM[p][:, :, 3, :], func=AF.Copy)), reads=[RSBM[p]], writes=[RSX[p]])
            S.op("pool", ("tensor_copy", dict(out=SX[p][:, :, 128:192], in_=TT[p][:, 2, :].rearrange("p (h j) -> p h j", h=2))), reads=[RTT[p]], writes=[RSX[p]])
            yield

        def stage2(ck):
            ci, cs, gci, p = ck
            for h in range(2):
                mm(PS_X[h][:, 0:192], identb[:], SX[p][:, h, :], [Ridb, RSX[p]], [RPS_X], st=True, sp=True)
            A_ = [SBM[p][:, h, 1, :] for h in range(2)]
            B_ = [SBM[p][:, h, 0, :] for h in range(2)]
            Rcur = RSBM[p]
            for lv in range(7):
                if lv < 6:
                    nb = lv % 2
                    for h in range(2):
                        mm(PS_AB[:, h, 0, :], B_[h], A_[h], [Rcur], [RPS_AB])
                        mm(PS_AB[:, h, 1, :], A_[h], B_[h], [Rcur], [RPS_AB])
                for h in range(2):
                    mm(PS_X[h][:, 0:192], A_[h], SX[p][:, h, :], [Rcur, RSX[p]], [RPS_X], st=False, sp=True, sg=True)
                if lv < 6:
                    S.op("act", ("activation", dict(out=SAB[nb][:].rearrange("p a b c -> p (a b c)"), in_=PS_AB[:].rearrange("p a b c -> p (a b c)"), func=AF.Copy)), reads=[RPS_AB], writes=[RSAB[nb]])
                S.op("dve", ("tensor_copy", dict(out=SX[p][:, 0, :], in_=PS_X[0][:, 0:192])), reads=[RPS_X], writes=[RSX[p]])
                S.op("dve", ("tensor_copy", dict(out=SX[p][:, 1, :], in_=PS_X[1][:, 0:192])), reads=[RPS_X], writes=[RSX[p]])
                if lv < 6:
                    A_ = [SAB[nb][:, h, 0, :] for h in range(2)]
                    B_ = [SAB[nb][:, h, 1, :] for h in range(2)]
                    Rcur = RSAB[nb]
                yield

        def stage3(ck):
            ci, cs, gci, p = ck
            for h in range(2):
                hs = HS[h]
                a0T = TT[p][:, 1, hs]
                mm(PS_G[hs, 0:128], a0T, SX[p][:, h, 0:128], [RTT[p], RSX[p]], [RPS_G])
                mm(PS_G[hs, 128:192], a0T, SX[p][:, h, 128:192], [RTT[p], RSX[p]], [RPS_G])
                yield
                mm(PS_G[:, 192 + 128 * h:320 + 128 * h], SBM[p][:, h, 2, :], SX[p][:, h, 0:128], [RSBM[p], RSX[p]], [RPS_G])
                mm(PS_K[:, 256 + 64 * h:320 + 64 * h], SBM[p][:, h, 2, :], SX[p][:, h, 128:192], [RSBM[p], RSX[p]], [RPS_K])
                yield
            S.op("dve", ("tensor_tensor", dict(out=Gb[:], in0=PS_G[:, 0:128], in1=r0_[:, cs], op=ALU.add)), reads=[RPS_G, Rr0_], writes=[RGb])
            yield
            S.op("dve", ("tensor_tensor", dict(out=Hb[:], in0=PS_G[:, 192:448].rearrange("p (h t) -> p h t", h=2), in1=MKR[p][:], op=ALU.add)), reads=[RPS_G, RMKR[p]], writes=[RHb])
            yield
            tcol = ci * C + totc
            S.op("dve", ("scalar_tensor_tensor", dict(out=Pb[:], in0=identP[:], scalar=e3_[:, tcol:tcol + 1], in1=PS_G[:, 128:192], op0=ALU.mult, op1=ALU.add)), reads=[RPS_G, Ridf, Re3_], writes=[RPb])
            yield
            S.op("dve", ("tensor_tensor", dict(out=Zb[:], in0=PS_K[:, 256:384].rearrange("p (h j) -> p h j", h=2), in1=TT[p][:, 3, :].rearrange("p (h j) -> p h j", h=2), op=ALU.add)), reads=[RPS_K, RTT[p]], writes=[RZb])
            yield
            for h in range(2):
                hs = HS[h]
                mm(PS_M[hs, 0, 0, :], STz[h][:], Gb[:], [RST[h], RGb], [RPS_M], st=True, sp=False)
                mm(PS_M[hs, 0, 0, :], TT[p][:, 0, hs], Hb[:, h, :], [RTT[p], RHb], [RPS_M], st=False, sp=True)
                yield
                mm(PS_M[hs, 0, 1, 0:64], Pb[:], STz[h][:], [RPb, RST[h]], [RPS_M], st=True, sp=False)
                mm(PS_M[hs, 0, 1, 0:64], Zb[:, h, :], TT[p][:, 0, hs], [RZb, RTT[p]], [RPS_M], st=False, sp=True)
                yield
            ysl = ysum[:, t0 + ci * C: t0 + (ci + 1) * C]
            if d == 0:
                S.op("act", ("activation", dict(out=ysl, in_=PS_M[:, 0, 0, :], func=AF.Copy)), reads=[RPS_M], writes=[Rys[gci]])
            else:
                S.op("act", ("activation", dict(out=ytmp[:, 0:128], in_=PS_M[:, 0, 0, :], func=AF.Copy)), reads=[RPS_M], writes=[Rytmp])
                S.op("dve", ("tensor_tensor", dict(out=ysl, in0=ytmp[:, 0:128], in1=ysl, op=ALU.add)), reads=[Rytmp, Rys[gci]], writes=[Rys[gci]])
            yield
            for h in range(2):
                hs = HS[h]
                S.op("act", ("activation", dict(out=STz[h][hs, :], in_=PS_M[hs, 0, 1, 0:64], func=AF.Copy)), reads=[RPS_M], writes=[RST[h]])
            yield

        return stage1, stage2, stage3

    def finalize_batch(b):
        for blk in range(nblk):
            t0 = blk * BLK
            ysl = ysum[:, t0:t0 + BLK]
            Rin = Rys[t0 // C: (t0 + BLK) // C]
            S.op("pe", ("matmul", dict(out=PS_P1, lhsT=bdm[:], rhs=ysl, start=True, stop=True)), reads=[Rbdm] + Rin, writes=[RPS_P1])
            S.op("dve", ("tensor_tensor", dict(out=fin1[:], in0=ysl, in1=PS_P1, op=ALU.subtract)), reads=[RPS_P1] + Rin, writes=[Rfin1])
            S.op("pool", ("tensor_tensor", dict(out=fin2[:], in0=fin1[:], in1=fin1[:], op=ALU.mult)), reads=[Rfin1], writes=[Rfin2])
            S.op("pe", ("matmul", dict(out=PS_P2, lhsT=bdm[:], rhs=fin2[:], start=True, stop=True)), reads=[Rbdm, Rfin2], writes=[RPS_P2])
            S.op("dve", ("tensor_scalar", dict(out=fin3[:], in0=PS_P2, scalar1=GN_EPS, scalar2=None, op0=ALU.add)), reads=[RPS_P2], writes=[Rfin3])
            S.op("act", ("activation", dict(out=fin3[:], in_=fin3[:], func=AF.Sqrt)), reads=[Rfin3], writes=[Rfin3])
            S.op("dve", ("reciprocal", dict(out=fin3[:], in_=fin3[:])), reads=[Rfin3], writes=[Rfin3])
            S.op("dve", ("tensor_tensor", dict(out=fin1[:], in0=fin1[:], in1=fin3[:], op=ALU.mult)), reads=[Rfin1, Rfin3], writes=[Rfin1])
            S.op("dve", ("tensor_scalar", dict(out=fin2[:], in0=fin1[:], scalar1=PM(15), scalar2=PM(16), op0=ALU.mult, op1=ALU.add)), reads=[Rfin1, Rprm], writes=[Rfin2])
            S.op("dve", ("tensor_tensor", dict(out=fin2[:], in0=fin2[:], in1=bsum[:, t0:t0 + BLK], op=ALU.add)), reads=[Rfin2, Rbs[blk]], writes=[Rfin2])
            out_toks.append(S.dma(("dma_start", dict(out=yout[:, b, t0:t0 + BLK], in_=fin2[:])), reads=[Rfin2]))

    sched_blocks = []
    for b in range(NB):
        for d in range(2):
            order = list(range(nblk)) if d == 0 else list(range(nblk - 1, -1, -1))
            for n_, blk in enumerate(order):
                sched_blocks.append((b, d, blk, n_ == 0, (n_ == len(order) - 1) and d == 1))
    pcount = 0
    for _ in prep_gen(sched_blocks[0][0], sched_blocks[0][1], sched_blocks[0][2], 0):
        pass
    for k, (b, d, blk, first_of_dir, last_of_batch) in enumerate(sched_blocks):
        pp = k % 2
        bwd = (d == 1)
        if first_of_dir:
            S.op("pool", ("memset", dict(ap=STz[0][:], constant=0.0)), writes=[RST[0]])
            S.op("pool", ("memset", dict(ap=STz[1][:], constant=0.0)), writes=[RST[1]])
        stage1, stage2, stage3 = block_stages(b, d, blk, pp)
        chunks = list(range(BLK // C)) if not bwd else list(range(BLK // C - 1, -1, -1))
        cks = [(ci, slice(ci * C, (ci + 1) * C), (blk * BLK // C) + ci, (pcount + n_) % 2) for n_, ci in enumerate(chunks)]
        pcount += len(chunks)
        if k + 1 < len(sched_blocks) and sched_blocks[k + 1][0] == b:
            nb_, nd_, nblk_ = sched_blocks[k + 1][:3]
            pgen = prep_gen(nb_, nd_, nblk_, (k + 1) % 2)
        else:
            pgen = iter(())
        for _ in stage1(cks[0]):
            pass
        for idx, ck in enumerate(cks):
            fill = itertools.chain(stage3(cks[idx - 1]) if idx > 0 else iter(()), stage1(cks[idx + 1]) if idx + 1 < len(cks) else iter(()))
            for _ in stage2(ck):
                for _k in range(NFILL):
                    next(fill, None)
                for _k in range(NPREP):
                    next(pgen, None)
            for _ in fill:
                pass
        for _ in stage3(cks[-1]):
            pass
        for _ in pgen:
            pass
        if last_of_batch:
            finalize_batch(b)
            if k + 1 < len(sched_blocks):
                nb_, nd_, nblk_ = sched_blocks[k + 1][:3]
                for _ in prep_gen(nb_, nd_, nblk_, (k + 1) % 2):
                    pass
    return out_toks


NT = 2048
NTH = NT + 2


def emit_p1(S, nc, I, O):
    sb = lambda name, shape, dt=F32: nc.alloc_sbuf_tensor(name, shape, dt)
    ps = lambda name, shape, dt=F32: nc.alloc_psum_tensor(name, shape, dt)
    R = Region
    op = S.op
    mm = lambda out, l, r_, rd, wr, st=True, sp=True: op("pe", ("matmul", dict(out=out, lhsT=l, rhs=r_, start=st, stop=sp)), reads=rd, writes=wr)

    identf = sb("identf", [128, 128]); identb = sb("identb", [128, 128], BF16); Rid = R()
    gE = sb("gE", [128, 8, 1]); gO = sb("gO", [128, 8, 1]); Rg = R()
    S.dma(("dma_start", dict(out=identf[:], in_=I["c_ident"])), writes=[Rid])
    op("dve", ("tensor_copy", dict(out=identb[:], in_=identf[:])), reads=[Rid], writes=[Rid])
    S.dma(("dma_start", dict(out=gE[:, :, 0], in_=I["e_norm_g"].rearrange("(k p) -> p k", p=128), allow_slow_non_contiguous=True)), writes=[Rg])
    S.dma(("dma_start", dict(out=gO[:, :, 0], in_=I["o_norm_g"].rearrange("(k p) -> p k", p=128), allow_slow_non_contiguous=True)), writes=[Rg])
    hnT = sb("hnT", [128, 8, NTH], BF16); RhnT = [R() for _ in range(18)]
    yT = nc.dram_tensor("yT_d", [16, 128, NT], BF16).ap(); RyT = [[R() for _ in range(4)] for _ in range(16)]
    U = sb("U", [128, 4096]); RU = R()
    xt = [sb("xt%d" % i, [128, 1024]) for i in range(2)]; Rxt = [R(), R()]
    yo = [sb("yo%d" % i, [128, 512], BF16) for i in range(2)]; Ryo = [R(), R()]
    ytl = [sb("ytl%d" % i, [128, 16, 128], BF16) for i in range(2)]; Rytl = [R(), R()]
    xn = sb("xn", [128, 1024], BF16); Rxn = R()
    sq = sb("sq", [128, 1024]); Rsq = R()
    st = sb("st", [128, 8]); Rst = R()
    stg = [sb("stg%d" % i, [128, 8, 256]) for i in range(2)]; Rstg = [R(), R()]
    wbf = [sb("wbf%d" % i, [128, 8, 512], BF16) for i in range(2)]; Rwbf = [R(), R()]
    wbig = sb("wbig", [128, 16, 1024], BF16); Rwbig = R()
    t1 = sb("t1", [128, 512]); Rt1 = R()
    t1b = sb("t1b", [128, 512]); t1s = [t1, t1b]; Rt1s = [Rt1, R()]
    t2b = sb("t2b", [128, 512]); t3b = sb("t3b", [128, 512])
    t2 = sb("t2", [128, 512]); Rt2 = R()
    t3 = sb("t3", [128, 512]); Rt3 = R()
    cw = sb("cw", [128, 8, 3]); Rcw = R()
    PS_a = ps("PS_a", [128, 512]); RPa = R()
    PS_b = ps("PS_b", [128, 512]); RPb = R()
    PS_c = ps("PS_c", [128, 512]); RPc = R()
    PS_d = ps("PS_d", [128, 512]); RPd = R()
    PS_t = ps("PS_t", [128, 8, 128], BF16); RPt = R()
    PS_m = ps("PS_m", [128, 8, 128]); RPm = R()
    for j_ in range(3):
        S.dma(("dma_start", dict(out=cw[:, :, j_], in_=I["e_conv_w"][j_].rearrange("(cb p) -> p cb", p=128), allow_slow_non_contiguous=True)), writes=[Rcw])

    def norm_tile(xtile, Rx, gt, dst_fn, Rdst, nvalid=128):
        op("act", ("activation", dict(out=sq[:], in_=xtile[:], func=AF.Square)), reads=[Rx], writes=[Rsq])
        op("dve", ("reduce_sum", dict(out=st[:, 0:1], in_=sq[:], axis=AX.X)), reads=[Rsq], writes=[Rst])
        op("dve", ("tensor_scalar", dict(out=st[:, 1:2], in0=st[:, 0:1], scalar1=1.0 / 1024, scalar2=1e-6, op0=ALU.mult, op1=ALU.add)), reads=[Rst], writes=[Rst])
        op("act", ("activation", dict(out=st[:, 2:3], in_=st[:, 1:2], func=AF.Sqrt)), reads=[Rst], writes=[Rst])
        op("dve", ("reciprocal", dict(out=st[:, 3:4], in_=st[:, 2:3])), reads=[Rst], writes=[Rst])
        op("dve", ("tensor_scalar", dict(out=xn[:], in0=xtile[:], scalar1=st[:, 3:4], scalar2=None, op0=ALU.mult)), reads=[Rx, Rst], writes=[Rxn])
        for k in range(8):
            op("pe", ("transpose", dict(out=PS_t[:, k, :], in_=xn[:, k * 128:(k + 1) * 128], identity=identb[:])), reads=[Rxn, Rid], writes=[RPt])
        dst_fn(gt)

    xh = I["xh"]
    for i in range(17):
        xb, Rx = xt[i % 2], Rxt[i % 2]
        if i < 16:
            S.dma(("dma_start", dict(out=xb[:], in_=xh[1 + 128 * i: 1 + 128 * (i + 1), :])), writes=[Rx])
            def dst(gt, i=i):
                op("dve", ("tensor_tensor", dict(out=hnT[:, :, 1 + 128 * i: 1 + 128 * (i + 1)], in0=PS_t[:], in1=gt[:].to_broadcast([128, 8, 128]), op=ALU.mult)), reads=[RPt, Rg], writes=[RhnT[i]])
        else:
            op("pool", ("memset", dict(ap=xb[:], constant=0.0)), writes=[Rx])
            S.dma(("dma_start", dict(out=xb[0:1, :], in_=xh[0:1, :])), writes=[Rx])
            S.dma(("dma_start", dict(out=xb[1:2, :], in_=xh[NT + 1:NT + 2, :])), writes=[Rx])
            def dst(gt):
                op("dve", ("tensor_tensor", dict(out=hnT[:, :, 0:1], in0=PS_t[:, :, 0:1], in1=gt[:], op=ALU.mult)), reads=[RPt, Rg], writes=[RhnT[16]])
                op("dve", ("tensor_tensor", dict(out=hnT[:, :, NT + 1:NT + 2], in0=PS_t[:, :, 1:2], in1=gt[:], op=ALU.mult)), reads=[RPt, Rg], writes=[RhnT[17]])
        norm_tile(xb, Rx, gE, dst, None)
    allh = RhnT

    wi = I["e_w_in"].rearrange("(k p) (s c) -> p k s c", p=128, c=1024)

    def load_w(buf, src4, nsp):
        for s_ in range(nsp):
            sb_ = s_ % 2
            S.dma(("dma_start", dict(out=stg[sb_][:, :, 0:128], in_=src4[:, :, s_, :])), writes=[Rstg[sb_]])
            op("act", ("activation", dict(out=wbf[buf][:, :, s_ * 128:(s_ + 1) * 128], in_=stg[sb_][:, :, 0:128], func=AF.Copy)), reads=[Rstg[sb_]], writes=[Rwbf[buf]])
        return wbf[buf][:, :, 0:nsp * 128].rearrange("p k (s c) -> p k s c", c=128)

    PSc0, RPc0, PSd0, RPd0 = PS_c, RPc, PS_d, RPd
    t2s = [t2, t2b]; Rt2s = [Rt2, R()]
    t3s = [t3, t3b]; Rt3s = [Rt3, R()]
    xc2 = sb("xc2", [128, NTH]); RU2 = R()
    chunksA = [(0, 512), (512, 512), (1024, 512), (1536, 512), (2048, 2)]
    RU_main = RU
    for cb in range(8):
        xc, RU = (U[:, 0:NTH], RU_main) if cb % 2 == 0 else (xc2[:, :], RU2)
        if cb % 2 == 0:
            for s_ in range(4):
                sb_ = s_ % 2
                S.dma(("dma_start", dict(out=stg[sb_][:], in_=wi[:, :, s_, cb * 128:(cb + 2) * 128])), writes=[Rstg[sb_]])
                op("act", ("activation", dict(out=wbf[0][:, :, s_ * 128:(s_ + 1) * 128], in_=stg[sb_][:, :, 0:128], func=AF.Copy)), reads=[Rstg[sb_]], writes=[Rwbf[0]])
                op("act", ("activation", dict(out=wbf[1][:, :, s_ * 128:(s_ + 1) * 128], in_=stg[sb_][:, :, 128:256], func=AF.Copy)), reads=[Rstg[sb_]], writes=[Rwbf[1]])
        w4 = wbf[cb % 2][:, :, 0:512].rearrange("p k (s c) -> p k s c", c=128)
        Rw = Rwbf[cb % 2]
        for ci_, (c0, n) in enumerate(chunksA):
            (PA, RA_), (PB, RB_) = (((PS_a, RPa), (PS_b, RPb)) if ci_ % 2 == 0 else ((PS_c, RPc), (PS_d, RPd)))
            for k in range(8):
                mm(PA[:, 0:n], w4[:, k, 0, :], hnT[:, k, c0:c0 + n], [Rw] + allh, [RA_], st=(k == 0), sp=(k == 7))
            for k in range(8):
                mm(PB[:, 0:n], w4[:, k, 2, :], hnT[:, k, c0:c0 + n], [Rw] + allh, [RB_], st=(k == 0), sp=(k == 7))
            t1_, Rt1_ = t1s[ci_ % 2], Rt1s[ci_ % 2]
            op("act", ("activation", dict(out=t1_[:, 0:n], in_=PA[:, 0:n], func=AF.Copy)), reads=[RA_], writes=[Rt1_])
            op("dve", ("tensor_tensor", dict(out=xc[:, c0:c0 + n], in0=PB[:, 0:n], in1=t1_[:, 0:n], op=ALU.mult)), reads=[RB_, Rt1_], writes=[RU])
        for j in range(4):
            c0 = 1 + 512 * j
            (PS_c, RPc), (PS_d, RPd) = ((PSc0, RPc0), (PSd0, RPd0)) if j % 2 == 1 else ((PS_a, RPa), (PS_b, RPb))
            t2, Rt2 = t2s[j % 2], Rt2s[j % 2]
            t3, Rt3 = t3s[j % 2], Rt3s[j % 2]
            for k in range(8):
                mm(PS_c[:], w4[:, k, 1, :], hnT[:, k, c0:c0 + 512], [Rw] + allh, [RPc], st=(k == 0), sp=(k == 7))
            for k in range(8):
                mm(PS_d[:], w4[:, k, 3, :], hnT[:, k, c0:c0 + 512], [Rw] + allh, [RPd], st=(k == 0), sp=(k == 7))
            op("dve", ("tensor_scalar", dict(out=t2[:], in0=xc[:, c0 - 1:c0 + 511], scalar1=cw[:, cb, 0:1], scalar2=None, op0=ALU.mult)), reads=[RU, Rcw], writes=[Rt2])
            op("dve", ("scalar_tensor_tensor", dict(out=t2[:], in0=xc[:, c0:c0 + 512], scalar=cw[:, cb, 1:2], in1=t2[:], op0=ALU.mult, op1=ALU.add)), reads=[RU, Rcw, Rt2], writes=[Rt2])
            op("dve", ("scalar_tensor_tensor", dict(out=t2[:], in0=xc[:, c0 + 1:c0 + 513], scalar=cw[:, cb, 2:3], in1=t2[:], op0=ALU.mult, op1=ALU.add)), reads=[RU, Rcw, Rt2], writes=[Rt2])
            op("act", ("activation", dict(out=t3[:], in_=PS_d[:], func=AF.Silu)), reads=[RPd], writes=[Rt3])
            op("dve", ("tensor_tensor", dict(out=t2[:], in0=PS_c[:], in1=t2[:], op=ALU.mult)), reads=[RPc, Rt2], writes=[Rt2])
            op("pool", ("tensor_tensor", dict(out=yo[j % 2][:], in0=t2[:], in1=t3[:], op=ALU.mult)), reads=[Rt2, Rt3], writes=[Ryo[j % 2]])
            S.dma(("dma_start", dict(out=yT[cb, :, 512 * j:512 * (j + 1)], in_=yo[j % 2][:])), reads=[Ryo[j % 2]], writes=[RyT[cb][j]], q="pool")

    PS_c, RPc, PS_d, RPd = PSc0, RPc0, PSd0, RPd0
    t2, Rt2, t3, Rt3 = t2s[0], Rt2s[0], t3s[0], Rt3s[0]
    RU = RU_main
    for hf in range(4):
        S.dma(("dma_start", dict(out=stg[hf % 2][:], in_=wi[:, :, 5, hf * 256:(hf + 1) * 256])), writes=[Rstg[hf % 2]])
        op("act", ("activation", dict(out=wbig[:, 0:8, hf * 256:(hf + 1) * 256], in_=stg[hf % 2][:], func=AF.Copy)), reads=[Rstg[hf % 2]], writes=[Rwbig])
    for hf in range(4):
        S.dma(("dma_start", dict(out=stg[hf % 2][:], in_=wi[:, :, 4, hf * 256:(hf + 1) * 256])), writes=[Rstg[hf % 2]])
        op("act", ("activation", dict(out=wbig[:, 8:16, hf * 256:(hf + 1) * 256], in_=stg[hf % 2][:], func=AF.Copy)), reads=[Rstg[hf % 2]], writes=[Rwbig])
    for hf in range(4):
        S.dma(("dma_start", dict(out=stg[hf % 2][:], in_=wi[:, :, 6, hf * 256:(hf + 1) * 256])), writes=[Rstg[hf % 2]])
        op("act", ("activation", dict(out=wbf[hf // 2][:, :, (hf % 2) * 256:(hf % 2 + 1) * 256], in_=stg[hf % 2][:], func=AF.Copy)), reads=[Rstg[hf % 2]], writes=[Rwbf[hf // 2]])
    wsn = sb("wsn", [128, 8, 128]); wsnb = sb("wsnb", [128, 8, 128], BF16); wsT = sb("wsT", [128, 8, 128], BF16); Rws = R()
    S.dma(("dma_start", dict(out=wsn[:], in_=I["e_sgu_w"].rearrange("g i j -> i g j"))), writes=[Rws])
    op("dve", ("tensor_copy", dict(out=wsnb[:], in_=wsn[:])), reads=[Rws], writes=[Rws])
    for g in range(8):
        op("pe", ("transpose", dict(out=PS_t[:, g, :], in_=wsnb[:, g, :], identity=identb[:])), reads=[Rws, Rid], writes=[RPt])
    op("act", ("activation", dict(out=wsT[:], in_=PS_t[:], func=AF.Copy)), reads=[RPt], writes=[Rws])
    bsB = sb("bsB", [128, 8, 128]); lnG = sb("lnG", [128, 1024]); lnB = sb("lnB", [128, 1024]); Rbc = R()
    S.dma(("dma_start", dict(out=bsB[:].rearrange("p g i -> p (g i)"), in_=I["e_sgu_b"].rearrange("g i -> (g i)").partition_broadcast(128))), writes=[Rbc])
    S.dma(("dma_start", dict(out=lnG[:], in_=I["e_sgu_ln_g"].partition_broadcast(128))), writes=[Rbc])
    S.dma(("dma_start", dict(out=lnB[:], in_=I["e_sgu_ln_b"].partition_broadcast(128))), writes=[Rbc])
    vsb = sb("vsb", [128, 1024]); Rvsb = R()
    vnb = sb("vnb", [128, 1024], BF16); Rvnb = R()
    mixall = U[:, 0:4096].rearrange("p (g t) -> p g t", g=8)
    for tg in range(4):
        for ti in range(4):
            c0 = 1 + 128 * (4 * tg + ti)
            for hf, (P_, RP_) in enumerate((((PS_a, RPa), (PS_b, RPb)) if ti % 2 == 0 else ((PS_c, RPc), (PS_d, RPd)))):
                for k in range(8):
                    mm(P_[:], hnT[:, k, c0:c0 + 128], wbig[:, k, hf * 512:(hf + 1) * 512], [Rwbig] + allh, [RP_], st=(k == 0), sp=(k == 7))
                op("act", ("activation", dict(out=vsb[:, hf * 512:(hf + 1) * 512], in_=P_[:], func=AF.Copy)), reads=[RP_], writes=[Rvsb])
            op("act", ("activation", dict(out=sq[:], in_=vsb[:], func=AF.Square)), reads=[Rvsb], writes=[Rsq])
            op("dve", ("reduce_sum", dict(out=st[:, 0:1], in_=vsb[:], axis=AX.X)), reads=[Rvsb], writes=[Rst])
            op("dve", ("reduce_sum", dict(out=st[:, 1:2], in_=sq[:], axis=AX.X)), reads=[Rsq], writes=[Rst])
            op("dve", ("tensor_scalar", dict(out=st[:, 2:3], in0=st[:, 0:1], scalar1=1.0 / 1024, scalar2=None, op0=ALU.mult)), reads=[Rst], writes=[Rst])
            op("dve", ("tensor_tensor", dict(out=st[:, 3:4], in0=st[:, 2:3], in1=st[:, 2:3], op=ALU.mult)), reads=[Rst], writes=[Rst])
            op("dve", ("scalar_tensor_tensor", dict(out=st[:, 4:5], in0=st[:, 1:2], scalar=1.0 / 1024, in1=st[:, 3:4], op0=ALU.mult, op1=ALU.subtract)), reads=[Rst], writes=[Rst])
            op("dve", ("tensor_scalar", dict(out=st[:, 4:5], in0=st[:, 4:5], scalar1=1e-5, scalar2=None, op0=ALU.add)), reads=[Rst], writes=[Rst])
            op("act", ("activation", dict(out=st[:, 5:6], in_=st[:, 4:5], func=AF.Sqrt)), reads=[Rst], writes=[Rst])
            op("dve", ("reciprocal", dict(out=st[:, 6:7], in_=st[:, 5:6])), reads=[Rst], writes=[Rst])
            op("dve", ("tensor_scalar", dict(out=vsb[:], in0=vsb[:], scalar1=st[:, 2:3], scalar2=st[:, 6:7], op0=ALU.subtract, op1=ALU.mult)), reads=[Rvsb, Rst], writes=[Rvsb])
            op("dve", ("tensor_tensor", dict(out=vsb[:], in0=vsb[:], in1=lnG[:], op=ALU.mult)), reads=[Rvsb, Rbc], writes=[Rvsb])
            op("pool", ("tensor_tensor", dict(out=vnb[:], in0=vsb[:], in1=lnB[:], op=ALU.add)), reads=[Rvsb, Rbc], writes=[Rvnb])
            for g in range(8):
                mm(PS_m[:, g, :], vnb[:, g * 128:(g + 1) * 128], wsT[:, g, :], [Rvnb, Rws], [RPm])
            op("dve", ("tensor_tensor", dict(out=mixall[:, :, ti * 128:(ti + 1) * 128], in0=PS_m[:], in1=bsB[:], op=ALU.add)), reads=[RPm, Rbc], writes=[RU])
        c0 = 1 + 512 * tg
        for g in range(8):
            buf = g % 2

            (PU, RPU), (PZ, RPZ) = ((PS_c, RPc), (PS_d, RPd)) if g % 2 == 0 else ((PS_a, RPa), (PS_b, RPb))
            t2, Rt2 = t2s[g % 2], Rt2s[g % 2]
            t3, Rt3 = t3s[g % 2], Rt3s[g % 2]
            for k in range(8):
                mm(PU[:], wbig[:, 8 + k, g * 128:(g + 1) * 128], hnT[:, k, c0:c0 + 512], [Rwbig] + allh, [RPU], st=(k == 0), sp=(k == 7))
            for k in range(8):
                mm(PZ[:], wbf[g // 4][:, k, (g % 4) * 128:(g % 4 + 1) * 128], hnT[:, k, c0:c0 + 512], [Rwbf[g // 4]] + allh, [RPZ], st=(k == 0), sp=(k == 7))
            op("act", ("activation", dict(out=t3[:], in_=PZ[:], func=AF.Silu)), reads=[RPZ], writes=[Rt3])
            op("dve", ("tensor_tensor", dict(out=t2[:], in0=PU[:], in1=mixall[:, g, :], op=ALU.mult)), reads=[RPU, RU], writes=[Rt2])
            op("pool", ("tensor_tensor", dict(out=yo[g % 2][:], in0=t2[:], in1=t3[:], op=ALU.mult)), reads=[Rt2, Rt3], writes=[Ryo[g % 2]])
            S.dma(("dma_start", dict(out=yT[8 + g, :, 512 * tg:512 * (tg + 1)], in_=yo[g % 2][:])), reads=[Ryo[g % 2]], writes=[RyT[8 + g][tg]], q="pool")

    wo = I["e_w_out"].rearrange("(k p) n -> p k n", p=128)
    for q in range(2):
        for hf in range(4):
            S.dma(("dma_start", dict(out=stg[hf % 2][:], in_=wo[:, 8 * q:8 * q + 8, hf * 256:(hf + 1) * 256])), writes=[Rstg[hf % 2]])
            op("act", ("activation", dict(out=wbig[:, 8 * q:8 * q + 8, hf * 256:(hf + 1) * 256], in_=stg[hf % 2][:], func=AF.Copy)), reads=[Rstg[hf % 2]], writes=[Rwbig])
    ally = [r for row in RyT for r in row]
    h1ts = [sb("h1t%d" % i, [128, 1024]) for i in range(2)]; Rh1s = [R(), R()]
    for i in range(16):
        xb, Rx = xt[i % 2], Rxt[i % 2]
        h1t, Rh1 = h1ts[i % 2], Rh1s[i % 2]
        S.dma(("dma_start", dict(out=xb[:], in_=xh[1 + 128 * i: 1 + 128 * (i + 1), :])), writes=[Rx])
        S.dma(("dma_start", dict(out=ytl[i % 2][:], in_=yT[:, :, 128 * i:128 * (i + 1)].rearrange("k p t -> p k t"))), reads=ally, writes=[Rytl[i % 2]])
        for hf, (P_, RP_) in enumerate((((PS_a, RPa), (PS_b, RPb)) if i % 2 == 0 else ((PS_c, RPc), (PS_d, RPd)))):
            for k in range(16):
                mm(P_[:], ytl[i % 2][:, k, :], wbig[:, k, hf * 512:(hf + 1) * 512], [Rwbig, Rytl[i % 2]], [RP_], st=(k == 0), sp=(k == 15))
            op("dve", ("tensor_tensor", dict(out=h1t[:, hf * 512:(hf + 1) * 512], in0=P_[:], in1=xb[:, hf * 512:(hf + 1) * 512], op=ALU.add)), reads=[RP_, Rx], writes=[Rh1])
        S.dma(("dma_start", dict(out=O["h1"][128 * i:128 * (i + 1), :], in_=h1t[:])), reads=[Rh1], q="act")

        def dst(gt, i=i):
            op("dve", ("tensor_tensor", dict(out=hnT[:, :, 1 + 128 * i: 1 + 128 * (i + 1)], in0=PS_t[:], in1=gt[:].to_broadcast([128, 8, 128]), op=ALU.mult)), reads=[RPt, Rg], writes=[RhnT[i]])
        norm_tile(h1t, Rh1, gO, dst, None)

    wi1 = I["o_w_in"].rearrange("(k p) n -> p k n", p=128)
    blocks = [(c * 128, c * 128, False) for c in range(25)]
    blocks += [(3200 + c * 128, 3200 + c * 128, True) for c in range(8)]
    blocks += [(4736 + c * 128, 4224 + c * 128, True) for c in range(4)]
    ob = [sb("ob%d" % i, [128, 512]) for i in range(2)]; Rob = [R(), R()]
    oi = 0
    toks = []
    bi = 0
    nblocks = len(blocks)
    pairbuf = 0
    while bi < nblocks:
        sc, dr, act = blocks[bi]
        paired = (bi + 1 < nblocks) and (blocks[bi + 1][0] == sc + 128) and (blocks[bi + 1][2] == act)
        ncol = 256 if paired else 128
        buf = pairbuf % 2
        pairbuf += 1
        S.dma(("dma_start", dict(out=stg[buf][:, :, 0:ncol], in_=wi1[:, :, sc:sc + ncol])), writes=[Rstg[buf]])
        op("dve", ("tensor_copy", dict(out=wbf[buf][:, :, 0:ncol], in_=stg[buf][:, :, 0:ncol])), reads=[Rstg[buf]], writes=[Rwbf[buf]])
        for sub in range(2 if paired else 1):
            sc_, dr_, act_ = blocks[bi + sub]
            for j in range(4):
                P_, RP_ = ((PS_a, RPa), (PS_b, RPb), (PS_c, RPc), (PS_d, RPd))[j]
                c0 = 1 + 512 * j
                for k in range(8):
                    mm(P_[:], wbf[buf][:, k, sub * 128:(sub + 1) * 128], hnT[:, k, c0:c0 + 512], [Rwbf[buf]] + allh, [RP_], st=(k == 0), sp=(k == 7))
                o_, Ro = ob[oi % 2], Rob[oi % 2]
                oi += 1
                op("act", ("activation", dict(out=o_[:], in_=P_[:], func=(AF.Silu if act_ else AF.Copy))), reads=[RP_], writes=[Ro])
                toks.append(S.dma(("dma_start", dict(out=O["pT"][dr_:dr_ + 128, 512 * j:512 * (j + 1)], in_=o_[:])), reads=[Ro], q="act"))
        bi += 2 if paired else 1
    for hf in range(2):
        S.dma(("dma_start", dict(out=stg[hf][:], in_=wi1[:, :, 4224 + 256 * hf:4224 + 256 * (hf + 1)])), writes=[Rstg[hf]])
        op("act", ("activation", dict(out=wbf[0][:, :, 256 * hf:256 * (hf + 1)], in_=stg[hf][:], func=AF.Copy)), reads=[Rstg[hf]], writes=[Rwbf[0]])
    for i in range(16):
        c0 = 1 + 128 * i
        P_, RP_ = ((PS_a, RPa), (PS_b, RPb))[i % 2]
        for k in range(8):
            mm(P_[:], hnT[:, k, c0:c0 + 128], wbf[0][:, k, :], [Rwbf[0]] + allh, [RP_], st=(k == 0), sp=(k == 7))
        o_, Ro = ob[oi % 2], Rob[oi % 2]
        oi += 1
        op("act", ("activation", dict(out=o_[:], in_=P_[:], func=AF.Copy)), reads=[RP_], writes=[Ro])
        toks.append(S.dma(("dma_start", dict(out=O["fd"][128 * i:128 * (i + 1), :], in_=o_[:])), reads=[Ro], q="act"))
    return toks


def full_barrier(S):
    keys = list(S.cnt.items())
    for e in S.ENGS:
        waits = []
        for k, v in keys:
            if k == e:
                continue
            if S.seen[e].get(k, 0) < v:
                S.seen[e][k] = v
                waits.append((k, v))
        if waits:
            S.prog[e].append([waits, None, ("_none", 0)])


def emit_fnet(S, nc, I, ydT):
    R = Region
    op = S.op
    mm = lambda out, l, r_, rd, wr, st=True, sp=True: op("pe", ("matmul", dict(out=out, lhsT=l, rhs=r_, start=st, stop=sp)), reads=rd, writes=wr)
    toks = []
    with ExitStack() as es:
        sb = lambda name, shape, dt=F32: es.enter_context(nc.sbuf_tensor(name, shape, dt))
        ps = lambda name, shape, dt=F32: es.enter_context(nc.psum_tensor(name, shape, dt))
        xs = sb("f_xs", [128, 4096]); Rxs = R()
        xb = sb("f_xb", [128, 64, 128], BF16); Rxb = R()
        Fb = sb("f_F", [128, 256], BF16); RF = R()
        A_sb = sb("f_A", [64, 128, 256], BF16); RA = R()
        PQ = sb("f_PQ", [128, 2, 64, 128], BF16); RPQ = R()
        Tg = [[sb("f_T%d%d" % (i, j), [64, 16, 128], BF16) for j in range(2)] for i in range(2)]; RTg = [R(), R()]
        wf32 = sb("f_w32", [128, 128]); wfb = sb("f_wb", [128, 128], BF16); Rwf = R()
        Ccb = sb("f_Cc", [128, 128], BF16); mScb = sb("f_mSc", [128, 128], BF16); Rcs = R()
        Gb = sb("f_G", [128, 256], BF16); RG = R()
        ob = [sb("f_ob%d" % i, [128, 512]) for i in range(2)]; Rob = [R(), R()]
        PS = [ps("f_ps%d" % i, [128, 512]) for i in range(2)]; RPS = [R(), R()]
        S.dma(("dma_start", dict(out=Fb[:], in_=I["c_F"])), writes=[RF])
        S.dma(("dma_start", dict(out=Ccb[:], in_=I["c_Cc"])), writes=[Rcs])
        S.dma(("dma_start", dict(out=mScb[:], in_=I["c_mSc"])), writes=[Rcs])
        S.dma(("dma_start", dict(out=wf32[:], in_=I["fw"])), writes=[Rwf])
        op("dve", ("tensor_copy", dict(out=wfb[:], in_=wf32[:])), reads=[Rwf], writes=[Rwf])
        xbf = xb[:].rearrange("p l c -> p (l c)")
        for hf in range(2):
            S.dma(("dma_start", dict(out=xs[:], in_=I["fx"][:, hf * 4096:(hf + 1) * 4096])), writes=[Rxs])
            op("pool", ("tensor_copy", dict(out=xbf[:, hf * 4096:(hf + 1) * 4096], in_=xs[:])), reads=[Rxs], writes=[Rxb])
        for c2 in range(64):
            P_, RP_ = PS[c2 % 2], RPS[c2 % 2]
            for j in range(2):
                mm(P_[0:64, j * 256:(j + 1) * 256], xb[:, :, 2 * c2 + j], Fb[:], [Rxb, RF], [RP_])
            op("act" if c2 % 2 == 0 else "dve", ("activation", dict(out=A_sb[0:64, 2 * c2:2 * c2 + 2, :], in_=P_[0:64, :].rearrange("p (j k) -> p j k", j=2), func=AF.Copy)) if c2 % 2 == 0 else
               ("tensor_copy", dict(out=A_sb[0:64, 2 * c2:2 * c2 + 2, :], in_=P_[0:64, :].rearrange("p (j k) -> p j k", j=2))), reads=[RP_], writes=[RA])
        T1d = I["c_T1"].rearrange("p (k h) -> p k h", h=128)
        T2d = I["c_T2"].rearrange("p (k h) -> p k h", h=128)
        ei = 0
        for grp in range(8):
            tb = grp % 2
            S.dma(("dma_start", dict(out=Tg[tb][0][:], in_=T1d[:, grp * 16:(grp + 1) * 16, :])), writes=[RTg[tb]])
            S.dma(("dma_start", dict(out=Tg[tb][1][:], in_=T2d[:, grp * 16:(grp + 1) * 16, :])), writes=[RTg[tb]])
            for q in range(4):
                P_, RP_ = PS[ei % 2], RPS[ei % 2]
                for j in range(4):
                    kk_ = q * 4 + j
                    kl = grp * 16 + kk_
                    mm(P_[:, j * 128:(j + 1) * 128], A_sb[0:64, :, kl], Tg[tb][0][0:64, kk_, :], [RA, RTg[tb]], [RP_], st=True, sp=False)
                    mm(P_[:, j * 128:(j + 1) * 128], A_sb[0:64, :, 128 + kl], Tg[tb][1][0:64, kk_, :], [RA, RTg[tb]], [RP_], st=False, sp=True)
                kl0 = grp * 16 + q * 4
                for qq in range(2):
                    op("act" if qq == 0 else "dve",
                       ("activation", dict(out=PQ[:, qq, :, kl0:kl0 + 4].rearrange("p h l -> p l h"), in_=P_[:].rearrange("p (l q h) -> p l q h", l=4, q=2)[:, :, qq, :], func=AF.Copy)) if qq == 0 else
                       ("tensor_copy", dict(out=PQ[:, qq, :, kl0:kl0 + 4].rearrange("p h l -> p l h"), in_=P_[:].rearrange("p (l q h) -> p l q h", l=4, q=2)[:, :, qq, :])),
                       reads=[RP_], writes=[RPQ])
                ei += 1
        P_, RP_ = PS[0], RPS[0]
        mm(P_[:, 0:128], Ccb[:], wfb[:], [Rcs, Rwf], [RP_])
        mm(P_[:, 128:256], mScb[:], wfb[:], [Rcs, Rwf], [RP_])
        op("act", ("activation", dict(out=Gb[:], in_=P_[:, 0:256], func=AF.Copy)), reads=[RP_], writes=[RG])
        for t4 in range(16):
            P_, RP_ = PS[(t4 + 1) % 2], RPS[(t4 + 1) % 2]
            for j in range(4):
                kh = 4 * t4 + j
                mm(P_[:, j * 128:(j + 1) * 128], Gb[:, 0:128], PQ[:, 0, kh, :], [RG, RPQ], [RP_], st=True, sp=False)
                mm(P_[:, j * 128:(j + 1) * 128], Gb[:, 128:256], PQ[:, 1, kh, :], [RG, RPQ], [RP_], st=False, sp=True)
            o_, Ro = ob[t4 % 2], Rob[t4 % 2]
            op("act", ("activation", dict(out=o_[:], in_=P_[:], func=AF.Copy)), reads=[RP_], writes=[Ro])
            toks.append(S.dma(("dma_start", dict(out=ydT[:, 512 * t4:512 * (t4 + 1)], in_=o_[:])), reads=[Ro]))
    full_barrier(S)
    return toks


def emit_p3(S, nc, I, yout):
    sb = lambda name, shape, dt=F32: nc.alloc_sbuf_tensor(name, shape, dt)
    ps = lambda name, shape, dt=F32: nc.alloc_psum_tensor(name, shape, dt)
    R = Region
    op = S.op
    mm = lambda out, l, r_, rd, wr, st=True, sp=True: op("pe", ("matmul", dict(out=out, lhsT=l, rhs=r_, start=st, stop=sp)), reads=rd, writes=wr)
    stg = [sb("stg%d" % i, [128, 8, 256]) for i in range(2)]; Rstg = [R(), R()]
    wO = sb("wO", [128, 12, 1024], BF16); RwO = R()
    gN = sb("gN", [128, 1024]); RgN = R()
    gt_all = sb("gt_all", [128, 12, 2048], BF16); Rgt = R()
    ya = [sb("ya%d" % i, [128, 512]) for i in range(2)]; Rya = [R(), R()]
    ga = [sb("ga%d" % i, [128, 512]) for i in range(2)]; Rga = [R(), R()]
    h1t = [sb("h1t%d" % i, [128, 1024]) for i in range(2)]; Rh1 = [R(), R()]
    h2 = sb("h2", [128, 1024]); Rh2 = R()
    sq = sb("sq", [128, 1024]); Rsq = R()
    st = sb("st", [128, 8]); Rst = R()
    yo = [sb("yo%d" % i, [128, 1024]) for i in range(2)]; Ryo = [R(), R()]
    PS_a = ps("PS_a", [128, 512]); RPa = R()
    PS_b = ps("PS_b", [128, 512]); RPb = R()
    wo3 = I["o_w_out"].rearrange("(k p) n -> p k n", p=128)
    si = 0
    for (k0, nk) in ((0, 8), (8, 4)):
        for cq in range(4):
            b_ = si % 2; si += 1
            S.dma(("dma_start", dict(out=stg[b_][:, 0:nk, :], in_=wo3[:, k0:k0 + nk, cq * 256:(cq + 1) * 256])), writes=[Rstg[b_]])
            op("act", ("activation", dict(out=wO[:, k0:k0 + nk, cq * 256:(cq + 1) * 256], in_=stg[b_][:, 0:nk, :], func=AF.Copy)), reads=[Rstg[b_]], writes=[RwO])
    S.dma(("dma_start", dict(out=gN[:], in_=I["final_norm_g"].partition_broadcast(128))), writes=[RgN])
    ii = 0
    for blk in range(12):
        src = I["ycT"][blk * 128:(blk + 1) * 128] if blk < 8 else I["ydT"][(blk - 8) * 128:(blk - 7) * 128]
        gsrc = I["gT"][blk * 128:(blk + 1) * 128]
        for j in range(4):
            b_ = ii % 2; ii += 1
            S.dma(("dma_start", dict(out=ya[b_][:], in_=src[:, 512 * j:512 * (j + 1)])), writes=[Rya[b_]])
            S.dma(("dma_start", dict(out=ga[b_][:], in_=gsrc[:, 512 * j:512 * (j + 1)])), writes=[Rga[b_]])
            op("dve", ("tensor_tensor", dict(out=gt_all[:, blk, 512 * j:512 * (j + 1)], in0=ya[b_][:], in1=ga[b_][:], op=ALU.mult)), reads=[Rya[b_], Rga[b_]], writes=[Rgt])
    toks = []
    for i in range(16):
        hb, Rh = h1t[i % 2], Rh1[i % 2]
        S.dma(("dma_start", dict(out=hb[:], in_=I["h1"][128 * i:128 * (i + 1), :])), writes=[Rh])
        for hf, (P_, RP_) in enumerate(((PS_a, RPa), (PS_b, RPb))):
            for k in range(12):
                mm(P_[:], gt_all[:, k, 128 * i:128 * (i + 1)], wO[:, k, hf * 512:(hf + 1) * 512], [Rgt, RwO], [RP_], st=(k == 0), sp=(k == 11))
            op("dve", ("tensor_tensor", dict(out=h2[:, hf * 512:(hf + 1) * 512], in0=P_[:], in1=hb[:, hf * 512:(hf + 1) * 512], op=ALU.add)), reads=[RP_, Rh], writes=[Rh2])
        op("act", ("activation", dict(out=sq[:], in_=h2[:], func=AF.Square)), reads=[Rh2], writes=[Rsq])
        op("dve", ("reduce_sum", dict(out=st[:, 0:1], in_=sq[:], axis=AX.X)), reads=[Rsq], writes=[Rst])
        op("dve", ("tensor_scalar", dict(out=st[:, 1:2], in0=st[:, 0:1], scalar1=1.0 / 1024, scalar2=1e-6, op0=ALU.mult, op1=ALU.add)), reads=[Rst], writes=[Rst])
        op("act", ("activation", dict(out=st[:, 2:3], in_=st[:, 1:2], func=AF.Sqrt)), reads=[Rst], writes=[Rst])
        op("dve", ("reciprocal", dict(out=st[:, 3:4], in_=st[:, 2:3])), reads=[Rst], writes=[Rst])
        op("dve", ("tensor_scalar", dict(out=h2[:], in0=h2[:], scalar1=st[:, 3:4], scalar2=None, op0=ALU.mult)), reads=[Rh2, Rst], writes=[Rh2])
        o_, Ro = yo[i % 2], Ryo[i % 2]
        op("dve", ("tensor_tensor", dict(out=o_[:], in0=h2[:], in1=gN[:], op=ALU.mult)), reads=[Rh2, RgN], writes=[Ro])
        toks.append(S.dma(("dma_start", dict(out=yout[128 * i:128 * (i + 1), :], in_=o_[:])), reads=[Ro], q="pool"))
    return toks


def _mk(nc, name, shape, dt=None, out=False):
    return nc.dram_tensor(name, list(shape), dt or F32, kind=("ExternalOutput" if out else "ExternalInput")).ap()


W1 = ["e_norm_g", "e_w_in", "e_conv_w", "e_sgu_ln_g", "e_sgu_ln_b", "e_sgu_w", "e_sgu_b", "e_w_out", "o_norm_g", "o_w_in"]


def build_l1(shapes):
    nc = bass.Bass("TRN2", target_bir_lowering=False)
    I = {"xh": _mk(nc, "xh", [2050, 1024]), "c_ident": _mk(nc, "c_ident", [128, 128])}
    for n in W1:
        I[n] = _mk(nc, n, shapes[n])
    O = {"h1": _mk(nc, "h1", [2048, 1024], out=True), "pT": _mk(nc, "pT", [4736, 2048], out=True),
         "fd": _mk(nc, "fd", [2048, 512], out=True)}
    S = Sched(nc)
    toks = emit_p1(S, nc, I, O)
    S.barrier_on("sp", toks)
    S.finalize()
    return nc


def build_l2(consts):
    NB, T = 2, 8192
    nc = bass.Bass("TRN2", target_bir_lowering=False)
    pr, pk, pv, pwa = (_mk(nc, n, [128, NB, T + 2]) for n in ("pr", "pk", "pv", "pwa"))
    prm = _mk(nc, "prm", [128, 17]); w2a2 = _mk(nc, "w2a2", [128, 2, 128])
    A = {k: _mk(nc, k, v.shape) for k, v in consts.items()}
    FI = {"fx": _mk(nc, "fx", [128, 8192]), "fw": _mk(nc, "fw", [128, 128]),
          "c_F": _mk(nc, "c_F", [128, 256], BF16), "c_T1": _mk(nc, "c_T1", [64, 16384], BF16),
          "c_T2": _mk(nc, "c_T2", [64, 16384], BF16), "c_Cc": _mk(nc, "c_Cc", [128, 128], BF16),
          "c_mSc": _mk(nc, "c_mSc", [128, 128], BF16)}
    yout = _mk(nc, "yout", [128, NB, T], out=True)
    ydT = _mk(nc, "ydT", [128, T], out=True)
    S = Sched(nc)
    toks = emit_fnet(S, nc, FI, ydT)
    toks += emit_rwkv(S, nc, A, pr, pk, pv, pwa, prm, w2a2, yout, NB, T)
    S.barrier_on("sp", toks)
    S.finalize()
    return nc


def build_l3():
    nc = bass.Bass("TRN2", target_bir_lowering=False)
    I = {"ycT": _mk(nc, "ycT", [1024, 2048]), "ydT": _mk(nc, "ydT", [512, 2048]), "gT": _mk(nc, "gT", [1536, 2048]),
         "h1": _mk(nc, "h1", [2048, 1024]), "o_w_out": _mk(nc, "o_w_out", [1536, 1024]),
         "final_norm_g": _mk(nc, "final_norm_g", [1024])}
    y = _mk(nc, "y", [2048, 1024], out=True)
    S = Sched(nc)
    toks = emit_p3(S, nc, I, y)
    S.barrier_on("sp", toks)
    S.finalize()
    return nc


def fnet_tables():
    import ml_dtypes
    N = 8192
    nh = np.arange(128); kl = np.arange(128)
    ang = 2 * np.pi * np.outer(nh, kl) / 128
    F = np.concatenate([np.cos(ang), np.sin(ang)], axis=1)
    nl = np.arange(64)[:, None, None]; klo = np.arange(128)[None, :, None]; kh = np.arange(64)[None, None, :]
    beta = 2 * np.pi * ((nl * (klo + 128 * kh)) % N) / N
    T1 = np.concatenate([np.cos(beta), np.sin(beta)], axis=2).reshape(64, 16384)
    T2 = np.concatenate([-np.sin(beta), np.cos(beta)], axis=2).reshape(64, 16384)
    c = np.arange(128); phi = 2 * np.pi * np.outer(c, c) / 128
    nrm = 1 / np.sqrt(N * 128)
    bf = lambda a: np.ascontiguousarray(a.astype(np.float32)).astype(ml_dtypes.bfloat16)
    return {"c_F": bf(F), "c_T1": bf(T1), "c_T2": bf(T2), "c_Cc": bf(np.cos(phi) * nrm), "c_mSc": bf(-np.sin(phi) * nrm)}


def kernel(**inputs):
    f32 = lambda a: np.ascontiguousarray(np.asarray(a), dtype=np.float32)
    inp = {k: f32(v) for k, v in inputs.items()}
    x = inp["x"]
    ncores = 8
    cores = list(range(ncores))
    w1 = {n: np.ascontiguousarray(inp[n][0]) for n in W1}
    ident = np.eye(128, dtype=np.float32)
    maps = []
    for c in cores:
        b, s0 = c // 4, (c % 4) * 2048
        xh = np.zeros((2050, 1024), np.float32)
        xh[1:2049] = x[b, s0:s0 + 2048]
        if s0 > 0:
            xh[0] = x[b, s0 - 1]
        if s0 + 2048 < 8192:
            xh[2049] = x[b, s0 + 2048]
        m = {"xh": xh, "c_ident": ident}
        m.update(w1)
        maps.append(m)
    nc1 = build_l1({n: w1[n].shape for n in W1})
    r1 = run_bass_kernel_spmd(nc1, maps, core_ids=cores).results
    PT = np.concatenate([np.asarray(r["pT"]) for r in r1], axis=1)
    FD = np.concatenate([np.asarray(r["fd"]) for r in r1], axis=0)
    consts = build_consts_np()
    ft = fnet_tables()
    mu, w0, w2, a0, a2 = inp["o_mu"][0], inp["o_w0"][0], inp["o_w2"][0], inp["o_a0"][0], inp["o_a2"][0]
    k_k, k_a, r_k = inp["o_k_k"][0], inp["o_k_a"][0], inp["o_r_k"][0].reshape(-1)
    lg, lb = inp["o_lnx_g"][0], inp["o_lnx_b"][0]
    PT3 = PT.reshape(4736, 2, 8192)
    pad = lambda a: np.ascontiguousarray(np.pad(a, ((0, 0), (0, 0), (1, 1))))
    maps = []
    for c in cores:
        ch = slice(c * 128, (c + 1) * 128)
        m = {"pr": pad(PT3[0:1024][ch]), "pk": pad(PT3[1024:2048][ch]), "pv": pad(PT3[2048:3072][ch]),
             "pwa": pad(PT3[3072:3200])}
        prm = np.zeros((128, 17), np.float32)
        for d in range(2):
            prm[:, 0 + d] = mu[d, 0:1024][ch]; prm[:, 2 + d] = mu[d, 1024:2048][ch]; prm[:, 4 + d] = mu[d, 2048:3072][ch]
            prm[:, 6 + d] = mu[d, 3072:3200]; prm[:, 8 + d] = w0[d][ch]; prm[:, 10 + d] = a0[d][ch]
        prm[:, 12] = k_k[ch]; prm[:, 13] = k_a[ch]; prm[:, 14] = r_k[ch]; prm[:, 15] = lg[ch]; prm[:, 16] = lb[ch]
        m["prm"] = prm
        m["w2a2"] = np.ascontiguousarray(np.concatenate([w2[:, :, ch], a2[:, :, ch]], axis=1).transpose(1, 0, 2))
        m.update(consts)
        b, g = c // 4, c % 4
        m["fx"] = np.ascontiguousarray(FD[b * 8192:(b + 1) * 8192, g * 128:(g + 1) * 128]).reshape(128, 8192)
        m["fw"] = np.ascontiguousarray(inp["o_fnet_w"][0, g])
        m.update(ft)
        maps.append(m)
    nc2 = build_l2(consts)
    r2 = run_bass_kernel_spmd(nc2, maps, core_ids=cores).results
    YC = np.concatenate([np.asarray(r["yout"]).reshape(128, 16384) for r in r2], axis=0)
    YD = np.concatenate([np.concatenate([np.asarray(r2[b * 4 + g]["ydT"]) for g in range(4)], axis=0) for b in range(2)], axis=1)
    maps = []
    for c in cores:
        ts = slice(c * 2048, (c + 1) * 2048)
        maps.append({"ycT": np.ascontiguousarray(YC[:, ts]), "ydT": np.ascontiguousarray(YD[:, ts]),
                     "gT": np.ascontiguousarray(PT[3200:4736, ts]), "h1": np.asarray(r1[c]["h1"]),
                     "o_w_out": np.ascontiguousarray(inp["o_w_out"][0]), "final_norm_g": inp["final_norm_g"]})
    nc3 = build_l3()
    r3 = run_bass_kernel_spmd(nc3, maps, core_ids=cores).results
    y = np.concatenate([np.asarray(r["y"]) for r in r3], axis=0).reshape(2, 8192, 1024)
    return y.astype(np.float32)
```

```python
from contextlib import ExitStack
import itertools
import numpy as np
import concourse.bass as bass
import concourse.mybir as mybir
from concourse.bass_utils import run_bass_kernel_spmd


F32 = mybir.dt.float32
BF16 = mybir.dt.bfloat16
AF = mybir.ActivationFunctionType
ALU = mybir.AluOpType
AX = mybir.AxisListType

N_DMA_SEMS = 8


class Region:
    __slots__ = ("w", "r", "name")

    def __init__(self, name=""):
        self.w = None
        self.r = {}
        self.name = name


class Sched:
    ENGS = ("pe", "dve", "act", "pool", "sp")

    def __init__(self, nc):
        self.nc = nc
        self.prog = {e: [] for e in self.ENGS}
        self.cnt = {}
        self.seen = {e: {} for e in self.ENGS}
        self.dma_rr = {e: 0 for e in self.ENGS}
        self.dma_last = {}
        self.same_engine_raw = True
        self.cut = 0
        self.raw_only = True
        self.nrec = 0
        self.log = []

    def _collect(self, eng, mykey, reads, writes):
        waits = {}

        def need(tok, kind):
            if tok is None:
                return
            k, v = tok
            if k == mykey:
                if eng == "pe":
                    return
                if not self.same_engine_raw:
                    return
                if self.raw_only and kind != "raw":
                    return
            if waits.get(k, 0) < v:
                waits[k] = v

        for R in reads:
            need(R.w, "raw")
        for R in writes:
            need(R.w, "waw")
            for k, v in R.r.items():
                need((k, v), "war")
        out = []
        seen = self.seen[eng]
        for k, v in waits.items():
            if seen.get(k, 0) < v:
                seen[k] = v
                out.append((k, v))
        return out

    def _commit(self, tok, reads, writes):
        for R in writes:
            R.w = tok
            R.r = {}
        k, v = tok
        for R in reads:
            if R.r.get(k, 0) < v:
                R.r[k] = v

    def op(self, eng, fn, reads=(), writes=()):
        self.nrec += 1
        if self.cut and self.nrec > self.cut:
            return None
        if self.cut:
            self.log.append((self.nrec, eng, fn[0] if isinstance(fn, tuple) else "fn", str(fn[1].get("out", ""))[:120] if isinstance(fn, tuple) else ""))
        key = eng
        waits = self._collect(eng, key, reads, writes)
        idx = self.cnt.get(key, 0) + 1
        self.cnt[key] = idx
        tok = (key, idx)
        self.prog[eng].append([waits, fn, tok])
        self._commit(tok, reads, writes)
        return tok

    def dma(self, fn, reads=(), writes=(), q="sp"):
        self.nrec += 1
        if self.cut and self.nrec > self.cut:
            return None
        i = self.dma_rr[q]
        self.dma_rr[q] = (i + 1) % N_DMA_SEMS
        key = "dma_%s_%d" % (q, i)
        waits = self._collect(q, key, reads, writes)
        prev = self.cnt.get(key, 0)
        if prev > 0 and self.seen[q].get(key, 0) < prev:
            self.seen[q][key] = prev
            waits.append((key, prev))
        idx = prev + 1
        self.cnt[key] = idx
        tok = (key, idx)
        self.prog[q].append([waits, fn, tok])
        self._commit(tok, reads, writes)
        return tok

    def finalize(self):
        nc = self.nc
        waited = {}
        for e in self.ENGS:
            for waits, fn, tok in self.prog[e]:
                for k, v in waits:
                    waited.setdefault(k, set()).add(v)
        self.final_waits = []
        sem_of = {}
        val_of = {}
        for k, s in waited.items():
            sem_of[k] = nc.alloc_semaphore("s_" + k)
            isdma = k.startswith("dma_")
            step = 16 if isdma else 1
            if isdma:
                val_of[k] = None
            else:
                val_of[k] = {v: (i + 1) for i, v in enumerate(sorted(s))}
        engobj = {"pe": nc.tensor, "dve": nc.vector, "act": nc.scalar,
                  "pool": nc.gpsimd, "sp": nc.sync}

        def value(k, v):
            if val_of[k] is None:
                return 16 * v
            return val_of[k][v]

        def emit(e):
            def body(eng):
                for waits, fn, tok in self.prog[e]:
                    for k, v in waits:
                        eng.wait_ge(sem_of[k], value(k, v))
                    if fn is None:
                        continue
                    if isinstance(fn, tuple):
                        ins = getattr(eng, fn[0])(**fn[1])
                    else:
                        ins = fn(eng)
                    k, v = tok
                    if k in sem_of:
                        if val_of[k] is None:
                            ins.then_inc(sem_of[k], 16)
                        elif v in val_of[k]:
                            ins.then_inc(sem_of[k], 1)
            return body

        with nc.Block() as block:
            for e, dec in (("sp", block.sync), ("pe", block.tensor), ("dve", block.vector),
                           ("act", block.scalar), ("pool", block.gpsimd)):
                if self.prog[e]:
                    dec(emit(e))
        self.n_sems = len(sem_of)
        return self.n_sems

    def barrier_on(self, eng, toks):
        waits = []
        for tk in toks:
            if tk is None:
                continue
            k, v = tk
            if self.seen[eng].get(k, 0) < v:
                self.seen[eng][k] = v
                waits.append((k, v))
        if waits:
            self.prog[eng].append([waits, None, ("_none", 0)])


C = 128
BLK = 512
NEG_E = -float(np.exp(-0.5))
GN_EPS = 64e-5


def build_consts_np():
    idx = np.arange(128)
    lt = (idx[:, None] < idx[None, :]).astype(np.float32)
    le = (idx[:, None] <= idx[None, :]).astype(np.float32)
    gt = lt.T.copy()
    ge = le.T.copy()
    m4f = np.stack([lt, gt, gt, le], axis=1)
    m4b = np.stack([gt, lt, lt, ge], axis=1)
    mk = np.stack([le, ge], axis=1)
    ident = np.eye(128, dtype=np.float32)
    bd = np.kron(np.eye(2, dtype=np.float32), np.ones((64, 64), np.float32))
    scanm = np.ones((128, BLK), np.float32)
    scanm[:, ::C] = 0.0
    return {"c_m4": np.stack([m4f, m4b], axis=1).reshape(128, 2 * 4 * 128).copy(),
            "c_mk": mk.reshape(128, 256).copy(), "c_ident": ident, "c_bd": bd, "c_scanm": scanm}


XST = False


def emit_rwkv(S, nc, A, pr, pk, pv, pwa, prm, w2a2, yout, NB, T):
    sb = lambda name, shape, dt=F32: nc.alloc_sbuf_tensor(name, shape, dt)
    ps = lambda name, shape, dt=F32: nc.alloc_psum_tensor(name, shape, dt)
    R = Region
    nblk = T // BLK

    m4f = sb("m4f", [128, 2, 4, 128]); Rm4 = R()
    mkf = sb("mkf", [128, 2, 128]); Rmk = R()
    identf = sb("identf", [128, 128]); Ridf = R()
    identb = sb("identb", [128, 128], BF16); Ridb = R()
    bdf = sb("bdf", [128, 128]); Rbd = R()
    bdr = sb("bdr", [128, 128]); Rbdr = R()
    bdm = sb("bdm", [128, 128]); Rbdm = R()
    scanm = sb("scanm", [128, BLK]); Rsc = R()
    prmt = sb("prmt", [128, 17]); Rprm = R()
    w2f = sb("w2f", [128, 2, 128]); Rw2f = R()
    w2b = sb("w2b", [128, 2, 128], BF16); Rw2b = R()
    S.dma(("dma_start", dict(out=m4f[:].rearrange("p a b c -> p (a b c)"), in_=A["c_m4"])), writes=[Rm4])
    S.dma(("dma_start", dict(out=mkf[:].rearrange("p a c -> p (a c)"), in_=A["c_mk"])), writes=[Rmk])
    S.dma(("dma_start", dict(out=identf[:], in_=A["c_ident"])), writes=[Ridf])
    S.dma(("dma_start", dict(out=bdf[:], in_=A["c_bd"])), writes=[Rbd])
    S.dma(("dma_start", dict(out=scanm[:], in_=A["c_scanm"])), writes=[Rsc])
    S.dma(("dma_start", dict(out=prmt[:], in_=prm)), writes=[Rprm])
    S.dma(("dma_start", dict(out=w2f[:], in_=w2a2)), writes=[Rw2f])
    S.op("dve", ("tensor_copy", dict(out=identb[:], in_=identf[:])), reads=[Ridf], writes=[Ridb])
    S.op("dve", ("tensor_copy", dict(out=w2b[:], in_=w2f[:])), reads=[Rw2f], writes=[Rw2b])
    PM = lambda c: prmt[:, c:c + 1]
    S.op("dve", ("tensor_scalar", dict(out=bdr[:], in0=bdf[:], scalar1=PM(14), scalar2=None, op0=ALU.mult)), reads=[Rbd, Rprm], writes=[Rbdr])
    S.op("dve", ("tensor_scalar", dict(out=bdm[:], in0=bdf[:], scalar1=1.0 / 64, scalar2=None, op0=ALU.mult)), reads=[Rbd], writes=[Rbdm])

    def T2(name, dt=F32, n=BLK):
        return sb(name, [128, n], dt), R()
    ld = {}
    for nm in ("pr", "pk", "pv", "pwa"):
        ld[nm] = (sb("ld_" + nm, [128, BLK + 2]), R())
    tmp, Rtmp = T2("tmp")
    qr, Rqr = T2("qr"); qk, Rqk = T2("qk"); qv, Rqv = T2("qv"); qwa, Rqwa = T2("qwa")
    twa, Rtwa = T2("twa", BF16)
    sw, Rsw = T2("sw"); asg, Rasg = T2("asg")
    logw, Rlogw = T2("logw"); lin, Rlin = T2("lin"); linm, Rlinm = T2("linm"); lexm, Rlexm = T2("lexm")
    lex, Rlex = T2("lex"); lint, Rlint = T2("lint")
    e1, Re1 = T2("e1"); e1x, Re1x = T2("e1x"); e2, Re2 = T2("e2"); e3S = [sb("e3%d" % i, [128, BLK]) for i in range(2)]; Re3S = [R(), R()]; e3x, Re3x = T2("e3x"); e4, Re4 = T2("e4")
    kk, Rkk = T2("kk"); kk2, Rkk2 = T2("kk2"); rin, Rrin = T2("rin"); kkn, Rkkn = T2("kkn")
    kp, Rkp = T2("kp"); bv, Rbv = T2("bv"); rk, Rrk = T2("rk")
    rtS = [sb("rt%d" % i, [128, BLK], BF16) for i in range(2)]; RrtS = [R(), R()]; atS = [sb("at%d" % i, [128, BLK], BF16) for i in range(2)]; RatS = [R(), R()]; ktS = [sb("kt%d" % i, [128, BLK], BF16) for i in range(2)]; RktS = [R(), R()]; btS = [sb("bt%d" % i, [128, BLK], BF16) for i in range(2)]; RbtS = [R(), R()]
    r0S = [sb("r0%d" % i, [128, BLK]) for i in range(2)]; Rr0S = [R(), R()]; a0bS = [sb("a0b%d" % i, [128, BLK], BF16) for i in range(2)]; Ra0bS = [R(), R()]; kEbS = [sb("kEb%d" % i, [128, BLK], BF16) for i in range(2)]; RkEbS = [R(), R()]; bEbS = [sb("bEb%d" % i, [128, BLK], BF16) for i in range(2)]; RbEbS = [R(), R()]
    qvbS = [sb("qvb%d" % i, [128, BLK], BF16) for i in range(2)]; RqvbS = [R(), R()]
    ysum = sb("ysum", [128, T]); Rys = [R() for _ in range(T // C)]
    bsum = sb("bsum", [128, T]); Rbs = [R() for _ in range(nblk)]
    TT = [sb("TT%d" % i, [128, 4, 128], BF16) for i in range(2)]; RTT = [R(), R()]
    SBM = [sb("SBM%d" % i, [128, 2, 4, 128], BF16) for i in range(2)]; RSBM = [R(), R()]
    MKR = [sb("MKR%d" % i, [128, 2, 128]) for i in range(2)]; RMKR = [R(), R()]
    SX = [sb("SX%d" % i, [128, 2, 192], BF16) for i in range(2)]; RSX = [R(), R()]
    SAB = [sb("SAB%d" % i, [128, 2, 2, 128], BF16) for i in range(2)]; RSAB = [R(), R()]
    Gb = sb("Gb", [128, 128], BF16); RGb = R()
    Hb = sb("Hb", [128, 2, 128], BF16); RHb = R()
    Pb = sb("Pb", [128, 64], BF16); RPb = R()
    Zb = sb("Zb", [128, 2, 64], BF16); RZb = R()
    STz = [sb("STz%d" % h, [128, 64], BF16) for h in range(2)]; RST = [R(), R()]
    identP = sb("identP", [128, 64]); mkb = sb("mkb", [128, 2, 2, 128])
    HS = [slice(0, 64), slice(64, 128)]
    fin1, Rfin1 = T2("fin1"); fin2, Rfin2 = T2("fin2"); fin3, Rfin3 = T2("fin3")

    PS_M = ps("PS_M", [128, 2, 4, 128]); RPS_M = R()
    PS_K = ps("PS_K", [128, 512]); RPS_K = R()
    PS_X = [ps("PS_X%d" % h, [128, 512]) for h in range(2)]; RPS_X = R()
    PS_AB = ps("PS_AB", [128, 2, 2, 128]); RPS_AB = R()
    PS_G = ps("PS_G", [128, 512]); RPS_G = R()
    PS_T = ps("PS_T", [128, 8, 128], BF16); RPS_T = R()
    PS_P1 = PS_AB[:].rearrange("p a b c -> p (a b c)"); RPS_P1 = RPS_AB
    PS_P2 = PS_P1; RPS_P2 = RPS_AB
    mm = lambda out, l, r_, rd, wr, st=True, sp=True, sg=False: S.op("pe", ("matmul", dict(out=out, lhsT=l, rhs=r_, start=st, stop=sp, skip_group_check=sg)), reads=rd, writes=wr)
    S.op("pool", ("tensor_copy", dict(out=identP[0:64, :], in_=identf[0:64, 0:64])), reads=[Ridf], writes=[Ridf])
    S.op("pool", ("tensor_copy", dict(out=identP[64:128, :], in_=identf[64:128, 64:128])), reads=[Ridf], writes=[Ridf])
    for h in range(2):
        S.op("pool", ("tensor_copy", dict(out=mkb[:, :, h, :], in_=mkf[:])), reads=[Rmk], writes=[Rmk])
    ytmp = sb("ytmp", [128, 128]); Rytmp = R()
    out_toks = []
    NFILL = 4
    NPREP = 2
    def prep_gen(b, d, blk, pp):
        bwd = (d == 1)
        midc, totc = (C // 2 - 1, C - 1) if not bwd else (C // 2, 0)
        t0 = blk * BLK
        rt_, Rrt_ = rtS[pp], RrtS[pp]
        at_, Rat_ = atS[pp], RatS[pp]
        kt_, Rkt_ = ktS[pp], RktS[pp]
        bt_, Rbt_ = btS[pp], RbtS[pp]
        r0_, Rr0_ = r0S[pp], Rr0S[pp]
        a0b_, Ra0b_ = a0bS[pp], Ra0bS[pp]
        kEb_, RkEb_ = kEbS[pp], RkEbS[pp]
        bEb_, RbEb_ = bEbS[pp], RbEbS[pp]
        qvb_, Rqvb_ = qvbS[pp], RqvbS[pp]
        e3_, Re3_ = e3S[pp], Re3S[pp]
        for nm, src in (("pr", pr), ("pk", pk), ("pv", pv), ("pwa", pwa)):
            tl, Rl = ld[nm]
            S.dma(("dma_start", dict(out=tl[:], in_=src[:, b, t0:t0 + BLK + 2])), writes=[Rl])
            yield
        sh = (slice(0, BLK) if not bwd else slice(2, BLK + 2))
        cur = slice(1, BLK + 1)
        for nm, q, Rq, mc in (("pr", qr, Rqr, 0), ("pk", qk, Rqk, 2), ("pv", qv, Rqv, 4), ("pwa", qwa, Rqwa, 6)):
            tl, Rl = ld[nm]
            S.op("dve", ("tensor_tensor", dict(out=tmp[:], in0=tl[:, sh], in1=tl[:, cur], op=ALU.subtract)), reads=[Rl], writes=[Rtmp])
            yield
            S.op("dve", ("scalar_tensor_tensor", dict(out=q[:], in0=tmp[:], scalar=PM(mc + d), in1=tl[:, cur], op0=ALU.mult, op1=ALU.add)), reads=[Rtmp, Rl, Rprm], writes=[Rq])
            yield
        S.op("act", ("activation", dict(out=twa[0:64, :], in_=qwa[0:64, :], func=AF.Tanh)), reads=[Rqwa], writes=[Rtwa])
        yield
        S.op("dve", ("tensor_copy", dict(out=twa[64:128, :], in_=qwa[64:128, :])), reads=[Rqwa], writes=[Rtwa])
        yield
        S.op("pe", ("matmul", dict(out=PS_P1, lhsT=w2b[0:64, d, :], rhs=twa[0:64, :], start=True, stop=True)), reads=[Rw2b, Rtwa], writes=[RPS_P1])
        S.op("act", ("activation", dict(out=sw[:], in_=PS_P1, func=AF.Sigmoid, bias=PM(8 + d))), reads=[RPS_P1, Rprm], writes=[Rsw])
        yield
        S.op("pe", ("matmul", dict(out=PS_P2, lhsT=w2b[64:128, d, :], rhs=twa[64:128, :], start=True, stop=True)), reads=[Rw2b, Rtwa], writes=[RPS_P2])
        S.op("act", ("activation", dict(out=asg[:], in_=PS_P2, func=AF.Sigmoid, bias=PM(10 + d))), reads=[RPS_P2, Rprm], writes=[Rasg])
        yield
        S.op("dve", ("tensor_scalar", dict(out=logw[:], in0=sw[:], scalar1=NEG_E, scalar2=None, op0=ALU.mult)), reads=[Rsw], writes=[Rlogw])
        yield
        S.op("dve", ("tensor_tensor_scan", dict(out=lin[:], data0=scanm[:], data1=logw[:], initial=0.0, op0=ALU.mult, op1=ALU.add)), reads=[Rsc, Rlogw], writes=[Rlin])
        yield
        lin3 = lambda tl: tl[:].rearrange("p (c t) -> p c t", t=C)
        bc = lambda tl, col: lin3(tl)[:, :, col:col + 1].to_broadcast([128, BLK // C, C])
        if bwd:
            S.op("dve", ("tensor_tensor", dict(out=lin3(tmp), in0=bc(lin, C - 1), in1=lin3(lin), op=ALU.subtract)), reads=[Rlin], writes=[Rtmp])
            yield
            S.op("dve", ("tensor_tensor", dict(out=lin[:], in0=tmp[:], in1=logw[:], op=ALU.add)), reads=[Rtmp, Rlogw], writes=[Rlin])
            yield
        S.op("dve", ("tensor_tensor", dict(out=lin3(linm), in0=lin3(lin), in1=bc(lin, midc), op=ALU.subtract)), reads=[Rlin], writes=[Rlinm])
        yield
        S.op("dve", ("tensor_tensor", dict(out=lexm[:], in0=linm[:], in1=logw[:], op=ALU.subtract)), reads=[Rlinm, Rlogw], writes=[Rlexm])
        yield
        S.op("dve", ("tensor_tensor", dict(out=lex[:], in0=lin[:], in1=logw[:], op=ALU.subtract)), reads=[Rlin, Rlogw], writes=[Rlex])
        yield
        S.op("dve", ("tensor_tensor", dict(out=lin3(lint), in0=lin3(lin), in1=bc(lin, totc), op=ALU.subtract)), reads=[Rlin], writes=[Rlint])
        yield
        S.op("act", ("activation", dict(out=e1[:], in_=linm[:], func=AF.Exp)), reads=[Rlinm], writes=[Re1])
        yield
        S.op("act", ("activation", dict(out=e1x[:], in_=lexm[:], func=AF.Exp)), reads=[Rlexm], writes=[Re1x])
        yield
        S.op("act", ("activation", dict(out=e2[:], in_=linm[:], func=AF.Exp, scale=-1.0)), reads=[Rlinm], writes=[Re2])
        yield
        S.op("act", ("activation", dict(out=e3_[:], in_=lin[:], func=AF.Exp)), reads=[Rlin], writes=[Re3_])
        yield
        S.op("act", ("activation", dict(out=e3x[:], in_=lex[:], func=AF.Exp)), reads=[Rlex], writes=[Re3x])
        yield
        S.op("act", ("activation", dict(out=e4[:], in_=lint[:], func=AF.Exp, scale=-1.0)), reads=[Rlint], writes=[Re4])
        yield
        S.op("dve", ("tensor_scalar", dict(out=kk[:], in0=qk[:], scalar1=PM(12), scalar2=None, op0=ALU.mult)), reads=[Rqk, Rprm], writes=[Rkk])
        yield
        S.op("pool", ("tensor_tensor", dict(out=kk2[:], in0=kk[:], in1=kk[:], op=ALU.mult)), reads=[Rkk], writes=[Rkk2])
        yield
        S.op("pe", ("matmul", dict(out=PS_P1, lhsT=bdf[:], rhs=kk2[:], start=True, stop=True)), reads=[Rbd, Rkk2], writes=[RPS_P1])
        S.op("dve", ("tensor_scalar", dict(out=rin[:], in0=PS_P1, scalar1=1e-12, scalar2=None, op0=ALU.max)), reads=[RPS_P1], writes=[Rrin])
        yield
        S.op("act", ("activation", dict(out=rin[:], in_=rin[:], func=AF.Sqrt)), reads=[Rrin], writes=[Rrin])
        yield
        S.op("dve", ("reciprocal", dict(out=rin[:], in_=rin[:])), reads=[Rrin], writes=[Rrin])
        yield
        S.op("dve", ("tensor_tensor", dict(out=kkn[:], in0=kk[:], in1=rin[:], op=ALU.mult)), reads=[Rkk, Rrin], writes=[Rkkn])
        yield
        S.op("dve", ("tensor_scalar", dict(out=tmp[:], in0=asg[:], scalar1=-1.0, scalar2=PM(13), op0=ALU.add, op1=ALU.mult)), reads=[Rasg, Rprm], writes=[Rtmp])
        yield
        S.op("dve", ("scalar_tensor_tensor", dict(out=kp[:], in0=tmp[:], scalar=1.0, in1=qk[:], op0=ALU.add, op1=ALU.mult)), reads=[Rtmp, Rqk], writes=[Rkp])
        yield
        S.op("pool", ("tensor_tensor", dict(out=bv[:], in0=kkn[:], in1=asg[:], op=ALU.mult)), reads=[Rkkn, Rasg], writes=[Rbv])
        yield
        S.op("pool", ("tensor_tensor", dict(out=rk[:], in0=qr[:], in1=kp[:], op=ALU.mult)), reads=[Rqr, Rkp], writes=[Rrk])
        yield
        S.op("pe", ("matmul", dict(out=PS_P2, lhsT=bdr[:], rhs=rk[:], start=True, stop=True)), reads=[Rbdr, Rrk], writes=[RPS_P2])
        bsl = bsum[:, t0:t0 + BLK]
        if d == 0:
            S.op("dve", ("tensor_tensor", dict(out=bsl, in0=PS_P2, in1=qv[:], op=ALU.mult)), reads=[RPS_P2, Rqv], writes=[Rbs[blk]])
            yield
        else:
            S.op("dve", ("tensor_tensor", dict(out=tmp[:], in0=PS_P2, in1=qv[:], op=ALU.mult)), reads=[RPS_P2, Rqv], writes=[Rtmp])
            yield
            S.op("pool", ("tensor_tensor", dict(out=bsl, in0=bsl, in1=tmp[:], op=ALU.add)), reads=[Rtmp, Rbs[blk]], writes=[Rbs[blk]])
            yield
        S.op("dve", ("tensor_tensor", dict(out=rt_[:], in0=qr[:], in1=e1[:], op=ALU.mult)), reads=[Rqr, Re1], writes=[Rrt_])
        yield
        S.op("dve", ("scalar_tensor_tensor", dict(out=at_[:], in0=kkn[:], scalar=-1.0, in1=e1x[:], op0=ALU.mult, op1=ALU.mult)), reads=[Rkkn, Re1x], writes=[Rat_])
        yield
        S.op("pool", ("tensor_tensor", dict(out=kt_[:], in0=kp[:], in1=e2[:], op=ALU.mult)), reads=[Rkp, Re2], writes=[Rkt_])
        yield
        S.op("pool", ("tensor_tensor", dict(out=bt_[:], in0=bv[:], in1=e2[:], op=ALU.mult)), reads=[Rbv, Re2], writes=[Rbt_])
        yield
        S.op("pool", ("tensor_tensor", dict(out=r0_[:], in0=qr[:], in1=e3_[:], op=ALU.mult)), reads=[Rqr, Re3_], writes=[Rr0_])
        yield
        S.op("dve", ("scalar_tensor_tensor", dict(out=a0b_[:], in0=kkn[:], scalar=-1.0, in1=e3x[:], op0=ALU.mult, op1=ALU.mult)), reads=[Rkkn, Re3x], writes=[Ra0b_])
        yield
        S.op("pool", ("tensor_tensor", dict(out=kEb_[:], in0=kp[:], in1=e4[:], op=ALU.mult)), reads=[Rkp, Re4], writes=[RkEb_])
        yield
        S.op("pool", ("tensor_tensor", dict(out=bEb_[:], in0=bv[:], in1=e4[:], op=ALU.mult)), reads=[Rbv, Re4], writes=[RbEb_])
        yield
        S.op("act", ("activation", dict(out=qvb_[:], in_=qv[:], func=AF.Copy)), reads=[Rqv], writes=[Rqvb_])
        yield


    def block_stages(b, d, blk, pp):
        bwd = (d == 1)
        midc, totc = (C // 2 - 1, C - 1) if not bwd else (C // 2, 0)
        t0 = blk * BLK
        rt_, Rrt_ = rtS[pp], RrtS[pp]
        at_, Rat_ = atS[pp], RatS[pp]
        kt_, Rkt_ = ktS[pp], RktS[pp]
        bt_, Rbt_ = btS[pp], RbtS[pp]
        r0_, Rr0_ = r0S[pp], Rr0S[pp]
        a0b_, Ra0b_ = a0bS[pp], Ra0bS[pp]
        kEb_, RkEb_ = kEbS[pp], RkEbS[pp]
        bEb_, RbEb_ = bEbS[pp], RbEbS[pp]
        qvb_, Rqvb_ = qvbS[pp], RqvbS[pp]
        e3_, Re3_ = e3S[pp], Re3S[pp]

        def stage1(ck):
            ci, cs, gci, p = ck
            for i, (src, Rs) in enumerate(((qvb_, Rqvb_), (a0b_, Ra0b_), (bEb_, RbEb_), (kEb_, RkEb_))):
                S.op("pe", ("transpose", dict(out=PS_T[:, i, :], in_=src[:, cs], identity=identb[:])), reads=[Rs, Ridb], writes=[RPS_T])
            yield
            S.op("act", ("activation", dict(out=TT[p][:], in_=PS_T[:, 0:4, :], func=AF.Copy)), reads=[RPS_T], writes=[RTT[p]])
            yield
            for h in range(2):
                hs = HS[h]
                mm(PS_M[:, h, 0, :], bt_[hs, cs], at_[hs, cs], [Rbt_, Rat_], [RPS_M])
                mm(PS_M[:, h, 1, :], at_[hs, cs], bt_[hs, cs], [Rbt_, Rat_], [RPS_M])
                yield
                mm(PS_M[:, h, 2, :], at_[hs, cs], kt_[hs, cs], [Rkt_, Rat_], [RPS_M])
                mm(PS_M[:, h, 3, :], bt_[hs, cs], rt_[hs, cs], [Rbt_, Rrt_], [RPS_M])
                yield
                mm((PS_K if h == 0 else PS_G)[:, 0:128], kt_[hs, cs], rt_[hs, cs], [Rkt_, Rrt_], [RPS_K if h == 0 else RPS_G])
                yield
            for h in range(2):
                S.op("dve", ("tensor_tensor", dict(out=SBM[p][:, h], in0=PS_M[:, h], in1=m4f[:, d, :, :], op=ALU.mult)), reads=[RPS_M, Rm4], writes=[RSBM[p]])
                yield
            S.op("dve", ("tensor_tensor", dict(out=MKR[p][:, 0, :], in0=PS_K[:, 0:128], in1=mkf[:, d, :], op=ALU.mult)), reads=[RPS_K, Rmk], writes=[RMKR[p]])
            yield
            S.op("dve", ("tensor_tensor", dict(out=MKR[p][:, 1, :], in0=PS_G[:, 0:128], in1=mkf[:, d, :], op=ALU.mult)), reads=[RPS_G, Rmk], writes=[RMKR[p]])
            yield
            S.op("act", ("activation", dict(out=SX[p][:, :, 0:128], in_=SBM[p][:, :, 3, :], func=AF.Copy)), reads=[RSBM[p]], writes=[RSX[p]])
            S.op("pool", ("tensor_copy", dict(out=SX[p][:, :, 128:192], in_=TT[p][:, 2, :].rearrange("p (h j) -> p h j", h=2))), reads=[RTT[p]], writes=[RSX[p]])
            yield

        def stage2(ck):
            ci, cs, gci, p = ck
            for h in range(2):
                mm(PS_X[h][:, 0:192], identb[:], SX[p][:, h, :], [Ridb, RSX[p]], [RPS_X], st=True, sp=True)
            A_ = [SBM[p][:, h, 1, :] for h in range(2)]
            B_ = [SBM[p][:, h, 0, :] for h in range(2)]
            Rcur = RSBM[p]
            for lv in range(7):
                if lv < 6:
                    nb = lv % 2
                    for h in range(2):
                        mm(PS_AB[:, h, 0, :], B_[h], A_[h], [Rcur], [RPS_AB])
                        mm(PS_AB[:, h, 1, :], A_[h], B_[h], [Rcur], [RPS_AB])
                for h in range(2):
                    mm(PS_X[h][:, 0:192], A_[h], SX[p][:, h, :], [Rcur, RSX[p]], [RPS_X], st=False, sp=True, sg=True)
                if lv < 6:
                    S.op("act", ("activation", dict(out=SAB[nb][:].rearrange("p a b c -> p (a b c)"), in_=PS_AB[:].rearrange("p a b c -> p (a b c)"), func=AF.Copy)), reads=[RPS_AB], writes=[RSAB[nb]])
                S.op("dve", ("tensor_copy", dict(out=SX[p][:, 0, :], in_=PS_X[0][:, 0:192])), reads=[RPS_X], writes=[RSX[p]])
                S.op("dve", ("tensor_copy", dict(out=SX[p][:, 1, :], in_=PS_X[1][:, 0:192])), reads=[RPS_X], writes=[RSX[p]])
                if lv < 6:
                    A_ = [SAB[nb][:, h, 0, :] for h in range(2)]
                    B_ = [SAB[nb][:, h, 1, :] for h in range(2)]
                    Rcur = RSAB[nb]
                yield

        def stage3(ck):
            ci, cs, gci, p = ck
            for h in range(2):
                hs = HS[h]
                a0T = TT[p][:, 1, hs]
                mm(PS_G[hs, 0:128], a0T, SX[p][:, h, 0:128], [RTT[p], RSX[p]], [RPS_G])
                mm(PS_G[hs, 128:192], a0T, SX[p][:, h, 128:192], [RTT[p], RSX[p]], [RPS_G])
                yield
                mm(PS_G[:, 192 + 128 * h:320 + 128 * h], SBM[p][:, h, 2, :], SX[p][:, h, 0:128], [RSBM[p], RSX[p]], [RPS_G])
                mm(PS_K[:, 256 + 64 * h:320 + 64 * h], SBM[p][:, h, 2, :], SX[p][:, h, 128:192], [RSBM[p], RSX[p]], [RPS_K])
                yield
            S.op("dve", ("tensor_tensor", dict(out=Gb[:], in0=PS_G[:, 0:128], in1=r0_[:, cs], op=ALU.add)), reads=[RPS_G, Rr0_], writes=[RGb])
            yield
            S.op("dve", ("tensor_tensor", dict(out=Hb[:], in0=PS_G[:, 192:448].rearrange("p (h t) -> p h t", h=2), in1=MKR[p][:], op=ALU.add)), reads=[RPS_G, RMKR[p]], writes=[RHb])
            yield
            tcol = ci * C + totc
            S.op("dve", ("scalar_tensor_tensor", dict(out=Pb[:], in0=identP[:], scalar=e3_[:, tcol:tcol + 1], in1=PS_G[:, 128:192], op0=ALU.mult, op1=ALU.add)), reads=[RPS_G, Ridf, Re3_], writes=[RPb])
            yield
            S.op("dve", ("tensor_tensor", dict(out=Zb[:], in0=PS_K[:, 256:384].rearrange("p (h j) -> p h j", h=2), in1=TT[p][:, 3, :].rearrange("p (h j) -> p h j", h=2), op=ALU.add)), reads=[RPS_K, RTT[p]], writes=[RZb])
            yield
            for h in range(2):
                hs = HS[h]
                mm(PS_M[hs, 0, 0, :], STz[h][:], Gb[:], [RST[h], RGb], [RPS_M], st=True, sp=False)
                mm(PS_M[hs, 0, 0, :], TT[p][:, 0, hs], Hb[:, h, :], [RTT[p], RHb], [RPS_M], st=False, sp=True)
                yield
                mm(PS_M[hs, 0, 1, 0:64], Pb[:], STz[h][:], [RPb, RST[h]], [RPS_M], st=True, sp=False)
                mm(PS_M[hs, 0, 1, 0:64], Zb[:, h, :], TT[p][:, 0, hs], [RZb, RTT[p]], [RPS_M], st=False, sp=True)
                yield
            ysl = ysum[:, t0 + ci * C: t0 + (ci + 1) * C]
            if d == 0:
                S.op("act", ("activation", dict(out=ysl, in_=PS_M[:, 0, 0, :], func=AF.Copy)), reads=[RPS_M], writes=[Rys[gci]])
            else:
                S.op("act", ("activation", dict(out=ytmp[:, 0:128], in_=PS_M[:, 0, 0, :], func=AF.Copy)), reads=[RPS_M], writes=[Rytmp])
                S.op("dve", ("tensor_tensor", dict(out=ysl, in0=ytmp[:, 0:128], in1=ysl, op=ALU.add)), reads=[Rytmp, Rys[gci]], writes=[Rys[gci]])
            yield
            for h in range(2):
                hs = HS[h]
                S.op("act", ("activation", dict(out=STz[h][hs, :], in_=PS_M[hs, 0, 1, 0:64], func=AF.Copy)), reads=[RPS_M], writes=[RST[h]])
            yield

        return stage1, stage2, stage3

    def finalize_batch(b):
        for blk in range(nblk):
            t0 = blk * BLK
            ysl = ysum[:, t0:t0 + BLK]
            Rin = Rys[t0 // C: (t0 + BLK) // C]
            S.op("pe", ("matmul", dict(out=PS_P1, lhsT=bdm[:], rhs=ysl, start=True, stop=True)), reads=[Rbdm] + Rin, writes=[RPS_P1])
            S.op("dve", ("tensor_tensor", dict(out=fin1[:], in0=ysl, in1=PS_P1, op=ALU.subtract)), reads=[RPS_P1] + Rin, writes=[Rfin1])
            S.op("pool", ("tensor_tensor", dict(out=fin2[:], in0=fin1[:], in1=fin1[:], op=ALU.mult)), reads=[Rfin1], writes=[Rfin2])
            S.op("pe", ("matmul", dict(out=PS_P2, lhsT=bdm[:], rhs=fin2[:], start=True, stop=True)), reads=[Rbdm, Rfin2], writes=[RPS_P2])
            S.op("dve", ("tensor_scalar", dict(out=fin3[:], in0=PS_P2, scalar1=GN_EPS, scalar2=None, op0=ALU.add)), reads=[RPS_P2], writes=[Rfin3])
            S.op("act", ("activation", dict(out=fin3[:], in_=fin3[:], func=AF.Sqrt)), reads=[Rfin3], writes=[Rfin3])
            S.op("dve", ("reciprocal", dict(out=fin3[:], in_=fin3[:])), reads=[Rfin3], writes=[Rfin3])
            S.op("dve", ("tensor_tensor", dict(out=fin1[:], in0=fin1[:], in1=fin3[:], op=ALU.mult)), reads=[Rfin1, Rfin3], writes=[Rfin1])
            S.op("dve", ("tensor_scalar", dict(out=fin2[:], in0=fin1[:], scalar1=PM(15), scalar2=PM(16), op0=ALU.mult, op1=ALU.add)), reads=[Rfin1, Rprm], writes=[Rfin2])
            S.op("dve", ("tensor_tensor", dict(out=fin2[:], in0=fin2[:], in1=bsum[:, t0:t0 + BLK], op=ALU.add)), reads=[Rfin2, Rbs[blk]], writes=[Rfin2])
            out_toks.append(S.dma(("dma_start", dict(out=yout[:, b, t0:t0 + BLK], in_=fin2[:])), reads=[Rfin2]))

    sched_blocks = []
    for b in range(NB):
        for d in range(2):
            order = list(range(nblk)) if d == 0 else list(range(nblk - 1, -1, -1))
            for n_, blk in enumerate(order):
                sched_blocks.append((b, d, blk, n_ == 0, (n_ == len(order) - 1) and d == 1))
    pcount = 0
    for _ in prep_gen(sched_blocks[0][0], sched_blocks[0][1], sched_blocks[0][2], 0):
        pass
    for k, (b, d, blk, first_of_dir, last_of_batch) in enumerate(sched_blocks):
        pp = k % 2
        bwd = (d == 1)
        if first_of_dir:
            S.op("pool", ("memset", dict(ap=STz[0][:], constant=0.0)), writes=[RST[0]])
            S.op("pool", ("memset", dict(ap=STz[1][:], constant=0.0)), writes=[RST[1]])
        stage1, stage2, stage3 = block_stages(b, d, blk, pp)
        chunks = list(range(BLK // C)) if not bwd else list(range(BLK // C - 1, -1, -1))
        cks = [(ci, slice(ci * C, (ci + 1) * C), (blk * BLK // C) + ci, (pcount + n_) % 2) for n_, ci in enumerate(chunks)]
        pcount += len(chunks)
        if k + 1 < len(sched_blocks) and sched_blocks[k + 1][0] == b:
            nb_, nd_, nblk_ = sched_blocks[k + 1][:3]
            pgen = prep_gen(nb_, nd_, nblk_, (k + 1) % 2)
        else:
            pgen = iter(())
        for _ in stage1(cks[0]):
            pass
        for idx, ck in enumerate(cks):
            fill = itertools.chain(stage3(cks[idx - 1]) if idx > 0 else iter(()), stage1(cks[idx + 1]) if idx + 1 < len(cks) else iter(()))
            for _ in stage2(ck):
                for _k in range(NFILL):
                    next(fill, None)
                for _k in range(NPREP):
                    next(pgen, None)
            for _ in fill:
                pass
        for _ in stage3(cks[-1]):
            pass
        for _ in pgen:
            pass
        if last_of_batch:
            finalize_batch(b)
            if k + 1 < len(sched_blocks):
                nb_, nd_, nblk_ = sched_blocks[k + 1][:3]
                for _ in prep_gen(nb_, nd_, nblk_, (k + 1) % 2):
                    pass
    return out_toks


NT = 2048
NTH = NT + 2


def emit_p1(S, nc, I, O):
    sb = lambda name, shape, dt=F32: nc.alloc_sbuf_tensor(name, shape, dt)
    ps = lambda name, shape, dt=F32: nc.alloc_psum_tensor(name, shape, dt)
    R = Region
    op = S.op
    mm = lambda out, l, r_, rd, wr, st=True, sp=True: op("pe", ("matmul", dict(out=out, lhsT=l, rhs=r_, start=st, stop=sp)), reads=rd, writes=wr)

    identf = sb("identf", [128, 128]); identb = sb("identb", [128, 128], BF16); Rid = R()
    gE = sb("gE", [128, 8, 1]); gO = sb("gO", [128, 8, 1]); Rg = R()
    S.dma(("dma_start", dict(out=identf[:], in_=I["c_ident"])), writes=[Rid])
    op("dve", ("tensor_copy", dict(out=identb[:], in_=identf[:])), reads=[Rid], writes=[Rid])
    S.dma(("dma_start", dict(out=gE[:, :, 0], in_=I["e_norm_g"].rearrange("(k p) -> p k", p=128), allow_slow_non_contiguous=True)), writes=[Rg])
    S.dma(("dma_start", dict(out=gO[:, :, 0], in_=I["o_norm_g"].rearrange("(k p) -> p k", p=128), allow_slow_non_contiguous=True)), writes=[Rg])
    hnT = sb("hnT", [128, 8, NTH], BF16); RhnT = [R() for _ in range(18)]
    yT = nc.dram_tensor("yT_d", [16, 128, NT], BF16).ap(); RyT = [[R() for _ in range(4)] for _ in range(16)]
    U = sb("U", [128, 4096]); RU = R()
    xt = [sb("xt%d" % i, [128, 1024]) for i in range(2)]; Rxt = [R(), R()]
    yo = [sb("yo%d" % i, [128, 512], BF16) for i in range(2)]; Ryo = [R(), R()]
    ytl = [sb("ytl%d" % i, [128, 16, 128], BF16) for i in range(2)]; Rytl = [R(), R()]
    xn = sb("xn", [128, 1024], BF16); Rxn = R()
    sq = sb("sq", [128, 1024]); Rsq = R()
    st = sb("st", [128, 8]); Rst = R()
    stg = [sb("stg%d" % i, [128, 8, 256]) for i in range(2)]; Rstg = [R(), R()]
    wbf = [sb("wbf%d" % i, [128, 8, 512], BF16) for i in range(2)]; Rwbf = [R(), R()]
    wbig = sb("wbig", [128, 16, 1024], BF16); Rwbig = R()
    t1 = sb("t1", [128, 512]); Rt1 = R()
    t1b = sb("t1b", [128, 512]); t1s = [t1, t1b]; Rt1s = [Rt1, R()]
    t2b = sb("t2b", [128, 512]); t3b = sb("t3b", [128, 512])
    t2 = sb("t2", [128, 512]); Rt2 = R()
    t3 = sb("t3", [128, 512]); Rt3 = R()
    cw = sb("cw", [128, 8, 3]); Rcw = R()
    PS_a = ps("PS_a", [128, 512]); RPa = R()
    PS_b = ps("PS_b", [128, 512]); RPb = R()
    PS_c = ps("PS_c", [128, 512]); RPc = R()
    PS_d = ps("PS_d", [128, 512]); RPd = R()
    PS_t = ps("PS_t", [128, 8, 128], BF16); RPt = R()
    PS_m = ps("PS_m", [128, 8, 128]); RPm = R()
    for j_ in range(3):
        S.dma(("dma_start", dict(out=cw[:, :, j_], in_=I["e_conv_w"][j_].rearrange("(cb p) -> p cb", p=128), allow_slow_non_contiguous=True)), writes=[Rcw])

    def norm_tile(xtile, Rx, gt, dst_fn, Rdst, nvalid=128):
        op("act", ("activation", dict(out=sq[:], in_=xtile[:], func=AF.Square)), reads=[Rx], writes=[Rsq])
        op("dve", ("reduce_sum", dict(out=st[:, 0:1], in_=sq[:], axis=AX.X)), reads=[Rsq], writes=[Rst])
        op("dve", ("tensor_scalar", dict(out=st[:, 1:2], in0=st[:, 0:1], scalar1=1.0 / 1024, scalar2=1e-6, op0=ALU.mult, op1=ALU.add)), reads=[Rst], writes=[Rst])
        op("act", ("activation", dict(out=st[:, 2:3], in_=st[:, 1:2], func=AF.Sqrt)), reads=[Rst], writes=[Rst])
        op("dve", ("reciprocal", dict(out=st[:, 3:4], in_=st[:, 2:3])), reads=[Rst], writes=[Rst])
        op("dve", ("tensor_scalar", dict(out=xn[:], in0=xtile[:], scalar1=st[:, 3:4], scalar2=None, op0=ALU.mult)), reads=[Rx, Rst], writes=[Rxn])
        for k in range(8):
            op("pe", ("transpose", dict(out=PS_t[:, k, :], in_=xn[:, k * 128:(k + 1) * 128], identity=identb[:])), reads=[Rxn, Rid], writes=[RPt])
        dst_fn(gt)

    xh = I["xh"]
    for i in range(17):
        xb, Rx = xt[i % 2], Rxt[i % 2]
        if i < 16:
            S.dma(("dma_start", dict(out=xb[:], in_=xh[1 + 128 * i: 1 + 128 * (i + 1), :])), writes=[Rx])
            def dst(gt, i=i):
                op("dve", ("tensor_tensor", dict(out=hnT[:, :, 1 + 128 * i: 1 + 128 * (i + 1)], in0=PS_t[:], in1=gt[:].to_broadcast([128, 8, 128]), op=ALU.mult)), reads=[RPt, Rg], writes=[RhnT[i]])
        else:
            op("pool", ("memset", dict(ap=xb[:], constant=0.0)), writes=[Rx])
            S.dma(("dma_start", dict(out=xb[0:1, :], in_=xh[0:1, :])), writes=[Rx])
            S.dma(("dma_start", dict(out=xb[1:2, :], in_=xh[NT + 1:NT + 2, :])), writes=[Rx])
            def dst(gt):
                op("dve", ("tensor_tensor", dict(out=hnT[:, :, 0:1], in0=PS_t[:, :, 0:1], in1=gt[:], op=ALU.mult)), reads=[RPt, Rg], writes=[RhnT[16]])
                op("dve", ("tensor_tensor", dict(out=hnT[:, :, NT + 1:NT + 2], in0=PS_t[:, :, 1:2], in1=gt[:], op=ALU.mult)), reads=[RPt, Rg], writes=[RhnT[17]])
        norm_tile(xb, Rx, gE, dst, None)
    allh = RhnT

    wi = I["e_w_in"].rearrange("(k p) (s c) -> p k s c", p=128, c=1024)

    def load_w(buf, src4, nsp):
        for s_ in range(nsp):
            sb_ = s_ % 2
            S.dma(("dma_start", dict(out=stg[sb_][:, :, 0:128], in_=src4[:, :, s_, :])), writes=[Rstg[sb_]])
            op("act", ("activation", dict(out=wbf[buf][:, :, s_ * 128:(s_ + 1) * 128], in_=stg[sb_][:, :, 0:128], func=AF.Copy)), reads=[Rstg[sb_]], writes=[Rwbf[buf]])
        return wbf[buf][:, :, 0:nsp * 128].rearrange("p k (s c) -> p k s c", c=128)

    PSc0, RPc0, PSd0, RPd0 = PS_c, RPc, PS_d, RPd
    t2s = [t2, t2b]; Rt2s = [Rt2, R()]
    t3s = [t3, t3b]; Rt3s = [Rt3, R()]
    xc2 = sb("xc2", [128, NTH]); RU2 = R()
    chunksA = [(0, 512), (512, 512), (1024, 512), (1536, 512), (2048, 2)]
    RU_main = RU
    for cb in range(8):
        xc, RU = (U[:, 0:NTH], RU_main) if cb % 2 == 0 else (xc2[:, :], RU2)
        if cb % 2 == 0:
            for s_ in range(4):
                sb_ = s_ % 2
                S.dma(("dma_start", dict(out=stg[sb_][:], in_=wi[:, :, s_, cb * 128:(cb + 2) * 128])), writes=[Rstg[sb_]])
                op("act", ("activation", dict(out=wbf[0][:, :, s_ * 128:(s_ + 1) * 128], in_=stg[sb_][:, :, 0:128], func=AF.Copy)), reads=[Rstg[sb_]], writes=[Rwbf[0]])
                op("act", ("activation", dict(out=wbf[1][:, :, s_ * 128:(s_ + 1) * 128], in_=stg[sb_][:, :, 128:256], func=AF.Copy)), reads=[Rstg[sb_]], writes=[Rwbf[1]])
        w4 = wbf[cb % 2][:, :, 0:512].rearrange("p k (s c) -> p k s c", c=128)
        Rw = Rwbf[cb % 2]
        for ci_, (c0, n) in enumerate(chunksA):
            (PA, RA_), (PB, RB_) = (((PS_a, RPa), (PS_b, RPb)) if ci_ % 2 == 0 else ((PS_c, RPc), (PS_d, RPd)))
            for k in range(8):
                mm(PA[:, 0:n], w4[:, k, 0, :], hnT[:, k, c0:c0 + n], [Rw] + allh, [RA_], st=(k == 0), sp=(k == 7))
            for k in range(8):
                mm(PB[:, 0:n], w4[:, k, 2, :], hnT[:, k, c0:c0 + n], [Rw] + allh, [RB_], st=(k == 0), sp=(k == 7))
            t1_, Rt1_ = t1s[ci_ % 2], Rt1s[ci_ % 2]
            op("act", ("activation", dict(out=t1_[:, 0:n], in_=PA[:, 0:n], func=AF.Copy)), reads=[RA_], writes=[Rt1_])
            op("dve", ("tensor_tensor", dict(out=xc[:, c0:c0 + n], in0=PB[:, 0:n], in1=t1_[:, 0:n], op=ALU.mult)), reads=[RB_, Rt1_], writes=[RU])
        for j in range(4):
            c0 = 1 + 512 * j
            (PS_c, RPc), (PS_d, RPd) = ((PSc0, RPc0), (PSd0, RPd0)) if j % 2 == 1 else ((PS_a, RPa), (PS_b, RPb))
            t2, Rt2 = t2s[j % 2], Rt2s[j % 2]
            t3, Rt3 = t3s[j % 2], Rt3s[j % 2]
            for k in range(8):
                mm(PS_c[:], w4[:, k, 1, :], hnT[:, k, c0:c0 + 512], [Rw] + allh, [RPc], st=(k == 0), sp=(k == 7))
            for k in range(8):
                mm(PS_d[:], w4[:, k, 3, :], hnT[:, k, c0:c0 + 512], [Rw] + allh, [RPd], st=(k == 0), sp=(k == 7))
            op("dve", ("tensor_scalar", dict(out=t2[:], in0=xc[:, c0 - 1:c0 + 511], scalar1=cw[:, cb, 0:1], scalar2=None, op0=ALU.mult)), reads=[RU, Rcw], writes=[Rt2])
            op("dve", ("scalar_tensor_tensor", dict(out=t2[:], in0=xc[:, c0:c0 + 512], scalar=cw[:, cb, 1:2], in1=t2[:], op0=ALU.mult, op1=ALU.add)), reads=[RU, Rcw, Rt2], writes=[Rt2])
            op("dve", ("scalar_tensor_tensor", dict(out=t2[:], in0=xc[:, c0 + 1:c0 + 513], scalar=cw[:, cb, 2:3], in1=t2[:], op0=ALU.mult, op1=ALU.add)), reads=[RU, Rcw, Rt2], writes=[Rt2])
            op("act", ("activation", dict(out=t3[:], in_=PS_d[:], func=AF.Silu)), reads=[RPd], writes=[Rt3])
            op("dve", ("tensor_tensor", dict(out=t2[:], in0=PS_c[:], in1=t2[:], op=ALU.mult)), reads=[RPc, Rt2], writes=[Rt2])
            op("pool", ("tensor_tensor", dict(out=yo[j % 2][:], in0=t2[:], in1=t3[:], op=ALU.mult)), reads=[Rt2, Rt3], writes=[Ryo[j % 2]])
            S.dma(("dma_start", dict(out=yT[cb, :, 512 * j:512 * (j + 1)], in_=yo[j % 2][:])), reads=[Ryo[j % 2]], writes=[RyT[cb][j]], q="pool")

    PS_c, RPc, PS_d, RPd = PSc0, RPc0, PSd0, RPd0
    t2, Rt2, t3, Rt3 = t2s[0], Rt2s[0], t3s[0], Rt3s[0]
    RU = RU_main
    for hf in range(4):
        S.dma(("dma_start", dict(out=stg[hf % 2][:], in_=wi[:, :, 5, hf * 256:(hf + 1) * 256])), writes=[Rstg[hf % 2]])
        op("act", ("activation", dict(out=wbig[:, 0:8, hf * 256:(hf + 1) * 256], in_=stg[hf % 2][:], func=AF.Copy)), reads=[Rstg[hf % 2]], writes=[Rwbig])
    for hf in range(4):
        S.dma(("dma_start", dict(out=stg[hf % 2][:], in_=wi[:, :, 4, hf * 256:(hf + 1) * 256])), writes=[Rstg[hf % 2]])
        op("act", ("activation", dict(out=wbig[:, 8:16, hf * 256:(hf + 1) * 256], in_=stg[hf % 2][:], func=AF.Copy)), reads=[Rstg[hf % 2]], writes=[Rwbig])
    for hf in range(4):
        S.dma(("dma_start", dict(out=stg[hf % 2][:], in_=wi[:, :, 6, hf * 256:(hf + 1) * 256])), writes=[Rstg[hf % 2]])
        op("act", ("activation", dict(out=wbf[hf // 2][:, :, (hf % 2) * 256:(hf % 2 + 1) * 256], in_=stg[hf % 2][:], func=AF.Copy)), reads=[Rstg[hf % 2]], writes=[Rwbf[hf // 2]])
    wsn = sb("wsn", [128, 8, 128]); wsnb = sb("wsnb", [128, 8, 128], BF16); wsT = sb("wsT", [128, 8, 128], BF16); Rws = R()
    S.dma(("dma_start", dict(out=wsn[:], in_=I["e_sgu_w"].rearrange("g i j -> i g j"))), writes=[Rws])
    op("dve", ("tensor_copy", dict(out=wsnb[:], in_=wsn[:])), reads=[Rws], writes=[Rws])
    for g in range(8):
        op("pe", ("transpose", dict(out=PS_t[:, g, :], in_=wsnb[:, g, :], identity=identb[:])), reads=[Rws, Rid], writes=[RPt])
    op("act", ("activation", dict(out=wsT[:], in_=PS_t[:], func=AF.Copy)), reads=[RPt], writes=[Rws])
    bsB = sb("bsB", [128, 8, 128]); lnG = sb("lnG", [128, 1024]); lnB = sb("lnB", [128, 1024]); Rbc = R()
    S.dma(("dma_start", dict(out=bsB[:].rearrange("p g i -> p (g i)"), in_=I["e_sgu_b"].rearrange("g i -> (g i)").partition_broadcast(128))), writes=[Rbc])
    S.dma(("dma_start", dict(out=lnG[:], in_=I["e_sgu_ln_g"].partition_broadcast(128))), writes=[Rbc])
    S.dma(("dma_start", dict(out=lnB[:], in_=I["e_sgu_ln_b"].partition_broadcast(128))), writes=[Rbc])
    vsb = sb("vsb", [128, 1024]); Rvsb = R()
    vnb = sb("vnb", [128, 1024], BF16); Rvnb = R()
    mixall = U[:, 0:4096].rearrange("p (g t) -> p g t", g=8)
    for tg in range(4):
        for ti in range(4):
            c0 = 1 + 128 * (4 * tg + ti)
            for hf, (P_, RP_) in enumerate((((PS_a, RPa), (PS_b, RPb)) if ti % 2 == 0 else ((PS_c, RPc), (PS_d, RPd)))):
                for k in range(8):
                    mm(P_[:], hnT[:, k, c0:c0 + 128], wbig[:, k, hf * 512:(hf + 1) * 512], [Rwbig] + allh, [RP_], st=(k == 0), sp=(k == 7))
                op("act", ("activation", dict(out=vsb[:, hf * 512:(hf + 1) * 512], in_=P_[:], func=AF.Copy)), reads=[RP_], writes=[Rvsb])
            op("act", ("activation", dict(out=sq[:], in_=vsb[:], func=AF.Square)), reads=[Rvsb], writes=[Rsq])
            op("dve", ("reduce_sum", dict(out=st[:, 0:1], in_=vsb[:], axis=AX.X)), reads=[Rvsb], writes=[Rst])
            op("dve", ("reduce_sum", dict(out=st[:, 1:2], in_=sq[:], axis=AX.X)), reads=[Rsq], writes=[Rst])
            op("dve", ("tensor_scalar", dict(out=st[:, 2:3], in0=st[:, 0:1], scalar1=1.0 / 1024, scalar2=None, op0=ALU.mult)), reads=[Rst], writes=[Rst])
            op("dve", ("tensor_tensor", dict(out=st[:, 3:4], in0=st[:, 2:3], in1=st[:, 2:3], op=ALU.mult)), reads=[Rst], writes=[Rst])
            op("dve", ("scalar_tensor_tensor", dict(out=st[:, 4:5], in0=st[:, 1:2], scalar=1.0 / 1024, in1=st[:, 3:4], op0=ALU.mult, op1=ALU.subtract)), reads=[Rst], writes=[Rst])
            op("dve", ("tensor_scalar", dict(out=st[:, 4:5], in0=st[:, 4:5], scalar1=1e-5, scalar2=None, op0=ALU.add)), reads=[Rst], writes=[Rst])
            op("act", ("activation", dict(out=st[:, 5:6], in_=st[:, 4:5], func=AF.Sqrt)), reads=[Rst], writes=[Rst])
            op("dve", ("reciprocal", dict(out=st[:, 6:7], in_=st[:, 5:6])), reads=[Rst], writes=[Rst])
            op("dve", ("tensor_scalar", dict(out=vsb[:], in0=vsb[:], scalar1=st[:, 2:3], scalar2=st[:, 6:7], op0=ALU.subtract, op1=ALU.mult)), reads=[Rvsb, Rst], writes=[Rvsb])
            op("dve", ("tensor_tensor", dict(out=vsb[:], in0=vsb[:], in1=lnG[:], op=ALU.mult)), reads=[Rvsb, Rbc], writes=[Rvsb])
            op("pool", ("tensor_tensor", dict(out=vnb[:], in0=vsb[:], in1=lnB[:], op=ALU.add)), reads=[Rvsb, Rbc], writes=[Rvnb])
            for g in range(8):
                mm(PS_m[:, g, :], vnb[:, g * 128:(g + 1) * 128], wsT[:, g, :], [Rvnb, Rws], [RPm])
            op("dve", ("tensor_tensor", dict(out=mixall[:, :, ti * 128:(ti + 1) * 128], in0=PS_m[:], in1=bsB[:], op=ALU.add)), reads=[RPm, Rbc], writes=[RU])
        c0 = 1 + 512 * tg
        for g in range(8):
            buf = g % 2

            (PU, RPU), (PZ, RPZ) = ((PS_c, RPc), (PS_d, RPd)) if g % 2 == 0 else ((PS_a, RPa), (PS_b, RPb))
            t2, Rt2 = t2s[g % 2], Rt2s[g % 2]
            t3, Rt3 = t3s[g % 2], Rt3s[g % 2]
            for k in range(8):
                mm(PU[:], wbig[:, 8 + k, g * 128:(g + 1) * 128], hnT[:, k, c0:c0 + 512], [Rwbig] + allh, [RPU], st=(k == 0), sp=(k == 7))
            for k in range(8):
                mm(PZ[:], wbf[g // 4][:, k, (g % 4) * 128:(g % 4 + 1) * 128], hnT[:, k, c0:c0 + 512], [Rwbf[g // 4]] + allh, [RPZ], st=(k == 0), sp=(k == 7))
            op("act", ("activation", dict(out=t3[:], in_=PZ[:], func=AF.Silu)), reads=[RPZ], writes=[Rt3])
            op("dve", ("tensor_tensor", dict(out=t2[:], in0=PU[:], in1=mixall[:, g, :], op=ALU.mult)), reads=[RPU, RU], writes=[Rt2])
            op("pool", ("tensor_tensor", dict(out=yo[g % 2][:], in0=t2[:], in1=t3[:], op=ALU.mult)), reads=[Rt2, Rt3], writes=[Ryo[g % 2]])
            S.dma(("dma_start", dict(out=yT[8 + g, :, 512 * tg:512 * (tg + 1)], in_=yo[g % 2][:])), reads=[Ryo[g % 2]], writes=[RyT[8 + g][tg]], q="pool")

    wo = I["e_w_out"].rearrange("(k p) n -> p k n", p=128)
    for q in range(2):
        for hf in range(4):
            S.dma(("dma_start", dict(out=stg[hf % 2][:], in_=wo[:, 8 * q:8 * q + 8, hf * 256:(hf + 1) * 256])), writes=[Rstg[hf % 2]])
            op("act", ("activation", dict(out=wbig[:, 8 * q:8 * q + 8, hf * 256:(hf + 1) * 256], in_=stg[hf % 2][:], func=AF.Copy)), reads=[Rstg[hf % 2]], writes=[Rwbig])
    ally = [r for row in RyT for r in row]
    h1ts = [sb("h1t%d" % i, [128, 1024]) for i in range(2)]; Rh1s = [R(), R()]
    for i in range(16):
        xb, Rx = xt[i % 2], Rxt[i % 2]
        h1t, Rh1 = h1ts[i % 2], Rh1s[i % 2]
        S.dma(("dma_start", dict(out=xb[:], in_=xh[1 + 128 * i: 1 + 128 * (i + 1), :])), writes=[Rx])
        S.dma(("dma_start", dict(out=ytl[i % 2][:], in_=yT[:, :, 128 * i:128 * (i + 1)].rearrange("k p t -> p k t"))), reads=ally, writes=[Rytl[i % 2]])
        for hf, (P_, RP_) in enumerate((((PS_a, RPa), (PS_b, RPb)) if i % 2 == 0 else ((PS_c, RPc), (PS_d, RPd)))):
            for k in range(16):
                mm(P_[:], ytl[i % 2][:, k, :], wbig[:, k, hf * 512:(hf + 1) * 512], [Rwbig, Rytl[i % 2]], [RP_], st=(k == 0), sp=(k == 15))
            op("dve", ("tensor_tensor", dict(out=h1t[:, hf * 512:(hf + 1) * 512], in0=P_[:], in1=xb[:, hf * 512:(hf + 1) * 512], op=ALU.add)), reads=[RP_, Rx], writes=[Rh1])
        S.dma(("dma_start", dict(out=O["h1"][128 * i:128 * (i + 1), :], in_=h1t[:])), reads=[Rh1], q="act")

        def dst(gt, i=i):
            op("dve", ("tensor_tensor", dict(out=hnT[:, :, 1 + 128 * i: 1 + 128 * (i + 1)], in0=PS_t[:], in1=gt[:].to_broadcast([128, 8, 128]), op=ALU.mult)), reads=[RPt, Rg], writes=[RhnT[i]])
        norm_tile(h1t, Rh1, gO, dst, None)

    wi1 = I["o_w_in"].rearrange("(k p) n -> p k n", p=128)
    blocks = [(c * 128, c * 128, False) for c in range(25)]
    blocks += [(3200 + c * 128, 3200 + c * 128, True) for c in range(8)]
    blocks += [(4736 + c * 128, 4224 + c * 128, True) for c in range(4)]
    ob = [sb("ob%d" % i, [128, 512]) for i in range(2)]; Rob = [R(), R()]
    oi = 0
    toks = []
    bi = 0
    nblocks = len(blocks)
    pairbuf = 0
    while bi < nblocks:
        sc, dr, act = blocks[bi]
        paired = (bi + 1 < nblocks) and (blocks[bi + 1][0] == sc + 128) and (blocks[bi + 1][2] == act)
        ncol = 256 if paired else 128
        buf = pairbuf % 2
        pairbuf += 1
        S.dma(("dma_start", dict(out=stg[buf][:, :, 0:ncol], in_=wi1[:, :, sc:sc + ncol])), writes=[Rstg[buf]])
        op("dve", ("tensor_copy", dict(out=wbf[buf][:, :, 0:ncol], in_=stg[buf][:, :, 0:ncol])), reads=[Rstg[buf]], writes=[Rwbf[buf]])
        for sub in range(2 if paired else 1):
            sc_, dr_, act_ = blocks[bi + sub]
            for j in range(4):
                P_, RP_ = ((PS_a, RPa), (PS_b, RPb), (PS_c, RPc), (PS_d, RPd))[j]
                c0 = 1 + 512 * j
                for k in range(8):
                    mm(P_[:], wbf[buf][:, k, sub * 128:(sub + 1) * 128], hnT[:, k, c0:c0 + 512], [Rwbf[buf]] + allh, [RP_], st=(k == 0), sp=(k == 7))
                o_, Ro = ob[oi % 2], Rob[oi % 2]
                oi += 1
                op("act", ("activation", dict(out=o_[:], in_=P_[:], func=(AF.Silu if act_ else AF.Copy))), reads=[RP_], writes=[Ro])
                toks.append(S.dma(("dma_start", dict(out=O["pT"][dr_:dr_ + 128, 512 * j:512 * (j + 1)], in_=o_[:])), reads=[Ro], q="act"))
        bi += 2 if paired else 1
    for hf in range(2):
        S.dma(("dma_start", dict(out=stg[hf][:], in_=wi1[:, :, 4224 + 256 * hf:4224 + 256 * (hf + 1)])), writes=[Rstg[hf]])
        op("act", ("activation", dict(out=wbf[0][:, :, 256 * hf:256 * (hf + 1)], in_=stg[hf][:], func=AF.Copy)), reads=[Rstg[hf]], writes=[Rwbf[0]])
    for i in range(16):
        c0 = 1 + 128 * i
        P_, RP_ = ((PS_a, RPa), (PS_b, RPb))[i % 2]
        for k in range(8):
            mm(P_[:], hnT[:, k, c0:c0 + 128], wbf[0][:, k, :], [Rwbf[0]] + allh, [RP_], st=(k == 0), sp=(k == 7))
        o_, Ro = ob[oi % 2], Rob[oi % 2]
        oi += 1
        op("act", ("activation", dict(out=o_[:], in_=P_[:], func=AF.Copy)), reads=[RP_], writes=[Ro])
        toks.append(S.dma(("dma_start", dict(out=O["fd"][128 * i:128 * (i + 1), :], in_=o_[:])), reads=[Ro], q="act"))
    return toks


def full_barrier(S):
    keys = list(S.cnt.items())
    for e in S.ENGS:
        waits = []
        for k, v in keys:
            if k == e:
                continue
            if S.seen[e].get(k, 0) < v:
                S.seen[e][k] = v
                waits.append((k, v))
        if waits:
            S.prog[e].append([waits, None, ("_none", 0)])


def emit_fnet(S, nc, I, ydT):
    R = Region
    op = S.op
    mm = lambda out, l, r_, rd, wr, st=True, sp=True: op("pe", ("matmul", dict(out=out, lhsT=l, rhs=r_, start=st, stop=sp)), reads=rd, writes=wr)
    toks = []
    with ExitStack() as es:
        sb = lambda name, shape, dt=F32: es.enter_context(nc.sbuf_tensor(name, shape, dt))
        ps = lambda name, shape, dt=F32: es.enter_context(nc.psum_tensor(name, shape, dt))
        xs = sb("f_xs", [128, 4096]); Rxs = R()
        xb = sb("f_xb", [128, 64, 128], BF16); Rxb = R()
        Fb = sb("f_F", [128, 256], BF16); RF = R()
        A_sb = sb("f_A", [64, 128, 256], BF16); RA = R()
        PQ = sb("f_PQ", [128, 2, 64, 128], BF16); RPQ = R()
        Tg = [[sb("f_T%d%d" % (i, j), [64, 16, 128], BF16) for j in range(2)] for i in range(2)]; RTg = [R(), R()]
        wf32 = sb("f_w32", [128, 128]); wfb = sb("f_wb", [128, 128], BF16); Rwf = R()
        Ccb = sb("f_Cc", [128, 128], BF16); mScb = sb("f_mSc", [128, 128], BF16); Rcs = R()
        Gb = sb("f_G", [128, 256], BF16); RG = R()
        ob = [sb("f_ob%d" % i, [128, 512]) for i in range(2)]; Rob = [R(), R()]
        PS = [ps("f_ps%d" % i, [128, 512]) for i in range(2)]; RPS = [R(), R()]
        S.dma(("dma_start", dict(out=Fb[:], in_=I["c_F"])), writes=[RF])
        S.dma(("dma_start", dict(out=Ccb[:], in_=I["c_Cc"])), writes=[Rcs])
        S.dma(("dma_start", dict(out=mScb[:], in_=I["c_mSc"])), writes=[Rcs])
        S.dma(("dma_start", dict(out=wf32[:], in_=I["fw"])), writes=[Rwf])
        op("dve", ("tensor_copy", dict(out=wfb[:], in_=wf32[:])), reads=[Rwf], writes=[Rwf])
        xbf = xb[:].rearrange("p l c -> p (l c)")
        for hf in range(2):
            S.dma(("dma_start", dict(out=xs[:], in_=I["fx"][:, hf * 4096:(hf + 1) * 4096])), writes=[Rxs])
            op("act", ("activation", dict(out=xbf[:, hf * 4096:(hf + 1) * 4096], in_=xs[:], func=AF.Copy)), reads=[Rxs], writes=[Rxb])
        for c2 in range(64):
            P_, RP_ = PS[c2 % 2], RPS[c2 % 2]
            for j in range(2):
                mm(P_[0:64, j * 256:(j + 1) * 256], xb[:, :, 2 * c2 + j], Fb[:], [Rxb, RF], [RP_])
            op("act" if c2 % 2 == 0 else "dve", ("activation", dict(out=A_sb[0:64, 2 * c2:2 * c2 + 2, :], in_=P_[0:64, :].rearrange("p (j k) -> p j k", j=2), func=AF.Copy)) if c2 % 2 == 0 else
               ("tensor_copy", dict(out=A_sb[0:64, 2 * c2:2 * c2 + 2, :], in_=P_[0:64, :].rearrange("p (j k) -> p j k", j=2))), reads=[RP_], writes=[RA])
        T1d = I["c_T1"].rearrange("p (k h) -> p k h", h=128)
        T2d = I["c_T2"].rearrange("p (k h) -> p k h", h=128)
        ei = 0
        for grp in range(8):
            tb = grp % 2
            S.dma(("dma_start", dict(out=Tg[tb][0][:], in_=T1d[:, grp * 16:(grp + 1) * 16, :])), writes=[RTg[tb]])
            S.dma(("dma_start", dict(out=Tg[tb][1][:], in_=T2d[:, grp * 16:(grp + 1) * 16, :])), writes=[RTg[tb]])
            for q in range(4):
                P_, RP_ = PS[ei % 2], RPS[ei % 2]
                for j in range(4):
                    kk_ = q * 4 + j
                    kl = grp * 16 + kk_
                    mm(P_[:, j * 128:(j + 1) * 128], A_sb[0:64, :, kl], Tg[tb][0][0:64, kk_, :], [RA, RTg[tb]], [RP_], st=True, sp=False)
                    mm(P_[:, j * 128:(j + 1) * 128], A_sb[0:64, :, 128 + kl], Tg[tb][1][0:64, kk_, :], [RA, RTg[tb]], [RP_], st=False, sp=True)
                kl0 = grp * 16 + q * 4
                for qq in range(2):
                    op("act" if qq == 0 else "dve",
                       ("activation", dict(out=PQ[:, qq, :, kl0:kl0 + 4].rearrange("p h l -> p l h"), in_=P_[:].rearrange("p (l q h) -> p l q h", l=4, q=2)[:, :, qq, :], func=AF.Copy)) if qq == 0 else
                       ("tensor_copy", dict(out=PQ[:, qq, :, kl0:kl0 + 4].rearrange("p h l -> p l h"), in_=P_[:].rearrange("p (l q h) -> p l q h", l=4, q=2)[:, :, qq, :])),
                       reads=[RP_], writes=[RPQ])
                ei += 1
        P_, RP_ = PS[0], RPS[0]
        mm(P_[:, 0:128], Ccb[:], wfb[:], [Rcs, Rwf], [RP_])
        mm(P_[:, 128:256], mScb[:], wfb[:], [Rcs, Rwf], [RP_])
        op("act", ("activation", dict(out=Gb[:], in_=P_[:, 0:256], func=AF.Copy)), reads=[RP_], writes=[RG])
        for t4 in range(16):
            P_, RP_ = PS[(t4 + 1) % 2], RPS[(t4 + 1) % 2]
            for j in range(4):
                kh = 4 * t4 + j
                mm(P_[:, j * 128:(j + 1) * 128], Gb[:, 0:128], PQ[:, 0, kh, :], [RG, RPQ], [RP_], st=True, sp=False)
                mm(P_[:, j * 128:(j + 1) * 128], Gb[:, 128:256], PQ[:, 1, kh, :], [RG, RPQ], [RP_], st=False, sp=True)
            o_, Ro = ob[t4 % 2], Rob[t4 % 2]
            op("act", ("activation", dict(out=o_[:], in_=P_[:], func=AF.Copy)), reads=[RP_], writes=[Ro])
            toks.append(S.dma(("dma_start", dict(out=ydT[:, 512 * t4:512 * (t4 + 1)], in_=o_[:])), reads=[Ro]))
    full_barrier(S)
    return toks


def emit_p3(S, nc, I, yout):
    sb = lambda name, shape, dt=F32: nc.alloc_sbuf_tensor(name, shape, dt)
    ps = lambda name, shape, dt=F32: nc.alloc_psum_tensor(name, shape, dt)
    R = Region
    op = S.op
    mm = lambda out, l, r_, rd, wr, st=True, sp=True: op("pe", ("matmul", dict(out=out, lhsT=l, rhs=r_, start=st, stop=sp)), reads=rd, writes=wr)
    stg = [sb("stg%d" % i, [128, 8, 256]) for i in range(2)]; Rstg = [R(), R()]
    wO = sb("wO", [128, 12, 1024], BF16); RwO = R()
    gN = sb("gN", [128, 1024]); RgN = R()
    gt_all = sb("gt_all", [128, 12, 2048], BF16); Rgt = R()
    ya = [sb("ya%d" % i, [128, 512]) for i in range(2)]; Rya = [R(), R()]
    ga = [sb("ga%d" % i, [128, 512]) for i in range(2)]; Rga = [R(), R()]
    h1t = [sb("h1t%d" % i, [128, 1024]) for i in range(2)]; Rh1 = [R(), R()]
    h2 = sb("h2", [128, 1024]); Rh2 = R()
    sq = sb("sq", [128, 1024]); Rsq = R()
    st = sb("st", [128, 8]); Rst = R()
    yo = [sb("yo%d" % i, [128, 1024]) for i in range(2)]; Ryo = [R(), R()]
    PS_a = ps("PS_a", [128, 512]); RPa = R()
    PS_b = ps("PS_b", [128, 512]); RPb = R()
    wo3 = I["o_w_out"].rearrange("(k p) n -> p k n", p=128)
    si = 0
    for (k0, nk) in ((0, 8), (8, 4)):
        for cq in range(4):
            b_ = si % 2; si += 1
            S.dma(("dma_start", dict(out=stg[b_][:, 0:nk, :], in_=wo3[:, k0:k0 + nk, cq * 256:(cq + 1) * 256])), writes=[Rstg[b_]])
            op("act", ("activation", dict(out=wO[:, k0:k0 + nk, cq * 256:(cq + 1) * 256], in_=stg[b_][:, 0:nk, :], func=AF.Copy)), reads=[Rstg[b_]], writes=[RwO])
    S.dma(("dma_start", dict(out=gN[:], in_=I["final_norm_g"].partition_broadcast(128))), writes=[RgN])
    ii = 0
    for blk in range(12):
        src = I["ycT"][blk * 128:(blk + 1) * 128] if blk < 8 else I["ydT"][(blk - 8) * 128:(blk - 7) * 128]
        gsrc = I["gT"][blk * 128:(blk + 1) * 128]
        for j in range(4):
            b_ = ii % 2; ii += 1
            S.dma(("dma_start", dict(out=ya[b_][:], in_=src[:, 512 * j:512 * (j + 1)])), writes=[Rya[b_]])
            S.dma(("dma_start", dict(out=ga[b_][:], in_=gsrc[:, 512 * j:512 * (j + 1)])), writes=[Rga[b_]])
            op("dve", ("tensor_tensor", dict(out=gt_all[:, blk, 512 * j:512 * (j + 1)], in0=ya[b_][:], in1=ga[b_][:], op=ALU.mult)), reads=[Rya[b_], Rga[b_]], writes=[Rgt])
    toks = []
    for i in range(16):
        hb, Rh = h1t[i % 2], Rh1[i % 2]
        S.dma(("dma_start", dict(out=hb[:], in_=I["h1"][128 * i:128 * (i + 1), :])), writes=[Rh])
        for hf, (P_, RP_) in enumerate(((PS_a, RPa), (PS_b, RPb))):
            for k in range(12):
                mm(P_[:], gt_all[:, k, 128 * i:128 * (i + 1)], wO[:, k, hf * 512:(hf + 1) * 512], [Rgt, RwO], [RP_], st=(k == 0), sp=(k == 11))
            op("dve", ("tensor_tensor", dict(out=h2[:, hf * 512:(hf + 1) * 512], in0=P_[:], in1=hb[:, hf * 512:(hf + 1) * 512], op=ALU.add)), reads=[RP_, Rh], writes=[Rh2])
        op("act", ("activation", dict(out=sq[:], in_=h2[:], func=AF.Square)), reads=[Rh2], writes=[Rsq])
        op("dve", ("reduce_sum", dict(out=st[:, 0:1], in_=sq[:], axis=AX.X)), reads=[Rsq], writes=[Rst])
        op("dve", ("tensor_scalar", dict(out=st[:, 1:2], in0=st[:, 0:1], scalar1=1.0 / 1024, scalar2=1e-6, op0=ALU.mult, op1=ALU.add)), reads=[Rst], writes=[Rst])
        op("act", ("activation", dict(out=st[:, 2:3], in_=st[:, 1:2], func=AF.Sqrt)), reads=[Rst], writes=[Rst])
        op("dve", ("reciprocal", dict(out=st[:, 3:4], in_=st[:, 2:3])), reads=[Rst], writes=[Rst])
        op("dve", ("tensor_scalar", dict(out=h2[:], in0=h2[:], scalar1=st[:, 3:4], scalar2=None, op0=ALU.mult)), reads=[Rh2, Rst], writes=[Rh2])
        o_, Ro = yo[i % 2], Ryo[i % 2]
        op("dve", ("tensor_tensor", dict(out=o_[:], in0=h2[:], in1=gN[:], op=ALU.mult)), reads=[Rh2, RgN], writes=[Ro])
        toks.append(S.dma(("dma_start", dict(out=yout[128 * i:128 * (i + 1), :], in_=o_[:])), reads=[Ro], q="pool"))
    return toks


def _mk(nc, name, shape, dt=None, out=False):
    return nc.dram_tensor(name, list(shape), dt or F32, kind=("ExternalOutput" if out else "ExternalInput")).ap()


W1 = ["e_norm_g", "e_w_in", "e_conv_w", "e_sgu_ln_g", "e_sgu_ln_b", "e_sgu_w", "e_sgu_b", "e_w_out", "o_norm_g", "o_w_in"]


def build_l1(shapes):
    nc = bass.Bass("TRN2", target_bir_lowering=False)
    I = {"xh": _mk(nc, "xh", [2050, 1024]), "c_ident": _mk(nc, "c_ident", [128, 128])}
    for n in W1:
        I[n] = _mk(nc, n, shapes[n])
    O = {"h1": _mk(nc, "h1", [2048, 1024], out=True), "pT": _mk(nc, "pT", [4736, 2048], out=True),
         "fd": _mk(nc, "fd", [2048, 512], out=True)}
    S = Sched(nc)
    toks = emit_p1(S, nc, I, O)
    S.barrier_on("sp", toks)
    S.finalize()
    return nc


def build_l2(consts):
    NB, T = 2, 8192
    nc = bass.Bass("TRN2", target_bir_lowering=False)
    pr, pk, pv, pwa = (_mk(nc, n, [128, NB, T + 2]) for n in ("pr", "pk", "pv", "pwa"))
    prm = _mk(nc, "prm", [128, 17]); w2a2 = _mk(nc, "w2a2", [128, 2, 128])
    A = {k: _mk(nc, k, v.shape) for k, v in consts.items()}
    FI = {"fx": _mk(nc, "fx", [128, 8192]), "fw": _mk(nc, "fw", [128, 128]),
          "c_F": _mk(nc, "c_F", [128, 256], BF16), "c_T1": _mk(nc, "c_T1", [64, 16384], BF16),
          "c_T2": _mk(nc, "c_T2", [64, 16384], BF16), "c_Cc": _mk(nc, "c_Cc", [128, 128], BF16),
          "c_mSc": _mk(nc, "c_mSc", [128, 128], BF16)}
    yout = _mk(nc, "yout", [128, NB, T], out=True)
    ydT = _mk(nc, "ydT", [128, T], out=True)
    S = Sched(nc)
    toks = emit_fnet(S, nc, FI, ydT)
    toks += emit_rwkv(S, nc, A, pr, pk, pv, pwa, prm, w2a2, yout, NB, T)
    S.barrier_on("sp", toks)
    S.finalize()
    return nc


def build_l3():
    nc = bass.Bass("TRN2", target_bir_lowering=False)
    I = {"ycT": _mk(nc, "ycT", [1024, 2048]), "ydT": _mk(nc, "ydT", [512, 2048]), "gT": _mk(nc, "gT", [1536, 2048]),
         "h1": _mk(nc, "h1", [2048, 1024]), "o_w_out": _mk(nc, "o_w_out", [1536, 1024]),
         "final_norm_g": _mk(nc, "final_norm_g", [1024])}
    y = _mk(nc, "y", [2048, 1024], out=True)
    S = Sched(nc)
    toks = emit_p3(S, nc, I, y)
    S.barrier_on("sp", toks)
    S.finalize()
    return nc


def fnet_tables():
    import ml_dtypes
    N = 8192
    nh = np.arange(128); kl = np.arange(128)
    ang = 2 * np.pi * np.outer(nh, kl) / 128
    F = np.concatenate([np.cos(ang), np.sin(ang)], axis=1)
    nl = np.arange(64)[:, None, None]; klo = np.arange(128)[None, :, None]; kh = np.arange(64)[None, None, :]
    beta = 2 * np.pi * ((nl * (klo + 128 * kh)) % N) / N
    T1 = np.concatenate([np.cos(beta), np.sin(beta)], axis=2).reshape(64, 16384)
    T2 = np.concatenate([-np.sin(beta), np.cos(beta)], axis=2).reshape(64, 16384)
    c = np.arange(128); phi = 2 * np.pi * np.outer(c, c) / 128
    nrm = 1 / np.sqrt(N * 128)
    bf = lambda a: np.ascontiguousarray(a.astype(np.float32)).astype(ml_dtypes.bfloat16)
    return {"c_F": bf(F), "c_T1": bf(T1), "c_T2": bf(T2), "c_Cc": bf(np.cos(phi) * nrm), "c_mSc": bf(-np.sin(phi) * nrm)}


def kernel(**inputs):
    f32 = lambda a: np.ascontiguousarray(np.asarray(a), dtype=np.float32)
    inp = {k: f32(v) for k, v in inputs.items()}
    x = inp["x"]
    ncores = 8
    cores = list(range(ncores))
    w1 = {n: np.ascontiguousarray(inp[n][0]) for n in W1}
    ident = np.eye(128, dtype=np.float32)
    maps = []
    for c in cores:
        b, s0 = c // 4, (c % 4) * 2048
        xh = np.zeros((2050, 1024), np.float32)
        xh[1:2049] = x[b, s0:s0 + 2048]
        if s0 > 0:
            xh[0] = x[b, s0 - 1]
        if s0 + 2048 < 8192:
            xh[2049] = x[b, s0 + 2048]
        m = {"xh": xh, "c_ident": ident}
        m.update(w1)
        maps.append(m)
    nc1 = build_l1({n: w1[n].shape for n in W1})
    r1 = run_bass_kernel_spmd(nc1, maps, core_ids=cores).results
    PT = np.concatenate([np.asarray(r["pT"]) for r in r1], axis=1)
    FD = np.concatenate([np.asarray(r["fd"]) for r in r1], axis=0)
    consts = build_consts_np()
    ft = fnet_tables()
    mu, w0, w2, a0, a2 = inp["o_mu"][0], inp["o_w0"][0], inp["o_w2"][0], inp["o_a0"][0], inp["o_a2"][0]
    k_k, k_a, r_k = inp["o_k_k"][0], inp["o_k_a"][0], inp["o_r_k"][0].reshape(-1)
    lg, lb = inp["o_lnx_g"][0], inp["o_lnx_b"][0]
    PT3 = PT.reshape(4736, 2, 8192)
    pad = lambda a: np.ascontiguousarray(np.pad(a, ((0, 0), (0, 0), (1, 1))))
    maps = []
    for c in cores:
        ch = slice(c * 128, (c + 1) * 128)
        m = {"pr": pad(PT3[0:1024][ch]), "pk": pad(PT3[1024:2048][ch]), "pv": pad(PT3[2048:3072][ch]),
             "pwa": pad(PT3[3072:3200])}
        prm = np.zeros((128, 17), np.float32)
        for d in range(2):
            prm[:, 0 + d] = mu[d, 0:1024][ch]; prm[:, 2 + d] = mu[d, 1024:2048][ch]; prm[:, 4 + d] = mu[d, 2048:3072][ch]
            prm[:, 6 + d] = mu[d, 3072:3200]; prm[:, 8 + d] = w0[d][ch]; prm[:, 10 + d] = a0[d][ch]
        prm[:, 12] = k_k[ch]; prm[:, 13] = k_a[ch]; prm[:, 14] = r_k[ch]; prm[:, 15] = lg[ch]; prm[:, 16] = lb[ch]
        m["prm"] = prm
        m["w2a2"] = np.ascontiguousarray(np.concatenate([w2[:, :, ch], a2[:, :, ch]], axis=1).transpose(1, 0, 2))
        m.update(consts)
        b, g = c // 4, c % 4
        m["fx"] = np.ascontiguousarray(FD[b * 8192:(b + 1) * 8192, g * 128:(g + 1) * 128]).reshape(128, 8192)
        m["fw"] = np.ascontiguousarray(inp["o_fnet_w"][0, g])
        m.update(ft)
        maps.append(m)
    nc2 = build_l2(consts)
    r2 = run_bass_kernel_spmd(nc2, maps, core_ids=cores).results
    YC = np.concatenate([np.asarray(r["yout"]).reshape(128, 16384) for r in r2], axis=0)
    YD = np.concatenate([np.concatenate([np.asarray(r2[b * 4 + g]["ydT"]) for g in range(4)], axis=0) for b in range(2)], axis=1)
    maps = []
    for c in cores:
        ts = slice(c * 2048, (c + 1) * 2048)
        maps.append({"ycT": np.ascontiguousarray(YC[:, ts]), "ydT": np.ascontiguousarray(YD[:, ts]),
                     "gT": np.ascontiguousarray(PT[3200:4736, ts]), "h1": np.asarray(r1[c]["h1"]),
                     "o_w_out": np.ascontiguousarray(inp["o_w_out"][0]), "final_norm_g": inp["final_norm_g"]})
    nc3 = build_l3()
    r3 = run_bass_kernel_spmd(nc3, maps, core_ids=cores).results
    y = np.concatenate([np.asarray(r["y"]) for r in r3], axis=0).reshape(2, 8192, 1024)
    return y.astype(np.float32)
```

```python
from contextlib import ExitStack
import itertools
import numpy as np
import concourse.bass as bass
import concourse.mybir as mybir
from concourse.bass_utils import run_bass_kernel_spmd


F32 = mybir.dt.float32
BF16 = mybir.dt.bfloat16
AF = mybir.ActivationFunctionType
ALU = mybir.AluOpType
AX = mybir.AxisListType

N_DMA_SEMS = 8


class Region:
    __slots__ = ("w", "r", "name")

    def __init__(self, name=""):
        self.w = None
        self.r = {}
        self.name = name


class Sched:
    ENGS = ("pe", "dve", "act", "pool", "sp")

    def __init__(self, nc):
        self.nc = nc
        self.prog = {e: [] for e in self.ENGS}
        self.cnt = {}
        self.seen = {e: {} for e in self.ENGS}
        self.dma_rr = {e: 0 for e in self.ENGS}
        self.dma_last = {}
        self.same_engine_raw = True
        self.cut = 0
        self.raw_only = True
        self.nrec = 0
        self.log = []

    def _collect(self, eng, mykey, reads, writes):
        waits = {}

        def need(tok, kind):
            if tok is None:
                return
            k, v = tok
            if k == mykey:
                if eng == "pe":
                    return
                if not self.same_engine_raw:
                    return
                if self.raw_only and kind != "raw":
                    return
            if waits.get(k, 0) < v:
                waits[k] = v

        for R in reads:
            need(R.w, "raw")
        for R in writes:
            need(R.w, "waw")
            for k, v in R.r.items():
                need((k, v), "war")
        out = []
        seen = self.seen[eng]
        for k, v in waits.items():
            if seen.get(k, 0) < v:
                seen[k] = v
                out.append((k, v))
        return out

    def _commit(self, tok, reads, writes):
        for R in writes:
            R.w = tok
            R.r = {}
        k, v = tok
        for R in reads:
            if R.r.get(k, 0) < v:
                R.r[k] = v

    def op(self, eng, fn, reads=(), writes=()):
        self.nrec += 1
        if self.cut and self.nrec > self.cut:
            return None
        if self.cut:
            self.log.append((self.nrec, eng, fn[0] if isinstance(fn, tuple) else "fn", str(fn[1].get("out", ""))[:120] if isinstance(fn, tuple) else ""))
        key = eng
        waits = self._collect(eng, key, reads, writes)
        idx = self.cnt.get(key, 0) + 1
        self.cnt[key] = idx
        tok = (key, idx)
        self.prog[eng].append([waits, fn, tok])
        self._commit(tok, reads, writes)
        return tok

    def dma(self, fn, reads=(), writes=(), q="sp"):
        self.nrec += 1
        if self.cut and self.nrec > self.cut:
            return None
        i = self.dma_rr[q]
        self.dma_rr[q] = (i + 1) % N_DMA_SEMS
        key = "dma_%s_%d" % (q, i)
        waits = self._collect(q, key, reads, writes)
        prev = self.cnt.get(key, 0)
        if prev > 0 and self.seen[q].get(key, 0) < prev:
            self.seen[q][key] = prev
            waits.append((key, prev))
        idx = prev + 1
        self.cnt[key] = idx
        tok = (key, idx)
        self.prog[q].append([waits, fn, tok])
        self._commit(tok, reads, writes)
        return tok

    def finalize(self):
        nc = self.nc
        waited = {}
        for e in self.ENGS:
            for waits, fn, tok in self.prog[e]:
                for k, v in waits:
                    waited.setdefault(k, set()).add(v)
        self.final_waits = []
        sem_of = {}
        val_of = {}
        for k, s in waited.items():
            sem_of[k] = nc.alloc_semaphore("s_" + k)
            isdma = k.startswith("dma_")
            step = 16 if isdma else 1
            if isdma:
                val_of[k] = None
            else:
                val_of[k] = {v: (i + 1) for i, v in enumerate(sorted(s))}
        engobj = {"pe": nc.tensor, "dve": nc.vector, "act": nc.scalar,
                  "pool": nc.gpsimd, "sp": nc.sync}

        def value(k, v):
            if val_of[k] is None:
                return 16 * v
            return val_of[k][v]

        def emit(e):
            def body(eng):
                for waits, fn, tok in self.prog[e]:
                    for k, v in waits:
                        eng.wait_ge(sem_of[k], value(k, v))
                    if fn is None:
                        continue
                    if isinstance(fn, tuple):
                        ins = getattr(eng, fn[0])(**fn[1])
                    else:
                        ins = fn(eng)
                    k, v = tok
                    if k in sem_of:
                        if val_of[k] is None:
                            ins.then_inc(sem_of[k], 16)
                        elif v in val_of[k]:
                            ins.then_inc(sem_of[k], 1)
            return body

        with nc.Block() as block:
            for e, dec in (("sp", block.sync), ("pe", block.tensor), ("dve", block.vector),
                           ("act", block.scalar), ("pool", block.gpsimd)):
                if self.prog[e]:
                    dec(emit(e))
        self.n_sems = len(sem_of)
        return self.n_sems

    def barrier_on(self, eng, toks):
        waits = []
        for tk in toks:
            if tk is None:
                continue
            k, v = tk
            if self.seen[eng].get(k, 0) < v:
                self.seen[eng][k] = v
                waits.append((k, v))
        if waits:
            self.prog[eng].append([waits, None, ("_none", 0)])


C = 128
BLK = 512
NEG_E = -float(np.exp(-0.5))
GN_EPS = 64e-5


def build_consts_np():
    idx = np.arange(128)
    lt = (idx[:, None] < idx[None, :]).astype(np.float32)
    le = (idx[:, None] <= idx[None, :]).astype(np.float32)
    gt = lt.T.copy()
    ge = le.T.copy()
    m4f = np.stack([lt, gt, gt, le], axis=1)
    m4b = np.stack([gt, lt, lt, ge], axis=1)
    mk = np.stack([le, ge], axis=1)
    ident = np.eye(128, dtype=np.float32)
    bd = np.kron(np.eye(2, dtype=np.float32), np.ones((64, 64), np.float32))
    scanm = np.ones((128, BLK), np.float32)
    scanm[:, ::C] = 0.0
    return {"c_m4": np.stack([m4f, m4b], axis=1).reshape(128, 2 * 4 * 128).copy(),
            "c_mk": mk.reshape(128, 256).copy(), "c_ident": ident, "c_bd": bd, "c_scanm": scanm}


XST = False


def emit_rwkv(S, nc, A, pr, pk, pv, pwa, prm, w2a2, yout, NB, T):
    sb = lambda name, shape, dt=F32: nc.alloc_sbuf_tensor(name, shape, dt)
    ps = lambda name, shape, dt=F32: nc.alloc_psum_tensor(name, shape, dt)
    R = Region
    nblk = T // BLK

    m4f = sb("m4f", [128, 2, 4, 128]); Rm4 = R()
    mkf = sb("mkf", [128, 2, 128]); Rmk = R()
    identf = sb("identf", [128, 128]); Ridf = R()
    identb = sb("identb", [128, 128], BF16); Ridb = R()
    bdf = sb("bdf", [128, 128]); Rbd = R()
    bdr = sb("bdr", [128, 128]); Rbdr = R()
    bdm = sb("bdm", [128, 128]); Rbdm = R()
    scanm = sb("scanm", [128, BLK]); Rsc = R()
    prmt = sb("prmt", [128, 17]); Rprm = R()
    w2f = sb("w2f", [128, 2, 128]); Rw2f = R()
    w2b = sb("w2b", [128, 2, 128], BF16); Rw2b = R()
    S.dma(("dma_start", dict(out=m4f[:].rearrange("p a b c -> p (a b c)"), in_=A["c_m4"])), writes=[Rm4])
    S.dma(("dma_start", dict(out=mkf[:].rearrange("p a c -> p (a c)"), in_=A["c_mk"])), writes=[Rmk])
    S.dma(("dma_start", dict(out=identf[:], in_=A["c_ident"])), writes=[Ridf])
    S.dma(("dma_start", dict(out=bdf[:], in_=A["c_bd"])), writes=[Rbd])
    S.dma(("dma_start", dict(out=scanm[:], in_=A["c_scanm"])), writes=[Rsc])
    S.dma(("dma_start", dict(out=prmt[:], in_=prm)), writes=[Rprm])
    S.dma(("dma_start", dict(out=w2f[:], in_=w2a2)), writes=[Rw2f])
    S.op("dve", ("tensor_copy", dict(out=identb[:], in_=identf[:])), reads=[Ridf], writes=[Ridb])
    S.op("dve", ("tensor_copy", dict(out=w2b[:], in_=w2f[:])), reads=[Rw2f], writes=[Rw2b])
    PM = lambda c: prmt[:, c:c + 1]
    S.op("dve", ("tensor_scalar", dict(out=bdr[:], in0=bdf[:], scalar1=PM(14), scalar2=None, op0=ALU.mult)), reads=[Rbd, Rprm], writes=[Rbdr])
    S.op("dve", ("tensor_scalar", dict(out=bdm[:], in0=bdf[:], scalar1=1.0 / 64, scalar2=None, op0=ALU.mult)), reads=[Rbd], writes=[Rbdm])

    def T2(name, dt=F32, n=BLK):
        return sb(name, [128, n], dt), R()
    ld = {}
    for nm in ("pr", "pk", "pv", "pwa"):
        ld[nm] = (sb("ld_" + nm, [128, BLK + 2]), R())
    tmp, Rtmp = T2("tmp")
    qr, Rqr = T2("qr"); qk, Rqk = T2("qk"); qv, Rqv = T2("qv"); qwa, Rqwa = T2("qwa")
    twa, Rtwa = T2("twa", BF16)
    sw, Rsw = T2("sw"); asg, Rasg = T2("asg")
    logw, Rlogw = T2("logw"); lin, Rlin = T2("lin"); linm, Rlinm = T2("linm"); lexm, Rlexm = T2("lexm")
    lex, Rlex = T2("lex"); lint, Rlint = T2("lint")
    e1, Re1 = T2("e1"); e1x, Re1x = T2("e1x"); e2, Re2 = T2("e2"); e3S = [sb("e3%d" % i, [128, BLK]) for i in range(2)]; Re3S = [R(), R()]; e3x, Re3x = T2("e3x"); e4, Re4 = T2("e4")
    kk, Rkk = T2("kk"); kk2, Rkk2 = T2("kk2"); rin, Rrin = T2("rin"); kkn, Rkkn = T2("kkn")
    kp, Rkp = T2("kp"); bv, Rbv = T2("bv"); rk, Rrk = T2("rk")
    rtS = [sb("rt%d" % i, [128, BLK], BF16) for i in range(2)]; RrtS = [R(), R()]; atS = [sb("at%d" % i, [128, BLK], BF16) for i in range(2)]; RatS = [R(), R()]; ktS = [sb("kt%d" % i, [128, BLK], BF16) for i in range(2)]; RktS = [R(), R()]; btS = [sb("bt%d" % i, [128, BLK], BF16) for i in range(2)]; RbtS = [R(), R()]
    r0S = [sb("r0%d" % i, [128, BLK]) for i in range(2)]; Rr0S = [R(), R()]; a0bS = [sb("a0b%d" % i, [128, BLK], BF16) for i in range(2)]; Ra0bS = [R(), R()]; kEbS = [sb("kEb%d" % i, [128, BLK], BF16) for i in range(2)]; RkEbS = [R(), R()]; bEbS = [sb("bEb%d" % i, [128, BLK], BF16) for i in range(2)]; RbEbS = [R(), R()]
    qvbS = [sb("qvb%d" % i, [128, BLK], BF16) for i in range(2)]; RqvbS = [R(), R()]
    ysum = sb("ysum", [128, T]); Rys = [R() for _ in range(T // C)]
    bsum = sb("bsum", [128, T]); Rbs = [R() for _ in range(nblk)]
    TT = [sb("TT%d" % i, [128, 4, 128], BF16) for i in range(2)]; RTT = [R(), R()]
    SBM = [sb("SBM%d" % i, [128, 2, 4, 128], BF16) for i in range(2)]; RSBM = [R(), R()]
    MKR = [sb("MKR%d" % i, [128, 2, 128]) for i in range(2)]; RMKR = [R(), R()]
    SX = [sb("SX%d" % i, [128, 2, 192], BF16) for i in range(2)]; RSX = [R(), R()]
    SAB = [sb("SAB%d" % i, [128, 2, 2, 128], BF16) for i in range(2)]; RSAB = [R(), R()]
    Gb = sb("Gb", [128, 128], BF16); RGb = R()
    Hb = sb("Hb", [128, 2, 128], BF16); RHb = R()
    Pb = sb("Pb", [128, 64], BF16); RPb = R()
    Zb = sb("Zb", [128, 2, 64], BF16); RZb = R()
    STz = [sb("STz%d" % h, [128, 64], BF16) for h in range(2)]; RST = [R(), R()]
    identP = sb("identP", [128, 64]); mkb = sb("mkb", [128, 2, 2, 128])
    HS = [slice(0, 64), slice(64, 128)]
    fin1, Rfin1 = T2("fin1"); fin2, Rfin2 = T2("fin2"); fin3, Rfin3 = T2("fin3")

    PS_M = ps("PS_M", [128, 2, 4, 128]); RPS_M = R()
    PS_K = ps("PS_K", [128, 512]); RPS_K = R()
    PS_X = [ps("PS_X%d" % h, [128, 512]) for h in range(2)]; RPS_X = R()
    PS_AB = ps("PS_AB", [128, 2, 2, 128]); RPS_AB = R()
    PS_G = ps("PS_G", [128, 512]); RPS_G = R()
    PS_T = ps("PS_T", [128, 8, 128], BF16); RPS_T = R()
    PS_P1 = PS_AB[:].rearrange("p a b c -> p (a b c)"); RPS_P1 = RPS_AB
    PS_P2 = PS_P1; RPS_P2 = RPS_AB
    mm = lambda out, l, r_, rd, wr, st=True, sp=True, sg=False: S.op("pe", ("matmul", dict(out=out, lhsT=l, rhs=r_, start=st, stop=sp, skip_group_check=sg)), reads=rd, writes=wr)
    S.op("pool", ("tensor_copy", dict(out=identP[0:64, :], in_=identf[0:64, 0:64])), reads=[Ridf], writes=[Ridf])
    S.op("pool", ("tensor_copy", dict(out=identP[64:128, :], in_=identf[64:128, 64:128])), reads=[Ridf], writes=[Ridf])
    for h in range(2):
        S.op("pool", ("tensor_copy", dict(out=mkb[:, :, h, :], in_=mkf[:])), reads=[Rmk], writes=[Rmk])
    ytmp = sb("ytmp", [128, 128]); Rytmp = R()
    out_toks = []
    NFILL = 4
    NPREP = 2
    def prep_gen(b, d, blk, pp):
        bwd = (d == 1)
        midc, totc = (C // 2 - 1, C - 1) if not bwd else (C // 2, 0)
        t0 = blk * BLK
        rt_, Rrt_ = rtS[pp], RrtS[pp]
        at_, Rat_ = atS[pp], RatS[pp]
        kt_, Rkt_ = ktS[pp], RktS[pp]
        bt_, Rbt_ = btS[pp], RbtS[pp]
        r0_, Rr0_ = r0S[pp], Rr0S[pp]
        a0b_, Ra0b_ = a0bS[pp], Ra0bS[pp]
        kEb_, RkEb_ = kEbS[pp], RkEbS[pp]
        bEb_, RbEb_ = bEbS[pp], RbEbS[pp]
        qvb_, Rqvb_ = qvbS[pp], RqvbS[pp]
        e3_, Re3_ = e3S[pp], Re3S[pp]
        for nm, src in (("pr", pr), ("pk", pk), ("pv", pv), ("pwa", pwa)):
            tl, Rl = ld[nm]
            S.dma(("dma_start", dict(out=tl[:], in_=src[:, b, t0:t0 + BLK + 2])), writes=[Rl])
            yield
        sh = (slice(0, BLK) if not bwd else slice(2, BLK + 2))
        cur = slice(1, BLK + 1)
        for nm, q, Rq, mc in (("pr", qr, Rqr, 0), ("pk", qk, Rqk, 2), ("pv", qv, Rqv, 4), ("pwa", qwa, Rqwa, 6)):
            tl, Rl = ld[nm]
            S.op("dve", ("tensor_tensor", dict(out=tmp[:], in0=tl[:, sh], in1=tl[:, cur], op=ALU.subtract)), reads=[Rl], writes=[Rtmp])
            yield
            S.op("dve", ("scalar_tensor_tensor", dict(out=q[:], in0=tmp[:], scalar=PM(mc + d), in1=tl[:, cur], op0=ALU.mult, op1=ALU.add)), reads=[Rtmp, Rl, Rprm], writes=[Rq])
            yield
        S.op("act", ("activation", dict(out=twa[0:64, :], in_=qwa[0:64, :], func=AF.Tanh)), reads=[Rqwa], writes=[Rtwa])
        yield
        S.op("dve", ("tensor_copy", dict(out=twa[64:128, :], in_=qwa[64:128, :])), reads=[Rqwa], writes=[Rtwa])
        yield
        S.op("pe", ("matmul", dict(out=PS_P1, lhsT=w2b[0:64, d, :], rhs=twa[0:64, :], start=True, stop=True)), reads=[Rw2b, Rtwa], writes=[RPS_P1])
        S.op("act", ("activation", dict(out=sw[:], in_=PS_P1, func=AF.Sigmoid, bias=PM(8 + d))), reads=[RPS_P1, Rprm], writes=[Rsw])
        yield
        S.op("pe", ("matmul", dict(out=PS_P2, lhsT=w2b[64:128, d, :], rhs=twa[64:128, :], start=True, stop=True)), reads=[Rw2b, Rtwa], writes=[RPS_P2])
        S.op("act", ("activation", dict(out=asg[:], in_=PS_P2, func=AF.Sigmoid, bias=PM(10 + d))), reads=[RPS_P2, Rprm], writes=[Rasg])
        yield
        S.op("dve", ("tensor_scalar", dict(out=logw[:], in0=sw[:], scalar1=NEG_E, scalar2=None, op0=ALU.mult)), reads=[Rsw], writes=[Rlogw])
        yield
        S.op("dve", ("tensor_tensor_scan", dict(out=lin[:], data0=scanm[:], data1=logw[:], initial=0.0, op0=ALU.mult, op1=ALU.add)), reads=[Rsc, Rlogw], writes=[Rlin])
        yield
        lin3 = lambda tl: tl[:].rearrange("p (c t) -> p c t", t=C)
        bc = lambda tl, col: lin3(tl)[:, :, col:col + 1].to_broadcast([128, BLK // C, C])
        if bwd:
            S.op("dve", ("tensor_tensor", dict(out=lin3(tmp), in0=bc(lin, C - 1), in1=lin3(lin), op=ALU.subtract)), reads=[Rlin], writes=[Rtmp])
            yield
            S.op("dve", ("tensor_tensor", dict(out=lin[:], in0=tmp[:], in1=logw[:], op=ALU.add)), reads=[Rtmp, Rlogw], writes=[Rlin])
            yield
        S.op("dve", ("tensor_tensor", dict(out=lin3(linm), in0=lin3(lin), in1=bc(lin, midc), op=ALU.subtract)), reads=[Rlin], writes=[Rlinm])
        yield
        S.op("dve", ("tensor_tensor", dict(out=lexm[:], in0=linm[:], in1=logw[:], op=ALU.subtract)), reads=[Rlinm, Rlogw], writes=[Rlexm])
        yield
        S.op("dve", ("tensor_tensor", dict(out=lex[:], in0=lin[:], in1=logw[:], op=ALU.subtract)), reads=[Rlin, Rlogw], writes=[Rlex])
        yield
        S.op("dve", ("tensor_tensor", dict(out=lin3(lint), in0=lin3(lin), in1=bc(lin, totc), op=ALU.subtract)), reads=[Rlin], writes=[Rlint])
        yield
        S.op("act", ("activation", dict(out=e1[:], in_=linm[:], func=AF.Exp)), reads=[Rlinm], writes=[Re1])
        yield
        S.op("act", ("activation", dict(out=e1x[:], in_=lexm[:], func=AF.Exp)), reads=[Rlexm], writes=[Re1x])
        yield
        S.op("act", ("activation", dict(out=e2[:], in_=linm[:], func=AF.Exp, scale=-1.0)), reads=[Rlinm], writes=[Re2])
        yield
        S.op("act", ("activation", dict(out=e3_[:], in_=lin[:], func=AF.Exp)), reads=[Rlin], writes=[Re3_])
        yield
        S.op("act", ("activation", dict(out=e3x[:], in_=lex[:], func=AF.Exp)), reads=[Rlex], writes=[Re3x])
        yield
        S.op("act", ("activation", dict(out=e4[:], in_=lint[:], func=AF.Exp, scale=-1.0)), reads=[Rlint], writes=[Re4])
        yield
        S.op("dve", ("tensor_scalar", dict(out=kk[:], in0=qk[:], scalar1=PM(12), scalar2=None, op0=ALU.mult)), reads=[Rqk, Rprm], writes=[Rkk])
        yield
        S.op("pool", ("tensor_tensor", dict(out=kk2[:], in0=kk[:], in1=kk[:], op=ALU.mult)), reads=[Rkk], writes=[Rkk2])
        yield
        S.op("pe", ("matmul", dict(out=PS_P1, lhsT=bdf[:], rhs=kk2[:], start=True, stop=True)), reads=[Rbd, Rkk2], writes=[RPS_P1])
        S.op("dve", ("tensor_scalar", dict(out=rin[:], in0=PS_P1, scalar1=1e-12, scalar2=None, op0=ALU.max)), reads=[RPS_P1], writes=[Rrin])
        yield
        S.op("act", ("activation", dict(out=rin[:], in_=rin[:], func=AF.Sqrt)), reads=[Rrin], writes=[Rrin])
        yield
        S.op("dve", ("reciprocal", dict(out=rin[:], in_=rin[:])), reads=[Rrin], writes=[Rrin])
        yield
        S.op("dve", ("tensor_tensor", dict(out=kkn[:], in0=kk[:], in1=rin[:], op=ALU.mult)), reads=[Rkk, Rrin], writes=[Rkkn])
        yield
        S.op("dve", ("tensor_scalar", dict(out=tmp[:], in0=asg[:], scalar1=-1.0, scalar2=PM(13), op0=ALU.add, op1=ALU.mult)), reads=[Rasg, Rprm], writes=[Rtmp])
        yield
        S.op("dve", ("scalar_tensor_tensor", dict(out=kp[:], in0=tmp[:], scalar=1.0, in1=qk[:], op0=ALU.add, op1=ALU.mult)), reads=[Rtmp, Rqk], writes=[Rkp])
        yield
        S.op("pool", ("tensor_tensor", dict(out=bv[:], in0=kkn[:], in1=asg[:], op=ALU.mult)), reads=[Rkkn, Rasg], writes=[Rbv])
        yield
        S.op("pool", ("tensor_tensor", dict(out=rk[:], in0=qr[:], in1=kp[:], op=ALU.mult)), reads=[Rqr, Rkp], writes=[Rrk])
        yield
        S.op("pe", ("matmul", dict(out=PS_P2, lhsT=bdr[:], rhs=rk[:], start=True, stop=True)), reads=[Rbdr, Rrk], writes=[RPS_P2])
        bsl = bsum[:, t0:t0 + BLK]
        if d == 0:
            S.op("dve", ("tensor_tensor", dict(out=bsl, in0=PS_P2, in1=qv[:], op=ALU.mult)), reads=[RPS_P2, Rqv], writes=[Rbs[blk]])
            yield
        else:
            S.op("dve", ("tensor_tensor", dict(out=tmp[:], in0=PS_P2, in1=qv[:], op=ALU.mult)), reads=[RPS_P2, Rqv], writes=[Rtmp])
            yield
            S.op("pool", ("tensor_tensor", dict(out=bsl, in0=bsl, in1=tmp[:], op=ALU.add)), reads=[Rtmp, Rbs[blk]], writes=[Rbs[blk]])
            yield
        S.op("dve", ("tensor_tensor", dict(out=rt_[:], in0=qr[:], in1=e1[:], op=ALU.mult)), reads=[Rqr, Re1], writes=[Rrt_])
        yield
        S.op("dve", ("scalar_tensor_tensor", dict(out=at_[:], in0=kkn[:], scalar=-1.0, in1=e1x[:], op0=ALU.mult, op1=ALU.mult)), reads=[Rkkn, Re1x], writes=[Rat_])
        yield
        S.op("pool", ("tensor_tensor", dict(out=kt_[:], in0=kp[:], in1=e2[:], op=ALU.mult)), reads=[Rkp, Re2], writes=[Rkt_])
        yield
        S.op("pool", ("tensor_tensor", dict(out=bt_[:], in0=bv[:], in1=e2[:], op=ALU.mult)), reads=[Rbv, Re2], writes=[Rbt_])
        yield
        S.op("pool", ("tensor_tensor", dict(out=r0_[:], in0=qr[:], in1=e3_[:], op=ALU.mult)), reads=[Rqr, Re3_], writes=[Rr0_])
        yield
        S.op("dve", ("scalar_tensor_tensor", dict(out=a0b_[:], in0=kkn[:], scalar=-1.0, in1=e3x[:], op0=ALU.mult, op1=ALU.mult)), reads=[Rkkn, Re3x], writes=[Ra0b_])
        yield
        S.op("pool", ("tensor_tensor", dict(out=kEb_[:], in0=kp[:], in1=e4[:], op=ALU.mult)), reads=[Rkp, Re4], writes=[RkEb_])
        yield
        S.op("pool", ("tensor_tensor", dict(out=bEb_[:], in0=bv[:], in1=e4[:], op=ALU.mult)), reads=[Rbv, Re4], writes=[RbEb_])
        yield
        S.op("act", ("activation", dict(out=qvb_[:], in_=qv[:], func=AF.Copy)), reads=[Rqv], writes=[Rqvb_])
        yield


    def block_stages(b, d, blk, pp):
        bwd = (d == 1)
        midc, totc = (C // 2 - 1, C - 1) if not bwd else (C // 2, 0)
        t0 = blk * BLK
        rt_, Rrt_ = rtS[pp], RrtS[pp]
        at_, Rat_ = atS[pp], RatS[pp]
        kt_, Rkt_ = ktS[pp], RktS[pp]
        bt_, Rbt_ = btS[pp], RbtS[pp]
        r0_, Rr0_ = r0S[pp], Rr0S[pp]
        a0b_, Ra0b_ = a0bS[pp], Ra0bS[pp]
        kEb_, RkEb_ = kEbS[pp], RkEbS[pp]
        bEb_, RbEb_ = bEbS[pp], RbEbS[pp]
        qvb_, Rqvb_ = qvbS[pp], RqvbS[pp]
        e3_, Re3_ = e3S[pp], Re3S[pp]

        def stage1(ck):
            ci, cs, gci, p = ck
            for i, (src, Rs) in enumerate(((qvb_, Rqvb_), (a0b_, Ra0b_), (bEb_, RbEb_), (kEb_, RkEb_))):
                S.op("pe", ("transpose", dict(out=PS_T[:, i, :], in_=src[:, cs], identity=identb[:])), reads=[Rs, Ridb], writes=[RPS_T])
            yield
            S.op("act", ("activation", dict(out=TT[p][:], in_=PS_T[:, 0:4, :], func=AF.Copy)), reads=[RPS_T], writes=[RTT[p]])
            yield
            for h in range(2):
                hs = HS[h]
                mm(PS_M[:, h, 0, :], bt_[hs, cs], at_[hs, cs], [Rbt_, Rat_], [RPS_M])
                mm(PS_M[:, h, 1, :], at_[hs, cs], bt_[hs, cs], [Rbt_, Rat_], [RPS_M])
                yield
                mm(PS_M[:, h, 2, :], at_[hs, cs], kt_[hs, cs], [Rkt_, Rat_], [RPS_M])
                mm(PS_M[:, h, 3, :], bt_[hs, cs], rt_[hs, cs], [Rbt_, Rrt_], [RPS_M])
                yield
                mm((PS_K if h == 0 else PS_G)[:, 0:128], kt_[hs, cs], rt_[hs, cs], [Rkt_, Rrt_], [RPS_K if h == 0 else RPS_G])
                yield
            for h in range(2):
                S.op("dve", ("tensor_tensor", dict(out=SBM[p][:, h], in0=PS_M[:, h], in1=m4f[:, d, :, :], op=ALU.mult)), reads=[RPS_M, Rm4], writes=[RSBM[p]])
                yield
            S.op("dve", ("tensor_tensor", dict(out=MKR[p][:, 0, :], in0=PS_K[:, 0:128], in1=mkf[:, d, :], op=ALU.mult)), reads=[RPS_K, Rmk], writes=[RMKR[p]])
            yield
            S.op("dve", ("tensor_tensor", dict(out=MKR[p][:, 1, :], in0=PS_G[:, 0:128], in1=mkf[:, d, :], op=ALU.mult)), reads=[RPS_G, Rmk], writes=[RMKR[p]])
            yield
            S.op("act", ("activation", dict(out=SX[p][:, :, 0:128], in_=SBM[p][:, :, 3, :], func=AF.Copy)), reads=[RSBM[p]], writes=[RSX[p]])
            S.op("pool", ("tensor_copy", dict(out=SX[p][:, :, 128:192], in_=TT[p][:, 2, :].rearrange("p (h j) -> p h j", h=2))), reads=[RTT[p]], writes=[RSX[p]])
            yield

        def stage2(ck):
            ci, cs, gci, p = ck
            for h in range(2):
                mm(PS_X[h][:, 0:192], identb[:], SX[p][:, h, :], [Ridb, RSX[p]], [RPS_X], st=True, sp=True)
            A_ = [SBM[p][:, h, 1, :] for h in range(2)]
            B_ = [SBM[p][:, h, 0, :] for h in range(2)]
            Rcur = RSBM[p]
            for lv in range(7):
                if lv < 6:
                    nb = lv % 2
                    for h in range(2):
                        mm(PS_AB[:, h, 0, :], B_[h], A_[h], [Rcur], [RPS_AB])
                        mm(PS_AB[:, h, 1, :], A_[h], B_[h], [Rcur], [RPS_AB])
                for h in range(2):
                    mm(PS_X[h][:, 0:192], A_[h], SX[p][:, h, :], [Rcur, RSX[p]], [RPS_X], st=False, sp=True, sg=True)
                if lv < 6:
                    S.op("act", ("activation", dict(out=SAB[nb][:].rearrange("p a b c -> p (a b c)"), in_=PS_AB[:].rearrange("p a b c -> p (a b c)"), func=AF.Copy)), reads=[RPS_AB], writes=[RSAB[nb]])
                S.op("dve", ("tensor_copy", dict(out=SX[p][:, 0, :], in_=PS_X[0][:, 0:192])), reads=[RPS_X], writes=[RSX[p]])
                S.op("dve", ("tensor_copy", dict(out=SX[p][:, 1, :], in_=PS_X[1][:, 0:192])), reads=[RPS_X], writes=[RSX[p]])
                if lv < 6:
                    A_ = [SAB[nb][:, h, 0, :] for h in range(2)]
                    B_ = [SAB[nb][:, h, 1, :] for h in range(2)]
                    Rcur = RSAB[nb]
                yield

        def stage3(ck):
            ci, cs, gci, p = ck
            for h in range(2):
                hs = HS[h]
                a0T = TT[p][:, 1, hs]
                mm(PS_G[hs, 0:128], a0T, SX[p][:, h, 0:128], [RTT[p], RSX[p]], [RPS_G])
                mm(PS_G[hs, 128:192], a0T, SX[p][:, h, 128:192], [RTT[p], RSX[p]], [RPS_G])
                yield
                mm(PS_G[:, 192 + 128 * h:320 + 128 * h], SBM[p][:, h, 2, :], SX[p][:, h, 0:128], [RSBM[p], RSX[p]], [RPS_G])
                mm(PS_K[:, 256 + 64 * h:320 + 64 * h], SBM[p][:, h, 2, :], SX[p][:, h, 128:192], [RSBM[p], RSX[p]], [RPS_K])
                yield
            S.op("dve", ("tensor_tensor", dict(out=Gb[:], in0=PS_G[:, 0:128], in1=r0_[:, cs], op=ALU.add)), reads=[RPS_G, Rr0_], writes=[RGb])
            yield
            S.op("dve", ("tensor_tensor", dict(out=Hb[:], in0=PS_G[:, 192:448].rearrange("p (h t) -> p h t", h=2), in1=MKR[p][:], op=ALU.add)), reads=[RPS_G, RMKR[p]], writes=[RHb])
            yield
            tcol = ci * C + totc
            S.op("dve", ("scalar_tensor_tensor", dict(out=Pb[:], in0=identP[:], scalar=e3_[:, tcol:tcol + 1], in1=PS_G[:, 128:192], op0=ALU.mult, op1=ALU.add)), reads=[RPS_G, Ridf, Re3_], writes=[RPb])
            yield
            S.op("dve", ("tensor_tensor", dict(out=Zb[:], in0=PS_K[:, 256:384].rearrange("p (h j) -> p h j", h=2), in1=TT[p][:, 3, :].rearrange("p (h j) -> p h j", h=2), op=ALU.add)), reads=[RPS_K, RTT[p]], writes=[RZb])
            yield
            for h in range(2):
                hs = HS[h]
                mm(PS_M[hs, 0, 0, :], STz[h][:], Gb[:], [RST[h], RGb], [RPS_M], st=True, sp=False)
                mm(PS_M[hs, 0, 0, :], TT[p][:, 0, hs], Hb[:, h, :], [RTT[p], RHb], [RPS_M], st=False, sp=True)
                yield
                mm(PS_M[hs, 0, 1, 0:64], Pb[:], STz[h][:], [RPb, RST[h]], [RPS_M], st=True, sp=False)
                mm(PS_M[hs, 0, 1, 0:64], Zb[:, h, :], TT[p][:, 0, hs], [RZb, RTT[p]], [RPS_M], st=False, sp=True)
                yield
            ysl = ysum[:, t0 + ci * C: t0 + (ci + 1) * C]
            if d == 0:
                S.op("act", ("activation", dict(out=ysl, in_=PS_M[:, 0, 0, :], func=AF.Copy)), reads=[RPS_M], writes=[Rys[gci]])
            else:
                S.op("act", ("activation", dict(out=ytmp[:, 0:128], in_=PS_M[:, 0, 0, :], func=AF.Copy)), reads=[RPS_M], writes=[Rytmp])
                S.op("dve", ("tensor_tensor", dict(out=ysl, in0=ytmp[:, 0:128], in1=ysl, op=ALU.add)), reads=[Rytmp, Rys[gci]], writes=[Rys[gci]])
            yield
            for h in range(2):
                hs = HS[h]
                S.op("act", ("activation", dict(out=STz[h][hs, :], in_=PS_M[hs, 0, 1, 0:64], func=AF.Copy)), reads=[RPS_M], writes=[RST[h]])
            yield

        return stage1, stage2, stage3

    def finalize_batch(b):
        for blk in range(nblk):
            t0 = blk * BLK
            ysl = ysum[:, t0:t0 + BLK]
            Rin = Rys[t0 // C: (t0 + BLK) // C]
            S.op("pe", ("matmul", dict(out=PS_P1, lhsT=bdm[:], rhs=ysl, start=True, stop=True)), reads=[Rbdm] + Rin, writes=[RPS_P1])
            S.op("dve", ("tensor_tensor", dict(out=fin1[:], in0=ysl, in1=PS_P1, op=ALU.subtract)), reads=[RPS_P1] + Rin, writes=[Rfin1])
            S.op("pool", ("tensor_tensor", dict(out=fin2[:], in0=fin1[:], in1=fin1[:], op=ALU.mult)), reads=[Rfin1], writes=[Rfin2])
            S.op("pe", ("matmul", dict(out=PS_P2, lhsT=bdm[:], rhs=fin2[:], start=True, stop=True)), reads=[Rbdm, Rfin2], writes=[RPS_P2])
            S.op("dve", ("tensor_scalar", dict(out=fin3[:], in0=PS_P2, scalar1=GN_EPS, scalar2=None, op0=ALU.add)), reads=[RPS_P2], writes=[Rfin3])
            S.op("act", ("activation", dict(out=fin3[:], in_=fin3[:], func=AF.Sqrt)), reads=[Rfin3], writes=[Rfin3])
            S.op("dve", ("reciprocal", dict(out=fin3[:], in_=fin3[:])), reads=[Rfin3], writes=[Rfin3])
            S.op("dve", ("tensor_tensor", dict(out=fin1[:], in0=fin1[:], in1=fin3[:], op=ALU.mult)), reads=[Rfin1, Rfin3], writes=[Rfin1])
            S.op("dve", ("tensor_scalar", dict(out=fin2[:], in0=fin1[:], scalar1=PM(15), scalar2=PM(16), op0=ALU.mult, op1=ALU.add)), reads=[Rfin1, Rprm], writes=[Rfin2])
            S.op("dve", ("tensor_tensor", dict(out=fin2[:], in0=fin2[:], in1=bsum[:, t0:t0 + BLK], op=ALU.add)), reads=[Rfin2, Rbs[blk]], writes=[Rfin2])
            out_toks.append(S.dma(("dma_start", dict(out=yout[:, b, t0:t0 + BLK], in_=fin2[:])), reads=[Rfin2]))

    sched_blocks = []
    for b in range(NB):
        for d in range(2):
            order = list(range(nblk)) if d == 0 else list(range(nblk - 1, -1, -1))
            for n_, blk in enumerate(order):
                sched_blocks.append((b, d, blk, n_ == 0, (n_ == len(order) - 1) and d == 1))
    pcount = 0
    NCH = BLK // C
    for b in range(NB):
        blocks_b = [(k, sbk) for k, sbk in enumerate(sched_blocks) if sbk[0] == b]
        k0 = blocks_b[0][0]
        for _ in prep_gen(b, sched_blocks[k0][1], sched_blocks[k0][2], k0 % 2):
            pass
        seq = []
        for (k, (b_, d, blk, first_of_dir, last_of_batch)) in blocks_b:
            bwd = (d == 1)
            st = block_stages(b, d, blk, k % 2)
            chunks = list(range(NCH)) if not bwd else list(range(NCH - 1, -1, -1))
            for n_, ci in enumerate(chunks):
                ck = (ci, slice(ci * C, (ci + 1) * C), (blk * BLK // C) + ci, pcount % 2)
                pcount += 1
                seq.append((ck, st, k, n_, first_of_dir and n_ == 0))
        pgen = iter(())
        S.op("pool", ("memset", dict(ap=STz[0][:], constant=0.0)), writes=[RST[0]])
        S.op("pool", ("memset", dict(ap=STz[1][:], constant=0.0)), writes=[RST[1]])
        for _ in seq[0][1][0](seq[0][0]):
            pass
        for i, (ck, st, k, n_, fod) in enumerate(seq):
            if n_ == 0:
                if k + 1 < len(sched_blocks) and sched_blocks[k + 1][0] == b:
                    nb_, nd_, nblk_ = sched_blocks[k + 1][:3]
                    pgen = prep_gen(nb_, nd_, nblk_, (k + 1) % 2)
                else:
                    pgen = iter(())
            if n_ == NCH - 1:
                for _ in pgen:
                    pass
            parts = []
            if i > 0:
                parts.append(seq[i - 1][1][2](seq[i - 1][0]))
            if i + 1 < len(seq):
                parts.append(seq[i + 1][1][0](seq[i + 1][0]))
            fill = itertools.chain(*parts)
            for _ in st[1](ck):
                for _k in range(NFILL):
                    next(fill, None)
                for _k in range(NPREP):
                    next(pgen, None)
            for _ in fill:
                pass
            if i + 1 < len(seq) and seq[i + 1][4]:
                for _ in st[2](ck):
                    pass
                S.op("pool", ("memset", dict(ap=STz[0][:], constant=0.0)), writes=[RST[0]])
                S.op("pool", ("memset", dict(ap=STz[1][:], constant=0.0)), writes=[RST[1]])
                seq[i] = (ck, (st[0], st[1], lambda ck_: iter(())), k, n_, fod)
        for _ in seq[-1][1][2](seq[-1][0]):
            pass
        finalize_batch(b)
    return out_toks


NT = 2048
NTH = NT + 2


def emit_p1(S, nc, I, O):
    sb = lambda name, shape, dt=F32: nc.alloc_sbuf_tensor(name, shape, dt)
    ps = lambda name, shape, dt=F32: nc.alloc_psum_tensor(name, shape, dt)
    R = Region
    op = S.op
    mm = lambda out, l, r_, rd, wr, st=True, sp=True: op("pe", ("matmul", dict(out=out, lhsT=l, rhs=r_, start=st, stop=sp)), reads=rd, writes=wr)

    identf = sb("identf", [128, 128]); identb = sb("identb", [128, 128], BF16); Rid = R()
    gE = sb("gE", [128, 8, 1]); gO = sb("gO", [128, 8, 1]); Rg = R()
    S.dma(("dma_start", dict(out=identf[:], in_=I["c_ident"])), writes=[Rid])
    op("dve", ("tensor_copy", dict(out=identb[:], in_=identf[:])), reads=[Rid], writes=[Rid])
    S.dma(("dma_start", dict(out=gE[:, :, 0], in_=I["e_norm_g"].rearrange("(k p) -> p k", p=128), allow_slow_non_contiguous=True)), writes=[Rg])
    S.dma(("dma_start", dict(out=gO[:, :, 0], in_=I["o_norm_g"].rearrange("(k p) -> p k", p=128), allow_slow_non_contiguous=True)), writes=[Rg])
    hnT = sb("hnT", [128, 8, NTH], BF16); RhnT = [R() for _ in range(18)]
    yT = nc.dram_tensor("yT_d", [16, 128, NT], BF16).ap(); RyT = [[R() for _ in range(4)] for _ in range(16)]
    U = sb("U", [128, 4096]); RU = R()
    xt = [sb("xt%d" % i, [128, 1024]) for i in range(2)]; Rxt = [R(), R()]
    yo = [sb("yo%d" % i, [128, 512], BF16) for i in range(2)]; Ryo = [R(), R()]
    ytl = [sb("ytl%d" % i, [128, 16, 128], BF16) for i in range(2)]; Rytl = [R(), R()]
    xn = sb("xn", [128, 1024], BF16); Rxn = R()
    sq = sb("sq", [128, 1024]); Rsq = R()
    st = sb("st", [128, 8]); Rst = R()
    stg = [sb("stg%d" % i, [128, 8, 256]) for i in range(2)]; Rstg = [R(), R()]
    wbf = [sb("wbf%d" % i, [128, 8, 512], BF16) for i in range(2)]; Rwbf = [R(), R()]
    wbig = sb("wbig", [128, 16, 1024], BF16); Rwbig = R()
    t1 = sb("t1", [128, 512]); Rt1 = R()
    t1b = sb("t1b", [128, 512]); t1s = [t1, t1b]; Rt1s = [Rt1, R()]
    t2b = sb("t2b", [128, 512]); t3b = sb("t3b", [128, 512])
    t2 = sb("t2", [128, 512]); Rt2 = R()
    t3 = sb("t3", [128, 512]); Rt3 = R()
    cw = sb("cw", [128, 8, 3]); Rcw = R()
    PS_a = ps("PS_a", [128, 512]); RPa = R()
    PS_b = ps("PS_b", [128, 512]); RPb = R()
    PS_c = ps("PS_c", [128, 512]); RPc = R()
    PS_d = ps("PS_d", [128, 512]); RPd = R()
    PS_t = ps("PS_t", [128, 8, 128], BF16); RPt = R()
    PS_m = ps("PS_m", [128, 8, 128]); RPm = R()
    for j_ in range(3):
        S.dma(("dma_start", dict(out=cw[:, :, j_], in_=I["e_conv_w"][j_].rearrange("(cb p) -> p cb", p=128), allow_slow_non_contiguous=True)), writes=[Rcw])

    def norm_tile(xtile, Rx, gt, dst_fn, Rdst, nvalid=128):
        op("act", ("activation", dict(out=sq[:], in_=xtile[:], func=AF.Square)), reads=[Rx], writes=[Rsq])
        op("dve", ("reduce_sum", dict(out=st[:, 0:1], in_=sq[:], axis=AX.X)), reads=[Rsq], writes=[Rst])
        op("dve", ("tensor_scalar", dict(out=st[:, 1:2], in0=st[:, 0:1], scalar1=1.0 / 1024, scalar2=1e-6, op0=ALU.mult, op1=ALU.add)), reads=[Rst], writes=[Rst])
        op("act", ("activation", dict(out=st[:, 2:3], in_=st[:, 1:2], func=AF.Sqrt)), reads=[Rst], writes=[Rst])
        op("dve", ("reciprocal", dict(out=st[:, 3:4], in_=st[:, 2:3])), reads=[Rst], writes=[Rst])
        op("dve", ("tensor_scalar", dict(out=xn[:], in0=xtile[:], scalar1=st[:, 3:4], scalar2=None, op0=ALU.mult)), reads=[Rx, Rst], writes=[Rxn])
        for k in range(8):
            op("pe", ("transpose", dict(out=PS_t[:, k, :], in_=xn[:, k * 128:(k + 1) * 128], identity=identb[:])), reads=[Rxn, Rid], writes=[RPt])
        dst_fn(gt)

    xh = I["xh"]
    for i in range(17):
        xb, Rx = xt[i % 2], Rxt[i % 2]
        if i < 16:
            S.dma(("dma_start", dict(out=xb[:], in_=xh[1 + 128 * i: 1 + 128 * (i + 1), :])), writes=[Rx])
            def dst(gt, i=i):
                op("dve", ("tensor_tensor", dict(out=hnT[:, :, 1 + 128 * i: 1 + 128 * (i + 1)], in0=PS_t[:], in1=gt[:].to_broadcast([128, 8, 128]), op=ALU.mult)), reads=[RPt, Rg], writes=[RhnT[i]])
        else:
            op("pool", ("memset", dict(ap=xb[:], constant=0.0)), writes=[Rx])
            S.dma(("dma_start", dict(out=xb[0:1, :], in_=xh[0:1, :])), writes=[Rx])
            S.dma(("dma_start", dict(out=xb[1:2, :], in_=xh[NT + 1:NT + 2, :])), writes=[Rx])
            def dst(gt):
                op("dve", ("tensor_tensor", dict(out=hnT[:, :, 0:1], in0=PS_t[:, :, 0:1], in1=gt[:], op=ALU.mult)), reads=[RPt, Rg], writes=[RhnT[16]])
                op("dve", ("tensor_tensor", dict(out=hnT[:, :, NT + 1:NT + 2], in0=PS_t[:, :, 1:2], in1=gt[:], op=ALU.mult)), reads=[RPt, Rg], writes=[RhnT[17]])
        norm_tile(xb, Rx, gE, dst, None)
    allh = RhnT

    wi = I["e_w_in"].rearrange("(k p) (s c) -> p k s c", p=128, c=1024)

    def load_w(buf, src4, nsp):
        for s_ in range(nsp):
            sb_ = s_ % 2
            S.dma(("dma_start", dict(out=stg[sb_][:, :, 0:128], in_=src4[:, :, s_, :])), writes=[Rstg[sb_]])
            op("act", ("activation", dict(out=wbf[buf][:, :, s_ * 128:(s_ + 1) * 128], in_=stg[sb_][:, :, 0:128], func=AF.Copy)), reads=[Rstg[sb_]], writes=[Rwbf[buf]])
        return wbf[buf][:, :, 0:nsp * 128].rearrange("p k (s c) -> p k s c", c=128)

    PSc0, RPc0, PSd0, RPd0 = PS_c, RPc, PS_d, RPd
    t2s = [t2, t2b]; Rt2s = [Rt2, R()]
    t3s = [t3, t3b]; Rt3s = [Rt3, R()]
    xc2 = sb("xc2", [128, NTH]); RU2 = R()
    chunksA = [(0, 512), (512, 512), (1024, 512), (1536, 512), (2048, 2)]
    RU_main = RU
    for cb in range(8):
        xc, RU = (U[:, 0:NTH], RU_main) if cb % 2 == 0 else (xc2[:, :], RU2)
        if cb % 2 == 0:
            for s_ in range(4):
                sb_ = s_ % 2
                S.dma(("dma_start", dict(out=stg[sb_][:], in_=wi[:, :, s_, cb * 128:(cb + 2) * 128])), writes=[Rstg[sb_]])
                op("act", ("activation", dict(out=wbf[0][:, :, s_ * 128:(s_ + 1) * 128], in_=stg[sb_][:, :, 0:128], func=AF.Copy)), reads=[Rstg[sb_]], writes=[Rwbf[0]])
                op("act", ("activation", dict(out=wbf[1][:, :, s_ * 128:(s_ + 1) * 128], in_=stg[sb_][:, :, 128:256], func=AF.Copy)), reads=[Rstg[sb_]], writes=[Rwbf[1]])
        w4 = wbf[cb % 2][:, :, 0:512].rearrange("p k (s c) -> p k s c", c=128)
        Rw = Rwbf[cb % 2]
        for ci_, (c0, n) in enumerate(chunksA):
            (PA, RA_), (PB, RB_) = (((PS_a, RPa), (PS_b, RPb)) if ci_ % 2 == 0 else ((PS_c, RPc), (PS_d, RPd)))
            for k in range(8):
                mm(PA[:, 0:n], w4[:, k, 0, :], hnT[:, k, c0:c0 + n], [Rw] + allh, [RA_], st=(k == 0), sp=(k == 7))
            for k in range(8):
                mm(PB[:, 0:n], w4[:, k, 2, :], hnT[:, k, c0:c0 + n], [Rw] + allh, [RB_], st=(k == 0), sp=(k == 7))
            t1_, Rt1_ = t1s[ci_ % 2], Rt1s[ci_ % 2]
            op("act", ("activation", dict(out=t1_[:, 0:n], in_=PA[:, 0:n], func=AF.Copy)), reads=[RA_], writes=[Rt1_])
            op("dve", ("tensor_tensor", dict(out=xc[:, c0:c0 + n], in0=PB[:, 0:n], in1=t1_[:, 0:n], op=ALU.mult)), reads=[RB_, Rt1_], writes=[RU])
        for j in range(4):
            c0 = 1 + 512 * j
            (PS_c, RPc), (PS_d, RPd) = ((PSc0, RPc0), (PSd0, RPd0)) if j % 2 == 1 else ((PS_a, RPa), (PS_b, RPb))
            t2, Rt2 = t2s[j % 2], Rt2s[j % 2]
            t3, Rt3 = t3s[j % 2], Rt3s[j % 2]
            for k in range(8):
                mm(PS_c[:], w4[:, k, 1, :], hnT[:, k, c0:c0 + 512], [Rw] + allh, [RPc], st=(k == 0), sp=(k == 7))
            for k in range(8):
                mm(PS_d[:], w4[:, k, 3, :], hnT[:, k, c0:c0 + 512], [Rw] + allh, [RPd], st=(k == 0), sp=(k == 7))
            op("dve", ("tensor_scalar", dict(out=t2[:], in0=xc[:, c0 - 1:c0 + 511], scalar1=cw[:, cb, 0:1], scalar2=None, op0=ALU.mult)), reads=[RU, Rcw], writes=[Rt2])
            op("dve", ("scalar_tensor_tensor", dict(out=t2[:], in0=xc[:, c0:c0 + 512], scalar=cw[:, cb, 1:2], in1=t2[:], op0=ALU.mult, op1=ALU.add)), reads=[RU, Rcw, Rt2], writes=[Rt2])
            op("dve", ("scalar_tensor_tensor", dict(out=t2[:], in0=xc[:, c0 + 1:c0 + 513], scalar=cw[:, cb, 2:3], in1=t2[:], op0=ALU.mult, op1=ALU.add)), reads=[RU, Rcw, Rt2], writes=[Rt2])
            op("act", ("activation", dict(out=t3[:], in_=PS_d[:], func=AF.Silu)), reads=[RPd], writes=[Rt3])
            op("dve", ("tensor_tensor", dict(out=t2[:], in0=PS_c[:], in1=t2[:], op=ALU.mult)), reads=[RPc, Rt2], writes=[Rt2])
            op("pool", ("tensor_tensor", dict(out=yo[j % 2][:], in0=t2[:], in1=t3[:], op=ALU.mult)), reads=[Rt2, Rt3], writes=[Ryo[j % 2]])
            S.dma(("dma_start", dict(out=yT[cb, :, 512 * j:512 * (j + 1)], in_=yo[j % 2][:])), reads=[Ryo[j % 2]], writes=[RyT[cb][j]], q="pool")

    PS_c, RPc, PS_d, RPd = PSc0, RPc0, PSd0, RPd0
    t2, Rt2, t3, Rt3 = t2s[0], Rt2s[0], t3s[0], Rt3s[0]
    RU = RU_main
    for hf in range(4):
        S.dma(("dma_start", dict(out=stg[hf % 2][:], in_=wi[:, :, 5, hf * 256:(hf + 1) * 256])), writes=[Rstg[hf % 2]])
        op("act", ("activation", dict(out=wbig[:, 0:8, hf * 256:(hf + 1) * 256], in_=stg[hf % 2][:], func=AF.Copy)), reads=[Rstg[hf % 2]], writes=[Rwbig])
    for hf in range(4):
        S.dma(("dma_start", dict(out=stg[hf % 2][:], in_=wi[:, :, 4, hf * 256:(hf + 1) * 256])), writes=[Rstg[hf % 2]])
        op("act", ("activation", dict(out=wbig[:, 8:16, hf * 256:(hf + 1) * 256], in_=stg[hf % 2][:], func=AF.Copy)), reads=[Rstg[hf % 2]], writes=[Rwbig])
    for hf in range(4):
        S.dma(("dma_start", dict(out=stg[hf % 2][:], in_=wi[:, :, 6, hf * 256:(hf + 1) * 256])), writes=[Rstg[hf % 2]])
        op("act", ("activation", dict(out=wbf[hf // 2][:, :, (hf % 2) * 256:(hf % 2 + 1) * 256], in_=stg[hf % 2][:], func=AF.Copy)), reads=[Rstg[hf % 2]], writes=[Rwbf[hf // 2]])
    wsn = sb("wsn", [128, 8, 128]); wsnb = sb("wsnb", [128, 8, 128], BF16); wsT = sb("wsT", [128, 8, 128], BF16); Rws = R()
    S.dma(("dma_start", dict(out=wsn[:], in_=I["e_sgu_w"].rearrange("g i j -> i g j"))), writes=[Rws])
    op("dve", ("tensor_copy", dict(out=wsnb[:], in_=wsn[:])), reads=[Rws], writes=[Rws])
    for g in range(8):
        op("pe", ("transpose", dict(out=PS_t[:, g, :], in_=wsnb[:, g, :], identity=identb[:])), reads=[Rws, Rid], writes=[RPt])
    op("act", ("activation", dict(out=wsT[:], in_=PS_t[:], func=AF.Copy)), reads=[RPt], writes=[Rws])
    bsB = sb("bsB", [128, 8, 128]); lnG = sb("lnG", [128, 1024]); lnB = sb("lnB", [128, 1024]); Rbc = R()
    S.dma(("dma_start", dict(out=bsB[:].rearrange("p g i -> p (g i)"), in_=I["e_sgu_b"].rearrange("g i -> (g i)").partition_broadcast(128))), writes=[Rbc])
    S.dma(("dma_start", dict(out=lnG[:], in_=I["e_sgu_ln_g"].partition_broadcast(128))), writes=[Rbc])
    S.dma(("dma_start", dict(out=lnB[:], in_=I["e_sgu_ln_b"].partition_broadcast(128))), writes=[Rbc])
    vsb = sb("vsb", [128, 1024]); Rvsb = R()
    vnb = sb("vnb", [128, 1024], BF16); Rvnb = R()
    mixall = U[:, 0:4096].rearrange("p (g t) -> p g t", g=8)
    for tg in range(4):
        for ti in range(4):
            c0 = 1 + 128 * (4 * tg + ti)
            for hf, (P_, RP_) in enumerate((((PS_a, RPa), (PS_b, RPb)) if ti % 2 == 0 else ((PS_c, RPc), (PS_d, RPd)))):
                for k in range(8):
                    mm(P_[:], hnT[:, k, c0:c0 + 128], wbig[:, k, hf * 512:(hf + 1) * 512], [Rwbig] + allh, [RP_], st=(k == 0), sp=(k == 7))
                op("act", ("activation", dict(out=vsb[:, hf * 512:(hf + 1) * 512], in_=P_[:], func=AF.Copy)), reads=[RP_], writes=[Rvsb])
            op("act", ("activation", dict(out=sq[:], in_=vsb[:], func=AF.Square)), reads=[Rvsb], writes=[Rsq])
            op("dve", ("reduce_sum", dict(out=st[:, 0:1], in_=vsb[:], axis=AX.X)), reads=[Rvsb], writes=[Rst])
            op("dve", ("reduce_sum", dict(out=st[:, 1:2], in_=sq[:], axis=AX.X)), reads=[Rsq], writes=[Rst])
            op("dve", ("tensor_scalar", dict(out=st[:, 2:3], in0=st[:, 0:1], scalar1=1.0 / 1024, scalar2=None, op0=ALU.mult)), reads=[Rst], writes=[Rst])
            op("dve", ("tensor_tensor", dict(out=st[:, 3:4], in0=st[:, 2:3], in1=st[:, 2:3], op=ALU.mult)), reads=[Rst], writes=[Rst])
            op("dve", ("scalar_tensor_tensor", dict(out=st[:, 4:5], in0=st[:, 1:2], scalar=1.0 / 1024, in1=st[:, 3:4], op0=ALU.mult, op1=ALU.subtract)), reads=[Rst], writes=[Rst])
            op("dve", ("tensor_scalar", dict(out=st[:, 4:5], in0=st[:, 4:5], scalar1=1e-5, scalar2=None, op0=ALU.add)), reads=[Rst], writes=[Rst])
            op("act", ("activation", dict(out=st[:, 5:6], in_=st[:, 4:5], func=AF.Sqrt)), reads=[Rst], writes=[Rst])
            op("dve", ("reciprocal", dict(out=st[:, 6:7], in_=st[:, 5:6])), reads=[Rst], writes=[Rst])
            op("dve", ("tensor_scalar", dict(out=vsb[:], in0=vsb[:], scalar1=st[:, 2:3], scalar2=st[:, 6:7], op0=ALU.subtract, op1=ALU.mult)), reads=[Rvsb, Rst], writes=[Rvsb])
            op("dve", ("tensor_tensor", dict(out=vsb[:], in0=vsb[:], in1=lnG[:], op=ALU.mult)), reads=[Rvsb, Rbc], writes=[Rvsb])
            op("pool", ("tensor_tensor", dict(out=vnb[:], in0=vsb[:], in1=lnB[:], op=ALU.add)), reads=[Rvsb, Rbc], writes=[Rvnb])
            for g in range(8):
                mm(PS_m[:, g, :], vnb[:, g * 128:(g + 1) * 128], wsT[:, g, :], [Rvnb, Rws], [RPm])
            op("dve", ("tensor_tensor", dict(out=mixall[:, :, ti * 128:(ti + 1) * 128], in0=PS_m[:], in1=bsB[:], op=ALU.add)), reads=[RPm, Rbc], writes=[RU])
        c0 = 1 + 512 * tg
        for g in range(8):
            buf = g % 2

            (PU, RPU), (PZ, RPZ) = ((PS_c, RPc), (PS_d, RPd)) if g % 2 == 0 else ((PS_a, RPa), (PS_b, RPb))
            t2, Rt2 = t2s[g % 2], Rt2s[g % 2]
            t3, Rt3 = t3s[g % 2], Rt3s[g % 2]
            for k in range(8):
                mm(PU[:], wbig[:, 8 + k, g * 128:(g + 1) * 128], hnT[:, k, c0:c0 + 512], [Rwbig] + allh, [RPU], st=(k == 0), sp=(k == 7))
            for k in range(8):
                mm(PZ[:], wbf[g // 4][:, k, (g % 4) * 128:(g % 4 + 1) * 128], hnT[:, k, c0:c0 + 512], [Rwbf[g // 4]] + allh, [RPZ], st=(k == 0), sp=(k == 7))
            op("act", ("activation", dict(out=t3[:], in_=PZ[:], func=AF.Silu)), reads=[RPZ], writes=[Rt3])
            op("dve", ("tensor_tensor", dict(out=t2[:], in0=PU[:], in1=mixall[:, g, :], op=ALU.mult)), reads=[RPU, RU], writes=[Rt2])
            op("pool", ("tensor_tensor", dict(out=yo[g % 2][:], in0=t2[:], in1=t3[:], op=ALU.mult)), reads=[Rt2, Rt3], writes=[Ryo[g % 2]])
            S.dma(("dma_start", dict(out=yT[8 + g, :, 512 * tg:512 * (tg + 1)], in_=yo[g % 2][:])), reads=[Ryo[g % 2]], writes=[RyT[8 + g][tg]], q="pool")

    wo = I["e_w_out"].rearrange("(k p) n -> p k n", p=128)
    for q in range(2):
        for hf in range(4):
            S.dma(("dma_start", dict(out=stg[hf % 2][:], in_=wo[:, 8 * q:8 * q + 8, hf * 256:(hf + 1) * 256])), writes=[Rstg[hf % 2]])
            op("act", ("activation", dict(out=wbig[:, 8 * q:8 * q + 8, hf * 256:(hf + 1) * 256], in_=stg[hf % 2][:], func=AF.Copy)), reads=[Rstg[hf % 2]], writes=[Rwbig])
    ally = [r for row in RyT for r in row]
    h1ts = [sb("h1t%d" % i, [128, 1024]) for i in range(2)]; Rh1s = [R(), R()]
    for i in range(16):
        xb, Rx = xt[i % 2], Rxt[i % 2]
        h1t, Rh1 = h1ts[i % 2], Rh1s[i % 2]
        S.dma(("dma_start", dict(out=xb[:], in_=xh[1 + 128 * i: 1 + 128 * (i + 1), :])), writes=[Rx])
        S.dma(("dma_start", dict(out=ytl[i % 2][:], in_=yT[:, :, 128 * i:128 * (i + 1)].rearrange("k p t -> p k t"))), reads=ally, writes=[Rytl[i % 2]])
        for hf, (P_, RP_) in enumerate((((PS_a, RPa), (PS_b, RPb)) if i % 2 == 0 else ((PS_c, RPc), (PS_d, RPd)))):
            for k in range(16):
                mm(P_[:], ytl[i % 2][:, k, :], wbig[:, k, hf * 512:(hf + 1) * 512], [Rwbig, Rytl[i % 2]], [RP_], st=(k == 0), sp=(k == 15))
            op("dve", ("tensor_tensor", dict(out=h1t[:, hf * 512:(hf + 1) * 512], in0=P_[:], in1=xb[:, hf * 512:(hf + 1) * 512], op=ALU.add)), reads=[RP_, Rx], writes=[Rh1])
        S.dma(("dma_start", dict(out=O["h1"][128 * i:128 * (i + 1), :], in_=h1t[:])), reads=[Rh1], q="act")

        def dst(gt, i=i):
            op("dve", ("tensor_tensor", dict(out=hnT[:, :, 1 + 128 * i: 1 + 128 * (i + 1)], in0=PS_t[:], in1=gt[:].to_broadcast([128, 8, 128]), op=ALU.mult)), reads=[RPt, Rg], writes=[RhnT[i]])
        norm_tile(h1t, Rh1, gO, dst, None)

    wi1 = I["o_w_in"].rearrange("(k p) n -> p k n", p=128)
    blocks = [(c * 128, c * 128, False) for c in range(25)]
    blocks += [(3200 + c * 128, 3200 + c * 128, True) for c in range(8)]
    blocks += [(4736 + c * 128, 4224 + c * 128, True) for c in range(4)]
    ob = [sb("ob%d" % i, [128, 512]) for i in range(2)]; Rob = [R(), R()]
    oi = 0
    toks = []
    bi = 0
    nblocks = len(blocks)
    pairbuf = 0
    while bi < nblocks:
        sc, dr, act = blocks[bi]
        paired = (bi + 1 < nblocks) and (blocks[bi + 1][0] == sc + 128) and (blocks[bi + 1][2] == act)
        ncol = 256 if paired else 128
        buf = pairbuf % 2
        pairbuf += 1
        S.dma(("dma_start", dict(out=stg[buf][:, :, 0:ncol], in_=wi1[:, :, sc:sc + ncol])), writes=[Rstg[buf]])
        op("dve", ("tensor_copy", dict(out=wbf[buf][:, :, 0:ncol], in_=stg[buf][:, :, 0:ncol])), reads=[Rstg[buf]], writes=[Rwbf[buf]])
        for sub in range(2 if paired else 1):
            sc_, dr_, act_ = blocks[bi + sub]
            for j in range(4):
                P_, RP_ = ((PS_a, RPa), (PS_b, RPb), (PS_c, RPc), (PS_d, RPd))[j]
                c0 = 1 + 512 * j
                for k in range(8):
                    mm(P_[:], wbf[buf][:, k, sub * 128:(sub + 1) * 128], hnT[:, k, c0:c0 + 512], [Rwbf[buf]] + allh, [RP_], st=(k == 0), sp=(k == 7))
                o_, Ro = ob[oi % 2], Rob[oi % 2]
                oi += 1
                op("act", ("activation", dict(out=o_[:], in_=P_[:], func=(AF.Silu if act_ else AF.Copy))), reads=[RP_], writes=[Ro])
                toks.append(S.dma(("dma_start", dict(out=O["pT"][dr_:dr_ + 128, 512 * j:512 * (j + 1)], in_=o_[:])), reads=[Ro], q="act"))
        bi += 2 if paired else 1
    for hf in range(2):
        S.dma(("dma_start", dict(out=stg[hf][:], in_=wi1[:, :, 4224 + 256 * hf:4224 + 256 * (hf + 1)])), writes=[Rstg[hf]])
        op("act", ("activation", dict(out=wbf[0][:, :, 256 * hf:256 * (hf + 1)], in_=stg[hf][:], func=AF.Copy)), reads=[Rstg[hf]], writes=[Rwbf[0]])
    for i in range(16):
        c0 = 1 + 128 * i
        P_, RP_ = ((PS_a, RPa), (PS_b, RPb))[i % 2]
        for k in range(8):
            mm(P_[:], hnT[:, k, c0:c0 + 128], wbf[0][:, k, :], [Rwbf[0]] + allh, [RP_], st=(k == 0), sp=(k == 7))
        o_, Ro = ob[oi % 2], Rob[oi % 2]
        oi += 1
        op("act", ("activation", dict(out=o_[:], in_=P_[:], func=AF.Copy)), reads=[RP_], writes=[Ro])
        toks.append(S.dma(("dma_start", dict(out=O["fd"][128 * i:128 * (i + 1), :], in_=o_[:])), reads=[Ro], q="act"))
    return toks


def full_barrier(S):
    keys = list(S.cnt.items())
    for e in S.ENGS:
        waits = []
        for k, v in keys:
            if k == e:
                continue
            if S.seen[e].get(k, 0) < v:
                S.seen[e][k] = v
                waits.append((k, v))
        if waits:
            S.prog[e].append([waits, None, ("_none", 0)])


def emit_fnet(S, nc, I, ydT):
    R = Region
    op = S.op
    mm = lambda out, l, r_, rd, wr, st=True, sp=True: op("pe", ("matmul", dict(out=out, lhsT=l, rhs=r_, start=st, stop=sp)), reads=rd, writes=wr)
    toks = []
    with ExitStack() as es:
        sb = lambda name, shape, dt=F32: es.enter_context(nc.sbuf_tensor(name, shape, dt))
        ps = lambda name, shape, dt=F32: es.enter_context(nc.psum_tensor(name, shape, dt))
        xs = sb("f_xs", [128, 4096]); Rxs = R()
        xb = sb("f_xb", [128, 64, 128], BF16); Rxb = R()
        Fb = sb("f_F", [128, 256], BF16); RF = R()
        A_sb = sb("f_A", [64, 128, 256], BF16); RA = R()
        PQ = sb("f_PQ", [128, 2, 64, 128], BF16); RPQ = R()
        Tg = [[sb("f_T%d%d" % (i, j), [64, 16, 128], BF16) for j in range(2)] for i in range(2)]; RTg = [R(), R()]
        wf32 = sb("f_w32", [128, 128]); wfb = sb("f_wb", [128, 128], BF16); Rwf = R()
        Ccb = sb("f_Cc", [128, 128], BF16); mScb = sb("f_mSc", [128, 128], BF16); Rcs = R()
        Gb = sb("f_G", [128, 256], BF16); RG = R()
        ob = [sb("f_ob%d" % i, [128, 512]) for i in range(2)]; Rob = [R(), R()]
        PS = [ps("f_ps%d" % i, [128, 512]) for i in range(2)]; RPS = [R(), R()]
        S.dma(("dma_start", dict(out=Fb[:], in_=I["c_F"])), writes=[RF])
        S.dma(("dma_start", dict(out=Ccb[:], in_=I["c_Cc"])), writes=[Rcs])
        S.dma(("dma_start", dict(out=mScb[:], in_=I["c_mSc"])), writes=[Rcs])
        S.dma(("dma_start", dict(out=wf32[:], in_=I["fw"])), writes=[Rwf])
        op("dve", ("tensor_copy", dict(out=wfb[:], in_=wf32[:])), reads=[Rwf], writes=[Rwf])
        xbf = xb[:].rearrange("p l c -> p (l c)")
        for hf in range(2):
            S.dma(("dma_start", dict(out=xs[:], in_=I["fx"][:, hf * 4096:(hf + 1) * 4096])), writes=[Rxs])
            op("act", ("activation", dict(out=xbf[:, hf * 4096:(hf + 1) * 4096], in_=xs[:], func=AF.Copy)), reads=[Rxs], writes=[Rxb])
        for c2 in range(64):
            P_, RP_ = PS[c2 % 2], RPS[c2 % 2]
            for j in range(2):
                mm(P_[0:64, j * 256:(j + 1) * 256], xb[:, :, 2 * c2 + j], Fb[:], [Rxb, RF], [RP_])
            op("act" if c2 % 2 == 0 else "dve", ("activation", dict(out=A_sb[0:64, 2 * c2:2 * c2 + 2, :], in_=P_[0:64, :].rearrange("p (j k) -> p j k", j=2), func=AF.Copy)) if c2 % 2 == 0 else
               ("tensor_copy", dict(out=A_sb[0:64, 2 * c2:2 * c2 + 2, :], in_=P_[0:64, :].rearrange("p (j k) -> p j k", j=2))), reads=[RP_], writes=[RA])
        T1d = I["c_T1"].rearrange("p (k h) -> p k h", h=128)
        T2d = I["c_T2"].rearrange("p (k h) -> p k h", h=128)
        ei = 0
        for grp in range(8):
            tb = grp % 2
            S.dma(("dma_start", dict(out=Tg[tb][0][:], in_=T1d[:, grp * 16:(grp + 1) * 16, :])), writes=[RTg[tb]])
            S.dma(("dma_start", dict(out=Tg[tb][1][:], in_=T2d[:, grp * 16:(grp + 1) * 16, :])), writes=[RTg[tb]])
            for q in range(4):
                P_, RP_ = PS[ei % 2], RPS[ei % 2]
                for j in range(4):
                    kk_ = q * 4 + j
                    kl = grp * 16 + kk_
                    mm(P_[:, j * 128:(j + 1) * 128], A_sb[0:64, :, kl], Tg[tb][0][0:64, kk_, :], [RA, RTg[tb]], [RP_], st=True, sp=False)
                    mm(P_[:, j * 128:(j + 1) * 128], A_sb[0:64, :, 128 + kl], Tg[tb][1][0:64, kk_, :], [RA, RTg[tb]], [RP_], st=False, sp=True)
                kl0 = grp * 16 + q * 4
                for qq in range(2):
                    op("act" if qq == 0 else "dve",
                       ("activation", dict(out=PQ[:, qq, :, kl0:kl0 + 4].rearrange("p h l -> p l h"), in_=P_[:].rearrange("p (l q h) -> p l q h", l=4, q=2)[:, :, qq, :], func=AF.Copy)) if qq == 0 else
                       ("tensor_copy", dict(out=PQ[:, qq, :, kl0:kl0 + 4].rearrange("p h l -> p l h"), in_=P_[:].rearrange("p (l q h) -> p l q h", l=4, q=2)[:, :, qq, :])),
                       reads=[RP_], writes=[RPQ])
                ei += 1
        P_, RP_ = PS[0], RPS[0]
        mm(P_[:, 0:128], Ccb[:], wfb[:], [Rcs, Rwf], [RP_])
        mm(P_[:, 128:256], mScb[:], wfb[:], [Rcs, Rwf], [RP_])
        op("act", ("activation", dict(out=Gb[:], in_=P_[:, 0:256], func=AF.Copy)), reads=[RP_], writes=[RG])
        for t4 in range(16):
            P_, RP_ = PS[(t4 + 1) % 2], RPS[(t4 + 1) % 2]
            for j in range(4):
                kh = 4 * t4 + j
                mm(P_[:, j * 128:(j + 1) * 128], Gb[:, 0:128], PQ[:, 0, kh, :], [RG, RPQ], [RP_], st=True, sp=False)
                mm(P_[:, j * 128:(j + 1) * 128], Gb[:, 128:256], PQ[:, 1, kh, :], [RG, RPQ], [RP_], st=False, sp=True)
            o_, Ro = ob[t4 % 2], Rob[t4 % 2]
            op("act", ("activation", dict(out=o_[:], in_=P_[:], func=AF.Copy)), reads=[RP_], writes=[Ro])
            toks.append(S.dma(("dma_start", dict(out=ydT[:, 512 * t4:512 * (t4 + 1)], in_=o_[:])), reads=[Ro]))
    full_barrier(S)
    return toks


def emit_p3(S, nc, I, yout):
    sb = lambda name, shape, dt=F32: nc.alloc_sbuf_tensor(name, shape, dt)
    ps = lambda name, shape, dt=F32: nc.alloc_psum_tensor(name, shape, dt)
    R = Region
    op = S.op
    mm = lambda out, l, r_, rd, wr, st=True, sp=True: op("pe", ("matmul", dict(out=out, lhsT=l, rhs=r_, start=st, stop=sp)), reads=rd, writes=wr)
    stg = [sb("stg%d" % i, [128, 8, 256]) for i in range(2)]; Rstg = [R(), R()]
    wO = sb("wO", [128, 12, 1024], BF16); RwO = R()
    gN = sb("gN", [128, 1024]); RgN = R()
    gt_all = sb("gt_all", [128, 12, 2048], BF16); Rgt = R()
    ya = [sb("ya%d" % i, [128, 512]) for i in range(2)]; Rya = [R(), R()]
    ga = [sb("ga%d" % i, [128, 512]) for i in range(2)]; Rga = [R(), R()]
    h1t = [sb("h1t%d" % i, [128, 1024]) for i in range(2)]; Rh1 = [R(), R()]
    h2 = sb("h2", [128, 1024]); Rh2 = R()
    sq = sb("sq", [128, 1024]); Rsq = R()
    st = sb("st", [128, 8]); Rst = R()
    yo = [sb("yo%d" % i, [128, 1024]) for i in range(2)]; Ryo = [R(), R()]
    PS_a = ps("PS_a", [128, 512]); RPa = R()
    PS_b = ps("PS_b", [128, 512]); RPb = R()
    wo3 = I["o_w_out"].rearrange("(k p) n -> p k n", p=128)
    si = 0
    for (k0, nk) in ((0, 8), (8, 4)):
        for cq in range(4):
            b_ = si % 2; si += 1
            S.dma(("dma_start", dict(out=stg[b_][:, 0:nk, :], in_=wo3[:, k0:k0 + nk, cq * 256:(cq + 1) * 256])), writes=[Rstg[b_]])
            op("act", ("activation", dict(out=wO[:, k0:k0 + nk, cq * 256:(cq + 1) * 256], in_=stg[b_][:, 0:nk, :], func=AF.Copy)), reads=[Rstg[b_]], writes=[RwO])
    S.dma(("dma_start", dict(out=gN[:], in_=I["final_norm_g"].partition_broadcast(128))), writes=[RgN])
    ii = 0
    for blk in range(12):
        src = I["ycT"][blk * 128:(blk + 1) * 128] if blk < 8 else I["ydT"][(blk - 8) * 128:(blk - 7) * 128]
        gsrc = I["gT"][blk * 128:(blk + 1) * 128]
        for j in range(4):
            b_ = ii % 2; ii += 1
            S.dma(("dma_start", dict(out=ya[b_][:], in_=src[:, 512 * j:512 * (j + 1)])), writes=[Rya[b_]])
            S.dma(("dma_start", dict(out=ga[b_][:], in_=gsrc[:, 512 * j:512 * (j + 1)])), writes=[Rga[b_]])
            op("dve", ("tensor_tensor", dict(out=gt_all[:, blk, 512 * j:512 * (j + 1)], in0=ya[b_][:], in1=ga[b_][:], op=ALU.mult)), reads=[Rya[b_], Rga[b_]], writes=[Rgt])
    toks = []
    for i in range(16):
        hb, Rh = h1t[i % 2], Rh1[i % 2]
        S.dma(("dma_start", dict(out=hb[:], in_=I["h1"][128 * i:128 * (i + 1), :])), writes=[Rh])
        for hf, (P_, RP_) in enumerate(((PS_a, RPa), (PS_b, RPb))):
            for k in range(12):
                mm(P_[:], gt_all[:, k, 128 * i:128 * (i + 1)], wO[:, k, hf * 512:(hf + 1) * 512], [Rgt, RwO], [RP_], st=(k == 0), sp=(k == 11))
            op("dve", ("tensor_tensor", dict(out=h2[:, hf * 512:(hf + 1) * 512], in0=P_[:], in1=hb[:, hf * 512:(hf + 1) * 512], op=ALU.add)), reads=[RP_, Rh], writes=[Rh2])
        op("act", ("activation", dict(out=sq[:], in_=h2[:], func=AF.Square)), reads=[Rh2], writes=[Rsq])
        op("dve", ("reduce_sum", dict(out=st[:, 0:1], in_=sq[:], axis=AX.X)), reads=[Rsq], writes=[Rst])
        op("dve", ("tensor_scalar", dict(out=st[:, 1:2], in0=st[:, 0:1], scalar1=1.0 / 1024, scalar2=1e-6, op0=ALU.mult, op1=ALU.add)), reads=[Rst], writes=[Rst])
        op("act", ("activation", dict(out=st[:, 2:3], in_=st[:, 1:2], func=AF.Sqrt)), reads=[Rst], writes=[Rst])
        op("dve", ("reciprocal", dict(out=st[:, 3:4], in_=st[:, 2:3])), reads=[Rst], writes=[Rst])
        op("dve", ("tensor_scalar", dict(out=h2[:], in0=h2[:], scalar1=st[:, 3:4], scalar2=None, op0=ALU.mult)), reads=[Rh2, Rst], writes=[Rh2])
        o_, Ro = yo[i % 2], Ryo[i % 2]
        op("dve", ("tensor_tensor", dict(out=o_[:], in0=h2[:], in1=gN[:], op=ALU.mult)), reads=[Rh2, RgN], writes=[Ro])
        toks.append(S.dma(("dma_start", dict(out=yout[128 * i:128 * (i + 1), :], in_=o_[:])), reads=[Ro], q="pool"))
    return toks


def _mk(nc, name, shape, dt=None, out=False):
    return nc.dram_tensor(name, list(shape), dt or F32, kind=("ExternalOutput" if out else "ExternalInput")).ap()


W1 = ["e_norm_g", "e_w_in", "e_conv_w", "e_sgu_ln_g", "e_sgu_ln_b", "e_sgu_w", "e_sgu_b", "e_w_out", "o_norm_g", "o_w_in"]


def build_l1(shapes):
    nc = bass.Bass("TRN2", target_bir_lowering=False)
    I = {"xh": _mk(nc, "xh", [2050, 1024]), "c_ident": _mk(nc, "c_ident", [128, 128])}
    for n in W1:
        I[n] = _mk(nc, n, shapes[n])
    O = {"h1": _mk(nc, "h1", [2048, 1024], out=True), "pT": _mk(nc, "pT", [4736, 2048], out=True),
         "fd": _mk(nc, "fd", [2048, 512], out=True)}
    S = Sched(nc)
    toks = emit_p1(S, nc, I, O)
    S.barrier_on("sp", toks)
    S.finalize()
    return nc


def build_l2(consts):
    NB, T = 2, 8192
    nc = bass.Bass("TRN2", target_bir_lowering=False)
    pr, pk, pv, pwa = (_mk(nc, n, [128, NB, T + 2]) for n in ("pr", "pk", "pv", "pwa"))
    prm = _mk(nc, "prm", [128, 17]); w2a2 = _mk(nc, "w2a2", [128, 2, 128])
    A = {k: _mk(nc, k, v.shape) for k, v in consts.items()}
    FI = {"fx": _mk(nc, "fx", [128, 8192]), "fw": _mk(nc, "fw", [128, 128]),
          "c_F": _mk(nc, "c_F", [128, 256], BF16), "c_T1": _mk(nc, "c_T1", [64, 16384], BF16),
          "c_T2": _mk(nc, "c_T2", [64, 16384], BF16), "c_Cc": _mk(nc, "c_Cc", [128, 128], BF16),
          "c_mSc": _mk(nc, "c_mSc", [128, 128], BF16)}
    yout = _mk(nc, "yout", [128, NB, T], out=True)
    ydT = _mk(nc, "ydT", [128, T], out=True)
    S = Sched(nc)
    toks = emit_fnet(S, nc, FI, ydT)
    toks += emit_rwkv(S, nc, A, pr, pk, pv, pwa, prm, w2a2, yout, NB, T)
    S.barrier_on("sp", toks)
    S.finalize()
    return nc


def build_l3():
    nc = bass.Bass("TRN2", target_bir_lowering=False)
    I = {"ycT": _mk(nc, "ycT", [1024, 2048]), "ydT": _mk(nc, "ydT", [512, 2048]), "gT": _mk(nc, "gT", [1536, 2048]),
         "h1": _mk(nc, "h1", [2048, 1024]), "o_w_out": _mk(nc, "o_w_out", [1536, 1024]),
         "final_norm_g": _mk(nc, "final_norm_g", [1024])}
    y = _mk(nc, "y", [2048, 1024], out=True)
    S = Sched(nc)
    toks = emit_p3(S, nc, I, y)
    S.barrier_on("sp", toks)
    S.finalize()
    return nc


def fnet_tables():
    import ml_dtypes
    N = 8192
    nh = np.arange(128); kl = np.arange(128)
    ang = 2 * np.pi * np.outer(nh, kl) / 128
    F = np.concatenate([np.cos(ang), np.sin(ang)], axis=1)
    nl = np.arange(64)[:, None, None]; klo = np.arange(128)[None, :, None]; kh = np.arange(64)[None, None, :]
    beta = 2 * np.pi * ((nl * (klo + 128 * kh)) % N) / N
    T1 = np.concatenate([np.cos(beta), np.sin(beta)], axis=2).reshape(64, 16384)
    T2 = np.concatenate([-np.sin(beta), np.cos(beta)], axis=2).reshape(64, 16384)
    c = np.arange(128); phi = 2 * np.pi * np.outer(c, c) / 128
    nrm = 1 / np.sqrt(N * 128)
    bf = lambda a: np.ascontiguousarray(a.astype(np.float32)).astype(ml_dtypes.bfloat16)
    return {"c_F": bf(F), "c_T1": bf(T1), "c_T2": bf(T2), "c_Cc": bf(np.cos(phi) * nrm), "c_mSc": bf(-np.sin(phi) * nrm)}


def kernel(**inputs):
    f32 = lambda a: np.ascontiguousarray(np.asarray(a), dtype=np.float32)
    inp = {k: f32(v) for k, v in inputs.items()}
    x = inp["x"]
    ncores = 8
    cores = list(range(ncores))
    w1 = {n: np.ascontiguousarray(inp[n][0]) for n in W1}
    ident = np.eye(128, dtype=np.float32)
    maps = []
    for c in cores:
        b, s0 = c // 4, (c % 4) * 2048
        xh = np.zeros((2050, 1024), np.float32)
        xh[1:2049] = x[b, s0:s0 + 2048]
        if s0 > 0:
            xh[0] = x[b, s0 - 1]
        if s0 + 2048 < 8192:
            xh[2049] = x[b, s0 + 2048]
        m = {"xh": xh, "c_ident": ident}
        m.update(w1)
        maps.append(m)
    nc1 = build_l1({n: w1[n].shape for n in W1})
    r1 = run_bass_kernel_spmd(nc1, maps, core_ids=cores).results
    PT = np.concatenate([np.asarray(r["pT"]) for r in r1], axis=1)
    FD = np.concatenate([np.asarray(r["fd"]) for r in r1], axis=0)
    consts = build_consts_np()
    ft = fnet_tables()
    mu, w0, w2, a0, a2 = inp["o_mu"][0], inp["o_w0"][0], inp["o_w2"][0], inp["o_a0"][0], inp["o_a2"][0]
    k_k, k_a, r_k = inp["o_k_k"][0], inp["o_k_a"][0], inp["o_r_k"][0].reshape(-1)
    lg, lb = inp["o_lnx_g"][0], inp["o_lnx_b"][0]
    PT3 = PT.reshape(4736, 2, 8192)
    pad = lambda a: np.ascontiguousarray(np.pad(a, ((0, 0), (0, 0), (1, 1))))
    maps = []
    for c in cores:
        ch = slice(c * 128, (c + 1) * 128)
        m = {"pr": pad(PT3[0:1024][ch]), "pk": pad(PT3[1024:2048][ch]), "pv": pad(PT3[2048:3072][ch]),
             "pwa": pad(PT3[3072:3200])}
        prm = np.zeros((128, 17), np.float32)
        for d in range(2):
            prm[:, 0 + d] = mu[d, 0:1024][ch]; prm[:, 2 + d] = mu[d, 1024:2048][ch]; prm[:, 4 + d] = mu[d, 2048:3072][ch]
            prm[:, 6 + d] = mu[d, 3072:3200]; prm[:, 8 + d] = w0[d][ch]; prm[:, 10 + d] = a0[d][ch]
        prm[:, 12] = k_k[ch]; prm[:, 13] = k_a[ch]; prm[:, 14] = r_k[ch]; prm[:, 15] = lg[ch]; prm[:, 16] = lb[ch]
        m["prm"] = prm
        m["w2a2"] = np.ascontiguousarray(np.concatenate([w2[:, :, ch], a2[:, :, ch]], axis=1).transpose(1, 0, 2))
        m.update(consts)
        b, g = c // 4, c % 4
        m["fx"] = np.ascontiguousarray(FD[b * 8192:(b + 1) * 8192, g * 128:(g + 1) * 128]).reshape(128, 8192)
        m["fw"] = np.ascontiguousarray(inp["o_fnet_w"][0, g])
        m.update(ft)
        maps.append(m)
    nc2 = build_l2(consts)
    r2 = run_bass_kernel_spmd(nc2, maps, core_ids=cores).results
    YC = np.concatenate([np.asarray(r["yout"]).reshape(128, 16384) for r in r2], axis=0)
    YD = np.concatenate([np.concatenate([np.asarray(r2[b * 4 + g]["ydT"]) for g in range(4)], axis=0) for b in range(2)], axis=1)
    maps = []
    for c in cores:
        ts = slice(c * 2048, (c + 1) * 2048)
        maps.append({"ycT": np.ascontiguousarray(YC[:, ts]), "ydT": np.ascontiguousarray(YD[:, ts]),
                     "gT": np.ascontiguousarray(PT[3200:4736, ts]), "h1": np.asarray(r1[c]["h1"]),
                     "o_w_out": np.ascontiguousarray(inp["o_w_out"][0]), "final_norm_g": inp["final_norm_g"]})
    nc3 = build_l3()
    r3 = run_bass_kernel_spmd(nc3, maps, core_ids=cores).results
    y = np.concatenate([np.asarray(r["y"]) for r in r3], axis=0).reshape(2, 8192, 1024)
    return y.astype(np.float32)
```

```python
from contextlib import ExitStack
import itertools
import numpy as np
import concourse.bass as bass
import concourse.mybir as mybir
from concourse.bass_utils import run_bass_kernel_spmd


F32 = mybir.dt.float32
BF16 = mybir.dt.bfloat16
AF = mybir.ActivationFunctionType
ALU = mybir.AluOpType
AX = mybir.AxisListType

N_DMA_SEMS = 8


class Region:
    __slots__ = ("w", "r", "name")

    def __init__(self, name=""):
        self.w = None
        self.r = {}
        self.name = name


class Sched:
    ENGS = ("pe", "dve", "act", "pool", "sp")

    def __init__(self, nc):
        self.nc = nc
        self.prog = {e: [] for e in self.ENGS}
        self.cnt = {}
        self.seen = {e: {} for e in self.ENGS}
        self.dma_rr = {e: 0 for e in self.ENGS}
        self.dma_last = {}
        self.same_engine_raw = True
        self.cut = 0
        self.raw_only = True
        self.nrec = 0
        self.log = []

    def _collect(self, eng, mykey, reads, writes):
        waits = {}

        def need(tok, kind):
            if tok is None:
                return
            k, v = tok
            if k == mykey:
                if eng == "pe":
                    return
                if not self.same_engine_raw:
                    return
                if self.raw_only and kind != "raw":
                    return
            if waits.get(k, 0) < v:
                waits[k] = v

        for R in reads:
            need(R.w, "raw")
        for R in writes:
            need(R.w, "waw")
            for k, v in R.r.items():
                need((k, v), "war")
        out = []
        seen = self.seen[eng]
        for k, v in waits.items():
            if seen.get(k, 0) < v:
                seen[k] = v
                out.append((k, v))
        return out

    def _commit(self, tok, reads, writes):
        for R in writes:
            R.w = tok
            R.r = {}
        k, v = tok
        for R in reads:
            if R.r.get(k, 0) < v:
                R.r[k] = v

    def op(self, eng, fn, reads=(), writes=()):
        self.nrec += 1
        if self.cut and self.nrec > self.cut:
            return None
        if self.cut:
            self.log.append((self.nrec, eng, fn[0] if isinstance(fn, tuple) else "fn", str(fn[1].get("out", ""))[:120] if isinstance(fn, tuple) else ""))
        key = eng
        waits = self._collect(eng, key, reads, writes)
        idx = self.cnt.get(key, 0) + 1
        self.cnt[key] = idx
        tok = (key, idx)
        self.prog[eng].append([waits, fn, tok])
        self._commit(tok, reads, writes)
        return tok

    def dma(self, fn, reads=(), writes=(), q="sp"):
        self.nrec += 1
        if self.cut and self.nrec > self.cut:
            return None
        i = self.dma_rr[q]
        self.dma_rr[q] = (i + 1) % N_DMA_SEMS
        key = "dma_%s_%d" % (q, i)
        waits = self._collect(q, key, reads, writes)
        prev = self.cnt.get(key, 0)
        if prev > 0 and self.seen[q].get(key, 0) < prev:
            self.seen[q][key] = prev
            waits.append((key, prev))
        idx = prev + 1
        self.cnt[key] = idx
        tok = (key, idx)
        self.prog[q].append([waits, fn, tok])
        self._commit(tok, reads, writes)
        return tok

    def finalize(self):
        nc = self.nc
        waited = {}
        for e in self.ENGS:
            for waits, fn, tok in self.prog[e]:
                for k, v in waits:
                    waited.setdefault(k, set()).add(v)
        self.final_waits = []
        sem_of = {}
        val_of = {}
        for k, s in waited.items():
            sem_of[k] = nc.alloc_semaphore("s_" + k)
            isdma = k.startswith("dma_")
            step = 16 if isdma else 1
            if isdma:
                val_of[k] = None
            else:
                val_of[k] = {v: (i + 1) for i, v in enumerate(sorted(s))}
        engobj = {"pe": nc.tensor, "dve": nc.vector, "act": nc.scalar,
                  "pool": nc.gpsimd, "sp": nc.sync}

        def value(k, v):
            if val_of[k] is None:
                return 16 * v
            return val_of[k][v]

        def emit(e):
            def body(eng):
                for waits, fn, tok in self.prog[e]:
                    for k, v in waits:
                        eng.wait_ge(sem_of[k], value(k, v))
                    if fn is None:
                        continue
                    if isinstance(fn, tuple):
                        ins = getattr(eng, fn[0])(**fn[1])
                    else:
                        ins = fn(eng)
                    k, v = tok
                    if k in sem_of:
                        if val_of[k] is None:
                            ins.then_inc(sem_of[k], 16)
                        elif v in val_of[k]:
                            ins.then_inc(sem_of[k], 1)
            return body

        with nc.Block() as block:
            for e, dec in (("sp", block.sync), ("pe", block.tensor), ("dve", block.vector),
                           ("act", block.scalar), ("pool", block.gpsimd)):
                if self.prog[e]:
                    dec(emit(e))
        self.n_sems = len(sem_of)
        return self.n_sems

    def barrier_on(self, eng, toks):
        waits = []
        for tk in toks:
            if tk is None:
                continue
            k, v = tk
            if self.seen[eng].get(k, 0) < v:
                self.seen[eng][k] = v
                waits.append((k, v))
        if waits:
            self.prog[eng].append([waits, None, ("_none", 0)])


C = 128
BLK = 512
NEG_E = -float(np.exp(-0.5))
GN_EPS = 64e-5


def build_consts_np():
    idx = np.arange(128)
    lt = (idx[:, None] < idx[None, :]).astype(np.float32)
    le = (idx[:, None] <= idx[None, :]).astype(np.float32)
    gt = lt.T.copy()
    ge = le.T.copy()
    m4f = np.stack([lt, gt, gt, le], axis=1)
    m4b = np.stack([gt, lt, lt, ge], axis=1)
    mk = np.stack([le, ge], axis=1)
    ident = np.eye(128, dtype=np.float32)
    bd = np.kron(np.eye(2, dtype=np.float32), np.ones((64, 64), np.float32))
    scanm = np.ones((128, BLK), np.float32)
    scanm[:, ::C] = 0.0
    return {"c_m4": np.stack([m4f, m4b], axis=1).reshape(128, 2 * 4 * 128).copy(),
            "c_mk": mk.reshape(128, 256).copy(), "c_ident": ident, "c_bd": bd, "c_scanm": scanm}


XST = False


def emit_rwkv(S, nc, A, pr, pk, pv, pwa, prm, w2a2, yout, NB, T):
    sb = lambda name, shape, dt=F32: nc.alloc_sbuf_tensor(name, shape, dt)
    ps = lambda name, shape, dt=F32: nc.alloc_psum_tensor(name, shape, dt)
    R = Region
    nblk = T // BLK

    m4f = sb("m4f", [128, 2, 4, 128]); Rm4 = R()
    mkf = sb("mkf", [128, 2, 128]); Rmk = R()
    identf = sb("identf", [128, 128]); Ridf = R()
    identb = sb("identb", [128, 128], BF16); Ridb = R()
    bdf = sb("bdf", [128, 128]); Rbd = R()
    bdr = sb("bdr", [128, 128]); Rbdr = R()
    bdm = sb("bdm", [128, 128]); Rbdm = R()
    scanm = sb("scanm", [128, BLK]); Rsc = R()
    prmt = sb("prmt", [128, 17]); Rprm = R()
    w2f = sb("w2f", [128, 2, 128]); Rw2f = R()
    w2b = sb("w2b", [128, 2, 128], BF16); Rw2b = R()
    S.dma(("dma_start", dict(out=m4f[:].rearrange("p a b c -> p (a b c)"), in_=A["c_m4"])), writes=[Rm4])
    S.dma(("dma_start", dict(out=mkf[:].rearrange("p a c -> p (a c)"), in_=A["c_mk"])), writes=[Rmk])
    S.dma(("dma_start", dict(out=identf[:], in_=A["c_ident"])), writes=[Ridf])
    S.dma(("dma_start", dict(out=bdf[:], in_=A["c_bd"])), writes=[Rbd])
    S.dma(("dma_start", dict(out=scanm[:], in_=A["c_scanm"])), writes=[Rsc])
    S.dma(("dma_start", dict(out=prmt[:], in_=prm)), writes=[Rprm])
    S.dma(("dma_start", dict(out=w2f[:], in_=w2a2)), writes=[Rw2f])
    S.op("dve", ("tensor_copy", dict(out=identb[:], in_=identf[:])), reads=[Ridf], writes=[Ridb])
    S.op("dve", ("tensor_copy", dict(out=w2b[:], in_=w2f[:])), reads=[Rw2f], writes=[Rw2b])
    PM = lambda c: prmt[:, c:c + 1]
    S.op("dve", ("tensor_scalar", dict(out=bdr[:], in0=bdf[:], scalar1=PM(14), scalar2=None, op0=ALU.mult)), reads=[Rbd, Rprm], writes=[Rbdr])
    S.op("dve", ("tensor_scalar", dict(out=bdm[:], in0=bdf[:], scalar1=1.0 / 64, scalar2=None, op0=ALU.mult)), reads=[Rbd], writes=[Rbdm])

    def T2(name, dt=F32, n=BLK):
        return sb(name, [128, n], dt), R()
    ld = {}
    for nm in ("pr", "pk", "pv", "pwa"):
        ld[nm] = (sb("ld_" + nm, [128, BLK + 2]), R())
    tmp, Rtmp = T2("tmp")
    qr, Rqr = T2("qr"); qk, Rqk = T2("qk"); qv, Rqv = T2("qv"); qwa, Rqwa = T2("qwa")
    twa, Rtwa = T2("twa", BF16)
    sw, Rsw = T2("sw"); asg, Rasg = T2("asg")
    logw, Rlogw = T2("logw"); lin, Rlin = T2("lin"); linm, Rlinm = T2("linm"); lexm, Rlexm = T2("lexm")
    lex, Rlex = T2("lex"); lint, Rlint = T2("lint")
    e1, Re1 = T2("e1"); e1x, Re1x = T2("e1x"); e2, Re2 = T2("e2"); e3S = [sb("e3%d" % i, [128, BLK]) for i in range(2)]; Re3S = [R(), R()]; e3x, Re3x = T2("e3x"); e4, Re4 = T2("e4")
    kk, Rkk = T2("kk"); kk2, Rkk2 = T2("kk2"); rin, Rrin = T2("rin"); kkn, Rkkn = T2("kkn")
    kp, Rkp = T2("kp"); bv, Rbv = T2("bv"); rk, Rrk = T2("rk")
    rtS = [sb("rt%d" % i, [128, BLK], BF16) for i in range(2)]; RrtS = [R(), R()]; atS = [sb("at%d" % i, [128, BLK], BF16) for i in range(2)]; RatS = [R(), R()]; ktS = [sb("kt%d" % i, [128, BLK], BF16) for i in range(2)]; RktS = [R(), R()]; btS = [sb("bt%d" % i, [128, BLK], BF16) for i in range(2)]; RbtS = [R(), R()]
    r0S = [sb("r0%d" % i, [128, BLK]) for i in range(2)]; Rr0S = [R(), R()]; a0bS = [sb("a0b%d" % i, [128, BLK], BF16) for i in range(2)]; Ra0bS = [R(), R()]; kEbS = [sb("kEb%d" % i, [128, BLK], BF16) for i in range(2)]; RkEbS = [R(), R()]; bEbS = [sb("bEb%d" % i, [128, BLK], BF16) for i in range(2)]; RbEbS = [R(), R()]
    qvbS = [sb("qvb%d" % i, [128, BLK], BF16) for i in range(2)]; RqvbS = [R(), R()]
    ysum = sb("ysum", [128, T]); Rys = [R() for _ in range(T // C)]
    bsum = sb("bsum", [128, T]); Rbs = [R() for _ in range(nblk)]
    TT = [sb("TT%d" % i, [128, 4, 128], BF16) for i in range(2)]; RTT = [R(), R()]
    SBM = [sb("SBM%d" % i, [128, 2, 4, 128], BF16) for i in range(2)]; RSBM = [R(), R()]
    MKR = [sb("MKR%d" % i, [128, 2, 128]) for i in range(2)]; RMKR = [R(), R()]
    SX = [sb("SX%d" % i, [128, 2, 192], BF16) for i in range(2)]; RSX = [R(), R()]
    SAB = [sb("SAB%d" % i, [128, 2, 2, 128], BF16) for i in range(2)]; RSAB = [R(), R()]
    Gb = sb("Gb", [128, 128], BF16); RGb = R()
    Hb = sb("Hb", [128, 2, 128], BF16); RHb = R()
    Pb = sb("Pb", [128, 64], BF16); RPb = R()
    Zb = sb("Zb", [128, 2, 64], BF16); RZb = R()
    STz = [sb("STz%d" % h, [128, 64], BF16) for h in range(2)]; RST = [R(), R()]
    identP = sb("identP", [128, 64]); mkb = sb("mkb", [128, 2, 2, 128])
    HS = [slice(0, 64), slice(64, 128)]
    fin1, Rfin1 = T2("fin1"); fin2, Rfin2 = T2("fin2"); fin3, Rfin3 = T2("fin3")

    PS_M = ps("PS_M", [128, 2, 4, 128]); RPS_M = R()
    PS_K = ps("PS_K", [128, 512]); RPS_K = R()
    PS_X = [ps("PS_X%d" % h, [128, 512]) for h in range(2)]; RPS_X = R()
    PS_AB = ps("PS_AB", [128, 2, 2, 128]); RPS_AB = R()
    PS_G = ps("PS_G", [128, 512]); RPS_G = R()
    PS_T = ps("PS_T", [128, 8, 128], BF16); RPS_T = R()
    PS_P1 = PS_AB[:].rearrange("p a b c -> p (a b c)"); RPS_P1 = RPS_AB
    PS_P2 = PS_P1; RPS_P2 = RPS_AB
    mm = lambda out, l, r_, rd, wr, st=True, sp=True, sg=False: S.op("pe", ("matmul", dict(out=out, lhsT=l, rhs=r_, start=st, stop=sp, skip_group_check=sg)), reads=rd, writes=wr)
    S.op("pool", ("tensor_copy", dict(out=identP[0:64, :], in_=identf[0:64, 0:64])), reads=[Ridf], writes=[Ridf])
    S.op("pool", ("tensor_copy", dict(out=identP[64:128, :], in_=identf[64:128, 64:128])), reads=[Ridf], writes=[Ridf])
    for h in range(2):
        S.op("pool", ("tensor_copy", dict(out=mkb[:, :, h, :], in_=mkf[:])), reads=[Rmk], writes=[Rmk])
    ytmp = sb("ytmp", [128, 128]); Rytmp = R()
    out_toks = []
    NFILL = 4
    NPREP = 2
    def prep_gen(b, d, blk, pp):
        bwd = (d == 1)
        midc, totc = (C // 2 - 1, C - 1) if not bwd else (C // 2, 0)
        t0 = blk * BLK
        rt_, Rrt_ = rtS[pp], RrtS[pp]
        at_, Rat_ = atS[pp], RatS[pp]
        kt_, Rkt_ = ktS[pp], RktS[pp]
        bt_, Rbt_ = btS[pp], RbtS[pp]
        r0_, Rr0_ = r0S[pp], Rr0S[pp]
        a0b_, Ra0b_ = a0bS[pp], Ra0bS[pp]
        kEb_, RkEb_ = kEbS[pp], RkEbS[pp]
        bEb_, RbEb_ = bEbS[pp], RbEbS[pp]
        qvb_, Rqvb_ = qvbS[pp], RqvbS[pp]
        e3_, Re3_ = e3S[pp], Re3S[pp]
        for nm, src in (("pr", pr), ("pk", pk), ("pv", pv), ("pwa", pwa)):
            tl, Rl = ld[nm]
            S.dma(("dma_start", dict(out=tl[:], in_=src[:, b, t0:t0 + BLK + 2])), writes=[Rl])
            yield
        sh = (slice(0, BLK) if not bwd else slice(2, BLK + 2))
        cur = slice(1, BLK + 1)
        for nm, q, Rq, mc in (("pr", qr, Rqr, 0), ("pk", qk, Rqk, 2), ("pv", qv, Rqv, 4), ("pwa", qwa, Rqwa, 6)):
            tl, Rl = ld[nm]
            S.op("dve", ("tensor_tensor", dict(out=tmp[:], in0=tl[:, sh], in1=tl[:, cur], op=ALU.subtract)), reads=[Rl], writes=[Rtmp])
            yield
            S.op("dve", ("scalar_tensor_tensor", dict(out=q[:], in0=tmp[:], scalar=PM(mc + d), in1=tl[:, cur], op0=ALU.mult, op1=ALU.add)), reads=[Rtmp, Rl, Rprm], writes=[Rq])
            yield
        S.op("act", ("activation", dict(out=twa[0:64, :], in_=qwa[0:64, :], func=AF.Tanh)), reads=[Rqwa], writes=[Rtwa])
        yield
        S.op("dve", ("tensor_copy", dict(out=twa[64:128, :], in_=qwa[64:128, :])), reads=[Rqwa], writes=[Rtwa])
        yield
        S.op("pe", ("matmul", dict(out=PS_P1, lhsT=w2b[0:64, d, :], rhs=twa[0:64, :], start=True, stop=True)), reads=[Rw2b, Rtwa], writes=[RPS_P1])
        S.op("act", ("activation", dict(out=sw[:], in_=PS_P1, func=AF.Sigmoid, bias=PM(8 + d))), reads=[RPS_P1, Rprm], writes=[Rsw])
        yield
        S.op("pe", ("matmul", dict(out=PS_P2, lhsT=w2b[64:128, d, :], rhs=twa[64:128, :], start=True, stop=True)), reads=[Rw2b, Rtwa], writes=[RPS_P2])
        S.op("act", ("activation", dict(out=asg[:], in_=PS_P2, func=AF.Sigmoid, bias=PM(10 + d))), reads=[RPS_P2, Rprm], writes=[Rasg])
        yield
        S.op("dve", ("tensor_scalar", dict(out=logw[:], in0=sw[:], scalar1=NEG_E, scalar2=None, op0=ALU.mult)), reads=[Rsw], writes=[Rlogw])
        yield
        S.op("dve", ("tensor_tensor_scan", dict(out=lin[:], data0=scanm[:], data1=logw[:], initial=0.0, op0=ALU.mult, op1=ALU.add)), reads=[Rsc, Rlogw], writes=[Rlin])
        yield
        lin3 = lambda tl: tl[:].rearrange("p (c t) -> p c t", t=C)
        bc = lambda tl, col: lin3(tl)[:, :, col:col + 1].to_broadcast([128, BLK // C, C])
        if bwd:
            S.op("dve", ("tensor_tensor", dict(out=lin3(tmp), in0=bc(lin, C - 1), in1=lin3(lin), op=ALU.subtract)), reads=[Rlin], writes=[Rtmp])
            yield
            S.op("dve", ("tensor_tensor", dict(out=lin[:], in0=tmp[:], in1=logw[:], op=ALU.add)), reads=[Rtmp, Rlogw], writes=[Rlin])
            yield
        S.op("dve", ("tensor_tensor", dict(out=lin3(linm), in0=lin3(lin), in1=bc(lin, midc), op=ALU.subtract)), reads=[Rlin], writes=[Rlinm])
        yield
        S.op("dve", ("tensor_tensor", dict(out=lexm[:], in0=linm[:], in1=logw[:], op=ALU.subtract)), reads=[Rlinm, Rlogw], writes=[Rlexm])
        yield
        S.op("dve", ("tensor_tensor", dict(out=lex[:], in0=lin[:], in1=logw[:], op=ALU.subtract)), reads=[Rlin, Rlogw], writes=[Rlex])
        yield
        S.op("dve", ("tensor_tensor", dict(out=lin3(lint), in0=lin3(lin), in1=bc(lin, totc), op=ALU.subtract)), reads=[Rlin], writes=[Rlint])
        yield
        S.op("act", ("activation", dict(out=e1[:], in_=linm[:], func=AF.Exp)), reads=[Rlinm], writes=[Re1])
        yield
        S.op("act", ("activation", dict(out=e1x[:], in_=lexm[:], func=AF.Exp)), reads=[Rlexm], writes=[Re1x])
        yield
        S.op("act", ("activation", dict(out=e2[:], in_=linm[:], func=AF.Exp, scale=-1.0)), reads=[Rlinm], writes=[Re2])
        yield
        S.op("act", ("activation", dict(out=e3_[:], in_=lin[:], func=AF.Exp)), reads=[Rlin], writes=[Re3_])
        yield
        S.op("act", ("activation", dict(out=e3x[:], in_=lex[:], func=AF.Exp)), reads=[Rlex], writes=[Re3x])
        yield
        S.op("act", ("activation", dict(out=e4[:], in_=lint[:], func=AF.Exp, scale=-1.0)), reads=[Rlint], writes=[Re4])
        yield
        S.op("dve", ("tensor_scalar", dict(out=kk[:], in0=qk[:], scalar1=PM(12), scalar2=None, op0=ALU.mult)), reads=[Rqk, Rprm], writes=[Rkk])
        yield
        S.op("pool", ("tensor_tensor", dict(out=kk2[:], in0=kk[:], in1=kk[:], op=ALU.mult)), reads=[Rkk], writes=[Rkk2])
        yield
        S.op("pe", ("matmul", dict(out=PS_P1, lhsT=bdf[:], rhs=kk2[:], start=True, stop=True)), reads=[Rbd, Rkk2], writes=[RPS_P1])
        S.op("dve", ("tensor_scalar", dict(out=rin[:], in0=PS_P1, scalar1=1e-12, scalar2=None, op0=ALU.max)), reads=[RPS_P1], writes=[Rrin])
        yield
        S.op("act", ("activation", dict(out=rin[:], in_=rin[:], func=AF.Sqrt)), reads=[Rrin], writes=[Rrin])
        yield
        S.op("dve", ("reciprocal", dict(out=rin[:], in_=rin[:])), reads=[Rrin], writes=[Rrin])
        yield
        S.op("dve", ("tensor_tensor", dict(out=kkn[:], in0=kk[:], in1=rin[:], op=ALU.mult)), reads=[Rkk, Rrin], writes=[Rkkn])
        yield
        S.op("dve", ("tensor_scalar", dict(out=tmp[:], in0=asg[:], scalar1=-1.0, scalar2=PM(13), op0=ALU.add, op1=ALU.mult)), reads=[Rasg, Rprm], writes=[Rtmp])
        yield
        S.op("dve", ("scalar_tensor_tensor", dict(out=kp[:], in0=tmp[:], scalar=1.0, in1=qk[:], op0=ALU.add, op1=ALU.mult)), reads=[Rtmp, Rqk], writes=[Rkp])
        yield
        S.op("pool", ("tensor_tensor", dict(out=bv[:], in0=kkn[:], in1=asg[:], op=ALU.mult)), reads=[Rkkn, Rasg], writes=[Rbv])
        yield
        S.op("pool", ("tensor_tensor", dict(out=rk[:], in0=qr[:], in1=kp[:], op=ALU.mult)), reads=[Rqr, Rkp], writes=[Rrk])
        yield
        S.op("pe", ("matmul", dict(out=PS_P2, lhsT=bdr[:], rhs=rk[:], start=True, stop=True)), reads=[Rbdr, Rrk], writes=[RPS_P2])
        bsl = bsum[:, t0:t0 + BLK]
        if d == 0:
            S.op("dve", ("tensor_tensor", dict(out=bsl, in0=PS_P2, in1=qv[:], op=ALU.mult)), reads=[RPS_P2, Rqv], writes=[Rbs[blk]])
            yield
        else:
            S.op("dve", ("tensor_tensor", dict(out=tmp[:], in0=PS_P2, in1=qv[:], op=ALU.mult)), reads=[RPS_P2, Rqv], writes=[Rtmp])
            yield
            S.op("pool", ("tensor_tensor", dict(out=bsl, in0=bsl, in1=tmp[:], op=ALU.add)), reads=[Rtmp, Rbs[blk]], writes=[Rbs[blk]])
            yield
        S.op("dve", ("tensor_tensor", dict(out=rt_[:], in0=qr[:], in1=e1[:], op=ALU.mult)), reads=[Rqr, Re1], writes=[Rrt_])
        yield
        S.op("dve", ("scalar_tensor_tensor", dict(out=at_[:], in0=kkn[:], scalar=-1.0, in1=e1x[:], op0=ALU.mult, op1=ALU.mult)), reads=[Rkkn, Re1x], writes=[Rat_])
        yield
        S.op("pool", ("tensor_tensor", dict(out=kt_[:], in0=kp[:], in1=e2[:], op=ALU.mult)), reads=[Rkp, Re2], writes=[Rkt_])
        yield
        S.op("pool", ("tensor_tensor", dict(out=bt_[:], in0=bv[:], in1=e2[:], op=ALU.mult)), reads=[Rbv, Re2], writes=[Rbt_])
        yield
        S.op("pool", ("tensor_tensor", dict(out=r0_[:], in0=qr[:], in1=e3_[:], op=ALU.mult)), reads=[Rqr, Re3_], writes=[Rr0_])
        yield
        S.op("dve", ("scalar_tensor_tensor", dict(out=a0b_[:], in0=kkn[:], scalar=-1.0, in1=e3x[:], op0=ALU.mult, op1=ALU.mult)), reads=[Rkkn, Re3x], writes=[Ra0b_])
        yield
        S.op("pool", ("tensor_tensor", dict(out=kEb_[:], in0=kp[:], in1=e4[:], op=ALU.mult)), reads=[Rkp, Re4], writes=[RkEb_])
        yield
        S.op("pool", ("tensor_tensor", dict(out=bEb_[:], in0=bv[:], in1=e4[:], op=ALU.mult)), reads=[Rbv, Re4], writes=[RbEb_])
        yield
        S.op("act", ("activation", dict(out=qvb_[:], in_=qv[:], func=AF.Copy)), reads=[Rqv], writes=[Rqvb_])
        yield


    def block_stages(b, d, blk, pp):
        bwd = (d == 1)
        midc, totc = (C // 2 - 1, C - 1) if not bwd else (C // 2, 0)
        t0 = blk * BLK
        rt_, Rrt_ = rtS[pp], RrtS[pp]
        at_, Rat_ = atS[pp], RatS[pp]
        kt_, Rkt_ = ktS[pp], RktS[pp]
        bt_, Rbt_ = btS[pp], RbtS[pp]
        r0_, Rr0_ = r0S[pp], Rr0S[pp]
        a0b_, Ra0b_ = a0bS[pp], Ra0bS[pp]
        kEb_, RkEb_ = kEbS[pp], RkEbS[pp]
        bEb_, RbEb_ = bEbS[pp], RbEbS[pp]
        qvb_, Rqvb_ = qvbS[pp], RqvbS[pp]
        e3_, Re3_ = e3S[pp], Re3S[pp]

        def stage1(ck):
            ci, cs, gci, p = ck
            for i, (src, Rs) in enumerate(((qvb_, Rqvb_), (a0b_, Ra0b_), (bEb_, RbEb_), (kEb_, RkEb_))):
                S.op("pe", ("transpose", dict(out=PS_T[:, i, :], in_=src[:, cs], identity=identb[:])), reads=[Rs, Ridb], writes=[RPS_T])
            yield
            S.op("act", ("activation", dict(out=TT[p][:], in_=PS_T[:, 0:4, :], func=AF.Copy)), reads=[RPS_T], writes=[RTT[p]])
            yield
            for h in range(2):
                hs = HS[h]
                mm(PS_M[:, h, 0, :], bt_[hs, cs], at_[hs, cs], [Rbt_, Rat_], [RPS_M])
                mm(PS_M[:, h, 1, :], at_[hs, cs], bt_[hs, cs], [Rbt_, Rat_], [RPS_M])
                yield
                mm(PS_M[:, h, 2, :], at_[hs, cs], kt_[hs, cs], [Rkt_, Rat_], [RPS_M])
                mm(PS_M[:, h, 3, :], bt_[hs, cs], rt_[hs, cs], [Rbt_, Rrt_], [RPS_M])
                yield
                mm((PS_K if h == 0 else PS_G)[:, 0:128], kt_[hs, cs], rt_[hs, cs], [Rkt_, Rrt_], [RPS_K if h == 0 else RPS_G])
                yield
            for h in range(2):
                S.op("dve", ("tensor_tensor", dict(out=SBM[p][:, h], in0=PS_M[:, h], in1=m4f[:, d, :, :], op=ALU.mult)), reads=[RPS_M, Rm4], writes=[RSBM[p]])
                yield
            S.op("dve", ("tensor_tensor", dict(out=MKR[p][:, 0, :], in0=PS_K[:, 0:128], in1=mkf[:, d, :], op=ALU.mult)), reads=[RPS_K, Rmk], writes=[RMKR[p]])
            yield
            S.op("dve", ("tensor_tensor", dict(out=MKR[p][:, 1, :], in0=PS_G[:, 0:128], in1=mkf[:, d, :], op=ALU.mult)), reads=[RPS_G, Rmk], writes=[RMKR[p]])
            yield
            S.op("act", ("activation", dict(out=SX[p][:, :, 0:128], in_=SBM[p][:, :, 3, :], func=AF.Copy)), reads=[RSBM[p]], writes=[RSX[p]])
            S.op("pool", ("tensor_copy", dict(out=SX[p][:, :, 128:192], in_=TT[p][:, 2, :].rearrange("p (h j) -> p h j", h=2))), reads=[RTT[p]], writes=[RSX[p]])
            yield

        def stage2(ck):
            ci, cs, gci, p = ck
            for h in range(2):
                mm(PS_X[h][:, 0:192], identb[:], SX[p][:, h, :], [Ridb, RSX[p]], [RPS_X], st=True, sp=True)
            A_ = [SBM[p][:, h, 1, :] for h in range(2)]
            B_ = [SBM[p][:, h, 0, :] for h in range(2)]
            Rcur = RSBM[p]
            for lv in range(7):
                if lv < 6:
                    nb = lv % 2
                    for h in range(2):
                        mm(PS_AB[:, h, 0, :], B_[h], A_[h], [Rcur], [RPS_AB])
                        mm(PS_AB[:, h, 1, :], A_[h], B_[h], [Rcur], [RPS_AB])
                for h in range(2):
                    mm(PS_X[h][:, 0:192], A_[h], SX[p][:, h, :], [Rcur, RSX[p]], [RPS_X], st=False, sp=True, sg=True)
                if lv < 6:
                    S.op("act", ("activation", dict(out=SAB[nb][:].rearrange("p a b c -> p (a b c)"), in_=PS_AB[:].rearrange("p a b c -> p (a b c)"), func=AF.Copy)), reads=[RPS_AB], writes=[RSAB[nb]])
                S.op("dve", ("tensor_copy", dict(out=SX[p][:, 0, :], in_=PS_X[0][:, 0:192])), reads=[RPS_X], writes=[RSX[p]])
                S.op("dve", ("tensor_copy", dict(out=SX[p][:, 1, :], in_=PS_X[1][:, 0:192])), reads=[RPS_X], writes=[RSX[p]])
                if lv < 6:
                    A_ = [SAB[nb][:, h, 0, :] for h in range(2)]
                    B_ = [SAB[nb][:, h, 1, :] for h in range(2)]
                    Rcur = RSAB[nb]
                yield

        def stage3(ck):
            ci, cs, gci, p = ck
            for h in range(2):
                hs = HS[h]
                a0T = TT[p][:, 1, hs]
                mm(PS_G[hs, 0:128], a0T, SX[p][:, h, 0:128], [RTT[p], RSX[p]], [RPS_G])
                mm(PS_G[hs, 128:192], a0T, SX[p][:, h, 128:192], [RTT[p], RSX[p]], [RPS_G])
                yield
                mm(PS_G[:, 192 + 128 * h:320 + 128 * h], SBM[p][:, h, 2, :], SX[p][:, h, 0:128], [RSBM[p], RSX[p]], [RPS_G])
                mm(PS_K[:, 256 + 64 * h:320 + 64 * h], SBM[p][:, h, 2, :], SX[p][:, h, 128:192], [RSBM[p], RSX[p]], [RPS_K])
                yield
            S.op("dve", ("tensor_tensor", dict(out=Gb[:], in0=PS_G[:, 0:128], in1=r0_[:, cs], op=ALU.add)), reads=[RPS_G, Rr0_], writes=[RGb])
            yield
            S.op("dve", ("tensor_tensor", dict(out=Hb[:], in0=PS_G[:, 192:448].rearrange("p (h t) -> p h t", h=2), in1=MKR[p][:], op=ALU.add)), reads=[RPS_G, RMKR[p]], writes=[RHb])
            yield
            tcol = ci * C + totc
            S.op("dve", ("scalar_tensor_tensor", dict(out=Pb[:], in0=identP[:], scalar=e3_[:, tcol:tcol + 1], in1=PS_G[:, 128:192], op0=ALU.mult, op1=ALU.add)), reads=[RPS_G, Ridf, Re3_], writes=[RPb])
            yield
            S.op("dve", ("tensor_tensor", dict(out=Zb[:], in0=PS_K[:, 256:384].rearrange("p (h j) -> p h j", h=2), in1=TT[p][:, 3, :].rearrange("p (h j) -> p h j", h=2), op=ALU.add)), reads=[RPS_K, RTT[p]], writes=[RZb])
            yield
            for h in range(2):
                hs = HS[h]
                mm(PS_M[hs, 0, 0, :], STz[h][:], Gb[:], [RST[h], RGb], [RPS_M], st=True, sp=False)
                mm(PS_M[hs, 0, 0, :], TT[p][:, 0, hs], Hb[:, h, :], [RTT[p], RHb], [RPS_M], st=False, sp=True)
                yield
                mm(PS_M[hs, 0, 1, 0:64], Pb[:], STz[h][:], [RPb, RST[h]], [RPS_M], st=True, sp=False)
                mm(PS_M[hs, 0, 1, 0:64], Zb[:, h, :], TT[p][:, 0, hs], [RZb, RTT[p]], [RPS_M], st=False, sp=True)
                yield
            ysl = ysum[:, t0 + ci * C: t0 + (ci + 1) * C]
            if d == 0:
                S.op("act", ("activation", dict(out=ysl, in_=PS_M[:, 0, 0, :], func=AF.Copy)), reads=[RPS_M], writes=[Rys[gci]])
            else:
                S.op("act", ("activation", dict(out=ytmp[:, 0:128], in_=PS_M[:, 0, 0, :], func=AF.Copy)), reads=[RPS_M], writes=[Rytmp])
                S.op("dve", ("tensor_tensor", dict(out=ysl, in0=ytmp[:, 0:128], in1=ysl, op=ALU.add)), reads=[Rytmp, Rys[gci]], writes=[Rys[gci]])
            yield
            for h in range(2):
                hs = HS[h]
                S.op("act", ("activation", dict(out=STz[h][hs, :], in_=PS_M[hs, 0, 1, 0:64], func=AF.Copy)), reads=[RPS_M], writes=[RST[h]])
            yield

        return stage1, stage2, stage3

    def finalize_batch(b):
        for blk in range(nblk):
            t0 = blk * BLK
            ysl = ysum[:, t0:t0 + BLK]
            Rin = Rys[t0 // C: (t0 + BLK) // C]
            S.op("pe", ("matmul", dict(out=PS_P1, lhsT=bdm[:], rhs=ysl, start=True, stop=True)), reads=[Rbdm] + Rin, writes=[RPS_P1])
            S.op("dve", ("tensor_tensor", dict(out=fin1[:], in0=ysl, in1=PS_P1, op=ALU.subtract)), reads=[RPS_P1] + Rin, writes=[Rfin1])
            S.op("pool", ("tensor_tensor", dict(out=fin2[:], in0=fin1[:], in1=fin1[:], op=ALU.mult)), reads=[Rfin1], writes=[Rfin2])
            S.op("pe", ("matmul", dict(out=PS_P2, lhsT=bdm[:], rhs=fin2[:], start=True, stop=True)), reads=[Rbdm, Rfin2], writes=[RPS_P2])
            S.op("dve", ("tensor_scalar", dict(out=fin3[:], in0=PS_P2, scalar1=GN_EPS, scalar2=None, op0=ALU.add)), reads=[RPS_P2], writes=[Rfin3])
            S.op("act", ("activation", dict(out=fin3[:], in_=fin3[:], func=AF.Sqrt)), reads=[Rfin3], writes=[Rfin3])
            S.op("dve", ("reciprocal", dict(out=fin3[:], in_=fin3[:])), reads=[Rfin3], writes=[Rfin3])
            S.op("dve", ("tensor_tensor", dict(out=fin1[:], in0=fin1[:], in1=fin3[:], op=ALU.mult)), reads=[Rfin1, Rfin3], writes=[Rfin1])
            S.op("dve", ("tensor_scalar", dict(out=fin2[:], in0=fin1[:], scalar1=PM(15), scalar2=PM(16), op0=ALU.mult, op1=ALU.add)), reads=[Rfin1, Rprm], writes=[Rfin2])
            S.op("dve", ("tensor_tensor", dict(out=fin2[:], in0=fin2[:], in1=bsum[:, t0:t0 + BLK], op=ALU.add)), reads=[Rfin2, Rbs[blk]], writes=[Rfin2])
            out_toks.append(S.dma(("dma_start", dict(out=yout[:, b, t0:t0 + BLK], in_=fin2[:])), reads=[Rfin2]))

    sched_blocks = []
    for b in range(NB):
        for d in range(2):
            order = list(range(nblk)) if d == 0 else list(range(nblk - 1, -1, -1))
            for n_, blk in enumerate(order):
                sched_blocks.append((b, d, blk, n_ == 0, (n_ == len(order) - 1) and d == 1))
    pcount = 0
    NCH = BLK // C
    for b in range(NB):
        blocks_b = [(k, sbk) for k, sbk in enumerate(sched_blocks) if sbk[0] == b]
        k0 = blocks_b[0][0]
        for _ in prep_gen(b, sched_blocks[k0][1], sched_blocks[k0][2], k0 % 2):
            pass
        seq = []
        for (k, (b_, d, blk, first_of_dir, last_of_batch)) in blocks_b:
            bwd = (d == 1)
            st = block_stages(b, d, blk, k % 2)
            chunks = list(range(NCH)) if not bwd else list(range(NCH - 1, -1, -1))
            for n_, ci in enumerate(chunks):
                ck = (ci, slice(ci * C, (ci + 1) * C), (blk * BLK // C) + ci, pcount % 2)
                pcount += 1
                seq.append((ck, st, k, n_, first_of_dir and n_ == 0))
        pgen = iter(())
        S.op("pool", ("memset", dict(ap=STz[0][:], constant=0.0)), writes=[RST[0]])
        S.op("pool", ("memset", dict(ap=STz[1][:], constant=0.0)), writes=[RST[1]])
        for _ in seq[0][1][0](seq[0][0]):
            pass
        for i, (ck, st, k, n_, fod) in enumerate(seq):
            if n_ == 0:
                if k + 1 < len(sched_blocks) and sched_blocks[k + 1][0] == b:
                    nb_, nd_, nblk_ = sched_blocks[k + 1][:3]
                    pgen = prep_gen(nb_, nd_, nblk_, (k + 1) % 2)
                else:
                    pgen = iter(())
            if n_ == NCH - 1:
                for _ in pgen:
                    pass
            parts = []
            if i > 0:
                parts.append(seq[i - 1][1][2](seq[i - 1][0]))
            if i + 1 < len(seq):
                parts.append(seq[i + 1][1][0](seq[i + 1][0]))
            fill = itertools.chain(*parts)
            for _ in st[1](ck):
                for _k in range(NFILL):
                    next(fill, None)
                for _k in range(NPREP):
                    next(pgen, None)
            for _ in fill:
                pass
            if i + 1 < len(seq) and seq[i + 1][4]:
                for _ in st[2](ck):
                    pass
                S.op("pool", ("memset", dict(ap=STz[0][:], constant=0.0)), writes=[RST[0]])
                S.op("pool", ("memset", dict(ap=STz[1][:], constant=0.0)), writes=[RST[1]])
                seq[i] = (ck, (st[0], st[1], lambda ck_: iter(())), k, n_, fod)
        for _ in seq[-1][1][2](seq[-1][0]):
            pass
        finalize_batch(b)
    return out_toks


NT = 2048
NTH = NT + 2


def emit_p1(S, nc, I, O):
    sb = lambda name, shape, dt=F32: nc.alloc_sbuf_tensor(name, shape, dt)
    ps = lambda name, shape, dt=F32: nc.alloc_psum_tensor(name, shape, dt)
    R = Region
    op = S.op
    mm = lambda out, l, r_, rd, wr, st=True, sp=True: op("pe", ("matmul", dict(out=out, lhsT=l, rhs=r_, start=st, stop=sp)), reads=rd, writes=wr)

    identf = sb("identf", [128, 128]); identb = sb("identb", [128, 128], BF16); Rid = R()
    gE = sb("gE", [128, 8, 1]); gO = sb("gO", [128, 8, 1]); Rg = R()
    S.dma(("dma_start", dict(out=identf[:], in_=I["c_ident"])), writes=[Rid])
    op("dve", ("tensor_copy", dict(out=identb[:], in_=identf[:])), reads=[Rid], writes=[Rid])
    S.dma(("dma_start", dict(out=gE[:, :, 0], in_=I["e_norm_g"].rearrange("(k p) -> p k", p=128), allow_slow_non_contiguous=True)), writes=[Rg])
    S.dma(("dma_start", dict(out=gO[:, :, 0], in_=I["o_norm_g"].rearrange("(k p) -> p k", p=128), allow_slow_non_contiguous=True)), writes=[Rg])
    hnT = sb("hnT", [128, 8, NTH], BF16); RhnT = [R() for _ in range(18)]
    yT = nc.dram_tensor("yT_d", [16, 128, NT], BF16).ap(); RyT = [[R() for _ in range(4)] for _ in range(16)]
    U = sb("U", [128, 4096]); RU = R()
    xt = [sb("xt%d" % i, [128, 1024]) for i in range(2)]; Rxt = [R(), R()]
    yo = [sb("yo%d" % i, [128, 512], BF16) for i in range(2)]; Ryo = [R(), R()]
    ytl = [sb("ytl%d" % i, [128, 16, 128], BF16) for i in range(2)]; Rytl = [R(), R()]
    xn = sb("xn", [128, 1024], BF16); Rxn = R()
    sq = sb("sq", [128, 1024]); Rsq = R()
    st = sb("st", [128, 8]); Rst = R()
    stg = [sb("stg%d" % i, [128, 8, 256]) for i in range(2)]; Rstg = [R(), R()]
    wbf = [sb("wbf%d" % i, [128, 8, 512], BF16) for i in range(2)]; Rwbf = [R(), R()]
    wbig = sb("wbig", [128, 16, 1024], BF16); Rwbig = R()
    t1 = sb("t1", [128, 512]); Rt1 = R()
    t1b = sb("t1b", [128, 512]); t1s = [t1, t1b]; Rt1s = [Rt1, R()]
    t2b = sb("t2b", [128, 512]); t3b = sb("t3b", [128, 512])
    t2 = sb("t2", [128, 512]); Rt2 = R()
    t3 = sb("t3", [128, 512]); Rt3 = R()
    cw = sb("cw", [128, 8, 3]); Rcw = R()
    PS_a = ps("PS_a", [128, 512]); RPa = R()
    PS_b = ps("PS_b", [128, 512]); RPb = R()
    PS_c = ps("PS_c", [128, 512]); RPc = R()
    PS_d = ps("PS_d", [128, 512]); RPd = R()
    PS_t = ps("PS_t", [128, 8, 128], BF16); RPt = R()
    PS_m = ps("PS_m", [128, 8, 128]); RPm = R()
    for j_ in range(3):
        S.dma(("dma_start", dict(out=cw[:, :, j_], in_=I["e_conv_w"][j_].rearrange("(cb p) -> p cb", p=128), allow_slow_non_contiguous=True)), writes=[Rcw])

    def norm_tile(xtile, Rx, gt, dst_fn, Rdst, nvalid=128):
        op("pool", ("memset", dict(ap=st[:, 0:1], constant=0.0)), writes=[Rst])
        op("act", ("activation", dict(out=sq[:], in_=xtile[:], func=AF.Square, accum_out=st[:, 0:1])), reads=[Rx, Rst], writes=[Rsq, Rst])
        op("dve", ("tensor_scalar", dict(out=st[:, 1:2], in0=st[:, 0:1], scalar1=1.0 / 1024, scalar2=1e-6, op0=ALU.mult, op1=ALU.add)), reads=[Rst], writes=[Rst])
        op("act", ("activation", dict(out=st[:, 2:3], in_=st[:, 1:2], func=AF.Sqrt)), reads=[Rst], writes=[Rst])
        op("dve", ("reciprocal", dict(out=st[:, 3:4], in_=st[:, 2:3])), reads=[Rst], writes=[Rst])
        op("dve", ("tensor_scalar", dict(out=xn[:], in0=xtile[:], scalar1=st[:, 3:4], scalar2=None, op0=ALU.mult)), reads=[Rx, Rst], writes=[Rxn])
        for k in range(8):
            op("pe", ("transpose", dict(out=PS_t[:, k, :], in_=xn[:, k * 128:(k + 1) * 128], identity=identb[:])), reads=[Rxn, Rid], writes=[RPt])
        dst_fn(gt)

    xh = I["xh"]
    for i in range(17):
        xb, Rx = xt[i % 2], Rxt[i % 2]
        if i < 16:
            S.dma(("dma_start", dict(out=xb[:], in_=xh[1 + 128 * i: 1 + 128 * (i + 1), :])), writes=[Rx])
            def dst(gt, i=i):
                op("dve", ("tensor_tensor", dict(out=hnT[:, :, 1 + 128 * i: 1 + 128 * (i + 1)], in0=PS_t[:], in1=gt[:].to_broadcast([128, 8, 128]), op=ALU.mult)), reads=[RPt, Rg], writes=[RhnT[i]])
        else:
            op("pool", ("memset", dict(ap=xb[:], constant=0.0)), writes=[Rx])
            S.dma(("dma_start", dict(out=xb[0:1, :], in_=xh[0:1, :])), writes=[Rx])
            S.dma(("dma_start", dict(out=xb[1:2, :], in_=xh[NT + 1:NT + 2, :])), writes=[Rx])
            def dst(gt):
                op("dve", ("tensor_tensor", dict(out=hnT[:, :, 0:1], in0=PS_t[:, :, 0:1], in1=gt[:], op=ALU.mult)), reads=[RPt, Rg], writes=[RhnT[16]])
                op("dve", ("tensor_tensor", dict(out=hnT[:, :, NT + 1:NT + 2], in0=PS_t[:, :, 1:2], in1=gt[:], op=ALU.mult)), reads=[RPt, Rg], writes=[RhnT[17]])
        norm_tile(xb, Rx, gE, dst, None)
    allh = RhnT

    wi = I["e_w_in"].rearrange("(k p) (s c) -> p k s c", p=128, c=1024)

    def load_w(buf, src4, nsp):
        for s_ in range(nsp):
            sb_ = s_ % 2
            S.dma(("dma_start", dict(out=stg[sb_][:, :, 0:128], in_=src4[:, :, s_, :])), writes=[Rstg[sb_]])
            op("act", ("activation", dict(out=wbf[buf][:, :, s_ * 128:(s_ + 1) * 128], in_=stg[sb_][:, :, 0:128], func=AF.Copy)), reads=[Rstg[sb_]], writes=[Rwbf[buf]])
        return wbf[buf][:, :, 0:nsp * 128].rearrange("p k (s c) -> p k s c", c=128)

    PSc0, RPc0, PSd0, RPd0 = PS_c, RPc, PS_d, RPd
    t2s = [t2, t2b]; Rt2s = [Rt2, R()]
    t3s = [t3, t3b]; Rt3s = [Rt3, R()]
    xc2 = sb("xc2", [128, NTH]); RU2 = R()
    chunksA = [(0, 512), (512, 512), (1024, 512), (1536, 512), (2048, 2)]
    RU_main = RU
    for cb in range(8):
        xc, RU = (U[:, 0:NTH], RU_main) if cb % 2 == 0 else (xc2[:, :], RU2)
        if cb % 2 == 0:
            for s_ in range(4):
                sb_ = s_ % 2
                S.dma(("dma_start", dict(out=stg[sb_][:], in_=wi[:, :, s_, cb * 128:(cb + 2) * 128])), writes=[Rstg[sb_]])
                op("act", ("activation", dict(out=wbf[0][:, :, s_ * 128:(s_ + 1) * 128], in_=stg[sb_][:, :, 0:128], func=AF.Copy)), reads=[Rstg[sb_]], writes=[Rwbf[0]])
                op("act", ("activation", dict(out=wbf[1][:, :, s_ * 128:(s_ + 1) * 128], in_=stg[sb_][:, :, 128:256], func=AF.Copy)), reads=[Rstg[sb_]], writes=[Rwbf[1]])
        w4 = wbf[cb % 2][:, :, 0:512].rearrange("p k (s c) -> p k s c", c=128)
        Rw = Rwbf[cb % 2]
        for ci_, (c0, n) in enumerate(chunksA):
            (PA, RA_), (PB, RB_) = (((PS_a, RPa), (PS_b, RPb)) if ci_ % 2 == 0 else ((PS_c, RPc), (PS_d, RPd)))
            for k in range(8):
                mm(PA[:, 0:n], w4[:, k, 0, :], hnT[:, k, c0:c0 + n], [Rw] + allh, [RA_], st=(k == 0), sp=(k == 7))
            for k in range(8):
                mm(PB[:, 0:n], w4[:, k, 2, :], hnT[:, k, c0:c0 + n], [Rw] + allh, [RB_], st=(k == 0), sp=(k == 7))
            t1_, Rt1_ = t1s[ci_ % 2], Rt1s[ci_ % 2]
            op("act", ("activation", dict(out=t1_[:, 0:n], in_=PA[:, 0:n], func=AF.Copy)), reads=[RA_], writes=[Rt1_])
            op("dve", ("tensor_tensor", dict(out=xc[:, c0:c0 + n], in0=PB[:, 0:n], in1=t1_[:, 0:n], op=ALU.mult)), reads=[RB_, Rt1_], writes=[RU])
        for j in range(4):
            c0 = 1 + 512 * j
            (PS_c, RPc), (PS_d, RPd) = ((PSc0, RPc0), (PSd0, RPd0)) if j % 2 == 1 else ((PS_a, RPa), (PS_b, RPb))
            t2, Rt2 = t2s[j % 2], Rt2s[j % 2]
            t3, Rt3 = t3s[j % 2], Rt3s[j % 2]
            for k in range(8):
                mm(PS_c[:], w4[:, k, 1, :], hnT[:, k, c0:c0 + 512], [Rw] + allh, [RPc], st=(k == 0), sp=(k == 7))
            for k in range(8):
                mm(PS_d[:], w4[:, k, 3, :], hnT[:, k, c0:c0 + 512], [Rw] + allh, [RPd], st=(k == 0), sp=(k == 7))
            op("dve", ("tensor_scalar", dict(out=t2[:], in0=xc[:, c0 - 1:c0 + 511], scalar1=cw[:, cb, 0:1], scalar2=None, op0=ALU.mult)), reads=[RU, Rcw], writes=[Rt2])
            op("dve", ("scalar_tensor_tensor", dict(out=t2[:], in0=xc[:, c0:c0 + 512], scalar=cw[:, cb, 1:2], in1=t2[:], op0=ALU.mult, op1=ALU.add)), reads=[RU, Rcw, Rt2], writes=[Rt2])
            op("dve", ("scalar_tensor_tensor", dict(out=t2[:], in0=xc[:, c0 + 1:c0 + 513], scalar=cw[:, cb, 2:3], in1=t2[:], op0=ALU.mult, op1=ALU.add)), reads=[RU, Rcw, Rt2], writes=[Rt2])
            op("act", ("activation", dict(out=t3[:], in_=PS_d[:], func=AF.Silu)), reads=[RPd], writes=[Rt3])
            op("dve", ("tensor_tensor", dict(out=t2[:], in0=PS_c[:], in1=t2[:], op=ALU.mult)), reads=[RPc, Rt2], writes=[Rt2])
            op("pool", ("tensor_tensor", dict(out=yo[j % 2][:], in0=t2[:], in1=t3[:], op=ALU.mult)), reads=[Rt2, Rt3], writes=[Ryo[j % 2]])
            S.dma(("dma_start", dict(out=yT[cb, :, 512 * j:512 * (j + 1)], in_=yo[j % 2][:])), reads=[Ryo[j % 2]], writes=[RyT[cb][j]], q="pool")

    PS_c, RPc, PS_d, RPd = PSc0, RPc0, PSd0, RPd0
    t2, Rt2, t3, Rt3 = t2s[0], Rt2s[0], t3s[0], Rt3s[0]
    RU = RU_main
    for hf in range(4):
        S.dma(("dma_start", dict(out=stg[hf % 2][:], in_=wi[:, :, 5, hf * 256:(hf + 1) * 256])), writes=[Rstg[hf % 2]])
        op("act", ("activation", dict(out=wbig[:, 0:8, hf * 256:(hf + 1) * 256], in_=stg[hf % 2][:], func=AF.Copy)), reads=[Rstg[hf % 2]], writes=[Rwbig])
    for hf in range(4):
        S.dma(("dma_start", dict(out=stg[hf % 2][:], in_=wi[:, :, 4, hf * 256:(hf + 1) * 256])), writes=[Rstg[hf % 2]])
        op("act", ("activation", dict(out=wbig[:, 8:16, hf * 256:(hf + 1) * 256], in_=stg[hf % 2][:], func=AF.Copy)), reads=[Rstg[hf % 2]], writes=[Rwbig])
    for hf in range(4):
        S.dma(("dma_start", dict(out=stg[hf % 2][:], in_=wi[:, :, 6, hf * 256:(hf + 1) * 256])), writes=[Rstg[hf % 2]])
        op("act", ("activation", dict(out=wbf[hf // 2][:, :, (hf % 2) * 256:(hf % 2 + 1) * 256], in_=stg[hf % 2][:], func=AF.Copy)), reads=[Rstg[hf % 2]], writes=[Rwbf[hf // 2]])
    wsn = sb("wsn", [128, 8, 128]); wsnb = sb("wsnb", [128, 8, 128], BF16); wsT = sb("wsT", [128, 8, 128], BF16); Rws = R()
    S.dma(("dma_start", dict(out=wsn[:], in_=I["e_sgu_w"].rearrange("g i j -> i g j"))), writes=[Rws])
    op("dve", ("tensor_copy", dict(out=wsnb[:], in_=wsn[:])), reads=[Rws], writes=[Rws])
    for g in range(8):
        op("pe", ("transpose", dict(out=PS_t[:, g, :], in_=wsnb[:, g, :], identity=identb[:])), reads=[Rws, Rid], writes=[RPt])
    op("act", ("activation", dict(out=wsT[:], in_=PS_t[:], func=AF.Copy)), reads=[RPt], writes=[Rws])
    bsB = sb("bsB", [128, 8, 128]); lnG = sb("lnG", [128, 1024]); lnB = sb("lnB", [128, 1024]); Rbc = R()
    S.dma(("dma_start", dict(out=bsB[:].rearrange("p g i -> p (g i)"), in_=I["e_sgu_b"].rearrange("g i -> (g i)").partition_broadcast(128))), writes=[Rbc])
    S.dma(("dma_start", dict(out=lnG[:], in_=I["e_sgu_ln_g"].partition_broadcast(128))), writes=[Rbc])
    S.dma(("dma_start", dict(out=lnB[:], in_=I["e_sgu_ln_b"].partition_broadcast(128))), writes=[Rbc])
    vsb = sb("vsb", [128, 1024]); Rvsb = R()
    vnb = sb("vnb", [128, 1024], BF16); Rvnb = R()
    mixall = U[:, 0:4096].rearrange("p (g t) -> p g t", g=8)
    for tg in range(4):
        for ti in range(4):
            c0 = 1 + 128 * (4 * tg + ti)
            for hf, (P_, RP_) in enumerate((((PS_a, RPa), (PS_b, RPb)) if ti % 2 == 0 else ((PS_c, RPc), (PS_d, RPd)))):
                for k in range(8):
                    mm(P_[:], hnT[:, k, c0:c0 + 128], wbig[:, k, hf * 512:(hf + 1) * 512], [Rwbig] + allh, [RP_], st=(k == 0), sp=(k == 7))
                op("act", ("activation", dict(out=vsb[:, hf * 512:(hf + 1) * 512], in_=P_[:], func=AF.Copy)), reads=[RP_], writes=[Rvsb])
            op("act", ("activation", dict(out=sq[:], in_=vsb[:], func=AF.Square)), reads=[Rvsb], writes=[Rsq])
            op("dve", ("reduce_sum", dict(out=st[:, 0:1], in_=vsb[:], axis=AX.X)), reads=[Rvsb], writes=[Rst])
            op("dve", ("reduce_sum", dict(out=st[:, 1:2], in_=sq[:], axis=AX.X)), reads=[Rsq], writes=[Rst])
            op("dve", ("tensor_scalar", dict(out=st[:, 2:3], in0=st[:, 0:1], scalar1=1.0 / 1024, scalar2=None, op0=ALU.mult)), reads=[Rst], writes=[Rst])
            op("dve", ("tensor_tensor", dict(out=st[:, 3:4], in0=st[:, 2:3], in1=st[:, 2:3], op=ALU.mult)), reads=[Rst], writes=[Rst])
            op("dve", ("scalar_tensor_tensor", dict(out=st[:, 4:5], in0=st[:, 1:2], scalar=1.0 / 1024, in1=st[:, 3:4], op0=ALU.mult, op1=ALU.subtract)), reads=[Rst], writes=[Rst])
            op("dve", ("tensor_scalar", dict(out=st[:, 4:5], in0=st[:, 4:5], scalar1=1e-5, scalar2=None, op0=ALU.add)), reads=[Rst], writes=[Rst])
            op("act", ("activation", dict(out=st[:, 5:6], in_=st[:, 4:5], func=AF.Sqrt)), reads=[Rst], writes=[Rst])
            op("dve", ("reciprocal", dict(out=st[:, 6:7], in_=st[:, 5:6])), reads=[Rst], writes=[Rst])
            op("dve", ("tensor_scalar", dict(out=vsb[:], in0=vsb[:], scalar1=st[:, 2:3], scalar2=st[:, 6:7], op0=ALU.subtract, op1=ALU.mult)), reads=[Rvsb, Rst], writes=[Rvsb])
            op("dve", ("tensor_tensor", dict(out=vsb[:], in0=vsb[:], in1=lnG[:], op=ALU.mult)), reads=[Rvsb, Rbc], writes=[Rvsb])
            op("pool", ("tensor_tensor", dict(out=vnb[:], in0=vsb[:], in1=lnB[:], op=ALU.add)), reads=[Rvsb, Rbc], writes=[Rvnb])
            for g in range(8):
                mm(PS_m[:, g, :], vnb[:, g * 128:(g + 1) * 128], wsT[:, g, :], [Rvnb, Rws], [RPm])
            op("dve", ("tensor_tensor", dict(out=mixall[:, :, ti * 128:(ti + 1) * 128], in0=PS_m[:], in1=bsB[:], op=ALU.add)), reads=[RPm, Rbc], writes=[RU])
        c0 = 1 + 512 * tg
        for g in range(8):
            buf = g % 2

            (PU, RPU), (PZ, RPZ) = ((PS_c, RPc), (PS_d, RPd)) if g % 2 == 0 else ((PS_a, RPa), (PS_b, RPb))
            t2, Rt2 = t2s[g % 2], Rt2s[g % 2]
            t3, Rt3 = t3s[g % 2], Rt3s[g % 2]
            for k in range(8):
                mm(PU[:], wbig[:, 8 + k, g * 128:(g + 1) * 128], hnT[:, k, c0:c0 + 512], [Rwbig] + allh, [RPU], st=(k == 0), sp=(k == 7))
            for k in range(8):
                mm(PZ[:], wbf[g // 4][:, k, (g % 4) * 128:(g % 4 + 1) * 128], hnT[:, k, c0:c0 + 512], [Rwbf[g // 4]] + allh, [RPZ], st=(k == 0), sp=(k == 7))
            op("act", ("activation", dict(out=t3[:], in_=PZ[:], func=AF.Silu)), reads=[RPZ], writes=[Rt3])
            op("dve", ("tensor_tensor", dict(out=t2[:], in0=PU[:], in1=mixall[:, g, :], op=ALU.mult)), reads=[RPU, RU], writes=[Rt2])
            op("pool", ("tensor_tensor", dict(out=yo[g % 2][:], in0=t2[:], in1=t3[:], op=ALU.mult)), reads=[Rt2, Rt3], writes=[Ryo[g % 2]])
            S.dma(("dma_start", dict(out=yT[8 + g, :, 512 * tg:512 * (tg + 1)], in_=yo[g % 2][:])), reads=[Ryo[g % 2]], writes=[RyT[8 + g][tg]], q="pool")

    wo = I["e_w_out"].rearrange("(k p) n -> p k n", p=128)
    for q in range(2):
        for hf in range(4):
            S.dma(("dma_start", dict(out=stg[hf % 2][:], in_=wo[:, 8 * q:8 * q + 8, hf * 256:(hf + 1) * 256])), writes=[Rstg[hf % 2]])
            op("act", ("activation", dict(out=wbig[:, 8 * q:8 * q + 8, hf * 256:(hf + 1) * 256], in_=stg[hf % 2][:], func=AF.Copy)), reads=[Rstg[hf % 2]], writes=[Rwbig])
    ally = [r for row in RyT for r in row]
    h1ts = [sb("h1t%d" % i, [128, 1024]) for i in range(2)]; Rh1s = [R(), R()]
    for i in range(16):
        xb, Rx = xt[i % 2], Rxt[i % 2]
        h1t, Rh1 = h1ts[i % 2], Rh1s[i % 2]
        S.dma(("dma_start", dict(out=xb[:], in_=xh[1 + 128 * i: 1 + 128 * (i + 1), :])), writes=[Rx])
        S.dma(("dma_start", dict(out=ytl[i % 2][:], in_=yT[:, :, 128 * i:128 * (i + 1)].rearrange("k p t -> p k t"))), reads=ally, writes=[Rytl[i % 2]])
        for hf, (P_, RP_) in enumerate((((PS_a, RPa), (PS_b, RPb)) if i % 2 == 0 else ((PS_c, RPc), (PS_d, RPd)))):
            for k in range(16):
                mm(P_[:], ytl[i % 2][:, k, :], wbig[:, k, hf * 512:(hf + 1) * 512], [Rwbig, Rytl[i % 2]], [RP_], st=(k == 0), sp=(k == 15))
            op("dve", ("tensor_tensor", dict(out=h1t[:, hf * 512:(hf + 1) * 512], in0=P_[:], in1=xb[:, hf * 512:(hf + 1) * 512], op=ALU.add)), reads=[RP_, Rx], writes=[Rh1])
        S.dma(("dma_start", dict(out=O["h1"][128 * i:128 * (i + 1), :], in_=h1t[:])), reads=[Rh1], q="act")

        def dst(gt, i=i):
            op("dve", ("tensor_tensor", dict(out=hnT[:, :, 1 + 128 * i: 1 + 128 * (i + 1)], in0=PS_t[:], in1=gt[:].to_broadcast([128, 8, 128]), op=ALU.mult)), reads=[RPt, Rg], writes=[RhnT[i]])
        norm_tile(h1t, Rh1, gO, dst, None)

    wi1 = I["o_w_in"].rearrange("(k p) n -> p k n", p=128)
    blocks = [(c * 128, c * 128, False) for c in range(25)]
    blocks += [(3200 + c * 128, 3200 + c * 128, True) for c in range(8)]
    blocks += [(4736 + c * 128, 4224 + c * 128, True) for c in range(4)]
    ob = [sb("ob%d" % i, [128, 512]) for i in range(2)]; Rob = [R(), R()]
    oi = 0
    toks = []
    bi = 0
    nblocks = len(blocks)
    pairbuf = 0
    while bi < nblocks:
        sc, dr, act = blocks[bi]
        paired = (bi + 1 < nblocks) and (blocks[bi + 1][0] == sc + 128) and (blocks[bi + 1][2] == act)
        ncol = 256 if paired else 128
        buf = pairbuf % 2
        pairbuf += 1
        S.dma(("dma_start", dict(out=stg[buf][:, :, 0:ncol], in_=wi1[:, :, sc:sc + ncol])), writes=[Rstg[buf]])
        op("dve", ("tensor_copy", dict(out=wbf[buf][:, :, 0:ncol], in_=stg[buf][:, :, 0:ncol])), reads=[Rstg[buf]], writes=[Rwbf[buf]])
        for sub in range(2 if paired else 1):
            sc_, dr_, act_ = blocks[bi + sub]
            for j in range(4):
                P_, RP_ = ((PS_a, RPa), (PS_b, RPb), (PS_c, RPc), (PS_d, RPd))[j]
                c0 = 1 + 512 * j
                for k in range(8):
                    mm(P_[:], wbf[buf][:, k, sub * 128:(sub + 1) * 128], hnT[:, k, c0:c0 + 512], [Rwbf[buf]] + allh, [RP_], st=(k == 0), sp=(k == 7))
                o_, Ro = ob[oi % 2], Rob[oi % 2]
                oi += 1
                op("act", ("activation", dict(out=o_[:], in_=P_[:], func=(AF.Silu if act_ else AF.Copy))), reads=[RP_], writes=[Ro])
                toks.append(S.dma(("dma_start", dict(out=O["pT"][dr_:dr_ + 128, 512 * j:512 * (j + 1)], in_=o_[:])), reads=[Ro], q="act"))
        bi += 2 if paired else 1
    for hf in range(2):
        S.dma(("dma_start", dict(out=stg[hf][:], in_=wi1[:, :, 4224 + 256 * hf:4224 + 256 * (hf + 1)])), writes=[Rstg[hf]])
        op("act", ("activation", dict(out=wbf[0][:, :, 256 * hf:256 * (hf + 1)], in_=stg[hf][:], func=AF.Copy)), reads=[Rstg[hf]], writes=[Rwbf[0]])
    for i in range(16):
        c0 = 1 + 128 * i
        P_, RP_ = ((PS_a, RPa), (PS_b, RPb))[i % 2]
        for k in range(8):
            mm(P_[:], hnT[:, k, c0:c0 + 128], wbf[0][:, k, :], [Rwbf[0]] + allh, [RP_], st=(k == 0), sp=(k == 7))
        o_, Ro = ob[oi % 2], Rob[oi % 2]
        oi += 1
        op("act", ("activation", dict(out=o_[:], in_=P_[:], func=AF.Copy)), reads=[RP_], writes=[Ro])
        toks.append(S.dma(("dma_start", dict(out=O["fd"][128 * i:128 * (i + 1), :], in_=o_[:])), reads=[Ro], q="act"))
    return toks


def full_barrier(S):
    keys = list(S.cnt.items())
    for e in S.ENGS:
        waits = []
        for k, v in keys:
            if k == e:
                continue
            if S.seen[e].get(k, 0) < v:
                S.seen[e][k] = v
                waits.append((k, v))
        if waits:
            S.prog[e].append([waits, None, ("_none", 0)])


def emit_fnet(S, nc, I, ydT):
    R = Region
    op = S.op
    mm = lambda out, l, r_, rd, wr, st=True, sp=True: op("pe", ("matmul", dict(out=out, lhsT=l, rhs=r_, start=st, stop=sp)), reads=rd, writes=wr)
    toks = []
    with ExitStack() as es:
        sb = lambda name, shape, dt=F32: es.enter_context(nc.sbuf_tensor(name, shape, dt))
        ps = lambda name, shape, dt=F32: es.enter_context(nc.psum_tensor(name, shape, dt))
        xs = sb("f_xs", [128, 4096]); Rxs = R()
        xb = sb("f_xb", [128, 64, 128], BF16); Rxb = R()
        Fb = sb("f_F", [128, 256], BF16); RF = R()
        A_sb = sb("f_A", [64, 128, 256], BF16); RA = R()
        PQ = sb("f_PQ", [128, 2, 64, 128], BF16); RPQ = R()
        Tg = [[sb("f_T%d%d" % (i, j), [64, 16, 128], BF16) for j in range(2)] for i in range(2)]; RTg = [R(), R()]
        wf32 = sb("f_w32", [128, 128]); wfb = sb("f_wb", [128, 128], BF16); Rwf = R()
        Ccb = sb("f_Cc", [128, 128], BF16); mScb = sb("f_mSc", [128, 128], BF16); Rcs = R()
        Gb = sb("f_G", [128, 256], BF16); RG = R()
        ob = [sb("f_ob%d" % i, [128, 512]) for i in range(2)]; Rob = [R(), R()]
        PS = [ps("f_ps%d" % i, [128, 512]) for i in range(2)]; RPS = [R(), R()]
        S.dma(("dma_start", dict(out=Fb[:], in_=I["c_F"])), writes=[RF])
        S.dma(("dma_start", dict(out=Ccb[:], in_=I["c_Cc"])), writes=[Rcs])
        S.dma(("dma_start", dict(out=mScb[:], in_=I["c_mSc"])), writes=[Rcs])
        S.dma(("dma_start", dict(out=wf32[:], in_=I["fw"])), writes=[Rwf])
        op("dve", ("tensor_copy", dict(out=wfb[:], in_=wf32[:])), reads=[Rwf], writes=[Rwf])
        xbf = xb[:].rearrange("p l c -> p (l c)")
        for hf in range(2):
            S.dma(("dma_start", dict(out=xs[:], in_=I["fx"][:, hf * 4096:(hf + 1) * 4096])), writes=[Rxs])
            op("act", ("activation", dict(out=xbf[:, hf * 4096:(hf + 1) * 4096], in_=xs[:], func=AF.Copy)), reads=[Rxs], writes=[Rxb])
        for c2 in range(64):
            P_, RP_ = PS[c2 % 2], RPS[c2 % 2]
            for j in range(2):
                mm(P_[0:64, j * 256:(j + 1) * 256], xb[:, :, 2 * c2 + j], Fb[:], [Rxb, RF], [RP_])
            op("act" if c2 % 2 == 0 else "dve", ("activation", dict(out=A_sb[0:64, 2 * c2:2 * c2 + 2, :], in_=P_[0:64, :].rearrange("p (j k) -> p j k", j=2), func=AF.Copy)) if c2 % 2 == 0 else
               ("tensor_copy", dict(out=A_sb[0:64, 2 * c2:2 * c2 + 2, :], in_=P_[0:64, :].rearrange("p (j k) -> p j k", j=2))), reads=[RP_], writes=[RA])
        T1d = I["c_T1"].rearrange("p (k h) -> p k h", h=128)
        T2d = I["c_T2"].rearrange("p (k h) -> p k h", h=128)
        ei = 0
        for grp in range(8):
            tb = grp % 2
            S.dma(("dma_start", dict(out=Tg[tb][0][:], in_=T1d[:, grp * 16:(grp + 1) * 16, :])), writes=[RTg[tb]])
            S.dma(("dma_start", dict(out=Tg[tb][1][:], in_=T2d[:, grp * 16:(grp + 1) * 16, :])), writes=[RTg[tb]])
            for q in range(4):
                P_, RP_ = PS[ei % 2], RPS[ei % 2]
                for j in range(4):
                    kk_ = q * 4 + j
                    kl = grp * 16 + kk_
                    mm(P_[:, j * 128:(j + 1) * 128], A_sb[0:64, :, kl], Tg[tb][0][0:64, kk_, :], [RA, RTg[tb]], [RP_], st=True, sp=False)
                    mm(P_[:, j * 128:(j + 1) * 128], A_sb[0:64, :, 128 + kl], Tg[tb][1][0:64, kk_, :], [RA, RTg[tb]], [RP_], st=False, sp=True)
                kl0 = grp * 16 + q * 4
                for qq in range(2):
                    op("act" if qq == 0 else "dve",
                       ("activation", dict(out=PQ[:, qq, :, kl0:kl0 + 4].rearrange("p h l -> p l h"), in_=P_[:].rearrange("p (l q h) -> p l q h", l=4, q=2)[:, :, qq, :], func=AF.Copy)) if qq == 0 else
                       ("tensor_copy", dict(out=PQ[:, qq, :, kl0:kl0 + 4].rearrange("p h l -> p l h"), in_=P_[:].rearrange("p (l q h) -> p l q h", l=4, q=2)[:, :, qq, :])),
                       reads=[RP_], writes=[RPQ])
                ei += 1
        P_, RP_ = PS[0], RPS[0]
        mm(P_[:, 0:128], Ccb[:], wfb[:], [Rcs, Rwf], [RP_])
        mm(P_[:, 128:256], mScb[:], wfb[:], [Rcs, Rwf], [RP_])
        op("act", ("activation", dict(out=Gb[:], in_=P_[:, 0:256], func=AF.Copy)), reads=[RP_], writes=[RG])
        for t4 in range(16):
            P_, RP_ = PS[(t4 + 1) % 2], RPS[(t4 + 1) % 2]
            for j in range(4):
                kh = 4 * t4 + j
                mm(P_[:, j * 128:(j + 1) * 128], Gb[:, 0:128], PQ[:, 0, kh, :], [RG, RPQ], [RP_], st=True, sp=False)
                mm(P_[:, j * 128:(j + 1) * 128], Gb[:, 128:256], PQ[:, 1, kh, :], [RG, RPQ], [RP_], st=False, sp=True)
            o_, Ro = ob[t4 % 2], Rob[t4 % 2]
            op("act", ("activation", dict(out=o_[:], in_=P_[:], func=AF.Copy)), reads=[RP_], writes=[Ro])
            toks.append(S.dma(("dma_start", dict(out=ydT[:, 512 * t4:512 * (t4 + 1)], in_=o_[:])), reads=[Ro]))
    full_barrier(S)
    return toks


def emit_p3(S, nc, I, yout):
    sb = lambda name, shape, dt=F32: nc.alloc_sbuf_tensor(name, shape, dt)
    ps = lambda name, shape, dt=F32: nc.alloc_psum_tensor(name, shape, dt)
    R = Region
    op = S.op
    mm = lambda out, l, r_, rd, wr, st=True, sp=True: op("pe", ("matmul", dict(out=out, lhsT=l, rhs=r_, start=st, stop=sp)), reads=rd, writes=wr)
    stg = [sb("stg%d" % i, [128, 8, 256]) for i in range(2)]; Rstg = [R(), R()]
    wO = sb("wO", [128, 12, 1024], BF16); RwO = R()
    gN = sb("gN", [128, 1024]); RgN = R()
    gt_all = sb("gt_all", [128, 12, 2048], BF16); Rgt = R()
    ya = [sb("ya%d" % i, [128, 512]) for i in range(2)]; Rya = [R(), R()]
    ga = [sb("ga%d" % i, [128, 512]) for i in range(2)]; Rga = [R(), R()]
    h1t = [sb("h1t%d" % i, [128, 1024]) for i in range(2)]; Rh1 = [R(), R()]
    h2 = sb("h2", [128, 1024]); Rh2 = R()
    sq = sb("sq", [128, 1024]); Rsq = R()
    st = sb("st", [128, 8]); Rst = R()
    yo = [sb("yo%d" % i, [128, 1024]) for i in range(2)]; Ryo = [R(), R()]
    PS_a = ps("PS_a", [128, 512]); RPa = R()
    PS_b = ps("PS_b", [128, 512]); RPb = R()
    wo3 = I["o_w_out"].rearrange("(k p) n -> p k n", p=128)
    si = 0
    for (k0, nk) in ((0, 8), (8, 4)):
        for cq in range(4):
            b_ = si % 2; si += 1
            S.dma(("dma_start", dict(out=stg[b_][:, 0:nk, :], in_=wo3[:, k0:k0 + nk, cq * 256:(cq + 1) * 256])), writes=[Rstg[b_]])
            op("act", ("activation", dict(out=wO[:, k0:k0 + nk, cq * 256:(cq + 1) * 256], in_=stg[b_][:, 0:nk, :], func=AF.Copy)), reads=[Rstg[b_]], writes=[RwO])
    S.dma(("dma_start", dict(out=gN[:], in_=I["final_norm_g"].partition_broadcast(128))), writes=[RgN])
    ii = 0
    for blk in range(12):
        src = I["ycT"][blk * 128:(blk + 1) * 128] if blk < 8 else I["ydT"][(blk - 8) * 128:(blk - 7) * 128]
        gsrc = I["gT"][blk * 128:(blk + 1) * 128]
        for j in range(4):
            b_ = ii % 2; ii += 1
            S.dma(("dma_start", dict(out=ya[b_][:], in_=src[:, 512 * j:512 * (j + 1)])), writes=[Rya[b_]])
            S.dma(("dma_start", dict(out=ga[b_][:], in_=gsrc[:, 512 * j:512 * (j + 1)])), writes=[Rga[b_]])
            op("dve", ("tensor_tensor", dict(out=gt_all[:, blk, 512 * j:512 * (j + 1)], in0=ya[b_][:], in1=ga[b_][:], op=ALU.mult)), reads=[Rya[b_], Rga[b_]], writes=[Rgt])
    toks = []
    for i in range(16):
        hb, Rh = h1t[i % 2], Rh1[i % 2]
        S.dma(("dma_start", dict(out=hb[:], in_=I["h1"][128 * i:128 * (i + 1), :])), writes=[Rh])
        for hf, (P_, RP_) in enumerate(((PS_a, RPa), (PS_b, RPb))):
            for k in range(12):
                mm(P_[:], gt_all[:, k, 128 * i:128 * (i + 1)], wO[:, k, hf * 512:(hf + 1) * 512], [Rgt, RwO], [RP_], st=(k == 0), sp=(k == 11))
            op("dve", ("tensor_tensor", dict(out=h2[:, hf * 512:(hf + 1) * 512], in0=P_[:], in1=hb[:, hf * 512:(hf + 1) * 512], op=ALU.add)), reads=[RP_, Rh], writes=[Rh2])
        op("pool", ("memset", dict(ap=st[:, 0:1], constant=0.0)), writes=[Rst])
        op("act", ("activation", dict(out=sq[:], in_=h2[:], func=AF.Square, accum_out=st[:, 0:1])), reads=[Rh2, Rst], writes=[Rsq, Rst])
        op("dve", ("tensor_scalar", dict(out=st[:, 1:2], in0=st[:, 0:1], scalar1=1.0 / 1024, scalar2=1e-6, op0=ALU.mult, op1=ALU.add)), reads=[Rst], writes=[Rst])
        op("act", ("activation", dict(out=st[:, 2:3], in_=st[:, 1:2], func=AF.Sqrt)), reads=[Rst], writes=[Rst])
        op("dve", ("reciprocal", dict(out=st[:, 3:4], in_=st[:, 2:3])), reads=[Rst], writes=[Rst])
        op("dve", ("tensor_scalar", dict(out=h2[:], in0=h2[:], scalar1=st[:, 3:4], scalar2=None, op0=ALU.mult)), reads=[Rh2, Rst], writes=[Rh2])
        o_, Ro = yo[i % 2], Ryo[i % 2]
        op("dve", ("tensor_tensor", dict(out=o_[:], in0=h2[:], in1=gN[:], op=ALU.mult)), reads=[Rh2, RgN], writes=[Ro])
        toks.append(S.dma(("dma_start", dict(out=yout[128 * i:128 * (i + 1), :], in_=o_[:])), reads=[Ro], q="pool"))
    return toks


def _mk(nc, name, shape, dt=None, out=False):
    return nc.dram_tensor(name, list(shape), dt or F32, kind=("ExternalOutput" if out else "ExternalInput")).ap()


W1 = ["e_norm_g", "e_w_in", "e_conv_w", "e_sgu_ln_g", "e_sgu_ln_b", "e_sgu_w", "e_sgu_b", "e_w_out", "o_norm_g", "o_w_in"]


def build_l1(shapes):
    nc = bass.Bass("TRN2", target_bir_lowering=False)
    I = {"xh": _mk(nc, "xh", [2050, 1024]), "c_ident": _mk(nc, "c_ident", [128, 128])}
    for n in W1:
        I[n] = _mk(nc, n, shapes[n])
    O = {"h1": _mk(nc, "h1", [2048, 1024], out=True), "pT": _mk(nc, "pT", [4736, 2048], out=True),
         "fd": _mk(nc, "fd", [2048, 512], out=True)}
    S = Sched(nc)
    toks = emit_p1(S, nc, I, O)
    S.barrier_on("sp", toks)
    S.finalize()
    return nc


def build_l2(consts):
    NB, T = 2, 8192
    nc = bass.Bass("TRN2", target_bir_lowering=False)
    pr, pk, pv, pwa = (_mk(nc, n, [128, NB, T + 2]) for n in ("pr", "pk", "pv", "pwa"))
    prm = _mk(nc, "prm", [128, 17]); w2a2 = _mk(nc, "w2a2", [128, 2, 128])
    A = {k: _mk(nc, k, v.shape) for k, v in consts.items()}
    FI = {"fx": _mk(nc, "fx", [128, 8192]), "fw": _mk(nc, "fw", [128, 128]),
          "c_F": _mk(nc, "c_F", [128, 256], BF16), "c_T1": _mk(nc, "c_T1", [64, 16384], BF16),
          "c_T2": _mk(nc, "c_T2", [64, 16384], BF16), "c_Cc": _mk(nc, "c_Cc", [128, 128], BF16),
          "c_mSc": _mk(nc, "c_mSc", [128, 128], BF16)}
    yout = _mk(nc, "yout", [128, NB, T], out=True)
    ydT = _mk(nc, "ydT", [128, T], out=True)
    S = Sched(nc)
    toks = emit_fnet(S, nc, FI, ydT)
    toks += emit_rwkv(S, nc, A, pr, pk, pv, pwa, prm, w2a2, yout, NB, T)
    S.barrier_on("sp", toks)
    S.finalize()
    return nc


def build_l3():
    nc = bass.Bass("TRN2", target_bir_lowering=False)
    I = {"ycT": _mk(nc, "ycT", [1024, 2048]), "ydT": _mk(nc, "ydT", [512, 2048]), "gT": _mk(nc, "gT", [1536, 2048]),
         "h1": _mk(nc, "h1", [2048, 1024]), "o_w_out": _mk(nc, "o_w_out", [1536, 1024]),
         "final_norm_g": _mk(nc, "final_norm_g", [1024])}
    y = _mk(nc, "y", [2048, 1024], out=True)
    S = Sched(nc)
    toks = emit_p3(S, nc, I, y)
    S.barrier_on("sp", toks)
    S.finalize()
    return nc


def fnet_tables():
    import ml_dtypes
    N = 8192
    nh = np.arange(128); kl = np.arange(128)
    ang = 2 * np.pi * np.outer(nh, kl) / 128
    F = np.concatenate([np.cos(ang), np.sin(ang)], axis=1)
    nl = np.arange(64)[:, None, None]; klo = np.arange(128)[None, :, None]; kh = np.arange(64)[None, None, :]
    beta = 2 * np.pi * ((nl * (klo + 128 * kh)) % N) / N
    T1 = np.concatenate([np.cos(beta), np.sin(beta)], axis=2).reshape(64, 16384)
    T2 = np.concatenate([-np.sin(beta), np.cos(beta)], axis=2).reshape(64, 16384)
    c = np.arange(128); phi = 2 * np.pi * np.outer(c, c) / 128
    nrm = 1 / np.sqrt(N * 128)
    bf = lambda a: np.ascontiguousarray(a.astype(np.float32)).astype(ml_dtypes.bfloat16)
    return {"c_F": bf(F), "c_T1": bf(T1), "c_T2": bf(T2), "c_Cc": bf(np.cos(phi) * nrm), "c_mSc": bf(-np.sin(phi) * nrm)}


def kernel(**inputs):
    f32 = lambda a: np.ascontiguousarray(np.asarray(a), dtype=np.float32)
    inp = {k: f32(v) for k, v in inputs.items()}
    x = inp["x"]
    ncores = 8
    cores = list(range(ncores))
    w1 = {n: np.ascontiguousarray(inp[n][0]) for n in W1}
    ident = np.eye(128, dtype=np.float32)
    maps = []
    for c in cores:
        b, s0 = c // 4, (c % 4) * 2048
        xh = np.zeros((2050, 1024), np.float32)
        xh[1:2049] = x[b, s0:s0 + 2048]
        if s0 > 0:
            xh[0] = x[b, s0 - 1]
        if s0 + 2048 < 8192:
            xh[2049] = x[b, s0 + 2048]
        m = {"xh": xh, "c_ident": ident}
        m.update(w1)
        maps.append(m)
    nc1 = build_l1({n: w1[n].shape for n in W1})
    r1 = run_bass_kernel_spmd(nc1, maps, core_ids=cores).results
    PT = np.concatenate([np.asarray(r["pT"]) for r in r1], axis=1)
    FD = np.concatenate([np.asarray(r["fd"]) for r in r1], axis=0)
    consts = build_consts_np()
    ft = fnet_tables()
    mu, w0, w2, a0, a2 = inp["o_mu"][0], inp["o_w0"][0], inp["o_w2"][0], inp["o_a0"][0], inp["o_a2"][0]
    k_k, k_a, r_k = inp["o_k_k"][0], inp["o_k_a"][0], inp["o_r_k"][0].reshape(-1)
    lg, lb = inp["o_lnx_g"][0], inp["o_lnx_b"][0]
    PT3 = PT.reshape(4736, 2, 8192)
    pad = lambda a: np.ascontiguousarray(np.pad(a, ((0, 0), (0, 0), (1, 1))))
    maps = []
    for c in cores:
        ch = slice(c * 128, (c + 1) * 128)
        m = {"pr": pad(PT3[0:1024][ch]), "pk": pad(PT3[1024:2048][ch]), "pv": pad(PT3[2048:3072][ch]),
             "pwa": pad(PT3[3072:3200])}
        prm = np.zeros((128, 17), np.float32)
        for d in range(2):
            prm[:, 0 + d] = mu[d, 0:1024][ch]; prm[:, 2 + d] = mu[d, 1024:2048][ch]; prm[:, 4 + d] = mu[d, 2048:3072][ch]
            prm[:, 6 + d] = mu[d, 3072:3200]; prm[:, 8 + d] = w0[d][ch]; prm[:, 10 + d] = a0[d][ch]
        prm[:, 12] = k_k[ch]; prm[:, 13] = k_a[ch]; prm[:, 14] = r_k[ch]; prm[:, 15] = lg[ch]; prm[:, 16] = lb[ch]
        m["prm"] = prm
        m["w2a2"] = np.ascontiguousarray(np.concatenate([w2[:, :, ch], a2[:, :, ch]], axis=1).transpose(1, 0, 2))
        m.update(consts)
        b, g = c // 4, c % 4
        m["fx"] = np.ascontiguousarray(FD[b * 8192:(b + 1) * 8192, g * 128:(g + 1) * 128]).reshape(128, 8192)
        m["fw"] = np.ascontiguousarray(inp["o_fnet_w"][0, g])
        m.update(ft)
        maps.append(m)
    nc2 = build_l2(consts)
    r2 = run_bass_kernel_spmd(nc2, maps, core_ids=cores).results
    YC = np.concatenate([np.asarray(r["yout"]).reshape(128, 16384) for r in r2], axis=0)
    YD = np.concatenate([np.concatenate([np.asarray(r2[b * 4 + g]["ydT"]) for g in range(4)], axis=0) for b in range(2)], axis=1)
    maps = []
    for c in cores:
        ts = slice(c * 2048, (c + 1) * 2048)
        maps.append({"ycT": np.ascontiguousarray(YC[:, ts]), "ydT": np.ascontiguousarray(YD[:, ts]),
                     "gT": np.ascontiguousarray(PT[3200:4736, ts]), "h1": np.asarray(r1[c]["h1"]),
                     "o_w_out": np.ascontiguousarray(inp["o_w_out"][0]), "final_norm_g": inp["final_norm_g"]})
    nc3 = build_l3()
    r3 = run_bass_kernel_spmd(nc3, maps, core_ids=cores).results
    y = np.concatenate([np.asarray(r["y"]) for r in r3], axis=0).reshape(2, 8192, 1024)
    return y.astype(np.float32)
```

```python
from contextlib import ExitStack
import itertools
import numpy as np
import concourse.bass as bass
import concourse.mybir as mybir
from concourse.bass_utils import run_bass_kernel_spmd


F32 = mybir.dt.float32
BF16 = mybir.dt.bfloat16
AF = mybir.ActivationFunctionType
ALU = mybir.AluOpType
AX = mybir.AxisListType

N_DMA_SEMS = 8


class Region:
    __slots__ = ("w", "r", "name")

    def __init__(self, name=""):
        self.w = None
        self.r = {}
        self.name = name


class Sched:
    ENGS = ("pe", "dve", "act", "pool", "sp")

    def __init__(self, nc):
        self.nc = nc
        self.prog = {e: [] for e in self.ENGS}
        self.cnt = {}
        self.seen = {e: {} for e in self.ENGS}
        self.dma_rr = {e: 0 for e in self.ENGS}
        self.dma_last = {}
        self.same_engine_raw = True
        self.cut = 0
        self.raw_only = True
        self.nrec = 0
        self.log = []

    def _collect(self, eng, mykey, reads, writes):
        waits = {}

        def need(tok, kind):
            if tok is None:
                return
            k, v = tok
            if k == mykey:
                if eng == "pe":
                    return
                if not self.same_engine_raw:
                    return
                if self.raw_only and kind != "raw":
                    return
            if waits.get(k, 0) < v:
                waits[k] = v

        for R in reads:
            need(R.w, "raw")
        for R in writes:
            need(R.w, "waw")
            for k, v in R.r.items():
                need((k, v), "war")
        out = []
        seen = self.seen[eng]
        for k, v in waits.items():
            if seen.get(k, 0) < v:
                seen[k] = v
                out.append((k, v))
        return out

    def _commit(self, tok, reads, writes):
        for R in writes:
            R.w = tok
            R.r = {}
        k, v = tok
        for R in reads:
            if R.r.get(k, 0) < v:
                R.r[k] = v

    def op(self, eng, fn, reads=(), writes=()):
        self.nrec += 1
        if self.cut and self.nrec > self.cut:
            return None
        if self.cut:
            self.log.append((self.nrec, eng, fn[0] if isinstance(fn, tuple) else "fn", str(fn[1].get("out", ""))[:120] if isinstance(fn, tuple) else ""))
        key = eng
        waits = self._collect(eng, key, reads, writes)
        idx = self.cnt.get(key, 0) + 1
        self.cnt[key] = idx
        tok = (key, idx)
        self.prog[eng].append([waits, fn, tok])
        self._commit(tok, reads, writes)
        return tok

    def dma(self, fn, reads=(), writes=(), q="sp"):
        self.nrec += 1
        if self.cut and self.nrec > self.cut:
            return None
        i = self.dma_rr[q]
        self.dma_rr[q] = (i + 1) % N_DMA_SEMS
        key = "dma_%s_%d" % (q, i)
        waits = self._collect(q, key, reads, writes)
        prev = self.cnt.get(key, 0)
        if prev > 0 and self.seen[q].get(key, 0) < prev:
            self.seen[q][key] = prev
            waits.append((key, prev))
        idx = prev + 1
        self.cnt[key] = idx
        tok = (key, idx)
        self.prog[q].append([waits, fn, tok])
        self._commit(tok, reads, writes)
        return tok

    def finalize(self):
        nc = self.nc
        waited = {}
        for e in self.ENGS:
            for waits, fn, tok in self.prog[e]:
                for k, v in waits:
                    waited.setdefault(k, set()).add(v)
        self.final_waits = []
        sem_of = {}
        val_of = {}
        for k, s in waited.items():
            sem_of[k] = nc.alloc_semaphore("s_" + k)
            isdma = k.startswith("dma_")
            step = 16 if isdma else 1
            if isdma:
                val_of[k] = None
            else:
                val_of[k] = {v: (i + 1) for i, v in enumerate(sorted(s))}
        engobj = {"pe": nc.tensor, "dve": nc.vector, "act": nc.scalar,
                  "pool": nc.gpsimd, "sp": nc.sync}

        def value(k, v):
            if val_of[k] is None:
                return 16 * v
            return val_of[k][v]

        def emit(e):
            def body(eng):
                for waits, fn, tok in self.prog[e]:
                    for k, v in waits:
                        eng.wait_ge(sem_of[k], value(k, v))
                    if fn is None:
                        continue
                    if isinstance(fn, tuple):
                        ins = getattr(eng, fn[0])(**fn[1])
                    else:
                        ins = fn(eng)
                    k, v = tok
                    if k in sem_of:
                        if val_of[k] is None:
                            ins.then_inc(sem_of[k], 16)
                        elif v in val_of[k]:
                            ins.then_inc(sem_of[k], 1)
            return body

        with nc.Block() as block:
            for e, dec in (("sp", block.sync), ("pe", block.tensor), ("dve", block.vector),
                           ("act", block.scalar), ("pool", block.gpsimd)):
                if self.prog[e]:
                    dec(emit(e))
        self.n_sems = len(sem_of)
        return self.n_sems

    def barrier_on(self, eng, toks):
        waits = []
        for tk in toks:
            if tk is None:
                continue
            k, v = tk
            if self.seen[eng].get(k, 0) < v:
                self.seen[eng][k] = v
                waits.append((k, v))
        if waits:
            self.prog[eng].append([waits, None, ("_none", 0)])


C = 128
BLK = 512
NEG_E = -float(np.exp(-0.5))
GN_EPS = 64e-5


def build_consts_np():
    idx = np.arange(128)
    lt = (idx[:, None] < idx[None, :]).astype(np.float32)
    le = (idx[:, None] <= idx[None, :]).astype(np.float32)
    gt = lt.T.copy()
    ge = le.T.copy()
    m4f = np.stack([lt, gt, gt, le], axis=1)
    m4b = np.stack([gt, lt, lt, ge], axis=1)
    mk = np.stack([le, ge], axis=1)
    ident = np.eye(128, dtype=np.float32)
    bd = np.kron(np.eye(2, dtype=np.float32), np.ones((64, 64), np.float32))
    scanm = np.ones((128, BLK), np.float32)
    scanm[:, ::C] = 0.0
    return {"c_m4": np.stack([m4f, m4b], axis=1).reshape(128, 2 * 4 * 128).copy(),
            "c_mk": mk.reshape(128, 256).copy(), "c_ident": ident, "c_bd": bd, "c_scanm": scanm}


XST = False


def emit_rwkv(S, nc, A, pr, pk, pv, pwa, prm, w2a2, yout, NB, T):
    sb = lambda name, shape, dt=F32: nc.alloc_sbuf_tensor(name, shape, dt)
    ps = lambda name, shape, dt=F32: nc.alloc_psum_tensor(name, shape, dt)
    R = Region
    nblk = T // BLK

    m4f = sb("m4f", [128, 2, 4, 128]); Rm4 = R()
    mkf = sb("mkf", [128, 2, 128]); Rmk = R()
    identf = sb("identf", [128, 128]); Ridf = R()
    identb = sb("identb", [128, 128], BF16); Ridb = R()
    bdf = sb("bdf", [128, 128]); Rbd = R()
    bdr = sb("bdr", [128, 128]); Rbdr = R()
    bdm = sb("bdm", [128, 128]); Rbdm = R()
    scanm = sb("scanm", [128, BLK]); Rsc = R()
    prmt = sb("prmt", [128, 17]); Rprm = R()
    w2f = sb("w2f", [128, 2, 128]); Rw2f = R()
    w2b = sb("w2b", [128, 2, 128], BF16); Rw2b = R()
    S.dma(("dma_start", dict(out=m4f[:].rearrange("p a b c -> p (a b c)"), in_=A["c_m4"])), writes=[Rm4])
    S.dma(("dma_start", dict(out=mkf[:].rearrange("p a c -> p (a c)"), in_=A["c_mk"])), writes=[Rmk])
    S.dma(("dma_start", dict(out=identf[:], in_=A["c_ident"])), writes=[Ridf])
    S.dma(("dma_start", dict(out=bdf[:], in_=A["c_bd"])), writes=[Rbd])
    S.dma(("dma_start", dict(out=scanm[:], in_=A["c_scanm"])), writes=[Rsc])
    S.dma(("dma_start", dict(out=prmt[:], in_=prm)), writes=[Rprm])
    S.dma(("dma_start", dict(out=w2f[:], in_=w2a2)), writes=[Rw2f])
    S.op("dve", ("tensor_copy", dict(out=identb[:], in_=identf[:])), reads=[Ridf], writes=[Ridb])
    S.op("dve", ("tensor_copy", dict(out=w2b[:], in_=w2f[:])), reads=[Rw2f], writes=[Rw2b])
    PM = lambda c: prmt[:, c:c + 1]
    S.op("dve", ("tensor_scalar", dict(out=bdr[:], in0=bdf[:], scalar1=PM(14), scalar2=None, op0=ALU.mult)), reads=[Rbd, Rprm], writes=[Rbdr])
    S.op("dve", ("tensor_scalar", dict(out=bdm[:], in0=bdf[:], scalar1=1.0 / 64, scalar2=None, op0=ALU.mult)), reads=[Rbd], writes=[Rbdm])

    def T2(name, dt=F32, n=BLK):
        return sb(name, [128, n], dt), R()
    ld = {}
    for nm in ("pr", "pk", "pv", "pwa"):
        ld[nm] = (sb("ld_" + nm, [128, BLK + 2]), R())
    tmp, Rtmp = T2("tmp")
    qr, Rqr = T2("qr"); qk, Rqk = T2("qk"); qv, Rqv = T2("qv"); qwa, Rqwa = T2("qwa")
    twa, Rtwa = T2("twa", BF16)
    sw, Rsw = T2("sw"); asg, Rasg = T2("asg")
    logw, Rlogw = T2("logw"); lin, Rlin = T2("lin"); linm, Rlinm = T2("linm"); lexm, Rlexm = T2("lexm")
    lex, Rlex = T2("lex"); lint, Rlint = T2("lint")
    e1, Re1 = T2("e1"); e1x, Re1x = T2("e1x"); e2, Re2 = T2("e2"); e3S = [sb("e3%d" % i, [128, BLK]) for i in range(2)]; Re3S = [R(), R()]; e3x, Re3x = T2("e3x"); e4, Re4 = T2("e4")
    kk, Rkk = T2("kk"); kk2, Rkk2 = T2("kk2"); rin, Rrin = T2("rin"); kkn, Rkkn = T2("kkn")
    kp, Rkp = T2("kp"); bv, Rbv = T2("bv"); rk, Rrk = T2("rk")
    rtS = [sb("rt%d" % i, [128, BLK], BF16) for i in range(2)]; RrtS = [R(), R()]; atS = [sb("at%d" % i, [128, BLK], BF16) for i in range(2)]; RatS = [R(), R()]; ktS = [sb("kt%d" % i, [128, BLK], BF16) for i in range(2)]; RktS = [R(), R()]; btS = [sb("bt%d" % i, [128, BLK], BF16) for i in range(2)]; RbtS = [R(), R()]
    r0S = [sb("r0%d" % i, [128, BLK]) for i in range(2)]; Rr0S = [R(), R()]; a0bS = [sb("a0b%d" % i, [128, BLK], BF16) for i in range(2)]; Ra0bS = [R(), R()]; kEbS = [sb("kEb%d" % i, [128, BLK], BF16) for i in range(2)]; RkEbS = [R(), R()]; bEbS = [sb("bEb%d" % i, [128, BLK], BF16) for i in range(2)]; RbEbS = [R(), R()]
    qvbS = [sb("qvb%d" % i, [128, BLK], BF16) for i in range(2)]; RqvbS = [R(), R()]
    ysum = sb("ysum", [128, T]); Rys = [R() for _ in range(T // C)]
    bsum = sb("bsum", [128, T]); Rbs = [R() for _ in range(nblk)]
    TT = [sb("TT%d" % i, [128, 4, 128], BF16) for i in range(2)]; RTT = [R(), R()]
    SBM = [sb("SBM%d" % i, [128, 2, 4, 128], BF16) for i in range(2)]; RSBM = [R(), R()]
    MKR = [sb("MKR%d" % i, [128, 2, 128]) for i in range(2)]; RMKR = [R(), R()]
    SX = [sb("SX%d" % i, [128, 2, 192], BF16) for i in range(2)]; RSX = [R(), R()]
    SAB = [sb("SAB%d" % i, [128, 2, 2, 128], BF16) for i in range(2)]; RSAB = [R(), R()]
    Gb = sb("Gb", [128, 128], BF16); RGb = R()
    Hb = sb("Hb", [128, 2, 128], BF16); RHb = R()
    Pb = sb("Pb", [128, 64], BF16); RPb = R()
    Zb = sb("Zb", [128, 2, 64], BF16); RZb = R()
    STz = [sb("STz%d" % h, [128, 64], BF16) for h in range(2)]; RST = [R(), R()]
    identP = sb("identP", [128, 64]); mkb = sb("mkb", [128, 2, 2, 128])
    HS = [slice(0, 64), slice(64, 128)]
    fin1, Rfin1 = T2("fin1"); fin2, Rfin2 = T2("fin2"); fin3, Rfin3 = T2("fin3")

    PS_M = ps("PS_M", [128, 2, 4, 128]); RPS_M = R()
    PS_K = ps("PS_K", [128, 512]); RPS_K = R()
    PS_X = [ps("PS_X%d" % h, [128, 512]) for h in range(2)]; RPS_X = R()
    PS_AB = ps("PS_AB", [128, 2, 2, 128]); RPS_AB = R()
    PS_G = ps("PS_G", [128, 512]); RPS_G = R()
    PS_T = ps("PS_T", [128, 8, 128], BF16); RPS_T = R()
    PS_P1 = PS_AB[:].rearrange("p a b c -> p (a b c)"); RPS_P1 = RPS_AB
    PS_P2 = PS_P1; RPS_P2 = RPS_AB
    mm = lambda out, l, r_, rd, wr, st=True, sp=True, sg=False: S.op("pe", ("matmul", dict(out=out, lhsT=l, rhs=r_, start=st, stop=sp, skip_group_check=sg)), reads=rd, writes=wr)
    S.op("pool", ("tensor_copy", dict(out=identP[0:64, :], in_=identf[0:64, 0:64])), reads=[Ridf], writes=[Ridf])
    S.op("pool", ("tensor_copy", dict(out=identP[64:128, :], in_=identf[64:128, 64:128])), reads=[Ridf], writes=[Ridf])
    for h in range(2):
        S.op("pool", ("tensor_copy", dict(out=mkb[:, :, h, :], in_=mkf[:])), reads=[Rmk], writes=[Rmk])
    ytmp = sb("ytmp", [128, 128]); Rytmp = R()
    out_toks = []
    NFILL = 4
    NPREP = 2
    def prep_gen(b, d, blk, pp):
        bwd = (d == 1)
        midc, totc = (C // 2 - 1, C - 1) if not bwd else (C // 2, 0)
        t0 = blk * BLK
        rt_, Rrt_ = rtS[pp], RrtS[pp]
        at_, Rat_ = atS[pp], RatS[pp]
        kt_, Rkt_ = ktS[pp], RktS[pp]
        bt_, Rbt_ = btS[pp], RbtS[pp]
        r0_, Rr0_ = r0S[pp], Rr0S[pp]
        a0b_, Ra0b_ = a0bS[pp], Ra0bS[pp]
        kEb_, RkEb_ = kEbS[pp], RkEbS[pp]
        bEb_, RbEb_ = bEbS[pp], RbEbS[pp]
        qvb_, Rqvb_ = qvbS[pp], RqvbS[pp]
        e3_, Re3_ = e3S[pp], Re3S[pp]
        for nm, src in (("pr", pr), ("pk", pk), ("pv", pv), ("pwa", pwa)):
            tl, Rl = ld[nm]
            S.dma(("dma_start", dict(out=tl[:], in_=src[:, b, t0:t0 + BLK + 2])), writes=[Rl])
            yield
        sh = (slice(0, BLK) if not bwd else slice(2, BLK + 2))
        cur = slice(1, BLK + 1)
        for nm, q, Rq, mc in (("pr", qr, Rqr, 0), ("pk", qk, Rqk, 2), ("pv", qv, Rqv, 4), ("pwa", qwa, Rqwa, 6)):
            tl, Rl = ld[nm]
            S.op("dve", ("tensor_tensor", dict(out=tmp[:], in0=tl[:, sh], in1=tl[:, cur], op=ALU.subtract)), reads=[Rl], writes=[Rtmp])
            yield
            S.op("dve", ("scalar_tensor_tensor", dict(out=q[:], in0=tmp[:], scalar=PM(mc + d), in1=tl[:, cur], op0=ALU.mult, op1=ALU.add)), reads=[Rtmp, Rl, Rprm], writes=[Rq])
            yield
        S.op("act", ("activation", dict(out=twa[0:64, :], in_=qwa[0:64, :], func=AF.Tanh)), reads=[Rqwa], writes=[Rtwa])
        yield
        S.op("dve", ("tensor_copy", dict(out=twa[64:128, :], in_=qwa[64:128, :])), reads=[Rqwa], writes=[Rtwa])
        yield
        S.op("pe", ("matmul", dict(out=PS_P1, lhsT=w2b[0:64, d, :], rhs=twa[0:64, :], start=True, stop=True)), reads=[Rw2b, Rtwa], writes=[RPS_P1])
        S.op("act", ("activation", dict(out=sw[:], in_=PS_P1, func=AF.Sigmoid, bias=PM(8 + d))), reads=[RPS_P1, Rprm], writes=[Rsw])
        yield
        S.op("pe", ("matmul", dict(out=PS_P2, lhsT=w2b[64:128, d, :], rhs=twa[64:128, :], start=True, stop=True)), reads=[Rw2b, Rtwa], writes=[RPS_P2])
        S.op("act", ("activation", dict(out=asg[:], in_=PS_P2, func=AF.Sigmoid, bias=PM(10 + d))), reads=[RPS_P2, Rprm], writes=[Rasg])
        yield
        S.op("dve", ("tensor_scalar", dict(out=logw[:], in0=sw[:], scalar1=NEG_E, scalar2=None, op0=ALU.mult)), reads=[Rsw], writes=[Rlogw])
        yield
        S.op("dve", ("tensor_tensor_scan", dict(out=lin[:], data0=scanm[:], data1=logw[:], initial=0.0, op0=ALU.mult, op1=ALU.add)), reads=[Rsc, Rlogw], writes=[Rlin])
        yield
        lin3 = lambda tl: tl[:].rearrange("p (c t) -> p c t", t=C)
        bc = lambda tl, col: lin3(tl)[:, :, col:col + 1].to_broadcast([128, BLK // C, C])
        if bwd:
            S.op("dve", ("tensor_tensor", dict(out=lin3(tmp), in0=bc(lin, C - 1), in1=lin3(lin), op=ALU.subtract)), reads=[Rlin], writes=[Rtmp])
            yield
            S.op("dve", ("tensor_tensor", dict(out=lin[:], in0=tmp[:], in1=logw[:], op=ALU.add)), reads=[Rtmp, Rlogw], writes=[Rlin])
            yield
        S.op("dve", ("tensor_tensor", dict(out=lin3(linm), in0=lin3(lin), in1=bc(lin, midc), op=ALU.subtract)), reads=[Rlin], writes=[Rlinm])
        yield
        S.op("dve", ("tensor_tensor", dict(out=lexm[:], in0=linm[:], in1=logw[:], op=ALU.subtract)), reads=[Rlinm, Rlogw], writes=[Rlexm])
        yield
        S.op("dve", ("tensor_tensor", dict(out=lex[:], in0=lin[:], in1=logw[:], op=ALU.subtract)), reads=[Rlin, Rlogw], writes=[Rlex])
        yield
        S.op("dve", ("tensor_tensor", dict(out=lin3(lint), in0=lin3(lin), in1=bc(lin, totc), op=ALU.subtract)), reads=[Rlin], writes=[Rlint])
        yield
        S.op("act", ("activation", dict(out=e1[:], in_=linm[:], func=AF.Exp)), reads=[Rlinm], writes=[Re1])
        yield
        S.op("act", ("activation", dict(out=e1x[:], in_=lexm[:], func=AF.Exp)), reads=[Rlexm], writes=[Re1x])
        yield
        S.op("act", ("activation", dict(out=e2[:], in_=linm[:], func=AF.Exp, scale=-1.0)), reads=[Rlinm], writes=[Re2])
        yield
        S.op("act", ("activation", dict(out=e3_[:], in_=lin[:], func=AF.Exp)), reads=[Rlin], writes=[Re3_])
        yield
        S.op("act", ("activation", dict(out=e3x[:], in_=lex[:], func=AF.Exp)), reads=[Rlex], writes=[Re3x])
        yield
        S.op("act", ("activation", dict(out=e4[:], in_=lint[:], func=AF.Exp, scale=-1.0)), reads=[Rlint], writes=[Re4])
        yield
        S.op("dve", ("tensor_scalar", dict(out=kk[:], in0=qk[:], scalar1=PM(12), scalar2=None, op0=ALU.mult)), reads=[Rqk, Rprm], writes=[Rkk])
        yield
        S.op("pool", ("tensor_tensor", dict(out=kk2[:], in0=kk[:], in1=kk[:], op=ALU.mult)), reads=[Rkk], writes=[Rkk2])
        yield
        S.op("pe", ("matmul", dict(out=PS_P1, lhsT=bdf[:], rhs=kk2[:], start=True, stop=True)), reads=[Rbd, Rkk2], writes=[RPS_P1])
        S.op("dve", ("tensor_scalar", dict(out=rin[:], in0=PS_P1, scalar1=1e-12, scalar2=None, op0=ALU.max)), reads=[RPS_P1], writes=[Rrin])
        yield
        S.op("act", ("activation", dict(out=rin[:], in_=rin[:], func=AF.Sqrt)), reads=[Rrin], writes=[Rrin])
        yield
        S.op("dve", ("reciprocal", dict(out=rin[:], in_=rin[:])), reads=[Rrin], writes=[Rrin])
        yield
        S.op("dve", ("tensor_tensor", dict(out=kkn[:], in0=kk[:], in1=rin[:], op=ALU.mult)), reads=[Rkk, Rrin], writes=[Rkkn])
        yield
        S.op("dve", ("tensor_scalar", dict(out=tmp[:], in0=asg[:], scalar1=-1.0, scalar2=PM(13), op0=ALU.add, op1=ALU.mult)), reads=[Rasg, Rprm], writes=[Rtmp])
        yield
        S.op("dve", ("scalar_tensor_tensor", dict(out=kp[:], in0=tmp[:], scalar=1.0, in1=qk[:], op0=ALU.add, op1=ALU.mult)), reads=[Rtmp, Rqk], writes=[Rkp])
        yield
        S.op("pool", ("tensor_tensor", dict(out=bv[:], in0=kkn[:], in1=asg[:], op=ALU.mult)), reads=[Rkkn, Rasg], writes=[Rbv])
        yield
        S.op("pool", ("tensor_tensor", dict(out=rk[:], in0=qr[:], in1=kp[:], op=ALU.mult)), reads=[Rqr, Rkp], writes=[Rrk])
        yield
        S.op("pe", ("matmul", dict(out=PS_P2, lhsT=bdr[:], rhs=rk[:], start=True, stop=True)), reads=[Rbdr, Rrk], writes=[RPS_P2])
        bsl = bsum[:, t0:t0 + BLK]
        if d == 0:
            S.op("dve", ("tensor_tensor", dict(out=bsl, in0=PS_P2, in1=qv[:], op=ALU.mult)), reads=[RPS_P2, Rqv], writes=[Rbs[blk]])
            yield
        else:
            S.op("dve", ("tensor_tensor", dict(out=tmp[:], in0=PS_P2, in1=qv[:], op=ALU.mult)), reads=[RPS_P2, Rqv], writes=[Rtmp])
            yield
            S.op("pool", ("tensor_tensor", dict(out=bsl, in0=bsl, in1=tmp[:], op=ALU.add)), reads=[Rtmp, Rbs[blk]], writes=[Rbs[blk]])
            yield
        S.op("dve", ("tensor_tensor", dict(out=rt_[:], in0=qr[:], in1=e1[:], op=ALU.mult)), reads=[Rqr, Re1], writes=[Rrt_])
        yield
        S.op("dve", ("scalar_tensor_tensor", dict(out=at_[:], in0=kkn[:], scalar=-1.0, in1=e1x[:], op0=ALU.mult, op1=ALU.mult)), reads=[Rkkn, Re1x], writes=[Rat_])
        yield
        S.op("pool", ("tensor_tensor", dict(out=kt_[:], in0=kp[:], in1=e2[:], op=ALU.mult)), reads=[Rkp, Re2], writes=[Rkt_])
        yield
        S.op("pool", ("tensor_tensor", dict(out=bt_[:], in0=bv[:], in1=e2[:], op=ALU.mult)), reads=[Rbv, Re2], writes=[Rbt_])
        yield
        S.op("pool", ("tensor_tensor", dict(out=r0_[:], in0=qr[:], in1=e3_[:], op=ALU.mult)), reads=[Rqr, Re3_], writes=[Rr0_])
        yield
        S.op("dve", ("scalar_tensor_tensor", dict(out=a0b_[:], in0=kkn[:], scalar=-1.0, in1=e3x[:], op0=ALU.mult, op1=ALU.mult)), reads=[Rkkn, Re3x], writes=[Ra0b_])
        yield
        S.op("pool", ("tensor_tensor", dict(out=kEb_[:], in0=kp[:], in1=e4[:], op=ALU.mult)), reads=[Rkp, Re4], writes=[RkEb_])
        yield
        S.op("pool", ("tensor_tensor", dict(out=bEb_[:], in0=bv[:], in1=e4[:], op=ALU.mult)), reads=[Rbv, Re4], writes=[RbEb_])
        yield
        S.op("act", ("activation", dict(out=qvb_[:], in_=qv[:], func=AF.Copy)), reads=[Rqv], writes=[Rqvb_])
        yield


    def block_stages(b, d, blk, pp):
        bwd = (d == 1)
        midc, totc = (C // 2 - 1, C - 1) if not bwd else (C // 2, 0)
        t0 = blk * BLK
        rt_, Rrt_ = rtS[pp], RrtS[pp]
        at_, Rat_ = atS[pp], RatS[pp]
        kt_, Rkt_ = ktS[pp], RktS[pp]
        bt_, Rbt_ = btS[pp], RbtS[pp]
        r0_, Rr0_ = r0S[pp], Rr0S[pp]
        a0b_, Ra0b_ = a0bS[pp], Ra0bS[pp]
        kEb_, RkEb_ = kEbS[pp], RkEbS[pp]
        bEb_, RbEb_ = bEbS[pp], RbEbS[pp]
        qvb_, Rqvb_ = qvbS[pp], RqvbS[pp]
        e3_, Re3_ = e3S[pp], Re3S[pp]

        def stage1(ck):
            ci, cs, gci, p = ck
            for i, (src, Rs) in enumerate(((qvb_, Rqvb_), (a0b_, Ra0b_), (bEb_, RbEb_), (kEb_, RkEb_))):
                S.op("pe", ("transpose", dict(out=PS_T[:, i, :], in_=src[:, cs], identity=identb[:])), reads=[Rs, Ridb], writes=[RPS_T])
            yield
            S.op("act", ("activation", dict(out=TT[p][:], in_=PS_T[:, 0:4, :], func=AF.Copy)), reads=[RPS_T], writes=[RTT[p]])
            yield
            for h in range(2):
                hs = HS[h]
                mm(PS_M[:, h, 0, :], bt_[hs, cs], at_[hs, cs], [Rbt_, Rat_], [RPS_M])
                mm(PS_M[:, h, 1, :], at_[hs, cs], bt_[hs, cs], [Rbt_, Rat_], [RPS_M])
                yield
                mm(PS_M[:, h, 2, :], at_[hs, cs], kt_[hs, cs], [Rkt_, Rat_], [RPS_M])
                mm(PS_M[:, h, 3, :], bt_[hs, cs], rt_[hs, cs], [Rbt_, Rrt_], [RPS_M])
                yield
                mm((PS_K if h == 0 else PS_G)[:, 0:128], kt_[hs, cs], rt_[hs, cs], [Rkt_, Rrt_], [RPS_K if h == 0 else RPS_G])
                yield
            for h in range(2):
                S.op("dve", ("tensor_tensor", dict(out=SBM[p][:, h], in0=PS_M[:, h], in1=m4f[:, d, :, :], op=ALU.mult)), reads=[RPS_M, Rm4], writes=[RSBM[p]])
                yield
            S.op("dve", ("tensor_tensor", dict(out=MKR[p][:, 0, :], in0=PS_K[:, 0:128], in1=mkf[:, d, :], op=ALU.mult)), reads=[RPS_K, Rmk], writes=[RMKR[p]])
            yield
            S.op("dve", ("tensor_tensor", dict(out=MKR[p][:, 1, :], in0=PS_G[:, 0:128], in1=mkf[:, d, :], op=ALU.mult)), reads=[RPS_G, Rmk], writes=[RMKR[p]])
            yield
            S.op("act", ("activation", dict(out=SX[p][:, :, 0:128], in_=SBM[p][:, :, 3, :], func=AF.Copy)), reads=[RSBM[p]], writes=[RSX[p]])
            S.op("pool", ("tensor_copy", dict(out=SX[p][:, :, 128:192], in_=TT[p][:, 2, :].rearrange("p (h j) -> p h j", h=2))), reads=[RTT[p]], writes=[RSX[p]])
            yield

        def stage2(ck):
            ci, cs, gci, p = ck
            for h in range(2):
                mm(PS_X[h][:, 0:192], identb[:], SX[p][:, h, :], [Ridb, RSX[p]], [RPS_X], st=True, sp=True)
            A_ = [SBM[p][:, h, 1, :] for h in range(2)]
            B_ = [SBM[p][:, h, 0, :] for h in range(2)]
            Rcur = RSBM[p]
            for lv in range(7):
                if lv < 6:
                    nb = lv % 2
                    for h in range(2):
                        mm(PS_AB[:, h, 0, :], B_[h], A_[h], [Rcur], [RPS_AB])
                        mm(PS_AB[:, h, 1, :], A_[h], B_[h], [Rcur], [RPS_AB])
                for h in range(2):
                    mm(PS_X[h][:, 0:192], A_[h], SX[p][:, h, :], [Rcur, RSX[p]], [RPS_X], st=False, sp=True, sg=True)
                if lv < 6:
                    S.op("act", ("activation", dict(out=SAB[nb][:].rearrange("p a b c -> p (a b c)"), in_=PS_AB[:].rearrange("p a b c -> p (a b c)"), func=AF.Copy)), reads=[RPS_AB], writes=[RSAB[nb]])
                S.op("dve", ("tensor_copy", dict(out=SX[p][:, 0, :], in_=PS_X[0][:, 0:192])), reads=[RPS_X], writes=[RSX[p]])
                S.op("dve", ("tensor_copy", dict(out=SX[p][:, 1, :], in_=PS_X[1][:, 0:192])), reads=[RPS_X], writes=[RSX[p]])
                if lv < 6:
                    A_ = [SAB[nb][:, h, 0, :] for h in range(2)]
                    B_ = [SAB[nb][:, h, 1, :] for h in range(2)]
                    Rcur = RSAB[nb]
                yield

        def stage3(ck):
            ci, cs, gci, p = ck
            for h in range(2):
                hs = HS[h]
                a0T = TT[p][:, 1, hs]
                mm(PS_G[hs, 0:128], a0T, SX[p][:, h, 0:128], [RTT[p], RSX[p]], [RPS_G])
                mm(PS_G[hs, 128:192], a0T, SX[p][:, h, 128:192], [RTT[p], RSX[p]], [RPS_G])
                yield
                mm(PS_G[:, 192 + 128 * h:320 + 128 * h], SBM[p][:, h, 2, :], SX[p][:, h, 0:128], [RSBM[p], RSX[p]], [RPS_G])
                mm(PS_K[:, 256 + 64 * h:320 + 64 * h], SBM[p][:, h, 2, :], SX[p][:, h, 128:192], [RSBM[p], RSX[p]], [RPS_K])
                yield
            S.op("dve", ("tensor_tensor", dict(out=Gb[:], in0=PS_G[:, 0:128], in1=r0_[:, cs], op=ALU.add)), reads=[RPS_G, Rr0_], writes=[RGb])
            yield
            S.op("dve", ("tensor_tensor", dict(out=Hb[:], in0=PS_G[:, 192:448].rearrange("p (h t) -> p h t", h=2), in1=MKR[p][:], op=ALU.add)), reads=[RPS_G, RMKR[p]], writes=[RHb])
            yield
            tcol = ci * C + totc
            S.op("dve", ("scalar_tensor_tensor", dict(out=Pb[:], in0=identP[:], scalar=e3_[:, tcol:tcol + 1], in1=PS_G[:, 128:192], op0=ALU.mult, op1=ALU.add)), reads=[RPS_G, Ridf, Re3_], writes=[RPb])
            yield
            S.op("dve", ("tensor_tensor", dict(out=Zb[:], in0=PS_K[:, 256:384].rearrange("p (h j) -> p h j", h=2), in1=TT[p][:, 3, :].rearrange("p (h j) -> p h j", h=2), op=ALU.add)), reads=[RPS_K, RTT[p]], writes=[RZb])
            yield
            for h in range(2):
                hs = HS[h]
                mm(PS_M[hs, 0, 0, :], STz[h][:], Gb[:], [RST[h], RGb], [RPS_M], st=True, sp=False)
                mm(PS_M[hs, 0, 0, :], TT[p][:, 0, hs], Hb[:, h, :], [RTT[p], RHb], [RPS_M], st=False, sp=True)
                yield
                mm(PS_M[hs, 0, 1, 0:64], Pb[:], STz[h][:], [RPb, RST[h]], [RPS_M], st=True, sp=False)
                mm(PS_M[hs, 0, 1, 0:64], Zb[:, h, :], TT[p][:, 0, hs], [RZb, RTT[p]], [RPS_M], st=False, sp=True)
                yield
            ysl = ysum[:, t0 + ci * C: t0 + (ci + 1) * C]
            if d == 0:
                S.op("act", ("activation", dict(out=ysl, in_=PS_M[:, 0, 0, :], func=AF.Copy)), reads=[RPS_M], writes=[Rys[gci]])
            else:
                S.op("act", ("activation", dict(out=ytmp[:, 0:128], in_=PS_M[:, 0, 0, :], func=AF.Copy)), reads=[RPS_M], writes=[Rytmp])
                S.op("dve", ("tensor_tensor", dict(out=ysl, in0=ytmp[:, 0:128], in1=ysl, op=ALU.add)), reads=[Rytmp, Rys[gci]], writes=[Rys[gci]])
            yield
            for h in range(2):
                hs = HS[h]
                S.op("act", ("activation", dict(out=STz[h][hs, :], in_=PS_M[hs, 0, 1, 0:64], func=AF.Copy)), reads=[RPS_M], writes=[RST[h]])
            yield

        return stage1, stage2, stage3

    def finalize_batch(b):
        for blk in range(nblk):
            t0 = blk * BLK
            ysl = ysum[:, t0:t0 + BLK]
            Rin = Rys[t0 // C: (t0 + BLK) // C]
            S.op("pe", ("matmul", dict(out=PS_P1, lhsT=bdm[:], rhs=ysl, start=True, stop=True)), reads=[Rbdm] + Rin, writes=[RPS_P1])
            S.op("dve", ("tensor_tensor", dict(out=fin1[:], in0=ysl, in1=PS_P1, op=ALU.subtract)), reads=[RPS_P1] + Rin, writes=[Rfin1])
            S.op("pool", ("tensor_tensor", dict(out=fin2[:], in0=fin1[:], in1=fin1[:], op=ALU.mult)), reads=[Rfin1], writes=[Rfin2])
            S.op("pe", ("matmul", dict(out=PS_P2, lhsT=bdm[:], rhs=fin2[:], start=True, stop=True)), reads=[Rbdm, Rfin2], writes=[RPS_P2])
            S.op("dve", ("tensor_scalar", dict(out=fin3[:], in0=PS_P2, scalar1=GN_EPS, scalar2=None, op0=ALU.add)), reads=[RPS_P2], writes=[Rfin3])
            S.op("act", ("activation", dict(out=fin3[:], in_=fin3[:], func=AF.Sqrt)), reads=[Rfin3], writes=[Rfin3])
            S.op("dve", ("reciprocal", dict(out=fin3[:], in_=fin3[:])), reads=[Rfin3], writes=[Rfin3])
            S.op("dve", ("tensor_tensor", dict(out=fin1[:], in0=fin1[:], in1=fin3[:], op=ALU.mult)), reads=[Rfin1, Rfin3], writes=[Rfin1])
            S.op("dve", ("tensor_scalar", dict(out=fin2[:], in0=fin1[:], scalar1=PM(15), scalar2=PM(16), op0=ALU.mult, op1=ALU.add)), reads=[Rfin1, Rprm], writes=[Rfin2])
            S.op("dve", ("tensor_tensor", dict(out=fin2[:], in0=fin2[:], in1=bsum[:, t0:t0 + BLK], op=ALU.add)), reads=[Rfin2, Rbs[blk]], writes=[Rfin2])
            out_toks.append(S.dma(("dma_start", dict(out=yout[:, b, t0:t0 + BLK], in_=fin2[:])), reads=[Rfin2]))

    sched_blocks = []
    for b in range(NB):
        for d in range(2):
            order = list(range(nblk)) if d == 0 else list(range(nblk - 1, -1, -1))
            for n_, blk in enumerate(order):
                sched_blocks.append((b, d, blk, n_ == 0, (n_ == len(order) - 1) and d == 1))
    pcount = 0
    NCH = BLK // C
    for b in range(NB):
        blocks_b = [(k, sbk) for k, sbk in enumerate(sched_blocks) if sbk[0] == b]
        k0 = blocks_b[0][0]
        for _ in prep_gen(b, sched_blocks[k0][1], sched_blocks[k0][2], k0 % 2):
            pass
        seq = []
        for (k, (b_, d, blk, first_of_dir, last_of_batch)) in blocks_b:
            bwd = (d == 1)
            st = block_stages(b, d, blk, k % 2)
            chunks = list(range(NCH)) if not bwd else list(range(NCH - 1, -1, -1))
            for n_, ci in enumerate(chunks):
                ck = (ci, slice(ci * C, (ci + 1) * C), (blk * BLK // C) + ci, pcount % 2)
                pcount += 1
                seq.append((ck, st, k, n_, first_of_dir and n_ == 0))
        pgen = iter(())
        S.op("pool", ("memset", dict(ap=STz[0][:], constant=0.0)), writes=[RST[0]])
        S.op("pool", ("memset", dict(ap=STz[1][:], constant=0.0)), writes=[RST[1]])
        for _ in seq[0][1][0](seq[0][0]):
            pass
        for i, (ck, st, k, n_, fod) in enumerate(seq):
            if n_ == 0:
                if k + 1 < len(sched_blocks) and sched_blocks[k + 1][0] == b:
                    nb_, nd_, nblk_ = sched_blocks[k + 1][:3]
                    pgen = prep_gen(nb_, nd_, nblk_, (k + 1) % 2)
                else:
                    pgen = iter(())
            if n_ == NCH - 1:
                for _ in pgen:
                    pass
            parts = []
            if i > 0:
                parts.append(seq[i - 1][1][2](seq[i - 1][0]))
            if i + 1 < len(seq):
                parts.append(seq[i + 1][1][0](seq[i + 1][0]))
            fill = itertools.chain(*parts)
            for _ in st[1](ck):
                for _k in range(NFILL):
                    next(fill, None)
                for _k in range(NPREP):
                    next(pgen, None)
            for _ in fill:
                pass
            if i + 1 < len(seq) and seq[i + 1][4]:
                for _ in st[2](ck):
                    pass
                S.op("pool", ("memset", dict(ap=STz[0][:], constant=0.0)), writes=[RST[0]])
                S.op("pool", ("memset", dict(ap=STz[1][:], constant=0.0)), writes=[RST[1]])
                seq[i] = (ck, (st[0], st[1], lambda ck_: iter(())), k, n_, fod)
        for _ in seq[-1][1][2](seq[-1][0]):
            pass
        finalize_batch(b)
    return out_toks


NT = 2048
NTH = NT + 2


def emit_p1(S, nc, I, O):
    sb = lambda name, shape, dt=F32: nc.alloc_sbuf_tensor(name, shape, dt)
    ps = lambda name, shape, dt=F32: nc.alloc_psum_tensor(name, shape, dt)
    R = Region
    op = S.op
    mm = lambda out, l, r_, rd, wr, st=True, sp=True: op("pe", ("matmul", dict(out=out, lhsT=l, rhs=r_, start=st, stop=sp)), reads=rd, writes=wr)

    identf = sb("identf", [128, 128]); identb = sb("identb", [128, 128], BF16); Rid = R()
    gE = sb("gE", [128, 8, 1]); gO = sb("gO", [128, 8, 1]); Rg = R()
    S.dma(("dma_start", dict(out=identf[:], in_=I["c_ident"])), writes=[Rid])
    op("dve", ("tensor_copy", dict(out=identb[:], in_=identf[:])), reads=[Rid], writes=[Rid])
    S.dma(("dma_start", dict(out=gE[:, :, 0], in_=I["e_norm_g"].rearrange("(k p) -> p k", p=128), allow_slow_non_contiguous=True)), writes=[Rg])
    S.dma(("dma_start", dict(out=gO[:, :, 0], in_=I["o_norm_g"].rearrange("(k p) -> p k", p=128), allow_slow_non_contiguous=True)), writes=[Rg])
    hnT = sb("hnT", [128, 8, NTH], BF16); RhnT = [R() for _ in range(18)]
    yT = nc.dram_tensor("yT_d", [16, 128, NT], BF16).ap(); RyT = [[R() for _ in range(4)] for _ in range(16)]
    U = sb("U", [128, 4096]); RU = R()
    xt = [sb("xt%d" % i, [128, 1024]) for i in range(2)]; Rxt = [R(), R()]
    yo = [sb("yo%d" % i, [128, 512], BF16) for i in range(2)]; Ryo = [R(), R()]
    ytl = [sb("ytl%d" % i, [128, 16, 128], BF16) for i in range(2)]; Rytl = [R(), R()]
    xn = sb("xn", [128, 1024], BF16); Rxn = R()
    sq = sb("sq", [128, 1024]); Rsq = R()
    st = sb("st", [128, 8]); Rst = R()
    stg = [sb("stg%d" % i, [128, 8, 256]) for i in range(2)]; Rstg = [R(), R()]
    wbf = [sb("wbf%d" % i, [128, 8, 512], BF16) for i in range(2)]; Rwbf = [R(), R()]
    wbig = sb("wbig", [128, 16, 1024], BF16); Rwbig = R()
    t1 = sb("t1", [128, 512]); Rt1 = R()
    t1b = sb("t1b", [128, 512]); t1s = [t1, t1b]; Rt1s = [Rt1, R()]
    t2b = sb("t2b", [128, 512]); t3b = sb("t3b", [128, 512])
    t2 = sb("t2", [128, 512]); Rt2 = R()
    t3 = sb("t3", [128, 512]); Rt3 = R()
    cw = sb("cw", [128, 8, 3]); Rcw = R()
    PS_a = ps("PS_a", [128, 512]); RPa = R()
    PS_b = ps("PS_b", [128, 512]); RPb = R()
    PS_c = ps("PS_c", [128, 512]); RPc = R()
    PS_d = ps("PS_d", [128, 512]); RPd = R()
    PS_t = ps("PS_t", [128, 8, 128], BF16); RPt = R()
    PS_m = ps("PS_m", [128, 8, 128]); RPm = R()
    for j_ in range(3):
        S.dma(("dma_start", dict(out=cw[:, :, j_], in_=I["e_conv_w"][j_].rearrange("(cb p) -> p cb", p=128), allow_slow_non_contiguous=True)), writes=[Rcw])

    def norm_tile(xtile, Rx, gt, dst_fn, Rdst, nvalid=128):
        op("pool", ("memset", dict(ap=st[:, 0:1], constant=0.0)), writes=[Rst])
        op("act", ("activation", dict(out=sq[:], in_=xtile[:], func=AF.Square, accum_out=st[:, 0:1])), reads=[Rx, Rst], writes=[Rsq, Rst])
        op("dve", ("tensor_scalar", dict(out=st[:, 1:2], in0=st[:, 0:1], scalar1=1.0 / 1024, scalar2=1e-6, op0=ALU.mult, op1=ALU.add)), reads=[Rst], writes=[Rst])
        op("act", ("activation", dict(out=st[:, 2:3], in_=st[:, 1:2], func=AF.Sqrt)), reads=[Rst], writes=[Rst])
        op("dve", ("reciprocal", dict(out=st[:, 3:4], in_=st[:, 2:3])), reads=[Rst], writes=[Rst])
        op("dve", ("tensor_scalar", dict(out=xn[:], in0=xtile[:], scalar1=st[:, 3:4], scalar2=None, op0=ALU.mult)), reads=[Rx, Rst], writes=[Rxn])
        for k in range(8):
            op("pe", ("transpose", dict(out=PS_t[:, k, :], in_=xn[:, k * 128:(k + 1) * 128], identity=identb[:])), reads=[Rxn, Rid], writes=[RPt])
        dst_fn(gt)

    xh = I["xh"]
    for i in range(17):
        xb, Rx = xt[i % 2], Rxt[i % 2]
        if i < 16:
            S.dma(("dma_start", dict(out=xb[:], in_=xh[1 + 128 * i: 1 + 128 * (i + 1), :])), writes=[Rx])
            def dst(gt, i=i):
                op("dve", ("tensor_tensor", dict(out=hnT[:, :, 1 + 128 * i: 1 + 128 * (i + 1)], in0=PS_t[:], in1=gt[:].to_broadcast([128, 8, 128]), op=ALU.mult)), reads=[RPt, Rg], writes=[RhnT[i]])
        else:
            op("pool", ("memset", dict(ap=xb[:], constant=0.0)), writes=[Rx])
            S.dma(("dma_start", dict(out=xb[0:1, :], in_=xh[0:1, :])), writes=[Rx])
            S.dma(("dma_start", dict(out=xb[1:2, :], in_=xh[NT + 1:NT + 2, :])), writes=[Rx])
            def dst(gt):
                op("dve", ("tensor_tensor", dict(out=hnT[:, :, 0:1], in0=PS_t[:, :, 0:1], in1=gt[:], op=ALU.mult)), reads=[RPt, Rg], writes=[RhnT[16]])
                op("dve", ("tensor_tensor", dict(out=hnT[:, :, NT + 1:NT + 2], in0=PS_t[:, :, 1:2], in1=gt[:], op=ALU.mult)), reads=[RPt, Rg], writes=[RhnT[17]])
        norm_tile(xb, Rx, gE, dst, None)
    allh = RhnT

    wi = I["e_w_in"].rearrange("(k p) (s c) -> p k s c", p=128, c=1024)

    def load_w(buf, src4, nsp):
        for s_ in range(nsp):
            sb_ = s_ % 2
            S.dma(("dma_start", dict(out=stg[sb_][:, :, 0:128], in_=src4[:, :, s_, :])), writes=[Rstg[sb_]])
            op("act", ("activation", dict(out=wbf[buf][:, :, s_ * 128:(s_ + 1) * 128], in_=stg[sb_][:, :, 0:128], func=AF.Copy)), reads=[Rstg[sb_]], writes=[Rwbf[buf]])
        return wbf[buf][:, :, 0:nsp * 128].rearrange("p k (s c) -> p k s c", c=128)

    PSc0, RPc0, PSd0, RPd0 = PS_c, RPc, PS_d, RPd
    t2s = [t2, t2b]; Rt2s = [Rt2, R()]
    t3s = [t3, t3b]; Rt3s = [Rt3, R()]
    xc2 = sb("xc2", [128, NTH]); RU2 = R()
    chunksA = [(0, 512), (512, 512), (1024, 512), (1536, 512), (2048, 2)]
    RU_main = RU
    for cb in range(8):
        xc, RU = (U[:, 0:NTH], RU_main) if cb % 2 == 0 else (xc2[:, :], RU2)
        if cb % 2 == 0:
            for s_ in range(4):
                sb_ = s_ % 2
                S.dma(("dma_start", dict(out=stg[sb_][:], in_=wi[:, :, s_, cb * 128:(cb + 2) * 128])), writes=[Rstg[sb_]])
                op("act", ("activation", dict(out=wbf[0][:, :, s_ * 128:(s_ + 1) * 128], in_=stg[sb_][:, :, 0:128], func=AF.Copy)), reads=[Rstg[sb_]], writes=[Rwbf[0]])
                op("act", ("activation", dict(out=wbf[1][:, :, s_ * 128:(s_ + 1) * 128], in_=stg[sb_][:, :, 128:256], func=AF.Copy)), reads=[Rstg[sb_]], writes=[Rwbf[1]])
        w4 = wbf[cb % 2][:, :, 0:512].rearrange("p k (s c) -> p k s c", c=128)
        Rw = Rwbf[cb % 2]
        for ci_, (c0, n) in enumerate(chunksA):
            (PA, RA_), (PB, RB_) = (((PS_a, RPa), (PS_b, RPb)) if ci_ % 2 == 0 else ((PS_c, RPc), (PS_d, RPd)))
            for k in range(8):
                mm(PA[:, 0:n], w4[:, k, 0, :], hnT[:, k, c0:c0 + n], [Rw] + allh, [RA_], st=(k == 0), sp=(k == 7))
            for k in range(8):
                mm(PB[:, 0:n], w4[:, k, 2, :], hnT[:, k, c0:c0 + n], [Rw] + allh, [RB_], st=(k == 0), sp=(k == 7))
            t1_, Rt1_ = t1s[ci_ % 2], Rt1s[ci_ % 2]
            op("act", ("activation", dict(out=t1_[:, 0:n], in_=PA[:, 0:n], func=AF.Copy)), reads=[RA_], writes=[Rt1_])
            op("dve", ("tensor_tensor", dict(out=xc[:, c0:c0 + n], in0=PB[:, 0:n], in1=t1_[:, 0:n], op=ALU.mult)), reads=[RB_, Rt1_], writes=[RU])
        for j in range(4):
            c0 = 1 + 512 * j
            (PS_c, RPc), (PS_d, RPd) = ((PSc0, RPc0), (PSd0, RPd0)) if j % 2 == 1 else ((PS_a, RPa), (PS_b, RPb))
            t2, Rt2 = t2s[j % 2], Rt2s[j % 2]
            t3, Rt3 = t3s[j % 2], Rt3s[j % 2]
            for k in range(8):
                mm(PS_c[:], w4[:, k, 1, :], hnT[:, k, c0:c0 + 512], [Rw] + allh, [RPc], st=(k == 0), sp=(k == 7))
            for k in range(8):
                mm(PS_d[:], w4[:, k, 3, :], hnT[:, k, c0:c0 + 512], [Rw] + allh, [RPd], st=(k == 0), sp=(k == 7))
            op("dve", ("tensor_scalar", dict(out=t2[:], in0=xc[:, c0 - 1:c0 + 511], scalar1=cw[:, cb, 0:1], scalar2=None, op0=ALU.mult)), reads=[RU, Rcw], writes=[Rt2])
            op("dve", ("scalar_tensor_tensor", dict(out=t2[:], in0=xc[:, c0:c0 + 512], scalar=cw[:, cb, 1:2], in1=t2[:], op0=ALU.mult, op1=ALU.add)), reads=[RU, Rcw, Rt2], writes=[Rt2])
            op("dve", ("scalar_tensor_tensor", dict(out=t2[:], in0=xc[:, c0 + 1:c0 + 513], scalar=cw[:, cb, 2:3], in1=t2[:], op0=ALU.mult, op1=ALU.add)), reads=[RU, Rcw, Rt2], writes=[Rt2])
            op("act", ("activation", dict(out=t3[:], in_=PS_d[:], func=AF.Silu)), reads=[RPd], writes=[Rt3])
            op("dve", ("tensor_tensor", dict(out=t2[:], in0=PS_c[:], in1=t2[:], op=ALU.mult)), reads=[RPc, Rt2], writes=[Rt2])
            op("pool", ("tensor_tensor", dict(out=yo[j % 2][:], in0=t2[:], in1=t3[:], op=ALU.mult)), reads=[Rt2, Rt3], writes=[Ryo[j % 2]])
            S.dma(("dma_start", dict(out=yT[cb, :, 512 * j:512 * (j + 1)], in_=yo[j % 2][:])), reads=[Ryo[j % 2]], writes=[RyT[cb][j]], q="pool")

    PS_c, RPc, PS_d, RPd = PSc0, RPc0, PSd0, RPd0
    t2, Rt2, t3, Rt3 = t2s[0], Rt2s[0], t3s[0], Rt3s[0]
    RU = RU_main
    for hf in range(4):
        S.dma(("dma_start", dict(out=stg[hf % 2][:], in_=wi[:, :, 5, hf * 256:(hf + 1) * 256])), writes=[Rstg[hf % 2]])
        op("act", ("activation", dict(out=wbig[:, 0:8, hf * 256:(hf + 1) * 256], in_=stg[hf % 2][:], func=AF.Copy)), reads=[Rstg[hf % 2]], writes=[Rwbig])
    for hf in range(4):
        S.dma(("dma_start", dict(out=stg[hf % 2][:], in_=wi[:, :, 4, hf * 256:(hf + 1) * 256])), writes=[Rstg[hf % 2]])
        op("act", ("activation", dict(out=wbig[:, 8:16, hf * 256:(hf + 1) * 256], in_=stg[hf % 2][:], func=AF.Copy)), reads=[Rstg[hf % 2]], writes=[Rwbig])
    for hf in range(4):
        S.dma(("dma_start", dict(out=stg[hf % 2][:], in_=wi[:, :, 6, hf * 256:(hf + 1) * 256])), writes=[Rstg[hf % 2]])
        op("act", ("activation", dict(out=wbf[hf // 2][:, :, (hf % 2) * 256:(hf % 2 + 1) * 256], in_=stg[hf % 2][:], func=AF.Copy)), reads=[Rstg[hf % 2]], writes=[Rwbf[hf // 2]])
    wsn = sb("wsn", [128, 8, 128]); wsnb = sb("wsnb", [128, 8, 128], BF16); wsT = sb("wsT", [128, 8, 128], BF16); Rws = R()
    S.dma(("dma_start", dict(out=wsn[:], in_=I["e_sgu_w"].rearrange("g i j -> i g j"))), writes=[Rws])
    op("dve", ("tensor_copy", dict(out=wsnb[:], in_=wsn[:])), reads=[Rws], writes=[Rws])
    for g in range(8):
        op("pe", ("transpose", dict(out=PS_t[:, g, :], in_=wsnb[:, g, :], identity=identb[:])), reads=[Rws, Rid], writes=[RPt])
    op("act", ("activation", dict(out=wsT[:], in_=PS_t[:], func=AF.Copy)), reads=[RPt], writes=[Rws])
    bsB = sb("bsB", [128, 8, 128]); lnG = sb("lnG", [128, 1024]); lnB = sb("lnB", [128, 1024]); Rbc = R()
    S.dma(("dma_start", dict(out=bsB[:].rearrange("p g i -> p (g i)"), in_=I["e_sgu_b"].rearrange("g i -> (g i)").partition_broadcast(128))), writes=[Rbc])
    S.dma(("dma_start", dict(out=lnG[:], in_=I["e_sgu_ln_g"].partition_broadcast(128))), writes=[Rbc])
    S.dma(("dma_start", dict(out=lnB[:], in_=I["e_sgu_ln_b"].partition_broadcast(128))), writes=[Rbc])
    vsb = sb("vsb", [128, 1024]); Rvsb = R()
    vnb = sb("vnb", [128, 1024], BF16); Rvnb = R()
    mixall = U[:, 0:4096].rearrange("p (g t) -> p g t", g=8)
    for tg in range(4):
        for ti in range(4):
            c0 = 1 + 128 * (4 * tg + ti)
            for hf, (P_, RP_) in enumerate((((PS_a, RPa), (PS_b, RPb)) if ti % 2 == 0 else ((PS_c, RPc), (PS_d, RPd)))):
                for k in range(8):
                    mm(P_[:], hnT[:, k, c0:c0 + 128], wbig[:, k, hf * 512:(hf + 1) * 512], [Rwbig] + allh, [RP_], st=(k == 0), sp=(k == 7))
                if hf == 0:
                    op("pool", ("memset", dict(ap=st[:, :], constant=0.0)), writes=[Rst])
                op("act", ("activation", dict(out=vsb[:, hf * 512:(hf + 1) * 512], in_=P_[:], func=AF.Copy, accum_out=(st[:, 0:1] if hf == 0 else st[:, 7:8]))), reads=[RP_, Rst], writes=[Rvsb, Rst])
            op("act", ("activation", dict(out=sq[:], in_=vsb[:], func=AF.Square, accum_out=st[:, 1:2])), reads=[Rvsb, Rst], writes=[Rsq, Rst])
            op("dve", ("tensor_tensor", dict(out=st[:, 0:1], in0=st[:, 0:1], in1=st[:, 7:8], op=ALU.add)), reads=[Rst], writes=[Rst])
            op("dve", ("tensor_scalar", dict(out=st[:, 2:3], in0=st[:, 0:1], scalar1=1.0 / 1024, scalar2=None, op0=ALU.mult)), reads=[Rst], writes=[Rst])
            op("dve", ("tensor_tensor", dict(out=st[:, 3:4], in0=st[:, 2:3], in1=st[:, 2:3], op=ALU.mult)), reads=[Rst], writes=[Rst])
            op("dve", ("scalar_tensor_tensor", dict(out=st[:, 4:5], in0=st[:, 1:2], scalar=1.0 / 1024, in1=st[:, 3:4], op0=ALU.mult, op1=ALU.subtract)), reads=[Rst], writes=[Rst])
            op("dve", ("tensor_scalar", dict(out=st[:, 4:5], in0=st[:, 4:5], scalar1=1e-5, scalar2=None, op0=ALU.add)), reads=[Rst], writes=[Rst])
            op("act", ("activation", dict(out=st[:, 5:6], in_=st[:, 4:5], func=AF.Sqrt)), reads=[Rst], writes=[Rst])
            op("dve", ("reciprocal", dict(out=st[:, 6:7], in_=st[:, 5:6])), reads=[Rst], writes=[Rst])
            op("dve", ("tensor_scalar", dict(out=vsb[:], in0=vsb[:], scalar1=st[:, 2:3], scalar2=st[:, 6:7], op0=ALU.subtract, op1=ALU.mult)), reads=[Rvsb, Rst], writes=[Rvsb])
            op("dve", ("tensor_tensor", dict(out=vsb[:], in0=vsb[:], in1=lnG[:], op=ALU.mult)), reads=[Rvsb, Rbc], writes=[Rvsb])
            op("dve", ("tensor_tensor", dict(out=vnb[:], in0=vsb[:], in1=lnB[:], op=ALU.add)), reads=[Rvsb, Rbc], writes=[Rvnb])
            for g in range(8):
                mm(PS_m[:, g, :], vnb[:, g * 128:(g + 1) * 128], wsT[:, g, :], [Rvnb, Rws], [RPm])
            op("dve", ("tensor_tensor", dict(out=mixall[:, :, ti * 128:(ti + 1) * 128], in0=PS_m[:], in1=bsB[:], op=ALU.add)), reads=[RPm, Rbc], writes=[RU])
        c0 = 1 + 512 * tg
        for g in range(8):
            buf = g % 2

            (PU, RPU), (PZ, RPZ) = ((PS_c, RPc), (PS_d, RPd)) if g % 2 == 0 else ((PS_a, RPa), (PS_b, RPb))
            t2, Rt2 = t2s[g % 2], Rt2s[g % 2]
            t3, Rt3 = t3s[g % 2], Rt3s[g % 2]
            for k in range(8):
                mm(PU[:], wbig[:, 8 + k, g * 128:(g + 1) * 128], hnT[:, k, c0:c0 + 512], [Rwbig] + allh, [RPU], st=(k == 0), sp=(k == 7))
            for k in range(8):
                mm(PZ[:], wbf[g // 4][:, k, (g % 4) * 128:(g % 4 + 1) * 128], hnT[:, k, c0:c0 + 512], [Rwbf[g // 4]] + allh, [RPZ], st=(k == 0), sp=(k == 7))
            op("act", ("activation", dict(out=t3[:], in_=PZ[:], func=AF.Silu)), reads=[RPZ], writes=[Rt3])
            op("dve", ("tensor_tensor", dict(out=t2[:], in0=PU[:], in1=mixall[:, g, :], op=ALU.mult)), reads=[RPU, RU], writes=[Rt2])
            op("pool", ("tensor_tensor", dict(out=yo[g % 2][:], in0=t2[:], in1=t3[:], op=ALU.mult)), reads=[Rt2, Rt3], writes=[Ryo[g % 2]])
            S.dma(("dma_start", dict(out=yT[8 + g, :, 512 * tg:512 * (tg + 1)], in_=yo[g % 2][:])), reads=[Ryo[g % 2]], writes=[RyT[8 + g][tg]], q="pool")

    wo = I["e_w_out"].rearrange("(k p) n -> p k n", p=128)
    for q in range(2):
        for hf in range(4):
            S.dma(("dma_start", dict(out=stg[hf % 2][:], in_=wo[:, 8 * q:8 * q + 8, hf * 256:(hf + 1) * 256])), writes=[Rstg[hf % 2]])
            op("act", ("activation", dict(out=wbig[:, 8 * q:8 * q + 8, hf * 256:(hf + 1) * 256], in_=stg[hf % 2][:], func=AF.Copy)), reads=[Rstg[hf % 2]], writes=[Rwbig])
    ally = [r for row in RyT for r in row]
    h1ts = [sb("h1t%d" % i, [128, 1024]) for i in range(2)]; Rh1s = [R(), R()]
    for i in range(16):
        xb, Rx = xt[i % 2], Rxt[i % 2]
        h1t, Rh1 = h1ts[i % 2], Rh1s[i % 2]
        S.dma(("dma_start", dict(out=xb[:], in_=xh[1 + 128 * i: 1 + 128 * (i + 1), :])), writes=[Rx])
        S.dma(("dma_start", dict(out=ytl[i % 2][:], in_=yT[:, :, 128 * i:128 * (i + 1)].rearrange("k p t -> p k t"))), reads=ally, writes=[Rytl[i % 2]])
        for hf, (P_, RP_) in enumerate((((PS_a, RPa), (PS_b, RPb)) if i % 2 == 0 else ((PS_c, RPc), (PS_d, RPd)))):
            for k in range(16):
                mm(P_[:], ytl[i % 2][:, k, :], wbig[:, k, hf * 512:(hf + 1) * 512], [Rwbig, Rytl[i % 2]], [RP_], st=(k == 0), sp=(k == 15))
            op("dve", ("tensor_tensor", dict(out=h1t[:, hf * 512:(hf + 1) * 512], in0=P_[:], in1=xb[:, hf * 512:(hf + 1) * 512], op=ALU.add)), reads=[RP_, Rx], writes=[Rh1])
        S.dma(("dma_start", dict(out=O["h1"][128 * i:128 * (i + 1), :], in_=h1t[:])), reads=[Rh1], q="act")

        def dst(gt, i=i):
            op("dve", ("tensor_tensor", dict(out=hnT[:, :, 1 + 128 * i: 1 + 128 * (i + 1)], in0=PS_t[:], in1=gt[:].to_broadcast([128, 8, 128]), op=ALU.mult)), reads=[RPt, Rg], writes=[RhnT[i]])
        norm_tile(h1t, Rh1, gO, dst, None)

    wi1 = I["o_w_in"].rearrange("(k p) n -> p k n", p=128)
    blocks = [(c * 128, c * 128, False) for c in range(25)]
    blocks += [(3200 + c * 128, 3200 + c * 128, True) for c in range(8)]
    blocks += [(4736 + c * 128, 4224 + c * 128, True) for c in range(4)]
    ob = [sb("ob%d" % i, [128, 512]) for i in range(2)]; Rob = [R(), R()]
    oi = 0
    toks = []
    bi = 0
    nblocks = len(blocks)
    pairbuf = 0
    while bi < nblocks:
        sc, dr, act = blocks[bi]
        paired = (bi + 1 < nblocks) and (blocks[bi + 1][0] == sc + 128) and (blocks[bi + 1][2] == act)
        ncol = 256 if paired else 128
        buf = pairbuf % 2
        pairbuf += 1
        S.dma(("dma_start", dict(out=stg[buf][:, :, 0:ncol], in_=wi1[:, :, sc:sc + ncol])), writes=[Rstg[buf]])
        op("dve", ("tensor_copy", dict(out=wbf[buf][:, :, 0:ncol], in_=stg[buf][:, :, 0:ncol])), reads=[Rstg[buf]], writes=[Rwbf[buf]])
        for sub in range(2 if paired else 1):
            sc_, dr_, act_ = blocks[bi + sub]
            for j in range(4):
                P_, RP_ = ((PS_a, RPa), (PS_b, RPb), (PS_c, RPc), (PS_d, RPd))[j]
                c0 = 1 + 512 * j
                for k in range(8):
                    mm(P_[:], wbf[buf][:, k, sub * 128:(sub + 1) * 128], hnT[:, k, c0:c0 + 512], [Rwbf[buf]] + allh, [RP_], st=(k == 0), sp=(k == 7))
                o_, Ro = ob[oi % 2], Rob[oi % 2]
                oi += 1
                op("act", ("activation", dict(out=o_[:], in_=P_[:], func=(AF.Silu if act_ else AF.Copy))), reads=[RP_], writes=[Ro])
                toks.append(S.dma(("dma_start", dict(out=O["pT"][dr_:dr_ + 128, 512 * j:512 * (j + 1)], in_=o_[:])), reads=[Ro], q="act"))
        bi += 2 if paired else 1
    for hf in range(2):
        S.dma(("dma_start", dict(out=stg[hf][:], in_=wi1[:, :, 4224 + 256 * hf:4224 + 256 * (hf + 1)])), writes=[Rstg[hf]])
        op("act", ("activation", dict(out=wbf[0][:, :, 256 * hf:256 * (hf + 1)], in_=stg[hf][:], func=AF.Copy)), reads=[Rstg[hf]], writes=[Rwbf[0]])
    for i in range(16):
        c0 = 1 + 128 * i
        P_, RP_ = ((PS_a, RPa), (PS_b, RPb))[i % 2]
        for k in range(8):
            mm(P_[:], hnT[:, k, c0:c0 + 128], wbf[0][:, k, :], [Rwbf[0]] + allh, [RP_], st=(k == 0), sp=(k == 7))
        o_, Ro = ob[oi % 2], Rob[oi % 2]
        oi += 1
        op("act", ("activation", dict(out=o_[:], in_=P_[:], func=AF.Copy)), reads=[RP_], writes=[Ro])
        toks.append(S.dma(("dma_start", dict(out=O["fd"][128 * i:128 * (i + 1), :], in_=o_[:])), reads=[Ro], q="act"))
    return toks


def full_barrier(S):
    keys = list(S.cnt.items())
    for e in S.ENGS:
        waits = []
        for k, v in keys:
            if k == e:
                continue
            if S.seen[e].get(k, 0) < v:
                S.seen[e][k] = v
                waits.append((k, v))
        if waits:
            S.prog[e].append([waits, None, ("_none", 0)])


def emit_fnet(S, nc, I, ydT):
    R = Region
    op = S.op
    mm = lambda out, l, r_, rd, wr, st=True, sp=True: op("pe", ("matmul", dict(out=out, lhsT=l, rhs=r_, start=st, stop=sp)), reads=rd, writes=wr)
    toks = []
    with ExitStack() as es:
        sb = lambda name, shape, dt=F32: es.enter_context(nc.sbuf_tensor(name, shape, dt))
        ps = lambda name, shape, dt=F32: es.enter_context(nc.psum_tensor(name, shape, dt))
        xs = sb("f_xs", [128, 4096]); Rxs = R()
        xb = sb("f_xb", [128, 64, 128], BF16); Rxb = R()
        Fb = sb("f_F", [128, 256], BF16); RF = R()
        A_sb = sb("f_A", [64, 128, 256], BF16); RA = R()
        PQ = sb("f_PQ", [128, 2, 64, 128], BF16); RPQ = R()
        Tg = [[sb("f_T%d%d" % (i, j), [64, 16, 128], BF16) for j in range(2)] for i in range(2)]; RTg = [R(), R()]
        wf32 = sb("f_w32", [128, 128]); wfb = sb("f_wb", [128, 128], BF16); Rwf = R()
        Ccb = sb("f_Cc", [128, 128], BF16); mScb = sb("f_mSc", [128, 128], BF16); Rcs = R()
        Gb = sb("f_G", [128, 256], BF16); RG = R()
        ob = [sb("f_ob%d" % i, [128, 512]) for i in range(2)]; Rob = [R(), R()]
        PS = [ps("f_ps%d" % i, [128, 512]) for i in range(2)]; RPS = [R(), R()]
        S.dma(("dma_start", dict(out=Fb[:], in_=I["c_F"])), writes=[RF])
        S.dma(("dma_start", dict(out=Ccb[:], in_=I["c_Cc"])), writes=[Rcs])
        S.dma(("dma_start", dict(out=mScb[:], in_=I["c_mSc"])), writes=[Rcs])
        S.dma(("dma_start", dict(out=wf32[:], in_=I["fw"])), writes=[Rwf])
        op("dve", ("tensor_copy", dict(out=wfb[:], in_=wf32[:])), reads=[Rwf], writes=[Rwf])
        xbf = xb[:].rearrange("p l c -> p (l c)")
        for hf in range(2):
            S.dma(("dma_start", dict(out=xs[:], in_=I["fx"][:, hf * 4096:(hf + 1) * 4096])), writes=[Rxs])
            op("act", ("activation", dict(out=xbf[:, hf * 4096:(hf + 1) * 4096], in_=xs[:], func=AF.Copy)), reads=[Rxs], writes=[Rxb])
        for c2 in range(64):
            P_, RP_ = PS[c2 % 2], RPS[c2 % 2]
            for j in range(2):
                mm(P_[0:64, j * 256:(j + 1) * 256], xb[:, :, 2 * c2 + j], Fb[:], [Rxb, RF], [RP_])
            op("act" if c2 % 2 == 0 else "dve", ("activation", dict(out=A_sb[0:64, 2 * c2:2 * c2 + 2, :], in_=P_[0:64, :].rearrange("p (j k) -> p j k", j=2), func=AF.Copy)) if c2 % 2 == 0 else
               ("tensor_copy", dict(out=A_sb[0:64, 2 * c2:2 * c2 + 2, :], in_=P_[0:64, :].rearrange("p (j k) -> p j k", j=2))), reads=[RP_], writes=[RA])
        T1d = I["c_T1"].rearrange("p (k h) -> p k h", h=128)
        T2d = I["c_T2"].rearrange("p (k h) -> p k h", h=128)
        ei = 0
        for grp in range(8):
            tb = grp % 2
            S.dma(("dma_start", dict(out=Tg[tb][0][:], in_=T1d[:, grp * 16:(grp + 1) * 16, :])), writes=[RTg[tb]])
            S.dma(("dma_start", dict(out=Tg[tb][1][:], in_=T2d[:, grp * 16:(grp + 1) * 16, :])), writes=[RTg[tb]])
            for q in range(4):
                P_, RP_ = PS[ei % 2], RPS[ei % 2]
                for j in range(4):
                    kk_ = q * 4 + j
                    kl = grp * 16 + kk_
                    mm(P_[:, j * 128:(j + 1) * 128], A_sb[0:64, :, kl], Tg[tb][0][0:64, kk_, :], [RA, RTg[tb]], [RP_], st=True, sp=False)
                    mm(P_[:, j * 128:(j + 1) * 128], A_sb[0:64, :, 128 + kl], Tg[tb][1][0:64, kk_, :], [RA, RTg[tb]], [RP_], st=False, sp=True)
                kl0 = grp * 16 + q * 4
                for qq in range(2):
                    op("act" if qq == 0 else "dve",
                       ("activation", dict(out=PQ[:, qq, :, kl0:kl0 + 4].rearrange("p h l -> p l h"), in_=P_[:].rearrange("p (l q h) -> p l q h", l=4, q=2)[:, :, qq, :], func=AF.Copy)) if qq == 0 else
                       ("tensor_copy", dict(out=PQ[:, qq, :, kl0:kl0 + 4].rearrange("p h l -> p l h"), in_=P_[:].rearrange("p (l q h) -> p l q h", l=4, q=2)[:, :, qq, :])),
                       reads=[RP_], writes=[RPQ])
                ei += 1
        P_, RP_ = PS[0], RPS[0]
        mm(P_[:, 0:128], Ccb[:], wfb[:], [Rcs, Rwf], [RP_])
        mm(P_[:, 128:256], mScb[:], wfb[:], [Rcs, Rwf], [RP_])
        op("act", ("activation", dict(out=Gb[:], in_=P_[:, 0:256], func=AF.Copy)), reads=[RP_], writes=[RG])
        for t4 in range(16):
            P_, RP_ = PS[(t4 + 1) % 2], RPS[(t4 + 1) % 2]
            for j in range(4):
                kh = 4 * t4 + j
                mm(P_[:, j * 128:(j + 1) * 128], Gb[:, 0:128], PQ[:, 0, kh, :], [RG, RPQ], [RP_], st=True, sp=False)
                mm(P_[:, j * 128:(j + 1) * 128], Gb[:, 128:256], PQ[:, 1, kh, :], [RG, RPQ], [RP_], st=False, sp=True)
            o_, Ro = ob[t4 % 2], Rob[t4 % 2]
            op("act", ("activation", dict(out=o_[:], in_=P_[:], func=AF.Copy)), reads=[RP_], writes=[Ro])
            toks.append(S.dma(("dma_start", dict(out=ydT[:, 512 * t4:512 * (t4 + 1)], in_=o_[:])), reads=[Ro]))
    full_barrier(S)
    return toks


def emit_p3(S, nc, I, yout):
    sb = lambda name, shape, dt=F32: nc.alloc_sbuf_tensor(name, shape, dt)
    ps = lambda name, shape, dt=F32: nc.alloc_psum_tensor(name, shape, dt)
    R = Region
    op = S.op
    mm = lambda out, l, r_, rd, wr, st=True, sp=True: op("pe", ("matmul", dict(out=out, lhsT=l, rhs=r_, start=st, stop=sp)), reads=rd, writes=wr)
    stg = [sb("stg%d" % i, [128, 8, 256]) for i in range(2)]; Rstg = [R(), R()]
    wO = sb("wO", [128, 12, 1024], BF16); RwO = R()
    gN = sb("gN", [128, 1024]); RgN = R()
    gt_all = sb("gt_all", [128, 12, 2048], BF16); Rgt = R()
    ya = [sb("ya%d" % i, [128, 512]) for i in range(2)]; Rya = [R(), R()]
    ga = [sb("ga%d" % i, [128, 512]) for i in range(2)]; Rga = [R(), R()]
    h1t = [sb("h1t%d" % i, [128, 1024]) for i in range(2)]; Rh1 = [R(), R()]
    h2 = sb("h2", [128, 1024]); Rh2 = R()
    sq = sb("sq", [128, 1024]); Rsq = R()
    st = sb("st", [128, 8]); Rst = R()
    yo = [sb("yo%d" % i, [128, 1024]) for i in range(2)]; Ryo = [R(), R()]
    PS_a = ps("PS_a", [128, 512]); RPa = R()
    PS_b = ps("PS_b", [128, 512]); RPb = R()
    wo3 = I["o_w_out"].rearrange("(k p) n -> p k n", p=128)
    si = 0
    for (k0, nk) in ((0, 8), (8, 4)):
        for cq in range(4):
            b_ = si % 2; si += 1
            S.dma(("dma_start", dict(out=stg[b_][:, 0:nk, :], in_=wo3[:, k0:k0 + nk, cq * 256:(cq + 1) * 256])), writes=[Rstg[b_]])
            op("act", ("activation", dict(out=wO[:, k0:k0 + nk, cq * 256:(cq + 1) * 256], in_=stg[b_][:, 0:nk, :], func=AF.Copy)), reads=[Rstg[b_]], writes=[RwO])
    S.dma(("dma_start", dict(out=gN[:], in_=I["final_norm_g"].partition_broadcast(128))), writes=[RgN])
    ii = 0
    for blk in range(12):
        src = I["ycT"][blk * 128:(blk + 1) * 128] if blk < 8 else I["ydT"][(blk - 8) * 128:(blk - 7) * 128]
        gsrc = I["gT"][blk * 128:(blk + 1) * 128]
        for j in range(4):
            b_ = ii % 2; ii += 1
            S.dma(("dma_start", dict(out=ya[b_][:], in_=src[:, 512 * j:512 * (j + 1)])), writes=[Rya[b_]])
            S.dma(("dma_start", dict(out=ga[b_][:], in_=gsrc[:, 512 * j:512 * (j + 1)])), writes=[Rga[b_]])
            op("dve", ("tensor_tensor", dict(out=gt_all[:, blk, 512 * j:512 * (j + 1)], in0=ya[b_][:], in1=ga[b_][:], op=ALU.mult)), reads=[Rya[b_], Rga[b_]], writes=[Rgt])
    toks = []
    for i in range(16):
        hb, Rh = h1t[i % 2], Rh1[i % 2]
        S.dma(("dma_start", dict(out=hb[:], in_=I["h1"][128 * i:128 * (i + 1), :])), writes=[Rh])
        for hf, (P_, RP_) in enumerate(((PS_a, RPa), (PS_b, RPb))):
            for k in range(12):
                mm(P_[:], gt_all[:, k, 128 * i:128 * (i + 1)], wO[:, k, hf * 512:(hf + 1) * 512], [Rgt, RwO], [RP_], st=(k == 0), sp=(k == 11))
            op("dve", ("tensor_tensor", dict(out=h2[:, hf * 512:(hf + 1) * 512], in0=P_[:], in1=hb[:, hf * 512:(hf + 1) * 512], op=ALU.add)), reads=[RP_, Rh], writes=[Rh2])
        op("pool", ("memset", dict(ap=st[:, 0:1], constant=0.0)), writes=[Rst])
        op("act", ("activation", dict(out=sq[:], in_=h2[:], func=AF.Square, accum_out=st[:, 0:1])), reads=[Rh2, Rst], writes=[Rsq, Rst])
        op("dve", ("tensor_scalar", dict(out=st[:, 1:2], in0=st[:, 0:1], scalar1=1.0 / 1024, scalar2=1e-6, op0=ALU.mult, op1=ALU.add)), reads=[Rst], writes=[Rst])
        op("act", ("activation", dict(out=st[:, 2:3], in_=st[:, 1:2], func=AF.Sqrt)), reads=[Rst], writes=[Rst])
        op("dve", ("reciprocal", dict(out=st[:, 3:4], in_=st[:, 2:3])), reads=[Rst], writes=[Rst])
        op("dve", ("tensor_scalar", dict(out=h2[:], in0=h2[:], scalar1=st[:, 3:4], scalar2=None, op0=ALU.mult)), reads=[Rh2, Rst], writes=[Rh2])
        o_, Ro = yo[i % 2], Ryo[i % 2]
        op("dve", ("tensor_tensor", dict(out=o_[:], in0=h2[:], in1=gN[:], op=ALU.mult)), reads=[Rh2, RgN], writes=[Ro])
        toks.append(S.dma(("dma_start", dict(out=yout[128 * i:128 * (i + 1), :], in_=o_[:])), reads=[Ro], q="pool"))
    return toks


def _mk(nc, name, shape, dt=None, out=False):
    return nc.dram_tensor(name, list(shape), dt or F32, kind=("ExternalOutput" if out else "ExternalInput")).ap()


W1 = ["e_norm_g", "e_w_in", "e_conv_w", "e_sgu_ln_g", "e_sgu_ln_b", "e_sgu_w", "e_sgu_b", "e_w_out", "o_norm_g", "o_w_in"]


def build_l1(shapes):
    nc = bass.Bass("TRN2", target_bir_lowering=False)
    I = {"xh": _mk(nc, "xh", [2050, 1024]), "c_ident": _mk(nc, "c_ident", [128, 128])}
    for n in W1:
        I[n] = _mk(nc, n, shapes[n])
    O = {"h1": _mk(nc, "h1", [2048, 1024], out=True), "pT": _mk(nc, "pT", [4736, 2048], out=True),
         "fd": _mk(nc, "fd", [2048, 512], out=True)}
    S = Sched(nc)
    toks = emit_p1(S, nc, I, O)
    S.barrier_on("sp", toks)
    S.finalize()
    return nc


def build_l2(consts):
    NB, T = 2, 8192
    nc = bass.Bass("TRN2", target_bir_lowering=False)
    pr, pk, pv, pwa = (_mk(nc, n, [128, NB, T + 2]) for n in ("pr", "pk", "pv", "pwa"))
    prm = _mk(nc, "prm", [128, 17]); w2a2 = _mk(nc, "w2a2", [128, 2, 128])
    A = {k: _mk(nc, k, v.shape) for k, v in consts.items()}
    FI = {"fx": _mk(nc, "fx", [128, 8192]), "fw": _mk(nc, "fw", [128, 128]),
          "c_F": _mk(nc, "c_F", [128, 256], BF16), "c_T1": _mk(nc, "c_T1", [64, 16384], BF16),
          "c_T2": _mk(nc, "c_T2", [64, 16384], BF16), "c_Cc": _mk(nc, "c_Cc", [128, 128], BF16),
          "c_mSc": _mk(nc, "c_mSc", [128, 128], BF16)}
    yout = _mk(nc, "yout", [128, NB, T], out=True)
    ydT = _mk(nc, "ydT", [128, T], out=True)
    S = Sched(nc)
    toks = emit_fnet(S, nc, FI, ydT)
    toks += emit_rwkv(S, nc, A, pr, pk, pv, pwa, prm, w2a2, yout, NB, T)
    S.barrier_on("sp", toks)
    S.finalize()
    return nc


def build_l3():
    nc = bass.Bass("TRN2", target_bir_lowering=False)
    I = {"ycT": _mk(nc, "ycT", [1024, 2048]), "ydT": _mk(nc, "ydT", [512, 2048]), "gT": _mk(nc, "gT", [1536, 2048]),
         "h1": _mk(nc, "h1", [2048, 1024]), "o_w_out": _mk(nc, "o_w_out", [1536, 1024]),
         "final_norm_g": _mk(nc, "final_norm_g", [1024])}
    y = _mk(nc, "y", [2048, 1024], out=True)
    S = Sched(nc)
    toks = emit_p3(S, nc, I, y)
    S.barrier_on("sp", toks)
    S.finalize()
    return nc


def fnet_tables():
    import ml_dtypes
    N = 8192
    nh = np.arange(128); kl = np.arange(128)
    ang = 2 * np.pi * np.outer(nh, kl) / 128
    F = np.concatenate([np.cos(ang), np.sin(ang)], axis=1)
    nl = np.arange(64)[:, None, None]; klo = np.arange(128)[None, :, None]; kh = np.arange(64)[None, None, :]
    beta = 2 * np.pi * ((nl * (klo + 128 * kh)) % N) / N
    T1 = np.concatenate([np.cos(beta), np.sin(beta)], axis=2).reshape(64, 16384)
    T2 = np.concatenate([-np.sin(beta), np.cos(beta)], axis=2).reshape(64, 16384)
    c = np.arange(128); phi = 2 * np.pi * np.outer(c, c) / 128
    nrm = 1 / np.sqrt(N * 128)
    bf = lambda a: np.ascontiguousarray(a.astype(np.float32)).astype(ml_dtypes.bfloat16)
    return {"c_F": bf(F), "c_T1": bf(T1), "c_T2": bf(T2), "c_Cc": bf(np.cos(phi) * nrm), "c_mSc": bf(-np.sin(phi) * nrm)}


def kernel(**inputs):
    f32 = lambda a: np.ascontiguousarray(np.asarray(a), dtype=np.float32)
    inp = {k: f32(v) for k, v in inputs.items()}
    x = inp["x"]
    ncores = 8
    cores = list(range(ncores))
    w1 = {n: np.ascontiguousarray(inp[n][0]) for n in W1}
    ident = np.eye(128, dtype=np.float32)
    maps = []
    for c in cores:
        b, s0 = c // 4, (c % 4) * 2048
        xh = np.zeros((2050, 1024), np.float32)
        xh[1:2049] = x[b, s0:s0 + 2048]
        if s0 > 0:
            xh[0] = x[b, s0 - 1]
        if s0 + 2048 < 8192:
            xh[2049] = x[b, s0 + 2048]
        m = {"xh": xh, "c_ident": ident}
        m.update(w1)
        maps.append(m)
    nc1 = build_l1({n: w1[n].shape for n in W1})
    r1 = run_bass_kernel_spmd(nc1, maps, core_ids=cores).results
    PT = np.concatenate([np.asarray(r["pT"]) for r in r1], axis=1)
    FD = np.concatenate([np.asarray(r["fd"]) for r in r1], axis=0)
    consts = build_consts_np()
    ft = fnet_tables()
    mu, w0, w2, a0, a2 = inp["o_mu"][0], inp["o_w0"][0], inp["o_w2"][0], inp["o_a0"][0], inp["o_a2"][0]
    k_k, k_a, r_k = inp["o_k_k"][0], inp["o_k_a"][0], inp["o_r_k"][0].reshape(-1)
    lg, lb = inp["o_lnx_g"][0], inp["o_lnx_b"][0]
    PT3 = PT.reshape(4736, 2, 8192)
    pad = lambda a: np.ascontiguousarray(np.pad(a, ((0, 0), (0, 0), (1, 1))))
    maps = []
    for c in cores:
        ch = slice(c * 128, (c + 1) * 128)
        m = {"pr": pad(PT3[0:1024][ch]), "pk": pad(PT3[1024:2048][ch]), "pv": pad(PT3[2048:3072][ch]),
             "pwa": pad(PT3[3072:3200])}
        prm = np.zeros((128, 17), np.float32)
        for d in range(2):
            prm[:, 0 + d] = mu[d, 0:1024][ch]; prm[:, 2 + d] = mu[d, 1024:2048][ch]; prm[:, 4 + d] = mu[d, 2048:3072][ch]
            prm[:, 6 + d] = mu[d, 3072:3200]; prm[:, 8 + d] = w0[d][ch]; prm[:, 10 + d] = a0[d][ch]
        prm[:, 12] = k_k[ch]; prm[:, 13] = k_a[ch]; prm[:, 14] = r_k[ch]; prm[:, 15] = lg[ch]; prm[:, 16] = lb[ch]
        m["prm"] = prm
        m["w2a2"] = np.ascontiguousarray(np.concatenate([w2[:, :, ch], a2[:, :, ch]], axis=1).transpose(1, 0, 2))
        m.update(consts)
        b, g = c // 4, c % 4
        m["fx"] = np.ascontiguousarray(FD[b * 8192:(b + 1) * 8192, g * 128:(g + 1) * 128]).reshape(128, 8192)
        m["fw"] = np.ascontiguousarray(inp["o_fnet_w"][0, g])
        m.update(ft)
        maps.append(m)
    nc2 = build_l2(consts)
    r2 = run_bass_kernel_spmd(nc2, maps, core_ids=cores).results
    YC = np.concatenate([np.asarray(r["yout"]).reshape(128, 16384) for r in r2], axis=0)
    YD = np.concatenate([np.concatenate([np.asarray(r2[b * 4 + g]["ydT"]) for g in range(4)], axis=0) for b in range(2)], axis=1)
    maps = []
    for c in cores:
        ts = slice(c * 2048, (c + 1) * 2048)
        maps.append({"ycT": np.ascontiguousarray(YC[:, ts]), "ydT": np.ascontiguousarray(YD[:, ts]),
                     "gT": np.ascontiguousarray(PT[3200:4736, ts]), "h1": np.asarray(r1[c]["h1"]),
                     "o_w_out": np.ascontiguousarray(inp["o_w_out"][0]), "final_norm_g": inp["final_norm_g"]})
    nc3 = build_l3()
    r3 = run_bass_kernel_spmd(nc3, maps, core_ids=cores).results
    y = np.concatenate([np.asarray(r["y"]) for r in r3], axis=0).reshape(2, 8192, 1024)
    return y.astype(np.float32)
```
